# Optimizing a Trainium2 kernel written in Bass

```python
import math
import jax, jax.numpy as jnp
from jax import lax
import numpy as np

D_MODEL = 1024
BATCH = 2
SEQ = 8192
DEPTH = 2

CTX_LEN = 256
GRID_W = 64
EPS = 1e-6
BRANCH = D_MODEL // 4
D_MIX = 4 * BRANCH
CONV_W = 4
GLA_HEADS = 4
GLA_DK = BRANCH // 8
GLA_DV = BRANCH // GLA_HEADS
GLA_GATE_RANK = 16
GLA_GATE_TAU = 16.0
GLA_CHUNK = 64
LRU_BLOCKS = 4
LRU_BLOCK_W = BRANCH // LRU_BLOCKS
LRU_C = 8.0
DIFF_HEADS = 4
DIFF_DV = BRANCH // DIFF_HEADS
DIFF_D = DIFF_DV // 2
Q_BLOCK = 128
ROPE_BASE = 10000.0
SSD_HEADS = 4
SSD_P = BRANCH // SSD_HEADS
SSD_GROUPS = 2
SSD_N = 64
SSD_CHUNK = 64
SSD_CONV_DIM = SSD_HEADS * SSD_P + 2 * SSD_GROUPS * SSD_N
GLA_SIZES = (GLA_HEADS * GLA_DK, GLA_HEADS * GLA_DK, GLA_HEADS * GLA_DV, GLA_GATE_RANK, GLA_GATE_RANK, BRANCH)
LRU_SIZES = (BRANCH, BRANCH)
DIFF_SIZES = (DIFF_HEADS * 2 * DIFF_D, DIFF_HEADS * 2 * DIFF_D, DIFF_HEADS * DIFF_DV, BRANCH)
SSD_SIZES = (SSD_CONV_DIM, SSD_HEADS, SSD_HEADS, BRANCH)
GROUP_SIZES = (sum(GLA_SIZES), sum(LRU_SIZES), sum(DIFF_SIZES), sum(SSD_SIZES))
D_IN = sum(GROUP_SIZES)

kernel_name = 'hybrid_gla_rglru_diffattn_ssd_block'


def split_sizes(u, sizes):
    idx = []
    acc = 0
    for s in sizes[:-1]:
        acc += s
        idx.append(acc)
    return jnp.split(u, idx, axis=-1)


def rmsnorm(x, w):
    xf = x.astype(jnp.float32)
    y = xf * lax.rsqrt(jnp.mean(xf * xf, axis=-1, keepdims=True) + EPS)
    return (y * w.astype(jnp.float32)).astype(x.dtype)


def dwconv(x, w, b):
    t = x.shape[1]
    left = CONV_W // 2
    xp = jnp.pad(x, ((0, 0), (left, CONV_W - 1 - left), (0, 0)))
    y = b + xp[:, 0:t] * w[0]
    for j in range(1, CONV_W):
        y = y + xp[:, j:j + t] * w[j]
    return y


def chunk_state_scan(decay, upd, s0):
    def step(s, inp):
        d, u = inp
        return d * s + u, s
    s_fin, s_in = lax.scan(step, s0, (decay, upd))
    return s_in, s_fin


def gla_chunked(q, k, v, log_a, s0):
    b, t, h, _ = q.shape
    n = t // GLA_CHUNK
    r = lambda a: a.reshape(b, n, GLA_CHUNK, h, a.shape[-1])
    q, k, v, log_a = r(q), r(k), r(v), r(log_a)
    g = jnp.cumsum(log_a.astype(jnp.float32), axis=2)
    g_last = g[:, :, -1:]
    qg = q * jnp.exp(g)
    kg = k * jnp.exp(-g)
    kd = k * jnp.exp(g_last - g)
    mask = jnp.tril(jnp.ones((GLA_CHUNK, GLA_CHUNK), dtype=bool))
    att = jnp.where(mask, jnp.einsum('bnihk,bnjhk->bnhij', qg, kg), 0.0)
    o = jnp.einsum('bnhij,bnjhv->bnihv', att, v)
    upd = jnp.einsum('bnjhk,bnjhv->nbhkv', kd, v)
    decay = jnp.exp(g_last[:, :, 0]).transpose(1, 0, 2, 3)[..., None]
    s_in, s_fin = chunk_state_scan(decay, upd, s0)
    o = o + jnp.einsum('bnihk,nbhkv->bnihv', qg, s_in)
    return o.reshape(b, t, h, -1), s_fin


def gla_dir(q, k, v, lr, w2, b2, s0, reverse):
    log_a = jax.nn.log_sigmoid((lr @ w2 + b2).astype(jnp.float32)) / GLA_GATE_TAU
    log_a = log_a.reshape(k.shape)
    if reverse:
        q, k, v, log_a = (jnp.flip(a, axis=1) for a in (q, k, v, log_a))
    o, s = gla_chunked(q, k, v, log_a, s0)
    if reverse:
        o = jnp.flip(o, axis=1)
    return o, s


def gla_mixer(u_ctx, u_lat, w2, b2, norm_w, with_ctx_out):
    def prep(u):
        b, t, _ = u.shape
        q, k, v, lr_f, lr_b, g = split_sizes(u, GLA_SIZES)
        q = q.reshape(b, t, GLA_HEADS, GLA_DK) * (GLA_DK ** -0.5)
        k = k.reshape(b, t, GLA_HEADS, GLA_DK)
        v = v.reshape(b, t, GLA_HEADS, GLA_DV)
        return q, k, v, lr_f, lr_b, g
    qc, kc, vc, lcf, lcb, gc = prep(u_ctx)
    ql, kl, vl, llf, llb, gl = prep(u_lat)
    s0 = jnp.zeros((u_ctx.shape[0], GLA_HEADS, GLA_DK, GLA_DV), jnp.float32)
    ocf, scf = gla_dir(qc, kc, vc, lcf, w2[0], b2[0], s0, False)
    ocb, scb = gla_dir(qc, kc, vc, lcb, w2[1], b2[1], s0, True)
    olf, _ = gla_dir(ql, kl, vl, llf, w2[0], b2[0], scf, False)
    olb, _ = gla_dir(ql, kl, vl, llb, w2[1], b2[1], scb, True)

    def finish(o, g):
        b, t = g.shape[:2]
        y = rmsnorm(o, norm_w).reshape(b, t, BRANCH) * jax.nn.silu(g)
        return y.astype(g.dtype)
    yc = finish(ocf + ocb, gc) if with_ctx_out else None
    return yc, finish(olf + olb, gl)


def lru_combine(e1, e2):
    a1, u1 = e1
    a2, u2 = e2
    return a1 * a2, a2 * u1 + u2


def rglru_dir(x, w_a, b_a, w_x, b_x, lam, h0, reverse):
    b, t, _ = x.shape
    xb = x.reshape(b, t, LRU_BLOCKS, LRU_BLOCK_W)
    r = jax.nn.sigmoid(jnp.einsum('btgi,gij->btgj', xb, w_a).reshape(b, t, BRANCH) + b_a)
    i = jax.nn.sigmoid(jnp.einsum('btgi,gij->btgj', xb, w_x).reshape(b, t, BRANCH) + b_x)
    log_a = -LRU_C * r.astype(jnp.float32) * jax.nn.softplus(-lam.astype(jnp.float32))
    a = jnp.exp(log_a)
    u = jnp.sqrt(-jnp.expm1(2.0 * log_a)) * (i * x).astype(jnp.float32)
    if reverse:
        a, u = jnp.flip(a, axis=1), jnp.flip(u, axis=1)
    u = u.at[:, 0].add(a[:, 0] * h0)
    _, h = lax.associative_scan(lru_combine, (a, u), axis=1)
    h_last = h[:, -1]
    if reverse:
        h = jnp.flip(h, axis=1)
    return h, h_last


def rglru_mixer(u_ctx, u_lat, conv_w, conv_b, w_a, b_a, w_x, b_x, lam, with_ctx_out):
    xc, gc = split_sizes(u_ctx, LRU_SIZES)
    xl, gl = split_sizes(u_lat, LRU_SIZES)
    xc = dwconv(xc, conv_w, conv_b)
    xl = dwconv(xl, conv_w, conv_b)
    h0 = jnp.zeros((xc.shape[0], BRANCH), jnp.float32)
    hcf, scf = rglru_dir(xc, w_a[0], b_a[0], w_x[0], b_x[0], lam[0], h0, False)
    hcb, scb = rglru_dir(xc, w_a[1], b_a[1], w_x[1], b_x[1], lam[1], h0, True)
    hlf, _ = rglru_dir(xl, w_a[0], b_a[0], w_x[0], b_x[0], lam[0], scf, False)
    hlb, _ = rglru_dir(xl, w_a[1], b_a[1], w_x[1], b_x[1], lam[1], scb, True)
    yl = ((hlf + hlb) * jax.nn.silu(gl)).astype(gl.dtype)
    yc = ((hcf + hcb) * jax.nn.silu(gc)).astype(gc.dtype) if with_ctx_out else None
    return yc, yl


def axial_rope_tables(rows):
    n_freq = DIFF_D // 4
    inv = ROPE_BASE ** (-jnp.arange(n_freq, dtype=jnp.float32) / n_freq)
    tpos = jnp.arange(rows * GRID_W)
    pos_r = (tpos // GRID_W).astype(jnp.float32)
    pos_c = (tpos % GRID_W).astype(jnp.float32)
    ang_r = pos_r[:, None] * inv
    ang_c = pos_c[:, None] * inv
    ang = jnp.concatenate([ang_r, ang_r, ang_c, ang_c], axis=-1)
    return jnp.cos(ang), jnp.sin(ang)


def rotate_half_axial(x):
    h = DIFF_D // 4
    xr = x.reshape(x.shape[:-1] + (2, 2, h))
    return jnp.concatenate([-xr[..., 1:2, :], xr[..., 0:1, :]], axis=-2).reshape(x.shape)


def apply_rope(x, cos, sin):
    cs = cos[:, None, None, :]
    sn = sin[:, None, None, :]
    return (x * cs + rotate_half_axial(x) * sn).astype(x.dtype)


def diff_attend(q, k, v, lam):
    s = jnp.einsum('bqhcd,bkhcd->bhcqk', q, k).astype(jnp.float32) * (DIFF_D ** -0.5)
    p = jax.nn.softmax(s, axis=-1)
    w = p[:, :, 0] - lam * p[:, :, 1]
    return jnp.einsum('bhqk,bkhv->bqhv', w.astype(v.dtype), v)


def diff_attn_mixer(u_ctx, u_lat, lam_vecs, subln_w, cos, sin, lam_init, with_ctx_out):
    def prep(u):
        b, t, _ = u.shape
        q, k, v, g = split_sizes(u, DIFF_SIZES)
        return (q.reshape(b, t, DIFF_HEADS, 2, DIFF_D), k.reshape(b, t, DIFF_HEADS, 2, DIFF_D),
                v.reshape(b, t, DIFF_HEADS, DIFF_DV), g)
    qc, kc, vc, gc = prep(u_ctx)
    ql, kl, vl, gl = prep(u_lat)
    lv = lam_vecs.astype(jnp.float32)
    lam = jnp.exp(jnp.sum(lv[0] * lv[1])) - jnp.exp(jnp.sum(lv[2] * lv[3])) + lam_init
    ql = apply_rope(ql, cos, sin)
    kl = apply_rope(kl, cos, sin)
    k_all = jnp.concatenate([kl, kc.astype(kl.dtype)], axis=1)
    v_all = jnp.concatenate([vl, vc.astype(vl.dtype)], axis=1)
    b, t = u_lat.shape[:2]
    nb = t // Q_BLOCK
    qb = jnp.moveaxis(ql.reshape(b, nb, Q_BLOCK, DIFF_HEADS, 2, DIFF_D), 1, 0)
    ol = lax.map(lambda qq: diff_attend(qq, k_all, v_all, lam), qb)
    ol = jnp.moveaxis(ol, 0, 1).reshape(b, t, DIFF_HEADS, DIFF_DV)

    def finish(o, g):
        bb, tt = g.shape[:2]
        y = (rmsnorm(o, subln_w) * (1.0 - lam_init)).reshape(bb, tt, BRANCH) * jax.nn.silu(g)
        return y.astype(g.dtype)
    yc = finish(diff_attend(qc, kc, vc, lam), gc) if with_ctx_out else None
    return yc, finish(ol, gl)


def ssd_chunked(x, dt, a, bm, cm, s0):
    b, t, h, p = x.shape
    n = t // SSD_CHUNK
    x = x.reshape(b, n, SSD_CHUNK, h, p)
    dt = dt.reshape(b, n, SSD_CHUNK, h)
    bm = bm.reshape(b, n, SSD_CHUNK, h, -1)
    cm = cm.reshape(b, n, SSD_CHUNK, h, -1)
    cum = jnp.cumsum(dt * a, axis=2)
    seg = cum[:, :, :, None, :] - cum[:, :, None, :, :]
    mask = jnp.tril(jnp.ones((SSD_CHUNK, SSD_CHUNK), dtype=bool))[:, :, None]
    decay_ls = jnp.exp(jnp.where(mask, seg, -jnp.inf))
    cb = jnp.einsum('bclhn,bcshn->bclsh', cm, bm)
    y = jnp.einsum('bclsh,bcsh,bcshp->bclhp', cb * decay_ls, dt, x)
    w_state = jnp.exp(cum[:, :, -1:] - cum) * dt
    upd = jnp.einsum('bcshn,bcsh,bcshp->cbhpn', bm, w_state, x)
    decay = jnp.exp(cum[:, :, -1]).transpose(1, 0, 2)[..., None, None]
    s_in, s_fin = chunk_state_scan(decay, upd, s0)
    y = y + jnp.einsum('bclhn,cbhpn,bclh->bclhp', cm, s_in, jnp.exp(cum))
    return y.reshape(b, t, h, p), s_fin


def ssd_dir(x, dt_raw, dt_bias, a_log, bm, cm, s0, reverse):
    dt = jax.nn.softplus(dt_raw.astype(jnp.float32) + dt_bias.astype(jnp.float32))
    a = -jnp.exp(a_log.astype(jnp.float32))
    if reverse:
        x, dt, bm, cm = (jnp.flip(z, axis=1) for z in (x, dt, bm, cm))
    y, s = ssd_chunked(x, dt, a, bm, cm, s0)
    if reverse:
        y = jnp.flip(y, axis=1)
    return y, s


def ssd_mixer(u_ctx, u_lat, conv_w, conv_b, dt_bias, a_log, d_skip, norm_w, with_ctx_out):
    rep = SSD_HEADS // SSD_GROUPS

    def prep(u):
        b, t, _ = u.shape
        xbc, dt_f, dt_b, z = split_sizes(u, SSD_SIZES)
        xbc = jax.nn.silu(dwconv(xbc, conv_w, conv_b))
        xs, bm, cm = split_sizes(xbc, (SSD_HEADS * SSD_P, SSD_GROUPS * SSD_N, SSD_GROUPS * SSD_N))
        xs = xs.reshape(b, t, SSD_HEADS, SSD_P)
        bm = jnp.repeat(bm.reshape(b, t, SSD_GROUPS, SSD_N), rep, axis=2)
        cm = jnp.repeat(cm.reshape(b, t, SSD_GROUPS, SSD_N), rep, axis=2)
        return xs, bm, cm, dt_f, dt_b, z
    xc, bc, cc, dcf, dcb, zc = prep(u_ctx)
    xl, bl, cl, dlf, dlb, zl = prep(u_lat)
    s0 = jnp.zeros((u_ctx.shape[0], SSD_HEADS, SSD_P, SSD_N), jnp.float32)
    ycf, scf = ssd_dir(xc, dcf, dt_bias[0], a_log[0], bc, cc, s0, False)
    ycb, scb = ssd_dir(xc, dcb, dt_bias[1], a_log[1], bc, cc, s0, True)
    ylf, _ = ssd_dir(xl, dlf, dt_bias[0], a_log[0], bl, cl, scf, False)
    ylb, _ = ssd_dir(xl, dlb, dt_bias[1], a_log[1], bl, cl, scb, True)

    def finish(yf, yb, xs, z):
        b, t = z.shape[:2]
        y = (yf + yb + d_skip[:, None] * xs).reshape(b, t, BRANCH)
        return rmsnorm(y.astype(z.dtype) * jax.nn.silu(z), norm_w)
    yc = finish(ycf, ycb, xc, zc) if with_ctx_out else None
    return yc, finish(ylf, ylb, xl, zl)


def setup_inputs(seed: int = 0) -> dict:
    key = jax.random.key(seed)
    ks = iter(jax.random.split(key, 40))
    f32 = jnp.float32

    def nrm(shape, scale):
        return jax.random.normal(next(ks), shape, f32) * scale

    L = DEPTH
    D = D_MODEL
    x = nrm((BATCH, SEQ, D), 1.0)
    c = nrm((BATCH, D), 1.0)
    ctx = nrm((BATCH, CTX_LEN, D), 1.0)
    c_ctx = nrm((D,), 1.0)
    w_mod = nrm((L, D, 3 * D), 0.3 * D ** -0.5)
    b_mod = nrm((L, 3 * D), 0.02)
    norm_w = 1.0 + nrm((L, D), 0.02)
    w_in = nrm((L, D, D_IN), D ** -0.5)
    w_out = nrm((L, D_MIX, D), D_MIX ** -0.5)
    gla_w2 = nrm((L, 2, GLA_GATE_RANK, GLA_HEADS * GLA_DK), GLA_GATE_RANK ** -0.5)
    gla_b2 = nrm((L, 2, GLA_HEADS * GLA_DK), 0.1)
    gla_norm_w = 1.0 + nrm((L, GLA_DV), 0.02)
    lru_conv_w = nrm((L, CONV_W, BRANCH), 0.5)
    lru_conv_b = nrm((L, BRANCH), 0.02)
    lru_wa = nrm((L, 2, LRU_BLOCKS, LRU_BLOCK_W, LRU_BLOCK_W), LRU_BLOCK_W ** -0.5)
    lru_ba = nrm((L, 2, BRANCH), 0.02)
    lru_wx = nrm((L, 2, LRU_BLOCKS, LRU_BLOCK_W, LRU_BLOCK_W), LRU_BLOCK_W ** -0.5)
    lru_bx = nrm((L, 2, BRANCH), 0.02)
    a_init = jax.random.uniform(next(ks), (L, 2, BRANCH), f32, 0.9, 0.999)
    lru_lam = jnp.log(a_init) - jnp.log1p(-a_init)
    diff_lam = nrm((L, 4, DIFF_D), 0.1)
    diff_subln_w = 1.0 + nrm((L, DIFF_DV), 0.02)
    ssd_conv_w = nrm((L, CONV_W, SSD_CONV_DIM), 0.5)
    ssd_conv_b = nrm((L, SSD_CONV_DIM), 0.02)
    dt0 = jnp.exp(jax.random.uniform(next(ks), (L, 2, SSD_HEADS), f32, math.log(1e-3), math.log(1e-1)))
    ssd_dt_bias = dt0 + jnp.log(-jnp.expm1(-dt0))
    ssd_a_log = jnp.log(jax.random.uniform(next(ks), (L, 2, SSD_HEADS), f32, 1.0, 16.0))
    ssd_d = 1.0 + nrm((L, SSD_HEADS), 0.1)
    ssd_norm_w = 1.0 + nrm((L, BRANCH), 0.02)
    final_norm_w = 1.0 + nrm((D,), 0.02)
    return {'x': x, 'c': c, 'ctx': ctx, 'c_ctx': c_ctx, 'w_mod': w_mod, 'b_mod': b_mod, 'norm_w': norm_w,
            'w_in': w_in, 'w_out': w_out, 'gla_w2': gla_w2, 'gla_b2': gla_b2, 'gla_norm_w': gla_norm_w,
            'lru_conv_w': lru_conv_w, 'lru_conv_b': lru_conv_b, 'lru_wa': lru_wa, 'lru_ba': lru_ba,
            'lru_wx': lru_wx, 'lru_bx': lru_bx, 'lru_lam': lru_lam, 'diff_lam': diff_lam,
            'diff_subln_w': diff_subln_w, 'ssd_conv_w': ssd_conv_w, 'ssd_conv_b': ssd_conv_b,
            'ssd_dt_bias': ssd_dt_bias, 'ssd_a_log': ssd_a_log, 'ssd_d': ssd_d, 'ssd_norm_w': ssd_norm_w,
            'final_norm_w': final_norm_w}


def reference(x, c, ctx, c_ctx, w_mod, b_mod, norm_w, w_in, w_out, gla_w2, gla_b2, gla_norm_w,
              lru_conv_w, lru_conv_b, lru_wa, lru_ba, lru_wx, lru_bx, lru_lam, diff_lam, diff_subln_w,
              ssd_conv_w, ssd_conv_b, ssd_dt_bias, ssd_a_log, ssd_d, ssd_norm_w, final_norm_w):
    rows = x.shape[1] // GRID_W
    cos, sin = axial_rope_tables(rows)
    h_lat, h_ctx = x, ctx
    silu_c = jax.nn.silu(c)
    silu_cc = jax.nn.silu(c_ctx)
    for l in range(DEPTH):
        last = l == DEPTH - 1
        lam_init = 0.8 - 0.6 * math.exp(-0.3 * l)
        mod_lat = (silu_c @ w_mod[l] + b_mod[l])[:, None, :]
        mod_ctx = silu_cc @ w_mod[l] + b_mod[l]
        shift_l, scale_l, gate_l = jnp.split(mod_lat, 3, axis=-1)
        shift_c, scale_c, gate_c = jnp.split(mod_ctx, 3, axis=-1)
        u_lat = (rmsnorm(h_lat, norm_w[l]) * (1.0 + scale_l) + shift_l) @ w_in[l]
        u_ctx = (rmsnorm(h_ctx, norm_w[l]) * (1.0 + scale_c) + shift_c) @ w_in[l]
        ga_c, gb_c, gc_c, gd_c = split_sizes(u_ctx, GROUP_SIZES)
        ga_l, gb_l, gc_l, gd_l = split_sizes(u_lat, GROUP_SIZES)
        ya_c, ya_l = gla_mixer(ga_c, ga_l, gla_w2[l], gla_b2[l], gla_norm_w[l], not last)
        yb_c, yb_l = rglru_mixer(gb_c, gb_l, lru_conv_w[l], lru_conv_b[l], lru_wa[l], lru_ba[l],
                                 lru_wx[l], lru_bx[l], lru_lam[l], not last)
        yc_c, yc_l = diff_attn_mixer(gc_c, gc_l, diff_lam[l], diff_subln_w[l], cos, sin, lam_init, not last)
        yd_c, yd_l = ssd_mixer(gd_c, gd_l, ssd_conv_w[l], ssd_conv_b[l], ssd_dt_bias[l], ssd_a_log[l],
                               ssd_d[l], ssd_norm_w[l], not last)
        y_lat = jnp.concatenate([ya_l, yb_l, yc_l, yd_l], axis=-1) @ w_out[l]
        h_lat = h_lat + gate_l * y_lat
        if not last:
            y_ctx = jnp.concatenate([ya_c, yb_c, yc_c, yd_c], axis=-1) @ w_out[l]
            h_ctx = h_ctx + gate_c * y_ctx
    return rmsnorm(h_lat, final_norm_w)
```

```python
import numpy as np
from contextlib import ExitStack
import concourse.bass as bass
import concourse.mybir as mybir
from concourse.bass_utils import run_bass_kernel_spmd

F32 = mybir.dt.float32
BF16 = mybir.dt.bfloat16
AF = mybir.ActivationFunctionType
ALU = mybir.AluOpType
AX = mybir.AxisListType
ENGS = ('pe', 'act', 'dve', 'pool', 'sp')
ESZ = {F32: 4, BF16: 2}


class Prog:
    def __init__(self, nc, stack, n_dma_sems=40):
        self.nc = nc
        self.stack = stack
        self.sem = {e: stack.enter_context(nc.semaphore('sem_' + e)) for e in ENGS}
        self.cnt = {e: 0 for e in ENGS}
        self.known = {e: {} for e in ENGS}
        self.stream = {e: [] for e in ENGS}
        self.dsem = [stack.enter_context(nc.semaphore('dsem%d' % i)) for i in range(n_dma_sems)]
        self.dcum = [0] * n_dma_sems
        self.drr = 0
        self.rows = {}
        self.trk = {}
        self.final_events = []
        self.nwaits = 0
        self.ndma = 0
        self.arenas = {}

    def sbuf(self, name, shape, dtype=F32):
        t = self.stack.enter_context(self.nc.sbuf_tensor(name, list(shape), dtype))
        self.rows[name] = int(np.prod(shape[1:])) * ESZ[dtype]
        return t

    def psum(self, name, shape, dtype=F32):
        t = self.stack.enter_context(self.nc.psum_tensor(name, list(shape), dtype))
        self.rows[name] = int(np.prod(shape[1:])) * ESZ[dtype]
        return t

    def arena(self, name, nbytes):
        t = self.sbuf(name, [128, nbytes // 4], F32)
        a = Arena(self, name, t, nbytes)
        self.arenas[name] = a
        return a

    def _box(self, ap):
        b = self._box0(ap)
        a = self.arenas.get(b[0])
        if a is not None:
            rid = a.region_of(b[3], b[4])
            return ((b[0], rid),) + b[1:]
        return b

    def _box0(self, ap):
        name = ap.tensor.name
        es = ESZ.get(ap.dtype, 4)
        off = int(ap.offset) * es
        pairs = ap.ap
        if name in self.rows:
            rs = self.rows[name]
            p0 = off // rs
            pst, pc = pairs[0]
            p1 = p0 + (pc if pst != 0 else 1)
            fo = off % rs
            rest = pairs[1:]
        else:
            p0, p1 = 0, 1
            fo = off
            rest = pairs
        lo = fo
        hi = fo
        for st, c in rest:
            d = st * (c - 1) * es
            if d < 0:
                lo += d
            else:
                hi += d
        return (name, p0, p1, lo, hi + es)

    @staticmethod
    def _ov(a, b):
        return a[1] < b[2] and b[1] < a[2] and a[3] < b[4] and b[3] < a[4]

    @staticmethod
    def _inside(a, b):
        return a[1] >= b[1] and a[2] <= b[2] and a[3] >= b[3] and a[4] <= b[4]

    def _deps(self, reads, writes):
        deps = []
        for ap in reads:
            b = self._box(ap)
            t = self.trk.get(b[0])
            if t:
                for (wb, ev, clk) in t['w']:
                    if self._ov(b, wb):
                        deps.append((ev, clk, 'raw'))
        for ap in writes:
            b = self._box(ap)
            t = self.trk.get(b[0])
            if t:
                for (wb, ev, clk) in t['w']:
                    if self._ov(b, wb):
                        deps.append((ev, clk, 'waw'))
                for (rb, ev, clk) in t['r']:
                    if self._ov(b, rb):
                        deps.append((ev, clk, 'war'))
        return deps

    @staticmethod
    def _merge(lst):
        merged = {}
        for (rb, rev, rclk) in lst:
            m = merged.get(rev[0])
            if m is None:
                merged[rev[0]] = (rb, rev, rclk)
            else:
                mb, mev, mclk = m
                nb = (rb[0], min(rb[1], mb[1]), max(rb[2], mb[2]), min(rb[3], mb[3]), max(rb[4], mb[4]))
                merged[rev[0]] = (nb, rev, rclk) if rev[1] > mev[1] else (nb, mev, mclk)
        return list(merged.values())

    def _record(self, reads, writes, ev, clk):
        for ap in reads:
            b = self._box(ap)
            t = self.trk.setdefault(b[0], {'w': [], 'r': []})
            rl = t['r']
            for i, (rb, rev, rclk) in enumerate(rl):
                if rev[0] == ev[0] and self._inside(rb, b):
                    rl[i] = (b, ev, clk)
                    break
            else:
                rl.append((b, ev, clk))
                if len(rl) > 40:
                    t['r'] = self._merge(rl)
        for ap in writes:
            b = self._box(ap)
            t = self.trk.setdefault(b[0], {'w': [], 'r': []})
            t['w'] = [e for e in t['w'] if not self._inside(e[0], b)]
            t['r'] = [e for e in t['r'] if not self._inside(e[0], b)]
            t['w'].append((b, ev, clk))
            if len(t['w']) > 40:
                t['w'] = self._merge(t['w'])

    def _waits_for(self, eng, deps):
        kn = self.known[eng]
        waits = {}
        for (ev, clk, kind) in deps:
            key, val = ev
            if key == eng:
                if eng == 'pe':
                    continue
                if eng in ('act', 'dve') and kind != 'raw':
                    continue
            if kn.get(key, 0) >= val:
                continue
            if waits.get(key, 0) < val:
                waits[key] = val
        for (ev, clk, kind) in deps:
            key, val = ev
            if key in waits and waits[key] >= val:
                for k2, v2 in clk.items():
                    if kn.get(k2, 0) < v2:
                        kn[k2] = v2
        for k, v in waits.items():
            if kn.get(k, 0) < v:
                kn[k] = v
        return list(waits.items())

    def op(self, eng, fn, reads=(), writes=()):
        deps = self._deps(reads, writes)
        waits = self._waits_for(eng, deps)
        self.cnt[eng] += 1
        ev = (eng, self.cnt[eng])
        clk = dict(self.known[eng])
        clk[eng] = self.cnt[eng]
        self.stream[eng].append((waits, fn, ('e', eng)))
        self._record(reads, writes, ev, clk)
        self.nwaits += len(waits)
        return ev

    def dma(self, q, out, in_, final=False, **kw):
        i = self.drr
        self.drr = (self.drr + 1) % len(self.dsem)
        deps = self._deps([in_], [out])
        key = ('d', i)
        deps.append(((key, self.dcum[i]), {}, 'raw'))
        waits = self._waits_for(q, deps)
        self.dcum[i] += 16
        ev = (key, self.dcum[i])
        clk = dict(self.known[q])
        clk[key] = self.dcum[i]
        self.stream[q].append((waits, (lambda e, out=out, in_=in_, kw=kw: e.dma_start(out=out, in_=in_, **kw)), ('d', i)))
        self._record([in_], [out], ev, clk)
        if final:
            self.final_events.append(ev)
        self.nwaits += len(waits)
        self.ndma += 1
        return ev

    def collective(self, kind, ins, outs, groups, q='pool'):
        if not hasattr(self, 'csem'):
            self.csem = []
            self.ccum = []
        self.csem.append(self.stack.enter_context(self.nc.semaphore('csem%d' % len(self.csem))))
        self.ccum.append(0)
        i = len(self.csem) - 1
        deps = self._deps(list(ins), list(outs))
        key = ('c', i)
        waits = self._waits_for(q, deps)
        self.ccum[i] += 1
        ev = (key, self.ccum[i])
        clk = dict(self.known[q])
        clk[key] = self.ccum[i]
        self.stream[q].append((waits, (lambda e: e.collective_compute(kind, ALU.bypass, replica_groups=groups,
                                                                      ins=[a.opt() for a in ins], outs=[a.opt() for a in outs])),
                               ('c', i)))
        self._record(list(ins), list(outs), ev, clk)
        self.nwaits += len(waits)
        return ev

    def finish(self, eng='sp'):
        self.stream[eng].append((list(self.final_events), None, None))

    def _semobj(self, key):
        if isinstance(key, tuple):
            return self.dsem[key[1]] if key[0] == 'd' else self.csem[key[1]]
        return self.sem[key]

    def emit(self):
        for e in ENGS:
            assert self.cnt[e] < 60000, (e, self.cnt[e])

        def replay(name, eng):
            for (waits, fn, inc) in self.stream[name]:
                for (key, val) in waits:
                    eng.wait_ge(self._semobj(key), val)
                if fn is None:
                    continue
                ins = fn(eng)
                if inc[0] == 'e':
                    ins.then_inc(self.sem[inc[1]], 1)
                elif inc[0] == 'c':
                    ins.then_inc(self.csem[inc[1]])
                else:
                    ins.then_inc(self.dsem[inc[1]], 16)

        with self.nc.Block() as block:
            @block.sync
            def _(e):
                replay('sp', e)

            @block.scalar
            def _(e):
                replay('act', e)

            @block.vector
            def _(e):
                replay('dve', e)

            @block.gpsimd
            def _(e):
                replay('pool', e)

            @block.tensor
            def _(e):
                replay('pe', e)

    def mm(self, out, lhsT, rhs, start=True, stop=True):
        return self.op('pe', lambda e: e.matmul(out, lhsT, rhs, start=start, stop=stop),
                       reads=[lhsT, rhs], writes=[out])

    def transpose(self, out, in_, ident):
        return self.op('pe', lambda e: e.transpose(out, in_, ident), reads=[in_, ident], writes=[out])

    def act(self, out, in_, func, bias=None, scale=None, accum_out=None):
        reads = [in_]
        kw = {}
        if bias is not None:
            kw['bias'] = bias
            if not isinstance(bias, (int, float)):
                reads.append(bias)
        if scale is not None:
            kw['scale'] = scale
            if not isinstance(scale, (int, float)):
                reads.append(scale)
        writes = [out]
        if accum_out is not None:
            kw['accum_out'] = accum_out
            writes.append(accum_out)
        return self.op('act', lambda e: e.activation(out, in_, func, **kw), reads=reads, writes=writes)

    def tt(self, eng, out, in0, in1, op):
        return self.op(eng, lambda e: e.tensor_tensor(out, in0, in1, op), reads=[in0, in1], writes=[out])

    def ts(self, eng, out, in0, s1, s2, op0, op1=None):
        reads = [in0]
        for s in (s1, s2):
            if s is not None and not isinstance(s, (int, float)):
                reads.append(s)
        if op1 is None:
            return self.op(eng, lambda e: e.tensor_scalar(out, in0, s1, None, op0), reads=reads, writes=[out])
        return self.op(eng, lambda e: e.tensor_scalar(out, in0, s1, s2, op0, op1), reads=reads, writes=[out])

    def stt(self, out, in0, scalar, in1, op0, op1):
        reads = [in0, in1]
        if not isinstance(scalar, (int, float)):
            reads.append(scalar)
        return self.op('dve', lambda e: e.scalar_tensor_tensor(out, in0, scalar, in1, op0, op1),
                       reads=reads, writes=[out])

    def copy(self, eng, out, in_):
        if eng == 'act':
            return self.op('act', lambda e: e.copy(out, in_), reads=[in_], writes=[out])
        return self.op(eng, lambda e: e.tensor_copy(out, in_), reads=[in_], writes=[out])

    def memset(self, eng, ap, val):
        return self.op(eng, lambda e: e.memset(ap, val), reads=[], writes=[ap])

    def scan(self, out, d0, d1, init, op0=ALU.mult, op1=ALU.add):
        reads = [d0, d1]
        if not isinstance(init, (int, float)):
            reads.append(init)
        return self.op('dve', lambda e: e.tensor_tensor_scan(out, d0, d1, init, op0, op1), reads=reads, writes=[out])

    def recip(self, out, in_):
        return self.op('dve', lambda e: e.reciprocal(out, in_), reads=[in_], writes=[out])


class Arena:
    def __init__(self, P, name, t, nbytes):
        self.P, self.name, self.t, self.nbytes = P, name, t, nbytes
        self.top = 0
        self.regions = []
        self.old = []
        self.nrid = 0

    def reset(self):
        self.old.extend(self.regions)
        self.regions = []
        self.top = 0

    def mark(self):
        return (self.top, len(self.regions))

    def release(self, mk):
        top, nreg = mk
        self.old.extend(self.regions[nreg:])
        self.regions = self.regions[:nreg]
        self.top = top

    def region_of(self, lo, hi):
        for (a, b, rid) in self.regions:
            if lo >= a and hi <= b:
                return rid
        raise AssertionError("arena access outside any region %s %d %d" % (self.name, lo, hi))

    def alloc(self, shape, dtype=F32):
        n = int(np.prod(shape)) * ESZ[dtype]
        n = (n + 63) // 64 * 64
        lo, hi = self.top, self.top + n
        assert hi <= self.nbytes, ("arena overflow", self.name, hi, self.nbytes)
        self.top = hi
        rid = self.nrid
        self.nrid += 1
        self.regions.append((lo, hi, rid))
        inh = []
        keep = []
        for (a, b, orid) in self.old:
            if a < hi and lo < b:
                t = self.P.trk.get((self.name, orid))
                if t:
                    inh.extend(t['w'])
                    inh.extend(t['r'])
                keep.append((a, b, orid))
            else:
                keep.append((a, b, orid))
        self.old = keep
        if inh:
            full = ((self.name, rid), 0, 128, lo, hi)
            m = Prog._merge([(full, ev, clk) for (_, ev, clk) in inh])
            self.P.trk[(self.name, rid)] = {'w': m, 'r': []}
        v = self.t[:, lo // 4:hi // 4]
        if dtype != F32:
            v = v.bitcast(dtype)
        nel = int(np.prod(shape))
        v = v[:, 0:nel]
        if len(shape) == 2:
            v = v.rearrange("p (a b) -> p a b", a=shape[0])
        elif len(shape) == 3:
            v = v.rearrange("p (a b c) -> p a b c", a=shape[0], b=shape[1])
        return v


def _launch(build, in_maps, n_cores=8, trace=False):
    nc = bass.Bass("TRN2", target_bir_lowering=False)
    with ExitStack() as stack:
        P = Prog(nc, stack)
        build(nc, P)
        P.finish()
        P.emit()
    res = run_bass_kernel_spmd(nc, in_maps, core_ids=list(range(n_cores)), trace=trace)
    if trace:
        return res
    return res.results

D = 1024
SEQ = 8192
CTXL = 256
NTOK = SEQ + CTXL
QT = 2112
EPS = 1e-6
TILES_Q = [(0, 64, 1), (64, 512, 0), (576, 512, 0), (1088, 512, 0), (1600, 512, 0)]


class Banks:
    def __init__(self, P):
        self.pp = [P.psum("pp%d" % i, [128, 1024], F32) for i in range(4)]
        self.b = [self.pp[i // 2][:, (i % 2) * 512:(i % 2 + 1) * 512] for i in range(8)]


def load_consts_tok(nc, P):
    ones = P.sbuf("ones128", [128, 128], F32)
    P.memset('pool', ones[:], 1.0)
    return {'ones': ones}


def compute_mod(nc, P, K, banks, cvec_d, wmod_d, bmod_d, fc_list, modT):
    cT = P.sbuf("cT", [128, 8, 2], F32)
    P.dma('sp', cT[:], cvec_d)
    e = P.sbuf("cTe", [128, 8, 2], F32)
    P.act(e[:], cT[:], AF.Exp, scale=-1.0)
    P.ts('dve', e[:], e[:], 1.0, None, ALU.add)
    P.recip(e[:], e[:])
    sc = P.sbuf("cTs", [128, 8, 2], F32)
    P.tt('dve', sc[:], cT[:], e[:], ALU.mult)
    bT = P.sbuf("bmodT", [128, 24], F32)
    P.dma('sp', bT[:], bmod_d)
    wbuf = [P.sbuf("wmodbuf%d" % i, [128, 8, 512], F32) for i in range(2)]
    ps = banks.b[7]
    groups = sorted(set(fc // 4 for fc in fc_list))
    for gi, g in enumerate(groups):
        wb = wbuf[gi % 2]
        for kc in range(8):
            P.dma('sp' if kc % 2 == 0 else 'act', wb[:, kc, :], wmod_d[kc * 128:(kc + 1) * 128, g * 512:(g + 1) * 512])
        for fc in range(g * 4, g * 4 + 4):
            if fc not in fc_list:
                continue
            for kc in range(8):
                P.mm(ps[:, fc * 2:fc * 2 + 2], wb[:, kc, (fc % 4) * 128:(fc % 4 + 1) * 128], sc[:, kc, :],
                     start=(kc == 0), stop=(kc == 7))
    for fc in fc_list:
        P.tt('dve', modT[:, fc, :], ps[:, fc * 2:fc * 2 + 2], bT[:, fc:fc + 1].to_broadcast([128, 2]), ALU.add)


def norm_coeffs(nc, P, modT, normw_d, name):
    nw = P.sbuf(name + "_nw", [128, 8], F32)
    P.dma('sp', nw[:], normw_d)
    A = P.sbuf(name + "_A", [128, 8, 2], F32)
    P.ts('dve', A[:], modT[:, 8:16, :], 1.0, None, ALU.add)
    P.tt('dve', A[:], A[:], nw[:].unsqueeze(2).to_broadcast([128, 8, 2]), ALU.mult)
    return A


def phase_norm(nc, P, K, banks, RT, A, Bsh, xn_d, tiles, tmp):
    xn_v = xn_d.rearrange("(kc p) t -> p kc t", p=128)
    for ti, (c0, n, isctx) in enumerate(tiles):
        j = 1 if isctx else 0
        ps = banks.b[ti % 2]
        for kc in range(8):
            sq = tmp['sq'][kc % 2]
            P.act(sq[:, 0:n], RT[:, kc, c0:c0 + n], AF.Square)
            P.mm(ps[:, 0:n], K['ones'][:], sq[:, 0:n], start=(kc == 0), stop=(kc == 7))
        rstd = tmp['rstd'][ti % 2]
        P.act(rstd[:, 0:n], ps[:, 0:n], AF.Ln, scale=1.0 / D, bias=K['eps'][:])
        P.act(rstd[:, 0:n], rstd[:, 0:n], AF.Exp, scale=-0.5)
        xb = tmp['xb'][ti % 2]
        for kc in range(8):
            t1 = tmp['t1'][kc % 2]
            P.stt(t1[:, 0:n], RT[:, kc, c0:c0 + n], A[:, kc, j:j + 1], rstd[:, 0:n], ALU.mult, ALU.mult)
            P.act(xb[:, kc, 0:n], t1[:, 0:n], AF.Identity, bias=Bsh[:, kc, j:j + 1])
        P.dma('pool', xn_v[:, :, c0:c0 + n], xb[:, :, 0:n], final=True)


def phase_final(nc, P, K, banks, RT, fnw_d, out_d, tiles, tmp):
    fw = P.sbuf("fnw_sb", [128, 8], F32)
    P.dma('sp', fw[:], fnw_d)
    out_v = out_d.rearrange("(kc p) t -> p kc t", p=128)
    for ti, (c0, n, isctx) in enumerate(tiles):
        ps = banks.b[ti % 2]
        for kc in range(8):
            sq = tmp['sq'][kc % 2]
            P.act(sq[:, 0:n], RT[:, kc, c0:c0 + n], AF.Square)
            P.mm(ps[:, 0:n], K['ones'][:], sq[:, 0:n], start=(kc == 0), stop=(kc == 7))
        rstd = tmp['rstd'][ti % 2]
        P.act(rstd[:, 0:n], ps[:, 0:n], AF.Ln, scale=1.0 / D, bias=K['eps'][:])
        P.act(rstd[:, 0:n], rstd[:, 0:n], AF.Exp, scale=-0.5)
        for kc in range(8):
            P.stt(RT[:, kc, c0:c0 + n], RT[:, kc, c0:c0 + n], fw[:, kc:kc + 1], rstd[:, 0:n], ALU.mult, ALU.mult)
        P.dma('pool', out_v[:, :, c0 - 64:c0 - 64 + n], RT[:, :, c0:c0 + n], final=True)


def phase_outproj(nc, P, K, banks, RT, yT_d, wout_d, ssdw_d, gateT, tiles, tmp):
    wo = P.sbuf("wout_bf", [128, 8, D], BF16)
    for kc in range(8):
        st = tmp['wstage'][kc % 2]
        P.dma('sp' if kc % 2 == 0 else 'act', st[:], wout_d[kc * 128:(kc + 1) * 128, :])
        P.copy('pool', wo[:, kc, :], st[:])
    sw = P.sbuf("ssdnw_sb", [128, 2], F32)
    P.dma('sp', sw[:], ssdw_d)
    yT_v = yT_d.rearrange("(kc p) t -> p kc t", p=128)
    for ti, (c0, n, isctx) in enumerate(tiles):
        j = 1 if isctx else 0
        yb = tmp['yb'][ti % 2]
        P.dma('sp', yb[:, :, 0:n], yT_v[:, :, c0:c0 + n])
        ps = banks.b[2 + ti % 2]
        for i, kc in enumerate((6, 7)):
            sq = tmp['sq'][i]
            P.act(sq[:, 0:n], yb[:, kc, 0:n], AF.Square)
            P.mm(ps[:, 0:n], K['ones'][:], sq[:, 0:n], start=(i == 0), stop=(i == 1))
        rstd = tmp['rstd'][ti % 2]
        P.act(rstd[:, 0:n], ps[:, 0:n], AF.Ln, scale=1.0 / 256, bias=K['eps'][:])
        P.act(rstd[:, 0:n], rstd[:, 0:n], AF.Exp, scale=-0.5)
        for i, kc in enumerate((6, 7)):
            P.stt(yb[:, kc, 0:n], yb[:, kc, 0:n], sw[:, i:i + 1], rstd[:, 0:n], ALU.mult, ALU.mult)
        for fc in range(8):
            po = banks.b[4 + fc % 4]
            for kc in range(8):
                P.mm(po[:, 0:n], wo[:, kc, fc * 128:(fc + 1) * 128], yb[:, kc, 0:n], start=(kc == 0), stop=(kc == 7))
            P.stt(RT[:, fc, c0:c0 + n], po[:, 0:n], gateT[:, 16 + fc, j:j + 1], RT[:, fc, c0:c0 + n], ALU.mult, ALU.add)


def alloc_tok_tmp(P):
    return {
        'sq': [P.sbuf("t_sq%d" % i, [128, 512], F32) for i in range(2)],
        'rstd': [P.sbuf("t_rstd%d" % i, [128, 512], F32) for i in range(2)],
        't1': [P.sbuf("t_t1%d" % i, [128, 512], F32) for i in range(2)],
        'xb': [P.sbuf("t_xb%d" % i, [128, 8, 512], BF16) for i in range(2)],
    }


def build_tok_kernel(nc, P, first, last, with_outproj):
    banks = Banks(P)
    K = load_consts_tok(nc, P)
    epst = P.sbuf("eps_t", [128, 1], F32)
    P.memset('pool', epst[:], EPS)
    K['eps'] = epst
    tmp = alloc_tok_tmp(P)
    RT_d = nc.dram_tensor("RT", [D, QT], F32, kind="ExternalInput").ap()
    cvec_d = nc.dram_tensor("cvec", [128, 8, 2], F32, kind="ExternalInput").ap()
    RT = P.sbuf("RT_sb", [128, 8, QT], F32)
    RT_v = RT_d.rearrange("(kc p) t -> p kc t", p=128)
    for kc in range(8):
        P.dma('sp' if kc % 2 == 0 else 'act', RT[:, kc, :], RT_v[:, kc, :])
    modT = P.sbuf("modT", [128, 24, 2], F32)
    if with_outproj:
        wmodg_d = nc.dram_tensor("wmod_g", [D, 3 * D], F32, kind="ExternalInput").ap()
        bmodg_d = nc.dram_tensor("bmod_g", [128, 24], F32, kind="ExternalInput").ap()
        yT_d = nc.dram_tensor("yT", [D, QT], BF16, kind="ExternalInput").ap()
        wout_d = nc.dram_tensor("wout", [D, D], F32, kind="ExternalInput").ap()
        ssdw_d = nc.dram_tensor("ssdnw", [128, 2], F32, kind="ExternalInput").ap()
        tmp['wstage'] = [P.sbuf("t_wst%d" % i, [128, D], F32) for i in range(2)]
        tmp['yb'] = [P.sbuf("t_yb%d" % i, [128, 8, 512], BF16) for i in range(2)]
        gateT = P.sbuf("gateT", [128, 24, 2], F32)
        compute_mod(nc, P, K, banks, cvec_d, wmodg_d, bmodg_d, list(range(16, 24)), gateT)
        tiles = TILES_Q[1:] if last else TILES_Q
        phase_outproj(nc, P, K, banks, RT, yT_d, wout_d, ssdw_d, gateT, tiles, tmp)
    if last:
        fnw_d = nc.dram_tensor("fnw", [128, 8], F32, kind="ExternalInput").ap()
        out_d = nc.dram_tensor("outT", [D, 2048], F32, kind="ExternalOutput").ap()
        phase_final(nc, P, K, banks, RT, fnw_d, out_d, TILES_Q[1:], tmp)
    else:
        wmod_d = nc.dram_tensor("wmod_n", [D, 3 * D], F32, kind="ExternalInput").ap()
        bmod_d = nc.dram_tensor("bmod_n", [128, 24], F32, kind="ExternalInput").ap()
        normw_d = nc.dram_tensor("normw", [128, 8], F32, kind="ExternalInput").ap()
        xn_d = nc.dram_tensor("xnT", [D, QT], BF16, kind="ExternalOutput").ap()
        if with_outproj:
            Rout_d = nc.dram_tensor("RTout", [D, QT], F32, kind="ExternalOutput").ap()
        modT2 = modT
        _cm_second(nc, P, K, banks, cvec_d, wmod_d, bmod_d, list(range(0, 16)), modT2, with_outproj)
        A = norm_coeffs(nc, P, modT2, normw_d, "nc")
        phase_norm(nc, P, K, banks, RT, A, modT2, xn_d, TILES_Q, tmp)
        if with_outproj:
            Ro_v = Rout_d.rearrange("(kc p) t -> p kc t", p=128)
            for kc in range(8):
                P.dma('pool', Ro_v[:, kc, :], RT[:, kc, :], final=True)


_CM_STATE = {}


def _cm_second(nc, P, K, banks, cvec_d, wmod_d, bmod_d, fc_list, modT, second):
    if not second:
        return compute_mod(nc, P, K, banks, cvec_d, wmod_d, bmod_d, fc_list, modT)
    orig = P.sbuf

    def renamed(name, shape, dtype=F32):
        return orig(name + "_2", shape, dtype)
    P.sbuf = renamed
    try:
        compute_mod(nc, P, K, banks, cvec_d, wmod_d, bmod_d, fc_list, modT)
    finally:
        P.sbuf = orig


def pk(v):
    v = np.asarray(v)
    return np.ascontiguousarray(v.reshape(-1, 128).T)

TOK_TILES = [(0, 256)] + [(256 + 512 * i, 512) for i in range(16)]
SCALE_QK = 32.0 ** -0.5
QK_REP = 1
DEFER = True


def store_y(P, io, mi, t0, n, src):
    if 'y_store' in io:
        io['y_store'](mi, t0, n, src)
    else:
        P.dma('pool', io['yT'][mi, :, t0:t0 + n], src, final=True)


def load_weights_bf16(nc, P, A, w_d, ncols, stage):
    Wsb = A.alloc([8, ncols], BF16)
    for kc in range(8):
        st = stage[kc % 2]
        P.dma('sp' if kc % 2 == 0 else 'act', st[:, 0:ncols], w_d[kc * 128:(kc + 1) * 128, :])
        P.copy('dve', Wsb[:, kc, :], st[:, 0:ncols])
    return Wsb


def inproj(nc, P, banks, xn_v, Wsb, tiles, fm_groups, tm_group, xbufs, bank_ids=(0, 1, 2, 3), pre_tile=None):
    pending = []
    for ti, (t0, n) in enumerate(tiles):
        xb = xbufs[ti % 2]
        if callable(xn_v):
            xn_v(xb, t0, n)
        else:
            P.dma('sp', xb[:, :, 0:n], xn_v[:, :, t0:t0 + n])
        if pre_tile is not None:
            pre_tile(ti, t0, n)
        new_pending = []
        for gi, (c0, M, lat_only, fn) in enumerate(fm_groups):
            if lat_only and t0 < CTXL:
                continue
            ps = banks.b[bank_ids[gi % len(bank_ids)]]
            for kc in range(8):
                P.mm(ps[0:M, 0:n], Wsb[:, kc, c0:c0 + M], xb[:, kc, 0:n], start=(kc == 0), stop=(kc == 7))
            if fn is not None:
                r = fn(ps, t0, n, ti)
                if r is not None:
                    if DEFER:
                        new_pending.append(r)
                    else:
                        r()
        if tm_group is not None:
            c0, ncols, fn, tmbanks = tm_group
            for sub in range(n // 128):
                ps = banks.b[tmbanks[sub % len(tmbanks)]]
                for kc in range(8):
                    P.mm(ps[:, 0:ncols], xb[:, kc, sub * 128:(sub + 1) * 128], Wsb[:, kc, c0:c0 + ncols],
                         start=(kc == 0), stop=(kc == 7))
                fn(ps, t0 + sub * 128)
        for r in pending:
            r2 = r()
            if r2 is not None:
                new_pending.append(r2)
        pending = new_pending
    while pending:
        nxt = []
        for r in pending:
            r2 = r()
            if r2 is not None:
                nxt.append(r2)
        pending = nxt


def silu_evac(P, out_bf, ps_ap, tmp_e, M, n):
    P.act(tmp_e[0:M, 0:n], ps_ap, AF.Tanh, scale=0.5)
    P.stt(out_bf, tmp_e[0:M, 0:n], 1.0, ps_ap, ALU.add, ALU.mult)


def attention_phase(nc, P, A, banks, K, xn_v, io, layer, with_ctx):
    A.reset()
    lam_init = 0.8 - 0.6 * float(np.exp(-0.3 * layer))
    stage = [A.alloc([640], F32) for _ in range(2)]
    Wsb = load_weights_bf16(nc, P, A, io['w_attn'], 640, stage)
    xbufs = [A.alloc([8, 512], BF16) for _ in range(2)]
    Qa = A.alloc([NTOK], BF16)
    Ka = A.alloc([NTOK], BF16)
    Vt = A.alloc([66, 128], BF16)
    sg = A.alloc([NTOK], BF16)
    cosb = [A.alloc([512], F32) for _ in range(2)]
    sinb = [A.alloc([512], F32) for _ in range(2)]
    t1 = [A.alloc([512], F32) for _ in range(2)]
    t2 = [A.alloc([512], F32) for _ in range(2)]
    te = [A.alloc([512], F32) for _ in range(2)]
    P.memset('pool', Vt[:, :, 65:128], 0.0)
    P.memset('pool', Vt[:, :, 64:65], 1.0)

    def pre_tile(ti, t0, n):
        if t0 >= CTXL:
            P.dma('act', cosb[ti % 2][:], io['cosT'][:, t0 - CTXL:t0 - CTXL + n])
            P.dma('act', sinb[ti % 2][:], io['sinT'][:, t0 - CTXL:t0 - CTXL + n])

    state = {}

    def ev_plain(dst):
        def f(ps, t0, n, ti):
            if t0 < CTXL:
                P.copy('act', dst[:, t0:t0 + n], ps[:, 0:n])
            else:
                state['ps1'] = ps
        return f

    def ev_rot(dst):
        def f(ps, t0, n, ti):
            a = t1[ti % 2]
            b = t2[ti % 2]
            P.tt('dve', a[:, 0:n], state['ps1'][:, 0:n], cosb[ti % 2][:, 0:n], ALU.mult)
            P.tt('dve', b[:, 0:n], ps[:, 0:n], sinb[ti % 2][:, 0:n], ALU.mult)
            P.tt('pool', dst[:, t0:t0 + n], a[:, 0:n], b[:, 0:n], ALU.add)
        return f

    def ev_gate(ps, t0, n, ti):
        silu_evac(P, sg[0:64, t0:t0 + n], ps[0:64, 0:n], te[ti % 2], 64, n)

    vT = [A.alloc([512], BF16) for _ in range(2)]

    def ev_vT(ps, t0, n, ti):
        v_ = vT[ti % 2]
        P.copy('act', v_[0:64, 0:n], ps[0:64, 0:n])

        def rest():
            for sub in range(n // 128):
                pt = banks.b[6 + sub % 2].bitcast(BF16)
                P.transpose(pt[:, 0:64], v_[0:64, sub * 128:(sub + 1) * 128], K['identb'][0:64, 0:64])
                P.copy('dve', Vt[:, t0 // 128 + sub, 0:64], pt[:, 0:64])
        return rest

    fm = [(0, 128, False, ev_plain(Qa)), (128, 128, True, ev_rot(Qa)),
          (256, 128, False, ev_plain(Ka)), (384, 128, True, ev_rot(Ka)),
          (512, 64, False, ev_gate), (576, 64, False, ev_vT)]
    inproj(nc, P, banks, xn_v, Wsb, TOK_TILES, fm, None, xbufs, bank_ids=(0, 1, 2, 3, 4, 5), pre_tile=pre_tile)

    lv = A.alloc([128], F32)
    P.dma('sp', lv[0:64, :], io['diff_lam'].partition_broadcast(64))
    pr = A.alloc([64], F32)
    s2 = A.alloc([2], F32)
    P.tt('dve', pr[0:64, 0:32], lv[0:64, 0:32], lv[0:64, 32:64], ALU.mult)
    P.tt('dve', pr[0:64, 32:64], lv[0:64, 64:96], lv[0:64, 96:128], ALU.mult)
    P.op('dve', lambda e: e.tensor_reduce(s2[0:64, 0:2], pr[0:64, :].rearrange("p (a b) -> p a b", a=2), AX.X, ALU.add),
         reads=[pr[0:64, :]], writes=[s2[0:64, 0:2]])
    P.act(s2[0:64, :], s2[0:64, :], AF.Exp)
    nlam = A.alloc([1], F32)
    P.tt('dve', nlam[0:64, :], s2[0:64, 1:2], s2[0:64, 0:1], ALU.subtract)
    P.ts('dve', nlam[0:64, :], nlam[0:64, :], -lam_init, None, ALU.add)
    subw = A.alloc([1], F32)
    P.dma('sp', subw[0:64, :], io['diff_subln_w'])
    P.ts('dve', subw[0:64, :], subw[0:64, :], 1.0 - lam_init, None, ALU.mult)

    Eb = [A.alloc([1024], BF16) for _ in range(4)]
    osb = [A.alloc([512], F32) for _ in range(2)]
    fz = [A.alloc([512], F32) for _ in range(4)]
    yb = [A.alloc([512], BF16) for _ in range(2)]
    qblocks = []
    if with_ctx:
        qblocks.append((0, 256, [0, 1]))
    for i in range(16):
        qblocks.append((256 + 512 * i, 512, list(range(66))))
    o_acc = [banks.b[6], banks.b[7]]
    for qi, (q0, nq, kbs) in enumerate(qblocks):
        def qk(kb):
            S = banks.pp[kb % 3]
            for rep in range(QK_REP):
                P.mm(S[:, 0:nq], Ka[0:32, kb * 128:(kb + 1) * 128], Qa[0:32, q0:q0 + nq])
                P.mm(S[:, 512:512 + nq], Ka[64:96, kb * 128:(kb + 1) * 128], Qa[64:96, q0:q0 + nq])
        for k_ in kbs[0:3]:
            qk(k_)
        for ki, kb in enumerate(kbs):
            S = banks.pp[kb % 3]
            E = Eb[ki % 4]
            if nq == 512:
                P.act(E[:, :], S[:, :], AF.Exp, scale=SCALE_QK)
            else:
                Ev = E.rearrange("p (a b) -> p a b", a=2)[:, :, 0:nq]
                Sv = S.rearrange("p (a b) -> p a b", a=2)[:, :, 0:nq]
                P.act(Ev, Sv, AF.Exp, scale=SCALE_QK)
            if ki + 3 < len(kbs):
                qk(kbs[ki + 3])
            for c in range(2):
                P.mm(o_acc[c][:, 0:nq], Vt[:, kb, :], E[:, c * 512:c * 512 + nq], start=(ki == 0), stop=(ki == len(kbs) - 1))
        for c in range(2):
            P.copy('act', osb[c][0:65, 0:nq], o_acc[c][0:65, 0:nq])
        zb = [banks.b[0], banks.b[1]]
        for c in range(2):
            P.mm(zb[c][0:64, 0:nq], K['selZ'][0:65, :], osb[c][0:65, 0:nq])
        for c in range(2):
            P.act(fz[c][0:64, 0:nq], zb[c][0:64, 0:nq], AF.Ln)
            P.act(fz[c][0:64, 0:nq], fz[c][0:64, 0:nq], AF.Exp, scale=-1.0)
            P.tt('dve', fz[c][0:64, 0:nq], fz[c][0:64, 0:nq], osb[c][0:64, 0:nq], ALU.mult)
        o = fz[2]
        P.stt(o[0:64, 0:nq], fz[1][0:64, 0:nq], nlam[0:64, 0:1], fz[0][0:64, 0:nq], ALU.mult, ALU.add)
        P.act(fz[3][0:64, 0:nq], o[0:64, 0:nq], AF.Square)
        P.mm(zb[0][0:64, 0:nq], K['ones'][0:64, 0:64], fz[3][0:64, 0:nq])
        P.act(fz[3][0:64, 0:nq], zb[0][0:64, 0:nq], AF.Ln, scale=1.0 / 64, bias=K['eps'][0:64, :])
        P.act(fz[3][0:64, 0:nq], fz[3][0:64, 0:nq], AF.Exp, scale=-0.5)
        P.stt(o[0:64, 0:nq], o[0:64, 0:nq], subw[0:64, 0:1], fz[3][0:64, 0:nq], ALU.mult, ALU.mult)
        y = yb[qi % 2]
        P.stt(y[0:64, 0:nq], o[0:64, 0:nq], 0.5, sg[0:64, q0:q0 + nq], ALU.mult, ALU.mult)
        store_y(P, io, 2, q0, nq, y[0:64, 0:nq])
        if 'after_q' in io:
            io['after_q'](q0, nq)


TP = 8456
PADC = 2
PADL = 260


def pcol(t0):
    return t0 + PADC if t0 < CTXL else t0 + (PADL - CTXL)


PTILES = [(PADC, 256)] + [(PADL + 512 * i, 512) for i in range(16)]


def conv4(P, dst, src, cw, cb, np_, c_lo, c_hi):
    n = c_hi - c_lo
    P.ts('dve', dst[0:np_, c_lo:c_hi], src[0:np_, c_lo - 2:c_hi - 2], cw[0:np_, 0:1], cb[0:np_, 0:1], ALU.mult, ALU.add)
    for k in range(1, 4):
        P.stt(dst[0:np_, c_lo:c_hi], src[0:np_, c_lo + k - 2:c_hi + k - 2], cw[0:np_, k:k + 1], dst[0:np_, c_lo:c_hi],
              ALU.mult, ALU.add)


def lru_phase(nc, P, A, banks, K, xn_v, io, layer, with_ctx):
    A.reset()
    stage = [A.alloc([192], F32) for _ in range(2)]
    Wsb = load_weights_bf16(nc, P, A, io['w_lru'], 192, stage)
    xbufs = [A.alloc([8, 512], BF16) for _ in range(2)]
    XA = A.alloc([TP], F32)
    XU = A.alloc([TP], F32)
    sg = A.alloc([NTOK], BF16)
    te = [A.alloc([512], F32) for _ in range(2)]
    par = A.alloc([16], F32)
    P.dma('sp', par[:, 0:9], io['lru_par'])
    cw, cb = par[:, 0:4], par[:, 4:5]
    nba, nbx = par[:, 9:10], par[:, 10:11]
    c8, c16 = par[:, 11:12], par[:, 12:13]
    P.ts('dve', nba, par[:, 5:6], 0.5, None, ALU.mult)
    P.ts('dve', nbx, par[:, 6:7], 0.5, None, ALU.mult)
    P.act(c8, par[:, 7:8], AF.Exp, scale=-1.0)
    P.act(c8, c8, AF.Ln, bias=K['one'][:, 0:1])
    P.ts('dve', c16, c8, -16.0, None, ALU.mult)
    P.ts('dve', c8, c8, -8.0, None, ALU.mult)
    gw32 = A.alloc([256], F32)
    P.dma('sp', gw32[0:64, :], io['lru_gw'])
    gw = A.alloc([256], BF16)
    P.copy('pool', gw[0:64, :], gw32[0:64, :])
    II = A.alloc([64], F32)
    P.copy('pool', II[0:64, :], K['ident'][0:64, 0:64])
    P.copy('pool', II[64:128, :], K['ident'][64:128, 64:128])
    P.memset('pool', XA[:, 0:PADC], 0.0)
    P.memset('pool', XA[:, PADC + CTXL:PADL], 0.0)
    P.memset('pool', XA[:, PADL + SEQ:TP], 0.0)

    def ev_x(ps, t0, n, ti):
        c = pcol(t0)
        P.copy('act', XA[:, c:c + n], ps[:, 0:n])

    def ev_gate(ps, t0, n, ti):
        silu_evac(P, sg[0:64, t0:t0 + n], ps[0:64, 0:n], te[ti % 2], 64, n)

    inproj(nc, P, banks, xn_v, Wsb, TOK_TILES, [(0, 128, False, ev_x), (128, 64, False, ev_gate)], None, xbufs,
           bank_ids=(0, 1, 2, 3))
    conv4(P, XU, XA, cw, cb, 128, PADC, PADC + CTXL)
    for i in range(4):
        conv4(P, XU, XA, cw, cb, 128, PADL + 2048 * i, PADL + 2048 * (i + 1))
    xcb = [A.alloc([512], BF16) for _ in range(2)]
    gr = [A.alloc([512], F32) for _ in range(2)]
    gi_ = [A.alloc([512], F32) for _ in range(2)]
    gs = [A.alloc([512], F32) for _ in range(2)]
    for ti, (c0, n) in enumerate(PTILES):
        xb = xcb[ti % 2]
        P.copy('pool', xb[0:64, 0:n], XU[0:64, c0:c0 + n])
        psr, psi = banks.b[(2 * ti) % 8], banks.b[(2 * ti + 1) % 8]
        P.mm(psr[:, 0:n], gw[0:64, 0:128], xb[0:64, 0:n])
        P.mm(psi[:, 0:n], gw[0:64, 128:256], xb[0:64, 0:n])
        r, ii, s = gr[ti % 2], gi_[ti % 2], gs[ti % 2]
        P.act(r[:, 0:n], psr[:, 0:n], AF.Tanh, scale=0.5, bias=nba)
        P.act(ii[:, 0:n], psi[:, 0:n], AF.Tanh, scale=0.5, bias=nbx)
        P.ts('dve', r[:, 0:n], r[:, 0:n], 0.5, 0.5, ALU.mult, ALU.add)
        P.ts('dve', ii[:, 0:n], ii[:, 0:n], 0.5, 0.5, ALU.mult, ALU.add)
        P.act(XA[:, c0:c0 + n], r[:, 0:n], AF.Exp, scale=c8)
        P.act(s[:, 0:n], r[:, 0:n], AF.Exp, scale=c16)
        P.act(s[:, 0:n], s[:, 0:n], AF.Ln, scale=-1.0, bias=K['one'][:, 0:1])
        P.act(s[:, 0:n], s[:, 0:n], AF.Exp, scale=0.5)
        P.tt('pool', ii[:, 0:n], ii[:, 0:n], s[:, 0:n], ALU.mult)
        P.tt('dve', XU[:, c0:c0 + n], XU[:, c0:c0 + n], ii[:, 0:n], ALU.mult)
    fa, fu = XA[0:64], XU[0:64]
    ba_, bu = XA[64:128], XU[64:128]
    P.scan(fu[:, PADC:PADC + CTXL], fa[:, PADC:PADC + CTXL], fu[:, PADC:PADC + CTXL], 0.0)
    for i in range(4):
        lo = PADL + 2048 * i
        init = fu[:, PADC + CTXL - 1:PADC + CTXL] if i == 0 else fu[:, lo - 1:lo]
        P.scan(fu[:, lo:lo + 2048], fa[:, lo:lo + 2048], fu[:, lo:lo + 2048], init)
    P.scan(bu[:, PADC:PADC + CTXL][:, ::-1], ba_[:, PADC:PADC + CTXL][:, ::-1], bu[:, PADC:PADC + CTXL][:, ::-1], 0.0)
    for i in range(3, -1, -1):
        lo = PADL + 2048 * i
        init = bu[:, PADC:PADC + 1] if i == 3 else bu[:, lo + 2048:lo + 2049]
        P.scan(bu[:, lo:lo + 2048][:, ::-1], ba_[:, lo:lo + 2048][:, ::-1], bu[:, lo:lo + 2048][:, ::-1], init)
    yb = [A.alloc([512], BF16) for _ in range(2)]
    for ti, (t0, n) in enumerate(TOK_TILES):
        if t0 < CTXL and not with_ctx:
            continue
        c0 = pcol(t0)
        ps = banks.b[ti % 4]
        P.mm(ps[0:64, 0:n], II[:, :], XU[:, c0:c0 + n])
        y = yb[ti % 2]
        P.stt(y[0:64, 0:n], ps[0:64, 0:n], 0.5, sg[0:64, t0:t0 + n], ALU.mult, ALU.mult)
        store_y(P, io, 1, t0, n, y[0:64, 0:n])


def gla_phase(nc, P, A, banks, K, xn_v, io, layer, with_ctx):
    A.reset()
    NC_ = 132
    QG = A.alloc([NTOK], BF16)
    KG = A.alloc([NTOK], BF16)
    KDt = A.alloc([66, 64], BF16)
    Vt = A.alloc([66, 64], BF16)
    sg = A.alloc([NTOK], BF16)
    ST = A.alloc([NC_ + 2, 64], F32)
    STb = A.alloc([NC_ + 2, 64], BF16)
    DEC = A.alloc([NC_], F32)
    par = A.alloc([8], F32)
    P.dma('sp', par[0:64, 0:2], io['gla_par'])
    nb2 = par[0:64, 2:3]
    P.ts('dve', nb2, par[0:64, 0:1], -1.0, None, ALU.mult)
    lnsc = par[0:64, 3:4]
    P.memset('dve', lnsc, float(np.log(32.0 ** -0.5)))
    w2_32 = A.alloc([64], F32)
    P.dma('sp', w2_32[0:32, :], io['gla_w2bd'])
    w2b = A.alloc([64], BF16)
    P.copy('dve', w2b[0:32, :], w2_32[0:32, :])
    mk = A.mark()
    stage = [A.alloc([320], F32) for _ in range(2)]
    Wsb = load_weights_bf16(nc, P, A, io['w_gla'], 320, stage)
    xbufs = [A.alloc([8, 512], BF16) for _ in range(2)]
    te = [A.alloc([512], F32) for _ in range(2)]
    lrb = [A.alloc([512], BF16) for _ in range(2)]
    Lb = [A.alloc([512], F32) for _ in range(2)]
    Gb = [A.alloc([512], F32) for _ in range(2)]
    Eq = [A.alloc([512], F32) for _ in range(2)]
    Ek = [A.alloc([512], F32) for _ in range(2)]
    dG = [A.alloc([512], F32) for _ in range(2)]
    kdT = [A.alloc([512], BF16) for _ in range(2)]
    P.memset('dve', ST[0:64, 0:2, :], 0.0)
    P.memset('dve', ST[0:64, NC_:NC_ + 2, :], 0.0)
    st = {}

    qs = [A.alloc([512], F32) for _ in range(2)]
    ks = [A.alloc([512], F32) for _ in range(2)]

    def ev_q(ps, t0, n, ti):
        P.copy('act', qs[ti % 2][0:64, 0:n], ps[0:64, 0:n])

    def ev_k(ps, t0, n, ti):
        P.copy('act', ks[ti % 2][0:64, 0:n], ps[0:64, 0:n])

    def ev_gate(ps, t0, n, ti):
        P.copy('dve', sg[0:64, t0:t0 + n], ps[0:64, 0:n])

    def ev_lr(ps, t0, n, ti):
        i2 = ti % 2
        nch = n // 64
        c0 = t0 // 64
        P.copy('act', lrb[i2][0:32, 0:n], ps[0:32, 0:n])

        def rest():
            pz = banks.b[7]
            P.mm(pz[0:64, 0:n], w2b[0:32, 0:64], lrb[i2][0:32, 0:n])
            L, G = Lb[i2], Gb[i2]
            P.act(L[0:64, 0:n], pz[0:64, 0:n], AF.Exp, scale=-1.0, bias=nb2)
            P.act(L[0:64, 0:n], L[0:64, 0:n], AF.Ln, bias=K['one'][0:64, 0:1])
            P.scan(G[0:32, 0:n], K['scanmask'][0:32, 0:n], L[0:32, 0:n], 0.0)
            P.scan(G[32:64, 0:n][:, ::-1], K['scanmask'][32:64, 0:n][:, ::-1], L[32:64, 0:n][:, ::-1], 0.0)
            P.act(Eq[i2][0:64, 0:n], G[0:64, 0:n], AF.Exp, scale=-1.0 / 16, bias=lnsc)
            P.act(Ek[i2][0:64, 0:n], G[0:64, 0:n], AF.Exp, scale=1.0 / 16)
            P.tt('dve', QG[0:64, t0:t0 + n], qs[i2][0:64, 0:n], Eq[i2][0:64, 0:n], ALU.mult)
            P.tt('dve', KG[0:64, t0:t0 + n], ks[i2][0:64, 0:n], Ek[i2][0:64, 0:n], ALU.mult)
            G3 = G[:, 0:n].rearrange("p (c s) -> p c s", s=64)
            d3 = dG[i2][:, 0:n].rearrange("p (c s) -> p c s", s=64)
            P.tt('dve', d3[0:32], G3[0:32, :, 63:64].to_broadcast([32, nch, 64]), G3[0:32], ALU.subtract)
            P.tt('dve', d3[32:64], G3[32:64, :, 0:1].to_broadcast([32, nch, 64]), G3[32:64], ALU.subtract)
            P.act(dG[i2][0:64, 0:n], dG[i2][0:64, 0:n], AF.Exp, scale=-1.0 / 16)
            P.tt('dve', kdT[i2][0:64, 0:n], ks[i2][0:64, 0:n], dG[i2][0:64, 0:n], ALU.mult)
            P.act(DEC[0:32, c0:c0 + nch], G3[0:32, :, 63], AF.Exp, scale=-1.0 / 16)
            P.act(DEC[32:64, c0:c0 + nch], G3[32:64, :, 0], AF.Exp, scale=-1.0 / 16)
            def rest2():
                for sub in range(n // 128):
                    pt = banks.b[5 + sub % 2].bitcast(BF16)
                    P.transpose(pt[:, 0:64], kdT[i2][0:64, sub * 128:(sub + 1) * 128], K['identb'][0:64, 0:64])
                    P.copy('dve', KDt[:, t0 // 128 + sub, :], pt[:, 0:64])
            return rest2
        return rest

    vT = [A.alloc([512], BF16) for _ in range(2)]

    def ev_vT(ps, t0, n, ti):
        v_ = vT[ti % 2]
        P.copy('act', v_[0:64, 0:n], ps[0:64, 0:n])

        for sub in range(n // 128):
            pt = banks.b[5 + sub % 2].bitcast(BF16)
            P.transpose(pt[:, 64:128], v_[0:64, sub * 128:(sub + 1) * 128], K['identb'][0:64, 0:64])
            P.copy('dve', Vt[:, t0 // 128 + sub, :], pt[:, 64:128])

    fm = [(0, 64, False, ev_q), (64, 64, False, ev_k), (192, 64, False, ev_gate), (256, 64, False, ev_vT), (128, 32, False, ev_lr)]
    inproj(nc, P, banks, xn_v, Wsb, TOK_TILES, fm, None, xbufs, bank_ids=(0, 1, 2, 3, 4))
    for ti, (t0, n) in enumerate(TOK_TILES):
        P.act(te[ti % 2][0:64, 0:n], sg[0:64, t0:t0 + n], AF.Tanh, scale=0.5)
        P.stt(sg[0:64, t0:t0 + n], te[ti % 2][0:64, 0:n], 1.0, sg[0:64, t0:t0 + n], ALU.add, ALU.mult)

    for g0 in range(0, 66, 8):
        npair = min(8, 66 - g0)
        for i in range(npair):
            p = g0 + i
            for j in range(2):
                P.mm(banks.b[j][0:64, i * 64:(i + 1) * 64], KDt[64 * j:64 * j + 64, p, :], Vt[64 * j:64 * j + 64, p, :])
        for j in range(2):
            src = banks.b[j][:, 0:npair * 64].rearrange("p (c v) -> p c v", v=64)
            c0 = 2 * g0 + j
            P.copy('act', ST[0:32, c0 + 2:c0 + 1 + 2 * npair:2, :], src[0:32])
            P.copy('dve', ST[32:64, c0:c0 + 2 * npair - 1:2, :], src[32:64])
    fsteps = [(c + 2, c + 1, c) for c in range(1, NC_)]
    bsteps = [(c, c + 1, c) for c in (2, 1, 0)] + [(131, 0, 131)] + [(c, c + 1, c) for c in range(130, 4, -1)]
    for i in range(max(len(fsteps), len(bsteps))):
        if i < len(fsteps):
            o_, i_, d_ = fsteps[i]
            P.stt(ST[0:32, o_, :], ST[0:32, i_, :], DEC[0:32, d_:d_ + 1], ST[0:32, o_, :], ALU.mult, ALU.add)
        if i < len(bsteps):
            o_, i_, d_ = bsteps[i]
            P.stt(ST[32:64, o_, :], ST[32:64, i_, :], DEC[32:64, d_:d_ + 1], ST[32:64, o_, :], ALU.mult, ALU.add)
    P.copy('dve', ST[32:64, 132, :], ST[32:64, 0, :])
    P.copy('dve', STb[0:64], ST[0:64])

    A.release(mk)
    tm = [A.alloc([512], F32) for _ in range(2)]
    At = [A.alloc([512], BF16) for _ in range(2)]
    ob = [A.alloc([512], F32) for _ in range(2)]
    yb = [A.alloc([512], BF16) for _ in range(2)]
    for ti, (t0, n) in enumerate(TOK_TILES):
        if t0 < CTXL and not with_ctx:
            continue
        i2 = ti % 2
        npair = n // 128
        X, Y, Z = banks.b[2], banks.b[3], banks.b[4 + i2]
        for i in range(npair):
            cs = slice(t0 + 128 * i, t0 + 128 * (i + 1))
            P.mm(X[:, 128 * i:128 * (i + 1)], KG[0:32, cs], QG[0:32, cs])
            P.mm(Y[:, 128 * i:128 * (i + 1)], KG[32:64, cs], QG[32:64, cs])
        P.tt('dve', tm[i2][:, 0:n], X[:, 0:n], K['maskF'][:, 0:n], ALU.mult)
        P.tt('dve', At[i2][:, 0:n], Y[:, 0:n], K['maskB'][:, 0:n], ALU.mult)
        P.tt('pool', At[i2][:, 0:n], At[i2][:, 0:n], tm[i2][:, 0:n], ALU.add)
        for i in range(npair):
            p = t0 // 128 + i
            P.mm(Z[0:64, 128 * i:128 * (i + 1)], Vt[:, p, :], At[i2][:, 128 * i:128 * (i + 1)], start=True, stop=False)
            for j in range(2):
                c = 2 * p + j
                cs = slice(t0 + 128 * i + 64 * j, t0 + 128 * i + 64 * (j + 1))
                kk = 32 if c == 3 else 64
                P.mm(Z[0:64, 128 * i + 64 * j:128 * i + 64 * (j + 1)], STb[0:kk, c + 1, :], QG[0:kk, cs],
                     start=False, stop=(j == 1))
        o = ob[i2]
        P.act(o[0:64, 0:n], Z[0:64, 0:n], AF.Square)
        zb = banks.b[6 + i2]
        P.mm(zb[0:64, 0:n], K['ones'][0:64, 0:64], o[0:64, 0:n])
        P.act(o[0:64, 0:n], zb[0:64, 0:n], AF.Ln, scale=1.0 / 64, bias=K['eps'][0:64, :])
        P.act(o[0:64, 0:n], o[0:64, 0:n], AF.Exp, scale=-0.5)
        P.stt(o[0:64, 0:n], Z[0:64, 0:n], par[0:64, 1:2], o[0:64, 0:n], ALU.mult, ALU.mult)
        y = yb[i2]
        P.stt(y[0:64, 0:n], o[0:64, 0:n], 0.5, sg[0:64, t0:t0 + n], ALU.mult, ALU.mult)
        store_y(P, io, 0, t0, n, y[0:64, 0:n])


def ssd_phase(nc, P, A, banks, K, xn_v, io, layer, with_ctx):
    A.reset()
    NC_ = 132
    XB = A.alloc([TP], BF16)
    C2 = A.alloc([TP], BF16)
    sgz = A.alloc([NTOK], BF16)
    par = A.alloc([16], F32)
    P.dma('sp', par[:, 0:5], io['ssd_par'])
    cpar = A.alloc([2, 5], F32)
    P.dma('sp', cpar[:], io['ssd_cpar'])
    na = par[:, 5:7]
    P.act(na, par[:, 2:4], AF.Exp)
    P.ts('dve', na, na, -1.0, None, ALU.mult)
    dts_d = io['ssd_scr'][0:2]
    crow_d = io['ssd_scr'][2:4]
    clr_d = io['ssd_scr'][4:6]
    mk = A.mark()
    stage = [A.alloc([322], F32) for _ in range(2)]
    Wsb = load_weights_bf16(nc, P, A, io['w_ssd'], 322, stage)
    xbufs = [A.alloc([8, 512], BF16) for _ in range(2)]
    XR1 = A.alloc([TP], F32)
    XR2 = A.alloc([TP], F32)
    XC = A.alloc([2052], F32)
    te = [A.alloc([512], F32) for _ in range(2)]
    dtt = [A.alloc([512], F32) for _ in range(2)]
    for X in (XR1, XR2):
        P.memset('pool', X[:, 0:PADC], 0.0)
        P.memset('pool', X[:, PADC + CTXL:PADL], 0.0)
        P.memset('pool', X[:, PADL + SEQ:TP], 0.0)

    def ev_raw(dst):
        def f(ps, t0, n, ti):
            c = pcol(t0)
            P.copy('act', dst[:, c:c + n], ps[:, 0:n])
        return f

    def ev_gate(ps, t0, n, ti):
        silu_evac(P, sgz[0:64, t0:t0 + n], ps[0:64, 0:n], te[ti % 2], 64, n)

    def ev_dt(ps, t0, n, ti):
        P.copy('dve', dtt[ti % 2][0:2, 0:n], ps[0:2, 0:n])
        P.dma('pool', dts_d[:, t0:t0 + n], dtt[ti % 2][0:2, 0:n])

    fm = [(0, 128, False, ev_raw(XR1)), (128, 128, False, ev_raw(XR2)), (256, 64, False, ev_gate), (320, 2, False, ev_dt)]
    inproj(nc, P, banks, xn_v, Wsb, TOK_TILES, fm, None, xbufs, bank_ids=(0, 1, 2, 3))
    pieces = [(PADC, PADC + CTXL)] + [(PADL + 2048 * i, PADL + 2048 * (i + 1)) for i in range(4)]
    for gi, (src, dst) in enumerate(((XR1, XB), (XR2, C2))):
        for (lo, hi) in pieces:
            n = hi - lo
            P.act(XC[:, 2:2 + n], src[:, lo - 2:hi - 2], AF.Identity, scale=cpar[:, gi, 0:1], bias=cpar[:, gi, 4:5])
            for k in range(1, 4):
                P.stt(XC[:, 2:2 + n], src[:, lo + k - 2:hi + k - 2], cpar[:, gi, k:k + 1], XC[:, 2:2 + n], ALU.mult, ALU.add)
            P.act(dst[:, lo:hi], XC[:, 2:2 + n], AF.Silu)
    A.release(mk)
    PM = A.alloc([2, 128], F32)
    for d in range(2):
        P.dma('sp', PM[0:66, d, :], dts_d[d].rearrange("(p s) -> p s", s=128))
    DT = A.alloc([2, 128], F32)
    for d in range(2):
        P.act(DT[0:66, d, :], PM[0:66, d, :], AF.Exp, bias=par[0:66, d:d + 1])
    P.act(DT[0:66], DT[0:66], AF.Ln, bias=K['one'][0:66, 0:1])
    DA = A.alloc([2, 128], F32)
    for d in range(2):
        P.ts('dve', DA[0:66, d, :], DT[0:66, d, :], na[0:66, d:d + 1], None, ALU.mult)
    CUM = A.alloc([2, 128], F32)
    P.scan(CUM[0:66, 0, :], K['scanmask2'][0:66, 0, :], DA[0:66, 0, :], 0.0)
    P.scan(CUM[0:66, 1, :][:, ::-1], K['scanmask2'][0:66, 1, :][:, ::-1], DA[0:66, 1, :][:, ::-1], 0.0)
    P.dma('pool', crow_d[0].rearrange("(p s) -> p s", s=128), CUM[0:66, 0, :])
    P.dma('pool', crow_d[1].rearrange("(p s) -> p s", s=128), CUM[0:66, 1, :])
    LND = A.alloc([2, 128], F32)
    P.act(LND[0:66], DT[0:66], AF.Ln)
    CQ0 = A.alloc([2, 128], F32)
    P.tt('dve', CQ0[0:66], CUM[0:66], LND[0:66], ALU.subtract)
    CL = A.alloc([2, 2], F32)
    C4 = CUM[0:66].rearrange("p d (c s) -> p d c s", s=64)
    P.copy('dve', CL[0:66, 0, :], C4[:, 0, :, 63])
    P.copy('dve', CL[0:66, 1, :], C4[:, 1, :, 0])
    P.dma('pool', clr_d[0, 0:132].rearrange("(p c) -> p c", c=2), CL[0:66, 0, :])
    P.dma('pool', clr_d[1, 0:132].rearrange("(p c) -> p c", c=2), CL[0:66, 1, :])
    W0 = A.alloc([2, 128], F32)
    W4 = W0[0:66].rearrange("p d (c s) -> p d c s", s=64)
    P.tt('dve', W4, CL[0:66].unsqueeze(3).to_broadcast([66, 2, 2, 64]), C4, ALU.subtract)
    P.act(W0[0:66], W0[0:66], AF.Exp)
    P.tt('dve', W0[0:66], W0[0:66], DT[0:66], ALU.mult)
    CQ = A.alloc([2, 66], F32)
    WT = A.alloc([2, 66], F32)
    for d in range(2):
        for (src, dst) in ((CQ0, CQ), (W0, WT)):
            pt = banks.b[(2 * d) % 8 + (0 if src is CQ0 else 1)]
            P.transpose(pt[:, 0:66], src[0:66, d, :], K['ident'][0:66, 0:66])
            P.copy('act', dst[:, d, :], pt[:, 0:66])
    clr = A.alloc([132], F32)
    P.dma('sp', clr[0:2, :], clr_d[:, 0:132])
    DEC = A.alloc([132], F32)
    pd = banks.b[4]
    P.mm(pd[:, 0:132], K['sel2'][0:2, :], clr[0:2, :])
    P.act(DEC[:, :], pd[:, 0:132], AF.Exp)
    xBt = A.alloc([66, 128], BF16)
    for p in range(66):
        c0 = pcol(128 * p)
        pt = banks.b[p % 4].bitcast(BF16)
        P.transpose(pt[:, 0:128], XB[:, c0:c0 + 128], K['identb'][:, :])
        P.copy('act' if p % 2 == 0 else 'dve', xBt[:, p, :], pt[:, 0:128])
    BW = A.alloc([66, 128], BF16)
    for d in range(2):
        P.tt('dve', BW[:, :, 64 * d:64 * d + 64], xBt[:, :, 64:128], WT[:, d, :].unsqueeze(2).to_broadcast([128, 66, 64]), ALU.mult)
    ST = A.alloc([NC_ + 2, 64], F32)
    STb = A.alloc([NC_ + 2, 64], BF16)
    P.memset('pool', ST[:, 0:2, :], 0.0)
    P.memset('pool', ST[:, NC_:NC_ + 2, :], 0.0)
    for g0 in range(0, 66, 8):
        npair = min(8, 66 - g0)
        for i in range(npair):
            p = g0 + i
            for j in range(2):
                P.mm(banks.b[j][:, i * 64:(i + 1) * 64], BW[64 * j:64 * j + 64, p, :], xBt[64 * j:64 * j + 64, p, 0:64])
        for j in range(2):
            src = banks.b[j][:, 0:npair * 64].rearrange("p (c v) -> p c v", v=64)
            c0 = 2 * g0 + j
            P.copy('act', ST[0:64, c0 + 2:c0 + 1 + 2 * npair:2, :], src[0:64])
            P.copy('dve', ST[64:128, c0:c0 + 2 * npair - 1:2, :], src[64:128])
    fsteps = [(c + 2, c + 1, c) for c in range(1, NC_)]
    bsteps = [(c, c + 1, c) for c in (2, 1, 0)] + [(131, 0, 131)] + [(c, c + 1, c) for c in range(130, 4, -1)]
    for i in range(max(len(fsteps), len(bsteps))):
        if i < len(fsteps):
            o_, i_, d_ = fsteps[i]
            P.stt(ST[0:64, o_, :], ST[0:64, i_, :], DEC[0:64, d_:d_ + 1], ST[0:64, o_, :], ALU.mult, ALU.add)
        if i < len(bsteps):
            o_, i_, d_ = bsteps[i]
            P.stt(ST[64:128, o_, :], ST[64:128, i_, :], DEC[64:128, d_:d_ + 1], ST[64:128, o_, :], ALU.mult, ALU.add)
    P.copy('dve', ST[64:128, 132, :], ST[64:128, 0, :])
    P.copy('dve', STb[:], ST[:])
    crt = [A.alloc([512], F32) for _ in range(2)]
    SG = [A.alloc([2, 512], F32) for _ in range(2)]
    Ls = [A.alloc([512], F32) for _ in range(2)]
    Mt = [A.alloc([512], BF16) for _ in range(2)]
    ec = [A.alloc([512], F32) for _ in range(2)]
    Cs = [A.alloc([512], BF16) for _ in range(2)]
    yv = [A.alloc([512], F32) for _ in range(2)]
    yb = [A.alloc([512], BF16) for _ in range(2)]
    for ti, (t0, n) in enumerate(TOK_TILES):
        if t0 < CTXL and not with_ctx:
            continue
        i2 = ti % 2
        npair = n // 128
        p0 = t0 // 128
        pc0 = pcol(t0)
        cr = crt[i2]
        P.dma('sp', cr[0:2, 0:n], crow_d[:, t0:t0 + n])
        pe_ = banks.b[0]
        P.mm(pe_[:, 0:n], K['sel2'][0:2, :], cr[0:2, 0:n])
        P.act(ec[i2][:, 0:n], pe_[:, 0:n], AF.Exp)
        P.tt('dve', Cs[i2][:, 0:n], C2[:, pc0:pc0 + n], ec[i2][:, 0:n], ALU.mult)
        for d in range(2):
            pb = banks.b[1 + d]
            P.mm(pb[:, 0:n], K['selrow'][0:2, d, :], cr[0:2, 0:n], start=True, stop=False)
            P.mm(pb[:, 0:n], K['identb'][:, :], K['negF' if d == 0 else 'negB'][:, 0:n], start=False, stop=True)
            P.tt('dve', SG[i2][:, d, 0:n].rearrange("p (a b) -> p a b", b=128),
                 pb[:, 0:n].rearrange("p (a b) -> p a b", b=128),
                 CQ[:, d, p0:p0 + npair].unsqueeze(2).to_broadcast([128, npair, 128]), ALU.subtract)
        P.act(SG[i2][:, :, 0:n], SG[i2][:, :, 0:n], AF.Exp)
        P.tt('pool', Ls[i2][:, 0:n], SG[i2][:, 0, 0:n], SG[i2][:, 1, 0:n], ALU.add)
        pcb = banks.b[3]
        for i in range(npair):
            cs = slice(pc0 + 128 * i, pc0 + 128 * (i + 1))
            P.mm(pcb[:, 128 * i:128 * (i + 1)], XB[64:128, cs], C2[64:128, cs])
        P.tt('dve', Mt[i2][:, 0:n], pcb[:, 0:n], Ls[i2][:, 0:n], ALU.mult)
        Y = banks.b[4 + i2]
        for i in range(npair):
            p = p0 + i
            P.mm(Y[0:64, 128 * i:128 * (i + 1)], xBt[:, p, 0:64], Mt[i2][:, 128 * i:128 * (i + 1)], start=True, stop=False)
            for j in range(2):
                c = 2 * p + j
                kk = 64 if c == 3 else 128
                P.mm(Y[0:64, 128 * i + 64 * j:128 * i + 64 * (j + 1)], STb[0:kk, c + 1, :],
                     Cs[i2][0:kk, 128 * i + 64 * j:128 * i + 64 * (j + 1)], start=False, stop=(j == 1))
        P.stt(yv[i2][0:64, 0:n], XB[0:64, pc0:pc0 + n], par[0:64, 4:5], Y[0:64, 0:n], ALU.mult, ALU.add)
        y = yb[i2]
        P.stt(y[0:64, 0:n], yv[i2][0:64, 0:n], 0.5, sgz[0:64, t0:t0 + n], ALU.mult, ALU.mult)
        store_y(P, io, 3, t0, n, y[0:64, 0:n])

OFF_A, OFF_B, OFF_C, OFF_D = 0, 800, 1312, 2336


def rope_tables():
    n_freq = 8
    inv = (np.float32(10000.0) ** (-(np.arange(n_freq, dtype=np.float32)) / np.float32(n_freq))).astype(np.float32)
    t = np.arange(SEQ)
    pos_r = (t // 64).astype(np.float32)
    pos_c = (t % 64).astype(np.float32)
    ang_r = pos_r[:, None] * inv
    ang_c = pos_c[:, None] * inv
    ang = np.concatenate([ang_r, ang_r, ang_c, ang_c], axis=-1).astype(np.float32)
    cos = np.cos(ang).astype(np.float32).T
    sin = np.sin(ang).astype(np.float32).T
    sign = np.ones(32, np.float32)
    for a in range(2):
        sign[a * 16:a * 16 + 8] = -1.0
    sins = sin * sign[:, None]
    cosT = np.zeros((128, SEQ), np.float32)
    sinT = np.zeros((128, SEQ), np.float32)
    for c in range(2):
        cosT[64 * c:64 * c + 32] = cos
        sinT[64 * c:64 * c + 32] = sins
    return cosT, sinT


def rot_perm():
    perm = np.zeros(32, np.int64)
    for a in range(2):
        for f in range(8):
            perm[a * 16 + f] = a * 16 + 8 + f
            perm[a * 16 + 8 + f] = a * 16 + f
    return perm


def prep_w_attn(w_in_l, h):
    W = np.zeros((D, 640), np.float32)
    perm = rot_perm()
    for c in range(2):
        qc = OFF_C + h * 64 + c * 32
        kc = OFF_C + 256 + h * 64 + c * 32
        W[:, 64 * c:64 * c + 32] = w_in_l[:, qc:qc + 32]
        W[:, 128 + 64 * c:128 + 64 * c + 32] = w_in_l[:, qc + perm]
        W[:, 256 + 64 * c:256 + 64 * c + 32] = w_in_l[:, kc:kc + 32]
        W[:, 384 + 64 * c:384 + 64 * c + 32] = w_in_l[:, kc + perm]
    W[:, 512:576] = w_in_l[:, OFF_C + 768 + h * 64:OFF_C + 768 + h * 64 + 64]
    W[:, 576:640] = w_in_l[:, OFF_C + 512 + h * 64:OFF_C + 512 + h * 64 + 64]
    return W


def mix_consts():
    selZ = np.zeros((128, 64), np.float32)
    selZ[64, :] = 1.0
    t = np.arange(512)
    scanmask = np.zeros((64, 512), np.float32)
    scanmask[0:32] = (t % 64 != 0).astype(np.float32)[None]
    scanmask[32:64] = (t % 64 != 63).astype(np.float32)[None]
    j = np.arange(128)[:, None]
    i = np.arange(128)[None, :]
    same = (j // 64) == (i // 64)
    mF = (same & (j <= i)).astype(np.float32)
    mB = (same & (j >= i)).astype(np.float32)
    scanmask2 = np.zeros((128, 2, 128), np.float32)
    s_ = np.arange(128)
    scanmask2[:, 0, :] = (s_ % 64 != 0).astype(np.float32)[None]
    scanmask2[:, 1, :] = (s_ % 64 != 63).astype(np.float32)[None]
    sel2 = np.zeros((2, 128), np.float32)
    sel2[0, 0:64] = 1.0
    sel2[1, 64:128] = 1.0
    selrow = np.zeros((2, 2, 128), np.float32)
    selrow[0, 0, :] = 1.0
    selrow[1, 1, :] = 1.0
    NEG = -30000.0
    negF = np.where(mF > 0, 0.0, NEG).astype(np.float32)
    negB = np.where(mB > 0, 0.0, NEG).astype(np.float32)
    return {'selZ': selZ, 'ident': np.eye(128, dtype=np.float32), 'scanmask': scanmask,
            'maskF': np.tile(mF, (1, 4)), 'maskB': np.tile(mB, (1, 4)), 'scanmask2': scanmask2,
            'sel2': sel2, 'selrow': selrow, 'negF': np.tile(negF, (1, 4)), 'negB': np.tile(negB, (1, 4))}


def build_mix_kernel(nc, P, layer, with_ctx, mixers):
    banks = Banks(P)
    K = load_consts_tok(nc, P)
    epst = P.sbuf("eps_t", [128, 1], F32)
    P.memset('pool', epst[:], EPS)
    K['eps'] = epst
    selZ_d = nc.dram_tensor("selZ", [128, 64], F32, kind="ExternalInput").ap()
    selZ = P.sbuf("selZ_sb", [128, 64], F32)
    P.dma('sp', selZ[:], selZ_d)
    K['selZ'] = selZ
    ident_d = nc.dram_tensor("ident", [128, 128], F32, kind="ExternalInput").ap()
    ident = P.sbuf("ident_sb", [128, 128], F32)
    P.dma('sp', ident[:], ident_d)
    identb = P.sbuf("identb_sb", [128, 128], BF16)
    P.copy('pool', identb[:], ident[:])
    K['ident'] = ident
    K['identb'] = identb
    one = P.sbuf("one_t", [128, 1], F32)
    P.memset('pool', one[:], 1.0)
    K['one'] = one
    for nm, shp in (('scanmask', [64, 512]), ('maskF', [128, 512]), ('maskB', [128, 512]),
                    ('scanmask2', [128, 2, 128]), ('sel2', [2, 128]), ('selrow', [2, 2, 128])):
        d_ = nc.dram_tensor(nm, shp, F32, kind="ExternalInput").ap()
        t_ = P.sbuf(nm + "_sb", shp, F32)
        P.dma('sp', t_[:], d_)
        K[nm] = t_
    for nm in ('negF', 'negB'):
        d_ = nc.dram_tensor(nm, [128, 512], F32, kind="ExternalInput").ap()
        t_ = P.sbuf(nm + "_f", [128, 512], F32)
        P.dma('sp', t_[:], d_)
        tb_ = P.sbuf(nm + "_sb", [128, 512], BF16)
        P.copy('pool', tb_[:], t_[:])
        K[nm] = tb_
    io = {}
    xn_d = nc.dram_tensor("xn", [D, NTOK], BF16, kind="ExternalInput").ap()
    xn_v = xn_d.rearrange("(kc p) t -> p kc t", p=128)
    io['yT'] = nc.dram_tensor("yT", [4, 64, NTOK], BF16, kind="ExternalOutput").ap()
    A = P.arena("arena", 190 * 1024)
    if 'attn' in mixers:
        io['w_attn'] = nc.dram_tensor("w_attn", [D, 640], F32, kind="ExternalInput").ap()
        io['cosT'] = nc.dram_tensor("cosT", [128, SEQ], F32, kind="ExternalInput").ap()
        io['sinT'] = nc.dram_tensor("sinT", [128, SEQ], F32, kind="ExternalInput").ap()
        io['diff_lam'] = nc.dram_tensor("diff_lam", [128], F32, kind="ExternalInput").ap()
        io['diff_subln_w'] = nc.dram_tensor("diff_subln_w", [64, 1], F32, kind="ExternalInput").ap()
        attention_phase(nc, P, A, banks, K, xn_v, io, layer, with_ctx)
    if 'gla' in mixers:
        _add_gla(nc, P, A, banks, K, xn_v, io, layer, with_ctx)
    if 'ssd' in mixers:
        io['w_ssd'] = nc.dram_tensor("w_ssd", [D, 322], F32, kind="ExternalInput").ap()
        io['ssd_par'] = nc.dram_tensor("ssd_par", [128, 5], F32, kind="ExternalInput").ap()
        io['ssd_cpar'] = nc.dram_tensor("ssd_cpar", [128, 2, 5], F32, kind="ExternalInput").ap()
        io['ssd_scr'] = nc.dram_tensor("ssd_scr", [6, NTOK], F32, kind="Internal").ap()
        ssd_phase(nc, P, A, banks, K, xn_v, io, layer, with_ctx)
    if 'lru' in mixers:
        io['w_lru'] = nc.dram_tensor("w_lru", [D, 192], F32, kind="ExternalInput").ap()
        io['lru_par'] = nc.dram_tensor("lru_par", [128, 9], F32, kind="ExternalInput").ap()
        io['lru_gw'] = nc.dram_tensor("lru_gw", [64, 256], F32, kind="ExternalInput").ap()
        lru_phase(nc, P, A, banks, K, xn_v, io, layer, with_ctx)


def _add_gla(nc, P, A, banks, K, xn_v, io, layer, with_ctx):
    io['w_gla'] = nc.dram_tensor("w_gla", [D, 320], F32, kind="ExternalInput").ap()
    io['gla_par'] = nc.dram_tensor("gla_par", [64, 2], F32, kind="ExternalInput").ap()
    io['gla_w2bd'] = nc.dram_tensor("gla_w2bd", [32, 64], F32, kind="ExternalInput").ap()
    gla_phase(nc, P, A, banks, K, xn_v, io, layer, with_ctx)


def prep_gla(inp, l, h):
    w_in_l = inp['w_in'][l]
    W = np.zeros((D, 288), np.float32)
    q = w_in_l[:, OFF_A + h * 32:OFF_A + h * 32 + 32]
    k = w_in_l[:, OFF_A + 128 + h * 32:OFF_A + 128 + h * 32 + 32]
    W[:, 0:32] = q
    W[:, 32:64] = q
    W[:, 64:96] = k
    W[:, 96:128] = k
    W[:, 128:144] = w_in_l[:, OFF_A + 512:OFF_A + 528]
    W[:, 144:160] = w_in_l[:, OFF_A + 528:OFF_A + 544]
    W[:, 192:256] = w_in_l[:, OFF_A + 544 + h * 64:OFF_A + 544 + h * 64 + 64]
    W[:, 256:288] = 0
    Wv = w_in_l[:, OFF_A + 256 + h * 64:OFF_A + 256 + h * 64 + 64]
    W2 = np.zeros((D, 320), np.float32)
    W2[:, 0:256] = W[:, 0:256]
    W2[:, 256:320] = Wv
    par = np.zeros((64, 2), np.float32)
    par[0:32, 0] = inp['gla_b2'][l][0][h * 32:(h + 1) * 32]
    par[32:64, 0] = inp['gla_b2'][l][1][h * 32:(h + 1) * 32]
    par[:, 1] = inp['gla_norm_w'][l]
    w2bd = np.zeros((32, 64), np.float32)
    w2bd[0:16, 0:32] = inp['gla_w2'][l][0][:, h * 32:(h + 1) * 32]
    w2bd[16:32, 32:64] = inp['gla_w2'][l][1][:, h * 32:(h + 1) * 32]
    return {"w_gla": W2, "gla_par": par, "gla_w2bd": w2bd}


def prep_ssd(inp, l, h):
    w_in_l = inp['w_in'][l]
    gr = h // 2
    W = np.zeros((D, 322), np.float32)
    cx = slice(OFF_D + h * 64, OFF_D + h * 64 + 64)
    cB = slice(OFF_D + 256 + gr * 64, OFF_D + 256 + gr * 64 + 64)
    cC = slice(OFF_D + 384 + gr * 64, OFF_D + 384 + gr * 64 + 64)
    W[:, 0:64] = w_in_l[:, cx]
    W[:, 64:128] = w_in_l[:, cB]
    W[:, 128:192] = w_in_l[:, cC]
    W[:, 192:256] = w_in_l[:, cC]
    W[:, 256:320] = w_in_l[:, OFF_D + 520 + h * 64:OFF_D + 520 + h * 64 + 64]
    W[:, 320] = w_in_l[:, OFF_D + 512 + h]
    W[:, 321] = w_in_l[:, OFF_D + 516 + h]
    par = np.zeros((128, 5), np.float32)
    par[:, 0] = inp['ssd_dt_bias'][l][0][h]
    par[:, 1] = inp['ssd_dt_bias'][l][1][h]
    par[:, 2] = inp['ssd_a_log'][l][0][h]
    par[:, 3] = inp['ssd_a_log'][l][1][h]
    par[:, 4] = inp['ssd_d'][l][h]
    cw, cb = inp['ssd_conv_w'][l], inp['ssd_conv_b'][l]
    cpar = np.zeros((128, 2, 5), np.float32)
    ch = [np.r_[h * 64:h * 64 + 64, 256 + gr * 64:256 + gr * 64 + 64],
          np.r_[384 + gr * 64:384 + gr * 64 + 64, 384 + gr * 64:384 + gr * 64 + 64]]
    for g in range(2):
        cpar[:, g, 0:4] = cw[:, ch[g]].T
        cpar[:, g, 4] = cb[ch[g]]
    return {"w_ssd": W, "ssd_par": par, "ssd_cpar": cpar}


def prep_lru(inp, l, h):
    w_in_l = inp['w_in'][l]
    W = np.zeros((D, 192), np.float32)
    xs = w_in_l[:, OFF_B + h * 64:OFF_B + h * 64 + 64]
    W[:, 0:64] = xs
    W[:, 64:128] = xs
    W[:, 128:192] = w_in_l[:, OFF_B + 256 + h * 64:OFF_B + 256 + h * 64 + 64]
    sl = slice(h * 64, h * 64 + 64)
    par = np.zeros((128, 9), np.float32)
    for d in range(2):
        rows = slice(64 * d, 64 * d + 64)
        par[rows, 0:4] = inp['lru_conv_w'][l][:, sl].T
        par[rows, 4] = inp['lru_conv_b'][l][sl]
        par[rows, 5] = inp['lru_ba'][l][d][sl]
        par[rows, 6] = inp['lru_bx'][l][d][sl]
        par[rows, 7] = inp['lru_lam'][l][d][sl]
    gw = np.zeros((64, 256), np.float32)
    for d in range(2):
        gw[:, 64 * d:64 * d + 64] = inp['lru_wa'][l][d][h]
        gw[:, 128 + 64 * d:128 + 64 * d + 64] = inp['lru_wx'][l][d][h]
    return {"w_lru": W, "lru_par": par, "lru_gw": gw}


def mix_inputs(inp, l, h, mixers, cosT, sinT, cst):
    m = dict(cst)
    if 'ssd' in mixers:
        m.update(prep_ssd(inp, l, h))
    if 'gla' in mixers:
        m.update(prep_gla(inp, l, h))
    if 'attn' in mixers:
        m.update({"w_attn": prep_w_attn(inp['w_in'][l], h), "cosT": cosT, "sinT": sinT,
                  "diff_lam": np.ascontiguousarray(inp['diff_lam'][l].reshape(-1)),
                  "diff_subln_w": np.ascontiguousarray(inp['diff_subln_w'][l].reshape(64, 1))})
    if 'lru' in mixers:
        m.update(prep_lru(inp, l, h))
    return m


def _rt_of(hl, hc, b, q):
    return np.ascontiguousarray(np.concatenate([hc[b, 64 * q:64 * q + 64].T, hl[b, 2048 * q:2048 * (q + 1)].T], axis=1))


def _gather_cols(parts):
    return np.ascontiguousarray(np.concatenate([p[:, 0:64] for p in parts] + [p[:, 64:] for p in parts], axis=1))


def kernel_unfused(**inp):
    inp = {k: np.asarray(v) for k, v in inp.items()}
    x, ctx, c, c_ctx = inp['x'], inp['ctx'], inp['c'], inp['c_ctx']
    cosT, sinT = rope_tables()
    cst = mix_consts()
    cvecs = [np.ascontiguousarray(np.stack([pk(c[b]), pk(c_ctx)], axis=2)) for b in range(2)]
    maps = []
    for core in range(8):
        b, q = core // 4, core % 4
        maps.append({"RT": _rt_of(x, ctx, b, q), "cvec": cvecs[b], "wmod_n": inp['w_mod'][0],
                     "bmod_n": pk(inp['b_mod'][0]), "normw": pk(inp['norm_w'][0])})
    res = _launch(lambda nc, P: build_tok_kernel(nc, P, True, False, False), maps)
    RT = [m["RT"] for m in maps]
    xnT = [np.asarray(r["xnT"]) for r in res]
    out = None
    for l in range(2):
        last = (l == 1)
        xn_full = [_gather_cols(xnT[4 * b:4 * b + 4]) for b in range(2)]
        maps = []
        for core in range(8):
            b, h = core // 4, core % 4
            m = {"xn": xn_full[b]}
            m.update(mix_inputs(inp, l, h, ('gla', 'lru', 'attn', 'ssd'), cosT, sinT, cst))
            maps.append(m)
        res = _launch(lambda nc, P: build_mix_kernel(nc, P, l, not last, ('gla', 'lru', 'attn', 'ssd')), maps)
        yT = [np.asarray(r["yT"]) for r in res]
        maps = []
        for core in range(8):
            b, q = core // 4, core % 4
            yb = np.stack([yT[4 * b + h] for h in range(4)], axis=1).reshape(1024, NTOK)
            yq = np.ascontiguousarray(np.concatenate([yb[:, 64 * q:64 * q + 64],
                                                      yb[:, CTXL + 2048 * q:CTXL + 2048 * (q + 1)]], axis=1))
            m = {"RT": RT[core], "cvec": cvecs[b], "yT": yq, "wout": inp['w_out'][l], "ssdnw": pk(inp['ssd_norm_w'][l]),
                 "wmod_g": inp['w_mod'][l], "bmod_g": pk(inp['b_mod'][l])}
            if last:
                m["fnw"] = pk(inp['final_norm_w'])
            else:
                m.update({"wmod_n": inp['w_mod'][l + 1], "bmod_n": pk(inp['b_mod'][l + 1]), "normw": pk(inp['norm_w'][l + 1])})
            maps.append(m)
        res = _launch(lambda nc, P: build_tok_kernel(nc, P, False, last, True), maps)
        if last:
            out = np.zeros((2, SEQ, D), np.float32)
            for core in range(8):
                b, q = core // 4, core % 4
                out[b, 2048 * q:2048 * (q + 1), :] = np.asarray(res[core]["outT"]).T
        else:
            RT = [np.asarray(r["RTout"]) for r in res]
            xnT = [np.asarray(r["xnT"]) for r in res]
    return out

CHUNKS = [(0, 256)] + [(256 + 2048 * k, 2048) for k in range(4)]
GROUPS = [[0, 1, 2, 3], [4, 5, 6, 7]]


def fs_mod(nc, P, K, banks, A, cvec_d, wmod_d, bmod_d, name):
    mk_ = A.mark()
    cT = A.alloc([8, 2], F32)
    P.dma('sp', cT[:], cvec_d)
    e = A.alloc([8, 2], F32)
    P.act(e[:], cT[:], AF.Exp, scale=-1.0)
    P.ts('dve', e[:], e[:], 1.0, None, ALU.add)
    P.recip(e[:], e[:])
    sc = A.alloc([8, 2], F32)
    P.tt('dve', sc[:], cT[:], e[:], ALU.mult)
    bT = A.alloc([6], F32)
    P.dma('sp', bT[:], bmod_d)
    wb = A.alloc([8, 768], F32)
    for kc in range(8):
        P.dma('sp' if kc % 2 == 0 else 'act', wb[:, kc, :], wmod_d[kc * 128:(kc + 1) * 128, :])
    ps = banks.b[7]
    for fc in range(6):
        for kc in range(8):
            P.mm(ps[:, fc * 2:fc * 2 + 2], wb[:, kc, fc * 128:(fc + 1) * 128], sc[:, kc, :], start=(kc == 0), stop=(kc == 7))
    modT = P.sbuf(name, [128, 6, 2], F32)
    P.tt('dve', modT[:], ps[:, 0:12].rearrange("p (a b) -> p a b", b=2), bT[:].unsqueeze(2).to_broadcast([128, 6, 2]), ALU.add)
    A.release(mk_)
    return modT


def fs_token_phase(nc, P, K, banks, A, io, L, mode):
    A.reset()
    D_ = io['dram']
    first, last = mode == 'first', mode == 'last'
    if not first:
        gateT = fs_mod(nc, P, K, banks, A, io['cvec'], io['wmod'][L], io['bmod'][L], "modT_g%d" % L)
        wo = A.alloc([8, 256], BF16)
        mk2_ = A.mark()
        wo32 = A.alloc([8, 256], F32)
        for kc in range(8):
            P.dma('sp' if kc % 2 == 0 else 'act', wo32[:, kc, :], io['wout'][L][kc * 128:(kc + 1) * 128, :])
        P.copy('dve', wo[:], wo32[:])
        A.release(mk2_)
        sw = A.alloc([4], F32)
        P.dma('sp', sw[:], io['ssdnw'][L])
    if not last:
        LN = 0 if first else L + 1
        modN = fs_mod(nc, P, K, banks, A, io['cvec'], io['wmod'][LN], io['bmod'][LN], "modT_n%d" % LN)
        nw = A.alloc([2], F32)
        P.dma('sp', nw[:], io['normw'][LN])
        Acoef = A.alloc([2, 2], F32)
        P.ts('dve', Acoef[:], modN[:, 2:4, :], 1.0, None, ALU.add)
        P.tt('dve', Acoef[:], Acoef[:], nw[:].unsqueeze(2).to_broadcast([128, 2, 2]), ALU.mult)
    else:
        fw = A.alloc([2], F32)
        P.dma('sp', fw[:], io['fnw'])
    Rb = [A.alloc([2, 2048], F32) for _ in range(2)]
    yb_ = [A.alloc([8, 512], BF16) for _ in range(2)]
    sq = [A.alloc([512], BF16) for _ in range(4)]
    rst = [A.alloc([512], F32) for _ in range(2)]
    rst2 = [A.alloc([512], F32) for _ in range(2)]
    t1 = [A.alloc([512], F32) for _ in range(4)]
    ssr = [A.alloc([2048], F32) for _ in range(2)]
    ssg = [A.alloc([2048], F32) for _ in range(2)]
    xo = [A.alloc([2, 512], BF16) for _ in range(2)]
    Rsrc = io['RT_in'] if first else D_['Rs']
    Rsrc_v = Rsrc.rearrange("(fc p) t -> p fc t", p=128)
    Rs_v = D_['Rs'].rearrange("(fc p) t -> p fc t", p=128)
    stage = 'n%d' % (0 if first else L + 1) if not last else 'fin'
    chunks = [(ci, t0, n) for ci, (t0, n) in enumerate(CHUNKS) if not (last and ci == 0)]
    Rb2 = [A.alloc([2, 2048], F32) for _ in range(2)]

    def S2(chunk):
        ci, t0, n = chunk
        isctx = 1 if ci == 0 else 0
        R = Rb2[ci % 2]
        P.dma('sp', R[:, :, 0:n], (Rs_v if not last else Rs_v)[:, :, t0:t0 + n])
        subs = [(s0, min(512, n - s0)) for s0 in range(0, n, 512)]
        sg_ = ssg[ci % 2]
        P.dma('act', sg_[0:4, 0:n], D_['ssg_' + stage][ci])
        for si, (s0, m) in enumerate(subs):
            pt = banks.b[6 + si % 2]
            P.mm(pt[:, 0:m], K['ones'][0:4, :], sg_[0:4, s0:s0 + m])
            r_ = rst2[si % 2]
            P.act(r_[:, 0:m], pt[:, 0:m], AF.Ln, scale=1.0 / D, bias=K['eps'][:])
            P.act(r_[:, 0:m], r_[:, 0:m], AF.Exp, scale=-0.5)
            if not last:
                x_ = xo[si % 2]
                for fc in range(2):
                    t_ = t1[fc]
                    P.stt(t_[:, 0:m], R[:, fc, s0:s0 + m], Acoef[:, fc, isctx:isctx + 1], r_[:, 0:m], ALU.mult, ALU.mult)
                    P.act(x_[:, fc, 0:m], t_[:, 0:m], AF.Identity, bias=modN[:, fc, isctx:isctx + 1])
                xb_v = D_['xnb_' + stage][ci].rearrange("(fc p) t -> p fc t", p=128)
                P.dma('sp', xb_v[:, :, s0:s0 + m], x_[:, :, 0:m])
            else:
                o_v = io['outT'].rearrange("(fc p) t -> p fc t", p=128)
                for fc in range(2):
                    t_ = t1[(si * 2 + fc) % 4]
                    P.stt(t_[:, 0:m], R[:, fc, s0:s0 + m], fw[:, fc:fc + 1], r_[:, 0:m], ALU.mult, ALU.mult)
                    P.dma('sp', o_v[:, fc, t0 - CTXL + s0:t0 - CTXL + s0 + m], t_[:, 0:m], final=True)
        if not last:
            P.collective("AllGather", [D_['xnb_' + stage][ci]], [D_['xng_' + stage][ci]], GROUPS)

    jobs = []
    for (ci, t0, n) in chunks:
        subs = [(s0, min(512, n - s0)) for s0 in range(0, n, 512)]
        for si, (s0, m_) in enumerate(subs):
            jobs.append(dict(ci=ci, t0=t0, n=n, s0=s0, m=m_, first=(si == 0), lastsub=(si == len(subs) - 1)))
    ybuf3 = yb_ + [A.alloc([8, 512], BF16)]

    def P1(j, ji):
        ci, t0, n, s0, m = j['ci'], j['t0'], j['n'], j['s0'], j['m']
        R = Rb[ci % 2]
        if j['first']:
            P.dma('sp', R[:, :, 0:n], Rsrc_v[:, :, t0:t0 + n])
        if first:
            return
        gy = D_['gy%d' % L][ci].rearrange("(kc p) t -> p kc t", p=128)
        yt = ybuf3[ji % 3]
        P.dma('act', yt[:, :, 0:m], gy[:, :, s0:s0 + m])
        ps = banks.b[0 + ji % 2]
        for i, kc in enumerate((1, 3, 5, 7)):
            s_ = sq[i % 2]
            P.act(s_[64:128, 0:m], yt[64:128, kc, 0:m], AF.Square)
            P.mm(ps[:, 0:m], K['onesb'][64:128, :], s_[64:128, 0:m], start=(i == 0), stop=(i == 3))
        r_ = rst[ji % 2]
        P.act(r_[:, 0:m], ps[:, 0:m], AF.Ln, scale=1.0 / 256, bias=K['eps'][:])
        P.act(r_[:, 0:m], r_[:, 0:m], AF.Exp, scale=-0.5)
        for i, kc in enumerate((1, 3, 5, 7)):
            P.stt(yt[64:128, kc, 0:m], yt[64:128, kc, 0:m], sw[64:128, i:i + 1], r_[64:128, 0:m], ALU.mult, ALU.mult)

    def P2(j, ji):
        if first:
            return
        ci, s0, m = j['ci'], j['s0'], j['m']
        isctx = 1 if ci == 0 else 0
        R = Rb[ci % 2]
        yt = ybuf3[ji % 3]
        for fc in range(2):
            po = banks.b[2 + (2 * ji + fc) % 4]
            for kc in range(8):
                P.mm(po[:, 0:m], wo[:, kc, fc * 128:(fc + 1) * 128], yt[:, kc, 0:m], start=(kc == 0), stop=(kc == 7))
            P.stt(R[:, fc, s0:s0 + m], po[:, 0:m], gateT[:, 4 + fc, isctx:isctx + 1], R[:, fc, s0:s0 + m], ALU.mult, ALU.add)

    def P3(j, ji):
        ci, t0, n, s0, m = j['ci'], j['t0'], j['n'], j['s0'], j['m']
        R = Rb[ci % 2]
        srow = ssr[ci % 2]
        pss = banks.b[6 + ji % 2]
        for fc in range(2):
            s_ = sq[2 + fc]
            P.act(s_[:, 0:m], R[:, fc, s0:s0 + m], AF.Square)
            P.mm(pss[:, 0:m], K['onesb'][:, :], s_[:, 0:m], start=(fc == 0), stop=(fc == 1))
        P.copy('dve', srow[0:1, s0:s0 + m], pss[0:1, 0:m])
        if j['lastsub']:
            P.dma('sp', D_['ssb_' + stage][ci], srow[0:1, 0:n])
            P.collective("AllGather", [D_['ssb_' + stage][ci]], [D_['ssg_' + stage][ci]], GROUPS)
            P.dma('sp', Rs_v[:, :, t0:t0 + n], R[:, :, 0:n])
            k_ = [c[0] for c in chunks].index(ci)
            if k_ >= 1:
                S2(chunks[k_ - 1])

    nj = len(jobs)
    for it in range(nj + 2):
        if it < nj:
            P1(jobs[it], it)
        if 0 <= it - 1 < nj:
            P2(jobs[it - 1], it - 1)
        if 0 <= it - 2 < nj:
            P3(jobs[it - 2], it - 2)
    S2(chunks[-1])


def build_fused(nc, P):
    banks = Banks(P)
    K = load_consts_tok(nc, P)
    epst = P.sbuf("eps_t", [128, 1], F32)
    P.memset('pool', epst[:], EPS)
    K['eps'] = epst
    one = P.sbuf("one_t", [128, 1], F32)
    P.memset('pool', one[:], 1.0)
    K['one'] = one
    for nm, shp in (('selZ', [128, 64]), ('ident', [128, 128]), ('scanmask', [64, 512]), ('maskF', [128, 512]),
                    ('maskB', [128, 512]), ('scanmask2', [128, 2, 128]), ('sel2', [2, 128]), ('selrow', [2, 2, 128])):
        d_ = nc.dram_tensor(nm, shp, F32, kind="ExternalInput").ap()
        t_ = P.sbuf(nm + "_sb", shp, F32)
        P.dma('sp', t_[:], d_)
        K[nm] = t_
    identb = P.sbuf("identb_sb", [128, 128], BF16)
    P.copy('pool', identb[:], K['ident'][:])
    K['identb'] = identb
    onesb = P.sbuf("onesb_sb", [128, 128], BF16)
    P.memset('pool', onesb[:], 1.0)
    K['onesb'] = onesb
    for nm in ('negF', 'negB'):
        d_ = nc.dram_tensor(nm, [128, 512], F32, kind="ExternalInput").ap()
        t_ = P.sbuf(nm + "_f", [128, 512], F32)
        P.dma('sp', t_[:], d_)
        tb_ = P.sbuf(nm + "_sb", [128, 512], BF16)
        P.copy('pool', tb_[:], t_[:])
        K[nm] = tb_
    A = P.arena("arena", 184 * 1024)

    def din(name, shape, dt=F32):
        return nc.dram_tensor(name, list(shape), dt, kind="ExternalInput").ap()

    def dscr(name, shape, dt=F32):
        return nc.dram_tensor(name, list(shape), dt).ap()

    io = {'RT_in': din("RT", [256, NTOK]), 'cvec': din("cvec", [128, 8, 2]),
          'wmod': [din("wmod%d" % l, [D, 768]) for l in range(2)], 'bmod': [din("bmod%d" % l, [128, 6]) for l in range(2)],
          'normw': [din("normw%d" % l, [128, 2]) for l in range(2)], 'wout': [din("wout%d" % l, [D, 256]) for l in range(2)],
          'ssdnw': [din("ssdnw%d" % l, [128, 4]) for l in range(2)], 'fnw': din("fnw", [128, 2]),
          'outT': nc.dram_tensor("outT", [256, SEQ], F32, kind="ExternalOutput").ap()}
    Dm = {'Rs': dscr("Rs", [256, NTOK])}
    for stage in ('n0', 'n1', 'fin'):
        Dm['ssb_' + stage] = [dscr("ssb_%s_%d" % (stage, ci), [1, n]) for ci, (t0, n) in enumerate(CHUNKS)]
        Dm['ssg_' + stage] = [dscr("ssg_%s_%d" % (stage, ci), [4, n]) for ci, (t0, n) in enumerate(CHUNKS)]
    for stage in ('n0', 'n1'):
        Dm['xnb_' + stage] = [dscr("xnb_%s_%d" % (stage, ci), [256, n], BF16) for ci, (t0, n) in enumerate(CHUNKS)]
        Dm['xng_' + stage] = [dscr("xng_%s_%d" % (stage, ci), [D, n], BF16) for ci, (t0, n) in enumerate(CHUNKS)]
    for l in range(2):
        Dm['yb%d' % l] = [dscr("yb%d_%d" % (l, ci), [256, n], BF16) for ci, (t0, n) in enumerate(CHUNKS)]
        Dm['gy%d' % l] = [dscr("gy%d_%d" % (l, ci), [D, n], BF16) for ci, (t0, n) in enumerate(CHUNKS)]
    io['dram'] = Dm
    cosT = din("cosT", [128, SEQ])
    sinT = din("sinT", [128, SEQ])
    fs_token_phase(nc, P, K, banks, A, io, 0, 'first')
    for l in range(2):
        last = (l == 1)
        with_ctx = not last
        xng = Dm['xng_n%d' % l]

        def xn_load(xb, t0, n, xng=xng):
            if t0 < CTXL:
                P.dma('sp', xb[:, :, 0:n], xng[0].rearrange("(kc p) t -> p kc t", p=128)[:, :, t0:t0 + n])
            else:
                k = (t0 - CTXL) // 2048
                c0 = (t0 - CTXL) % 2048
                P.dma('sp', xb[:, :, 0:n], xng[k + 1].rearrange("(kc p) t -> p kc t", p=128)[:, :, c0:c0 + n])

        ybl = Dm['yb%d' % l]

        def y_store(mi, t0, n, src, ybl=ybl):
            if t0 < CTXL:
                P.dma('pool', ybl[0][mi * 64:(mi + 1) * 64, t0:t0 + n], src)
            else:
                k = (t0 - CTXL) // 2048
                c0 = (t0 - CTXL) % 2048
                P.dma('pool', ybl[k + 1][mi * 64:(mi + 1) * 64, c0:c0 + n], src)

        mio = {'y_store': y_store, 'cosT': cosT, 'sinT': sinT}
        mio['w_gla'] = din("w_gla%d" % l, [D, 320])
        mio['gla_par'] = din("gla_par%d" % l, [64, 2])
        mio['gla_w2bd'] = din("gla_w2bd%d" % l, [32, 64])
        mio['w_lru'] = din("w_lru%d" % l, [D, 192])
        mio['lru_par'] = din("lru_par%d" % l, [128, 9])
        mio['lru_gw'] = din("lru_gw%d" % l, [64, 256])
        mio['w_ssd'] = din("w_ssd%d" % l, [D, 322])
        mio['ssd_par'] = din("ssd_par%d" % l, [128, 5])
        mio['ssd_cpar'] = din("ssd_cpar%d" % l, [128, 2, 5])
        mio['ssd_scr'] = dscr("ssd_scr%d" % l, [6, NTOK])
        mio['w_attn'] = din("w_attn%d" % l, [D, 640])
        mio['diff_lam'] = din("diff_lam%d" % l, [128])
        mio['diff_subln_w'] = din("diff_subln_w%d" % l, [64, 1])
        gla_phase(nc, P, A, banks, K, xn_load, mio, l, with_ctx)
        lru_phase(nc, P, A, banks, K, xn_load, mio, l, with_ctx)
        ssd_phase(nc, P, A, banks, K, xn_load, mio, l, with_ctx)
        def after_q(q0, nq, l=l):
            if q0 < CTXL:
                P.collective("AllGather", [Dm['yb%d' % l][0]], [Dm['gy%d' % l][0]], GROUPS)
            elif (q0 - CTXL + nq) % 2048 == 0:
                k = (q0 - CTXL) // 2048
                P.collective("AllGather", [Dm['yb%d' % l][k + 1]], [Dm['gy%d' % l][k + 1]], GROUPS)
        mio['after_q'] = after_q
        attention_phase(nc, P, A, banks, K, xn_load, mio, l, with_ctx)
        fs_token_phase(nc, P, K, banks, A, io, l, 'last' if last else 'mid')


def fused_inputs(inp, core, cosT, sinT, cst):
    b, q = core // 4, core % 4
    h = q
    x, ctx, c, c_ctx = inp['x'], inp['ctx'], inp['c'], inp['c_ctx']
    fsl = slice(256 * q, 256 * q + 256)
    m = dict(cst)
    m['RT'] = np.ascontiguousarray(np.concatenate([ctx[b][:, fsl].T, x[b][:, fsl].T], axis=1))
    m['cvec'] = np.ascontiguousarray(np.stack([pk(c[b]), pk(c_ctx)], axis=2))
    m['cosT'] = cosT
    m['sinT'] = sinT
    m['fnw'] = pk(inp['final_norm_w'][fsl])
    perm = np.array([mi * 256 + hh * 64 + j for hh in range(4) for mi in range(4) for j in range(64)])
    for l in range(2):
        cols = np.r_[256 * q:256 * q + 256, 1024 + 256 * q:1024 + 256 * q + 256, 2048 + 256 * q:2048 + 256 * q + 256]
        m['wmod%d' % l] = np.ascontiguousarray(inp['w_mod'][l][:, cols])
        m['bmod%d' % l] = pk(inp['b_mod'][l][cols])
        m['normw%d' % l] = pk(inp['norm_w'][l][fsl])
        m['wout%d' % l] = np.ascontiguousarray(inp['w_out'][l][perm][:, fsl])
        sw = np.zeros((128, 4), np.float32)
        for i in range(4):
            sw[64:128, i] = inp['ssd_norm_w'][l][i * 64:(i + 1) * 64]
        m['ssdnw%d' % l] = sw
        mi_ = mix_inputs(inp, l, h, ('gla', 'lru', 'attn', 'ssd'), cosT, sinT, cst)
        for k_, v_ in mi_.items():
            if k_ in cst or k_ in ('cosT', 'sinT'):
                continue
            m[k_ + str(l)] = v_
    return m


def kernel(**inp):
    inp = {k: np.asarray(v) for k, v in inp.items()}
    cosT, sinT = rope_tables()
    cst = mix_consts()
    maps = [fused_inputs(inp, core, cosT, sinT, cst) for core in range(8)]
    res = _launch(build_fused, maps)
    out = np.zeros((2, SEQ, D), np.float32)
    for core in range(8):
        b, q = core // 4, core % 4
        out[b, :, 256 * q:256 * q + 256] = np.asarray(res[core]["outT"]).T
    return out
```

```python
import numpy as np
from contextlib import ExitStack
import concourse.bass as bass
import concourse.mybir as mybir
from concourse.bass_utils import run_bass_kernel_spmd

F32 = mybir.dt.float32
BF16 = mybir.dt.bfloat16
AF = mybir.ActivationFunctionType
ALU = mybir.AluOpType
AX = mybir.AxisListType
ENGS = ('pe', 'act', 'dve', 'pool', 'sp')
ESZ = {F32: 4, BF16: 2}


class Prog:
    def __init__(self, nc, stack, n_dma_sems=40):
        self.nc = nc
        self.stack = stack
        self.sem = {e: stack.enter_context(nc.semaphore('sem_' + e)) for e in ENGS}
        self.cnt = {e: 0 for e in ENGS}
        self.known = {e: {} for e in ENGS}
        self.stream = {e: [] for e in ENGS}
        self.dsem = [stack.enter_context(nc.semaphore('dsem%d' % i)) for i in range(n_dma_sems)]
        self.dcum = [0] * n_dma_sems
        self.drr = 0
        self.rows = {}
        self.trk = {}
        self.final_events = []
        self.nwaits = 0
        self.ndma = 0
        self.arenas = {}

    def sbuf(self, name, shape, dtype=F32):
        t = self.stack.enter_context(self.nc.sbuf_tensor(name, list(shape), dtype))
        self.rows[name] = int(np.prod(shape[1:])) * ESZ[dtype]
        return t

    def psum(self, name, shape, dtype=F32):
        t = self.stack.enter_context(self.nc.psum_tensor(name, list(shape), dtype))
        self.rows[name] = int(np.prod(shape[1:])) * ESZ[dtype]
        return t

    def arena(self, name, nbytes):
        t = self.sbuf(name, [128, nbytes // 4], F32)
        a = Arena(self, name, t, nbytes)
        self.arenas[name] = a
        return a

    def _box(self, ap):
        b = self._box0(ap)
        a = self.arenas.get(b[0])
        if a is not None:
            rid = a.region_of(b[3], b[4])
            return ((b[0], rid),) + b[1:]
        return b

    def _box0(self, ap):
        name = ap.tensor.name
        es = ESZ.get(ap.dtype, 4)
        off = int(ap.offset) * es
        pairs = ap.ap
        if name in self.rows:
            rs = self.rows[name]
            p0 = off // rs
            pst, pc = pairs[0]
            p1 = p0 + (pc if pst != 0 else 1)
            fo = off % rs
            rest = pairs[1:]
        else:
            p0, p1 = 0, 1
            fo = off
            rest = pairs
        lo = fo
        hi = fo
        for st, c in rest:
            d = st * (c - 1) * es
            if d < 0:
                lo += d
            else:
                hi += d
        return (name, p0, p1, lo, hi + es)

    @staticmethod
    def _ov(a, b):
        return a[1] < b[2] and b[1] < a[2] and a[3] < b[4] and b[3] < a[4]

    @staticmethod
    def _inside(a, b):
        return a[1] >= b[1] and a[2] <= b[2] and a[3] >= b[3] and a[4] <= b[4]

    def _deps(self, reads, writes):
        deps = []
        for ap in reads:
            b = self._box(ap)
            t = self.trk.get(b[0])
            if t:
                for (wb, ev, clk) in t['w']:
                    if self._ov(b, wb):
                        deps.append((ev, clk, 'raw'))
        for ap in writes:
            b = self._box(ap)
            t = self.trk.get(b[0])
            if t:
                for (wb, ev, clk) in t['w']:
                    if self._ov(b, wb):
                        deps.append((ev, clk, 'waw'))
                for (rb, ev, clk) in t['r']:
                    if self._ov(b, rb):
                        deps.append((ev, clk, 'war'))
        return deps

    @staticmethod
    def _merge(lst):
        merged = {}
        for (rb, rev, rclk) in lst:
            m = merged.get(rev[0])
            if m is None:
                merged[rev[0]] = (rb, rev, rclk)
            else:
                mb, mev, mclk = m
                nb = (rb[0], min(rb[1], mb[1]), max(rb[2], mb[2]), min(rb[3], mb[3]), max(rb[4], mb[4]))
                merged[rev[0]] = (nb, rev, rclk) if rev[1] > mev[1] else (nb, mev, mclk)
        return list(merged.values())

    def _record(self, reads, writes, ev, clk):
        for ap in reads:
            b = self._box(ap)
            t = self.trk.setdefault(b[0], {'w': [], 'r': []})
            rl = t['r']
            for i, (rb, rev, rclk) in enumerate(rl):
                if rev[0] == ev[0] and self._inside(rb, b):
                    rl[i] = (b, ev, clk)
                    break
            else:
                rl.append((b, ev, clk))
                if len(rl) > 40:
                    t['r'] = self._merge(rl)
        for ap in writes:
            b = self._box(ap)
            t = self.trk.setdefault(b[0], {'w': [], 'r': []})
            t['w'] = [e for e in t['w'] if not self._inside(e[0], b)]
            t['r'] = [e for e in t['r'] if not self._inside(e[0], b)]
            t['w'].append((b, ev, clk))
            if len(t['w']) > 40:
                t['w'] = self._merge(t['w'])

    def _waits_for(self, eng, deps):
        kn = self.known[eng]
        waits = {}
        for (ev, clk, kind) in deps:
            key, val = ev
            if key == eng:
                if eng == 'pe':
                    continue
                if eng in ('act', 'dve') and kind != 'raw':
                    continue
            if kn.get(key, 0) >= val:
                continue
            if waits.get(key, 0) < val:
                waits[key] = val
        for (ev, clk, kind) in deps:
            key, val = ev
            if key in waits and waits[key] >= val:
                for k2, v2 in clk.items():
                    if kn.get(k2, 0) < v2:
                        kn[k2] = v2
        for k, v in waits.items():
            if kn.get(k, 0) < v:
                kn[k] = v
        return list(waits.items())

    def op(self, eng, fn, reads=(), writes=()):
        deps = self._deps(reads, writes)
        waits = self._waits_for(eng, deps)
        self.cnt[eng] += 1
        ev = (eng, self.cnt[eng])
        clk = dict(self.known[eng])
        clk[eng] = self.cnt[eng]
        self.stream[eng].append((waits, fn, ('e', eng)))
        self._record(reads, writes, ev, clk)
        self.nwaits += len(waits)
        return ev

    def dma(self, q, out, in_, final=False, **kw):
        i = self.drr
        self.drr = (self.drr + 1) % len(self.dsem)
        deps = self._deps([in_], [out])
        key = ('d', i)
        deps.append(((key, self.dcum[i]), {}, 'raw'))
        waits = self._waits_for(q, deps)
        self.dcum[i] += 16
        ev = (key, self.dcum[i])
        clk = dict(self.known[q])
        clk[key] = self.dcum[i]
        self.stream[q].append((waits, (lambda e, out=out, in_=in_, kw=kw: e.dma_start(out=out, in_=in_, **kw)), ('d', i)))
        self._record([in_], [out], ev, clk)
        if final:
            self.final_events.append(ev)
        self.nwaits += len(waits)
        self.ndma += 1
        return ev

    def collective(self, kind, ins, outs, groups, q='pool'):
        if not hasattr(self, 'csem'):
            self.csem = []
            self.ccum = []
        self.csem.append(self.stack.enter_context(self.nc.semaphore('csem%d' % len(self.csem))))
        self.ccum.append(0)
        i = len(self.csem) - 1
        deps = self._deps(list(ins), list(outs))
        key = ('c', i)
        waits = self._waits_for(q, deps)
        self.ccum[i] += 1
        ev = (key, self.ccum[i])
        clk = dict(self.known[q])
        clk[key] = self.ccum[i]
        self.stream[q].append((waits, (lambda e: e.collective_compute(kind, ALU.bypass, replica_groups=groups,
                                                                      ins=[a.opt() for a in ins], outs=[a.opt() for a in outs])),
                               ('c', i)))
        self._record(list(ins), list(outs), ev, clk)
        self.nwaits += len(waits)
        return ev

    def finish(self, eng='sp'):
        self.stream[eng].append((list(self.final_events), None, None))

    def _semobj(self, key):
        if isinstance(key, tuple):
            return self.dsem[key[1]] if key[0] == 'd' else self.csem[key[1]]
        return self.sem[key]

    def emit(self):
        for e in ENGS:
            assert self.cnt[e] < 60000, (e, self.cnt[e])

        def replay(name, eng):
            for (waits, fn, inc) in self.stream[name]:
                for (key, val) in waits:
                    eng.wait_ge(self._semobj(key), val)
                if fn is None:
                    continue
                ins = fn(eng)
                if inc[0] == 'e':
                    ins.then_inc(self.sem[inc[1]], 1)
                elif inc[0] == 'c':
                    ins.then_inc(self.csem[inc[1]])
                else:
                    ins.then_inc(self.dsem[inc[1]], 16)

        with self.nc.Block() as block:
            @block.sync
            def _(e):
                replay('sp', e)

            @block.scalar
            def _(e):
                replay('act', e)

            @block.vector
            def _(e):
                replay('dve', e)

            @block.gpsimd
            def _(e):
                replay('pool', e)

            @block.tensor
            def _(e):
                replay('pe', e)

    def mm(self, out, lhsT, rhs, start=True, stop=True):
        return self.op('pe', lambda e: e.matmul(out, lhsT, rhs, start=start, stop=stop),
                       reads=[lhsT, rhs], writes=[out])

    def transpose(self, out, in_, ident):
        return self.op('pe', lambda e: e.transpose(out, in_, ident), reads=[in_, ident], writes=[out])

    def act(self, out, in_, func, bias=None, scale=None, accum_out=None):
        reads = [in_]
        kw = {}
        if bias is not None:
            kw['bias'] = bias
            if not isinstance(bias, (int, float)):
                reads.append(bias)
        if scale is not None:
            kw['scale'] = scale
            if not isinstance(scale, (int, float)):
                reads.append(scale)
        writes = [out]
        if accum_out is not None:
            kw['accum_out'] = accum_out
            writes.append(accum_out)
        return self.op('act', lambda e: e.activation(out, in_, func, **kw), reads=reads, writes=writes)

    def tt(self, eng, out, in0, in1, op):
        return self.op(eng, lambda e: e.tensor_tensor(out, in0, in1, op), reads=[in0, in1], writes=[out])

    def ts(self, eng, out, in0, s1, s2, op0, op1=None):
        reads = [in0]
        for s in (s1, s2):
            if s is not None and not isinstance(s, (int, float)):
                reads.append(s)
        if op1 is None:
            return self.op(eng, lambda e: e.tensor_scalar(out, in0, s1, None, op0), reads=reads, writes=[out])
        return self.op(eng, lambda e: e.tensor_scalar(out, in0, s1, s2, op0, op1), reads=reads, writes=[out])

    def stt(self, out, in0, scalar, in1, op0, op1):
        reads = [in0, in1]
        if not isinstance(scalar, (int, float)):
            reads.append(scalar)
        return self.op('dve', lambda e: e.scalar_tensor_tensor(out, in0, scalar, in1, op0, op1),
                       reads=reads, writes=[out])

    def copy(self, eng, out, in_):
        if eng == 'act':
            return self.op('act', lambda e: e.copy(out, in_), reads=[in_], writes=[out])
        return self.op(eng, lambda e: e.tensor_copy(out, in_), reads=[in_], writes=[out])

    def memset(self, eng, ap, val):
        return self.op(eng, lambda e: e.memset(ap, val), reads=[], writes=[ap])

    def scan(self, out, d0, d1, init, op0=ALU.mult, op1=ALU.add):
        reads = [d0, d1]
        if not isinstance(init, (int, float)):
            reads.append(init)
        return self.op('dve', lambda e: e.tensor_tensor_scan(out, d0, d1, init, op0, op1), reads=reads, writes=[out])

    def recip(self, out, in_):
        return self.op('dve', lambda e: e.reciprocal(out, in_), reads=[in_], writes=[out])


class Arena:
    def __init__(self, P, name, t, nbytes):
        self.P, self.name, self.t, self.nbytes = P, name, t, nbytes
        self.top = 0
        self.regions = []
        self.old = []
        self.nrid = 0

    def reset(self):
        self.old.extend(self.regions)
        self.regions = []
        self.top = 0

    def mark(self):
        return (self.top, len(self.regions))

    def release(self, mk):
        top, nreg = mk
        self.old.extend(self.regions[nreg:])
        self.regions = self.regions[:nreg]
        self.top = top

    def region_of(self, lo, hi):
        for (a, b, rid) in self.regions:
            if lo >= a and hi <= b:
                return rid
        raise AssertionError("arena access outside any region %s %d %d" % (self.name, lo, hi))

    def alloc(self, shape, dtype=F32):
        n = int(np.prod(shape)) * ESZ[dtype]
        n = (n + 63) // 64 * 64
        lo, hi = self.top, self.top + n
        assert hi <= self.nbytes, ("arena overflow", self.name, hi, self.nbytes)
        self.top = hi
        rid = self.nrid
        self.nrid += 1
        self.regions.append((lo, hi, rid))
        inh = []
        keep = []
        for (a, b, orid) in self.old:
            if a < hi and lo < b:
                t = self.P.trk.get((self.name, orid))
                if t:
                    inh.extend(t['w'])
                    inh.extend(t['r'])
                keep.append((a, b, orid))
            else:
                keep.append((a, b, orid))
        self.old = keep
        if inh:
            full = ((self.name, rid), 0, 128, lo, hi)
            m = Prog._merge([(full, ev, clk) for (_, ev, clk) in inh])
            self.P.trk[(self.name, rid)] = {'w': m, 'r': []}
        v = self.t[:, lo // 4:hi // 4]
        if dtype != F32:
            v = v.bitcast(dtype)
        nel = int(np.prod(shape))
        v = v[:, 0:nel]
        if len(shape) == 2:
            v = v.rearrange("p (a b) -> p a b", a=shape[0])
        elif len(shape) == 3:
            v = v.rearrange("p (a b c) -> p a b c", a=shape[0], b=shape[1])
        return v


def _launch(build, in_maps, n_cores=8, trace=False):
    nc = bass.Bass("TRN2", target_bir_lowering=False)
    with ExitStack() as stack:
        P = Prog(nc, stack)
        build(nc, P)
        P.finish()
        P.emit()
    res = run_bass_kernel_spmd(nc, in_maps, core_ids=list(range(n_cores)), trace=trace)
    if trace:
        return res
    return res.results

D = 1024
SEQ = 8192
CTXL = 256
NTOK = SEQ + CTXL
QT = 2112
EPS = 1e-6
TILES_Q = [(0, 64, 1), (64, 512, 0), (576, 512, 0), (1088, 512, 0), (1600, 512, 0)]


class Banks:
    def __init__(self, P):
        self.pp = [P.psum("pp%d" % i, [128, 1024], F32) for i in range(4)]
        self.b = [self.pp[i // 2][:, (i % 2) * 512:(i % 2 + 1) * 512] for i in range(8)]


def load_consts_tok(nc, P):
    ones = P.sbuf("ones128", [128, 128], F32)
    P.memset('pool', ones[:], 1.0)
    return {'ones': ones}


def compute_mod(nc, P, K, banks, cvec_d, wmod_d, bmod_d, fc_list, modT):
    cT = P.sbuf("cT", [128, 8, 2], F32)
    P.dma('sp', cT[:], cvec_d)
    e = P.sbuf("cTe", [128, 8, 2], F32)
    P.act(e[:], cT[:], AF.Exp, scale=-1.0)
    P.ts('dve', e[:], e[:], 1.0, None, ALU.add)
    P.recip(e[:], e[:])
    sc = P.sbuf("cTs", [128, 8, 2], F32)
    P.tt('dve', sc[:], cT[:], e[:], ALU.mult)
    bT = P.sbuf("bmodT", [128, 24], F32)
    P.dma('sp', bT[:], bmod_d)
    wbuf = [P.sbuf("wmodbuf%d" % i, [128, 8, 512], F32) for i in range(2)]
    ps = banks.b[7]
    groups = sorted(set(fc // 4 for fc in fc_list))
    for gi, g in enumerate(groups):
        wb = wbuf[gi % 2]
        for kc in range(8):
            P.dma('sp' if kc % 2 == 0 else 'act', wb[:, kc, :], wmod_d[kc * 128:(kc + 1) * 128, g * 512:(g + 1) * 512])
        for fc in range(g * 4, g * 4 + 4):
            if fc not in fc_list:
                continue
            for kc in range(8):
                P.mm(ps[:, fc * 2:fc * 2 + 2], wb[:, kc, (fc % 4) * 128:(fc % 4 + 1) * 128], sc[:, kc, :],
                     start=(kc == 0), stop=(kc == 7))
    for fc in fc_list:
        P.tt('dve', modT[:, fc, :], ps[:, fc * 2:fc * 2 + 2], bT[:, fc:fc + 1].to_broadcast([128, 2]), ALU.add)


def norm_coeffs(nc, P, modT, normw_d, name):
    nw = P.sbuf(name + "_nw", [128, 8], F32)
    P.dma('sp', nw[:], normw_d)
    A = P.sbuf(name + "_A", [128, 8, 2], F32)
    P.ts('dve', A[:], modT[:, 8:16, :], 1.0, None, ALU.add)
    P.tt('dve', A[:], A[:], nw[:].unsqueeze(2).to_broadcast([128, 8, 2]), ALU.mult)
    return A


def phase_norm(nc, P, K, banks, RT, A, Bsh, xn_d, tiles, tmp):
    xn_v = xn_d.rearrange("(kc p) t -> p kc t", p=128)
    for ti, (c0, n, isctx) in enumerate(tiles):
        j = 1 if isctx else 0
        ps = banks.b[ti % 2]
        for kc in range(8):
            sq = tmp['sq'][kc % 2]
            P.act(sq[:, 0:n], RT[:, kc, c0:c0 + n], AF.Square)
            P.mm(ps[:, 0:n], K['ones'][:], sq[:, 0:n], start=(kc == 0), stop=(kc == 7))
        rstd = tmp['rstd'][ti % 2]
        P.act(rstd[:, 0:n], ps[:, 0:n], AF.Ln, scale=1.0 / D, bias=K['eps'][:])
        P.act(rstd[:, 0:n], rstd[:, 0:n], AF.Exp, scale=-0.5)
        xb = tmp['xb'][ti % 2]
        for kc in range(8):
            t1 = tmp['t1'][kc % 2]
            P.stt(t1[:, 0:n], RT[:, kc, c0:c0 + n], A[:, kc, j:j + 1], rstd[:, 0:n], ALU.mult, ALU.mult)
            P.act(xb[:, kc, 0:n], t1[:, 0:n], AF.Identity, bias=Bsh[:, kc, j:j + 1])
        P.dma('pool', xn_v[:, :, c0:c0 + n], xb[:, :, 0:n], final=True)


def phase_final(nc, P, K, banks, RT, fnw_d, out_d, tiles, tmp):
    fw = P.sbuf("fnw_sb", [128, 8], F32)
    P.dma('sp', fw[:], fnw_d)
    out_v = out_d.rearrange("(kc p) t -> p kc t", p=128)
    for ti, (c0, n, isctx) in enumerate(tiles):
        ps = banks.b[ti % 2]
        for kc in range(8):
            sq = tmp['sq'][kc % 2]
            P.act(sq[:, 0:n], RT[:, kc, c0:c0 + n], AF.Square)
            P.mm(ps[:, 0:n], K['ones'][:], sq[:, 0:n], start=(kc == 0), stop=(kc == 7))
        rstd = tmp['rstd'][ti % 2]
        P.act(rstd[:, 0:n], ps[:, 0:n], AF.Ln, scale=1.0 / D, bias=K['eps'][:])
        P.act(rstd[:, 0:n], rstd[:, 0:n], AF.Exp, scale=-0.5)
        for kc in range(8):
            P.stt(RT[:, kc, c0:c0 + n], RT[:, kc, c0:c0 + n], fw[:, kc:kc + 1], rstd[:, 0:n], ALU.mult, ALU.mult)
        P.dma('pool', out_v[:, :, c0 - 64:c0 - 64 + n], RT[:, :, c0:c0 + n], final=True)


def phase_outproj(nc, P, K, banks, RT, yT_d, wout_d, ssdw_d, gateT, tiles, tmp):
    wo = P.sbuf("wout_bf", [128, 8, D], BF16)
    for kc in range(8):
        st = tmp['wstage'][kc % 2]
        P.dma('sp' if kc % 2 == 0 else 'act', st[:], wout_d[kc * 128:(kc + 1) * 128, :])
        P.copy('pool', wo[:, kc, :], st[:])
    sw = P.sbuf("ssdnw_sb", [128, 2], F32)
    P.dma('sp', sw[:], ssdw_d)
    yT_v = yT_d.rearrange("(kc p) t -> p kc t", p=128)
    for ti, (c0, n, isctx) in enumerate(tiles):
        j = 1 if isctx else 0
        yb = tmp['yb'][ti % 2]
        P.dma('sp', yb[:, :, 0:n], yT_v[:, :, c0:c0 + n])
        ps = banks.b[2 + ti % 2]
        for i, kc in enumerate((6, 7)):
            sq = tmp['sq'][i]
            P.act(sq[:, 0:n], yb[:, kc, 0:n], AF.Square)
            P.mm(ps[:, 0:n], K['ones'][:], sq[:, 0:n], start=(i == 0), stop=(i == 1))
        rstd = tmp['rstd'][ti % 2]
        P.act(rstd[:, 0:n], ps[:, 0:n], AF.Ln, scale=1.0 / 256, bias=K['eps'][:])
        P.act(rstd[:, 0:n], rstd[:, 0:n], AF.Exp, scale=-0.5)
        for i, kc in enumerate((6, 7)):
            P.stt(yb[:, kc, 0:n], yb[:, kc, 0:n], sw[:, i:i + 1], rstd[:, 0:n], ALU.mult, ALU.mult)
        for fc in range(8):
            po = banks.b[4 + fc % 4]
            for kc in range(8):
                P.mm(po[:, 0:n], wo[:, kc, fc * 128:(fc + 1) * 128], yb[:, kc, 0:n], start=(kc == 0), stop=(kc == 7))
            P.stt(RT[:, fc, c0:c0 + n], po[:, 0:n], gateT[:, 16 + fc, j:j + 1], RT[:, fc, c0:c0 + n], ALU.mult, ALU.add)


def alloc_tok_tmp(P):
    return {
        'sq': [P.sbuf("t_sq%d" % i, [128, 512], F32) for i in range(2)],
        'rstd': [P.sbuf("t_rstd%d" % i, [128, 512], F32) for i in range(2)],
        't1': [P.sbuf("t_t1%d" % i, [128, 512], F32) for i in range(2)],
        'xb': [P.sbuf("t_xb%d" % i, [128, 8, 512], BF16) for i in range(2)],
    }


def build_tok_kernel(nc, P, first, last, with_outproj):
    banks = Banks(P)
    K = load_consts_tok(nc, P)
    epst = P.sbuf("eps_t", [128, 1], F32)
    P.memset('pool', epst[:], EPS)
    K['eps'] = epst
    tmp = alloc_tok_tmp(P)
    RT_d = nc.dram_tensor("RT", [D, QT], F32, kind="ExternalInput").ap()
    cvec_d = nc.dram_tensor("cvec", [128, 8, 2], F32, kind="ExternalInput").ap()
    RT = P.sbuf("RT_sb", [128, 8, QT], F32)
    RT_v = RT_d.rearrange("(kc p) t -> p kc t", p=128)
    for kc in range(8):
        P.dma('sp' if kc % 2 == 0 else 'act', RT[:, kc, :], RT_v[:, kc, :])
    modT = P.sbuf("modT", [128, 24, 2], F32)
    if with_outproj:
        wmodg_d = nc.dram_tensor("wmod_g", [D, 3 * D], F32, kind="ExternalInput").ap()
        bmodg_d = nc.dram_tensor("bmod_g", [128, 24], F32, kind="ExternalInput").ap()
        yT_d = nc.dram_tensor("yT", [D, QT], BF16, kind="ExternalInput").ap()
        wout_d = nc.dram_tensor("wout", [D, D], F32, kind="ExternalInput").ap()
        ssdw_d = nc.dram_tensor("ssdnw", [128, 2], F32, kind="ExternalInput").ap()
        tmp['wstage'] = [P.sbuf("t_wst%d" % i, [128, D], F32) for i in range(2)]
        tmp['yb'] = [P.sbuf("t_yb%d" % i, [128, 8, 512], BF16) for i in range(2)]
        gateT = P.sbuf("gateT", [128, 24, 2], F32)
        compute_mod(nc, P, K, banks, cvec_d, wmodg_d, bmodg_d, list(range(16, 24)), gateT)
        tiles = TILES_Q[1:] if last else TILES_Q
        phase_outproj(nc, P, K, banks, RT, yT_d, wout_d, ssdw_d, gateT, tiles, tmp)
    if last:
        fnw_d = nc.dram_tensor("fnw", [128, 8], F32, kind="ExternalInput").ap()
        out_d = nc.dram_tensor("outT", [D, 2048], F32, kind="ExternalOutput").ap()
        phase_final(nc, P, K, banks, RT, fnw_d, out_d, TILES_Q[1:], tmp)
    else:
        wmod_d = nc.dram_tensor("wmod_n", [D, 3 * D], F32, kind="ExternalInput").ap()
        bmod_d = nc.dram_tensor("bmod_n", [128, 24], F32, kind="ExternalInput").ap()
        normw_d = nc.dram_tensor("normw", [128, 8], F32, kind="ExternalInput").ap()
        xn_d = nc.dram_tensor("xnT", [D, QT], BF16, kind="ExternalOutput").ap()
        if with_outproj:
            Rout_d = nc.dram_tensor("RTout", [D, QT], F32, kind="ExternalOutput").ap()
        modT2 = modT
        _cm_second(nc, P, K, banks, cvec_d, wmod_d, bmod_d, list(range(0, 16)), modT2, with_outproj)
        A = norm_coeffs(nc, P, modT2, normw_d, "nc")
        phase_norm(nc, P, K, banks, RT, A, modT2, xn_d, TILES_Q, tmp)
        if with_outproj:
            Ro_v = Rout_d.rearrange("(kc p) t -> p kc t", p=128)
            for kc in range(8):
                P.dma('pool', Ro_v[:, kc, :], RT[:, kc, :], final=True)


_CM_STATE = {}


def _cm_second(nc, P, K, banks, cvec_d, wmod_d, bmod_d, fc_list, modT, second):
    if not second:
        return compute_mod(nc, P, K, banks, cvec_d, wmod_d, bmod_d, fc_list, modT)
    orig = P.sbuf

    def renamed(name, shape, dtype=F32):
        return orig(name + "_2", shape, dtype)
    P.sbuf = renamed
    try:
        compute_mod(nc, P, K, banks, cvec_d, wmod_d, bmod_d, fc_list, modT)
    finally:
        P.sbuf = orig


def pk(v):
    v = np.asarray(v)
    return np.ascontiguousarray(v.reshape(-1, 128).T)

TOK_TILES = [(0, 256)] + [(256 + 512 * i, 512) for i in range(16)]
SCALE_QK = 32.0 ** -0.5
QK_REP = 1
DEFER = True


def store_y(P, io, mi, t0, n, src):
    if 'y_store' in io:
        io['y_store'](mi, t0, n, src)
    else:
        P.dma('pool', io['yT'][mi, :, t0:t0 + n], src, final=True)


def load_weights_bf16(nc, P, A, w_d, ncols, stage):
    Wsb = A.alloc([8, ncols], BF16)
    for kc in range(8):
        st = stage[kc % 2]
        P.dma('sp' if kc % 2 == 0 else 'act', st[:, 0:ncols], w_d[kc * 128:(kc + 1) * 128, :])
        P.copy('dve', Wsb[:, kc, :], st[:, 0:ncols])
    return Wsb


def inproj(nc, P, banks, xn_v, Wsb, tiles, fm_groups, tm_group, xbufs, bank_ids=(0, 1, 2, 3), pre_tile=None):
    pending = []
    for ti, (t0, n) in enumerate(tiles):
        xb = xbufs[ti % 2]
        if callable(xn_v):
            xn_v(xb, t0, n)
        else:
            P.dma('sp', xb[:, :, 0:n], xn_v[:, :, t0:t0 + n])
        if pre_tile is not None:
            pre_tile(ti, t0, n)
        new_pending = []
        for gi, (c0, M, lat_only, fn) in enumerate(fm_groups):
            if lat_only and t0 < CTXL:
                continue
            ps = banks.b[bank_ids[gi % len(bank_ids)]]
            for kc in range(8):
                P.mm(ps[0:M, 0:n], Wsb[:, kc, c0:c0 + M], xb[:, kc, 0:n], start=(kc == 0), stop=(kc == 7))
            if fn is not None:
                r = fn(ps, t0, n, ti)
                if r is not None:
                    if DEFER:
                        new_pending.append(r)
                    else:
                        r()
        if tm_group is not None:
            c0, ncols, fn, tmbanks = tm_group
            for sub in range(n // 128):
                ps = banks.b[tmbanks[sub % len(tmbanks)]]
                for kc in range(8):
                    P.mm(ps[:, 0:ncols], xb[:, kc, sub * 128:(sub + 1) * 128], Wsb[:, kc, c0:c0 + ncols],
                         start=(kc == 0), stop=(kc == 7))
                fn(ps, t0 + sub * 128)
        for r in pending:
            r2 = r()
            if r2 is not None:
                new_pending.append(r2)
        pending = new_pending
    while pending:
        nxt = []
        for r in pending:
            r2 = r()
            if r2 is not None:
                nxt.append(r2)
        pending = nxt


def silu_evac(P, out_bf, ps_ap, tmp_e, M, n):
    P.act(tmp_e[0:M, 0:n], ps_ap, AF.Tanh, scale=0.5)
    P.stt(out_bf, tmp_e[0:M, 0:n], 1.0, ps_ap, ALU.add, ALU.mult)


def attention_phase(nc, P, A, banks, K, xn_v, io, layer, with_ctx):
    A.reset()
    lam_init = 0.8 - 0.6 * float(np.exp(-0.3 * layer))
    stage = [A.alloc([640], F32) for _ in range(2)]
    Wsb = load_weights_bf16(nc, P, A, io['w_attn'], 640, stage)
    xbufs = [A.alloc([8, 512], BF16) for _ in range(2)]
    Qa = A.alloc([NTOK], BF16)
    Ka = A.alloc([NTOK], BF16)
    Vt = A.alloc([66, 128], BF16)
    sg = A.alloc([NTOK], BF16)
    cosb = [A.alloc([512], F32) for _ in range(2)]
    sinb = [A.alloc([512], F32) for _ in range(2)]
    t1 = [A.alloc([512], F32) for _ in range(2)]
    t2 = [A.alloc([512], F32) for _ in range(2)]
    te = [A.alloc([512], F32) for _ in range(2)]
    P.memset('pool', Vt[:, :, 65:128], 0.0)
    P.memset('pool', Vt[:, :, 64:65], 1.0)

    def pre_tile(ti, t0, n):
        if t0 >= CTXL:
            P.dma('act', cosb[ti % 2][:], io['cosT'][:, t0 - CTXL:t0 - CTXL + n])
            P.dma('act', sinb[ti % 2][:], io['sinT'][:, t0 - CTXL:t0 - CTXL + n])

    state = {}

    def ev_plain(dst):
        def f(ps, t0, n, ti):
            if t0 < CTXL:
                P.copy('act', dst[:, t0:t0 + n], ps[:, 0:n])
            else:
                state['ps1'] = ps
        return f

    def ev_rot(dst):
        def f(ps, t0, n, ti):
            a = t1[ti % 2]
            b = t2[ti % 2]
            P.tt('dve', a[:, 0:n], state['ps1'][:, 0:n], cosb[ti % 2][:, 0:n], ALU.mult)
            P.tt('dve', b[:, 0:n], ps[:, 0:n], sinb[ti % 2][:, 0:n], ALU.mult)
            P.tt('pool', dst[:, t0:t0 + n], a[:, 0:n], b[:, 0:n], ALU.add)
        return f

    def ev_gate(ps, t0, n, ti):
        silu_evac(P, sg[0:64, t0:t0 + n], ps[0:64, 0:n], te[ti % 2], 64, n)

    vT = [A.alloc([512], BF16) for _ in range(2)]

    def ev_vT(ps, t0, n, ti):
        v_ = vT[ti % 2]
        P.copy('act', v_[0:64, 0:n], ps[0:64, 0:n])

        def rest():
            for sub in range(n // 128):
                pt = banks.b[6 + sub % 2].bitcast(BF16)
                P.transpose(pt[:, 0:64], v_[0:64, sub * 128:(sub + 1) * 128], K['identb'][0:64, 0:64])
                P.copy('dve', Vt[:, t0 // 128 + sub, 0:64], pt[:, 0:64])
        return rest

    fm = [(0, 128, False, ev_plain(Qa)), (128, 128, True, ev_rot(Qa)),
          (256, 128, False, ev_plain(Ka)), (384, 128, True, ev_rot(Ka)),
          (512, 64, False, ev_gate), (576, 64, False, ev_vT)]
    inproj(nc, P, banks, xn_v, Wsb, TOK_TILES, fm, None, xbufs, bank_ids=(0, 1, 2, 3, 4, 5), pre_tile=pre_tile)

    lv = A.alloc([128], F32)
    P.dma('sp', lv[0:64, :], io['diff_lam'].partition_broadcast(64))
    pr = A.alloc([64], F32)
    s2 = A.alloc([2], F32)
    P.tt('dve', pr[0:64, 0:32], lv[0:64, 0:32], lv[0:64, 32:64], ALU.mult)
    P.tt('dve', pr[0:64, 32:64], lv[0:64, 64:96], lv[0:64, 96:128], ALU.mult)
    P.op('dve', lambda e: e.tensor_reduce(s2[0:64, 0:2], pr[0:64, :].rearrange("p (a b) -> p a b", a=2), AX.X, ALU.add),
         reads=[pr[0:64, :]], writes=[s2[0:64, 0:2]])
    P.act(s2[0:64, :], s2[0:64, :], AF.Exp)
    nlam = A.alloc([1], F32)
    P.tt('dve', nlam[0:64, :], s2[0:64, 1:2], s2[0:64, 0:1], ALU.subtract)
    P.ts('dve', nlam[0:64, :], nlam[0:64, :], -lam_init, None, ALU.add)
    subw = A.alloc([1], F32)
    P.dma('sp', subw[0:64, :], io['diff_subln_w'])
    P.ts('dve', subw[0:64, :], subw[0:64, :], 1.0 - lam_init, None, ALU.mult)

    Eb = [A.alloc([1024], BF16) for _ in range(4)]
    osb = [A.alloc([512], F32) for _ in range(2)]
    fz = [A.alloc([512], F32) for _ in range(4)]
    yb = [A.alloc([512], BF16) for _ in range(2)]
    qblocks = []
    if with_ctx:
        qblocks.append((0, 256, [0, 1]))
    for i in range(16):
        qblocks.append((256 + 512 * i, 512, list(range(66))))
    o_acc = [banks.b[6], banks.b[7]]
    for qi, (q0, nq, kbs) in enumerate(qblocks):
        def qk(kb):
            S = banks.pp[kb % 3]
            for rep in range(QK_REP):
                P.mm(S[:, 0:nq], Ka[0:32, kb * 128:(kb + 1) * 128], Qa[0:32, q0:q0 + nq])
                P.mm(S[:, 512:512 + nq], Ka[64:96, kb * 128:(kb + 1) * 128], Qa[64:96, q0:q0 + nq])
        for k_ in kbs[0:3]:
            qk(k_)
        for ki, kb in enumerate(kbs):
            S = banks.pp[kb % 3]
            E = Eb[ki % 4]
            if nq == 512:
                P.act(E[:, :], S[:, :], AF.Exp, scale=SCALE_QK)
            else:
                Ev = E.rearrange("p (a b) -> p a b", a=2)[:, :, 0:nq]
                Sv = S.rearrange("p (a b) -> p a b", a=2)[:, :, 0:nq]
                P.act(Ev, Sv, AF.Exp, scale=SCALE_QK)
            if ki + 3 < len(kbs):
                qk(kbs[ki + 3])
            for c in range(2):
                P.mm(o_acc[c][:, 0:nq], Vt[:, kb, :], E[:, c * 512:c * 512 + nq], start=(ki == 0), stop=(ki == len(kbs) - 1))
        for c in range(2):
            P.copy('act', osb[c][0:65, 0:nq], o_acc[c][0:65, 0:nq])
        zb = [banks.b[0], banks.b[1]]
        for c in range(2):
            P.mm(zb[c][0:64, 0:nq], K['selZ'][0:65, :], osb[c][0:65, 0:nq])
        for c in range(2):
            P.act(fz[c][0:64, 0:nq], zb[c][0:64, 0:nq], AF.Ln)
            P.act(fz[c][0:64, 0:nq], fz[c][0:64, 0:nq], AF.Exp, scale=-1.0)
            P.tt('dve', fz[c][0:64, 0:nq], fz[c][0:64, 0:nq], osb[c][0:64, 0:nq], ALU.mult)
        o = fz[2]
        P.stt(o[0:64, 0:nq], fz[1][0:64, 0:nq], nlam[0:64, 0:1], fz[0][0:64, 0:nq], ALU.mult, ALU.add)
        P.act(fz[3][0:64, 0:nq], o[0:64, 0:nq], AF.Square)
        P.mm(zb[0][0:64, 0:nq], K['ones'][0:64, 0:64], fz[3][0:64, 0:nq])
        P.act(fz[3][0:64, 0:nq], zb[0][0:64, 0:nq], AF.Ln, scale=1.0 / 64, bias=K['eps'][0:64, :])
        P.act(fz[3][0:64, 0:nq], fz[3][0:64, 0:nq], AF.Exp, scale=-0.5)
        P.stt(o[0:64, 0:nq], o[0:64, 0:nq], subw[0:64, 0:1], fz[3][0:64, 0:nq], ALU.mult, ALU.mult)
        y = yb[qi % 2]
        P.stt(y[0:64, 0:nq], o[0:64, 0:nq], 0.5, sg[0:64, q0:q0 + nq], ALU.mult, ALU.mult)
        store_y(P, io, 2, q0, nq, y[0:64, 0:nq])
        if 'after_q' in io:
            io['after_q'](q0, nq)


TP = 8456
PADC = 2
PADL = 260


def pcol(t0):
    return t0 + PADC if t0 < CTXL else t0 + (PADL - CTXL)


PTILES = [(PADC, 256)] + [(PADL + 512 * i, 512) for i in range(16)]


def conv4(P, dst, src, cw, cb, np_, c_lo, c_hi):
    n = c_hi - c_lo
    P.ts('dve', dst[0:np_, c_lo:c_hi], src[0:np_, c_lo - 2:c_hi - 2], cw[0:np_, 0:1], cb[0:np_, 0:1], ALU.mult, ALU.add)
    for k in range(1, 4):
        P.stt(dst[0:np_, c_lo:c_hi], src[0:np_, c_lo + k - 2:c_hi + k - 2], cw[0:np_, k:k + 1], dst[0:np_, c_lo:c_hi],
              ALU.mult, ALU.add)


def lru_phase(nc, P, A, banks, K, xn_v, io, layer, with_ctx):
    A.reset()
    stage = [A.alloc([192], F32) for _ in range(2)]
    Wsb = load_weights_bf16(nc, P, A, io['w_lru'], 192, stage)
    xbufs = [A.alloc([8, 512], BF16) for _ in range(2)]
    XA = A.alloc([TP], F32)
    XU = A.alloc([TP], F32)
    sg = A.alloc([NTOK], BF16)
    te = [A.alloc([512], F32) for _ in range(2)]
    par = A.alloc([16], F32)
    P.dma('sp', par[:, 0:9], io['lru_par'])
    cw, cb = par[:, 0:4], par[:, 4:5]
    nba, nbx = par[:, 9:10], par[:, 10:11]
    c8, c16 = par[:, 11:12], par[:, 12:13]
    P.ts('dve', nba, par[:, 5:6], 0.5, None, ALU.mult)
    P.ts('dve', nbx, par[:, 6:7], 0.5, None, ALU.mult)
    P.act(c8, par[:, 7:8], AF.Exp, scale=-1.0)
    P.act(c8, c8, AF.Ln, bias=K['one'][:, 0:1])
    P.ts('dve', c16, c8, -16.0, None, ALU.mult)
    P.ts('dve', c8, c8, -8.0, None, ALU.mult)
    gw32 = A.alloc([256], F32)
    P.dma('sp', gw32[0:64, :], io['lru_gw'])
    gw = A.alloc([256], BF16)
    P.copy('pool', gw[0:64, :], gw32[0:64, :])
    II = A.alloc([64], F32)
    P.copy('pool', II[0:64, :], K['ident'][0:64, 0:64])
    P.copy('pool', II[64:128, :], K['ident'][64:128, 64:128])
    P.memset('pool', XA[:, 0:PADC], 0.0)
    P.memset('pool', XA[:, PADC + CTXL:PADL], 0.0)
    P.memset('pool', XA[:, PADL + SEQ:TP], 0.0)

    def ev_x(ps, t0, n, ti):
        c = pcol(t0)
        P.copy('act', XA[:, c:c + n], ps[:, 0:n])

    def ev_gate(ps, t0, n, ti):
        silu_evac(P, sg[0:64, t0:t0 + n], ps[0:64, 0:n], te[ti % 2], 64, n)

    inproj(nc, P, banks, xn_v, Wsb, TOK_TILES, [(0, 128, False, ev_x), (128, 64, False, ev_gate)], None, xbufs,
           bank_ids=(0, 1, 2, 3))
    conv4(P, XU, XA, cw, cb, 128, PADC, PADC + CTXL)
    for i in range(4):
        conv4(P, XU, XA, cw, cb, 128, PADL + 2048 * i, PADL + 2048 * (i + 1))
    xcb = [A.alloc([512], BF16) for _ in range(2)]
    gr = [A.alloc([512], F32) for _ in range(2)]
    gi_ = [A.alloc([512], F32) for _ in range(2)]
    gs = [A.alloc([512], F32) for _ in range(2)]
    for ti, (c0, n) in enumerate(PTILES):
        xb = xcb[ti % 2]
        P.copy('pool', xb[0:64, 0:n], XU[0:64, c0:c0 + n])
        psr, psi = banks.b[(2 * ti) % 8], banks.b[(2 * ti + 1) % 8]
        P.mm(psr[:, 0:n], gw[0:64, 0:128], xb[0:64, 0:n])
        P.mm(psi[:, 0:n], gw[0:64, 128:256], xb[0:64, 0:n])
        r, ii, s = gr[ti % 2], gi_[ti % 2], gs[ti % 2]
        P.act(r[:, 0:n], psr[:, 0:n], AF.Tanh, scale=0.5, bias=nba)
        P.act(ii[:, 0:n], psi[:, 0:n], AF.Tanh, scale=0.5, bias=nbx)
        P.ts('dve', r[:, 0:n], r[:, 0:n], 0.5, 0.5, ALU.mult, ALU.add)
        P.ts('dve', ii[:, 0:n], ii[:, 0:n], 0.5, 0.5, ALU.mult, ALU.add)
        P.act(XA[:, c0:c0 + n], r[:, 0:n], AF.Exp, scale=c8)
        P.act(s[:, 0:n], r[:, 0:n], AF.Exp, scale=c16)
        P.act(s[:, 0:n], s[:, 0:n], AF.Ln, scale=-1.0, bias=K['one'][:, 0:1])
        P.act(s[:, 0:n], s[:, 0:n], AF.Exp, scale=0.5)
        P.tt('pool', ii[:, 0:n], ii[:, 0:n], s[:, 0:n], ALU.mult)
        P.tt('dve', XU[:, c0:c0 + n], XU[:, c0:c0 + n], ii[:, 0:n], ALU.mult)
    fa, fu = XA[0:64], XU[0:64]
    ba_, bu = XA[64:128], XU[64:128]
    P.scan(fu[:, PADC:PADC + CTXL], fa[:, PADC:PADC + CTXL], fu[:, PADC:PADC + CTXL], 0.0)
    for i in range(4):
        lo = PADL + 2048 * i
        init = fu[:, PADC + CTXL - 1:PADC + CTXL] if i == 0 else fu[:, lo - 1:lo]
        P.scan(fu[:, lo:lo + 2048], fa[:, lo:lo + 2048], fu[:, lo:lo + 2048], init)
    P.scan(bu[:, PADC:PADC + CTXL][:, ::-1], ba_[:, PADC:PADC + CTXL][:, ::-1], bu[:, PADC:PADC + CTXL][:, ::-1], 0.0)
    for i in range(3, -1, -1):
        lo = PADL + 2048 * i
        init = bu[:, PADC:PADC + 1] if i == 3 else bu[:, lo + 2048:lo + 2049]
        P.scan(bu[:, lo:lo + 2048][:, ::-1], ba_[:, lo:lo + 2048][:, ::-1], bu[:, lo:lo + 2048][:, ::-1], init)
    yb = [A.alloc([512], BF16) for _ in range(2)]
    for ti, (t0, n) in enumerate(TOK_TILES):
        if t0 < CTXL and not with_ctx:
            continue
        c0 = pcol(t0)
        ps = banks.b[ti % 4]
        P.mm(ps[0:64, 0:n], II[:, :], XU[:, c0:c0 + n])
        y = yb[ti % 2]
        P.stt(y[0:64, 0:n], ps[0:64, 0:n], 0.5, sg[0:64, t0:t0 + n], ALU.mult, ALU.mult)
        store_y(P, io, 1, t0, n, y[0:64, 0:n])


def gla_phase(nc, P, A, banks, K, xn_v, io, layer, with_ctx):
    A.reset()
    NC_ = 132
    QG = A.alloc([NTOK], BF16)
    KG = A.alloc([NTOK], BF16)
    KDt = A.alloc([66, 64], BF16)
    Vt = A.alloc([66, 64], BF16)
    sg = A.alloc([NTOK], BF16)
    ST = A.alloc([NC_ + 2, 64], F32)
    STb = A.alloc([NC_ + 2, 64], BF16)
    DEC = A.alloc([NC_], F32)
    par = A.alloc([8], F32)
    P.dma('sp', par[0:64, 0:2], io['gla_par'])
    nb2 = par[0:64, 2:3]
    P.ts('dve', nb2, par[0:64, 0:1], -1.0, None, ALU.mult)
    lnsc = par[0:64, 3:4]
    P.memset('dve', lnsc, float(np.log(32.0 ** -0.5)))
    w2_32 = A.alloc([64], F32)
    P.dma('sp', w2_32[0:32, :], io['gla_w2bd'])
    w2b = A.alloc([64], BF16)
    P.copy('dve', w2b[0:32, :], w2_32[0:32, :])
    mk = A.mark()
    stage = [A.alloc([320], F32) for _ in range(2)]
    Wsb = load_weights_bf16(nc, P, A, io['w_gla'], 320, stage)
    xbufs = [A.alloc([8, 512], BF16) for _ in range(2)]
    te = [A.alloc([512], F32) for _ in range(2)]
    lrb = [A.alloc([512], BF16) for _ in range(2)]
    Lb = [A.alloc([512], F32) for _ in range(2)]
    Gb = [A.alloc([512], F32) for _ in range(2)]
    Eq = [A.alloc([512], F32) for _ in range(2)]
    Ek = [A.alloc([512], F32) for _ in range(2)]
    dG = [A.alloc([512], F32) for _ in range(2)]
    kdT = [A.alloc([512], BF16) for _ in range(2)]
    P.memset('dve', ST[0:64, 0:2, :], 0.0)
    P.memset('dve', ST[0:64, NC_:NC_ + 2, :], 0.0)
    st = {}

    qs = [A.alloc([512], F32) for _ in range(2)]
    ks = [A.alloc([512], F32) for _ in range(2)]

    def ev_q(ps, t0, n, ti):
        P.copy('act', qs[ti % 2][0:64, 0:n], ps[0:64, 0:n])

    def ev_k(ps, t0, n, ti):
        P.copy('act', ks[ti % 2][0:64, 0:n], ps[0:64, 0:n])

    def ev_gate(ps, t0, n, ti):
        P.copy('dve', sg[0:64, t0:t0 + n], ps[0:64, 0:n])

    def ev_lr(ps, t0, n, ti):
        i2 = ti % 2
        nch = n // 64
        c0 = t0 // 64
        P.copy('act', lrb[i2][0:32, 0:n], ps[0:32, 0:n])

        def rest():
            pz = banks.b[7]
            P.mm(pz[0:64, 0:n], w2b[0:32, 0:64], lrb[i2][0:32, 0:n])
            L, G = Lb[i2], Gb[i2]
            P.act(L[0:64, 0:n], pz[0:64, 0:n], AF.Exp, scale=-1.0, bias=nb2)
            P.act(L[0:64, 0:n], L[0:64, 0:n], AF.Ln, bias=K['one'][0:64, 0:1])
            P.scan(G[0:32, 0:n], K['scanmask'][0:32, 0:n], L[0:32, 0:n], 0.0)
            P.scan(G[32:64, 0:n][:, ::-1], K['scanmask'][32:64, 0:n][:, ::-1], L[32:64, 0:n][:, ::-1], 0.0)
            P.act(Eq[i2][0:64, 0:n], G[0:64, 0:n], AF.Exp, scale=-1.0 / 16, bias=lnsc)
            P.act(Ek[i2][0:64, 0:n], G[0:64, 0:n], AF.Exp, scale=1.0 / 16)
            P.tt('dve', QG[0:64, t0:t0 + n], qs[i2][0:64, 0:n], Eq[i2][0:64, 0:n], ALU.mult)
            P.tt('dve', KG[0:64, t0:t0 + n], ks[i2][0:64, 0:n], Ek[i2][0:64, 0:n], ALU.mult)
            G3 = G[:, 0:n].rearrange("p (c s) -> p c s", s=64)
            d3 = dG[i2][:, 0:n].rearrange("p (c s) -> p c s", s=64)
            P.tt('dve', d3[0:32], G3[0:32, :, 63:64].to_broadcast([32, nch, 64]), G3[0:32], ALU.subtract)
            P.tt('dve', d3[32:64], G3[32:64, :, 0:1].to_broadcast([32, nch, 64]), G3[32:64], ALU.subtract)
            P.act(dG[i2][0:64, 0:n], dG[i2][0:64, 0:n], AF.Exp, scale=-1.0 / 16)
            P.tt('dve', kdT[i2][0:64, 0:n], ks[i2][0:64, 0:n], dG[i2][0:64, 0:n], ALU.mult)
            P.act(DEC[0:32, c0:c0 + nch], G3[0:32, :, 63], AF.Exp, scale=-1.0 / 16)
            P.act(DEC[32:64, c0:c0 + nch], G3[32:64, :, 0], AF.Exp, scale=-1.0 / 16)
            def rest2():
                for sub in range(n // 128):
                    pt = banks.b[5 + sub % 2].bitcast(BF16)
                    P.transpose(pt[:, 0:64], kdT[i2][0:64, sub * 128:(sub + 1) * 128], K['identb'][0:64, 0:64])
                    P.copy('dve', KDt[:, t0 // 128 + sub, :], pt[:, 0:64])
            return rest2
        return rest

    vT = [A.alloc([512], BF16) for _ in range(2)]

    def ev_vT(ps, t0, n, ti):
        v_ = vT[ti % 2]
        P.copy('act', v_[0:64, 0:n], ps[0:64, 0:n])

        for sub in range(n // 128):
            pt = banks.b[5 + sub % 2].bitcast(BF16)
            P.transpose(pt[:, 64:128], v_[0:64, sub * 128:(sub + 1) * 128], K['identb'][0:64, 0:64])
            P.copy('dve', Vt[:, t0 // 128 + sub, :], pt[:, 64:128])

    fm = [(0, 64, False, ev_q), (64, 64, False, ev_k), (192, 64, False, ev_gate), (256, 64, False, ev_vT), (128, 32, False, ev_lr)]
    inproj(nc, P, banks, xn_v, Wsb, TOK_TILES, fm, None, xbufs, bank_ids=(0, 1, 2, 3, 4))
    for ti, (t0, n) in enumerate(TOK_TILES):
        P.act(te[ti % 2][0:64, 0:n], sg[0:64, t0:t0 + n], AF.Tanh, scale=0.5)
        P.stt(sg[0:64, t0:t0 + n], te[ti % 2][0:64, 0:n], 1.0, sg[0:64, t0:t0 + n], ALU.add, ALU.mult)

    for g0 in range(0, 66, 8):
        npair = min(8, 66 - g0)
        for i in range(npair):
            p = g0 + i
            for j in range(2):
                P.mm(banks.b[j][0:64, i * 64:(i + 1) * 64], KDt[64 * j:64 * j + 64, p, :], Vt[64 * j:64 * j + 64, p, :])
        for j in range(2):
            src = banks.b[j][:, 0:npair * 64].rearrange("p (c v) -> p c v", v=64)
            c0 = 2 * g0 + j
            P.copy('act', ST[0:32, c0 + 2:c0 + 1 + 2 * npair:2, :], src[0:32])
            P.copy('dve', ST[32:64, c0:c0 + 2 * npair - 1:2, :], src[32:64])
    fsteps = [(c + 2, c + 1, c) for c in range(1, NC_)]
    bsteps = [(c, c + 1, c) for c in (2, 1, 0)] + [(131, 0, 131)] + [(c, c + 1, c) for c in range(130, 4, -1)]
    for i in range(max(len(fsteps), len(bsteps))):
        if i < len(fsteps):
            o_, i_, d_ = fsteps[i]
            P.stt(ST[0:32, o_, :], ST[0:32, i_, :], DEC[0:32, d_:d_ + 1], ST[0:32, o_, :], ALU.mult, ALU.add)
        if i < len(bsteps):
            o_, i_, d_ = bsteps[i]
            P.stt(ST[32:64, o_, :], ST[32:64, i_, :], DEC[32:64, d_:d_ + 1], ST[32:64, o_, :], ALU.mult, ALU.add)
    P.copy('dve', ST[32:64, 132, :], ST[32:64, 0, :])
    P.copy('dve', STb[0:64], ST[0:64])

    A.release(mk)
    tm = [A.alloc([512], F32) for _ in range(2)]
    At = [A.alloc([512], BF16) for _ in range(2)]
    ob = [A.alloc([512], F32) for _ in range(2)]
    yb = [A.alloc([512], BF16) for _ in range(2)]
    tiles2 = [(ti, t0, n) for ti, (t0, n) in enumerate(TOK_TILES) if not (t0 < CTXL and not with_ctx)]

    def stageA(ti, t0, n):
        i2 = ti % 2
        npair = n // 128
        X, Y = banks.b[2], banks.b[3]
        for i in range(npair):
            cs = slice(t0 + 128 * i, t0 + 128 * (i + 1))
            P.mm(X[:, 128 * i:128 * (i + 1)], KG[0:32, cs], QG[0:32, cs])
            P.mm(Y[:, 128 * i:128 * (i + 1)], KG[32:64, cs], QG[32:64, cs])
        P.tt('dve', tm[i2][:, 0:n], X[:, 0:n], K['maskF'][:, 0:n], ALU.mult)
        P.tt('dve', At[i2][:, 0:n], Y[:, 0:n], K['maskB'][:, 0:n], ALU.mult)
        P.tt('pool', At[i2][:, 0:n], At[i2][:, 0:n], tm[i2][:, 0:n], ALU.add)

    def stageB(ti, t0, n):
        i2 = ti % 2
        npair = n // 128
        Z = banks.b[4 + i2]
        for i in range(npair):
            p = t0 // 128 + i
            P.mm(Z[0:64, 128 * i:128 * (i + 1)], Vt[:, p, :], At[i2][:, 128 * i:128 * (i + 1)], start=True, stop=False)
            for j in range(2):
                c = 2 * p + j
                cs = slice(t0 + 128 * i + 64 * j, t0 + 128 * i + 64 * (j + 1))
                kk = 32 if c == 3 else 64
                P.mm(Z[0:64, 128 * i + 64 * j:128 * i + 64 * (j + 1)], STb[0:kk, c + 1, :], QG[0:kk, cs],
                     start=False, stop=(j == 1))
        o = ob[i2]
        P.act(o[0:64, 0:n], Z[0:64, 0:n], AF.Square)

    def stageC(ti, t0, n):
        i2 = ti % 2
        Z = banks.b[4 + i2]
        o = ob[i2]
        zb = banks.b[6 + i2]
        P.mm(zb[0:64, 0:n], K['ones'][0:64, 0:64], o[0:64, 0:n])
        P.act(o[0:64, 0:n], zb[0:64, 0:n], AF.Ln, scale=1.0 / 64, bias=K['eps'][0:64, :])
        P.act(o[0:64, 0:n], o[0:64, 0:n], AF.Exp, scale=-0.5)
        P.stt(o[0:64, 0:n], Z[0:64, 0:n], par[0:64, 1:2], o[0:64, 0:n], ALU.mult, ALU.mult)
        y = yb[i2]
        P.stt(y[0:64, 0:n], o[0:64, 0:n], 0.5, sg[0:64, t0:t0 + n], ALU.mult, ALU.mult)
        store_y(P, io, 0, t0, n, y[0:64, 0:n])

    nt2 = len(tiles2)
    for it in range(nt2 + 1):
        if it < nt2:
            stageA(*tiles2[it])
        if it >= 1:
            stageB(*tiles2[it - 1])
            stageC(*tiles2[it - 1])


def ssd_phase(nc, P, A, banks, K, xn_v, io, layer, with_ctx):
    A.reset()
    NC_ = 132
    XB = A.alloc([TP], BF16)
    C2 = A.alloc([TP], BF16)
    sgz = A.alloc([NTOK], BF16)
    par = A.alloc([16], F32)
    P.dma('sp', par[:, 0:5], io['ssd_par'])
    cpar = A.alloc([2, 5], F32)
    P.dma('sp', cpar[:], io['ssd_cpar'])
    na = par[:, 5:7]
    P.act(na, par[:, 2:4], AF.Exp)
    P.ts('dve', na, na, -1.0, None, ALU.mult)
    dts_d = io['ssd_scr'][0:2]
    crow_d = io['ssd_scr'][2:4]
    clr_d = io['ssd_scr'][4:6]
    mk = A.mark()
    stage = [A.alloc([322], F32) for _ in range(2)]
    Wsb = load_weights_bf16(nc, P, A, io['w_ssd'], 322, stage)
    xbufs = [A.alloc([8, 512], BF16) for _ in range(2)]
    XR1 = A.alloc([TP], F32)
    XR2 = A.alloc([TP], F32)
    XC = A.alloc([2052], F32)
    te = [A.alloc([512], F32) for _ in range(2)]
    dtt = [A.alloc([512], F32) for _ in range(2)]
    for X in (XR1, XR2):
        P.memset('pool', X[:, 0:PADC], 0.0)
        P.memset('pool', X[:, PADC + CTXL:PADL], 0.0)
        P.memset('pool', X[:, PADL + SEQ:TP], 0.0)

    def ev_raw(dst):
        def f(ps, t0, n, ti):
            c = pcol(t0)
            P.copy('act', dst[:, c:c + n], ps[:, 0:n])
        return f

    def ev_gate(ps, t0, n, ti):
        silu_evac(P, sgz[0:64, t0:t0 + n], ps[0:64, 0:n], te[ti % 2], 64, n)

    def ev_dt(ps, t0, n, ti):
        P.copy('dve', dtt[ti % 2][0:2, 0:n], ps[0:2, 0:n])
        P.dma('pool', dts_d[:, t0:t0 + n], dtt[ti % 2][0:2, 0:n])

    fm = [(0, 128, False, ev_raw(XR1)), (128, 128, False, ev_raw(XR2)), (256, 64, False, ev_gate), (320, 2, False, ev_dt)]
    inproj(nc, P, banks, xn_v, Wsb, TOK_TILES, fm, None, xbufs, bank_ids=(0, 1, 2, 3))
    pieces = [(PADC, PADC + CTXL)] + [(PADL + 2048 * i, PADL + 2048 * (i + 1)) for i in range(4)]
    for gi, (src, dst) in enumerate(((XR1, XB), (XR2, C2))):
        for (lo, hi) in pieces:
            n = hi - lo
            P.act(XC[:, 2:2 + n], src[:, lo - 2:hi - 2], AF.Identity, scale=cpar[:, gi, 0:1], bias=cpar[:, gi, 4:5])
            for k in range(1, 4):
                P.stt(XC[:, 2:2 + n], src[:, lo + k - 2:hi + k - 2], cpar[:, gi, k:k + 1], XC[:, 2:2 + n], ALU.mult, ALU.add)
            P.act(dst[:, lo:hi], XC[:, 2:2 + n], AF.Silu)
    A.release(mk)
    PM = A.alloc([2, 128], F32)
    for d in range(2):
        P.dma('sp', PM[0:66, d, :], dts_d[d].rearrange("(p s) -> p s", s=128))
    DT = A.alloc([2, 128], F32)
    for d in range(2):
        P.act(DT[0:66, d, :], PM[0:66, d, :], AF.Exp, bias=par[0:66, d:d + 1])
    P.act(DT[0:66], DT[0:66], AF.Ln, bias=K['one'][0:66, 0:1])
    DA = A.alloc([2, 128], F32)
    for d in range(2):
        P.ts('dve', DA[0:66, d, :], DT[0:66, d, :], na[0:66, d:d + 1], None, ALU.mult)
    CUM = A.alloc([2, 128], F32)
    P.scan(CUM[0:66, 0, :], K['scanmask2'][0:66, 0, :], DA[0:66, 0, :], 0.0)
    P.scan(CUM[0:66, 1, :][:, ::-1], K['scanmask2'][0:66, 1, :][:, ::-1], DA[0:66, 1, :][:, ::-1], 0.0)
    P.dma('pool', crow_d[0].rearrange("(p s) -> p s", s=128), CUM[0:66, 0, :])
    P.dma('pool', crow_d[1].rearrange("(p s) -> p s", s=128), CUM[0:66, 1, :])
    LND = A.alloc([2, 128], F32)
    P.act(LND[0:66], DT[0:66], AF.Ln)
    CQ0 = A.alloc([2, 128], F32)
    P.tt('dve', CQ0[0:66], CUM[0:66], LND[0:66], ALU.subtract)
    CL = A.alloc([2, 2], F32)
    C4 = CUM[0:66].rearrange("p d (c s) -> p d c s", s=64)
    P.copy('dve', CL[0:66, 0, :], C4[:, 0, :, 63])
    P.copy('dve', CL[0:66, 1, :], C4[:, 1, :, 0])
    P.dma('pool', clr_d[0, 0:132].rearrange("(p c) -> p c", c=2), CL[0:66, 0, :])
    P.dma('pool', clr_d[1, 0:132].rearrange("(p c) -> p c", c=2), CL[0:66, 1, :])
    W0 = A.alloc([2, 128], F32)
    W4 = W0[0:66].rearrange("p d (c s) -> p d c s", s=64)
    P.tt('dve', W4, CL[0:66].unsqueeze(3).to_broadcast([66, 2, 2, 64]), C4, ALU.subtract)
    P.act(W0[0:66], W0[0:66], AF.Exp)
    P.tt('dve', W0[0:66], W0[0:66], DT[0:66], ALU.mult)
    CQ = A.alloc([2, 66], F32)
    WT = A.alloc([2, 66], F32)
    for d in range(2):
        for (src, dst) in ((CQ0, CQ), (W0, WT)):
            pt = banks.b[(2 * d) % 8 + (0 if src is CQ0 else 1)]
            P.transpose(pt[:, 0:66], src[0:66, d, :], K['ident'][0:66, 0:66])
            P.copy('act', dst[:, d, :], pt[:, 0:66])
    clr = A.alloc([132], F32)
    P.dma('sp', clr[0:2, :], clr_d[:, 0:132])
    DEC = A.alloc([132], F32)
    pd = banks.b[4]
    P.mm(pd[:, 0:132], K['sel2'][0:2, :], clr[0:2, :])
    P.act(DEC[:, :], pd[:, 0:132], AF.Exp)
    xBt = A.alloc([66, 128], BF16)
    for p in range(66):
        c0 = pcol(128 * p)
        pt = banks.b[p % 4].bitcast(BF16)
        P.transpose(pt[:, 0:128], XB[:, c0:c0 + 128], K['identb'][:, :])
        P.copy('act' if p % 2 == 0 else 'dve', xBt[:, p, :], pt[:, 0:128])
    BW = A.alloc([66, 128], BF16)
    for d in range(2):
        P.tt('dve', BW[:, :, 64 * d:64 * d + 64], xBt[:, :, 64:128], WT[:, d, :].unsqueeze(2).to_broadcast([128, 66, 64]), ALU.mult)
    ST = A.alloc([NC_ + 2, 64], F32)
    STb = A.alloc([NC_ + 2, 64], BF16)
    P.memset('pool', ST[:, 0:2, :], 0.0)
    P.memset('pool', ST[:, NC_:NC_ + 2, :], 0.0)
    for g0 in range(0, 66, 8):
        npair = min(8, 66 - g0)
        for i in range(npair):
            p = g0 + i
            for j in range(2):
                P.mm(banks.b[j][:, i * 64:(i + 1) * 64], BW[64 * j:64 * j + 64, p, :], xBt[64 * j:64 * j + 64, p, 0:64])
        for j in range(2):
            src = banks.b[j][:, 0:npair * 64].rearrange("p (c v) -> p c v", v=64)
            c0 = 2 * g0 + j
            P.copy('act', ST[0:64, c0 + 2:c0 + 1 + 2 * npair:2, :], src[0:64])
            P.copy('dve', ST[64:128, c0:c0 + 2 * npair - 1:2, :], src[64:128])
    fsteps = [(c + 2, c + 1, c) for c in range(1, NC_)]
    bsteps = [(c, c + 1, c) for c in (2, 1, 0)] + [(131, 0, 131)] + [(c, c + 1, c) for c in range(130, 4, -1)]
    for i in range(max(len(fsteps), len(bsteps))):
        if i < len(fsteps):
            o_, i_, d_ = fsteps[i]
            P.stt(ST[0:64, o_, :], ST[0:64, i_, :], DEC[0:64, d_:d_ + 1], ST[0:64, o_, :], ALU.mult, ALU.add)
        if i < len(bsteps):
            o_, i_, d_ = bsteps[i]
            P.stt(ST[64:128, o_, :], ST[64:128, i_, :], DEC[64:128, d_:d_ + 1], ST[64:128, o_, :], ALU.mult, ALU.add)
    P.copy('dve', ST[64:128, 132, :], ST[64:128, 0, :])
    P.copy('dve', STb[:], ST[:])
    crt = [A.alloc([512], F32) for _ in range(2)]
    SG = [A.alloc([2, 512], F32) for _ in range(2)]
    Ls = [A.alloc([512], F32) for _ in range(2)]
    Mt = [A.alloc([512], BF16) for _ in range(2)]
    ec = [A.alloc([512], F32) for _ in range(2)]
    Cs = [A.alloc([512], BF16) for _ in range(2)]
    yv = [A.alloc([512], F32) for _ in range(2)]
    yb = [A.alloc([512], BF16) for _ in range(2)]
    tiles2 = [(ti, t0, n) for ti, (t0, n) in enumerate(TOK_TILES) if not (t0 < CTXL and not with_ctx)]

    def stageA(ti, t0, n):
        i2 = ti % 2
        npair = n // 128
        p0 = t0 // 128
        pc0 = pcol(t0)
        cr = crt[i2]
        P.dma('sp', cr[0:2, 0:n], crow_d[:, t0:t0 + n])
        pe_ = banks.b[0]
        P.mm(pe_[:, 0:n], K['sel2'][0:2, :], cr[0:2, 0:n])
        P.act(ec[i2][:, 0:n], pe_[:, 0:n], AF.Exp)
        P.tt('dve', Cs[i2][:, 0:n], C2[:, pc0:pc0 + n], ec[i2][:, 0:n], ALU.mult)
        for d in range(2):
            pb = banks.b[1 + d]
            P.mm(pb[:, 0:n], K['selrow'][0:2, d, :], cr[0:2, 0:n], start=True, stop=False)
            P.mm(pb[:, 0:n], K['identb'][:, :], K['negF' if d == 0 else 'negB'][:, 0:n], start=False, stop=True)
            P.tt('dve', SG[i2][:, d, 0:n].rearrange("p (a b) -> p a b", b=128),
                 pb[:, 0:n].rearrange("p (a b) -> p a b", b=128),
                 CQ[:, d, p0:p0 + npair].unsqueeze(2).to_broadcast([128, npair, 128]), ALU.subtract)
        P.act(SG[i2][:, :, 0:n], SG[i2][:, :, 0:n], AF.Exp)
        P.tt('pool', Ls[i2][:, 0:n], SG[i2][:, 0, 0:n], SG[i2][:, 1, 0:n], ALU.add)
        pcb = banks.b[3]
        for i in range(npair):
            cs = slice(pc0 + 128 * i, pc0 + 128 * (i + 1))
            P.mm(pcb[:, 128 * i:128 * (i + 1)], XB[64:128, cs], C2[64:128, cs])
        P.tt('dve', Mt[i2][:, 0:n], pcb[:, 0:n], Ls[i2][:, 0:n], ALU.mult)

    def stageB(ti, t0, n):
        i2 = ti % 2
        npair = n // 128
        p0 = t0 // 128
        pc0 = pcol(t0)
        Y = banks.b[4 + i2]
        for i in range(npair):
            p = p0 + i
            P.mm(Y[0:64, 128 * i:128 * (i + 1)], xBt[:, p, 0:64], Mt[i2][:, 128 * i:128 * (i + 1)], start=True, stop=False)
            for j in range(2):
                c = 2 * p + j
                kk = 64 if c == 3 else 128
                P.mm(Y[0:64, 128 * i + 64 * j:128 * i + 64 * (j + 1)], STb[0:kk, c + 1, :],
                     Cs[i2][0:kk, 128 * i + 64 * j:128 * i + 64 * (j + 1)], start=False, stop=(j == 1))
        P.stt(yv[i2][0:64, 0:n], XB[0:64, pc0:pc0 + n], par[0:64, 4:5], Y[0:64, 0:n], ALU.mult, ALU.add)
        y = yb[i2]
        P.stt(y[0:64, 0:n], yv[i2][0:64, 0:n], 0.5, sgz[0:64, t0:t0 + n], ALU.mult, ALU.mult)
        store_y(P, io, 3, t0, n, y[0:64, 0:n])

    nt2 = len(tiles2)
    for it in range(nt2 + 1):
        if it < nt2:
            stageA(*tiles2[it])
        if it >= 1:
            stageB(*tiles2[it - 1])

OFF_A, OFF_B, OFF_C, OFF_D = 0, 800, 1312, 2336


def rope_tables():
    n_freq = 8
    inv = (np.float32(10000.0) ** (-(np.arange(n_freq, dtype=np.float32)) / np.float32(n_freq))).astype(np.float32)
    t = np.arange(SEQ)
    pos_r = (t // 64).astype(np.float32)
    pos_c = (t % 64).astype(np.float32)
    ang_r = pos_r[:, None] * inv
    ang_c = pos_c[:, None] * inv
    ang = np.concatenate([ang_r, ang_r, ang_c, ang_c], axis=-1).astype(np.float32)
    cos = np.cos(ang).astype(np.float32).T
    sin = np.sin(ang).astype(np.float32).T
    sign = np.ones(32, np.float32)
    for a in range(2):
        sign[a * 16:a * 16 + 8] = -1.0
    sins = sin * sign[:, None]
    cosT = np.zeros((128, SEQ), np.float32)
    sinT = np.zeros((128, SEQ), np.float32)
    for c in range(2):
        cosT[64 * c:64 * c + 32] = cos
        sinT[64 * c:64 * c + 32] = sins
    return cosT, sinT


def rot_perm():
    perm = np.zeros(32, np.int64)
    for a in range(2):
        for f in range(8):
            perm[a * 16 + f] = a * 16 + 8 + f
            perm[a * 16 + 8 + f] = a * 16 + f
    return perm


def prep_w_attn(w_in_l, h):
    W = np.zeros((D, 640), np.float32)
    perm = rot_perm()
    for c in range(2):
        qc = OFF_C + h * 64 + c * 32
        kc = OFF_C + 256 + h * 64 + c * 32
        W[:, 64 * c:64 * c + 32] = w_in_l[:, qc:qc + 32]
        W[:, 128 + 64 * c:128 + 64 * c + 32] = w_in_l[:, qc + perm]
        W[:, 256 + 64 * c:256 + 64 * c + 32] = w_in_l[:, kc:kc + 32]
        W[:, 384 + 64 * c:384 + 64 * c + 32] = w_in_l[:, kc + perm]
    W[:, 512:576] = w_in_l[:, OFF_C + 768 + h * 64:OFF_C + 768 + h * 64 + 64]
    W[:, 576:640] = w_in_l[:, OFF_C + 512 + h * 64:OFF_C + 512 + h * 64 + 64]
    return W


def mix_consts():
    selZ = np.zeros((128, 64), np.float32)
    selZ[64, :] = 1.0
    t = np.arange(512)
    scanmask = np.zeros((64, 512), np.float32)
    scanmask[0:32] = (t % 64 != 0).astype(np.float32)[None]
    scanmask[32:64] = (t % 64 != 63).astype(np.float32)[None]
    j = np.arange(128)[:, None]
    i = np.arange(128)[None, :]
    same = (j // 64) == (i // 64)
    mF = (same & (j <= i)).astype(np.float32)
    mB = (same & (j >= i)).astype(np.float32)
    scanmask2 = np.zeros((128, 2, 128), np.float32)
    s_ = np.arange(128)
    scanmask2[:, 0, :] = (s_ % 64 != 0).astype(np.float32)[None]
    scanmask2[:, 1, :] = (s_ % 64 != 63).astype(np.float32)[None]
    sel2 = np.zeros((2, 128), np.float32)
    sel2[0, 0:64] = 1.0
    sel2[1, 64:128] = 1.0
    selrow = np.zeros((2, 2, 128), np.float32)
    selrow[0, 0, :] = 1.0
    selrow[1, 1, :] = 1.0
    NEG = -30000.0
    negF = np.where(mF > 0, 0.0, NEG).astype(np.float32)
    negB = np.where(mB > 0, 0.0, NEG).astype(np.float32)
    return {'selZ': selZ, 'ident': np.eye(128, dtype=np.float32), 'scanmask': scanmask,
            'maskF': np.tile(mF, (1, 4)), 'maskB': np.tile(mB, (1, 4)), 'scanmask2': scanmask2,
            'sel2': sel2, 'selrow': selrow, 'negF': np.tile(negF, (1, 4)), 'negB': np.tile(negB, (1, 4))}


def build_mix_kernel(nc, P, layer, with_ctx, mixers):
    banks = Banks(P)
    K = load_consts_tok(nc, P)
    epst = P.sbuf("eps_t", [128, 1], F32)
    P.memset('pool', epst[:], EPS)
    K['eps'] = epst
    selZ_d = nc.dram_tensor("selZ", [128, 64], F32, kind="ExternalInput").ap()
    selZ = P.sbuf("selZ_sb", [128, 64], F32)
    P.dma('sp', selZ[:], selZ_d)
    K['selZ'] = selZ
    ident_d = nc.dram_tensor("ident", [128, 128], F32, kind="ExternalInput").ap()
    ident = P.sbuf("ident_sb", [128, 128], F32)
    P.dma('sp', ident[:], ident_d)
    identb = P.sbuf("identb_sb", [128, 128], BF16)
    P.copy('pool', identb[:], ident[:])
    K['ident'] = ident
    K['identb'] = identb
    one = P.sbuf("one_t", [128, 1], F32)
    P.memset('pool', one[:], 1.0)
    K['one'] = one
    for nm, shp in (('scanmask', [64, 512]), ('maskF', [128, 512]), ('maskB', [128, 512]),
                    ('scanmask2', [128, 2, 128]), ('sel2', [2, 128]), ('selrow', [2, 2, 128])):
        d_ = nc.dram_tensor(nm, shp, F32, kind="ExternalInput").ap()
        t_ = P.sbuf(nm + "_sb", shp, F32)
        P.dma('sp', t_[:], d_)
        K[nm] = t_
    for nm in ('negF', 'negB'):
        d_ = nc.dram_tensor(nm, [128, 512], F32, kind="ExternalInput").ap()
        t_ = P.sbuf(nm + "_f", [128, 512], F32)
        P.dma('sp', t_[:], d_)
        tb_ = P.sbuf(nm + "_sb", [128, 512], BF16)
        P.copy('pool', tb_[:], t_[:])
        K[nm] = tb_
    io = {}
    xn_d = nc.dram_tensor("xn", [D, NTOK], BF16, kind="ExternalInput").ap()
    xn_v = xn_d.rearrange("(kc p) t -> p kc t", p=128)
    io['yT'] = nc.dram_tensor("yT", [4, 64, NTOK], BF16, kind="ExternalOutput").ap()
    A = P.arena("arena", 190 * 1024)
    if 'attn' in mixers:
        io['w_attn'] = nc.dram_tensor("w_attn", [D, 640], F32, kind="ExternalInput").ap()
        io['cosT'] = nc.dram_tensor("cosT", [128, SEQ], F32, kind="ExternalInput").ap()
        io['sinT'] = nc.dram_tensor("sinT", [128, SEQ], F32, kind="ExternalInput").ap()
        io['diff_lam'] = nc.dram_tensor("diff_lam", [128], F32, kind="ExternalInput").ap()
        io['diff_subln_w'] = nc.dram_tensor("diff_subln_w", [64, 1], F32, kind="ExternalInput").ap()
        attention_phase(nc, P, A, banks, K, xn_v, io, layer, with_ctx)
    if 'gla' in mixers:
        _add_gla(nc, P, A, banks, K, xn_v, io, layer, with_ctx)
    if 'ssd' in mixers:
        io['w_ssd'] = nc.dram_tensor("w_ssd", [D, 322], F32, kind="ExternalInput").ap()
        io['ssd_par'] = nc.dram_tensor("ssd_par", [128, 5], F32, kind="ExternalInput").ap()
        io['ssd_cpar'] = nc.dram_tensor("ssd_cpar", [128, 2, 5], F32, kind="ExternalInput").ap()
        io['ssd_scr'] = nc.dram_tensor("ssd_scr", [6, NTOK], F32, kind="Internal").ap()
        ssd_phase(nc, P, A, banks, K, xn_v, io, layer, with_ctx)
    if 'lru' in mixers:
        io['w_lru'] = nc.dram_tensor("w_lru", [D, 192], F32, kind="ExternalInput").ap()
        io['lru_par'] = nc.dram_tensor("lru_par", [128, 9], F32, kind="ExternalInput").ap()
        io['lru_gw'] = nc.dram_tensor("lru_gw", [64, 256], F32, kind="ExternalInput").ap()
        lru_phase(nc, P, A, banks, K, xn_v, io, layer, with_ctx)


def _add_gla(nc, P, A, banks, K, xn_v, io, layer, with_ctx):
    io['w_gla'] = nc.dram_tensor("w_gla", [D, 320], F32, kind="ExternalInput").ap()
    io['gla_par'] = nc.dram_tensor("gla_par", [64, 2], F32, kind="ExternalInput").ap()
    io['gla_w2bd'] = nc.dram_tensor("gla_w2bd", [32, 64], F32, kind="ExternalInput").ap()
    gla_phase(nc, P, A, banks, K, xn_v, io, layer, with_ctx)


def prep_gla(inp, l, h):
    w_in_l = inp['w_in'][l]
    W = np.zeros((D, 288), np.float32)
    q = w_in_l[:, OFF_A + h * 32:OFF_A + h * 32 + 32]
    k = w_in_l[:, OFF_A + 128 + h * 32:OFF_A + 128 + h * 32 + 32]
    W[:, 0:32] = q
    W[:, 32:64] = q
    W[:, 64:96] = k
    W[:, 96:128] = k
    W[:, 128:144] = w_in_l[:, OFF_A + 512:OFF_A + 528]
    W[:, 144:160] = w_in_l[:, OFF_A + 528:OFF_A + 544]
    W[:, 192:256] = w_in_l[:, OFF_A + 544 + h * 64:OFF_A + 544 + h * 64 + 64]
    W[:, 256:288] = 0
    Wv = w_in_l[:, OFF_A + 256 + h * 64:OFF_A + 256 + h * 64 + 64]
    W2 = np.zeros((D, 320), np.float32)
    W2[:, 0:256] = W[:, 0:256]
    W2[:, 256:320] = Wv
    par = np.zeros((64, 2), np.float32)
    par[0:32, 0] = inp['gla_b2'][l][0][h * 32:(h + 1) * 32]
    par[32:64, 0] = inp['gla_b2'][l][1][h * 32:(h + 1) * 32]
    par[:, 1] = inp['gla_norm_w'][l]
    w2bd = np.zeros((32, 64), np.float32)
    w2bd[0:16, 0:32] = inp['gla_w2'][l][0][:, h * 32:(h + 1) * 32]
    w2bd[16:32, 32:64] = inp['gla_w2'][l][1][:, h * 32:(h + 1) * 32]
    return {"w_gla": W2, "gla_par": par, "gla_w2bd": w2bd}


def prep_ssd(inp, l, h):
    w_in_l = inp['w_in'][l]
    gr = h // 2
    W = np.zeros((D, 322), np.float32)
    cx = slice(OFF_D + h * 64, OFF_D + h * 64 + 64)
    cB = slice(OFF_D + 256 + gr * 64, OFF_D + 256 + gr * 64 + 64)
    cC = slice(OFF_D + 384 + gr * 64, OFF_D + 384 + gr * 64 + 64)
    W[:, 0:64] = w_in_l[:, cx]
    W[:, 64:128] = w_in_l[:, cB]
    W[:, 128:192] = w_in_l[:, cC]
    W[:, 192:256] = w_in_l[:, cC]
    W[:, 256:320] = w_in_l[:, OFF_D + 520 + h * 64:OFF_D + 520 + h * 64 + 64]
    W[:, 320] = w_in_l[:, OFF_D + 512 + h]
    W[:, 321] = w_in_l[:, OFF_D + 516 + h]
    par = np.zeros((128, 5), np.float32)
    par[:, 0] = inp['ssd_dt_bias'][l][0][h]
    par[:, 1] = inp['ssd_dt_bias'][l][1][h]
    par[:, 2] = inp['ssd_a_log'][l][0][h]
    par[:, 3] = inp['ssd_a_log'][l][1][h]
    par[:, 4] = inp['ssd_d'][l][h]
    cw, cb = inp['ssd_conv_w'][l], inp['ssd_conv_b'][l]
    cpar = np.zeros((128, 2, 5), np.float32)
    ch = [np.r_[h * 64:h * 64 + 64, 256 + gr * 64:256 + gr * 64 + 64],
          np.r_[384 + gr * 64:384 + gr * 64 + 64, 384 + gr * 64:384 + gr * 64 + 64]]
    for g in range(2):
        cpar[:, g, 0:4] = cw[:, ch[g]].T
        cpar[:, g, 4] = cb[ch[g]]
    return {"w_ssd": W, "ssd_par": par, "ssd_cpar": cpar}


def prep_lru(inp, l, h):
    w_in_l = inp['w_in'][l]
    W = np.zeros((D, 192), np.float32)
    xs = w_in_l[:, OFF_B + h * 64:OFF_B + h * 64 + 64]
    W[:, 0:64] = xs
    W[:, 64:128] = xs
    W[:, 128:192] = w_in_l[:, OFF_B + 256 + h * 64:OFF_B + 256 + h * 64 + 64]
    sl = slice(h * 64, h * 64 + 64)
    par = np.zeros((128, 9), np.float32)
    for d in range(2):
        rows = slice(64 * d, 64 * d + 64)
        par[rows, 0:4] = inp['lru_conv_w'][l][:, sl].T
        par[rows, 4] = inp['lru_conv_b'][l][sl]
        par[rows, 5] = inp['lru_ba'][l][d][sl]
        par[rows, 6] = inp['lru_bx'][l][d][sl]
        par[rows, 7] = inp['lru_lam'][l][d][sl]
    gw = np.zeros((64, 256), np.float32)
    for d in range(2):
        gw[:, 64 * d:64 * d + 64] = inp['lru_wa'][l][d][h]
        gw[:, 128 + 64 * d:128 + 64 * d + 64] = inp['lru_wx'][l][d][h]
    return {"w_lru": W, "lru_par": par, "lru_gw": gw}


def mix_inputs(inp, l, h, mixers, cosT, sinT, cst):
    m = dict(cst)
    if 'ssd' in mixers:
        m.update(prep_ssd(inp, l, h))
    if 'gla' in mixers:
        m.update(prep_gla(inp, l, h))
    if 'attn' in mixers:
        m.update({"w_attn": prep_w_attn(inp['w_in'][l], h), "cosT": cosT, "sinT": sinT,
                  "diff_lam": np.ascontiguousarray(inp['diff_lam'][l].reshape(-1)),
                  "diff_subln_w": np.ascontiguousarray(inp['diff_subln_w'][l].reshape(64, 1))})
    if 'lru' in mixers:
        m.update(prep_lru(inp, l, h))
    return m


def _rt_of(hl, hc, b, q):
    return np.ascontiguousarray(np.concatenate([hc[b, 64 * q:64 * q + 64].T, hl[b, 2048 * q:2048 * (q + 1)].T], axis=1))


def _gather_cols(parts):
    return np.ascontiguousarray(np.concatenate([p[:, 0:64] for p in parts] + [p[:, 64:] for p in parts], axis=1))


def kernel_unfused(**inp):
    inp = {k: np.asarray(v) for k, v in inp.items()}
    x, ctx, c, c_ctx = inp['x'], inp['ctx'], inp['c'], inp['c_ctx']
    cosT, sinT = rope_tables()
    cst = mix_consts()
    cvecs = [np.ascontiguousarray(np.stack([pk(c[b]), pk(c_ctx)], axis=2)) for b in range(2)]
    maps = []
    for core in range(8):
        b, q = core // 4, core % 4
        maps.append({"RT": _rt_of(x, ctx, b, q), "cvec": cvecs[b], "wmod_n": inp['w_mod'][0],
                     "bmod_n": pk(inp['b_mod'][0]), "normw": pk(inp['norm_w'][0])})
    res = _launch(lambda nc, P: build_tok_kernel(nc, P, True, False, False), maps)
    RT = [m["RT"] for m in maps]
    xnT = [np.asarray(r["xnT"]) for r in res]
    out = None
    for l in range(2):
        last = (l == 1)
        xn_full = [_gather_cols(xnT[4 * b:4 * b + 4]) for b in range(2)]
        maps = []
        for core in range(8):
            b, h = core // 4, core % 4
            m = {"xn": xn_full[b]}
            m.update(mix_inputs(inp, l, h, ('gla', 'lru', 'attn', 'ssd'), cosT, sinT, cst))
            maps.append(m)
        res = _launch(lambda nc, P: build_mix_kernel(nc, P, l, not last, ('gla', 'lru', 'attn', 'ssd')), maps)
        yT = [np.asarray(r["yT"]) for r in res]
        maps = []
        for core in range(8):
            b, q = core // 4, core % 4
            yb = np.stack([yT[4 * b + h] for h in range(4)], axis=1).reshape(1024, NTOK)
            yq = np.ascontiguousarray(np.concatenate([yb[:, 64 * q:64 * q + 64],
                                                      yb[:, CTXL + 2048 * q:CTXL + 2048 * (q + 1)]], axis=1))
            m = {"RT": RT[core], "cvec": cvecs[b], "yT": yq, "wout": inp['w_out'][l], "ssdnw": pk(inp['ssd_norm_w'][l]),
                 "wmod_g": inp['w_mod'][l], "bmod_g": pk(inp['b_mod'][l])}
            if last:
                m["fnw"] = pk(inp['final_norm_w'])
            else:
                m.update({"wmod_n": inp['w_mod'][l + 1], "bmod_n": pk(inp['b_mod'][l + 1]), "normw": pk(inp['norm_w'][l + 1])})
            maps.append(m)
        res = _launch(lambda nc, P: build_tok_kernel(nc, P, False, last, True), maps)
        if last:
            out = np.zeros((2, SEQ, D), np.float32)
            for core in range(8):
                b, q = core // 4, core % 4
                out[b, 2048 * q:2048 * (q + 1), :] = np.asarray(res[core]["outT"]).T
        else:
            RT = [np.asarray(r["RTout"]) for r in res]
            xnT = [np.asarray(r["xnT"]) for r in res]
    return out

CHUNKS = [(0, 256)] + [(256 + 2048 * k, 2048) for k in range(4)]
GROUPS = [[0, 1, 2, 3], [4, 5, 6, 7]]


def fs_mod(nc, P, K, banks, A, cvec_d, wmod_d, bmod_d, name):
    cT = A.alloc([8, 2], F32)
    P.dma('sp', cT[:], cvec_d)
    e = A.alloc([8, 2], F32)
    P.act(e[:], cT[:], AF.Exp, scale=-1.0)
    P.ts('dve', e[:], e[:], 1.0, None, ALU.add)
    P.recip(e[:], e[:])
    sc = A.alloc([8, 2], F32)
    P.tt('dve', sc[:], cT[:], e[:], ALU.mult)
    bT = A.alloc([6], F32)
    P.dma('sp', bT[:], bmod_d)
    wb = A.alloc([8, 768], F32)
    for kc in range(8):
        P.dma('sp' if kc % 2 == 0 else 'act', wb[:, kc, :], wmod_d[kc * 128:(kc + 1) * 128, :])
    ps = banks.b[7]
    for fc in range(6):
        for kc in range(8):
            P.mm(ps[:, fc * 2:fc * 2 + 2], wb[:, kc, fc * 128:(fc + 1) * 128], sc[:, kc, :], start=(kc == 0), stop=(kc == 7))
    modT = P.sbuf(name, [128, 6, 2], F32)
    P.tt('dve', modT[:], ps[:, 0:12].rearrange("p (a b) -> p a b", b=2), bT[:].unsqueeze(2).to_broadcast([128, 6, 2]), ALU.add)
    return modT


def fs_token_phase(nc, P, K, banks, A, io, L, mode):
    A.reset()
    D_ = io['dram']
    first, last = mode == 'first', mode == 'last'
    if not first:
        gateT = fs_mod(nc, P, K, banks, A, io['cvec'], io['wmod'][L], io['bmod'][L], "modT_g%d" % L)
        wo32 = A.alloc([8, 256], F32)
        for kc in range(8):
            P.dma('sp' if kc % 2 == 0 else 'act', wo32[:, kc, :], io['wout'][L][kc * 128:(kc + 1) * 128, :])
        wo = A.alloc([8, 256], BF16)
        P.copy('dve', wo[:], wo32[:])
        sw = A.alloc([4], F32)
        P.dma('sp', sw[:], io['ssdnw'][L])
    if not last:
        LN = 0 if first else L + 1
        modN = fs_mod(nc, P, K, banks, A, io['cvec'], io['wmod'][LN], io['bmod'][LN], "modT_n%d" % LN)
        nw = A.alloc([2], F32)
        P.dma('sp', nw[:], io['normw'][LN])
        Acoef = A.alloc([2, 2], F32)
        P.ts('dve', Acoef[:], modN[:, 2:4, :], 1.0, None, ALU.add)
        P.tt('dve', Acoef[:], Acoef[:], nw[:].unsqueeze(2).to_broadcast([128, 2, 2]), ALU.mult)
    else:
        fw = A.alloc([2], F32)
        P.dma('sp', fw[:], io['fnw'])
    Rb = [A.alloc([2, 2048], F32) for _ in range(2)]
    yb_ = [A.alloc([8, 512], BF16) for _ in range(2)]
    sq = [A.alloc([512], BF16) for _ in range(4)]
    rst = [A.alloc([512], F32) for _ in range(2)]
    t1 = [A.alloc([512], F32) for _ in range(4)]
    ssr = [A.alloc([2048], F32) for _ in range(2)]
    ssg = [A.alloc([2048], F32) for _ in range(2)]
    xo = [A.alloc([2, 512], BF16) for _ in range(2)]
    Rsrc = io['RT_in'] if first else D_['Rs']
    Rsrc_v = Rsrc.rearrange("(fc p) t -> p fc t", p=128)
    Rs_v = D_['Rs'].rearrange("(fc p) t -> p fc t", p=128)
    stage = 'n%d' % (0 if first else L + 1) if not last else 'fin'
    chunks = [(ci, t0, n) for ci, (t0, n) in enumerate(CHUNKS) if not (last and ci == 0)]
    jobs = []
    for (ci, t0, n) in chunks:
        subs = [(s0, min(512, n - s0)) for s0 in range(0, n, 512)]
        for si, (s0, m_) in enumerate(subs):
            jobs.append(dict(ci=ci, t0=t0, n=n, s0=s0, m=m_, first=(si == 0), lastsub=(si == len(subs) - 1)))
    ybuf3 = yb_ + [A.alloc([8, 512], BF16)]

    def P1(j, ji):
        ci, t0, n, s0, m = j['ci'], j['t0'], j['n'], j['s0'], j['m']
        R = Rb[ci % 2]
        if j['first']:
            P.dma('sp', R[:, :, 0:n], Rsrc_v[:, :, t0:t0 + n])
        if first:
            return
        gy = D_['gy%d' % L][ci].rearrange("(kc p) t -> p kc t", p=128)
        yt = ybuf3[ji % 3]
        P.dma('act', yt[:, :, 0:m], gy[:, :, s0:s0 + m])
        ps = banks.b[0 + ji % 2]
        for i, kc in enumerate((1, 3, 5, 7)):
            s_ = sq[i % 2]
            P.act(s_[64:128, 0:m], yt[64:128, kc, 0:m], AF.Square)
            P.mm(ps[:, 0:m], K['onesb'][64:128, :], s_[64:128, 0:m], start=(i == 0), stop=(i == 3))
        r_ = rst[ji % 2]
        P.act(r_[:, 0:m], ps[:, 0:m], AF.Ln, scale=1.0 / 256, bias=K['eps'][:])
        P.act(r_[:, 0:m], r_[:, 0:m], AF.Exp, scale=-0.5)
        for i, kc in enumerate((1, 3, 5, 7)):
            P.stt(yt[64:128, kc, 0:m], yt[64:128, kc, 0:m], sw[64:128, i:i + 1], r_[64:128, 0:m], ALU.mult, ALU.mult)

    def P2(j, ji):
        if first:
            return
        ci, s0, m = j['ci'], j['s0'], j['m']
        isctx = 1 if ci == 0 else 0
        R = Rb[ci % 2]
        yt = ybuf3[ji % 3]
        for fc in range(2):
            po = banks.b[2 + (2 * ji + fc) % 4]
            for kc in range(8):
                P.mm(po[:, 0:m], wo[:, kc, fc * 128:(fc + 1) * 128], yt[:, kc, 0:m], start=(kc == 0), stop=(kc == 7))
            P.stt(R[:, fc, s0:s0 + m], po[:, 0:m], gateT[:, 4 + fc, isctx:isctx + 1], R[:, fc, s0:s0 + m], ALU.mult, ALU.add)

    def P3(j, ji):
        ci, t0, n, s0, m = j['ci'], j['t0'], j['n'], j['s0'], j['m']
        R = Rb[ci % 2]
        srow = ssr[ci % 2]
        pss = banks.b[6 + ji % 2]
        for fc in range(2):
            s_ = sq[2 + fc]
            P.act(s_[:, 0:m], R[:, fc, s0:s0 + m], AF.Square)
            P.mm(pss[:, 0:m], K['onesb'][:, :], s_[:, 0:m], start=(fc == 0), stop=(fc == 1))
        P.copy('dve', srow[0:1, s0:s0 + m], pss[0:1, 0:m])
        if j['lastsub']:
            P.dma('sp', D_['ssb_' + stage][:, t0:t0 + n], srow[0:1, 0:n])
            P.dma('sp', Rs_v[:, :, t0:t0 + n], R[:, :, 0:n])

    nj = len(jobs)
    for it in range(nj + 2):
        if it < nj:
            P1(jobs[it], it)
        if 0 <= it - 1 < nj:
            P2(jobs[it - 1], it - 1)
        if 0 <= it - 2 < nj:
            P3(jobs[it - 2], it - 2)
    P.collective("AllGather", [D_['ssb_' + stage]], [D_['ssg_' + stage]], GROUPS)
    for (ci, t0, n) in chunks:
        isctx = 1 if ci == 0 else 0
        R = Rb[ci % 2]
        P.dma('sp', R[:, :, 0:n], Rs_v[:, :, t0:t0 + n])
        subs = [(s0, min(512, n - s0)) for s0 in range(0, n, 512)]
        sg_ = ssg[ci % 2]
        P.dma('act', sg_[0:4, 0:n], D_['ssg_' + stage][:, t0:t0 + n])
        for si, (s0, m) in enumerate(subs):
            pt = banks.b[6 + si % 2]
            P.mm(pt[:, 0:m], K['ones'][0:4, :], sg_[0:4, s0:s0 + m])
            r_ = rst[si % 2]
            P.act(r_[:, 0:m], pt[:, 0:m], AF.Ln, scale=1.0 / D, bias=K['eps'][:])
            P.act(r_[:, 0:m], r_[:, 0:m], AF.Exp, scale=-0.5)
            if not last:
                x_ = xo[si % 2]
                for fc in range(2):
                    t_ = t1[fc]
                    P.stt(t_[:, 0:m], R[:, fc, s0:s0 + m], Acoef[:, fc, isctx:isctx + 1], r_[:, 0:m], ALU.mult, ALU.mult)
                    P.act(x_[:, fc, 0:m], t_[:, 0:m], AF.Identity, bias=modN[:, fc, isctx:isctx + 1])
                xb_v = D_['xnb_' + stage][ci].rearrange("(fc p) t -> p fc t", p=128)
                P.dma('sp', xb_v[:, :, s0:s0 + m], x_[:, :, 0:m])
            else:
                o_v = io['outT'].rearrange("(fc p) t -> p fc t", p=128)
                for fc in range(2):
                    t_ = t1[(si * 2 + fc) % 4]
                    P.stt(t_[:, 0:m], R[:, fc, s0:s0 + m], fw[:, fc:fc + 1], r_[:, 0:m], ALU.mult, ALU.mult)
                    P.dma('sp', o_v[:, fc, t0 - CTXL + s0:t0 - CTXL + s0 + m], t_[:, 0:m], final=True)
        if not last:
            P.collective("AllGather", [D_['xnb_' + stage][ci]], [D_['xng_' + stage][ci]], GROUPS)


def build_fused(nc, P):
    banks = Banks(P)
    K = load_consts_tok(nc, P)
    epst = P.sbuf("eps_t", [128, 1], F32)
    P.memset('pool', epst[:], EPS)
    K['eps'] = epst
    one = P.sbuf("one_t", [128, 1], F32)
    P.memset('pool', one[:], 1.0)
    K['one'] = one
    for nm, shp in (('selZ', [128, 64]), ('ident', [128, 128]), ('scanmask', [64, 512]), ('maskF', [128, 512]),
                    ('maskB', [128, 512]), ('scanmask2', [128, 2, 128]), ('sel2', [2, 128]), ('selrow', [2, 2, 128])):
        d_ = nc.dram_tensor(nm, shp, F32, kind="ExternalInput").ap()
        t_ = P.sbuf(nm + "_sb", shp, F32)
        P.dma('sp', t_[:], d_)
        K[nm] = t_
    identb = P.sbuf("identb_sb", [128, 128], BF16)
    P.copy('pool', identb[:], K['ident'][:])
    K['identb'] = identb
    onesb = P.sbuf("onesb_sb", [128, 128], BF16)
    P.memset('pool', onesb[:], 1.0)
    K['onesb'] = onesb
    for nm in ('negF', 'negB'):
        d_ = nc.dram_tensor(nm, [128, 512], F32, kind="ExternalInput").ap()
        t_ = P.sbuf(nm + "_f", [128, 512], F32)
        P.dma('sp', t_[:], d_)
        tb_ = P.sbuf(nm + "_sb", [128, 512], BF16)
        P.copy('pool', tb_[:], t_[:])
        K[nm] = tb_
    A = P.arena("arena", 184 * 1024)

    def din(name, shape, dt=F32):
        return nc.dram_tensor(name, list(shape), dt, kind="ExternalInput").ap()

    def dscr(name, shape, dt=F32):
        return nc.dram_tensor(name, list(shape), dt).ap()

    io = {'RT_in': din("RT", [256, NTOK]), 'cvec': din("cvec", [128, 8, 2]),
          'wmod': [din("wmod%d" % l, [D, 768]) for l in range(2)], 'bmod': [din("bmod%d" % l, [128, 6]) for l in range(2)],
          'normw': [din("normw%d" % l, [128, 2]) for l in range(2)], 'wout': [din("wout%d" % l, [D, 256]) for l in range(2)],
          'ssdnw': [din("ssdnw%d" % l, [128, 4]) for l in range(2)], 'fnw': din("fnw", [128, 2]),
          'outT': nc.dram_tensor("outT", [256, SEQ], F32, kind="ExternalOutput").ap()}
    Dm = {'Rs': dscr("Rs", [256, NTOK])}
    for stage in ('n0', 'n1', 'fin'):
        Dm['ssb_' + stage] = dscr("ssb_%s" % stage, [1, NTOK])
        Dm['ssg_' + stage] = dscr("ssg_%s" % stage, [4, NTOK])
    for stage in ('n0', 'n1'):
        Dm['xnb_' + stage] = [dscr("xnb_%s_%d" % (stage, ci), [256, n], BF16) for ci, (t0, n) in enumerate(CHUNKS)]
        Dm['xng_' + stage] = [dscr("xng_%s_%d" % (stage, ci), [D, n], BF16) for ci, (t0, n) in enumerate(CHUNKS)]
    for l in range(2):
        Dm['yb%d' % l] = [dscr("yb%d_%d" % (l, ci), [256, n], BF16) for ci, (t0, n) in enumerate(CHUNKS)]
        Dm['gy%d' % l] = [dscr("gy%d_%d" % (l, ci), [D, n], BF16) for ci, (t0, n) in enumerate(CHUNKS)]
    io['dram'] = Dm
    cosT = din("cosT", [128, SEQ])
    sinT = din("sinT", [128, SEQ])
    fs_token_phase(nc, P, K, banks, A, io, 0, 'first')
    for l in range(2):
        last = (l == 1)
        with_ctx = not last
        xng = Dm['xng_n%d' % l]

        def xn_load(xb, t0, n, xng=xng):
            if t0 < CTXL:
                P.dma('sp', xb[:, :, 0:n], xng[0].rearrange("(kc p) t -> p kc t", p=128)[:, :, t0:t0 + n])
            else:
                k = (t0 - CTXL) // 2048
                c0 = (t0 - CTXL) % 2048
                P.dma('sp', xb[:, :, 0:n], xng[k + 1].rearrange("(kc p) t -> p kc t", p=128)[:, :, c0:c0 + n])

        ybl = Dm['yb%d' % l]

        def y_store(mi, t0, n, src, ybl=ybl):
            if t0 < CTXL:
                P.dma('pool', ybl[0][mi * 64:(mi + 1) * 64, t0:t0 + n], src)
            else:
                k = (t0 - CTXL) // 2048
                c0 = (t0 - CTXL) % 2048
                P.dma('pool', ybl[k + 1][mi * 64:(mi + 1) * 64, c0:c0 + n], src)

        mio = {'y_store': y_store, 'cosT': cosT, 'sinT': sinT}
        mio['w_gla'] = din("w_gla%d" % l, [D, 320])
        mio['gla_par'] = din("gla_par%d" % l, [64, 2])
        mio['gla_w2bd'] = din("gla_w2bd%d" % l, [32, 64])
        mio['w_lru'] = din("w_lru%d" % l, [D, 192])
        mio['lru_par'] = din("lru_par%d" % l, [128, 9])
        mio['lru_gw'] = din("lru_gw%d" % l, [64, 256])
        mio['w_ssd'] = din("w_ssd%d" % l, [D, 322])
        mio['ssd_par'] = din("ssd_par%d" % l, [128, 5])
        mio['ssd_cpar'] = din("ssd_cpar%d" % l, [128, 2, 5])
        mio['ssd_scr'] = dscr("ssd_scr%d" % l, [6, NTOK])
        mio['w_attn'] = din("w_attn%d" % l, [D, 640])
        mio['diff_lam'] = din("diff_lam%d" % l, [128])
        mio['diff_subln_w'] = din("diff_subln_w%d" % l, [64, 1])
        gla_phase(nc, P, A, banks, K, xn_load, mio, l, with_ctx)
        lru_phase(nc, P, A, banks, K, xn_load, mio, l, with_ctx)
        ssd_phase(nc, P, A, banks, K, xn_load, mio, l, with_ctx)
        def after_q(q0, nq, l=l):
            if q0 < CTXL:
                P.collective("AllGather", [Dm['yb%d' % l][0]], [Dm['gy%d' % l][0]], GROUPS)
            elif (q0 - CTXL + nq) % 2048 == 0:
                k = (q0 - CTXL) // 2048
                P.collective("AllGather", [Dm['yb%d' % l][k + 1]], [Dm['gy%d' % l][k + 1]], GROUPS)
        mio['after_q'] = after_q
        attention_phase(nc, P, A, banks, K, xn_load, mio, l, with_ctx)
        fs_token_phase(nc, P, K, banks, A, io, l, 'last' if last else 'mid')


def fused_inputs(inp, core, cosT, sinT, cst):
    b, q = core // 4, core % 4
    h = q
    x, ctx, c, c_ctx = inp['x'], inp['ctx'], inp['c'], inp['c_ctx']
    fsl = slice(256 * q, 256 * q + 256)
    m = dict(cst)
    m['RT'] = np.ascontiguousarray(np.concatenate([ctx[b][:, fsl].T, x[b][:, fsl].T], axis=1))
    m['cvec'] = np.ascontiguousarray(np.stack([pk(c[b]), pk(c_ctx)], axis=2))
    m['cosT'] = cosT
    m['sinT'] = sinT
    m['fnw'] = pk(inp['final_norm_w'][fsl])
    perm = np.array([mi * 256 + hh * 64 + j for hh in range(4) for mi in range(4) for j in range(64)])
    for l in range(2):
        cols = np.r_[256 * q:256 * q + 256, 1024 + 256 * q:1024 + 256 * q + 256, 2048 + 256 * q:2048 + 256 * q + 256]
        m['wmod%d' % l] = np.ascontiguousarray(inp['w_mod'][l][:, cols])
        m['bmod%d' % l] = pk(inp['b_mod'][l][cols])
        m['normw%d' % l] = pk(inp['norm_w'][l][fsl])
        m['wout%d' % l] = np.ascontiguousarray(inp['w_out'][l][perm][:, fsl])
        sw = np.zeros((128, 4), np.float32)
        for i in range(4):
            sw[64:128, i] = inp['ssd_norm_w'][l][i * 64:(i + 1) * 64]
        m['ssdnw%d' % l] = sw
        mi_ = mix_inputs(inp, l, h, ('gla', 'lru', 'attn', 'ssd'), cosT, sinT, cst)
        for k_, v_ in mi_.items():
            if k_ in cst or k_ in ('cosT', 'sinT'):
                continue
            m[k_ + str(l)] = v_
    return m


def kernel(**inp):
    inp = {k: np.asarray(v) for k, v in inp.items()}
    cosT, sinT = rope_tables()
    cst = mix_consts()
    maps = [fused_inputs(inp, core, cosT, sinT, cst) for core in range(8)]
    res = _launch(build_fused, maps)
    out = np.zeros((2, SEQ, D), np.float32)
    for core in range(8):
        b, q = core // 4, core % 4
        out[b, :, 256 * q:256 * q + 256] = np.asarray(res[core]["outT"]).T
    return out
```

```python
import numpy as np
from contextlib import ExitStack
import concourse.bass as bass
import concourse.mybir as mybir
from concourse.bass_utils import run_bass_kernel_spmd

F32 = mybir.dt.float32
BF16 = mybir.dt.bfloat16
AF = mybir.ActivationFunctionType
ALU = mybir.AluOpType
AX = mybir.AxisListType
ENGS = ('pe', 'act', 'dve', 'pool', 'sp')
ESZ = {F32: 4, BF16: 2}


class Prog:
    def __init__(self, nc, stack, n_dma_sems=40):
        self.nc = nc
        self.stack = stack
        self.sem = {e: stack.enter_context(nc.semaphore('sem_' + e)) for e in ENGS}
        self.cnt = {e: 0 for e in ENGS}
        self.known = {e: {} for e in ENGS}
        self.stream = {e: [] for e in ENGS}
        self.dsem = [stack.enter_context(nc.semaphore('dsem%d' % i)) for i in range(n_dma_sems)]
        self.dcum = [0] * n_dma_sems
        self.drr = 0
        self.rows = {}
        self.trk = {}
        self.final_events = []
        self.nwaits = 0
        self.ndma = 0
        self.arenas = {}

    def sbuf(self, name, shape, dtype=F32):
        t = self.stack.enter_context(self.nc.sbuf_tensor(name, list(shape), dtype))
        self.rows[name] = int(np.prod(shape[1:])) * ESZ[dtype]
        return t

    def psum(self, name, shape, dtype=F32):
        t = self.stack.enter_context(self.nc.psum_tensor(name, list(shape), dtype))
        self.rows[name] = int(np.prod(shape[1:])) * ESZ[dtype]
        return t

    def arena(self, name, nbytes):
        t = self.sbuf(name, [128, nbytes // 4], F32)
        a = Arena(self, name, t, nbytes)
        self.arenas[name] = a
        return a

    def _box(self, ap):
        b = self._box0(ap)
        a = self.arenas.get(b[0])
        if a is not None:
            rid = a.region_of(b[3], b[4])
            return ((b[0], rid),) + b[1:]
        return b

    def _box0(self, ap):
        name = ap.tensor.name
        es = ESZ.get(ap.dtype, 4)
        off = int(ap.offset) * es
        pairs = ap.ap
        if name in self.rows:
            rs = self.rows[name]
            p0 = off // rs
            pst, pc = pairs[0]
            p1 = p0 + (pc if pst != 0 else 1)
            fo = off % rs
            rest = pairs[1:]
        else:
            p0, p1 = 0, 1
            fo = off
            rest = pairs
        lo = fo
        hi = fo
        for st, c in rest:
            d = st * (c - 1) * es
            if d < 0:
                lo += d
            else:
                hi += d
        return (name, p0, p1, lo, hi + es)

    @staticmethod
    def _ov(a, b):
        return a[1] < b[2] and b[1] < a[2] and a[3] < b[4] and b[3] < a[4]

    @staticmethod
    def _inside(a, b):
        return a[1] >= b[1] and a[2] <= b[2] and a[3] >= b[3] and a[4] <= b[4]

    def _deps(self, reads, writes):
        deps = []
        for ap in reads:
            b = self._box(ap)
            t = self.trk.get(b[0])
            if t:
                for (wb, ev, clk) in t['w']:
                    if self._ov(b, wb):
                        deps.append((ev, clk, 'raw'))
        for ap in writes:
            b = self._box(ap)
            t = self.trk.get(b[0])
            if t:
                for (wb, ev, clk) in t['w']:
                    if self._ov(b, wb):
                        deps.append((ev, clk, 'waw'))
                for (rb, ev, clk) in t['r']:
                    if self._ov(b, rb):
                        deps.append((ev, clk, 'war'))
        return deps

    @staticmethod
    def _merge(lst):
        merged = {}
        for (rb, rev, rclk) in lst:
            m = merged.get(rev[0])
            if m is None:
                merged[rev[0]] = (rb, rev, rclk)
            else:
                mb, mev, mclk = m
                nb = (rb[0], min(rb[1], mb[1]), max(rb[2], mb[2]), min(rb[3], mb[3]), max(rb[4], mb[4]))
                merged[rev[0]] = (nb, rev, rclk) if rev[1] > mev[1] else (nb, mev, mclk)
        return list(merged.values())

    def _record(self, reads, writes, ev, clk):
        for ap in reads:
            b = self._box(ap)
            t = self.trk.setdefault(b[0], {'w': [], 'r': []})
            rl = t['r']
            for i, (rb, rev, rclk) in enumerate(rl):
                if rev[0] == ev[0] and self._inside(rb, b):
                    rl[i] = (b, ev, clk)
                    break
            else:
                rl.append((b, ev, clk))
                if len(rl) > 40:
                    t['r'] = self._merge(rl)
        for ap in writes:
            b = self._box(ap)
            t = self.trk.setdefault(b[0], {'w': [], 'r': []})
            t['w'] = [e for e in t['w'] if not self._inside(e[0], b)]
            t['r'] = [e for e in t['r'] if not self._inside(e[0], b)]
            t['w'].append((b, ev, clk))
            if len(t['w']) > 40:
                t['w'] = self._merge(t['w'])

    def _waits_for(self, eng, deps):
        kn = self.known[eng]
        waits = {}
        for (ev, clk, kind) in deps:
            key, val = ev
            if key == eng:
                if eng == 'pe':
                    continue
                if eng in ('act', 'dve') and kind != 'raw':
                    continue
            if kn.get(key, 0) >= val:
                continue
            if waits.get(key, 0) < val:
                waits[key] = val
        for (ev, clk, kind) in deps:
            key, val = ev
            if key in waits and waits[key] >= val:
                for k2, v2 in clk.items():
                    if kn.get(k2, 0) < v2:
                        kn[k2] = v2
        for k, v in waits.items():
            if kn.get(k, 0) < v:
                kn[k] = v
        return list(waits.items())

    def op(self, eng, fn, reads=(), writes=()):
        deps = self._deps(reads, writes)
        waits = self._waits_for(eng, deps)
        self.cnt[eng] += 1
        ev = (eng, self.cnt[eng])
        clk = dict(self.known[eng])
        clk[eng] = self.cnt[eng]
        self.stream[eng].append((waits, fn, ('e', eng)))
        self._record(reads, writes, ev, clk)
        self.nwaits += len(waits)
        return ev

    def dma(self, q, out, in_, final=False, **kw):
        i = self.drr
        self.drr = (self.drr + 1) % len(self.dsem)
        deps = self._deps([in_], [out])
        key = ('d', i)
        deps.append(((key, self.dcum[i]), {}, 'raw'))
        waits = self._waits_for(q, deps)
        self.dcum[i] += 16
        ev = (key, self.dcum[i])
        clk = dict(self.known[q])
        clk[key] = self.dcum[i]
        self.stream[q].append((waits, (lambda e, out=out, in_=in_, kw=kw: e.dma_start(out=out, in_=in_, **kw)), ('d', i)))
        self._record([in_], [out], ev, clk)
        if final:
            self.final_events.append(ev)
        self.nwaits += len(waits)
        self.ndma += 1
        return ev

    def collective(self, kind, ins, outs, groups, q='pool'):
        if not hasattr(self, 'csem'):
            self.csem = []
            self.ccum = []
        self.csem.append(self.stack.enter_context(self.nc.semaphore('csem%d' % len(self.csem))))
        self.ccum.append(0)
        i = len(self.csem) - 1
        deps = self._deps(list(ins), list(outs))
        key = ('c', i)
        waits = self._waits_for(q, deps)
        self.ccum[i] += 1
        ev = (key, self.ccum[i])
        clk = dict(self.known[q])
        clk[key] = self.ccum[i]
        self.stream[q].append((waits, (lambda e: e.collective_compute(kind, ALU.bypass, replica_groups=groups,
                                                                      ins=[a.opt() for a in ins], outs=[a.opt() for a in outs])),
                               ('c', i)))
        self._record(list(ins), list(outs), ev, clk)
        self.nwaits += len(waits)
        return ev

    def finish(self, eng='sp'):
        self.stream[eng].append((list(self.final_events), None, None))

    def _semobj(self, key):
        if isinstance(key, tuple):
            return self.dsem[key[1]] if key[0] == 'd' else self.csem[key[1]]
        return self.sem[key]

    def emit(self):
        for e in ENGS:
            assert self.cnt[e] < 60000, (e, self.cnt[e])

        def replay(name, eng):
            for (waits, fn, inc) in self.stream[name]:
                for (key, val) in waits:
                    eng.wait_ge(self._semobj(key), val)
                if fn is None:
                    continue
                ins = fn(eng)
                if inc[0] == 'e':
                    ins.then_inc(self.sem[inc[1]], 1)
                elif inc[0] == 'c':
                    ins.then_inc(self.csem[inc[1]])
                else:
                    ins.then_inc(self.dsem[inc[1]], 16)

        with self.nc.Block() as block:
            @block.sync
            def _(e):
                replay('sp', e)

            @block.scalar
            def _(e):
                replay('act', e)

            @block.vector
            def _(e):
                replay('dve', e)

            @block.gpsimd
            def _(e):
                replay('pool', e)

            @block.tensor
            def _(e):
                replay('pe', e)

    def mm(self, out, lhsT, rhs, start=True, stop=True):
        return self.op('pe', lambda e: e.matmul(out, lhsT, rhs, start=start, stop=stop),
                       reads=[lhsT, rhs], writes=[out])

    def transpose(self, out, in_, ident):
        return self.op('pe', lambda e: e.transpose(out, in_, ident), reads=[in_, ident], writes=[out])

    def act(self, out, in_, func, bias=None, scale=None, accum_out=None):
        reads = [in_]
        kw = {}
        if bias is not None:
            kw['bias'] = bias
            if not isinstance(bias, (int, float)):
                reads.append(bias)
        if scale is not None:
            kw['scale'] = scale
            if not isinstance(scale, (int, float)):
                reads.append(scale)
        writes = [out]
        if accum_out is not None:
            kw['accum_out'] = accum_out
            writes.append(accum_out)
        return self.op('act', lambda e: e.activation(out, in_, func, **kw), reads=reads, writes=writes)

    def tt(self, eng, out, in0, in1, op):
        return self.op(eng, lambda e: e.tensor_tensor(out, in0, in1, op), reads=[in0, in1], writes=[out])

    def ts(self, eng, out, in0, s1, s2, op0, op1=None):
        reads = [in0]
        for s in (s1, s2):
            if s is not None and not isinstance(s, (int, float)):
                reads.append(s)
        if op1 is None:
            return self.op(eng, lambda e: e.tensor_scalar(out, in0, s1, None, op0), reads=reads, writes=[out])
        return self.op(eng, lambda e: e.tensor_scalar(out, in0, s1, s2, op0, op1), reads=reads, writes=[out])

    def stt(self, out, in0, scalar, in1, op0, op1):
        reads = [in0, in1]
        if not isinstance(scalar, (int, float)):
            reads.append(scalar)
        return self.op('dve', lambda e: e.scalar_tensor_tensor(out, in0, scalar, in1, op0, op1),
                       reads=reads, writes=[out])

    def copy(self, eng, out, in_):
        if eng == 'act':
            return self.op('act', lambda e: e.copy(out, in_), reads=[in_], writes=[out])
        return self.op(eng, lambda e: e.tensor_copy(out, in_), reads=[in_], writes=[out])

    def memset(self, eng, ap, val):
        return self.op(eng, lambda e: e.memset(ap, val), reads=[], writes=[ap])

    def scan(self, out, d0, d1, init, op0=ALU.mult, op1=ALU.add):
        reads = [d0, d1]
        if not isinstance(init, (int, float)):
            reads.append(init)
        return self.op('dve', lambda e: e.tensor_tensor_scan(out, d0, d1, init, op0, op1), reads=reads, writes=[out])

    def recip(self, out, in_):
        return self.op('dve', lambda e: e.reciprocal(out, in_), reads=[in_], writes=[out])


class Arena:
    def __init__(self, P, name, t, nbytes):
        self.P, self.name, self.t, self.nbytes = P, name, t, nbytes
        self.top = 0
        self.regions = []
        self.old = []
        self.nrid = 0

    def reset(self):
        self.old.extend(self.regions)
        self.regions = []
        self.top = 0

    def mark(self):
        return (self.top, len(self.regions))

    def release(self, mk):
        top, nreg = mk
        self.old.extend(self.regions[nreg:])
        self.regions = self.regions[:nreg]
        self.top = top

    def region_of(self, lo, hi):
        for (a, b, rid) in self.regions:
            if lo >= a and hi <= b:
                return rid
        raise AssertionError("arena access outside any region %s %d %d" % (self.name, lo, hi))

    def alloc(self, shape, dtype=F32):
        n = int(np.prod(shape)) * ESZ[dtype]
        n = (n + 63) // 64 * 64
        lo, hi = self.top, self.top + n
        assert hi <= self.nbytes, ("arena overflow", self.name, hi, self.nbytes)
        self.top = hi
        rid = self.nrid
        self.nrid += 1
        self.regions.append((lo, hi, rid))
        inh = []
        keep = []
        for (a, b, orid) in self.old:
            if a < hi and lo < b:
                t = self.P.trk.get((self.name, orid))
                if t:
                    inh.extend(t['w'])
                    inh.extend(t['r'])
                keep.append((a, b, orid))
            else:
                keep.append((a, b, orid))
        self.old = keep
        if inh:
            full = ((self.name, rid), 0, 128, lo, hi)
            m = Prog._merge([(full, ev, clk) for (_, ev, clk) in inh])
            self.P.trk[(self.name, rid)] = {'w': m, 'r': []}
        v = self.t[:, lo // 4:hi // 4]
        if dtype != F32:
            v = v.bitcast(dtype)
        nel = int(np.prod(shape))
        v = v[:, 0:nel]
        if len(shape) == 2:
            v = v.rearrange("p (a b) -> p a b", a=shape[0])
        elif len(shape) == 3:
            v = v.rearrange("p (a b c) -> p a b c", a=shape[0], b=shape[1])
        return v


def _launch(build, in_maps, n_cores=8, trace=False):
    nc = bass.Bass("TRN2", target_bir_lowering=False)
    with ExitStack() as stack:
        P = Prog(nc, stack)
        build(nc, P)
        P.finish()
        P.emit()
    res = run_bass_kernel_spmd(nc, in_maps, core_ids=list(range(n_cores)), trace=trace)
    if trace:
        return res
    return res.results

D = 1024
SEQ = 8192
CTXL = 256
NTOK = SEQ + CTXL
QT = 2112
EPS = 1e-6
TILES_Q = [(0, 64, 1), (64, 512, 0), (576, 512, 0), (1088, 512, 0), (1600, 512, 0)]


class Banks:
    def __init__(self, P):
        self.pp = [P.psum("pp%d" % i, [128, 1024], F32) for i in range(4)]
        self.b = [self.pp[i // 2][:, (i % 2) * 512:(i % 2 + 1) * 512] for i in range(8)]


def load_consts_tok(nc, P):
    ones = P.sbuf("ones128", [128, 128], F32)
    P.memset('pool', ones[:], 1.0)
    return {'ones': ones}


def compute_mod(nc, P, K, banks, cvec_d, wmod_d, bmod_d, fc_list, modT):
    cT = P.sbuf("cT", [128, 8, 2], F32)
    P.dma('sp', cT[:], cvec_d)
    e = P.sbuf("cTe", [128, 8, 2], F32)
    P.act(e[:], cT[:], AF.Exp, scale=-1.0)
    P.ts('dve', e[:], e[:], 1.0, None, ALU.add)
    P.recip(e[:], e[:])
    sc = P.sbuf("cTs", [128, 8, 2], F32)
    P.tt('dve', sc[:], cT[:], e[:], ALU.mult)
    bT = P.sbuf("bmodT", [128, 24], F32)
    P.dma('sp', bT[:], bmod_d)
    wbuf = [P.sbuf("wmodbuf%d" % i, [128, 8, 512], F32) for i in range(2)]
    ps = banks.b[7]
    groups = sorted(set(fc // 4 for fc in fc_list))
    for gi, g in enumerate(groups):
        wb = wbuf[gi % 2]
        for kc in range(8):
            P.dma('sp' if kc % 2 == 0 else 'act', wb[:, kc, :], wmod_d[kc * 128:(kc + 1) * 128, g * 512:(g + 1) * 512])
        for fc in range(g * 4, g * 4 + 4):
            if fc not in fc_list:
                continue
            for kc in range(8):
                P.mm(ps[:, fc * 2:fc * 2 + 2], wb[:, kc, (fc % 4) * 128:(fc % 4 + 1) * 128], sc[:, kc, :],
                     start=(kc == 0), stop=(kc == 7))
    for fc in fc_list:
        P.tt('dve', modT[:, fc, :], ps[:, fc * 2:fc * 2 + 2], bT[:, fc:fc + 1].to_broadcast([128, 2]), ALU.add)


def norm_coeffs(nc, P, modT, normw_d, name):
    nw = P.sbuf(name + "_nw", [128, 8], F32)
    P.dma('sp', nw[:], normw_d)
    A = P.sbuf(name + "_A", [128, 8, 2], F32)
    P.ts('dve', A[:], modT[:, 8:16, :], 1.0, None, ALU.add)
    P.tt('dve', A[:], A[:], nw[:].unsqueeze(2).to_broadcast([128, 8, 2]), ALU.mult)
    return A


def phase_norm(nc, P, K, banks, RT, A, Bsh, xn_d, tiles, tmp):
    xn_v = xn_d.rearrange("(kc p) t -> p kc t", p=128)
    for ti, (c0, n, isctx) in enumerate(tiles):
        j = 1 if isctx else 0
        ps = banks.b[ti % 2]
        for kc in range(8):
            sq = tmp['sq'][kc % 2]
            P.act(sq[:, 0:n], RT[:, kc, c0:c0 + n], AF.Square)
            P.mm(ps[:, 0:n], K['ones'][:], sq[:, 0:n], start=(kc == 0), stop=(kc == 7))
        rstd = tmp['rstd'][ti % 2]
        P.act(rstd[:, 0:n], ps[:, 0:n], AF.Ln, scale=1.0 / D, bias=K['eps'][:])
        P.act(rstd[:, 0:n], rstd[:, 0:n], AF.Exp, scale=-0.5)
        xb = tmp['xb'][ti % 2]
        for kc in range(8):
            t1 = tmp['t1'][kc % 2]
            P.stt(t1[:, 0:n], RT[:, kc, c0:c0 + n], A[:, kc, j:j + 1], rstd[:, 0:n], ALU.mult, ALU.mult)
            P.act(xb[:, kc, 0:n], t1[:, 0:n], AF.Identity, bias=Bsh[:, kc, j:j + 1])
        P.dma('pool', xn_v[:, :, c0:c0 + n], xb[:, :, 0:n], final=True)


def phase_final(nc, P, K, banks, RT, fnw_d, out_d, tiles, tmp):
    fw = P.sbuf("fnw_sb", [128, 8], F32)
    P.dma('sp', fw[:], fnw_d)
    out_v = out_d.rearrange("(kc p) t -> p kc t", p=128)
    for ti, (c0, n, isctx) in enumerate(tiles):
        ps = banks.b[ti % 2]
        for kc in range(8):
            sq = tmp['sq'][kc % 2]
            P.act(sq[:, 0:n], RT[:, kc, c0:c0 + n], AF.Square)
            P.mm(ps[:, 0:n], K['ones'][:], sq[:, 0:n], start=(kc == 0), stop=(kc == 7))
        rstd = tmp['rstd'][ti % 2]
        P.act(rstd[:, 0:n], ps[:, 0:n], AF.Ln, scale=1.0 / D, bias=K['eps'][:])
        P.act(rstd[:, 0:n], rstd[:, 0:n], AF.Exp, scale=-0.5)
        for kc in range(8):
            P.stt(RT[:, kc, c0:c0 + n], RT[:, kc, c0:c0 + n], fw[:, kc:kc + 1], rstd[:, 0:n], ALU.mult, ALU.mult)
        P.dma('pool', out_v[:, :, c0 - 64:c0 - 64 + n], RT[:, :, c0:c0 + n], final=True)


def phase_outproj(nc, P, K, banks, RT, yT_d, wout_d, ssdw_d, gateT, tiles, tmp):
    wo = P.sbuf("wout_bf", [128, 8, D], BF16)
    for kc in range(8):
        st = tmp['wstage'][kc % 2]
        P.dma('sp' if kc % 2 == 0 else 'act', st[:], wout_d[kc * 128:(kc + 1) * 128, :])
        P.copy('pool', wo[:, kc, :], st[:])
    sw = P.sbuf("ssdnw_sb", [128, 2], F32)
    P.dma('sp', sw[:], ssdw_d)
    yT_v = yT_d.rearrange("(kc p) t -> p kc t", p=128)
    for ti, (c0, n, isctx) in enumerate(tiles):
        j = 1 if isctx else 0
        yb = tmp['yb'][ti % 2]
        P.dma('sp', yb[:, :, 0:n], yT_v[:, :, c0:c0 + n])
        ps = banks.b[2 + ti % 2]
        for i, kc in enumerate((6, 7)):
            sq = tmp['sq'][i]
            P.act(sq[:, 0:n], yb[:, kc, 0:n], AF.Square)
            P.mm(ps[:, 0:n], K['ones'][:], sq[:, 0:n], start=(i == 0), stop=(i == 1))
        rstd = tmp['rstd'][ti % 2]
        P.act(rstd[:, 0:n], ps[:, 0:n], AF.Ln, scale=1.0 / 256, bias=K['eps'][:])
        P.act(rstd[:, 0:n], rstd[:, 0:n], AF.Exp, scale=-0.5)
        for i, kc in enumerate((6, 7)):
            P.stt(yb[:, kc, 0:n], yb[:, kc, 0:n], sw[:, i:i + 1], rstd[:, 0:n], ALU.mult, ALU.mult)
        for fc in range(8):
            po = banks.b[4 + fc % 4]
            for kc in range(8):
                P.mm(po[:, 0:n], wo[:, kc, fc * 128:(fc + 1) * 128], yb[:, kc, 0:n], start=(kc == 0), stop=(kc == 7))
            P.stt(RT[:, fc, c0:c0 + n], po[:, 0:n], gateT[:, 16 + fc, j:j + 1], RT[:, fc, c0:c0 + n], ALU.mult, ALU.add)


def alloc_tok_tmp(P):
    return {
        'sq': [P.sbuf("t_sq%d" % i, [128, 512], F32) for i in range(2)],
        'rstd': [P.sbuf("t_rstd%d" % i, [128, 512], F32) for i in range(2)],
        't1': [P.sbuf("t_t1%d" % i, [128, 512], F32) for i in range(2)],
        'xb': [P.sbuf("t_xb%d" % i, [128, 8, 512], BF16) for i in range(2)],
    }


def build_tok_kernel(nc, P, first, last, with_outproj):
    banks = Banks(P)
    K = load_consts_tok(nc, P)
    epst = P.sbuf("eps_t", [128, 1], F32)
    P.memset('pool', epst[:], EPS)
    K['eps'] = epst
    tmp = alloc_tok_tmp(P)
    RT_d = nc.dram_tensor("RT", [D, QT], F32, kind="ExternalInput").ap()
    cvec_d = nc.dram_tensor("cvec", [128, 8, 2], F32, kind="ExternalInput").ap()
    RT = P.sbuf("RT_sb", [128, 8, QT], F32)
    RT_v = RT_d.rearrange("(kc p) t -> p kc t", p=128)
    for kc in range(8):
        P.dma('sp' if kc % 2 == 0 else 'act', RT[:, kc, :], RT_v[:, kc, :])
    modT = P.sbuf("modT", [128, 24, 2], F32)
    if with_outproj:
        wmodg_d = nc.dram_tensor("wmod_g", [D, 3 * D], F32, kind="ExternalInput").ap()
        bmodg_d = nc.dram_tensor("bmod_g", [128, 24], F32, kind="ExternalInput").ap()
        yT_d = nc.dram_tensor("yT", [D, QT], BF16, kind="ExternalInput").ap()
        wout_d = nc.dram_tensor("wout", [D, D], F32, kind="ExternalInput").ap()
        ssdw_d = nc.dram_tensor("ssdnw", [128, 2], F32, kind="ExternalInput").ap()
        tmp['wstage'] = [P.sbuf("t_wst%d" % i, [128, D], F32) for i in range(2)]
        tmp['yb'] = [P.sbuf("t_yb%d" % i, [128, 8, 512], BF16) for i in range(2)]
        gateT = P.sbuf("gateT", [128, 24, 2], F32)
        compute_mod(nc, P, K, banks, cvec_d, wmodg_d, bmodg_d, list(range(16, 24)), gateT)
        tiles = TILES_Q[1:] if last else TILES_Q
        phase_outproj(nc, P, K, banks, RT, yT_d, wout_d, ssdw_d, gateT, tiles, tmp)
    if last:
        fnw_d = nc.dram_tensor("fnw", [128, 8], F32, kind="ExternalInput").ap()
        out_d = nc.dram_tensor("outT", [D, 2048], F32, kind="ExternalOutput").ap()
        phase_final(nc, P, K, banks, RT, fnw_d, out_d, TILES_Q[1:], tmp)
    else:
        wmod_d = nc.dram_tensor("wmod_n", [D, 3 * D], F32, kind="ExternalInput").ap()
        bmod_d = nc.dram_tensor("bmod_n", [128, 24], F32, kind="ExternalInput").ap()
        normw_d = nc.dram_tensor("normw", [128, 8], F32, kind="ExternalInput").ap()
        xn_d = nc.dram_tensor("xnT", [D, QT], BF16, kind="ExternalOutput").ap()
        if with_outproj:
            Rout_d = nc.dram_tensor("RTout", [D, QT], F32, kind="ExternalOutput").ap()
        modT2 = modT
        _cm_second(nc, P, K, banks, cvec_d, wmod_d, bmod_d, list(range(0, 16)), modT2, with_outproj)
        A = norm_coeffs(nc, P, modT2, normw_d, "nc")
        phase_norm(nc, P, K, banks, RT, A, modT2, xn_d, TILES_Q, tmp)
        if with_outproj:
            Ro_v = Rout_d.rearrange("(kc p) t -> p kc t", p=128)
            for kc in range(8):
                P.dma('pool', Ro_v[:, kc, :], RT[:, kc, :], final=True)


_CM_STATE = {}


def _cm_second(nc, P, K, banks, cvec_d, wmod_d, bmod_d, fc_list, modT, second):
    if not second:
        return compute_mod(nc, P, K, banks, cvec_d, wmod_d, bmod_d, fc_list, modT)
    orig = P.sbuf

    def renamed(name, shape, dtype=F32):
        return orig(name + "_2", shape, dtype)
    P.sbuf = renamed
    try:
        compute_mod(nc, P, K, banks, cvec_d, wmod_d, bmod_d, fc_list, modT)
    finally:
        P.sbuf = orig


def pk(v):
    v = np.asarray(v)
    return np.ascontiguousarray(v.reshape(-1, 128).T)

TOK_TILES = [(0, 256)] + [(256 + 512 * i, 512) for i in range(16)]
SCALE_QK = 32.0 ** -0.5
QK_REP = 1
DEFER = True


def store_y(P, io, mi, t0, n, src):
    if 'y_store' in io:
        io['y_store'](mi, t0, n, src)
    else:
        P.dma('pool', io['yT'][mi, :, t0:t0 + n], src, final=True)


def load_weights_bf16(nc, P, A, w_d, ncols, stage):
    Wsb = A.alloc([8, ncols], BF16)
    for kc in range(8):
        st = stage[kc % 2]
        P.dma('sp' if kc % 2 == 0 else 'act', st[:, 0:ncols], w_d[kc * 128:(kc + 1) * 128, :])
        P.copy('dve', Wsb[:, kc, :], st[:, 0:ncols])
    return Wsb


def inproj(nc, P, banks, xn_v, Wsb, tiles, fm_groups, tm_group, xbufs, bank_ids=(0, 1, 2, 3), pre_tile=None):
    pending = []
    for ti, (t0, n) in enumerate(tiles):
        xb = xbufs[ti % 2]
        if callable(xn_v):
            xn_v(xb, t0, n)
        else:
            P.dma('sp', xb[:, :, 0:n], xn_v[:, :, t0:t0 + n])
        if pre_tile is not None:
            pre_tile(ti, t0, n)
        new_pending = []
        for gi, (c0, M, lat_only, fn) in enumerate(fm_groups):
            if lat_only and t0 < CTXL:
                continue
            ps = banks.b[bank_ids[gi % len(bank_ids)]]
            for kc in range(8):
                P.mm(ps[0:M, 0:n], Wsb[:, kc, c0:c0 + M], xb[:, kc, 0:n], start=(kc == 0), stop=(kc == 7))
            if fn is not None:
                r = fn(ps, t0, n, ti)
                if r is not None:
                    if DEFER:
                        new_pending.append(r)
                    else:
                        r()
        if tm_group is not None:
            c0, ncols, fn, tmbanks = tm_group
            for sub in range(n // 128):
                ps = banks.b[tmbanks[sub % len(tmbanks)]]
                for kc in range(8):
                    P.mm(ps[:, 0:ncols], xb[:, kc, sub * 128:(sub + 1) * 128], Wsb[:, kc, c0:c0 + ncols],
                         start=(kc == 0), stop=(kc == 7))
                fn(ps, t0 + sub * 128)
        for r in pending:
            r2 = r()
            if r2 is not None:
                new_pending.append(r2)
        pending = new_pending
    while pending:
        nxt = []
        for r in pending:
            r2 = r()
            if r2 is not None:
                nxt.append(r2)
        pending = nxt


def silu_evac(P, out_bf, ps_ap, tmp_e, M, n):
    P.act(tmp_e[0:M, 0:n], ps_ap, AF.Tanh, scale=0.5)
    P.stt(out_bf, tmp_e[0:M, 0:n], 1.0, ps_ap, ALU.add, ALU.mult)


def attention_phase(nc, P, A, banks, K, xn_v, io, layer, with_ctx):
    A.reset()
    lam_init = 0.8 - 0.6 * float(np.exp(-0.3 * layer))
    stage = [A.alloc([640], F32) for _ in range(2)]
    Wsb = load_weights_bf16(nc, P, A, io['w_attn'], 640, stage)
    xbufs = [A.alloc([8, 512], BF16) for _ in range(2)]
    Qa = A.alloc([NTOK], BF16)
    Ka = A.alloc([NTOK], BF16)
    Vt = A.alloc([66, 128], BF16)
    sg = A.alloc([NTOK], BF16)
    cosb = [A.alloc([512], F32) for _ in range(2)]
    sinb = [A.alloc([512], F32) for _ in range(2)]
    t1 = [A.alloc([512], F32) for _ in range(2)]
    t2 = [A.alloc([512], F32) for _ in range(2)]
    te = [A.alloc([512], F32) for _ in range(2)]
    P.memset('pool', Vt[:, :, 65:128], 0.0)
    P.memset('pool', Vt[:, :, 64:65], 1.0)

    def pre_tile(ti, t0, n):
        if t0 >= CTXL:
            P.dma('act', cosb[ti % 2][:], io['cosT'][:, t0 - CTXL:t0 - CTXL + n])
            P.dma('act', sinb[ti % 2][:], io['sinT'][:, t0 - CTXL:t0 - CTXL + n])

    state = {}

    def ev_plain(dst):
        def f(ps, t0, n, ti):
            if t0 < CTXL:
                P.copy('act', dst[:, t0:t0 + n], ps[:, 0:n])
            else:
                state['ps1'] = ps
        return f

    def ev_rot(dst):
        def f(ps, t0, n, ti):
            a = t1[ti % 2]
            b = t2[ti % 2]
            P.tt('dve', a[:, 0:n], state['ps1'][:, 0:n], cosb[ti % 2][:, 0:n], ALU.mult)
            P.tt('dve', b[:, 0:n], ps[:, 0:n], sinb[ti % 2][:, 0:n], ALU.mult)
            P.tt('pool', dst[:, t0:t0 + n], a[:, 0:n], b[:, 0:n], ALU.add)
        return f

    vT = [A.alloc([512], BF16) for _ in range(2)]

    def ev_gv(ps, t0, n, ti):
        silu_evac(P, sg[0:64, t0:t0 + n], ps[0:64, 0:n], te[ti % 2], 64, n)
        v_ = vT[ti % 2]
        P.copy('act', v_[64:128, 0:n], ps[64:128, 0:n])

        def rest():
            for sub in range(n // 128):
                pt = banks.b[6 + sub % 2].bitcast(BF16)
                P.transpose(pt[:, 0:64], v_[64:128, sub * 128:(sub + 1) * 128], K['identb'][64:128, 64:128])
                P.copy('dve', Vt[:, t0 // 128 + sub, 0:64], pt[:, 0:64])
        return rest

    fm = [(0, 128, False, ev_plain(Qa)), (128, 128, True, ev_rot(Qa)),
          (256, 128, False, ev_plain(Ka)), (384, 128, True, ev_rot(Ka)),
          (512, 128, False, ev_gv)]
    inproj(nc, P, banks, xn_v, Wsb, TOK_TILES, fm, None, xbufs, bank_ids=(0, 1, 2, 3, 4), pre_tile=pre_tile)

    lv = A.alloc([128], F32)
    P.dma('sp', lv[0:64, :], io['diff_lam'].partition_broadcast(64))
    pr = A.alloc([64], F32)
    s2 = A.alloc([2], F32)
    P.tt('dve', pr[0:64, 0:32], lv[0:64, 0:32], lv[0:64, 32:64], ALU.mult)
    P.tt('dve', pr[0:64, 32:64], lv[0:64, 64:96], lv[0:64, 96:128], ALU.mult)
    P.op('dve', lambda e: e.tensor_reduce(s2[0:64, 0:2], pr[0:64, :].rearrange("p (a b) -> p a b", a=2), AX.X, ALU.add),
         reads=[pr[0:64, :]], writes=[s2[0:64, 0:2]])
    P.act(s2[0:64, :], s2[0:64, :], AF.Exp)
    nlam = A.alloc([1], F32)
    P.tt('dve', nlam[0:64, :], s2[0:64, 1:2], s2[0:64, 0:1], ALU.subtract)
    P.ts('dve', nlam[0:64, :], nlam[0:64, :], -lam_init, None, ALU.add)
    subw = A.alloc([1], F32)
    P.dma('sp', subw[0:64, :], io['diff_subln_w'])
    P.ts('dve', subw[0:64, :], subw[0:64, :], 1.0 - lam_init, None, ALU.mult)

    Eb = [A.alloc([1024], BF16) for _ in range(4)]
    osb = [A.alloc([512], F32) for _ in range(2)]
    fz = [A.alloc([512], F32) for _ in range(4)]
    yb = [A.alloc([512], BF16) for _ in range(2)]
    qblocks = []
    if with_ctx:
        qblocks.append((0, 256, [0, 1]))
    for i in range(16):
        qblocks.append((256 + 512 * i, 512, list(range(66))))
    o_acc = [banks.b[6], banks.b[7]]
    for qi, (q0, nq, kbs) in enumerate(qblocks):
        def qk(kb):
            S = banks.pp[kb % 3]
            for rep in range(QK_REP):
                P.mm(S[:, 0:nq], Ka[0:32, kb * 128:(kb + 1) * 128], Qa[0:32, q0:q0 + nq])
                P.mm(S[:, 512:512 + nq], Ka[64:96, kb * 128:(kb + 1) * 128], Qa[64:96, q0:q0 + nq])
        for k_ in kbs[0:3]:
            qk(k_)
        for ki, kb in enumerate(kbs):
            S = banks.pp[kb % 3]
            E = Eb[ki % 4]
            if nq == 512:
                P.act(E[:, :], S[:, :], AF.Exp, scale=SCALE_QK)
            else:
                Ev = E.rearrange("p (a b) -> p a b", a=2)[:, :, 0:nq]
                Sv = S.rearrange("p (a b) -> p a b", a=2)[:, :, 0:nq]
                P.act(Ev, Sv, AF.Exp, scale=SCALE_QK)
            if ki + 3 < len(kbs):
                qk(kbs[ki + 3])
            for c in range(2):
                P.mm(o_acc[c][:, 0:nq], Vt[:, kb, :], E[:, c * 512:c * 512 + nq], start=(ki == 0), stop=(ki == len(kbs) - 1))
        for c in range(2):
            P.copy('act', osb[c][0:65, 0:nq], o_acc[c][0:65, 0:nq])
        zb = [banks.b[0], banks.b[1]]
        for c in range(2):
            P.mm(zb[c][0:64, 0:nq], K['selZ'][0:65, :], osb[c][0:65, 0:nq])
        for c in range(2):
            P.act(fz[c][0:64, 0:nq], zb[c][0:64, 0:nq], AF.Ln)
            P.act(fz[c][0:64, 0:nq], fz[c][0:64, 0:nq], AF.Exp, scale=-1.0)
            P.tt('dve', fz[c][0:64, 0:nq], fz[c][0:64, 0:nq], osb[c][0:64, 0:nq], ALU.mult)
        o = fz[2]
        P.stt(o[0:64, 0:nq], fz[1][0:64, 0:nq], nlam[0:64, 0:1], fz[0][0:64, 0:nq], ALU.mult, ALU.add)
        P.act(fz[3][0:64, 0:nq], o[0:64, 0:nq], AF.Square)
        P.mm(zb[0][0:64, 0:nq], K['ones'][0:64, 0:64], fz[3][0:64, 0:nq])
        P.act(fz[3][0:64, 0:nq], zb[0][0:64, 0:nq], AF.Ln, scale=1.0 / 64, bias=K['eps'][0:64, :])
        P.act(fz[3][0:64, 0:nq], fz[3][0:64, 0:nq], AF.Exp, scale=-0.5)
        P.stt(o[0:64, 0:nq], o[0:64, 0:nq], subw[0:64, 0:1], fz[3][0:64, 0:nq], ALU.mult, ALU.mult)
        y = yb[qi % 2]
        P.stt(y[0:64, 0:nq], o[0:64, 0:nq], 0.5, sg[0:64, q0:q0 + nq], ALU.mult, ALU.mult)
        store_y(P, io, 2, q0, nq, y[0:64, 0:nq])
        if 'after_q' in io:
            io['after_q'](q0, nq)


TP = 8456
PADC = 2
PADL = 260


def pcol(t0):
    return t0 + PADC if t0 < CTXL else t0 + (PADL - CTXL)


PTILES = [(PADC, 256)] + [(PADL + 512 * i, 512) for i in range(16)]


def conv4(P, dst, src, cw, cb, np_, c_lo, c_hi):
    n = c_hi - c_lo
    P.ts('dve', dst[0:np_, c_lo:c_hi], src[0:np_, c_lo - 2:c_hi - 2], cw[0:np_, 0:1], cb[0:np_, 0:1], ALU.mult, ALU.add)
    for k in range(1, 4):
        P.stt(dst[0:np_, c_lo:c_hi], src[0:np_, c_lo + k - 2:c_hi + k - 2], cw[0:np_, k:k + 1], dst[0:np_, c_lo:c_hi],
              ALU.mult, ALU.add)


def lru_phase(nc, P, A, banks, K, xn_v, io, layer, with_ctx):
    A.reset()
    stage = [A.alloc([192], F32) for _ in range(2)]
    Wsb = load_weights_bf16(nc, P, A, io['w_lru'], 192, stage)
    xbufs = [A.alloc([8, 512], BF16) for _ in range(2)]
    XA = A.alloc([TP], F32)
    XU = A.alloc([TP], F32)
    sg = A.alloc([NTOK], BF16)
    te = [A.alloc([512], F32) for _ in range(2)]
    par = A.alloc([16], F32)
    P.dma('sp', par[:, 0:9], io['lru_par'])
    cw, cb = par[:, 0:4], par[:, 4:5]
    nba, nbx = par[:, 9:10], par[:, 10:11]
    c8, c16 = par[:, 11:12], par[:, 12:13]
    P.ts('dve', nba, par[:, 5:6], 0.5, None, ALU.mult)
    P.ts('dve', nbx, par[:, 6:7], 0.5, None, ALU.mult)
    P.act(c8, par[:, 7:8], AF.Exp, scale=-1.0)
    P.act(c8, c8, AF.Ln, bias=K['one'][:, 0:1])
    P.ts('dve', c16, c8, -16.0, None, ALU.mult)
    P.ts('dve', c8, c8, -8.0, None, ALU.mult)
    gw32 = A.alloc([256], F32)
    P.dma('sp', gw32[0:64, :], io['lru_gw'])
    gw = A.alloc([256], BF16)
    P.copy('pool', gw[0:64, :], gw32[0:64, :])
    II = A.alloc([64], F32)
    P.copy('pool', II[0:64, :], K['ident'][0:64, 0:64])
    P.copy('pool', II[64:128, :], K['ident'][64:128, 64:128])
    P.memset('pool', XA[:, 0:PADC], 0.0)
    P.memset('pool', XA[:, PADC + CTXL:PADL], 0.0)
    P.memset('pool', XA[:, PADL + SEQ:TP], 0.0)

    def ev_x(ps, t0, n, ti):
        c = pcol(t0)
        P.copy('act', XA[:, c:c + n], ps[:, 0:n])

    def ev_gate(ps, t0, n, ti):
        silu_evac(P, sg[0:64, t0:t0 + n], ps[0:64, 0:n], te[ti % 2], 64, n)

    inproj(nc, P, banks, xn_v, Wsb, TOK_TILES, [(0, 128, False, ev_x), (128, 64, False, ev_gate)], None, xbufs,
           bank_ids=(0, 1, 2, 3))
    conv4(P, XU, XA, cw, cb, 128, PADC, PADC + CTXL)
    for i in range(4):
        conv4(P, XU, XA, cw, cb, 128, PADL + 2048 * i, PADL + 2048 * (i + 1))
    xcb = [A.alloc([512], BF16) for _ in range(2)]
    gr = [A.alloc([512], F32) for _ in range(2)]
    gi_ = [A.alloc([512], F32) for _ in range(2)]
    gs = [A.alloc([512], F32) for _ in range(2)]
    for ti, (c0, n) in enumerate(PTILES):
        xb = xcb[ti % 2]
        P.copy('pool', xb[0:64, 0:n], XU[0:64, c0:c0 + n])
        psr, psi = banks.b[(2 * ti) % 8], banks.b[(2 * ti + 1) % 8]
        P.mm(psr[:, 0:n], gw[0:64, 0:128], xb[0:64, 0:n])
        P.mm(psi[:, 0:n], gw[0:64, 128:256], xb[0:64, 0:n])
        r, ii, s = gr[ti % 2], gi_[ti % 2], gs[ti % 2]
        P.act(r[:, 0:n], psr[:, 0:n], AF.Tanh, scale=0.5, bias=nba)
        P.act(ii[:, 0:n], psi[:, 0:n], AF.Tanh, scale=0.5, bias=nbx)
        P.ts('dve', r[:, 0:n], r[:, 0:n], 0.5, 0.5, ALU.mult, ALU.add)
        P.ts('dve', ii[:, 0:n], ii[:, 0:n], 0.5, 0.5, ALU.mult, ALU.add)
        P.act(XA[:, c0:c0 + n], r[:, 0:n], AF.Exp, scale=c8)
        P.act(s[:, 0:n], r[:, 0:n], AF.Exp, scale=c16)
        P.act(s[:, 0:n], s[:, 0:n], AF.Ln, scale=-1.0, bias=K['one'][:, 0:1])
        P.act(s[:, 0:n], s[:, 0:n], AF.Exp, scale=0.5)
        P.tt('pool', ii[:, 0:n], ii[:, 0:n], s[:, 0:n], ALU.mult)
        P.tt('dve', XU[:, c0:c0 + n], XU[:, c0:c0 + n], ii[:, 0:n], ALU.mult)
    fa, fu = XA[0:64], XU[0:64]
    ba_, bu = XA[64:128], XU[64:128]
    P.scan(fu[:, PADC:PADC + CTXL], fa[:, PADC:PADC + CTXL], fu[:, PADC:PADC + CTXL], 0.0)
    for i in range(4):
        lo = PADL + 2048 * i
        init = fu[:, PADC + CTXL - 1:PADC + CTXL] if i == 0 else fu[:, lo - 1:lo]
        P.scan(fu[:, lo:lo + 2048], fa[:, lo:lo + 2048], fu[:, lo:lo + 2048], init)
    P.scan(bu[:, PADC:PADC + CTXL][:, ::-1], ba_[:, PADC:PADC + CTXL][:, ::-1], bu[:, PADC:PADC + CTXL][:, ::-1], 0.0)
    for i in range(3, -1, -1):
        lo = PADL + 2048 * i
        init = bu[:, PADC:PADC + 1] if i == 3 else bu[:, lo + 2048:lo + 2049]
        P.scan(bu[:, lo:lo + 2048][:, ::-1], ba_[:, lo:lo + 2048][:, ::-1], bu[:, lo:lo + 2048][:, ::-1], init)
    yb = [A.alloc([512], BF16) for _ in range(2)]
    for ti, (t0, n) in enumerate(TOK_TILES):
        if t0 < CTXL and not with_ctx:
            continue
        c0 = pcol(t0)
        ps = banks.b[ti % 4]
        P.mm(ps[0:64, 0:n], II[:, :], XU[:, c0:c0 + n])
        y = yb[ti % 2]
        P.stt(y[0:64, 0:n], ps[0:64, 0:n], 0.5, sg[0:64, t0:t0 + n], ALU.mult, ALU.mult)
        store_y(P, io, 1, t0, n, y[0:64, 0:n])


def gla_phase(nc, P, A, banks, K, xn_v, io, layer, with_ctx):
    A.reset()
    NC_ = 132
    QG = A.alloc([NTOK], BF16)
    KG = A.alloc([NTOK], BF16)
    KDt = A.alloc([66, 64], BF16)
    Vt = A.alloc([66, 64], BF16)
    sg = A.alloc([NTOK], BF16)
    ST = A.alloc([NC_ + 2, 64], F32)
    STb = A.alloc([NC_ + 2, 64], BF16)
    DEC = A.alloc([NC_], F32)
    par = A.alloc([8], F32)
    P.dma('sp', par[0:64, 0:2], io['gla_par'])
    nb2 = par[0:64, 2:3]
    P.ts('dve', nb2, par[0:64, 0:1], -1.0, None, ALU.mult)
    lnsc = par[0:64, 3:4]
    P.memset('dve', lnsc, float(np.log(32.0 ** -0.5)))
    w2_32 = A.alloc([64], F32)
    P.dma('sp', w2_32[64:96, :], io['gla_w2bd'])
    w2b = A.alloc([64], BF16)
    P.copy('dve', w2b[64:96, :], w2_32[64:96, :])
    mk = A.mark()
    stage = [A.alloc([320], F32) for _ in range(2)]
    Wsb = load_weights_bf16(nc, P, A, io['w_gla'], 320, stage)
    xbufs = [A.alloc([8, 512], BF16) for _ in range(2)]
    te = [A.alloc([512], F32) for _ in range(2)]
    lrb = [A.alloc([512], BF16) for _ in range(2)]
    Lb = [A.alloc([512], F32) for _ in range(2)]
    Gb = [A.alloc([512], F32) for _ in range(2)]
    Eq = [A.alloc([512], F32) for _ in range(2)]
    Ek = [A.alloc([512], F32) for _ in range(2)]
    dG = [A.alloc([512], F32) for _ in range(2)]
    kdT = [A.alloc([512], BF16) for _ in range(2)]
    P.memset('dve', ST[0:64, 0:2, :], 0.0)
    P.memset('dve', ST[0:64, NC_:NC_ + 2, :], 0.0)
    st = {}

    qs = [A.alloc([512], F32) for _ in range(2)]
    ks = [A.alloc([512], F32) for _ in range(2)]

    def ev_q(ps, t0, n, ti):
        P.copy('act', qs[ti % 2][0:64, 0:n], ps[0:64, 0:n])

    vT = [A.alloc([512], BF16) for _ in range(2)]

    def ev_gv(ps, t0, n, ti):
        P.copy('dve', sg[0:64, t0:t0 + n], ps[0:64, 0:n])
        v_ = vT[ti % 2]
        P.copy('act', v_[64:128, 0:n], ps[64:128, 0:n])
        for sub in range(n // 128):
            pt = banks.b[5 + sub % 2].bitcast(BF16)
            P.transpose(pt[:, 64:128], v_[64:128, sub * 128:(sub + 1) * 128], K['identb'][64:128, 64:128])
            P.copy('dve', Vt[:, t0 // 128 + sub, :], pt[:, 64:128])

    def ev_lr(ps, t0, n, ti):
        i2 = ti % 2
        nch = n // 64
        c0 = t0 // 64
        P.copy('act', ks[ti % 2][0:64, 0:n], ps[0:64, 0:n])
        P.copy('act', lrb[i2][64:96, 0:n], ps[64:96, 0:n])

        def rest():
            pz = banks.b[7]
            P.mm(pz[0:64, 0:n], w2b[64:96, 0:64], lrb[i2][64:96, 0:n])
            L, G = Lb[i2], Gb[i2]
            P.act(L[0:64, 0:n], pz[0:64, 0:n], AF.Exp, scale=-1.0, bias=nb2)
            P.act(L[0:64, 0:n], L[0:64, 0:n], AF.Ln, bias=K['one'][0:64, 0:1])
            P.scan(G[0:32, 0:n], K['scanmask'][0:32, 0:n], L[0:32, 0:n], 0.0)
            P.scan(G[32:64, 0:n][:, ::-1], K['scanmask'][32:64, 0:n][:, ::-1], L[32:64, 0:n][:, ::-1], 0.0)
            P.act(Eq[i2][0:64, 0:n], G[0:64, 0:n], AF.Exp, scale=-1.0 / 16, bias=lnsc)
            P.act(Ek[i2][0:64, 0:n], G[0:64, 0:n], AF.Exp, scale=1.0 / 16)
            P.tt('dve', QG[0:64, t0:t0 + n], qs[i2][0:64, 0:n], Eq[i2][0:64, 0:n], ALU.mult)
            P.tt('dve', KG[0:64, t0:t0 + n], ks[i2][0:64, 0:n], Ek[i2][0:64, 0:n], ALU.mult)
            G3 = G[:, 0:n].rearrange("p (c s) -> p c s", s=64)
            d3 = dG[i2][:, 0:n].rearrange("p (c s) -> p c s", s=64)
            P.tt('dve', d3[0:32], G3[0:32, :, 63:64].to_broadcast([32, nch, 64]), G3[0:32], ALU.subtract)
            P.tt('dve', d3[32:64], G3[32:64, :, 0:1].to_broadcast([32, nch, 64]), G3[32:64], ALU.subtract)
            P.act(dG[i2][0:64, 0:n], dG[i2][0:64, 0:n], AF.Exp, scale=-1.0 / 16)
            P.tt('dve', kdT[i2][0:64, 0:n], ks[i2][0:64, 0:n], dG[i2][0:64, 0:n], ALU.mult)
            P.act(DEC[0:32, c0:c0 + nch], G3[0:32, :, 63], AF.Exp, scale=-1.0 / 16)
            P.act(DEC[32:64, c0:c0 + nch], G3[32:64, :, 0], AF.Exp, scale=-1.0 / 16)
            def rest2():
                for sub in range(n // 128):
                    pt = banks.b[5 + sub % 2].bitcast(BF16)
                    P.transpose(pt[:, 0:64], kdT[i2][0:64, sub * 128:(sub + 1) * 128], K['identb'][0:64, 0:64])
                    P.copy('dve', KDt[:, t0 // 128 + sub, :], pt[:, 0:64])
            return rest2
        return rest

    fm = [(0, 64, False, ev_q), (64, 96, False, ev_lr), (192, 128, False, ev_gv)]
    inproj(nc, P, banks, xn_v, Wsb, TOK_TILES, fm, None, xbufs, bank_ids=(0, 1, 2))
    for ti, (t0, n) in enumerate(TOK_TILES):
        P.act(te[ti % 2][0:64, 0:n], sg[0:64, t0:t0 + n], AF.Tanh, scale=0.5)
        P.stt(sg[0:64, t0:t0 + n], te[ti % 2][0:64, 0:n], 1.0, sg[0:64, t0:t0 + n], ALU.add, ALU.mult)

    for g0 in range(0, 66, 8):
        npair = min(8, 66 - g0)
        for i in range(npair):
            p = g0 + i
            for j in range(2):
                P.mm(banks.b[j][0:64, i * 64:(i + 1) * 64], KDt[64 * j:64 * j + 64, p, :], Vt[64 * j:64 * j + 64, p, :])
        for j in range(2):
            src = banks.b[j][:, 0:npair * 64].rearrange("p (c v) -> p c v", v=64)
            c0 = 2 * g0 + j
            P.copy('act', ST[0:32, c0 + 2:c0 + 1 + 2 * npair:2, :], src[0:32])
            P.copy('dve', ST[32:64, c0:c0 + 2 * npair - 1:2, :], src[32:64])
    fsteps = [(c + 2, c + 1, c) for c in range(1, NC_)]
    bsteps = [(c, c + 1, c) for c in (2, 1, 0)] + [(131, 0, 131)] + [(c, c + 1, c) for c in range(130, 4, -1)]
    for i in range(max(len(fsteps), len(bsteps))):
        if i < len(fsteps):
            o_, i_, d_ = fsteps[i]
            P.stt(ST[0:32, o_, :], ST[0:32, i_, :], DEC[0:32, d_:d_ + 1], ST[0:32, o_, :], ALU.mult, ALU.add)
        if i < len(bsteps):
            o_, i_, d_ = bsteps[i]
            P.stt(ST[32:64, o_, :], ST[32:64, i_, :], DEC[32:64, d_:d_ + 1], ST[32:64, o_, :], ALU.mult, ALU.add)
    P.copy('dve', ST[32:64, 132, :], ST[32:64, 0, :])
    P.copy('dve', STb[0:64], ST[0:64])

    A.release(mk)
    tm = [A.alloc([512], F32) for _ in range(2)]
    At = [A.alloc([512], BF16) for _ in range(2)]
    ob = [A.alloc([512], F32) for _ in range(2)]
    yb = [A.alloc([512], BF16) for _ in range(2)]
    tiles2 = [(ti, t0, n) for ti, (t0, n) in enumerate(TOK_TILES) if not (t0 < CTXL and not with_ctx)]

    def stageA(ti, t0, n):
        i2 = ti % 2
        npair = n // 128
        X, Y = banks.b[2], banks.b[3]
        for i in range(npair):
            cs = slice(t0 + 128 * i, t0 + 128 * (i + 1))
            P.mm(X[:, 128 * i:128 * (i + 1)], KG[0:32, cs], QG[0:32, cs])
            P.mm(Y[:, 128 * i:128 * (i + 1)], KG[32:64, cs], QG[32:64, cs])
        P.tt('dve', tm[i2][:, 0:n], X[:, 0:n], K['maskF'][:, 0:n], ALU.mult)
        P.tt('dve', At[i2][:, 0:n], Y[:, 0:n], K['maskB'][:, 0:n], ALU.mult)
        P.tt('pool', At[i2][:, 0:n], At[i2][:, 0:n], tm[i2][:, 0:n], ALU.add)

    def stageB(ti, t0, n):
        i2 = ti % 2
        npair = n // 128
        Z = banks.b[4 + i2]
        for i in range(npair):
            p = t0 // 128 + i
            P.mm(Z[0:64, 128 * i:128 * (i + 1)], Vt[:, p, :], At[i2][:, 128 * i:128 * (i + 1)], start=True, stop=False)
            for j in range(2):
                c = 2 * p + j
                cs = slice(t0 + 128 * i + 64 * j, t0 + 128 * i + 64 * (j + 1))
                kk = 32 if c == 3 else 64
                P.mm(Z[0:64, 128 * i + 64 * j:128 * i + 64 * (j + 1)], STb[0:kk, c + 1, :], QG[0:kk, cs],
                     start=False, stop=(j == 1))
        o = ob[i2]
        P.act(o[0:64, 0:n], Z[0:64, 0:n], AF.Square)

    def stageC(ti, t0, n):
        i2 = ti % 2
        Z = banks.b[4 + i2]
        o = ob[i2]
        zb = banks.b[6 + i2]
        P.mm(zb[0:64, 0:n], K['ones'][0:64, 0:64], o[0:64, 0:n])
        P.act(o[0:64, 0:n], zb[0:64, 0:n], AF.Ln, scale=1.0 / 64, bias=K['eps'][0:64, :])
        P.act(o[0:64, 0:n], o[0:64, 0:n], AF.Exp, scale=-0.5)
        P.stt(o[0:64, 0:n], Z[0:64, 0:n], par[0:64, 1:2], o[0:64, 0:n], ALU.mult, ALU.mult)
        y = yb[i2]
        P.stt(y[0:64, 0:n], o[0:64, 0:n], 0.5, sg[0:64, t0:t0 + n], ALU.mult, ALU.mult)
        store_y(P, io, 0, t0, n, y[0:64, 0:n])

    nt2 = len(tiles2)
    for it in range(nt2 + 1):
        if it < nt2:
            stageA(*tiles2[it])
        if it >= 1:
            stageB(*tiles2[it - 1])
            stageC(*tiles2[it - 1])


def ssd_phase(nc, P, A, banks, K, xn_v, io, layer, with_ctx):
    A.reset()
    NC_ = 132
    XB = A.alloc([TP], BF16)
    C2 = A.alloc([TP], BF16)
    sgz = A.alloc([NTOK], BF16)
    par = A.alloc([16], F32)
    P.dma('sp', par[:, 0:5], io['ssd_par'])
    cpar = A.alloc([2, 5], F32)
    P.dma('sp', cpar[:], io['ssd_cpar'])
    na = par[:, 5:7]
    P.act(na, par[:, 2:4], AF.Exp)
    P.ts('dve', na, na, -1.0, None, ALU.mult)
    dts_d = io['ssd_scr'][0:2]
    crow_d = io['ssd_scr'][2:4]
    clr_d = io['ssd_scr'][4:6]
    mk = A.mark()
    stage = [A.alloc([322], F32) for _ in range(2)]
    Wsb = load_weights_bf16(nc, P, A, io['w_ssd'], 322, stage)
    xbufs = [A.alloc([8, 512], BF16) for _ in range(2)]
    XR1 = A.alloc([TP], F32)
    XR2 = A.alloc([TP], F32)
    XC = A.alloc([2052], F32)
    te = [A.alloc([512], F32) for _ in range(2)]
    dtt = [A.alloc([512], F32) for _ in range(2)]
    for X in (XR1, XR2):
        P.memset('pool', X[:, 0:PADC], 0.0)
        P.memset('pool', X[:, PADC + CTXL:PADL], 0.0)
        P.memset('pool', X[:, PADL + SEQ:TP], 0.0)

    def ev_raw(dst):
        def f(ps, t0, n, ti):
            c = pcol(t0)
            P.copy('act', dst[:, c:c + n], ps[:, 0:n])
        return f

    def ev_gate_dt(ps, t0, n, ti):
        silu_evac(P, sgz[0:64, t0:t0 + n], ps[0:64, 0:n], te[ti % 2], 64, n)
        P.copy('dve', dtt[ti % 2][64:66, 0:n], ps[64:66, 0:n])
        P.dma('pool', dts_d[:, t0:t0 + n], dtt[ti % 2][64:66, 0:n])

    fm = [(0, 128, False, ev_raw(XR1)), (128, 128, False, ev_raw(XR2)), (256, 66, False, ev_gate_dt)]
    inproj(nc, P, banks, xn_v, Wsb, TOK_TILES, fm, None, xbufs, bank_ids=(0, 1, 2, 3))
    pieces = [(PADC, PADC + CTXL)] + [(PADL + 2048 * i, PADL + 2048 * (i + 1)) for i in range(4)]
    for gi, (src, dst) in enumerate(((XR1, XB), (XR2, C2))):
        for (lo, hi) in pieces:
            n = hi - lo
            P.act(XC[:, 2:2 + n], src[:, lo - 2:hi - 2], AF.Identity, scale=cpar[:, gi, 0:1], bias=cpar[:, gi, 4:5])
            for k in range(1, 4):
                P.stt(XC[:, 2:2 + n], src[:, lo + k - 2:hi + k - 2], cpar[:, gi, k:k + 1], XC[:, 2:2 + n], ALU.mult, ALU.add)
            P.act(dst[:, lo:hi], XC[:, 2:2 + n], AF.Silu)
    A.release(mk)
    PM = A.alloc([2, 128], F32)
    for d in range(2):
        P.dma('sp', PM[0:66, d, :], dts_d[d].rearrange("(p s) -> p s", s=128))
    DT = A.alloc([2, 128], F32)
    for d in range(2):
        P.act(DT[0:66, d, :], PM[0:66, d, :], AF.Exp, bias=par[0:66, d:d + 1])
    P.act(DT[0:66], DT[0:66], AF.Ln, bias=K['one'][0:66, 0:1])
    DA = A.alloc([2, 128], F32)
    for d in range(2):
        P.ts('dve', DA[0:66, d, :], DT[0:66, d, :], na[0:66, d:d + 1], None, ALU.mult)
    CUM = A.alloc([2, 128], F32)
    P.scan(CUM[0:66, 0, :], K['scanmask2'][0:66, 0, :], DA[0:66, 0, :], 0.0)
    P.scan(CUM[0:66, 1, :][:, ::-1], K['scanmask2'][0:66, 1, :][:, ::-1], DA[0:66, 1, :][:, ::-1], 0.0)
    P.dma('pool', crow_d[0].rearrange("(p s) -> p s", s=128), CUM[0:66, 0, :])
    P.dma('pool', crow_d[1].rearrange("(p s) -> p s", s=128), CUM[0:66, 1, :])
    LND = A.alloc([2, 128], F32)
    P.act(LND[0:66], DT[0:66], AF.Ln)
    CQ0 = A.alloc([2, 128], F32)
    P.tt('dve', CQ0[0:66], CUM[0:66], LND[0:66], ALU.subtract)
    CL = A.alloc([2, 2], F32)
    C4 = CUM[0:66].rearrange("p d (c s) -> p d c s", s=64)
    P.copy('dve', CL[0:66, 0, :], C4[:, 0, :, 63])
    P.copy('dve', CL[0:66, 1, :], C4[:, 1, :, 0])
    P.dma('pool', clr_d[0, 0:132].rearrange("(p c) -> p c", c=2), CL[0:66, 0, :])
    P.dma('pool', clr_d[1, 0:132].rearrange("(p c) -> p c", c=2), CL[0:66, 1, :])
    W0 = A.alloc([2, 128], F32)
    W4 = W0[0:66].rearrange("p d (c s) -> p d c s", s=64)
    P.tt('dve', W4, CL[0:66].unsqueeze(3).to_broadcast([66, 2, 2, 64]), C4, ALU.subtract)
    P.act(W0[0:66], W0[0:66], AF.Exp)
    P.tt('dve', W0[0:66], W0[0:66], DT[0:66], ALU.mult)
    CQ = A.alloc([2, 66], F32)
    WT = A.alloc([2, 66], F32)
    for d in range(2):
        for (src, dst) in ((CQ0, CQ), (W0, WT)):
            pt = banks.b[(2 * d) % 8 + (0 if src is CQ0 else 1)]
            P.transpose(pt[:, 0:66], src[0:66, d, :], K['ident'][0:66, 0:66])
            P.copy('act', dst[:, d, :], pt[:, 0:66])
    clr = A.alloc([132], F32)
    P.dma('sp', clr[0:2, :], clr_d[:, 0:132])
    DEC = A.alloc([132], F32)
    pd = banks.b[4]
    P.mm(pd[:, 0:132], K['sel2'][0:2, :], clr[0:2, :])
    P.act(DEC[:, :], pd[:, 0:132], AF.Exp)
    xBt = A.alloc([66, 128], BF16)
    for p in range(66):
        c0 = pcol(128 * p)
        pt = banks.b[p % 4].bitcast(BF16)
        P.transpose(pt[:, 0:128], XB[:, c0:c0 + 128], K['identb'][:, :])
        P.copy('act' if p % 2 == 0 else 'dve', xBt[:, p, :], pt[:, 0:128])
    BW = A.alloc([66, 128], BF16)
    for d in range(2):
        P.tt('dve', BW[:, :, 64 * d:64 * d + 64], xBt[:, :, 64:128], WT[:, d, :].unsqueeze(2).to_broadcast([128, 66, 64]), ALU.mult)
    ST = A.alloc([NC_ + 2, 64], F32)
    STb = A.alloc([NC_ + 2, 64], BF16)
    P.memset('pool', ST[:, 0:2, :], 0.0)
    P.memset('pool', ST[:, NC_:NC_ + 2, :], 0.0)
    for g0 in range(0, 66, 8):
        npair = min(8, 66 - g0)
        for i in range(npair):
            p = g0 + i
            for j in range(2):
                P.mm(banks.b[j][:, i * 64:(i + 1) * 64], BW[64 * j:64 * j + 64, p, :], xBt[64 * j:64 * j + 64, p, 0:64])
        for j in range(2):
            src = banks.b[j][:, 0:npair * 64].rearrange("p (c v) -> p c v", v=64)
            c0 = 2 * g0 + j
            P.copy('act', ST[0:64, c0 + 2:c0 + 1 + 2 * npair:2, :], src[0:64])
            P.copy('dve', ST[64:128, c0:c0 + 2 * npair - 1:2, :], src[64:128])
    fsteps = [(c + 2, c + 1, c) for c in range(1, NC_)]
    bsteps = [(c, c + 1, c) for c in (2, 1, 0)] + [(131, 0, 131)] + [(c, c + 1, c) for c in range(130, 4, -1)]
    for i in range(max(len(fsteps), len(bsteps))):
        if i < len(fsteps):
            o_, i_, d_ = fsteps[i]
            P.stt(ST[0:64, o_, :], ST[0:64, i_, :], DEC[0:64, d_:d_ + 1], ST[0:64, o_, :], ALU.mult, ALU.add)
        if i < len(bsteps):
            o_, i_, d_ = bsteps[i]
            P.stt(ST[64:128, o_, :], ST[64:128, i_, :], DEC[64:128, d_:d_ + 1], ST[64:128, o_, :], ALU.mult, ALU.add)
    P.copy('dve', ST[64:128, 132, :], ST[64:128, 0, :])
    P.copy('dve', STb[:], ST[:])
    crt = [A.alloc([512], F32) for _ in range(2)]
    SG = [A.alloc([2, 512], F32) for _ in range(2)]
    Ls = [A.alloc([512], F32) for _ in range(2)]
    Mt = [A.alloc([512], BF16) for _ in range(2)]
    ec = [A.alloc([512], F32) for _ in range(2)]
    Cs = [A.alloc([512], BF16) for _ in range(2)]
    yv = [A.alloc([512], F32) for _ in range(2)]
    yb = [A.alloc([512], BF16) for _ in range(2)]
    tiles2 = [(ti, t0, n) for ti, (t0, n) in enumerate(TOK_TILES) if not (t0 < CTXL and not with_ctx)]

    def stageA(ti, t0, n):
        i2 = ti % 2
        npair = n // 128
        p0 = t0 // 128
        pc0 = pcol(t0)
        cr = crt[i2]
        P.dma('sp', cr[0:2, 0:n], crow_d[:, t0:t0 + n])
        pe_ = banks.b[0]
        P.mm(pe_[:, 0:n], K['sel2'][0:2, :], cr[0:2, 0:n])
        P.act(ec[i2][:, 0:n], pe_[:, 0:n], AF.Exp)
        P.tt('dve', Cs[i2][:, 0:n], C2[:, pc0:pc0 + n], ec[i2][:, 0:n], ALU.mult)
        for d in range(2):
            pb = banks.b[1 + d]
            P.mm(pb[:, 0:n], K['selrow'][0:2, d, :], cr[0:2, 0:n], start=True, stop=False)
            P.mm(pb[:, 0:n], K['identb'][:, :], K['negF' if d == 0 else 'negB'][:, 0:n], start=False, stop=True)
            P.tt('dve', SG[i2][:, d, 0:n].rearrange("p (a b) -> p a b", b=128),
                 pb[:, 0:n].rearrange("p (a b) -> p a b", b=128),
                 CQ[:, d, p0:p0 + npair].unsqueeze(2).to_broadcast([128, npair, 128]), ALU.subtract)
        P.act(SG[i2][:, :, 0:n], SG[i2][:, :, 0:n], AF.Exp)
        P.tt('pool', Ls[i2][:, 0:n], SG[i2][:, 0, 0:n], SG[i2][:, 1, 0:n], ALU.add)
        pcb = banks.b[3]
        for i in range(npair):
            cs = slice(pc0 + 128 * i, pc0 + 128 * (i + 1))
            P.mm(pcb[:, 128 * i:128 * (i + 1)], XB[64:128, cs], C2[64:128, cs])
        P.tt('dve', Mt[i2][:, 0:n], pcb[:, 0:n], Ls[i2][:, 0:n], ALU.mult)

    def stageB(ti, t0, n):
        i2 = ti % 2
        npair = n // 128
        p0 = t0 // 128
        pc0 = pcol(t0)
        Y = banks.b[4 + i2]
        for i in range(npair):
            p = p0 + i
            P.mm(Y[0:64, 128 * i:128 * (i + 1)], xBt[:, p, 0:64], Mt[i2][:, 128 * i:128 * (i + 1)], start=True, stop=False)
            for j in range(2):
                c = 2 * p + j
                kk = 64 if c == 3 else 128
                P.mm(Y[0:64, 128 * i + 64 * j:128 * i + 64 * (j + 1)], STb[0:kk, c + 1, :],
                     Cs[i2][0:kk, 128 * i + 64 * j:128 * i + 64 * (j + 1)], start=False, stop=(j == 1))
        P.stt(yv[i2][0:64, 0:n], XB[0:64, pc0:pc0 + n], par[0:64, 4:5], Y[0:64, 0:n], ALU.mult, ALU.add)
        y = yb[i2]
        P.stt(y[0:64, 0:n], yv[i2][0:64, 0:n], 0.5, sgz[0:64, t0:t0 + n], ALU.mult, ALU.mult)
        store_y(P, io, 3, t0, n, y[0:64, 0:n])

    nt2 = len(tiles2)
    for it in range(nt2 + 1):
        if it < nt2:
            stageA(*tiles2[it])
        if it >= 1:
            stageB(*tiles2[it - 1])

OFF_A, OFF_B, OFF_C, OFF_D = 0, 800, 1312, 2336


def rope_tables():
    n_freq = 8
    inv = (np.float32(10000.0) ** (-(np.arange(n_freq, dtype=np.float32)) / np.float32(n_freq))).astype(np.float32)
    t = np.arange(SEQ)
    pos_r = (t // 64).astype(np.float32)
    pos_c = (t % 64).astype(np.float32)
    ang_r = pos_r[:, None] * inv
    ang_c = pos_c[:, None] * inv
    ang = np.concatenate([ang_r, ang_r, ang_c, ang_c], axis=-1).astype(np.float32)
    cos = np.cos(ang).astype(np.float32).T
    sin = np.sin(ang).astype(np.float32).T
    sign = np.ones(32, np.float32)
    for a in range(2):
        sign[a * 16:a * 16 + 8] = -1.0
    sins = sin * sign[:, None]
    cosT = np.zeros((128, SEQ), np.float32)
    sinT = np.zeros((128, SEQ), np.float32)
    for c in range(2):
        cosT[64 * c:64 * c + 32] = cos
        sinT[64 * c:64 * c + 32] = sins
    return cosT, sinT


def rot_perm():
    perm = np.zeros(32, np.int64)
    for a in range(2):
        for f in range(8):
            perm[a * 16 + f] = a * 16 + 8 + f
            perm[a * 16 + 8 + f] = a * 16 + f
    return perm


def prep_w_attn(w_in_l, h):
    W = np.zeros((D, 640), np.float32)
    perm = rot_perm()
    for c in range(2):
        qc = OFF_C + h * 64 + c * 32
        kc = OFF_C + 256 + h * 64 + c * 32
        W[:, 64 * c:64 * c + 32] = w_in_l[:, qc:qc + 32]
        W[:, 128 + 64 * c:128 + 64 * c + 32] = w_in_l[:, qc + perm]
        W[:, 256 + 64 * c:256 + 64 * c + 32] = w_in_l[:, kc:kc + 32]
        W[:, 384 + 64 * c:384 + 64 * c + 32] = w_in_l[:, kc + perm]
    W[:, 512:576] = w_in_l[:, OFF_C + 768 + h * 64:OFF_C + 768 + h * 64 + 64]
    W[:, 576:640] = w_in_l[:, OFF_C + 512 + h * 64:OFF_C + 512 + h * 64 + 64]
    return W


def mix_consts():
    selZ = np.zeros((128, 64), np.float32)
    selZ[64, :] = 1.0
    t = np.arange(512)
    scanmask = np.zeros((64, 512), np.float32)
    scanmask[0:32] = (t % 64 != 0).astype(np.float32)[None]
    scanmask[32:64] = (t % 64 != 63).astype(np.float32)[None]
    j = np.arange(128)[:, None]
    i = np.arange(128)[None, :]
    same = (j // 64) == (i // 64)
    mF = (same & (j <= i)).astype(np.float32)
    mB = (same & (j >= i)).astype(np.float32)
    scanmask2 = np.zeros((128, 2, 128), np.float32)
    s_ = np.arange(128)
    scanmask2[:, 0, :] = (s_ % 64 != 0).astype(np.float32)[None]
    scanmask2[:, 1, :] = (s_ % 64 != 63).astype(np.float32)[None]
    sel2 = np.zeros((2, 128), np.float32)
    sel2[0, 0:64] = 1.0
    sel2[1, 64:128] = 1.0
    selrow = np.zeros((2, 2, 128), np.float32)
    selrow[0, 0, :] = 1.0
    selrow[1, 1, :] = 1.0
    NEG = -30000.0
    negF = np.where(mF > 0, 0.0, NEG).astype(np.float32)
    negB = np.where(mB > 0, 0.0, NEG).astype(np.float32)
    return {'selZ': selZ, 'ident': np.eye(128, dtype=np.float32), 'scanmask': scanmask,
            'maskF': np.tile(mF, (1, 4)), 'maskB': np.tile(mB, (1, 4)), 'scanmask2': scanmask2,
            'sel2': sel2, 'selrow': selrow, 'negF': np.tile(negF, (1, 4)), 'negB': np.tile(negB, (1, 4))}


def build_mix_kernel(nc, P, layer, with_ctx, mixers):
    banks = Banks(P)
    K = load_consts_tok(nc, P)
    epst = P.sbuf("eps_t", [128, 1], F32)
    P.memset('pool', epst[:], EPS)
    K['eps'] = epst
    selZ_d = nc.dram_tensor("selZ", [128, 64], F32, kind="ExternalInput").ap()
    selZ = P.sbuf("selZ_sb", [128, 64], F32)
    P.dma('sp', selZ[:], selZ_d)
    K['selZ'] = selZ
    ident_d = nc.dram_tensor("ident", [128, 128], F32, kind="ExternalInput").ap()
    ident = P.sbuf("ident_sb", [128, 128], F32)
    P.dma('sp', ident[:], ident_d)
    identb = P.sbuf("identb_sb", [128, 128], BF16)
    P.copy('pool', identb[:], ident[:])
    K['ident'] = ident
    K['identb'] = identb
    one = P.sbuf("one_t", [128, 1], F32)
    P.memset('pool', one[:], 1.0)
    K['one'] = one
    for nm, shp in (('scanmask', [64, 512]), ('maskF', [128, 512]), ('maskB', [128, 512]),
                    ('scanmask2', [128, 2, 128]), ('sel2', [2, 128]), ('selrow', [2, 2, 128])):
        d_ = nc.dram_tensor(nm, shp, F32, kind="ExternalInput").ap()
        t_ = P.sbuf(nm + "_sb", shp, F32)
        P.dma('sp', t_[:], d_)
        K[nm] = t_
    for nm in ('negF', 'negB'):
        d_ = nc.dram_tensor(nm, [128, 512], F32, kind="ExternalInput").ap()
        t_ = P.sbuf(nm + "_f", [128, 512], F32)
        P.dma('sp', t_[:], d_)
        tb_ = P.sbuf(nm + "_sb", [128, 512], BF16)
        P.copy('pool', tb_[:], t_[:])
        K[nm] = tb_
    io = {}
    xn_d = nc.dram_tensor("xn", [D, NTOK], BF16, kind="ExternalInput").ap()
    xn_v = xn_d.rearrange("(kc p) t -> p kc t", p=128)
    io['yT'] = nc.dram_tensor("yT", [4, 64, NTOK], BF16, kind="ExternalOutput").ap()
    A = P.arena("arena", 190 * 1024)
    if 'attn' in mixers:
        io['w_attn'] = nc.dram_tensor("w_attn", [D, 640], F32, kind="ExternalInput").ap()
        io['cosT'] = nc.dram_tensor("cosT", [128, SEQ], F32, kind="ExternalInput").ap()
        io['sinT'] = nc.dram_tensor("sinT", [128, SEQ], F32, kind="ExternalInput").ap()
        io['diff_lam'] = nc.dram_tensor("diff_lam", [128], F32, kind="ExternalInput").ap()
        io['diff_subln_w'] = nc.dram_tensor("diff_subln_w", [64, 1], F32, kind="ExternalInput").ap()
        attention_phase(nc, P, A, banks, K, xn_v, io, layer, with_ctx)
    if 'gla' in mixers:
        _add_gla(nc, P, A, banks, K, xn_v, io, layer, with_ctx)
    if 'ssd' in mixers:
        io['w_ssd'] = nc.dram_tensor("w_ssd", [D, 322], F32, kind="ExternalInput").ap()
        io['ssd_par'] = nc.dram_tensor("ssd_par", [128, 5], F32, kind="ExternalInput").ap()
        io['ssd_cpar'] = nc.dram_tensor("ssd_cpar", [128, 2, 5], F32, kind="ExternalInput").ap()
        io['ssd_scr'] = nc.dram_tensor("ssd_scr", [6, NTOK], F32, kind="Internal").ap()
        ssd_phase(nc, P, A, banks, K, xn_v, io, layer, with_ctx)
    if 'lru' in mixers:
        io['w_lru'] = nc.dram_tensor("w_lru", [D, 192], F32, kind="ExternalInput").ap()
        io['lru_par'] = nc.dram_tensor("lru_par", [128, 9], F32, kind="ExternalInput").ap()
        io['lru_gw'] = nc.dram_tensor("lru_gw", [64, 256], F32, kind="ExternalInput").ap()
        lru_phase(nc, P, A, banks, K, xn_v, io, layer, with_ctx)


def _add_gla(nc, P, A, banks, K, xn_v, io, layer, with_ctx):
    io['w_gla'] = nc.dram_tensor("w_gla", [D, 320], F32, kind="ExternalInput").ap()
    io['gla_par'] = nc.dram_tensor("gla_par", [64, 2], F32, kind="ExternalInput").ap()
    io['gla_w2bd'] = nc.dram_tensor("gla_w2bd", [32, 64], F32, kind="ExternalInput").ap()
    gla_phase(nc, P, A, banks, K, xn_v, io, layer, with_ctx)


def prep_gla(inp, l, h):
    w_in_l = inp['w_in'][l]
    W = np.zeros((D, 288), np.float32)
    q = w_in_l[:, OFF_A + h * 32:OFF_A + h * 32 + 32]
    k = w_in_l[:, OFF_A + 128 + h * 32:OFF_A + 128 + h * 32 + 32]
    W[:, 0:32] = q
    W[:, 32:64] = q
    W[:, 64:96] = k
    W[:, 96:128] = k
    W[:, 128:144] = w_in_l[:, OFF_A + 512:OFF_A + 528]
    W[:, 144:160] = w_in_l[:, OFF_A + 528:OFF_A + 544]
    W[:, 192:256] = w_in_l[:, OFF_A + 544 + h * 64:OFF_A + 544 + h * 64 + 64]
    W[:, 256:288] = 0
    Wv = w_in_l[:, OFF_A + 256 + h * 64:OFF_A + 256 + h * 64 + 64]
    W2 = np.zeros((D, 320), np.float32)
    W2[:, 0:256] = W[:, 0:256]
    W2[:, 256:320] = Wv
    par = np.zeros((64, 2), np.float32)
    par[0:32, 0] = inp['gla_b2'][l][0][h * 32:(h + 1) * 32]
    par[32:64, 0] = inp['gla_b2'][l][1][h * 32:(h + 1) * 32]
    par[:, 1] = inp['gla_norm_w'][l]
    w2bd = np.zeros((32, 64), np.float32)
    w2bd[0:16, 0:32] = inp['gla_w2'][l][0][:, h * 32:(h + 1) * 32]
    w2bd[16:32, 32:64] = inp['gla_w2'][l][1][:, h * 32:(h + 1) * 32]
    return {"w_gla": W2, "gla_par": par, "gla_w2bd": w2bd}


def prep_ssd(inp, l, h):
    w_in_l = inp['w_in'][l]
    gr = h // 2
    W = np.zeros((D, 322), np.float32)
    cx = slice(OFF_D + h * 64, OFF_D + h * 64 + 64)
    cB = slice(OFF_D + 256 + gr * 64, OFF_D + 256 + gr * 64 + 64)
    cC = slice(OFF_D + 384 + gr * 64, OFF_D + 384 + gr * 64 + 64)
    W[:, 0:64] = w_in_l[:, cx]
    W[:, 64:128] = w_in_l[:, cB]
    W[:, 128:192] = w_in_l[:, cC]
    W[:, 192:256] = w_in_l[:, cC]
    W[:, 256:320] = w_in_l[:, OFF_D + 520 + h * 64:OFF_D + 520 + h * 64 + 64]
    W[:, 320] = w_in_l[:, OFF_D + 512 + h]
    W[:, 321] = w_in_l[:, OFF_D + 516 + h]
    par = np.zeros((128, 5), np.float32)
    par[:, 0] = inp['ssd_dt_bias'][l][0][h]
    par[:, 1] = inp['ssd_dt_bias'][l][1][h]
    par[:, 2] = inp['ssd_a_log'][l][0][h]
    par[:, 3] = inp['ssd_a_log'][l][1][h]
    par[:, 4] = inp['ssd_d'][l][h]
    cw, cb = inp['ssd_conv_w'][l], inp['ssd_conv_b'][l]
    cpar = np.zeros((128, 2, 5), np.float32)
    ch = [np.r_[h * 64:h * 64 + 64, 256 + gr * 64:256 + gr * 64 + 64],
          np.r_[384 + gr * 64:384 + gr * 64 + 64, 384 + gr * 64:384 + gr * 64 + 64]]
    for g in range(2):
        cpar[:, g, 0:4] = cw[:, ch[g]].T
        cpar[:, g, 4] = cb[ch[g]]
    return {"w_ssd": W, "ssd_par": par, "ssd_cpar": cpar}


def prep_lru(inp, l, h):
    w_in_l = inp['w_in'][l]
    W = np.zeros((D, 192), np.float32)
    xs = w_in_l[:, OFF_B + h * 64:OFF_B + h * 64 + 64]
    W[:, 0:64] = xs
    W[:, 64:128] = xs
    W[:, 128:192] = w_in_l[:, OFF_B + 256 + h * 64:OFF_B + 256 + h * 64 + 64]
    sl = slice(h * 64, h * 64 + 64)
    par = np.zeros((128, 9), np.float32)
    for d in range(2):
        rows = slice(64 * d, 64 * d + 64)
        par[rows, 0:4] = inp['lru_conv_w'][l][:, sl].T
        par[rows, 4] = inp['lru_conv_b'][l][sl]
        par[rows, 5] = inp['lru_ba'][l][d][sl]
        par[rows, 6] = inp['lru_bx'][l][d][sl]
        par[rows, 7] = inp['lru_lam'][l][d][sl]
    gw = np.zeros((64, 256), np.float32)
    for d in range(2):
        gw[:, 64 * d:64 * d + 64] = inp['lru_wa'][l][d][h]
        gw[:, 128 + 64 * d:128 + 64 * d + 64] = inp['lru_wx'][l][d][h]
    return {"w_lru": W, "lru_par": par, "lru_gw": gw}


def mix_inputs(inp, l, h, mixers, cosT, sinT, cst):
    m = dict(cst)
    if 'ssd' in mixers:
        m.update(prep_ssd(inp, l, h))
    if 'gla' in mixers:
        m.update(prep_gla(inp, l, h))
    if 'attn' in mixers:
        m.update({"w_attn": prep_w_attn(inp['w_in'][l], h), "cosT": cosT, "sinT": sinT,
                  "diff_lam": np.ascontiguousarray(inp['diff_lam'][l].reshape(-1)),
                  "diff_subln_w": np.ascontiguousarray(inp['diff_subln_w'][l].reshape(64, 1))})
    if 'lru' in mixers:
        m.update(prep_lru(inp, l, h))
    return m


def _rt_of(hl, hc, b, q):
    return np.ascontiguousarray(np.concatenate([hc[b, 64 * q:64 * q + 64].T, hl[b, 2048 * q:2048 * (q + 1)].T], axis=1))


def _gather_cols(parts):
    return np.ascontiguousarray(np.concatenate([p[:, 0:64] for p in parts] + [p[:, 64:] for p in parts], axis=1))


def kernel_unfused(**inp):
    inp = {k: np.asarray(v) for k, v in inp.items()}
    x, ctx, c, c_ctx = inp['x'], inp['ctx'], inp['c'], inp['c_ctx']
    cosT, sinT = rope_tables()
    cst = mix_consts()
    cvecs = [np.ascontiguousarray(np.stack([pk(c[b]), pk(c_ctx)], axis=2)) for b in range(2)]
    maps = []
    for core in range(8):
        b, q = core // 4, core % 4
        maps.append({"RT": _rt_of(x, ctx, b, q), "cvec": cvecs[b], "wmod_n": inp['w_mod'][0],
                     "bmod_n": pk(inp['b_mod'][0]), "normw": pk(inp['norm_w'][0])})
    res = _launch(lambda nc, P: build_tok_kernel(nc, P, True, False, False), maps)
    RT = [m["RT"] for m in maps]
    xnT = [np.asarray(r["xnT"]) for r in res]
    out = None
    for l in range(2):
        last = (l == 1)
        xn_full = [_gather_cols(xnT[4 * b:4 * b + 4]) for b in range(2)]
        maps = []
        for core in range(8):
            b, h = core // 4, core % 4
            m = {"xn": xn_full[b]}
            m.update(mix_inputs(inp, l, h, ('gla', 'lru', 'attn', 'ssd'), cosT, sinT, cst))
            maps.append(m)
        res = _launch(lambda nc, P: build_mix_kernel(nc, P, l, not last, ('gla', 'lru', 'attn', 'ssd')), maps)
        yT = [np.asarray(r["yT"]) for r in res]
        maps = []
        for core in range(8):
            b, q = core // 4, core % 4
            yb = np.stack([yT[4 * b + h] for h in range(4)], axis=1).reshape(1024, NTOK)
            yq = np.ascontiguousarray(np.concatenate([yb[:, 64 * q:64 * q + 64],
                                                      yb[:, CTXL + 2048 * q:CTXL + 2048 * (q + 1)]], axis=1))
            m = {"RT": RT[core], "cvec": cvecs[b], "yT": yq, "wout": inp['w_out'][l], "ssdnw": pk(inp['ssd_norm_w'][l]),
                 "wmod_g": inp['w_mod'][l], "bmod_g": pk(inp['b_mod'][l])}
            if last:
                m["fnw"] = pk(inp['final_norm_w'])
            else:
                m.update({"wmod_n": inp['w_mod'][l + 1], "bmod_n": pk(inp['b_mod'][l + 1]), "normw": pk(inp['norm_w'][l + 1])})
            maps.append(m)
        res = _launch(lambda nc, P: build_tok_kernel(nc, P, False, last, True), maps)
        if last:
            out = np.zeros((2, SEQ, D), np.float32)
            for core in range(8):
                b, q = core // 4, core % 4
                out[b, 2048 * q:2048 * (q + 1), :] = np.asarray(res[core]["outT"]).T
        else:
            RT = [np.asarray(r["RTout"]) for r in res]
            xnT = [np.asarray(r["xnT"]) for r in res]
    return out

CHUNKS = [(0, 256)] + [(256 + 2048 * k, 2048) for k in range(4)]
GROUPS = [[0, 1, 2, 3], [4, 5, 6, 7]]


def fs_mod(nc, P, K, banks, A, cvec_d, wmod_d, bmod_d, name):
    cT = A.alloc([8, 2], F32)
    P.dma('sp', cT[:], cvec_d)
    e = A.alloc([8, 2], F32)
    P.act(e[:], cT[:], AF.Exp, scale=-1.0)
    P.ts('dve', e[:], e[:], 1.0, None, ALU.add)
    P.recip(e[:], e[:])
    sc = A.alloc([8, 2], F32)
    P.tt('dve', sc[:], cT[:], e[:], ALU.mult)
    bT = A.alloc([6], F32)
    P.dma('sp', bT[:], bmod_d)
    wb = A.alloc([8, 768], F32)
    for kc in range(8):
        P.dma('sp' if kc % 2 == 0 else 'act', wb[:, kc, :], wmod_d[kc * 128:(kc + 1) * 128, :])
    ps = banks.b[7]
    for fc in range(6):
        for kc in range(8):
            P.mm(ps[:, fc * 2:fc * 2 + 2], wb[:, kc, fc * 128:(fc + 1) * 128], sc[:, kc, :], start=(kc == 0), stop=(kc == 7))
    modT = P.sbuf(name, [128, 6, 2], F32)
    P.tt('dve', modT[:], ps[:, 0:12].rearrange("p (a b) -> p a b", b=2), bT[:].unsqueeze(2).to_broadcast([128, 6, 2]), ALU.add)
    return modT


def fs_token_phase(nc, P, K, banks, A, io, L, mode):
    A.reset()
    D_ = io['dram']
    first, last = mode == 'first', mode == 'last'
    if not first:
        gateT = fs_mod(nc, P, K, banks, A, io['cvec'], io['wmod'][L], io['bmod'][L], "modT_g%d" % L)
        wo32 = A.alloc([8, 256], F32)
        for kc in range(8):
            P.dma('sp' if kc % 2 == 0 else 'act', wo32[:, kc, :], io['wout'][L][kc * 128:(kc + 1) * 128, :])
        wo = A.alloc([8, 256], BF16)
        P.copy('dve', wo[:], wo32[:])
        sw = A.alloc([4], F32)
        P.dma('sp', sw[:], io['ssdnw'][L])
    if not last:
        LN = 0 if first else L + 1
        modN = fs_mod(nc, P, K, banks, A, io['cvec'], io['wmod'][LN], io['bmod'][LN], "modT_n%d" % LN)
        nw = A.alloc([2], F32)
        P.dma('sp', nw[:], io['normw'][LN])
        Acoef = A.alloc([2, 2], F32)
        P.ts('dve', Acoef[:], modN[:, 2:4, :], 1.0, None, ALU.add)
        P.tt('dve', Acoef[:], Acoef[:], nw[:].unsqueeze(2).to_broadcast([128, 2, 2]), ALU.mult)
    else:
        fw = A.alloc([2], F32)
        P.dma('sp', fw[:], io['fnw'])
    Rb = [A.alloc([2, 2048], F32) for _ in range(2)]
    yb_ = [A.alloc([8, 512], BF16) for _ in range(2)]
    sq = [A.alloc([512], BF16) for _ in range(4)]
    rst = [A.alloc([512], F32) for _ in range(2)]
    t1 = [A.alloc([512], F32) for _ in range(4)]
    ssr = [A.alloc([2048], F32) for _ in range(2)]
    ssg = [A.alloc([2048], F32) for _ in range(2)]
    xo = [A.alloc([2, 512], BF16) for _ in range(2)]
    Rsrc = io['RT_in'] if first else D_['Rs']
    Rsrc_v = Rsrc.rearrange("(fc p) t -> p fc t", p=128)
    Rs_v = D_['Rs'].rearrange("(fc p) t -> p fc t", p=128)
    stage = 'n%d' % (0 if first else L + 1) if not last else 'fin'
    chunks = [(ci, t0, n) for ci, (t0, n) in enumerate(CHUNKS) if not (last and ci == 0)]
    jobs = []
    for (ci, t0, n) in chunks:
        subs = [(s0, min(512, n - s0)) for s0 in range(0, n, 512)]
        for si, (s0, m_) in enumerate(subs):
            jobs.append(dict(ci=ci, t0=t0, n=n, s0=s0, m=m_, first=(si == 0), lastsub=(si == len(subs) - 1)))
    ybuf3 = yb_ + [A.alloc([8, 512], BF16)]

    def P1(j, ji):
        ci, t0, n, s0, m = j['ci'], j['t0'], j['n'], j['s0'], j['m']
        R = Rb[ci % 2]
        if j['first']:
            P.dma('sp', R[:, :, 0:n], Rsrc_v[:, :, t0:t0 + n])
        if first:
            return
        gy = D_['gy%d' % L][ci].rearrange("(kc p) t -> p kc t", p=128)
        yt = ybuf3[ji % 3]
        P.dma('act', yt[:, :, 0:m], gy[:, :, s0:s0 + m])
        ps = banks.b[0 + ji % 2]
        for i, kc in enumerate((1, 3, 5, 7)):
            s_ = sq[i % 2]
            P.act(s_[64:128, 0:m], yt[64:128, kc, 0:m], AF.Square)
            P.mm(ps[:, 0:m], K['onesb'][64:128, :], s_[64:128, 0:m], start=(i == 0), stop=(i == 3))
        r_ = rst[ji % 2]
        P.act(r_[:, 0:m], ps[:, 0:m], AF.Ln, scale=1.0 / 256, bias=K['eps'][:])
        P.act(r_[:, 0:m], r_[:, 0:m], AF.Exp, scale=-0.5)
        for i, kc in enumerate((1, 3, 5, 7)):
            P.stt(yt[64:128, kc, 0:m], yt[64:128, kc, 0:m], sw[64:128, i:i + 1], r_[64:128, 0:m], ALU.mult, ALU.mult)

    def P2(j, ji):
        if first:
            return
        ci, s0, m = j['ci'], j['s0'], j['m']
        isctx = 1 if ci == 0 else 0
        R = Rb[ci % 2]
        yt = ybuf3[ji % 3]
        for fc in range(2):
            po = banks.b[2 + (2 * ji + fc) % 4]
            for kc in range(8):
                P.mm(po[:, 0:m], wo[:, kc, fc * 128:(fc + 1) * 128], yt[:, kc, 0:m], start=(kc == 0), stop=(kc == 7))
            P.stt(R[:, fc, s0:s0 + m], po[:, 0:m], gateT[:, 4 + fc, isctx:isctx + 1], R[:, fc, s0:s0 + m], ALU.mult, ALU.add)

    def P3(j, ji):
        ci, t0, n, s0, m = j['ci'], j['t0'], j['n'], j['s0'], j['m']
        R = Rb[ci % 2]
        srow = ssr[ci % 2]
        pss = banks.b[6 + ji % 2]
        for fc in range(2):
            s_ = sq[2 + fc]
            P.act(s_[:, 0:m], R[:, fc, s0:s0 + m], AF.Square)
            P.mm(pss[:, 0:m], K['onesb'][:, :], s_[:, 0:m], start=(fc == 0), stop=(fc == 1))
        P.copy('dve', srow[0:1, s0:s0 + m], pss[0:1, 0:m])
        if j['lastsub']:
            P.dma('sp', D_['ssb_' + stage][:, t0:t0 + n], srow[0:1, 0:n])
            P.dma('sp', Rs_v[:, :, t0:t0 + n], R[:, :, 0:n])

    nj = len(jobs)
    for it in range(nj + 2):
        if it < nj:
            P1(jobs[it], it)
        if 0 <= it - 1 < nj:
            P2(jobs[it - 1], it - 1)
        if 0 <= it - 2 < nj:
            P3(jobs[it - 2], it - 2)
    P.collective("AllGather", [D_['ssb_' + stage]], [D_['ssg_' + stage]], GROUPS)
    for (ci, t0, n) in chunks:
        isctx = 1 if ci == 0 else 0
        R = Rb[ci % 2]
        P.dma('sp', R[:, :, 0:n], Rs_v[:, :, t0:t0 + n])
        subs = [(s0, min(512, n - s0)) for s0 in range(0, n, 512)]
        sg_ = ssg[ci % 2]
        P.dma('act', sg_[0:4, 0:n], D_['ssg_' + stage][:, t0:t0 + n])
        for si, (s0, m) in enumerate(subs):
            pt = banks.b[6 + si % 2]
            P.mm(pt[:, 0:m], K['ones'][0:4, :], sg_[0:4, s0:s0 + m])
            r_ = rst[si % 2]
            P.act(r_[:, 0:m], pt[:, 0:m], AF.Ln, scale=1.0 / D, bias=K['eps'][:])
            P.act(r_[:, 0:m], r_[:, 0:m], AF.Exp, scale=-0.5)
            if not last:
                x_ = xo[si % 2]
                for fc in range(2):
                    t_ = t1[fc]
                    P.stt(t_[:, 0:m], R[:, fc, s0:s0 + m], Acoef[:, fc, isctx:isctx + 1], r_[:, 0:m], ALU.mult, ALU.mult)
                    P.act(x_[:, fc, 0:m], t_[:, 0:m], AF.Identity, bias=modN[:, fc, isctx:isctx + 1])
                xb_v = D_['xnb_' + stage][ci].rearrange("(fc p) t -> p fc t", p=128)
                P.dma('sp', xb_v[:, :, s0:s0 + m], x_[:, :, 0:m])
            else:
                o_v = io['outT'].rearrange("(fc p) t -> p fc t", p=128)
                for fc in range(2):
                    t_ = t1[(si * 2 + fc) % 4]
                    P.stt(t_[:, 0:m], R[:, fc, s0:s0 + m], fw[:, fc:fc + 1], r_[:, 0:m], ALU.mult, ALU.mult)
                    P.dma('sp', o_v[:, fc, t0 - CTXL + s0:t0 - CTXL + s0 + m], t_[:, 0:m], final=True)
        if not last:
            P.collective("AllGather", [D_['xnb_' + stage][ci]], [D_['xng_' + stage][ci]], GROUPS)


def build_fused(nc, P):
    banks = Banks(P)
    K = load_consts_tok(nc, P)
    epst = P.sbuf("eps_t", [128, 1], F32)
    P.memset('pool', epst[:], EPS)
    K['eps'] = epst
    one = P.sbuf("one_t", [128, 1], F32)
    P.memset('pool', one[:], 1.0)
    K['one'] = one
    for nm, shp in (('selZ', [128, 64]), ('ident', [128, 128]), ('scanmask', [64, 512]), ('maskF', [128, 512]),
                    ('maskB', [128, 512]), ('scanmask2', [128, 2, 128]), ('sel2', [2, 128]), ('selrow', [2, 2, 128])):
        d_ = nc.dram_tensor(nm, shp, F32, kind="ExternalInput").ap()
        t_ = P.sbuf(nm + "_sb", shp, F32)
        P.dma('sp', t_[:], d_)
        K[nm] = t_
    identb = P.sbuf("identb_sb", [128, 128], BF16)
    P.copy('pool', identb[:], K['ident'][:])
    K['identb'] = identb
    onesb = P.sbuf("onesb_sb", [128, 128], BF16)
    P.memset('pool', onesb[:], 1.0)
    K['onesb'] = onesb
    for nm in ('negF', 'negB'):
        d_ = nc.dram_tensor(nm, [128, 512], F32, kind="ExternalInput").ap()
        t_ = P.sbuf(nm + "_f", [128, 512], F32)
        P.dma('sp', t_[:], d_)
        tb_ = P.sbuf(nm + "_sb", [128, 512], BF16)
        P.copy('pool', tb_[:], t_[:])
        K[nm] = tb_
    A = P.arena("arena", 184 * 1024)

    def din(name, shape, dt=F32):
        return nc.dram_tensor(name, list(shape), dt, kind="ExternalInput").ap()

    def dscr(name, shape, dt=F32):
        return nc.dram_tensor(name, list(shape), dt).ap()

    io = {'RT_in': din("RT", [256, NTOK]), 'cvec': din("cvec", [128, 8, 2]),
          'wmod': [din("wmod%d" % l, [D, 768]) for l in range(2)], 'bmod': [din("bmod%d" % l, [128, 6]) for l in range(2)],
          'normw': [din("normw%d" % l, [128, 2]) for l in range(2)], 'wout': [din("wout%d" % l, [D, 256]) for l in range(2)],
          'ssdnw': [din("ssdnw%d" % l, [128, 4]) for l in range(2)], 'fnw': din("fnw", [128, 2]),
          'outT': nc.dram_tensor("outT", [256, SEQ], F32, kind="ExternalOutput").ap()}
    Dm = {'Rs': dscr("Rs", [256, NTOK])}
    for stage in ('n0', 'n1', 'fin'):
        Dm['ssb_' + stage] = dscr("ssb_%s" % stage, [1, NTOK])
        Dm['ssg_' + stage] = dscr("ssg_%s" % stage, [4, NTOK])
    for stage in ('n0', 'n1'):
        Dm['xnb_' + stage] = [dscr("xnb_%s_%d" % (stage, ci), [256, n], BF16) for ci, (t0, n) in enumerate(CHUNKS)]
        Dm['xng_' + stage] = [dscr("xng_%s_%d" % (stage, ci), [D, n], BF16) for ci, (t0, n) in enumerate(CHUNKS)]
    for l in range(2):
        Dm['yb%d' % l] = [dscr("yb%d_%d" % (l, ci), [256, n], BF16) for ci, (t0, n) in enumerate(CHUNKS)]
        Dm['gy%d' % l] = [dscr("gy%d_%d" % (l, ci), [D, n], BF16) for ci, (t0, n) in enumerate(CHUNKS)]
    io['dram'] = Dm
    cosT = din("cosT", [128, SEQ])
    sinT = din("sinT", [128, SEQ])
    fs_token_phase(nc, P, K, banks, A, io, 0, 'first')
    for l in range(2):
        last = (l == 1)
        with_ctx = not last
        xng = Dm['xng_n%d' % l]

        def xn_load(xb, t0, n, xng=xng):
            if t0 < CTXL:
                P.dma('sp', xb[:, :, 0:n], xng[0].rearrange("(kc p) t -> p kc t", p=128)[:, :, t0:t0 + n])
            else:
                k = (t0 - CTXL) // 2048
                c0 = (t0 - CTXL) % 2048
                P.dma('sp', xb[:, :, 0:n], xng[k + 1].rearrange("(kc p) t -> p kc t", p=128)[:, :, c0:c0 + n])

        ybl = Dm['yb%d' % l]

        def y_store(mi, t0, n, src, ybl=ybl):
            if t0 < CTXL:
                P.dma('pool', ybl[0][mi * 64:(mi + 1) * 64, t0:t0 + n], src)
            else:
                k = (t0 - CTXL) // 2048
                c0 = (t0 - CTXL) % 2048
                P.dma('pool', ybl[k + 1][mi * 64:(mi + 1) * 64, c0:c0 + n], src)

        mio = {'y_store': y_store, 'cosT': cosT, 'sinT': sinT}
        mio['w_gla'] = din("w_gla%d" % l, [D, 320])
        mio['gla_par'] = din("gla_par%d" % l, [64, 2])
        mio['gla_w2bd'] = din("gla_w2bd%d" % l, [32, 64])
        mio['w_lru'] = din("w_lru%d" % l, [D, 192])
        mio['lru_par'] = din("lru_par%d" % l, [128, 9])
        mio['lru_gw'] = din("lru_gw%d" % l, [64, 256])
        mio['w_ssd'] = din("w_ssd%d" % l, [D, 322])
        mio['ssd_par'] = din("ssd_par%d" % l, [128, 5])
        mio['ssd_cpar'] = din("ssd_cpar%d" % l, [128, 2, 5])
        mio['ssd_scr'] = dscr("ssd_scr%d" % l, [6, NTOK])
        mio['w_attn'] = din("w_attn%d" % l, [D, 640])
        mio['diff_lam'] = din("diff_lam%d" % l, [128])
        mio['diff_subln_w'] = din("diff_subln_w%d" % l, [64, 1])
        gla_phase(nc, P, A, banks, K, xn_load, mio, l, with_ctx)
        lru_phase(nc, P, A, banks, K, xn_load, mio, l, with_ctx)
        ssd_phase(nc, P, A, banks, K, xn_load, mio, l, with_ctx)
        def after_q(q0, nq, l=l):
            if q0 < CTXL:
                P.collective("AllGather", [Dm['yb%d' % l][0]], [Dm['gy%d' % l][0]], GROUPS)
            elif (q0 - CTXL + nq) % 2048 == 0:
                k = (q0 - CTXL) // 2048
                P.collective("AllGather", [Dm['yb%d' % l][k + 1]], [Dm['gy%d' % l][k + 1]], GROUPS)
        mio['after_q'] = after_q
        attention_phase(nc, P, A, banks, K, xn_load, mio, l, with_ctx)
        fs_token_phase(nc, P, K, banks, A, io, l, 'last' if last else 'mid')


def fused_inputs(inp, core, cosT, sinT, cst):
    b, q = core // 4, core % 4
    h = q
    x, ctx, c, c_ctx = inp['x'], inp['ctx'], inp['c'], inp['c_ctx']
    fsl = slice(256 * q, 256 * q + 256)
    m = dict(cst)
    m['RT'] = np.ascontiguousarray(np.concatenate([ctx[b][:, fsl].T, x[b][:, fsl].T], axis=1))
    m['cvec'] = np.ascontiguousarray(np.stack([pk(c[b]), pk(c_ctx)], axis=2))
    m['cosT'] = cosT
    m['sinT'] = sinT
    m['fnw'] = pk(inp['final_norm_w'][fsl])
    perm = np.array([mi * 256 + hh * 64 + j for hh in range(4) for mi in range(4) for j in range(64)])
    for l in range(2):
        cols = np.r_[256 * q:256 * q + 256, 1024 + 256 * q:1024 + 256 * q + 256, 2048 + 256 * q:2048 + 256 * q + 256]
        m['wmod%d' % l] = np.ascontiguousarray(inp['w_mod'][l][:, cols])
        m['bmod%d' % l] = pk(inp['b_mod'][l][cols])
        m['normw%d' % l] = pk(inp['norm_w'][l][fsl])
        m['wout%d' % l] = np.ascontiguousarray(inp['w_out'][l][perm][:, fsl])
        sw = np.zeros((128, 4), np.float32)
        for i in range(4):
            sw[64:128, i] = inp['ssd_norm_w'][l][i * 64:(i + 1) * 64]
        m['ssdnw%d' % l] = sw
        mi_ = mix_inputs(inp, l, h, ('gla', 'lru', 'attn', 'ssd'), cosT, sinT, cst)
        for k_, v_ in mi_.items():
            if k_ in cst or k_ in ('cosT', 'sinT'):
                continue
            m[k_ + str(l)] = v_
    return m


def kernel(**inp):
    inp = {k: np.asarray(v) for k, v in inp.items()}
    cosT, sinT = rope_tables()
    cst = mix_consts()
    maps = [fused_inputs(inp, core, cosT, sinT, cst) for core in range(8)]
    res = _launch(build_fused, maps)
    out = np.zeros((2, SEQ, D), np.float32)
    for core in range(8):
        b, q = core // 4, core % 4
        out[b, :, 256 * q:256 * q + 256] = np.asarray(res[core]["outT"]).T
    return out
```

```python
import numpy as np
from contextlib import ExitStack
import concourse.bass as bass
import concourse.mybir as mybir
from concourse.bass_utils import run_bass_kernel_spmd

F32 = mybir.dt.float32
BF16 = mybir.dt.bfloat16
AF = mybir.ActivationFunctionType
ALU = mybir.AluOpType
AX = mybir.AxisListType
ENGS = ('pe', 'act', 'dve', 'pool', 'sp')
ESZ = {F32: 4, BF16: 2}


class Prog:
    def __init__(self, nc, stack, n_dma_sems=40):
        self.nc = nc
        self.stack = stack
        self.sem = {e: stack.enter_context(nc.semaphore('sem_' + e)) for e in ENGS}
        self.cnt = {e: 0 for e in ENGS}
        self.known = {e: {} for e in ENGS}
        self.stream = {e: [] for e in ENGS}
        self.dsem = [stack.enter_context(nc.semaphore('dsem%d' % i)) for i in range(n_dma_sems)]
        self.dcum = [0] * n_dma_sems
        self.drr = 0
        self.rows = {}
        self.trk = {}
        self.final_events = []
        self.nwaits = 0
        self.ndma = 0
        self.arenas = {}

    def sbuf(self, name, shape, dtype=F32):
        t = self.stack.enter_context(self.nc.sbuf_tensor(name, list(shape), dtype))
        self.rows[name] = int(np.prod(shape[1:])) * ESZ[dtype]
        return t

    def psum(self, name, shape, dtype=F32):
        t = self.stack.enter_context(self.nc.psum_tensor(name, list(shape), dtype))
        self.rows[name] = int(np.prod(shape[1:])) * ESZ[dtype]
        return t

    def arena(self, name, nbytes):
        t = self.sbuf(name, [128, nbytes // 4], F32)
        a = Arena(self, name, t, nbytes)
        self.arenas[name] = a
        return a

    def _box(self, ap):
        b = self._box0(ap)
        a = self.arenas.get(b[0])
        if a is not None:
            rid = a.region_of(b[3], b[4])
            return ((b[0], rid),) + b[1:]
        return b

    def _box0(self, ap):
        name = ap.tensor.name
        es = ESZ.get(ap.dtype, 4)
        off = int(ap.offset) * es
        pairs = ap.ap
        if name in self.rows:
            rs = self.rows[name]
            p0 = off // rs
            pst, pc = pairs[0]
            p1 = p0 + (pc if pst != 0 else 1)
            fo = off % rs
            rest = pairs[1:]
        else:
            p0, p1 = 0, 1
            fo = off
            rest = pairs
        lo = fo
        hi = fo
        for st, c in rest:
            d = st * (c - 1) * es
            if d < 0:
                lo += d
            else:
                hi += d
        return (name, p0, p1, lo, hi + es)

    @staticmethod
    def _ov(a, b):
        return a[1] < b[2] and b[1] < a[2] and a[3] < b[4] and b[3] < a[4]

    @staticmethod
    def _inside(a, b):
        return a[1] >= b[1] and a[2] <= b[2] and a[3] >= b[3] and a[4] <= b[4]

    def _deps(self, reads, writes):
        deps = []
        for ap in reads:
            b = self._box(ap)
            t = self.trk.get(b[0])
            if t:
                for (wb, ev, clk) in t['w']:
                    if self._ov(b, wb):
                        deps.append((ev, clk, 'raw'))
        for ap in writes:
            b = self._box(ap)
            t = self.trk.get(b[0])
            if t:
                for (wb, ev, clk) in t['w']:
                    if self._ov(b, wb):
                        deps.append((ev, clk, 'waw'))
                for (rb, ev, clk) in t['r']:
                    if self._ov(b, rb):
                        deps.append((ev, clk, 'war'))
        return deps

    @staticmethod
    def _merge(lst):
        merged = {}
        for (rb, rev, rclk) in lst:
            m = merged.get(rev[0])
            if m is None:
                merged[rev[0]] = (rb, rev, rclk)
            else:
                mb, mev, mclk = m
                nb = (rb[0], min(rb[1], mb[1]), max(rb[2], mb[2]), min(rb[3], mb[3]), max(rb[4], mb[4]))
                merged[rev[0]] = (nb, rev, rclk) if rev[1] > mev[1] else (nb, mev, mclk)
        return list(merged.values())

    def _record(self, reads, writes, ev, clk):
        for ap in reads:
            b = self._box(ap)
            t = self.trk.setdefault(b[0], {'w': [], 'r': []})
            rl = t['r']
            for i, (rb, rev, rclk) in enumerate(rl):
                if rev[0] == ev[0] and self._inside(rb, b):
                    rl[i] = (b, ev, clk)
                    break
            else:
                rl.append((b, ev, clk))
                if len(rl) > 40:
                    t['r'] = self._merge(rl)
        for ap in writes:
            b = self._box(ap)
            t = self.trk.setdefault(b[0], {'w': [], 'r': []})
            t['w'] = [e for e in t['w'] if not self._inside(e[0], b)]
            t['r'] = [e for e in t['r'] if not self._inside(e[0], b)]
            t['w'].append((b, ev, clk))
            if len(t['w']) > 40:
                t['w'] = self._merge(t['w'])

    def _waits_for(self, eng, deps):
        kn = self.known[eng]
        waits = {}
        for (ev, clk, kind) in deps:
            key, val = ev
            if key == eng:
                if eng == 'pe':
                    continue
                if eng in ('act', 'dve') and kind != 'raw':
                    continue
            if kn.get(key, 0) >= val:
                continue
            if waits.get(key, 0) < val:
                waits[key] = val
        for (ev, clk, kind) in deps:
            key, val = ev
            if key in waits and waits[key] >= val:
                for k2, v2 in clk.items():
                    if kn.get(k2, 0) < v2:
                        kn[k2] = v2
        for k, v in waits.items():
            if kn.get(k, 0) < v:
                kn[k] = v
        return list(waits.items())

    def op(self, eng, fn, reads=(), writes=()):
        deps = self._deps(reads, writes)
        waits = self._waits_for(eng, deps)
        self.cnt[eng] += 1
        ev = (eng, self.cnt[eng])
        clk = dict(self.known[eng])
        clk[eng] = self.cnt[eng]
        self.stream[eng].append((waits, fn, ('e', eng)))
        self._record(reads, writes, ev, clk)
        self.nwaits += len(waits)
        return ev

    def dma(self, q, out, in_, final=False, **kw):
        i = self.drr
        self.drr = (self.drr + 1) % len(self.dsem)
        deps = self._deps([in_], [out])
        key = ('d', i)
        deps.append(((key, self.dcum[i]), {}, 'raw'))
        waits = self._waits_for(q, deps)
        self.dcum[i] += 16
        ev = (key, self.dcum[i])
        clk = dict(self.known[q])
        clk[key] = self.dcum[i]
        self.stream[q].append((waits, (lambda e, out=out, in_=in_, kw=kw: e.dma_start(out=out, in_=in_, **kw)), ('d', i)))
        self._record([in_], [out], ev, clk)
        if final:
            self.final_events.append(ev)
        self.nwaits += len(waits)
        self.ndma += 1
        return ev

    def collective(self, kind, ins, outs, groups, q='pool'):
        if not hasattr(self, 'csem'):
            self.csem = []
            self.ccum = []
        self.csem.append(self.stack.enter_context(self.nc.semaphore('csem%d' % len(self.csem))))
        self.ccum.append(0)
        i = len(self.csem) - 1
        deps = self._deps(list(ins), list(outs))
        key = ('c', i)
        waits = self._waits_for(q, deps)
        self.ccum[i] += 1
        ev = (key, self.ccum[i])
        clk = dict(self.known[q])
        clk[key] = self.ccum[i]
        self.stream[q].append((waits, (lambda e: e.collective_compute(kind, ALU.bypass, replica_groups=groups,
                                                                      ins=[a.opt() for a in ins], outs=[a.opt() for a in outs])),
                               ('c', i)))
        self._record(list(ins), list(outs), ev, clk)
        self.nwaits += len(waits)
        return ev

    def finish(self, eng='sp'):
        self.stream[eng].append((list(self.final_events), None, None))

    def _semobj(self, key):
        if isinstance(key, tuple):
            return self.dsem[key[1]] if key[0] == 'd' else self.csem[key[1]]
        return self.sem[key]

    def emit(self):
        for e in ENGS:
            assert self.cnt[e] < 60000, (e, self.cnt[e])

        def replay(name, eng):
            for (waits, fn, inc) in self.stream[name]:
                for (key, val) in waits:
                    eng.wait_ge(self._semobj(key), val)
                if fn is None:
                    continue
                ins = fn(eng)
                if inc[0] == 'e':
                    ins.then_inc(self.sem[inc[1]], 1)
                elif inc[0] == 'c':
                    ins.then_inc(self.csem[inc[1]])
                else:
                    ins.then_inc(self.dsem[inc[1]], 16)

        with self.nc.Block() as block:
            @block.sync
            def _(e):
                replay('sp', e)

            @block.scalar
            def _(e):
                replay('act', e)

            @block.vector
            def _(e):
                replay('dve', e)

            @block.gpsimd
            def _(e):
                replay('pool', e)

            @block.tensor
            def _(e):
                replay('pe', e)

    def mm(self, out, lhsT, rhs, start=True, stop=True):
        return self.op('pe', lambda e: e.matmul(out, lhsT, rhs, start=start, stop=stop),
                       reads=[lhsT, rhs], writes=[out])

    def transpose(self, out, in_, ident):
        return self.op('pe', lambda e: e.transpose(out, in_, ident), reads=[in_, ident], writes=[out])

    def act(self, out, in_, func, bias=None, scale=None, accum_out=None):
        reads = [in_]
        kw = {}
        if bias is not None:
            kw['bias'] = bias
            if not isinstance(bias, (int, float)):
                reads.append(bias)
        if scale is not None:
            kw['scale'] = scale
            if not isinstance(scale, (int, float)):
                reads.append(scale)
        writes = [out]
        if accum_out is not None:
            kw['accum_out'] = accum_out
            writes.append(accum_out)
        return self.op('act', lambda e: e.activation(out, in_, func, **kw), reads=reads, writes=writes)

    def tt(self, eng, out, in0, in1, op):
        return self.op(eng, lambda e: e.tensor_tensor(out, in0, in1, op), reads=[in0, in1], writes=[out])

    def ts(self, eng, out, in0, s1, s2, op0, op1=None):
        reads = [in0]
        for s in (s1, s2):
            if s is not None and not isinstance(s, (int, float)):
                reads.append(s)
        if op1 is None:
            return self.op(eng, lambda e: e.tensor_scalar(out, in0, s1, None, op0), reads=reads, writes=[out])
        return self.op(eng, lambda e: e.tensor_scalar(out, in0, s1, s2, op0, op1), reads=reads, writes=[out])

    def stt(self, out, in0, scalar, in1, op0, op1):
        reads = [in0, in1]
        if not isinstance(scalar, (int, float)):
            reads.append(scalar)
        return self.op('dve', lambda e: e.scalar_tensor_tensor(out, in0, scalar, in1, op0, op1),
                       reads=reads, writes=[out])

    def copy(self, eng, out, in_):
        if eng == 'act':
            return self.op('act', lambda e: e.copy(out, in_), reads=[in_], writes=[out])
        return self.op(eng, lambda e: e.tensor_copy(out, in_), reads=[in_], writes=[out])

    def memset(self, eng, ap, val):
        return self.op(eng, lambda e: e.memset(ap, val), reads=[], writes=[ap])

    def scan(self, out, d0, d1, init, op0=ALU.mult, op1=ALU.add):
        reads = [d0, d1]
        if not isinstance(init, (int, float)):
            reads.append(init)
        return self.op('dve', lambda e: e.tensor_tensor_scan(out, d0, d1, init, op0, op1), reads=reads, writes=[out])

    def recip(self, out, in_):
        return self.op('dve', lambda e: e.reciprocal(out, in_), reads=[in_], writes=[out])


class Arena:
    def __init__(self, P, name, t, nbytes):
        self.P, self.name, self.t, self.nbytes = P, name, t, nbytes
        self.top = 0
        self.regions = []
        self.old = []
        self.nrid = 0

    def reset(self):
        self.old.extend(self.regions)
        self.regions = []
        self.top = 0

    def mark(self):
        return (self.top, len(self.regions))

    def release(self, mk):
        top, nreg = mk
        self.old.extend(self.regions[nreg:])
        self.regions = self.regions[:nreg]
        self.top = top

    def region_of(self, lo, hi):
        for (a, b, rid) in self.regions:
            if lo >= a and hi <= b:
                return rid
        raise AssertionError("arena access outside any region %s %d %d" % (self.name, lo, hi))

    def alloc(self, shape, dtype=F32):
        n = int(np.prod(shape)) * ESZ[dtype]
        n = (n + 63) // 64 * 64
        lo, hi = self.top, self.top + n
        assert hi <= self.nbytes, ("arena overflow", self.name, hi, self.nbytes)
        self.top = hi
        rid = self.nrid
        self.nrid += 1
        self.regions.append((lo, hi, rid))
        inh = []
        keep = []
        for (a, b, orid) in self.old:
            if a < hi and lo < b:
                t = self.P.trk.get((self.name, orid))
                if t:
                    inh.extend(t['w'])
                    inh.extend(t['r'])
                keep.append((a, b, orid))
            else:
                keep.append((a, b, orid))
        self.old = keep
        if inh:
            full = ((self.name, rid), 0, 128, lo, hi)
            m = Prog._merge([(full, ev, clk) for (_, ev, clk) in inh])
            self.P.trk[(self.name, rid)] = {'w': m, 'r': []}
        v = self.t[:, lo // 4:hi // 4]
        if dtype != F32:
            v = v.bitcast(dtype)
        nel = int(np.prod(shape))
        v = v[:, 0:nel]
        if len(shape) == 2:
            v = v.rearrange("p (a b) -> p a b", a=shape[0])
        elif len(shape) == 3:
            v = v.rearrange("p (a b c) -> p a b c", a=shape[0], b=shape[1])
        return v


def _launch(build, in_maps, n_cores=8, trace=False):
    nc = bass.Bass("TRN2", target_bir_lowering=False)
    with ExitStack() as stack:
        P = Prog(nc, stack)
        build(nc, P)
        P.finish()
        P.emit()
    res = run_bass_kernel_spmd(nc, in_maps, core_ids=list(range(n_cores)), trace=trace)
    if trace:
        return res
    return res.results

D = 1024
SEQ = 8192
CTXL = 256
NTOK = SEQ + CTXL
QT = 2112
EPS = 1e-6
TILES_Q = [(0, 64, 1), (64, 512, 0), (576, 512, 0), (1088, 512, 0), (1600, 512, 0)]


class Banks:
    def __init__(self, P):
        self.pp = [P.psum("pp%d" % i, [128, 1024], F32) for i in range(4)]
        self.b = [self.pp[i // 2][:, (i % 2) * 512:(i % 2 + 1) * 512] for i in range(8)]


def load_consts_tok(nc, P):
    ones = P.sbuf("ones128", [128, 128], F32)
    P.memset('pool', ones[:], 1.0)
    return {'ones': ones}


def compute_mod(nc, P, K, banks, cvec_d, wmod_d, bmod_d, fc_list, modT):
    cT = P.sbuf("cT", [128, 8, 2], F32)
    P.dma('sp', cT[:], cvec_d)
    e = P.sbuf("cTe", [128, 8, 2], F32)
    P.act(e[:], cT[:], AF.Exp, scale=-1.0)
    P.ts('dve', e[:], e[:], 1.0, None, ALU.add)
    P.recip(e[:], e[:])
    sc = P.sbuf("cTs", [128, 8, 2], F32)
    P.tt('dve', sc[:], cT[:], e[:], ALU.mult)
    bT = P.sbuf("bmodT", [128, 24], F32)
    P.dma('sp', bT[:], bmod_d)
    wbuf = [P.sbuf("wmodbuf%d" % i, [128, 8, 512], F32) for i in range(2)]
    ps = banks.b[7]
    groups = sorted(set(fc // 4 for fc in fc_list))
    for gi, g in enumerate(groups):
        wb = wbuf[gi % 2]
        for kc in range(8):
            P.dma('sp' if kc % 2 == 0 else 'act', wb[:, kc, :], wmod_d[kc * 128:(kc + 1) * 128, g * 512:(g + 1) * 512])
        for fc in range(g * 4, g * 4 + 4):
            if fc not in fc_list:
                continue
            for kc in range(8):
                P.mm(ps[:, fc * 2:fc * 2 + 2], wb[:, kc, (fc % 4) * 128:(fc % 4 + 1) * 128], sc[:, kc, :],
                     start=(kc == 0), stop=(kc == 7))
    for fc in fc_list:
        P.tt('dve', modT[:, fc, :], ps[:, fc * 2:fc * 2 + 2], bT[:, fc:fc + 1].to_broadcast([128, 2]), ALU.add)


def norm_coeffs(nc, P, modT, normw_d, name):
    nw = P.sbuf(name + "_nw", [128, 8], F32)
    P.dma('sp', nw[:], normw_d)
    A = P.sbuf(name + "_A", [128, 8, 2], F32)
    P.ts('dve', A[:], modT[:, 8:16, :], 1.0, None, ALU.add)
    P.tt('dve', A[:], A[:], nw[:].unsqueeze(2).to_broadcast([128, 8, 2]), ALU.mult)
    return A


def phase_norm(nc, P, K, banks, RT, A, Bsh, xn_d, tiles, tmp):
    xn_v = xn_d.rearrange("(kc p) t -> p kc t", p=128)
    for ti, (c0, n, isctx) in enumerate(tiles):
        j = 1 if isctx else 0
        ps = banks.b[ti % 2]
        for kc in range(8):
            sq = tmp['sq'][kc % 2]
            P.act(sq[:, 0:n], RT[:, kc, c0:c0 + n], AF.Square)
            P.mm(ps[:, 0:n], K['ones'][:], sq[:, 0:n], start=(kc == 0), stop=(kc == 7))
        rstd = tmp['rstd'][ti % 2]
        P.act(rstd[:, 0:n], ps[:, 0:n], AF.Ln, scale=1.0 / D, bias=K['eps'][:])
        P.act(rstd[:, 0:n], rstd[:, 0:n], AF.Exp, scale=-0.5)
        xb = tmp['xb'][ti % 2]
        for kc in range(8):
            t1 = tmp['t1'][kc % 2]
            P.stt(t1[:, 0:n], RT[:, kc, c0:c0 + n], A[:, kc, j:j + 1], rstd[:, 0:n], ALU.mult, ALU.mult)
            P.act(xb[:, kc, 0:n], t1[:, 0:n], AF.Identity, bias=Bsh[:, kc, j:j + 1])
        P.dma('pool', xn_v[:, :, c0:c0 + n], xb[:, :, 0:n], final=True)


def phase_final(nc, P, K, banks, RT, fnw_d, out_d, tiles, tmp):
    fw = P.sbuf("fnw_sb", [128, 8], F32)
    P.dma('sp', fw[:], fnw_d)
    out_v = out_d.rearrange("(kc p) t -> p kc t", p=128)
    for ti, (c0, n, isctx) in enumerate(tiles):
        ps = banks.b[ti % 2]
        for kc in range(8):
            sq = tmp['sq'][kc % 2]
            P.act(sq[:, 0:n], RT[:, kc, c0:c0 + n], AF.Square)
            P.mm(ps[:, 0:n], K['ones'][:], sq[:, 0:n], start=(kc == 0), stop=(kc == 7))
        rstd = tmp['rstd'][ti % 2]
        P.act(rstd[:, 0:n], ps[:, 0:n], AF.Ln, scale=1.0 / D, bias=K['eps'][:])
        P.act(rstd[:, 0:n], rstd[:, 0:n], AF.Exp, scale=-0.5)
        for kc in range(8):
            P.stt(RT[:, kc, c0:c0 + n], RT[:, kc, c0:c0 + n], fw[:, kc:kc + 1], rstd[:, 0:n], ALU.mult, ALU.mult)
        P.dma('pool', out_v[:, :, c0 - 64:c0 - 64 + n], RT[:, :, c0:c0 + n], final=True)


def phase_outproj(nc, P, K, banks, RT, yT_d, wout_d, ssdw_d, gateT, tiles, tmp):
    wo = P.sbuf("wout_bf", [128, 8, D], BF16)
    for kc in range(8):
        st = tmp['wstage'][kc % 2]
        P.dma('sp' if kc % 2 == 0 else 'act', st[:], wout_d[kc * 128:(kc + 1) * 128, :])
        P.copy('pool', wo[:, kc, :], st[:])
    sw = P.sbuf("ssdnw_sb", [128, 2], F32)
    P.dma('sp', sw[:], ssdw_d)
    yT_v = yT_d.rearrange("(kc p) t -> p kc t", p=128)
    for ti, (c0, n, isctx) in enumerate(tiles):
        j = 1 if isctx else 0
        yb = tmp['yb'][ti % 2]
        P.dma('sp', yb[:, :, 0:n], yT_v[:, :, c0:c0 + n])
        ps = banks.b[2 + ti % 2]
        for i, kc in enumerate((6, 7)):
            sq = tmp['sq'][i]
            P.act(sq[:, 0:n], yb[:, kc, 0:n], AF.Square)
            P.mm(ps[:, 0:n], K['ones'][:], sq[:, 0:n], start=(i == 0), stop=(i == 1))
        rstd = tmp['rstd'][ti % 2]
        P.act(rstd[:, 0:n], ps[:, 0:n], AF.Ln, scale=1.0 / 256, bias=K['eps'][:])
        P.act(rstd[:, 0:n], rstd[:, 0:n], AF.Exp, scale=-0.5)
        for i, kc in enumerate((6, 7)):
            P.stt(yb[:, kc, 0:n], yb[:, kc, 0:n], sw[:, i:i + 1], rstd[:, 0:n], ALU.mult, ALU.mult)
        for fc in range(8):
            po = banks.b[4 + fc % 4]
            for kc in range(8):
                P.mm(po[:, 0:n], wo[:, kc, fc * 128:(fc + 1) * 128], yb[:, kc, 0:n], start=(kc == 0), stop=(kc == 7))
            P.stt(RT[:, fc, c0:c0 + n], po[:, 0:n], gateT[:, 16 + fc, j:j + 1], RT[:, fc, c0:c0 + n], ALU.mult, ALU.add)


def alloc_tok_tmp(P):
    return {
        'sq': [P.sbuf("t_sq%d" % i, [128, 512], F32) for i in range(2)],
        'rstd': [P.sbuf("t_rstd%d" % i, [128, 512], F32) for i in range(2)],
        't1': [P.sbuf("t_t1%d" % i, [128, 512], F32) for i in range(2)],
        'xb': [P.sbuf("t_xb%d" % i, [128, 8, 512], BF16) for i in range(2)],
    }


def build_tok_kernel(nc, P, first, last, with_outproj):
    banks = Banks(P)
    K = load_consts_tok(nc, P)
    epst = P.sbuf("eps_t", [128, 1], F32)
    P.memset('pool', epst[:], EPS)
    K['eps'] = epst
    tmp = alloc_tok_tmp(P)
    RT_d = nc.dram_tensor("RT", [D, QT], F32, kind="ExternalInput").ap()
    cvec_d = nc.dram_tensor("cvec", [128, 8, 2], F32, kind="ExternalInput").ap()
    RT = P.sbuf("RT_sb", [128, 8, QT], F32)
    RT_v = RT_d.rearrange("(kc p) t -> p kc t", p=128)
    for kc in range(8):
        P.dma('sp' if kc % 2 == 0 else 'act', RT[:, kc, :], RT_v[:, kc, :])
    modT = P.sbuf("modT", [128, 24, 2], F32)
    if with_outproj:
        wmodg_d = nc.dram_tensor("wmod_g", [D, 3 * D], F32, kind="ExternalInput").ap()
        bmodg_d = nc.dram_tensor("bmod_g", [128, 24], F32, kind="ExternalInput").ap()
        yT_d = nc.dram_tensor("yT", [D, QT], BF16, kind="ExternalInput").ap()
        wout_d = nc.dram_tensor("wout", [D, D], F32, kind="ExternalInput").ap()
        ssdw_d = nc.dram_tensor("ssdnw", [128, 2], F32, kind="ExternalInput").ap()
        tmp['wstage'] = [P.sbuf("t_wst%d" % i, [128, D], F32) for i in range(2)]
        tmp['yb'] = [P.sbuf("t_yb%d" % i, [128, 8, 512], BF16) for i in range(2)]
        gateT = P.sbuf("gateT", [128, 24, 2], F32)
        compute_mod(nc, P, K, banks, cvec_d, wmodg_d, bmodg_d, list(range(16, 24)), gateT)
        tiles = TILES_Q[1:] if last else TILES_Q
        phase_outproj(nc, P, K, banks, RT, yT_d, wout_d, ssdw_d, gateT, tiles, tmp)
    if last:
        fnw_d = nc.dram_tensor("fnw", [128, 8], F32, kind="ExternalInput").ap()
        out_d = nc.dram_tensor("outT", [D, 2048], F32, kind="ExternalOutput").ap()
        phase_final(nc, P, K, banks, RT, fnw_d, out_d, TILES_Q[1:], tmp)
    else:
        wmod_d = nc.dram_tensor("wmod_n", [D, 3 * D], F32, kind="ExternalInput").ap()
        bmod_d = nc.dram_tensor("bmod_n", [128, 24], F32, kind="ExternalInput").ap()
        normw_d = nc.dram_tensor("normw", [128, 8], F32, kind="ExternalInput").ap()
        xn_d = nc.dram_tensor("xnT", [D, QT], BF16, kind="ExternalOutput").ap()
        if with_outproj:
            Rout_d = nc.dram_tensor("RTout", [D, QT], F32, kind="ExternalOutput").ap()
        modT2 = modT
        _cm_second(nc, P, K, banks, cvec_d, wmod_d, bmod_d, list(range(0, 16)), modT2, with_outproj)
        A = norm_coeffs(nc, P, modT2, normw_d, "nc")
        phase_norm(nc, P, K, banks, RT, A, modT2, xn_d, TILES_Q, tmp)
        if with_outproj:
            Ro_v = Rout_d.rearrange("(kc p) t -> p kc t", p=128)
            for kc in range(8):
                P.dma('pool', Ro_v[:, kc, :], RT[:, kc, :], final=True)


_CM_STATE = {}


def _cm_second(nc, P, K, banks, cvec_d, wmod_d, bmod_d, fc_list, modT, second):
    if not second:
        return compute_mod(nc, P, K, banks, cvec_d, wmod_d, bmod_d, fc_list, modT)
    orig = P.sbuf

    def renamed(name, shape, dtype=F32):
        return orig(name + "_2", shape, dtype)
    P.sbuf = renamed
    try:
        compute_mod(nc, P, K, banks, cvec_d, wmod_d, bmod_d, fc_list, modT)
    finally:
        P.sbuf = orig


def pk(v):
    v = np.asarray(v)
    return np.ascontiguousarray(v.reshape(-1, 128).T)

TOK_TILES = [(0, 256)] + [(256 + 512 * i, 512) for i in range(16)]
SCALE_QK = 32.0 ** -0.5
QK_REP = 1
DEFER = True


def store_y(P, io, mi, t0, n, src):
    if 'y_store' in io:
        io['y_store'](mi, t0, n, src)
    else:
        P.dma('pool', io['yT'][mi, :, t0:t0 + n], src, final=True)


def load_weights_bf16(nc, P, A, w_d, ncols, stage):
    Wsb = A.alloc([8, ncols], BF16)
    for kc in range(8):
        st = stage[kc % 2]
        P.dma('sp' if kc % 2 == 0 else 'act', st[:, 0:ncols], w_d[kc * 128:(kc + 1) * 128, :])
        P.copy('dve', Wsb[:, kc, :], st[:, 0:ncols])
    return Wsb


def inproj(nc, P, banks, xn_v, Wsb, tiles, fm_groups, tm_group, xbufs, bank_ids=(0, 1, 2, 3), pre_tile=None):
    pending = []
    for ti, (t0, n) in enumerate(tiles):
        xb = xbufs[ti % 2]
        if callable(xn_v):
            xn_v(xb, t0, n)
        else:
            P.dma('sp', xb[:, :, 0:n], xn_v[:, :, t0:t0 + n])
        if pre_tile is not None:
            pre_tile(ti, t0, n)
        new_pending = []
        for gi, (c0, M, lat_only, fn) in enumerate(fm_groups):
            if lat_only and t0 < CTXL:
                continue
            ps = banks.b[bank_ids[gi % len(bank_ids)]]
            for kc in range(8):
                P.mm(ps[0:M, 0:n], Wsb[:, kc, c0:c0 + M], xb[:, kc, 0:n], start=(kc == 0), stop=(kc == 7))
            if fn is not None:
                r = fn(ps, t0, n, ti)
                if r is not None:
                    if DEFER:
                        new_pending.append(r)
                    else:
                        r()
        if tm_group is not None:
            c0, ncols, fn, tmbanks = tm_group
            for sub in range(n // 128):
                ps = banks.b[tmbanks[sub % len(tmbanks)]]
                for kc in range(8):
                    P.mm(ps[:, 0:ncols], xb[:, kc, sub * 128:(sub + 1) * 128], Wsb[:, kc, c0:c0 + ncols],
                         start=(kc == 0), stop=(kc == 7))
                fn(ps, t0 + sub * 128)
        for r in pending:
            r2 = r()
            if r2 is not None:
                new_pending.append(r2)
        pending = new_pending
    while pending:
        nxt = []
        for r in pending:
            r2 = r()
            if r2 is not None:
                nxt.append(r2)
        pending = nxt


def silu_evac(P, out_bf, ps_ap, tmp_e, M, n):
    P.act(tmp_e[0:M, 0:n], ps_ap, AF.Tanh, scale=0.5)
    P.stt(out_bf, tmp_e[0:M, 0:n], 1.0, ps_ap, ALU.add, ALU.mult)


def attention_phase(nc, P, A, banks, K, xn_v, io, layer, with_ctx):
    A.reset()
    lam_init = 0.8 - 0.6 * float(np.exp(-0.3 * layer))
    stage = [A.alloc([640], F32) for _ in range(2)]
    Wsb = load_weights_bf16(nc, P, A, io['w_attn'], 640, stage)
    xbufs = [A.alloc([8, 512], BF16) for _ in range(2)]
    Qa = A.alloc([NTOK], BF16)
    Ka = A.alloc([NTOK], BF16)
    Vt = A.alloc([66, 128], BF16)
    sg = A.alloc([NTOK], BF16)
    cosb = [A.alloc([512], F32) for _ in range(2)]
    sinb = [A.alloc([512], F32) for _ in range(2)]
    t1 = [A.alloc([512], F32) for _ in range(2)]
    t2 = [A.alloc([512], F32) for _ in range(2)]
    te = [A.alloc([512], F32) for _ in range(2)]
    P.memset('pool', Vt[:, :, 65:128], 0.0)
    P.memset('pool', Vt[:, :, 64:65], 1.0)

    def pre_tile(ti, t0, n):
        if t0 >= CTXL:
            P.dma('act', cosb[ti % 2][:], io['cosT'][:, t0 - CTXL:t0 - CTXL + n])
            P.dma('act', sinb[ti % 2][:], io['sinT'][:, t0 - CTXL:t0 - CTXL + n])

    state = {}

    def ev_plain(dst):
        def f(ps, t0, n, ti):
            if t0 < CTXL:
                P.copy('act', dst[:, t0:t0 + n], ps[:, 0:n])
            else:
                state['ps1'] = ps
        return f

    def ev_rot(dst):
        def f(ps, t0, n, ti):
            a = t1[ti % 2]
            b = t2[ti % 2]
            P.tt('dve', a[:, 0:n], state['ps1'][:, 0:n], cosb[ti % 2][:, 0:n], ALU.mult)
            P.tt('dve', b[:, 0:n], ps[:, 0:n], sinb[ti % 2][:, 0:n], ALU.mult)
            P.tt('pool', dst[:, t0:t0 + n], a[:, 0:n], b[:, 0:n], ALU.add)
        return f

    vT = [A.alloc([512], BF16) for _ in range(2)]

    def ev_gv(ps, t0, n, ti):
        silu_evac(P, sg[0:64, t0:t0 + n], ps[0:64, 0:n], te[ti % 2], 64, n)
        v_ = vT[ti % 2]
        P.copy('act', v_[64:128, 0:n], ps[64:128, 0:n])

        def rest():
            for sub in range(n // 128):
                pt = banks.b[6 + sub % 2].bitcast(BF16)
                P.transpose(pt[:, 0:64], v_[64:128, sub * 128:(sub + 1) * 128], K['identb'][64:128, 64:128])
                P.copy('dve', Vt[:, t0 // 128 + sub, 0:64], pt[:, 0:64])
        return rest

    fm = [(0, 128, False, ev_plain(Qa)), (128, 128, True, ev_rot(Qa)),
          (256, 128, False, ev_plain(Ka)), (384, 128, True, ev_rot(Ka)),
          (512, 128, False, ev_gv)]
    inproj(nc, P, banks, xn_v, Wsb, TOK_TILES, fm, None, xbufs, bank_ids=(0, 1, 2, 3, 4), pre_tile=pre_tile)

    lv = A.alloc([128], F32)
    P.dma('sp', lv[0:64, :], io['diff_lam'].partition_broadcast(64))
    pr = A.alloc([64], F32)
    s2 = A.alloc([2], F32)
    P.tt('dve', pr[0:64, 0:32], lv[0:64, 0:32], lv[0:64, 32:64], ALU.mult)
    P.tt('dve', pr[0:64, 32:64], lv[0:64, 64:96], lv[0:64, 96:128], ALU.mult)
    P.op('dve', lambda e: e.tensor_reduce(s2[0:64, 0:2], pr[0:64, :].rearrange("p (a b) -> p a b", a=2), AX.X, ALU.add),
         reads=[pr[0:64, :]], writes=[s2[0:64, 0:2]])
    P.act(s2[0:64, :], s2[0:64, :], AF.Exp)
    nlam = A.alloc([1], F32)
    P.tt('dve', nlam[0:64, :], s2[0:64, 1:2], s2[0:64, 0:1], ALU.subtract)
    P.ts('dve', nlam[0:64, :], nlam[0:64, :], -lam_init, None, ALU.add)
    subw = A.alloc([1], F32)
    P.dma('sp', subw[0:64, :], io['diff_subln_w'])
    P.ts('dve', subw[0:64, :], subw[0:64, :], 1.0 - lam_init, None, ALU.mult)

    Eb = [A.alloc([1024], BF16) for _ in range(4)]
    osb = [A.alloc([512], F32) for _ in range(2)]
    fz = [A.alloc([512], F32) for _ in range(4)]
    yb = [A.alloc([512], BF16) for _ in range(2)]
    qblocks = []
    if with_ctx:
        qblocks.append((0, 256, [0, 1]))
    for i in range(16):
        qblocks.append((256 + 512 * i, 512, list(range(66))))
    o_acc = [banks.b[6], banks.b[7]]
    for qi, (q0, nq, kbs) in enumerate(qblocks):
        def qk(kb):
            S = banks.pp[kb % 3]
            for rep in range(QK_REP):
                P.mm(S[:, 0:nq], Ka[0:32, kb * 128:(kb + 1) * 128], Qa[0:32, q0:q0 + nq])
                P.mm(S[:, 512:512 + nq], Ka[64:96, kb * 128:(kb + 1) * 128], Qa[64:96, q0:q0 + nq])
        for k_ in kbs[0:3]:
            qk(k_)
        for ki, kb in enumerate(kbs):
            S = banks.pp[kb % 3]
            E = Eb[ki % 4]
            if nq == 512:
                P.act(E[:, :], S[:, :], AF.Exp, scale=SCALE_QK)
            else:
                Ev = E.rearrange("p (a b) -> p a b", a=2)[:, :, 0:nq]
                Sv = S.rearrange("p (a b) -> p a b", a=2)[:, :, 0:nq]
                P.act(Ev, Sv, AF.Exp, scale=SCALE_QK)
            if ki + 3 < len(kbs):
                qk(kbs[ki + 3])
            for c in range(2):
                P.mm(o_acc[c][:, 0:nq], Vt[:, kb, :], E[:, c * 512:c * 512 + nq], start=(ki == 0), stop=(ki == len(kbs) - 1))
        for c in range(2):
            P.copy('act', osb[c][0:65, 0:nq], o_acc[c][0:65, 0:nq])
        zb = [banks.b[0], banks.b[1]]
        for c in range(2):
            P.mm(zb[c][0:64, 0:nq], K['selZ'][0:65, :], osb[c][0:65, 0:nq])
        for c in range(2):
            P.act(fz[c][0:64, 0:nq], zb[c][0:64, 0:nq], AF.Ln)
            P.act(fz[c][0:64, 0:nq], fz[c][0:64, 0:nq], AF.Exp, scale=-1.0)
            P.tt('dve', fz[c][0:64, 0:nq], fz[c][0:64, 0:nq], osb[c][0:64, 0:nq], ALU.mult)
        o = fz[2]
        P.stt(o[0:64, 0:nq], fz[1][0:64, 0:nq], nlam[0:64, 0:1], fz[0][0:64, 0:nq], ALU.mult, ALU.add)
        P.act(fz[3][0:64, 0:nq], o[0:64, 0:nq], AF.Square)
        P.mm(zb[0][0:64, 0:nq], K['ones'][0:64, 0:64], fz[3][0:64, 0:nq])
        P.act(fz[3][0:64, 0:nq], zb[0][0:64, 0:nq], AF.Ln, scale=1.0 / 64, bias=K['eps'][0:64, :])
        P.act(fz[3][0:64, 0:nq], fz[3][0:64, 0:nq], AF.Exp, scale=-0.5)
        P.stt(o[0:64, 0:nq], o[0:64, 0:nq], subw[0:64, 0:1], fz[3][0:64, 0:nq], ALU.mult, ALU.mult)
        y = yb[qi % 2]
        P.stt(y[0:64, 0:nq], o[0:64, 0:nq], 0.5, sg[0:64, q0:q0 + nq], ALU.mult, ALU.mult)
        store_y(P, io, 2, q0, nq, y[0:64, 0:nq])
        if 'after_q' in io:
            io['after_q'](q0, nq)


TP = 8456
PADC = 2
PADL = 260


def pcol(t0):
    return t0 + PADC if t0 < CTXL else t0 + (PADL - CTXL)


PTILES = [(PADC, 256)] + [(PADL + 512 * i, 512) for i in range(16)]


def conv4(P, dst, src, cw, cb, np_, c_lo, c_hi):
    n = c_hi - c_lo
    P.ts('dve', dst[0:np_, c_lo:c_hi], src[0:np_, c_lo - 2:c_hi - 2], cw[0:np_, 0:1], cb[0:np_, 0:1], ALU.mult, ALU.add)
    for k in range(1, 4):
        P.stt(dst[0:np_, c_lo:c_hi], src[0:np_, c_lo + k - 2:c_hi + k - 2], cw[0:np_, k:k + 1], dst[0:np_, c_lo:c_hi],
              ALU.mult, ALU.add)


def lru_phase(nc, P, A, banks, K, xn_v, io, layer, with_ctx):
    A.reset()
    stage = [A.alloc([192], F32) for _ in range(2)]
    Wsb = load_weights_bf16(nc, P, A, io['w_lru'], 192, stage)
    xbufs = [A.alloc([8, 512], BF16) for _ in range(2)]
    XA = A.alloc([TP], F32)
    XU = A.alloc([TP], F32)
    sg = A.alloc([NTOK], BF16)
    te = [A.alloc([512], F32) for _ in range(2)]
    par = A.alloc([16], F32)
    P.dma('sp', par[:, 0:9], io['lru_par'])
    cw, cb = par[:, 0:4], par[:, 4:5]
    nba, nbx = par[:, 9:10], par[:, 10:11]
    c8, c16 = par[:, 11:12], par[:, 12:13]
    P.ts('dve', nba, par[:, 5:6], 0.5, None, ALU.mult)
    P.ts('dve', nbx, par[:, 6:7], 0.5, None, ALU.mult)
    P.act(c8, par[:, 7:8], AF.Exp, scale=-1.0)
    P.act(c8, c8, AF.Ln, bias=K['one'][:, 0:1])
    P.ts('dve', c16, c8, -16.0, None, ALU.mult)
    P.ts('dve', c8, c8, -8.0, None, ALU.mult)
    gw32 = A.alloc([256], F32)
    P.dma('sp', gw32[0:64, :], io['lru_gw'])
    gw = A.alloc([256], BF16)
    P.copy('pool', gw[0:64, :], gw32[0:64, :])
    II = A.alloc([64], F32)
    P.copy('pool', II[0:64, :], K['ident'][0:64, 0:64])
    P.copy('pool', II[64:128, :], K['ident'][64:128, 64:128])
    P.memset('pool', XA[:, 0:PADC], 0.0)
    P.memset('pool', XA[:, PADC + CTXL:PADL], 0.0)
    P.memset('pool', XA[:, PADL + SEQ:TP], 0.0)

    def ev_x(ps, t0, n, ti):
        c = pcol(t0)
        P.copy('act', XA[:, c:c + n], ps[:, 0:n])

    def ev_gate(ps, t0, n, ti):
        silu_evac(P, sg[0:64, t0:t0 + n], ps[0:64, 0:n], te[ti % 2], 64, n)

    inproj(nc, P, banks, xn_v, Wsb, TOK_TILES, [(0, 128, False, ev_x), (128, 64, False, ev_gate)], None, xbufs,
           bank_ids=(0, 1, 2, 3))
    conv4(P, XU, XA, cw, cb, 128, PADC, PADC + CTXL)
    for i in range(4):
        conv4(P, XU, XA, cw, cb, 128, PADL + 2048 * i, PADL + 2048 * (i + 1))
    xcb = [A.alloc([512], BF16) for _ in range(2)]
    gr = [A.alloc([512], F32) for _ in range(2)]
    gi_ = [A.alloc([512], F32) for _ in range(2)]
    gs = [A.alloc([512], F32) for _ in range(2)]
    for ti, (c0, n) in enumerate(PTILES):
        xb = xcb[ti % 2]
        P.copy('pool', xb[0:64, 0:n], XU[0:64, c0:c0 + n])
        psr, psi = banks.b[(2 * ti) % 8], banks.b[(2 * ti + 1) % 8]
        P.mm(psr[:, 0:n], gw[0:64, 0:128], xb[0:64, 0:n])
        P.mm(psi[:, 0:n], gw[0:64, 128:256], xb[0:64, 0:n])
        r, ii, s = gr[ti % 2], gi_[ti % 2], gs[ti % 2]
        P.act(r[:, 0:n], psr[:, 0:n], AF.Tanh, scale=0.5, bias=nba)
        P.act(ii[:, 0:n], psi[:, 0:n], AF.Tanh, scale=0.5, bias=nbx)
        P.ts('dve', r[:, 0:n], r[:, 0:n], 0.5, 0.5, ALU.mult, ALU.add)
        P.ts('dve', ii[:, 0:n], ii[:, 0:n], 0.5, 0.5, ALU.mult, ALU.add)
        P.act(XA[:, c0:c0 + n], r[:, 0:n], AF.Exp, scale=c8)
        P.act(s[:, 0:n], r[:, 0:n], AF.Exp, scale=c16)
        P.act(s[:, 0:n], s[:, 0:n], AF.Ln, scale=-1.0, bias=K['one'][:, 0:1])
        P.act(s[:, 0:n], s[:, 0:n], AF.Exp, scale=0.5)
        P.tt('pool', ii[:, 0:n], ii[:, 0:n], s[:, 0:n], ALU.mult)
        P.tt('dve', XU[:, c0:c0 + n], XU[:, c0:c0 + n], ii[:, 0:n], ALU.mult)
    fa, fu = XA[0:64], XU[0:64]
    ba_, bu = XA[64:128], XU[64:128]
    P.scan(fu[:, PADC:PADC + CTXL], fa[:, PADC:PADC + CTXL], fu[:, PADC:PADC + CTXL], 0.0)
    for i in range(4):
        lo = PADL + 2048 * i
        init = fu[:, PADC + CTXL - 1:PADC + CTXL] if i == 0 else fu[:, lo - 1:lo]
        P.scan(fu[:, lo:lo + 2048], fa[:, lo:lo + 2048], fu[:, lo:lo + 2048], init)
    P.scan(bu[:, PADC:PADC + CTXL][:, ::-1], ba_[:, PADC:PADC + CTXL][:, ::-1], bu[:, PADC:PADC + CTXL][:, ::-1], 0.0)
    for i in range(3, -1, -1):
        lo = PADL + 2048 * i
        init = bu[:, PADC:PADC + 1] if i == 3 else bu[:, lo + 2048:lo + 2049]
        P.scan(bu[:, lo:lo + 2048][:, ::-1], ba_[:, lo:lo + 2048][:, ::-1], bu[:, lo:lo + 2048][:, ::-1], init)
    yb = [A.alloc([512], BF16) for _ in range(2)]
    for ti, (t0, n) in enumerate(TOK_TILES):
        if t0 < CTXL and not with_ctx:
            continue
        c0 = pcol(t0)
        ps = banks.b[ti % 4]
        P.mm(ps[0:64, 0:n], II[:, :], XU[:, c0:c0 + n])
        y = yb[ti % 2]
        P.stt(y[0:64, 0:n], ps[0:64, 0:n], 0.5, sg[0:64, t0:t0 + n], ALU.mult, ALU.mult)
        store_y(P, io, 1, t0, n, y[0:64, 0:n])


def gla_phase(nc, P, A, banks, K, xn_v, io, layer, with_ctx):
    A.reset()
    NC_ = 132
    ST = A.alloc([NC_ + 2, 64], F32)
    STb = A.alloc([NC_ + 2, 64], BF16)
    if A.top < 52 * 1024:
        A.alloc([(52 * 1024 - A.top) // 4], F32)
    QG = A.alloc([NTOK], BF16)
    KG = A.alloc([NTOK], BF16)
    KDt = A.alloc([66, 64], BF16)
    Vt = A.alloc([66, 64], BF16)
    sg = A.alloc([NTOK], BF16)
    DEC = A.alloc([NC_], F32)
    par = A.alloc([8], F32)
    P.dma('sp', par[0:64, 0:2], io['gla_par'])
    nb2 = par[0:64, 2:3]
    P.ts('dve', nb2, par[0:64, 0:1], -1.0, None, ALU.mult)
    lnsc = par[0:64, 3:4]
    P.memset('dve', lnsc, float(np.log(32.0 ** -0.5)))
    w2_32 = A.alloc([64], F32)
    P.dma('sp', w2_32[64:96, :], io['gla_w2bd'])
    w2b = A.alloc([64], BF16)
    P.copy('dve', w2b[64:96, :], w2_32[64:96, :])
    mk = A.mark()
    stage = [A.alloc([320], F32) for _ in range(2)]
    Wsb = load_weights_bf16(nc, P, A, io['w_gla'], 320, stage)
    xbufs = [A.alloc([8, 512], BF16) for _ in range(2)]
    te = [A.alloc([512], F32) for _ in range(2)]
    lrb = [A.alloc([512], BF16) for _ in range(2)]
    Lb = [A.alloc([512], F32) for _ in range(2)]
    Gb = [A.alloc([512], F32) for _ in range(2)]
    Eq = [A.alloc([512], F32) for _ in range(2)]
    Ek = [A.alloc([512], F32) for _ in range(2)]
    dG = [A.alloc([512], F32) for _ in range(2)]
    kdT = [A.alloc([512], BF16) for _ in range(2)]
    P.memset('dve', ST[0:64, 0:2, :], 0.0)
    P.memset('dve', ST[0:64, NC_:NC_ + 2, :], 0.0)
    st = {}

    qs = [A.alloc([512], F32) for _ in range(2)]
    ks = [A.alloc([512], F32) for _ in range(2)]

    def ev_q(ps, t0, n, ti):
        P.copy('act', qs[ti % 2][0:64, 0:n], ps[0:64, 0:n])

    vT = [A.alloc([512], BF16) for _ in range(2)]

    def ev_gv(ps, t0, n, ti):
        P.copy('dve', sg[0:64, t0:t0 + n], ps[0:64, 0:n])
        v_ = vT[ti % 2]
        P.copy('act', v_[64:128, 0:n], ps[64:128, 0:n])
        for sub in range(n // 128):
            pt = banks.b[5 + sub % 2].bitcast(BF16)
            P.transpose(pt[:, 64:128], v_[64:128, sub * 128:(sub + 1) * 128], K['identb'][64:128, 64:128])
            P.copy('dve', Vt[:, t0 // 128 + sub, :], pt[:, 64:128])

    def ev_lr(ps, t0, n, ti):
        i2 = ti % 2
        nch = n // 64
        c0 = t0 // 64
        P.copy('act', ks[ti % 2][0:64, 0:n], ps[0:64, 0:n])
        P.copy('act', lrb[i2][64:96, 0:n], ps[64:96, 0:n])

        def rest():
            pz = banks.b[7]
            P.mm(pz[0:64, 0:n], w2b[64:96, 0:64], lrb[i2][64:96, 0:n])
            L, G = Lb[i2], Gb[i2]
            P.act(L[0:64, 0:n], pz[0:64, 0:n], AF.Exp, scale=-1.0, bias=nb2)
            P.act(L[0:64, 0:n], L[0:64, 0:n], AF.Ln, bias=K['one'][0:64, 0:1])
            P.scan(G[0:32, 0:n], K['scanmask'][0:32, 0:n], L[0:32, 0:n], 0.0)
            P.scan(G[32:64, 0:n][:, ::-1], K['scanmask'][32:64, 0:n][:, ::-1], L[32:64, 0:n][:, ::-1], 0.0)
            P.act(Eq[i2][0:64, 0:n], G[0:64, 0:n], AF.Exp, scale=-1.0 / 16, bias=lnsc)
            P.act(Ek[i2][0:64, 0:n], G[0:64, 0:n], AF.Exp, scale=1.0 / 16)
            P.tt('dve', QG[0:64, t0:t0 + n], qs[i2][0:64, 0:n], Eq[i2][0:64, 0:n], ALU.mult)
            P.tt('dve', KG[0:64, t0:t0 + n], ks[i2][0:64, 0:n], Ek[i2][0:64, 0:n], ALU.mult)
            G3 = G[:, 0:n].rearrange("p (c s) -> p c s", s=64)
            d3 = dG[i2][:, 0:n].rearrange("p (c s) -> p c s", s=64)
            P.tt('dve', d3[0:32], G3[0:32, :, 63:64].to_broadcast([32, nch, 64]), G3[0:32], ALU.subtract)
            P.tt('dve', d3[32:64], G3[32:64, :, 0:1].to_broadcast([32, nch, 64]), G3[32:64], ALU.subtract)
            P.act(dG[i2][0:64, 0:n], dG[i2][0:64, 0:n], AF.Exp, scale=-1.0 / 16)
            P.tt('dve', kdT[i2][0:64, 0:n], ks[i2][0:64, 0:n], dG[i2][0:64, 0:n], ALU.mult)
            P.act(DEC[0:32, c0:c0 + nch], G3[0:32, :, 63], AF.Exp, scale=-1.0 / 16)
            P.act(DEC[32:64, c0:c0 + nch], G3[32:64, :, 0], AF.Exp, scale=-1.0 / 16)
            def rest2():
                for sub in range(n // 128):
                    pt = banks.b[5 + sub % 2].bitcast(BF16)
                    P.transpose(pt[:, 0:64], kdT[i2][0:64, sub * 128:(sub + 1) * 128], K['identb'][0:64, 0:64])
                    P.copy('dve', KDt[:, t0 // 128 + sub, :], pt[:, 0:64])
            return rest2
        return rest

    fm = [(0, 64, False, ev_q), (64, 96, False, ev_lr), (192, 128, False, ev_gv)]
    inproj(nc, P, banks, xn_v, Wsb, TOK_TILES, fm, None, xbufs, bank_ids=(0, 1, 2))
    for ti, (t0, n) in enumerate(TOK_TILES):
        P.act(te[ti % 2][0:64, 0:n], sg[0:64, t0:t0 + n], AF.Tanh, scale=0.5)
        P.stt(sg[0:64, t0:t0 + n], te[ti % 2][0:64, 0:n], 1.0, sg[0:64, t0:t0 + n], ALU.add, ALU.mult)

    for g0 in range(0, 66, 8):
        npair = min(8, 66 - g0)
        for i in range(npair):
            p = g0 + i
            for j in range(2):
                P.mm(banks.b[j][0:64, i * 64:(i + 1) * 64], KDt[64 * j:64 * j + 64, p, :], Vt[64 * j:64 * j + 64, p, :])
        for j in range(2):
            src = banks.b[j][:, 0:npair * 64].rearrange("p (c v) -> p c v", v=64)
            c0 = 2 * g0 + j
            P.copy('act', ST[0:32, c0 + 2:c0 + 1 + 2 * npair:2, :], src[0:32])
            P.copy('dve', ST[32:64, c0:c0 + 2 * npair - 1:2, :], src[32:64])
    fsteps = [(c + 2, c + 1, c) for c in range(1, NC_)]
    bsteps = [(c, c + 1, c) for c in (2, 1, 0)] + [(131, 0, 131)] + [(c, c + 1, c) for c in range(130, 4, -1)]
    for i in range(max(len(fsteps), len(bsteps))):
        if i < len(fsteps):
            o_, i_, d_ = fsteps[i]
            P.stt(ST[0:32, o_, :], ST[0:32, i_, :], DEC[0:32, d_:d_ + 1], ST[0:32, o_, :], ALU.mult, ALU.add)
        if i < len(bsteps):
            o_, i_, d_ = bsteps[i]
            P.stt(ST[32:64, o_, :], ST[32:64, i_, :], DEC[32:64, d_:d_ + 1], ST[32:64, o_, :], ALU.mult, ALU.add)
    P.copy('dve', ST[32:64, 132, :], ST[32:64, 0, :])
    P.copy('dve', STb[0:64], ST[0:64])

    A.release(mk)
    tm = [A.alloc([512], F32) for _ in range(2)]
    At = [A.alloc([512], BF16) for _ in range(2)]
    ob = [A.alloc([512], F32) for _ in range(2)]
    yb = [A.alloc([512], BF16) for _ in range(2)]
    tiles2 = [(ti, t0, n) for ti, (t0, n) in enumerate(TOK_TILES) if not (t0 < CTXL and not with_ctx)]

    def stageA(ti, t0, n):
        i2 = ti % 2
        npair = n // 128
        X, Y = banks.b[2], banks.b[3]
        for i in range(npair):
            cs = slice(t0 + 128 * i, t0 + 128 * (i + 1))
            P.mm(X[:, 128 * i:128 * (i + 1)], KG[0:32, cs], QG[0:32, cs])
            P.mm(Y[:, 128 * i:128 * (i + 1)], KG[32:64, cs], QG[32:64, cs])
        P.tt('dve', tm[i2][:, 0:n], X[:, 0:n], K['maskF'][:, 0:n], ALU.mult)
        P.tt('dve', At[i2][:, 0:n], Y[:, 0:n], K['maskB'][:, 0:n], ALU.mult)
        P.tt('pool', At[i2][:, 0:n], At[i2][:, 0:n], tm[i2][:, 0:n], ALU.add)

    def stageB(ti, t0, n):
        i2 = ti % 2
        npair = n // 128
        Z = banks.b[4 + i2]
        for i in range(npair):
            p = t0 // 128 + i
            P.mm(Z[0:64, 128 * i:128 * (i + 1)], Vt[:, p, :], At[i2][:, 128 * i:128 * (i + 1)], start=True, stop=False)
            for j in range(2):
                c = 2 * p + j
                cs = slice(t0 + 128 * i + 64 * j, t0 + 128 * i + 64 * (j + 1))
                kk = 32 if c == 3 else 64
                P.mm(Z[0:64, 128 * i + 64 * j:128 * i + 64 * (j + 1)], STb[0:kk, c + 1, :], QG[0:kk, cs],
                     start=False, stop=(j == 1))
        o = ob[i2]
        P.act(o[0:64, 0:n], Z[0:64, 0:n], AF.Square)

    def stageC(ti, t0, n):
        i2 = ti % 2
        Z = banks.b[4 + i2]
        o = ob[i2]
        zb = banks.b[6 + i2]
        P.mm(zb[0:64, 0:n], K['ones'][0:64, 0:64], o[0:64, 0:n])
        P.act(o[0:64, 0:n], zb[0:64, 0:n], AF.Ln, scale=1.0 / 64, bias=K['eps'][0:64, :])
        P.act(o[0:64, 0:n], o[0:64, 0:n], AF.Exp, scale=-0.5)
        P.stt(o[0:64, 0:n], Z[0:64, 0:n], par[0:64, 1:2], o[0:64, 0:n], ALU.mult, ALU.mult)
        y = yb[i2]
        P.stt(y[0:64, 0:n], o[0:64, 0:n], 0.5, sg[0:64, t0:t0 + n], ALU.mult, ALU.mult)
        store_y(P, io, 0, t0, n, y[0:64, 0:n])

    nt2 = len(tiles2)
    for it in range(nt2 + 1):
        if it < nt2:
            stageA(*tiles2[it])
        if it >= 1:
            stageB(*tiles2[it - 1])
            stageC(*tiles2[it - 1])


def ssd_phase(nc, P, A, banks, K, xn_v, io, layer, with_ctx):
    A.reset()
    NC_ = 132
    XB = A.alloc([TP], BF16)
    C2 = A.alloc([TP], BF16)
    sgz = A.alloc([NTOK], BF16)
    par = A.alloc([16], F32)
    P.dma('sp', par[:, 0:5], io['ssd_par'])
    cpar = A.alloc([2, 5], F32)
    P.dma('sp', cpar[:], io['ssd_cpar'])
    na = par[:, 5:7]
    P.act(na, par[:, 2:4], AF.Exp)
    P.ts('dve', na, na, -1.0, None, ALU.mult)
    dts_d = io['ssd_scr'][0:2]
    crow_d = io['ssd_scr'][2:4]
    clr_d = io['ssd_scr'][4:6]
    mk = A.mark()
    stage = [A.alloc([322], F32) for _ in range(2)]
    Wsb = load_weights_bf16(nc, P, A, io['w_ssd'], 322, stage)
    xbufs = [A.alloc([8, 512], BF16) for _ in range(2)]
    XR1 = A.alloc([TP], F32)
    XR2 = A.alloc([TP], F32)
    XC = A.alloc([2052], F32)
    te = [A.alloc([512], F32) for _ in range(2)]
    dtt = [A.alloc([512], F32) for _ in range(2)]
    for X in (XR1, XR2):
        P.memset('pool', X[:, 0:PADC], 0.0)
        P.memset('pool', X[:, PADC + CTXL:PADL], 0.0)
        P.memset('pool', X[:, PADL + SEQ:TP], 0.0)

    def ev_raw(dst):
        def f(ps, t0, n, ti):
            c = pcol(t0)
            P.copy('act', dst[:, c:c + n], ps[:, 0:n])
        return f

    def ev_gate_dt(ps, t0, n, ti):
        silu_evac(P, sgz[0:64, t0:t0 + n], ps[0:64, 0:n], te[ti % 2], 64, n)
        P.copy('dve', dtt[ti % 2][64:66, 0:n], ps[64:66, 0:n])
        P.dma('pool', dts_d[:, t0:t0 + n], dtt[ti % 2][64:66, 0:n])

    fm = [(0, 128, False, ev_raw(XR1)), (128, 128, False, ev_raw(XR2)), (256, 66, False, ev_gate_dt)]
    inproj(nc, P, banks, xn_v, Wsb, TOK_TILES, fm, None, xbufs, bank_ids=(0, 1, 2, 3))
    pieces = [(PADC, PADC + CTXL)] + [(PADL + 2048 * i, PADL + 2048 * (i + 1)) for i in range(4)]
    for gi, (src, dst) in enumerate(((XR1, XB), (XR2, C2))):
        for (lo, hi) in pieces:
            n = hi - lo
            P.act(XC[:, 2:2 + n], src[:, lo - 2:hi - 2], AF.Identity, scale=cpar[:, gi, 0:1], bias=cpar[:, gi, 4:5])
            for k in range(1, 4):
                P.stt(XC[:, 2:2 + n], src[:, lo + k - 2:hi + k - 2], cpar[:, gi, k:k + 1], XC[:, 2:2 + n], ALU.mult, ALU.add)
            P.act(dst[:, lo:hi], XC[:, 2:2 + n], AF.Silu)
    A.release(mk)
    PM = A.alloc([2, 128], F32)
    for d in range(2):
        P.dma('sp', PM[0:66, d, :], dts_d[d].rearrange("(p s) -> p s", s=128))
    DT = A.alloc([2, 128], F32)
    for d in range(2):
        P.act(DT[0:66, d, :], PM[0:66, d, :], AF.Exp, bias=par[0:66, d:d + 1])
    P.act(DT[0:66], DT[0:66], AF.Ln, bias=K['one'][0:66, 0:1])
    DA = A.alloc([2, 128], F32)
    for d in range(2):
        P.ts('dve', DA[0:66, d, :], DT[0:66, d, :], na[0:66, d:d + 1], None, ALU.mult)
    CUM = A.alloc([2, 128], F32)
    P.scan(CUM[0:66, 0, :], K['scanmask2'][0:66, 0, :], DA[0:66, 0, :], 0.0)
    P.scan(CUM[0:66, 1, :][:, ::-1], K['scanmask2'][0:66, 1, :][:, ::-1], DA[0:66, 1, :][:, ::-1], 0.0)
    P.dma('pool', crow_d[0].rearrange("(p s) -> p s", s=128), CUM[0:66, 0, :])
    P.dma('pool', crow_d[1].rearrange("(p s) -> p s", s=128), CUM[0:66, 1, :])
    LND = A.alloc([2, 128], F32)
    P.act(LND[0:66], DT[0:66], AF.Ln)
    CQ0 = A.alloc([2, 128], F32)
    P.tt('dve', CQ0[0:66], CUM[0:66], LND[0:66], ALU.subtract)
    CL = A.alloc([2, 2], F32)
    C4 = CUM[0:66].rearrange("p d (c s) -> p d c s", s=64)
    P.copy('dve', CL[0:66, 0, :], C4[:, 0, :, 63])
    P.copy('dve', CL[0:66, 1, :], C4[:, 1, :, 0])
    P.dma('pool', clr_d[0, 0:132].rearrange("(p c) -> p c", c=2), CL[0:66, 0, :])
    P.dma('pool', clr_d[1, 0:132].rearrange("(p c) -> p c", c=2), CL[0:66, 1, :])
    W0 = A.alloc([2, 128], F32)
    W4 = W0[0:66].rearrange("p d (c s) -> p d c s", s=64)
    P.tt('dve', W4, CL[0:66].unsqueeze(3).to_broadcast([66, 2, 2, 64]), C4, ALU.subtract)
    P.act(W0[0:66], W0[0:66], AF.Exp)
    P.tt('dve', W0[0:66], W0[0:66], DT[0:66], ALU.mult)
    CQ = A.alloc([2, 66], F32)
    WT = A.alloc([2, 66], F32)
    for d in range(2):
        for (src, dst) in ((CQ0, CQ), (W0, WT)):
            pt = banks.b[(2 * d) % 8 + (0 if src is CQ0 else 1)]
            P.transpose(pt[:, 0:66], src[0:66, d, :], K['ident'][0:66, 0:66])
            P.copy('act', dst[:, d, :], pt[:, 0:66])
    clr = A.alloc([132], F32)
    P.dma('sp', clr[0:2, :], clr_d[:, 0:132])
    DEC = A.alloc([132], F32)
    pd = banks.b[4]
    P.mm(pd[:, 0:132], K['sel2'][0:2, :], clr[0:2, :])
    P.act(DEC[:, :], pd[:, 0:132], AF.Exp)
    xBt = A.alloc([66, 128], BF16)
    for p in range(66):
        c0 = pcol(128 * p)
        pt = banks.b[p % 4].bitcast(BF16)
        P.transpose(pt[:, 0:128], XB[:, c0:c0 + 128], K['identb'][:, :])
        P.copy('act' if p % 2 == 0 else 'dve', xBt[:, p, :], pt[:, 0:128])
    BW = A.alloc([66, 128], BF16)
    for d in range(2):
        P.tt('dve', BW[:, :, 64 * d:64 * d + 64], xBt[:, :, 64:128], WT[:, d, :].unsqueeze(2).to_broadcast([128, 66, 64]), ALU.mult)
    ST = A.alloc([NC_ + 2, 64], F32)
    STb = A.alloc([NC_ + 2, 64], BF16)
    P.memset('pool', ST[:, 0:2, :], 0.0)
    P.memset('pool', ST[:, NC_:NC_ + 2, :], 0.0)
    for g0 in range(0, 66, 8):
        npair = min(8, 66 - g0)
        for i in range(npair):
            p = g0 + i
            for j in range(2):
                P.mm(banks.b[j][:, i * 64:(i + 1) * 64], BW[64 * j:64 * j + 64, p, :], xBt[64 * j:64 * j + 64, p, 0:64])
        for j in range(2):
            src = banks.b[j][:, 0:npair * 64].rearrange("p (c v) -> p c v", v=64)
            c0 = 2 * g0 + j
            P.copy('act', ST[0:64, c0 + 2:c0 + 1 + 2 * npair:2, :], src[0:64])
            P.copy('dve', ST[64:128, c0:c0 + 2 * npair - 1:2, :], src[64:128])
    fsteps = [(c + 2, c + 1, c) for c in range(1, NC_)]
    bsteps = [(c, c + 1, c) for c in (2, 1, 0)] + [(131, 0, 131)] + [(c, c + 1, c) for c in range(130, 4, -1)]
    for i in range(max(len(fsteps), len(bsteps))):
        if i < len(fsteps):
            o_, i_, d_ = fsteps[i]
            P.stt(ST[0:64, o_, :], ST[0:64, i_, :], DEC[0:64, d_:d_ + 1], ST[0:64, o_, :], ALU.mult, ALU.add)
        if i < len(bsteps):
            o_, i_, d_ = bsteps[i]
            P.stt(ST[64:128, o_, :], ST[64:128, i_, :], DEC[64:128, d_:d_ + 1], ST[64:128, o_, :], ALU.mult, ALU.add)
    P.copy('dve', ST[64:128, 132, :], ST[64:128, 0, :])
    P.copy('dve', STb[:], ST[:])
    crt = [A.alloc([512], F32) for _ in range(2)]
    SG = [A.alloc([2, 512], F32) for _ in range(2)]
    Ls = [A.alloc([512], F32) for _ in range(2)]
    Mt = [A.alloc([512], BF16) for _ in range(2)]
    ec = [A.alloc([512], F32) for _ in range(2)]
    Cs = [A.alloc([512], BF16) for _ in range(2)]
    yv = [A.alloc([512], F32) for _ in range(2)]
    yb = [A.alloc([512], BF16) for _ in range(2)]
    tiles2 = [(ti, t0, n) for ti, (t0, n) in enumerate(TOK_TILES) if not (t0 < CTXL and not with_ctx)]

    def stageA(ti, t0, n):
        i2 = ti % 2
        npair = n // 128
        p0 = t0 // 128
        pc0 = pcol(t0)
        cr = crt[i2]
        P.dma('sp', cr[0:2, 0:n], crow_d[:, t0:t0 + n])
        pe_ = banks.b[0]
        P.mm(pe_[:, 0:n], K['sel2'][0:2, :], cr[0:2, 0:n])
        P.act(ec[i2][:, 0:n], pe_[:, 0:n], AF.Exp)
        P.tt('dve', Cs[i2][:, 0:n], C2[:, pc0:pc0 + n], ec[i2][:, 0:n], ALU.mult)
        for d in range(2):
            pb = banks.b[1 + d]
            P.mm(pb[:, 0:n], K['selrow'][0:2, d, :], cr[0:2, 0:n], start=True, stop=False)
            P.mm(pb[:, 0:n], K['identb'][:, :], K['negF' if d == 0 else 'negB'][:, 0:n], start=False, stop=True)
            P.tt('dve', SG[i2][:, d, 0:n].rearrange("p (a b) -> p a b", b=128),
                 pb[:, 0:n].rearrange("p (a b) -> p a b", b=128),
                 CQ[:, d, p0:p0 + npair].unsqueeze(2).to_broadcast([128, npair, 128]), ALU.subtract)
        P.act(SG[i2][:, :, 0:n], SG[i2][:, :, 0:n], AF.Exp)
        P.tt('pool', Ls[i2][:, 0:n], SG[i2][:, 0, 0:n], SG[i2][:, 1, 0:n], ALU.add)
        pcb = banks.b[3]
        for i in range(npair):
            cs = slice(pc0 + 128 * i, pc0 + 128 * (i + 1))
            P.mm(pcb[:, 128 * i:128 * (i + 1)], XB[64:128, cs], C2[64:128, cs])
        P.tt('dve', Mt[i2][:, 0:n], pcb[:, 0:n], Ls[i2][:, 0:n], ALU.mult)

    def stageB(ti, t0, n):
        i2 = ti % 2
        npair = n // 128
        p0 = t0 // 128
        pc0 = pcol(t0)
        Y = banks.b[4 + i2]
        for i in range(npair):
            p = p0 + i
            P.mm(Y[0:64, 128 * i:128 * (i + 1)], xBt[:, p, 0:64], Mt[i2][:, 128 * i:128 * (i + 1)], start=True, stop=False)
            for j in range(2):
                c = 2 * p + j
                kk = 64 if c == 3 else 128
                P.mm(Y[0:64, 128 * i + 64 * j:128 * i + 64 * (j + 1)], STb[0:kk, c + 1, :],
                     Cs[i2][0:kk, 128 * i + 64 * j:128 * i + 64 * (j + 1)], start=False, stop=(j == 1))
        P.stt(yv[i2][0:64, 0:n], XB[0:64, pc0:pc0 + n], par[0:64, 4:5], Y[0:64, 0:n], ALU.mult, ALU.add)
        y = yb[i2]
        P.stt(y[0:64, 0:n], yv[i2][0:64, 0:n], 0.5, sgz[0:64, t0:t0 + n], ALU.mult, ALU.mult)
        store_y(P, io, 3, t0, n, y[0:64, 0:n])

    nt2 = len(tiles2)
    for it in range(nt2 + 1):
        if it < nt2:
            stageA(*tiles2[it])
        if it >= 1:
            stageB(*tiles2[it - 1])

OFF_A, OFF_B, OFF_C, OFF_D = 0, 800, 1312, 2336


def rope_tables():
    n_freq = 8
    inv = (np.float32(10000.0) ** (-(np.arange(n_freq, dtype=np.float32)) / np.float32(n_freq))).astype(np.float32)
    t = np.arange(SEQ)
    pos_r = (t // 64).astype(np.float32)
    pos_c = (t % 64).astype(np.float32)
    ang_r = pos_r[:, None] * inv
    ang_c = pos_c[:, None] * inv
    ang = np.concatenate([ang_r, ang_r, ang_c, ang_c], axis=-1).astype(np.float32)
    cos = np.cos(ang).astype(np.float32).T
    sin = np.sin(ang).astype(np.float32).T
    sign = np.ones(32, np.float32)
    for a in range(2):
        sign[a * 16:a * 16 + 8] = -1.0
    sins = sin * sign[:, None]
    cosT = np.zeros((128, SEQ), np.float32)
    sinT = np.zeros((128, SEQ), np.float32)
    for c in range(2):
        cosT[64 * c:64 * c + 32] = cos
        sinT[64 * c:64 * c + 32] = sins
    return cosT, sinT


def rot_perm():
    perm = np.zeros(32, np.int64)
    for a in range(2):
        for f in range(8):
            perm[a * 16 + f] = a * 16 + 8 + f
            perm[a * 16 + 8 + f] = a * 16 + f
    return perm


def prep_w_attn(w_in_l, h):
    W = np.zeros((D, 640), np.float32)
    perm = rot_perm()
    for c in range(2):
        qc = OFF_C + h * 64 + c * 32
        kc = OFF_C + 256 + h * 64 + c * 32
        W[:, 64 * c:64 * c + 32] = w_in_l[:, qc:qc + 32]
        W[:, 128 + 64 * c:128 + 64 * c + 32] = w_in_l[:, qc + perm]
        W[:, 256 + 64 * c:256 + 64 * c + 32] = w_in_l[:, kc:kc + 32]
        W[:, 384 + 64 * c:384 + 64 * c + 32] = w_in_l[:, kc + perm]
    W[:, 512:576] = w_in_l[:, OFF_C + 768 + h * 64:OFF_C + 768 + h * 64 + 64]
    W[:, 576:640] = w_in_l[:, OFF_C + 512 + h * 64:OFF_C + 512 + h * 64 + 64]
    return W


def mix_consts():
    selZ = np.zeros((128, 64), np.float32)
    selZ[64, :] = 1.0
    t = np.arange(512)
    scanmask = np.zeros((64, 512), np.float32)
    scanmask[0:32] = (t % 64 != 0).astype(np.float32)[None]
    scanmask[32:64] = (t % 64 != 63).astype(np.float32)[None]
    j = np.arange(128)[:, None]
    i = np.arange(128)[None, :]
    same = (j // 64) == (i // 64)
    mF = (same & (j <= i)).astype(np.float32)
    mB = (same & (j >= i)).astype(np.float32)
    scanmask2 = np.zeros((128, 2, 128), np.float32)
    s_ = np.arange(128)
    scanmask2[:, 0, :] = (s_ % 64 != 0).astype(np.float32)[None]
    scanmask2[:, 1, :] = (s_ % 64 != 63).astype(np.float32)[None]
    sel2 = np.zeros((2, 128), np.float32)
    sel2[0, 0:64] = 1.0
    sel2[1, 64:128] = 1.0
    selrow = np.zeros((2, 2, 128), np.float32)
    selrow[0, 0, :] = 1.0
    selrow[1, 1, :] = 1.0
    NEG = -30000.0
    negF = np.where(mF > 0, 0.0, NEG).astype(np.float32)
    negB = np.where(mB > 0, 0.0, NEG).astype(np.float32)
    return {'selZ': selZ, 'ident': np.eye(128, dtype=np.float32), 'scanmask': scanmask,
            'maskF': np.tile(mF, (1, 4)), 'maskB': np.tile(mB, (1, 4)), 'scanmask2': scanmask2,
            'sel2': sel2, 'selrow': selrow, 'negF': np.tile(negF, (1, 4)), 'negB': np.tile(negB, (1, 4))}


def build_mix_kernel(nc, P, layer, with_ctx, mixers):
    banks = Banks(P)
    K = load_consts_tok(nc, P)
    epst = P.sbuf("eps_t", [128, 1], F32)
    P.memset('pool', epst[:], EPS)
    K['eps'] = epst
    selZ_d = nc.dram_tensor("selZ", [128, 64], F32, kind="ExternalInput").ap()
    selZ = P.sbuf("selZ_sb", [128, 64], F32)
    P.dma('sp', selZ[:], selZ_d)
    K['selZ'] = selZ
    ident_d = nc.dram_tensor("ident", [128, 128], F32, kind="ExternalInput").ap()
    ident = P.sbuf("ident_sb", [128, 128], F32)
    P.dma('sp', ident[:], ident_d)
    identb = P.sbuf("identb_sb", [128, 128], BF16)
    P.copy('pool', identb[:], ident[:])
    K['ident'] = ident
    K['identb'] = identb
    one = P.sbuf("one_t", [128, 1], F32)
    P.memset('pool', one[:], 1.0)
    K['one'] = one
    for nm, shp in (('scanmask', [64, 512]), ('maskF', [128, 512]), ('maskB', [128, 512]),
                    ('scanmask2', [128, 2, 128]), ('sel2', [2, 128]), ('selrow', [2, 2, 128])):
        d_ = nc.dram_tensor(nm, shp, F32, kind="ExternalInput").ap()
        t_ = P.sbuf(nm + "_sb", shp, F32)
        P.dma('sp', t_[:], d_)
        K[nm] = t_
    for nm in ('negF', 'negB'):
        d_ = nc.dram_tensor(nm, [128, 512], F32, kind="ExternalInput").ap()
        t_ = P.sbuf(nm + "_f", [128, 512], F32)
        P.dma('sp', t_[:], d_)
        tb_ = P.sbuf(nm + "_sb", [128, 512], BF16)
        P.copy('pool', tb_[:], t_[:])
        K[nm] = tb_
    io = {}
    xn_d = nc.dram_tensor("xn", [D, NTOK], BF16, kind="ExternalInput").ap()
    xn_v = xn_d.rearrange("(kc p) t -> p kc t", p=128)
    io['yT'] = nc.dram_tensor("yT", [4, 64, NTOK], BF16, kind="ExternalOutput").ap()
    A = P.arena("arena", 190 * 1024)
    if 'attn' in mixers:
        io['w_attn'] = nc.dram_tensor("w_attn", [D, 640], F32, kind="ExternalInput").ap()
        io['cosT'] = nc.dram_tensor("cosT", [128, SEQ], F32, kind="ExternalInput").ap()
        io['sinT'] = nc.dram_tensor("sinT", [128, SEQ], F32, kind="ExternalInput").ap()
        io['diff_lam'] = nc.dram_tensor("diff_lam", [128], F32, kind="ExternalInput").ap()
        io['diff_subln_w'] = nc.dram_tensor("diff_subln_w", [64, 1], F32, kind="ExternalInput").ap()
        attention_phase(nc, P, A, banks, K, xn_v, io, layer, with_ctx)
    if 'gla' in mixers:
        _add_gla(nc, P, A, banks, K, xn_v, io, layer, with_ctx)
    if 'ssd' in mixers:
        io['w_ssd'] = nc.dram_tensor("w_ssd", [D, 322], F32, kind="ExternalInput").ap()
        io['ssd_par'] = nc.dram_tensor("ssd_par", [128, 5], F32, kind="ExternalInput").ap()
        io['ssd_cpar'] = nc.dram_tensor("ssd_cpar", [128, 2, 5], F32, kind="ExternalInput").ap()
        io['ssd_scr'] = nc.dram_tensor("ssd_scr", [6, NTOK], F32, kind="Internal").ap()
        ssd_phase(nc, P, A, banks, K, xn_v, io, layer, with_ctx)
    if 'lru' in mixers:
        io['w_lru'] = nc.dram_tensor("w_lru", [D, 192], F32, kind="ExternalInput").ap()
        io['lru_par'] = nc.dram_tensor("lru_par", [128, 9], F32, kind="ExternalInput").ap()
        io['lru_gw'] = nc.dram_tensor("lru_gw", [64, 256], F32, kind="ExternalInput").ap()
        lru_phase(nc, P, A, banks, K, xn_v, io, layer, with_ctx)


def _add_gla(nc, P, A, banks, K, xn_v, io, layer, with_ctx):
    io['w_gla'] = nc.dram_tensor("w_gla", [D, 320], F32, kind="ExternalInput").ap()
    io['gla_par'] = nc.dram_tensor("gla_par", [64, 2], F32, kind="ExternalInput").ap()
    io['gla_w2bd'] = nc.dram_tensor("gla_w2bd", [32, 64], F32, kind="ExternalInput").ap()
    gla_phase(nc, P, A, banks, K, xn_v, io, layer, with_ctx)


def prep_gla(inp, l, h):
    w_in_l = inp['w_in'][l]
    W = np.zeros((D, 288), np.float32)
    q = w_in_l[:, OFF_A + h * 32:OFF_A + h * 32 + 32]
    k = w_in_l[:, OFF_A + 128 + h * 32:OFF_A + 128 + h * 32 + 32]
    W[:, 0:32] = q
    W[:, 32:64] = q
    W[:, 64:96] = k
    W[:, 96:128] = k
    W[:, 128:144] = w_in_l[:, OFF_A + 512:OFF_A + 528]
    W[:, 144:160] = w_in_l[:, OFF_A + 528:OFF_A + 544]
    W[:, 192:256] = w_in_l[:, OFF_A + 544 + h * 64:OFF_A + 544 + h * 64 + 64]
    W[:, 256:288] = 0
    Wv = w_in_l[:, OFF_A + 256 + h * 64:OFF_A + 256 + h * 64 + 64]
    W2 = np.zeros((D, 320), np.float32)
    W2[:, 0:256] = W[:, 0:256]
    W2[:, 256:320] = Wv
    par = np.zeros((64, 2), np.float32)
    par[0:32, 0] = inp['gla_b2'][l][0][h * 32:(h + 1) * 32]
    par[32:64, 0] = inp['gla_b2'][l][1][h * 32:(h + 1) * 32]
    par[:, 1] = inp['gla_norm_w'][l]
    w2bd = np.zeros((32, 64), np.float32)
    w2bd[0:16, 0:32] = inp['gla_w2'][l][0][:, h * 32:(h + 1) * 32]
    w2bd[16:32, 32:64] = inp['gla_w2'][l][1][:, h * 32:(h + 1) * 32]
    return {"w_gla": W2, "gla_par": par, "gla_w2bd": w2bd}


def prep_ssd(inp, l, h):
    w_in_l = inp['w_in'][l]
    gr = h // 2
    W = np.zeros((D, 322), np.float32)
    cx = slice(OFF_D + h * 64, OFF_D + h * 64 + 64)
    cB = slice(OFF_D + 256 + gr * 64, OFF_D + 256 + gr * 64 + 64)
    cC = slice(OFF_D + 384 + gr * 64, OFF_D + 384 + gr * 64 + 64)
    W[:, 0:64] = w_in_l[:, cx]
    W[:, 64:128] = w_in_l[:, cB]
    W[:, 128:192] = w_in_l[:, cC]
    W[:, 192:256] = w_in_l[:, cC]
    W[:, 256:320] = w_in_l[:, OFF_D + 520 + h * 64:OFF_D + 520 + h * 64 + 64]
    W[:, 320] = w_in_l[:, OFF_D + 512 + h]
    W[:, 321] = w_in_l[:, OFF_D + 516 + h]
    par = np.zeros((128, 5), np.float32)
    par[:, 0] = inp['ssd_dt_bias'][l][0][h]
    par[:, 1] = inp['ssd_dt_bias'][l][1][h]
    par[:, 2] = inp['ssd_a_log'][l][0][h]
    par[:, 3] = inp['ssd_a_log'][l][1][h]
    par[:, 4] = inp['ssd_d'][l][h]
    cw, cb = inp['ssd_conv_w'][l], inp['ssd_conv_b'][l]
    cpar = np.zeros((128, 2, 5), np.float32)
    ch = [np.r_[h * 64:h * 64 + 64, 256 + gr * 64:256 + gr * 64 + 64],
          np.r_[384 + gr * 64:384 + gr * 64 + 64, 384 + gr * 64:384 + gr * 64 + 64]]
    for g in range(2):
        cpar[:, g, 0:4] = cw[:, ch[g]].T
        cpar[:, g, 4] = cb[ch[g]]
    return {"w_ssd": W, "ssd_par": par, "ssd_cpar": cpar}


def prep_lru(inp, l, h):
    w_in_l = inp['w_in'][l]
    W = np.zeros((D, 192), np.float32)
    xs = w_in_l[:, OFF_B + h * 64:OFF_B + h * 64 + 64]
    W[:, 0:64] = xs
    W[:, 64:128] = xs
    W[:, 128:192] = w_in_l[:, OFF_B + 256 + h * 64:OFF_B + 256 + h * 64 + 64]
    sl = slice(h * 64, h * 64 + 64)
    par = np.zeros((128, 9), np.float32)
    for d in range(2):
        rows = slice(64 * d, 64 * d + 64)
        par[rows, 0:4] = inp['lru_conv_w'][l][:, sl].T
        par[rows, 4] = inp['lru_conv_b'][l][sl]
        par[rows, 5] = inp['lru_ba'][l][d][sl]
        par[rows, 6] = inp['lru_bx'][l][d][sl]
        par[rows, 7] = inp['lru_lam'][l][d][sl]
    gw = np.zeros((64, 256), np.float32)
    for d in range(2):
        gw[:, 64 * d:64 * d + 64] = inp['lru_wa'][l][d][h]
        gw[:, 128 + 64 * d:128 + 64 * d + 64] = inp['lru_wx'][l][d][h]
    return {"w_lru": W, "lru_par": par, "lru_gw": gw}


def mix_inputs(inp, l, h, mixers, cosT, sinT, cst):
    m = dict(cst)
    if 'ssd' in mixers:
        m.update(prep_ssd(inp, l, h))
    if 'gla' in mixers:
        m.update(prep_gla(inp, l, h))
    if 'attn' in mixers:
        m.update({"w_attn": prep_w_attn(inp['w_in'][l], h), "cosT": cosT, "sinT": sinT,
                  "diff_lam": np.ascontiguousarray(inp['diff_lam'][l].reshape(-1)),
                  "diff_subln_w": np.ascontiguousarray(inp['diff_subln_w'][l].reshape(64, 1))})
    if 'lru' in mixers:
        m.update(prep_lru(inp, l, h))
    return m


def _rt_of(hl, hc, b, q):
    return np.ascontiguousarray(np.concatenate([hc[b, 64 * q:64 * q + 64].T, hl[b, 2048 * q:2048 * (q + 1)].T], axis=1))


def _gather_cols(parts):
    return np.ascontiguousarray(np.concatenate([p[:, 0:64] for p in parts] + [p[:, 64:] for p in parts], axis=1))


def kernel_unfused(**inp):
    inp = {k: np.asarray(v) for k, v in inp.items()}
    x, ctx, c, c_ctx = inp['x'], inp['ctx'], inp['c'], inp['c_ctx']
    cosT, sinT = rope_tables()
    cst = mix_consts()
    cvecs = [np.ascontiguousarray(np.stack([pk(c[b]), pk(c_ctx)], axis=2)) for b in range(2)]
    maps = []
    for core in range(8):
        b, q = core // 4, core % 4
        maps.append({"RT": _rt_of(x, ctx, b, q), "cvec": cvecs[b], "wmod_n": inp['w_mod'][0],
                     "bmod_n": pk(inp['b_mod'][0]), "normw": pk(inp['norm_w'][0])})
    res = _launch(lambda nc, P: build_tok_kernel(nc, P, True, False, False), maps)
    RT = [m["RT"] for m in maps]
    xnT = [np.asarray(r["xnT"]) for r in res]
    out = None
    for l in range(2):
        last = (l == 1)
        xn_full = [_gather_cols(xnT[4 * b:4 * b + 4]) for b in range(2)]
        maps = []
        for core in range(8):
            b, h = core // 4, core % 4
            m = {"xn": xn_full[b]}
            m.update(mix_inputs(inp, l, h, ('gla', 'lru', 'attn', 'ssd'), cosT, sinT, cst))
            maps.append(m)
        res = _launch(lambda nc, P: build_mix_kernel(nc, P, l, not last, ('gla', 'lru', 'attn', 'ssd')), maps)
        yT = [np.asarray(r["yT"]) for r in res]
        maps = []
        for core in range(8):
            b, q = core // 4, core % 4
            yb = np.stack([yT[4 * b + h] for h in range(4)], axis=1).reshape(1024, NTOK)
            yq = np.ascontiguousarray(np.concatenate([yb[:, 64 * q:64 * q + 64],
                                                      yb[:, CTXL + 2048 * q:CTXL + 2048 * (q + 1)]], axis=1))
            m = {"RT": RT[core], "cvec": cvecs[b], "yT": yq, "wout": inp['w_out'][l], "ssdnw": pk(inp['ssd_norm_w'][l]),
                 "wmod_g": inp['w_mod'][l], "bmod_g": pk(inp['b_mod'][l])}
            if last:
                m["fnw"] = pk(inp['final_norm_w'])
            else:
                m.update({"wmod_n": inp['w_mod'][l + 1], "bmod_n": pk(inp['b_mod'][l + 1]), "normw": pk(inp['norm_w'][l + 1])})
            maps.append(m)
        res = _launch(lambda nc, P: build_tok_kernel(nc, P, False, last, True), maps)
        if last:
            out = np.zeros((2, SEQ, D), np.float32)
            for core in range(8):
                b, q = core // 4, core % 4
                out[b, 2048 * q:2048 * (q + 1), :] = np.asarray(res[core]["outT"]).T
        else:
            RT = [np.asarray(r["RTout"]) for r in res]
            xnT = [np.asarray(r["xnT"]) for r in res]
    return out

LATE_BYTES = 52 * 1024
CHUNKS = [(0, 256)] + [(256 + 2048 * k, 2048) for k in range(4)]
GROUPS = [[0, 1, 2, 3], [4, 5, 6, 7]]


def fs_mod(nc, P, K, banks, A, cvec_d, wmod_d, bmod_d, name):
    cT = A.alloc([8, 2], F32)
    P.dma('sp', cT[:], cvec_d)
    e = A.alloc([8, 2], F32)
    P.act(e[:], cT[:], AF.Exp, scale=-1.0)
    P.ts('dve', e[:], e[:], 1.0, None, ALU.add)
    P.recip(e[:], e[:])
    sc = A.alloc([8, 2], F32)
    P.tt('dve', sc[:], cT[:], e[:], ALU.mult)
    bT = A.alloc([6], F32)
    P.dma('sp', bT[:], bmod_d)
    wb = A.alloc([8, 768], F32)
    for kc in range(8):
        P.dma('sp' if kc % 2 == 0 else 'act', wb[:, kc, :], wmod_d[kc * 128:(kc + 1) * 128, :])
    ps = banks.b[7]
    for fc in range(6):
        for kc in range(8):
            P.mm(ps[:, fc * 2:fc * 2 + 2], wb[:, kc, fc * 128:(fc + 1) * 128], sc[:, kc, :], start=(kc == 0), stop=(kc == 7))
    modT = P.sbuf(name, [128, 6, 2], F32)
    P.tt('dve', modT[:], ps[:, 0:12].rearrange("p (a b) -> p a b", b=2), bT[:].unsqueeze(2).to_broadcast([128, 6, 2]), ALU.add)
    return modT


def fs_token_phase(nc, P, K, banks, A, io, L, mode):
    A.reset()
    D_ = io['dram']
    first, last = mode == 'first', mode == 'last'
    Rb = [A.alloc([2, 2048], F32) for _ in range(2)]
    ssg = [A.alloc([2048], F32) for _ in range(1)]
    rst = [A.alloc([512], F32) for _ in range(2)]
    t1 = [A.alloc([512], F32) for _ in range(2)]
    xo = [A.alloc([2, 512], BF16) for _ in range(2)]
    assert A.top <= LATE_BYTES, A.top
    if not first:
        gateT = fs_mod(nc, P, K, banks, A, io['cvec'], io['wmod'][L], io['bmod'][L], "modT_g%d" % L)
        wo32 = A.alloc([8, 256], F32)
        for kc in range(8):
            P.dma('sp' if kc % 2 == 0 else 'act', wo32[:, kc, :], io['wout'][L][kc * 128:(kc + 1) * 128, :])
        wo = A.alloc([8, 256], BF16)
        P.copy('dve', wo[:], wo32[:])
        sw = A.alloc([4], F32)
        P.dma('sp', sw[:], io['ssdnw'][L])
    if not last:
        LN = 0 if first else L + 1
        modN = fs_mod(nc, P, K, banks, A, io['cvec'], io['wmod'][LN], io['bmod'][LN], "modT_n%d" % LN)
        nw = A.alloc([2], F32)
        P.dma('sp', nw[:], io['normw'][LN])
        Acoef = A.alloc([2, 2], F32)
        P.ts('dve', Acoef[:], modN[:, 2:4, :], 1.0, None, ALU.add)
        P.tt('dve', Acoef[:], Acoef[:], nw[:].unsqueeze(2).to_broadcast([128, 2, 2]), ALU.mult)
    else:
        fw = A.alloc([2], F32)
        P.dma('sp', fw[:], io['fnw'])
    yb_ = [A.alloc([8, 512], BF16) for _ in range(2)]
    sq = [A.alloc([512], BF16) for _ in range(4)]
    ssr = [A.alloc([2048], F32) for _ in range(2)]
    Rsrc = io['RT_in'] if first else D_['Rs']
    Rsrc_v = Rsrc.rearrange("(fc p) t -> p fc t", p=128)
    Rs_v = D_['Rs'].rearrange("(fc p) t -> p fc t", p=128)
    stage = 'n%d' % (0 if first else L + 1) if not last else 'fin'
    chunks = [(ci, t0, n) for ci, (t0, n) in enumerate(CHUNKS) if not (last and ci == 0)]
    jobs = []
    for (ci, t0, n) in chunks:
        subs = [(s0, min(512, n - s0)) for s0 in range(0, n, 512)]
        for si, (s0, m_) in enumerate(subs):
            jobs.append(dict(ci=ci, t0=t0, n=n, s0=s0, m=m_, first=(si == 0), lastsub=(si == len(subs) - 1)))
    ybuf3 = yb_ + [A.alloc([8, 512], BF16)]

    def P1(j, ji):
        ci, t0, n, s0, m = j['ci'], j['t0'], j['n'], j['s0'], j['m']
        R = Rb[ci % 2]
        if j['first']:
            P.dma('sp', R[:, :, 0:n], Rsrc_v[:, :, t0:t0 + n])
        if first:
            return
        gy = D_['gy%d' % L][ci].rearrange("(kc p) t -> p kc t", p=128)
        yt = ybuf3[ji % 3]
        P.dma('act', yt[:, :, 0:m], gy[:, :, s0:s0 + m])
        ps = banks.b[0 + ji % 2]
        for i, kc in enumerate((1, 3, 5, 7)):
            s_ = sq[i % 2]
            P.act(s_[64:128, 0:m], yt[64:128, kc, 0:m], AF.Square)
            P.mm(ps[:, 0:m], K['onesb'][64:128, :], s_[64:128, 0:m], start=(i == 0), stop=(i == 3))
        r_ = rst[ji % 2]
        P.act(r_[:, 0:m], ps[:, 0:m], AF.Ln, scale=1.0 / 256, bias=K['eps'][:])
        P.act(r_[:, 0:m], r_[:, 0:m], AF.Exp, scale=-0.5)
        for i, kc in enumerate((1, 3, 5, 7)):
            P.stt(yt[64:128, kc, 0:m], yt[64:128, kc, 0:m], sw[64:128, i:i + 1], r_[64:128, 0:m], ALU.mult, ALU.mult)

    def P2(j, ji):
        if first:
            return
        ci, s0, m = j['ci'], j['s0'], j['m']
        isctx = 1 if ci == 0 else 0
        R = Rb[ci % 2]
        yt = ybuf3[ji % 3]
        for fc in range(2):
            po = banks.b[2 + (2 * ji + fc) % 4]
            for kc in range(8):
                P.mm(po[:, 0:m], wo[:, kc, fc * 128:(fc + 1) * 128], yt[:, kc, 0:m], start=(kc == 0), stop=(kc == 7))
            P.stt(R[:, fc, s0:s0 + m], po[:, 0:m], gateT[:, 4 + fc, isctx:isctx + 1], R[:, fc, s0:s0 + m], ALU.mult, ALU.add)

    def P3(j, ji):
        ci, t0, n, s0, m = j['ci'], j['t0'], j['n'], j['s0'], j['m']
        R = Rb[ci % 2]
        srow = ssr[ci % 2]
        pss = banks.b[6 + ji % 2]
        for fc in range(2):
            s_ = sq[2 + fc]
            P.act(s_[:, 0:m], R[:, fc, s0:s0 + m], AF.Square)
            P.mm(pss[:, 0:m], K['onesb'][:, :], s_[:, 0:m], start=(fc == 0), stop=(fc == 1))
        P.copy('dve', srow[0:1, s0:s0 + m], pss[0:1, 0:m])
        if j['lastsub']:
            P.dma('sp', D_['ssb_' + stage][:, t0:t0 + n], srow[0:1, 0:n])
            P.dma('sp', Rs_v[:, :, t0:t0 + n], R[:, :, 0:n])

    nj = len(jobs)
    for it in range(nj + 2):
        if it < nj:
            P1(jobs[it], it)
        if 0 <= it - 1 < nj:
            P2(jobs[it - 1], it - 1)
        if 0 <= it - 2 < nj:
            P3(jobs[it - 2], it - 2)
    P.collective("AllGather", [D_['ssb_' + stage]], [D_['ssg_' + stage]], GROUPS)
    for (ci, t0, n) in chunks:
        isctx = 1 if ci == 0 else 0
        R = Rb[ci % 2]
        P.dma('sp', R[:, :, 0:n], Rs_v[:, :, t0:t0 + n])
        subs = [(s0, min(512, n - s0)) for s0 in range(0, n, 512)]
        sg_ = ssg[0]
        P.dma('act', sg_[0:4, 0:n], D_['ssg_' + stage][:, t0:t0 + n])
        for si, (s0, m) in enumerate(subs):
            pt = banks.b[6 + si % 2]
            P.mm(pt[:, 0:m], K['ones'][0:4, :], sg_[0:4, s0:s0 + m])
            r_ = rst[si % 2]
            P.act(r_[:, 0:m], pt[:, 0:m], AF.Ln, scale=1.0 / D, bias=K['eps'][:])
            P.act(r_[:, 0:m], r_[:, 0:m], AF.Exp, scale=-0.5)
            if not last:
                x_ = xo[si % 2]
                for fc in range(2):
                    t_ = t1[fc]
                    P.stt(t_[:, 0:m], R[:, fc, s0:s0 + m], Acoef[:, fc, isctx:isctx + 1], r_[:, 0:m], ALU.mult, ALU.mult)
                    P.act(x_[:, fc, 0:m], t_[:, 0:m], AF.Identity, bias=modN[:, fc, isctx:isctx + 1])
                xb_v = D_['xnb_' + stage][ci].rearrange("(fc p) t -> p fc t", p=128)
                P.dma('sp', xb_v[:, :, s0:s0 + m], x_[:, :, 0:m])
            else:
                o_v = io['outT'].rearrange("(fc p) t -> p fc t", p=128)
                for fc in range(2):
                    t_ = t1[fc]
                    P.stt(t_[:, 0:m], R[:, fc, s0:s0 + m], fw[:, fc:fc + 1], r_[:, 0:m], ALU.mult, ALU.mult)
                    P.dma('sp', o_v[:, fc, t0 - CTXL + s0:t0 - CTXL + s0 + m], t_[:, 0:m], final=True)
        if not last:
            P.collective("AllGather", [D_['xnb_' + stage][ci]], [D_['xng_' + stage][ci]], GROUPS)


def build_fused(nc, P):
    banks = Banks(P)
    K = load_consts_tok(nc, P)
    epst = P.sbuf("eps_t", [128, 1], F32)
    P.memset('pool', epst[:], EPS)
    K['eps'] = epst
    one = P.sbuf("one_t", [128, 1], F32)
    P.memset('pool', one[:], 1.0)
    K['one'] = one
    for nm, shp in (('selZ', [128, 64]), ('ident', [128, 128]), ('scanmask', [64, 512]), ('maskF', [128, 512]),
                    ('maskB', [128, 512]), ('scanmask2', [128, 2, 128]), ('sel2', [2, 128]), ('selrow', [2, 2, 128])):
        d_ = nc.dram_tensor(nm, shp, F32, kind="ExternalInput").ap()
        t_ = P.sbuf(nm + "_sb", shp, F32)
        P.dma('sp', t_[:], d_)
        K[nm] = t_
    identb = P.sbuf("identb_sb", [128, 128], BF16)
    P.copy('pool', identb[:], K['ident'][:])
    K['identb'] = identb
    onesb = P.sbuf("onesb_sb", [128, 128], BF16)
    P.memset('pool', onesb[:], 1.0)
    K['onesb'] = onesb
    for nm in ('negF', 'negB'):
        d_ = nc.dram_tensor(nm, [128, 512], F32, kind="ExternalInput").ap()
        t_ = P.sbuf(nm + "_f", [128, 512], F32)
        P.dma('sp', t_[:], d_)
        tb_ = P.sbuf(nm + "_sb", [128, 512], BF16)
        P.copy('pool', tb_[:], t_[:])
        K[nm] = tb_
    A = P.arena("arena", 184 * 1024)

    def din(name, shape, dt=F32):
        return nc.dram_tensor(name, list(shape), dt, kind="ExternalInput").ap()

    def dscr(name, shape, dt=F32):
        return nc.dram_tensor(name, list(shape), dt).ap()

    io = {'RT_in': din("RT", [256, NTOK]), 'cvec': din("cvec", [128, 8, 2]),
          'wmod': [din("wmod%d" % l, [D, 768]) for l in range(2)], 'bmod': [din("bmod%d" % l, [128, 6]) for l in range(2)],
          'normw': [din("normw%d" % l, [128, 2]) for l in range(2)], 'wout': [din("wout%d" % l, [D, 256]) for l in range(2)],
          'ssdnw': [din("ssdnw%d" % l, [128, 4]) for l in range(2)], 'fnw': din("fnw", [128, 2]),
          'outT': nc.dram_tensor("outT", [256, SEQ], F32, kind="ExternalOutput").ap()}
    Dm = {'Rs': dscr("Rs", [256, NTOK])}
    for stage in ('n0', 'n1', 'fin'):
        Dm['ssb_' + stage] = dscr("ssb_%s" % stage, [1, NTOK])
        Dm['ssg_' + stage] = dscr("ssg_%s" % stage, [4, NTOK])
    for stage in ('n0', 'n1'):
        Dm['xnb_' + stage] = [dscr("xnb_%s_%d" % (stage, ci), [256, n], BF16) for ci, (t0, n) in enumerate(CHUNKS)]
        Dm['xng_' + stage] = [dscr("xng_%s_%d" % (stage, ci), [D, n], BF16) for ci, (t0, n) in enumerate(CHUNKS)]
    for l in range(2):
        Dm['yb%d' % l] = [dscr("yb%d_%d" % (l, ci), [256, n], BF16) for ci, (t0, n) in enumerate(CHUNKS)]
        Dm['gy%d' % l] = [dscr("gy%d_%d" % (l, ci), [D, n], BF16) for ci, (t0, n) in enumerate(CHUNKS)]
    io['dram'] = Dm
    cosT = din("cosT", [128, SEQ])
    sinT = din("sinT", [128, SEQ])
    fs_token_phase(nc, P, K, banks, A, io, 0, 'first')
    for l in range(2):
        last = (l == 1)
        with_ctx = not last
        xng = Dm['xng_n%d' % l]

        def xn_load(xb, t0, n, xng=xng):
            if t0 < CTXL:
                P.dma('sp', xb[:, :, 0:n], xng[0].rearrange("(kc p) t -> p kc t", p=128)[:, :, t0:t0 + n])
            else:
                k = (t0 - CTXL) // 2048
                c0 = (t0 - CTXL) % 2048
                P.dma('sp', xb[:, :, 0:n], xng[k + 1].rearrange("(kc p) t -> p kc t", p=128)[:, :, c0:c0 + n])

        ybl = Dm['yb%d' % l]

        def y_store(mi, t0, n, src, ybl=ybl):
            if t0 < CTXL:
                P.dma('pool', ybl[0][mi * 64:(mi + 1) * 64, t0:t0 + n], src)
            else:
                k = (t0 - CTXL) // 2048
                c0 = (t0 - CTXL) % 2048
                P.dma('pool', ybl[k + 1][mi * 64:(mi + 1) * 64, c0:c0 + n], src)

        mio = {'y_store': y_store, 'cosT': cosT, 'sinT': sinT}
        mio['w_gla'] = din("w_gla%d" % l, [D, 320])
        mio['gla_par'] = din("gla_par%d" % l, [64, 2])
        mio['gla_w2bd'] = din("gla_w2bd%d" % l, [32, 64])
        mio['w_lru'] = din("w_lru%d" % l, [D, 192])
        mio['lru_par'] = din("lru_par%d" % l, [128, 9])
        mio['lru_gw'] = din("lru_gw%d" % l, [64, 256])
        mio['w_ssd'] = din("w_ssd%d" % l, [D, 322])
        mio['ssd_par'] = din("ssd_par%d" % l, [128, 5])
        mio['ssd_cpar'] = din("ssd_cpar%d" % l, [128, 2, 5])
        mio['ssd_scr'] = dscr("ssd_scr%d" % l, [6, NTOK])
        mio['w_attn'] = din("w_attn%d" % l, [D, 640])
        mio['diff_lam'] = din("diff_lam%d" % l, [128])
        mio['diff_subln_w'] = din("diff_subln_w%d" % l, [64, 1])
        gla_phase(nc, P, A, banks, K, xn_load, mio, l, with_ctx)
        lru_phase(nc, P, A, banks, K, xn_load, mio, l, with_ctx)
        ssd_phase(nc, P, A, banks, K, xn_load, mio, l, with_ctx)
        def after_q(q0, nq, l=l):
            if q0 < CTXL:
                P.collective("AllGather", [Dm['yb%d' % l][0]], [Dm['gy%d' % l][0]], GROUPS)
            elif (q0 - CTXL + nq) % 2048 == 0:
                k = (q0 - CTXL) // 2048
                P.collective("AllGather", [Dm['yb%d' % l][k + 1]], [Dm['gy%d' % l][k + 1]], GROUPS)
        mio['after_q'] = after_q
        attention_phase(nc, P, A, banks, K, xn_load, mio, l, with_ctx)
        fs_token_phase(nc, P, K, banks, A, io, l, 'last' if last else 'mid')


def fused_inputs(inp, core, cosT, sinT, cst):
    b, q = core // 4, core % 4
    h = q
    x, ctx, c, c_ctx = inp['x'], inp['ctx'], inp['c'], inp['c_ctx']
    fsl = slice(256 * q, 256 * q + 256)
    m = dict(cst)
    m['RT'] = np.ascontiguousarray(np.concatenate([ctx[b][:, fsl].T, x[b][:, fsl].T], axis=1))
    m['cvec'] = np.ascontiguousarray(np.stack([pk(c[b]), pk(c_ctx)], axis=2))
    m['cosT'] = cosT
    m['sinT'] = sinT
    m['fnw'] = pk(inp['final_norm_w'][fsl])
    perm = np.array([mi * 256 + hh * 64 + j for hh in range(4) for mi in range(4) for j in range(64)])
    for l in range(2):
        cols = np.r_[256 * q:256 * q + 256, 1024 + 256 * q:1024 + 256 * q + 256, 2048 + 256 * q:2048 + 256 * q + 256]
        m['wmod%d' % l] = np.ascontiguousarray(inp['w_mod'][l][:, cols])
        m['bmod%d' % l] = pk(inp['b_mod'][l][cols])
        m['normw%d' % l] = pk(inp['norm_w'][l][fsl])
        m['wout%d' % l] = np.ascontiguousarray(inp['w_out'][l][perm][:, fsl])
        sw = np.zeros((128, 4), np.float32)
        for i in range(4):
            sw[64:128, i] = inp['ssd_norm_w'][l][i * 64:(i + 1) * 64]
        m['ssdnw%d' % l] = sw
        mi_ = mix_inputs(inp, l, h, ('gla', 'lru', 'attn', 'ssd'), cosT, sinT, cst)
        for k_, v_ in mi_.items():
            if k_ in cst or k_ in ('cosT', 'sinT'):
                continue
            m[k_ + str(l)] = v_
    return m


def kernel(**inp):
    inp = {k: np.asarray(v) for k, v in inp.items()}
    cosT, sinT = rope_tables()
    cst = mix_consts()
    maps = [fused_inputs(inp, core, cosT, sinT, cst) for core in range(8)]
    res = _launch(build_fused, maps)
    out = np.zeros((2, SEQ, D), np.float32)
    for core in range(8):
        b, q = core // 4, core % 4
        out[b, :, 256 * q:256 * q + 256] = np.asarray(res[core]["outT"]).T
    return out
```

```python
import numpy as np
from contextlib import ExitStack
import concourse.bass as bass
import concourse.mybir as mybir
from concourse.bass_utils import run_bass_kernel_spmd

F32 = mybir.dt.float32
BF16 = mybir.dt.bfloat16
AF = mybir.ActivationFunctionType
ALU = mybir.AluOpType
AX = mybir.AxisListType
ENGS = ('pe', 'act', 'dve', 'pool', 'sp')
ESZ = {F32: 4, BF16: 2}


class Prog:
    def __init__(self, nc, stack, n_dma_sems=40):
        self.nc = nc
        self.stack = stack
        self.sem = {e: stack.enter_context(nc.semaphore('sem_' + e)) for e in ENGS}
        self.cnt = {e: 0 for e in ENGS}
        self.known = {e: {} for e in ENGS}
        self.stream = {e: [] for e in ENGS}
        self.dsem = [stack.enter_context(nc.semaphore('dsem%d' % i)) for i in range(n_dma_sems)]
        self.dcum = [0] * n_dma_sems
        self.drr = 0
        self.rows = {}
        self.trk = {}
        self.final_events = []
        self.nwaits = 0
        self.ndma = 0
        self.arenas = {}

    def sbuf(self, name, shape, dtype=F32):
        t = self.stack.enter_context(self.nc.sbuf_tensor(name, list(shape), dtype))
        self.rows[name] = int(np.prod(shape[1:])) * ESZ[dtype]
        return t

    def psum(self, name, shape, dtype=F32):
        t = self.stack.enter_context(self.nc.psum_tensor(name, list(shape), dtype))
        self.rows[name] = int(np.prod(shape[1:])) * ESZ[dtype]
        return t

    def arena(self, name, nbytes):
        t = self.sbuf(name, [128, nbytes // 4], F32)
        a = Arena(self, name, t, nbytes)
        self.arenas[name] = a
        return a

    def _box(self, ap):
        b = self._box0(ap)
        a = self.arenas.get(b[0])
        if a is not None:
            rid = a.region_of(b[3], b[4])
            return ((b[0], rid),) + b[1:]
        return b

    def _box0(self, ap):
        name = ap.tensor.name
        es = ESZ.get(ap.dtype, 4)
        off = int(ap.offset) * es
        pairs = ap.ap
        if name in self.rows:
            rs = self.rows[name]
            p0 = off // rs
            pst, pc = pairs[0]
            p1 = p0 + (pc if pst != 0 else 1)
            fo = off % rs
            rest = pairs[1:]
        else:
            p0, p1 = 0, 1
            fo = off
            rest = pairs
        lo = fo
        hi = fo
        for st, c in rest:
            d = st * (c - 1) * es
            if d < 0:
                lo += d
            else:
                hi += d
        return (name, p0, p1, lo, hi + es)

    @staticmethod
    def _ov(a, b):
        return a[1] < b[2] and b[1] < a[2] and a[3] < b[4] and b[3] < a[4]

    @staticmethod
    def _inside(a, b):
        return a[1] >= b[1] and a[2] <= b[2] and a[3] >= b[3] and a[4] <= b[4]

    def _deps(self, reads, writes):
        deps = []
        for ap in reads:
            b = self._box(ap)
            t = self.trk.get(b[0])
            if t:
                for (wb, ev, clk) in t['w']:
                    if self._ov(b, wb):
                        deps.append((ev, clk, 'raw'))
        for ap in writes:
            b = self._box(ap)
            t = self.trk.get(b[0])
            if t:
                for (wb, ev, clk) in t['w']:
                    if self._ov(b, wb):
                        deps.append((ev, clk, 'waw'))
                for (rb, ev, clk) in t['r']:
                    if self._ov(b, rb):
                        deps.append((ev, clk, 'war'))
        return deps

    @staticmethod
    def _merge(lst):
        merged = {}
        for (rb, rev, rclk) in lst:
            m = merged.get(rev[0])
            if m is None:
                merged[rev[0]] = (rb, rev, rclk)
            else:
                mb, mev, mclk = m
                nb = (rb[0], min(rb[1], mb[1]), max(rb[2], mb[2]), min(rb[3], mb[3]), max(rb[4], mb[4]))
                merged[rev[0]] = (nb, rev, rclk) if rev[1] > mev[1] else (nb, mev, mclk)
        return list(merged.values())

    def _record(self, reads, writes, ev, clk):
        for ap in reads:
            b = self._box(ap)
            t = self.trk.setdefault(b[0], {'w': [], 'r': []})
            rl = t['r']
            for i, (rb, rev, rclk) in enumerate(rl):
                if rev[0] == ev[0] and self._inside(rb, b):
                    rl[i] = (b, ev, clk)
                    break
            else:
                rl.append((b, ev, clk))
                if len(rl) > 40:
                    t['r'] = self._merge(rl)
        for ap in writes:
            b = self._box(ap)
            t = self.trk.setdefault(b[0], {'w': [], 'r': []})
            t['w'] = [e for e in t['w'] if not self._inside(e[0], b)]
            t['r'] = [e for e in t['r'] if not self._inside(e[0], b)]
            t['w'].append((b, ev, clk))
            if len(t['w']) > 40:
                t['w'] = self._merge(t['w'])

    def _waits_for(self, eng, deps):
        kn = self.known[eng]
        waits = {}
        for (ev, clk, kind) in deps:
            key, val = ev
            if key == eng:
                if eng == 'pe':
                    continue
                if eng in ('act', 'dve') and kind != 'raw':
                    continue
            if kn.get(key, 0) >= val:
                continue
            if waits.get(key, 0) < val:
                waits[key] = val
        for (ev, clk, kind) in deps:
            key, val = ev
            if key in waits and waits[key] >= val:
                for k2, v2 in clk.items():
                    if kn.get(k2, 0) < v2:
                        kn[k2] = v2
        for k, v in waits.items():
            if kn.get(k, 0) < v:
                kn[k] = v
        return list(waits.items())

    def op(self, eng, fn, reads=(), writes=()):
        deps = self._deps(reads, writes)
        waits = self._waits_for(eng, deps)
        self.cnt[eng] += 1
        ev = (eng, self.cnt[eng])
        clk = dict(self.known[eng])
        clk[eng] = self.cnt[eng]
        self.stream[eng].append((waits, fn, ('e', eng)))
        self._record(reads, writes, ev, clk)
        self.nwaits += len(waits)
        return ev

    def dma(self, q, out, in_, final=False, **kw):
        i = self.drr
        self.drr = (self.drr + 1) % len(self.dsem)
        deps = self._deps([in_], [out])
        key = ('d', i)
        deps.append(((key, self.dcum[i]), {}, 'raw'))
        waits = self._waits_for(q, deps)
        self.dcum[i] += 16
        ev = (key, self.dcum[i])
        clk = dict(self.known[q])
        clk[key] = self.dcum[i]
        self.stream[q].append((waits, (lambda e, out=out, in_=in_, kw=kw: e.dma_start(out=out, in_=in_, **kw)), ('d', i)))
        self._record([in_], [out], ev, clk)
        if final:
            self.final_events.append(ev)
        self.nwaits += len(waits)
        self.ndma += 1
        return ev

    def collective(self, kind, ins, outs, groups, q='pool'):
        if not hasattr(self, 'csem'):
            self.csem = []
            self.ccum = []
        self.csem.append(self.stack.enter_context(self.nc.semaphore('csem%d' % len(self.csem))))
        self.ccum.append(0)
        i = len(self.csem) - 1
        deps = self._deps(list(ins), list(outs))
        key = ('c', i)
        waits = self._waits_for(q, deps)
        self.ccum[i] += 1
        ev = (key, self.ccum[i])
        clk = dict(self.known[q])
        clk[key] = self.ccum[i]
        self.stream[q].append((waits, (lambda e: e.collective_compute(kind, ALU.bypass, replica_groups=groups,
                                                                      ins=[a.opt() for a in ins], outs=[a.opt() for a in outs])),
                               ('c', i)))
        self._record(list(ins), list(outs), ev, clk)
        self.nwaits += len(waits)
        return ev

    def finish(self, eng='sp'):
        self.stream[eng].append((list(self.final_events), None, None))

    def _semobj(self, key):
        if isinstance(key, tuple):
            return self.dsem[key[1]] if key[0] == 'd' else self.csem[key[1]]
        return self.sem[key]

    def emit(self):
        for e in ENGS:
            assert self.cnt[e] < 60000, (e, self.cnt[e])

        def replay(name, eng):
            for (waits, fn, inc) in self.stream[name]:
                for (key, val) in waits:
                    eng.wait_ge(self._semobj(key), val)
                if fn is None:
                    continue
                ins = fn(eng)
                if inc[0] == 'e':
                    ins.then_inc(self.sem[inc[1]], 1)
                elif inc[0] == 'c':
                    ins.then_inc(self.csem[inc[1]])
                else:
                    ins.then_inc(self.dsem[inc[1]], 16)

        with self.nc.Block() as block:
            @block.sync
            def _(e):
                replay('sp', e)

            @block.scalar
            def _(e):
                replay('act', e)

            @block.vector
            def _(e):
                replay('dve', e)

            @block.gpsimd
            def _(e):
                replay('pool', e)

            @block.tensor
            def _(e):
                replay('pe', e)

    def mm(self, out, lhsT, rhs, start=True, stop=True):
        return self.op('pe', lambda e: e.matmul(out, lhsT, rhs, start=start, stop=stop),
                       reads=[lhsT, rhs], writes=[out])

    def transpose(self, out, in_, ident):
        return self.op('pe', lambda e: e.transpose(out, in_, ident), reads=[in_, ident], writes=[out])

    def act(self, out, in_, func, bias=None, scale=None, accum_out=None):
        reads = [in_]
        kw = {}
        if bias is not None:
            kw['bias'] = bias
            if not isinstance(bias, (int, float)):
                reads.append(bias)
        if scale is not None:
            kw['scale'] = scale
            if not isinstance(scale, (int, float)):
                reads.append(scale)
        writes = [out]
        if accum_out is not None:
            kw['accum_out'] = accum_out
            writes.append(accum_out)
        return self.op('act', lambda e: e.activation(out, in_, func, **kw), reads=reads, writes=writes)

    def tt(self, eng, out, in0, in1, op):
        return self.op(eng, lambda e: e.tensor_tensor(out, in0, in1, op), reads=[in0, in1], writes=[out])

    def ts(self, eng, out, in0, s1, s2, op0, op1=None):
        reads = [in0]
        for s in (s1, s2):
            if s is not None and not isinstance(s, (int, float)):
                reads.append(s)
        if op1 is None:
            return self.op(eng, lambda e: e.tensor_scalar(out, in0, s1, None, op0), reads=reads, writes=[out])
        return self.op(eng, lambda e: e.tensor_scalar(out, in0, s1, s2, op0, op1), reads=reads, writes=[out])

    def stt(self, out, in0, scalar, in1, op0, op1):
        reads = [in0, in1]
        if not isinstance(scalar, (int, float)):
            reads.append(scalar)
        return self.op('dve', lambda e: e.scalar_tensor_tensor(out, in0, scalar, in1, op0, op1),
                       reads=reads, writes=[out])

    def copy(self, eng, out, in_):
        if eng == 'act':
            return self.op('act', lambda e: e.copy(out, in_), reads=[in_], writes=[out])
        return self.op(eng, lambda e: e.tensor_copy(out, in_), reads=[in_], writes=[out])

    def memset(self, eng, ap, val):
        return self.op(eng, lambda e: e.memset(ap, val), reads=[], writes=[ap])

    def scan(self, out, d0, d1, init, op0=ALU.mult, op1=ALU.add):
        reads = [d0, d1]
        if not isinstance(init, (int, float)):
            reads.append(init)
        return self.op('dve', lambda e: e.tensor_tensor_scan(out, d0, d1, init, op0, op1), reads=reads, writes=[out])

    def recip(self, out, in_):
        return self.op('dve', lambda e: e.reciprocal(out, in_), reads=[in_], writes=[out])


class Arena:
    def __init__(self, P, name, t, nbytes):
        self.P, self.name, self.t, self.nbytes = P, name, t, nbytes
        self.top = 0
        self.regions = []
        self.old = []
        self.nrid = 0

    def reset(self):
        self.old.extend(self.regions)
        self.regions = []
        self.top = 0

    def mark(self):
        return (self.top, len(self.regions))

    def release(self, mk):
        top, nreg = mk
        self.old.extend(self.regions[nreg:])
        self.regions = self.regions[:nreg]
        self.top = top

    def region_of(self, lo, hi):
        for (a, b, rid) in self.regions:
            if lo >= a and hi <= b:
                return rid
        raise AssertionError("arena access outside any region %s %d %d" % (self.name, lo, hi))

    def alloc(self, shape, dtype=F32):
        n = int(np.prod(shape)) * ESZ[dtype]
        n = (n + 63) // 64 * 64
        lo, hi = self.top, self.top + n
        assert hi <= self.nbytes, ("arena overflow", self.name, hi, self.nbytes)
        self.top = hi
        rid = self.nrid
        self.nrid += 1
        self.regions.append((lo, hi, rid))
        inh = []
        keep = []
        for (a, b, orid) in self.old:
            if a < hi and lo < b:
                t = self.P.trk.get((self.name, orid))
                if t:
                    inh.extend(t['w'])
                    inh.extend(t['r'])
                keep.append((a, b, orid))
            else:
                keep.append((a, b, orid))
        self.old = keep
        if inh:
            full = ((self.name, rid), 0, 128, lo, hi)
            m = Prog._merge([(full, ev, clk) for (_, ev, clk) in inh])
            self.P.trk[(self.name, rid)] = {'w': m, 'r': []}
        v = self.t[:, lo // 4:hi // 4]
        if dtype != F32:
            v = v.bitcast(dtype)
        nel = int(np.prod(shape))
        v = v[:, 0:nel]
        if len(shape) == 2:
            v = v.rearrange("p (a b) -> p a b", a=shape[0])
        elif len(shape) == 3:
            v = v.rearrange("p (a b c) -> p a b c", a=shape[0], b=shape[1])
        return v


def _launch(build, in_maps, n_cores=8, trace=False):
    nc = bass.Bass("TRN2", target_bir_lowering=False)
    with ExitStack() as stack:
        P = Prog(nc, stack)
        build(nc, P)
        P.finish()
        P.emit()
    res = run_bass_kernel_spmd(nc, in_maps, core_ids=list(range(n_cores)), trace=trace)
    if trace:
        return res
    return res.results

D = 1024
SEQ = 8192
CTXL = 256
NTOK = SEQ + CTXL
QT = 2112
EPS = 1e-6
TILES_Q = [(0, 64, 1), (64, 512, 0), (576, 512, 0), (1088, 512, 0), (1600, 512, 0)]


class Banks:
    def __init__(self, P):
        self.pp = [P.psum("pp%d" % i, [128, 1024], F32) for i in range(4)]
        self.b = [self.pp[i // 2][:, (i % 2) * 512:(i % 2 + 1) * 512] for i in range(8)]


def load_consts_tok(nc, P):
    ones = P.sbuf("ones128", [128, 128], F32)
    P.memset('pool', ones[:], 1.0)
    return {'ones': ones}


def compute_mod(nc, P, K, banks, cvec_d, wmod_d, bmod_d, fc_list, modT):
    cT = P.sbuf("cT", [128, 8, 2], F32)
    P.dma('sp', cT[:], cvec_d)
    e = P.sbuf("cTe", [128, 8, 2], F32)
    P.act(e[:], cT[:], AF.Exp, scale=-1.0)
    P.ts('dve', e[:], e[:], 1.0, None, ALU.add)
    P.recip(e[:], e[:])
    sc = P.sbuf("cTs", [128, 8, 2], F32)
    P.tt('dve', sc[:], cT[:], e[:], ALU.mult)
    bT = P.sbuf("bmodT", [128, 24], F32)
    P.dma('sp', bT[:], bmod_d)
    wbuf = [P.sbuf("wmodbuf%d" % i, [128, 8, 512], F32) for i in range(2)]
    ps = banks.b[7]
    groups = sorted(set(fc // 4 for fc in fc_list))
    for gi, g in enumerate(groups):
        wb = wbuf[gi % 2]
        for kc in range(8):
            P.dma('sp' if kc % 2 == 0 else 'act', wb[:, kc, :], wmod_d[kc * 128:(kc + 1) * 128, g * 512:(g + 1) * 512])
        for fc in range(g * 4, g * 4 + 4):
            if fc not in fc_list:
                continue
            for kc in range(8):
                P.mm(ps[:, fc * 2:fc * 2 + 2], wb[:, kc, (fc % 4) * 128:(fc % 4 + 1) * 128], sc[:, kc, :],
                     start=(kc == 0), stop=(kc == 7))
    for fc in fc_list:
        P.tt('dve', modT[:, fc, :], ps[:, fc * 2:fc * 2 + 2], bT[:, fc:fc + 1].to_broadcast([128, 2]), ALU.add)


def norm_coeffs(nc, P, modT, normw_d, name):
    nw = P.sbuf(name + "_nw", [128, 8], F32)
    P.dma('sp', nw[:], normw_d)
    A = P.sbuf(name + "_A", [128, 8, 2], F32)
    P.ts('dve', A[:], modT[:, 8:16, :], 1.0, None, ALU.add)
    P.tt('dve', A[:], A[:], nw[:].unsqueeze(2).to_broadcast([128, 8, 2]), ALU.mult)
    return A


def phase_norm(nc, P, K, banks, RT, A, Bsh, xn_d, tiles, tmp):
    xn_v = xn_d.rearrange("(kc p) t -> p kc t", p=128)
    for ti, (c0, n, isctx) in enumerate(tiles):
        j = 1 if isctx else 0
        ps = banks.b[ti % 2]
        for kc in range(8):
            sq = tmp['sq'][kc % 2]
            P.act(sq[:, 0:n], RT[:, kc, c0:c0 + n], AF.Square)
            P.mm(ps[:, 0:n], K['ones'][:], sq[:, 0:n], start=(kc == 0), stop=(kc == 7))
        rstd = tmp['rstd'][ti % 2]
        P.act(rstd[:, 0:n], ps[:, 0:n], AF.Ln, scale=1.0 / D, bias=K['eps'][:])
        P.act(rstd[:, 0:n], rstd[:, 0:n], AF.Exp, scale=-0.5)
        xb = tmp['xb'][ti % 2]
        for kc in range(8):
            t1 = tmp['t1'][kc % 2]
            P.stt(t1[:, 0:n], RT[:, kc, c0:c0 + n], A[:, kc, j:j + 1], rstd[:, 0:n], ALU.mult, ALU.mult)
            P.act(xb[:, kc, 0:n], t1[:, 0:n], AF.Identity, bias=Bsh[:, kc, j:j + 1])
        P.dma('pool', xn_v[:, :, c0:c0 + n], xb[:, :, 0:n], final=True)


def phase_final(nc, P, K, banks, RT, fnw_d, out_d, tiles, tmp):
    fw = P.sbuf("fnw_sb", [128, 8], F32)
    P.dma('sp', fw[:], fnw_d)
    out_v = out_d.rearrange("(kc p) t -> p kc t", p=128)
    for ti, (c0, n, isctx) in enumerate(tiles):
        ps = banks.b[ti % 2]
        for kc in range(8):
            sq = tmp['sq'][kc % 2]
            P.act(sq[:, 0:n], RT[:, kc, c0:c0 + n], AF.Square)
            P.mm(ps[:, 0:n], K['ones'][:], sq[:, 0:n], start=(kc == 0), stop=(kc == 7))
        rstd = tmp['rstd'][ti % 2]
        P.act(rstd[:, 0:n], ps[:, 0:n], AF.Ln, scale=1.0 / D, bias=K['eps'][:])
        P.act(rstd[:, 0:n], rstd[:, 0:n], AF.Exp, scale=-0.5)
        for kc in range(8):
            P.stt(RT[:, kc, c0:c0 + n], RT[:, kc, c0:c0 + n], fw[:, kc:kc + 1], rstd[:, 0:n], ALU.mult, ALU.mult)
        P.dma('pool', out_v[:, :, c0 - 64:c0 - 64 + n], RT[:, :, c0:c0 + n], final=True)


def phase_outproj(nc, P, K, banks, RT, yT_d, wout_d, ssdw_d, gateT, tiles, tmp):
    wo = P.sbuf("wout_bf", [128, 8, D], BF16)
    for kc in range(8):
        st = tmp['wstage'][kc % 2]
        P.dma('sp' if kc % 2 == 0 else 'act', st[:], wout_d[kc * 128:(kc + 1) * 128, :])
        P.copy('pool', wo[:, kc, :], st[:])
    sw = P.sbuf("ssdnw_sb", [128, 2], F32)
    P.dma('sp', sw[:], ssdw_d)
    yT_v = yT_d.rearrange("(kc p) t -> p kc t", p=128)
    for ti, (c0, n, isctx) in enumerate(tiles):
        j = 1 if isctx else 0
        yb = tmp['yb'][ti % 2]
        P.dma('sp', yb[:, :, 0:n], yT_v[:, :, c0:c0 + n])
        ps = banks.b[2 + ti % 2]
        for i, kc in enumerate((6, 7)):
            sq = tmp['sq'][i]
            P.act(sq[:, 0:n], yb[:, kc, 0:n], AF.Square)
            P.mm(ps[:, 0:n], K['ones'][:], sq[:, 0:n], start=(i == 0), stop=(i == 1))
        rstd = tmp['rstd'][ti % 2]
        P.act(rstd[:, 0:n], ps[:, 0:n], AF.Ln, scale=1.0 / 256, bias=K['eps'][:])
        P.act(rstd[:, 0:n], rstd[:, 0:n], AF.Exp, scale=-0.5)
        for i, kc in enumerate((6, 7)):
            P.stt(yb[:, kc, 0:n], yb[:, kc, 0:n], sw[:, i:i + 1], rstd[:, 0:n], ALU.mult, ALU.mult)
        for fc in range(8):
            po = banks.b[4 + fc % 4]
            for kc in range(8):
                P.mm(po[:, 0:n], wo[:, kc, fc * 128:(fc + 1) * 128], yb[:, kc, 0:n], start=(kc == 0), stop=(kc == 7))
            P.stt(RT[:, fc, c0:c0 + n], po[:, 0:n], gateT[:, 16 + fc, j:j + 1], RT[:, fc, c0:c0 + n], ALU.mult, ALU.add)


def alloc_tok_tmp(P):
    return {
        'sq': [P.sbuf("t_sq%d" % i, [128, 512], F32) for i in range(2)],
        'rstd': [P.sbuf("t_rstd%d" % i, [128, 512], F32) for i in range(2)],
        't1': [P.sbuf("t_t1%d" % i, [128, 512], F32) for i in range(2)],
        'xb': [P.sbuf("t_xb%d" % i, [128, 8, 512], BF16) for i in range(2)],
    }


def build_tok_kernel(nc, P, first, last, with_outproj):
    banks = Banks(P)
    K = load_consts_tok(nc, P)
    epst = P.sbuf("eps_t", [128, 1], F32)
    P.memset('pool', epst[:], EPS)
    K['eps'] = epst
    tmp = alloc_tok_tmp(P)
    RT_d = nc.dram_tensor("RT", [D, QT], F32, kind="ExternalInput").ap()
    cvec_d = nc.dram_tensor("cvec", [128, 8, 2], F32, kind="ExternalInput").ap()
    RT = P.sbuf("RT_sb", [128, 8, QT], F32)
    RT_v = RT_d.rearrange("(kc p) t -> p kc t", p=128)
    for kc in range(8):
        P.dma('sp' if kc % 2 == 0 else 'act', RT[:, kc, :], RT_v[:, kc, :])
    modT = P.sbuf("modT", [128, 24, 2], F32)
    if with_outproj:
        wmodg_d = nc.dram_tensor("wmod_g", [D, 3 * D], F32, kind="ExternalInput").ap()
        bmodg_d = nc.dram_tensor("bmod_g", [128, 24], F32, kind="ExternalInput").ap()
        yT_d = nc.dram_tensor("yT", [D, QT], BF16, kind="ExternalInput").ap()
        wout_d = nc.dram_tensor("wout", [D, D], F32, kind="ExternalInput").ap()
        ssdw_d = nc.dram_tensor("ssdnw", [128, 2], F32, kind="ExternalInput").ap()
        tmp['wstage'] = [P.sbuf("t_wst%d" % i, [128, D], F32) for i in range(2)]
        tmp['yb'] = [P.sbuf("t_yb%d" % i, [128, 8, 512], BF16) for i in range(2)]
        gateT = P.sbuf("gateT", [128, 24, 2], F32)
        compute_mod(nc, P, K, banks, cvec_d, wmodg_d, bmodg_d, list(range(16, 24)), gateT)
        tiles = TILES_Q[1:] if last else TILES_Q
        phase_outproj(nc, P, K, banks, RT, yT_d, wout_d, ssdw_d, gateT, tiles, tmp)
    if last:
        fnw_d = nc.dram_tensor("fnw", [128, 8], F32, kind="ExternalInput").ap()
        out_d = nc.dram_tensor("outT", [D, 2048], F32, kind="ExternalOutput").ap()
        phase_final(nc, P, K, banks, RT, fnw_d, out_d, TILES_Q[1:], tmp)
    else:
        wmod_d = nc.dram_tensor("wmod_n", [D, 3 * D], F32, kind="ExternalInput").ap()
        bmod_d = nc.dram_tensor("bmod_n", [128, 24], F32, kind="ExternalInput").ap()
        normw_d = nc.dram_tensor("normw", [128, 8], F32, kind="ExternalInput").ap()
        xn_d = nc.dram_tensor("xnT", [D, QT], BF16, kind="ExternalOutput").ap()
        if with_outproj:
            Rout_d = nc.dram_tensor("RTout", [D, QT], F32, kind="ExternalOutput").ap()
        modT2 = modT
        _cm_second(nc, P, K, banks, cvec_d, wmod_d, bmod_d, list(range(0, 16)), modT2, with_outproj)
        A = norm_coeffs(nc, P, modT2, normw_d, "nc")
        phase_norm(nc, P, K, banks, RT, A, modT2, xn_d, TILES_Q, tmp)
        if with_outproj:
            Ro_v = Rout_d.rearrange("(kc p) t -> p kc t", p=128)
            for kc in range(8):
                P.dma('pool', Ro_v[:, kc, :], RT[:, kc, :], final=True)


_CM_STATE = {}


def _cm_second(nc, P, K, banks, cvec_d, wmod_d, bmod_d, fc_list, modT, second):
    if not second:
        return compute_mod(nc, P, K, banks, cvec_d, wmod_d, bmod_d, fc_list, modT)
    orig = P.sbuf

    def renamed(name, shape, dtype=F32):
        return orig(name + "_2", shape, dtype)
    P.sbuf = renamed
    try:
        compute_mod(nc, P, K, banks, cvec_d, wmod_d, bmod_d, fc_list, modT)
    finally:
        P.sbuf = orig


def pk(v):
    v = np.asarray(v)
    return np.ascontiguousarray(v.reshape(-1, 128).T)

TOK_TILES = [(0, 256)] + [(256 + 512 * i, 512) for i in range(16)]
SCALE_QK = 32.0 ** -0.5
QK_REP = 1
DEFER = True


def store_y(P, io, mi, t0, n, src):
    if 'y_store' in io:
        io['y_store'](mi, t0, n, src)
    else:
        P.dma('pool', io['yT'][mi, :, t0:t0 + n], src, final=True)


def load_weights_bf16(nc, P, A, w_d, ncols, stage):
    Wsb = A.alloc([8, ncols], BF16)
    for kc in range(8):
        st = stage[kc % 2]
        P.dma('sp' if kc % 2 == 0 else 'act', st[:, 0:ncols], w_d[kc * 128:(kc + 1) * 128, :])
        P.copy('dve', Wsb[:, kc, :], st[:, 0:ncols])
    return Wsb


def inproj(nc, P, banks, xn_v, Wsb, tiles, fm_groups, tm_group, xbufs, bank_ids=(0, 1, 2, 3), pre_tile=None):
    pending = []
    for ti, (t0, n) in enumerate(tiles):
        xb = xbufs[ti % 2]
        if callable(xn_v):
            xn_v(xb, t0, n)
        else:
            P.dma('sp', xb[:, :, 0:n], xn_v[:, :, t0:t0 + n])
        if pre_tile is not None:
            pre_tile(ti, t0, n)
        new_pending = []
        for gi, (c0, M, lat_only, fn) in enumerate(fm_groups):
            if lat_only and t0 < CTXL:
                continue
            ps = banks.b[bank_ids[gi % len(bank_ids)]]
            for kc in range(8):
                P.mm(ps[0:M, 0:n], Wsb[:, kc, c0:c0 + M], xb[:, kc, 0:n], start=(kc == 0), stop=(kc == 7))
            if fn is not None:
                r = fn(ps, t0, n, ti)
                if r is not None:
                    if DEFER:
                        new_pending.append(r)
                    else:
                        r()
        if tm_group is not None:
            c0, ncols, fn, tmbanks = tm_group
            for sub in range(n // 128):
                ps = banks.b[tmbanks[sub % len(tmbanks)]]
                for kc in range(8):
                    P.mm(ps[:, 0:ncols], xb[:, kc, sub * 128:(sub + 1) * 128], Wsb[:, kc, c0:c0 + ncols],
                         start=(kc == 0), stop=(kc == 7))
                fn(ps, t0 + sub * 128)
        for r in pending:
            r2 = r()
            if r2 is not None:
                new_pending.append(r2)
        pending = new_pending
    while pending:
        nxt = []
        for r in pending:
            r2 = r()
            if r2 is not None:
                nxt.append(r2)
        pending = nxt


def silu_evac(P, out_bf, ps_ap, tmp_e, M, n):
    P.act(tmp_e[0:M, 0:n], ps_ap, AF.Tanh, scale=0.5)
    P.stt(out_bf, tmp_e[0:M, 0:n], 1.0, ps_ap, ALU.add, ALU.mult)


def attention_phase(nc, P, A, banks, K, xn_v, io, layer, with_ctx):
    A.reset()
    lam_init = 0.8 - 0.6 * float(np.exp(-0.3 * layer))
    stage = [A.alloc([640], F32) for _ in range(2)]
    Wsb = load_weights_bf16(nc, P, A, io['w_attn'], 640, stage)
    xbufs = [A.alloc([8, 512], BF16) for _ in range(2)]
    Qa = A.alloc([NTOK], BF16)
    Ka = A.alloc([NTOK], BF16)
    Vt = A.alloc([66, 128], BF16)
    sg = A.alloc([NTOK], BF16)
    cosb = [A.alloc([512], F32) for _ in range(2)]
    sinb = [A.alloc([512], F32) for _ in range(2)]
    t1 = [A.alloc([512], F32) for _ in range(2)]
    t2 = [A.alloc([512], F32) for _ in range(2)]
    te = [A.alloc([512], F32) for _ in range(2)]
    P.memset('pool', Vt[:, :, 65:128], 0.0)
    P.memset('pool', Vt[:, :, 64:65], 1.0)

    def pre_tile(ti, t0, n):
        if t0 >= CTXL:
            P.dma('act', cosb[ti % 2][:], io['cosT'][:, t0 - CTXL:t0 - CTXL + n])
            P.dma('act', sinb[ti % 2][:], io['sinT'][:, t0 - CTXL:t0 - CTXL + n])

    state = {}

    def ev_plain(dst):
        def f(ps, t0, n, ti):
            if t0 < CTXL:
                P.copy('act', dst[:, t0:t0 + n], ps[:, 0:n])
            else:
                state['ps1'] = ps
        return f

    def ev_rot(dst):
        def f(ps, t0, n, ti):
            a = t1[ti % 2]
            b = t2[ti % 2]
            P.tt('dve', a[:, 0:n], state['ps1'][:, 0:n], cosb[ti % 2][:, 0:n], ALU.mult)
            P.tt('dve', b[:, 0:n], ps[:, 0:n], sinb[ti % 2][:, 0:n], ALU.mult)
            P.tt('pool', dst[:, t0:t0 + n], a[:, 0:n], b[:, 0:n], ALU.add)
        return f

    vT = [A.alloc([512], BF16) for _ in range(2)]

    def ev_gv(ps, t0, n, ti):
        silu_evac(P, sg[0:64, t0:t0 + n], ps[0:64, 0:n], te[ti % 2], 64, n)
        v_ = vT[ti % 2]
        P.copy('act', v_[64:128, 0:n], ps[64:128, 0:n])

        def rest():
            for sub in range(n // 128):
                pt = banks.b[6 + sub % 2].bitcast(BF16)
                P.transpose(pt[:, 0:64], v_[64:128, sub * 128:(sub + 1) * 128], K['identb'][64:128, 64:128])
                P.copy('dve', Vt[:, t0 // 128 + sub, 0:64], pt[:, 0:64])
        return rest

    fm = [(0, 128, False, ev_plain(Qa)), (128, 128, True, ev_rot(Qa)),
          (256, 128, False, ev_plain(Ka)), (384, 128, True, ev_rot(Ka)),
          (512, 128, False, ev_gv)]
    inproj(nc, P, banks, xn_v, Wsb, TOK_TILES, fm, None, xbufs, bank_ids=(0, 1, 2, 3, 4), pre_tile=pre_tile)

    lv = A.alloc([128], F32)
    P.dma('sp', lv[0:64, :], io['diff_lam'].partition_broadcast(64))
    pr = A.alloc([64], F32)
    s2 = A.alloc([2], F32)
    P.tt('dve', pr[0:64, 0:32], lv[0:64, 0:32], lv[0:64, 32:64], ALU.mult)
    P.tt('dve', pr[0:64, 32:64], lv[0:64, 64:96], lv[0:64, 96:128], ALU.mult)
    P.op('dve', lambda e: e.tensor_reduce(s2[0:64, 0:2], pr[0:64, :].rearrange("p (a b) -> p a b", a=2), AX.X, ALU.add),
         reads=[pr[0:64, :]], writes=[s2[0:64, 0:2]])
    P.act(s2[0:64, :], s2[0:64, :], AF.Exp)
    nlam = A.alloc([1], F32)
    P.tt('dve', nlam[0:64, :], s2[0:64, 1:2], s2[0:64, 0:1], ALU.subtract)
    P.ts('dve', nlam[0:64, :], nlam[0:64, :], -lam_init, None, ALU.add)
    subw = A.alloc([1], F32)
    P.dma('sp', subw[0:64, :], io['diff_subln_w'])
    P.ts('dve', subw[0:64, :], subw[0:64, :], 1.0 - lam_init, None, ALU.mult)

    Eb = [A.alloc([1024], BF16) for _ in range(4)]
    osb = [A.alloc([512], F32) for _ in range(2)]
    fz = [A.alloc([512], F32) for _ in range(4)]
    yb = [A.alloc([512], BF16) for _ in range(2)]
    qblocks = []
    if with_ctx:
        qblocks.append((0, 256, [0, 1]))
    for i in range(16):
        qblocks.append((256 + 512 * i, 512, list(range(66))))
    o_acc = [banks.b[6], banks.b[7]]
    for qi, (q0, nq, kbs) in enumerate(qblocks):
        def qk(kb):
            S = banks.pp[kb % 3]
            for rep in range(QK_REP):
                P.mm(S[:, 0:nq], Ka[0:32, kb * 128:(kb + 1) * 128], Qa[0:32, q0:q0 + nq])
                P.mm(S[:, 512:512 + nq], Ka[64:96, kb * 128:(kb + 1) * 128], Qa[64:96, q0:q0 + nq])
        for k_ in kbs[0:3]:
            qk(k_)
        for ki, kb in enumerate(kbs):
            S = banks.pp[kb % 3]
            E = Eb[ki % 4]
            if nq == 512:
                P.act(E[:, :], S[:, :], AF.Exp, scale=SCALE_QK)
            else:
                Ev = E.rearrange("p (a b) -> p a b", a=2)[:, :, 0:nq]
                Sv = S.rearrange("p (a b) -> p a b", a=2)[:, :, 0:nq]
                P.act(Ev, Sv, AF.Exp, scale=SCALE_QK)
            if ki + 3 < len(kbs):
                qk(kbs[ki + 3])
            for c in range(2):
                P.mm(o_acc[c][:, 0:nq], Vt[:, kb, :], E[:, c * 512:c * 512 + nq], start=(ki == 0), stop=(ki == len(kbs) - 1))
        for c in range(2):
            P.copy('act', osb[c][0:65, 0:nq], o_acc[c][0:65, 0:nq])
        zb = [banks.b[0], banks.b[1]]
        for c in range(2):
            P.mm(zb[c][0:64, 0:nq], K['selZ'][0:65, :], osb[c][0:65, 0:nq])
        for c in range(2):
            P.act(fz[c][0:64, 0:nq], zb[c][0:64, 0:nq], AF.Ln)
            P.act(fz[c][0:64, 0:nq], fz[c][0:64, 0:nq], AF.Exp, scale=-1.0)
            P.tt('dve', fz[c][0:64, 0:nq], fz[c][0:64, 0:nq], osb[c][0:64, 0:nq], ALU.mult)
        o = fz[2]
        P.stt(o[0:64, 0:nq], fz[1][0:64, 0:nq], nlam[0:64, 0:1], fz[0][0:64, 0:nq], ALU.mult, ALU.add)
        P.act(fz[3][0:64, 0:nq], o[0:64, 0:nq], AF.Square)
        P.mm(zb[0][0:64, 0:nq], K['ones'][0:64, 0:64], fz[3][0:64, 0:nq])
        P.act(fz[3][0:64, 0:nq], zb[0][0:64, 0:nq], AF.Ln, scale=1.0 / 64, bias=K['eps'][0:64, :])
        P.act(fz[3][0:64, 0:nq], fz[3][0:64, 0:nq], AF.Exp, scale=-0.5)
        P.stt(o[0:64, 0:nq], o[0:64, 0:nq], subw[0:64, 0:1], fz[3][0:64, 0:nq], ALU.mult, ALU.mult)
        y = yb[qi % 2]
        P.stt(y[0:64, 0:nq], o[0:64, 0:nq], 0.5, sg[0:64, q0:q0 + nq], ALU.mult, ALU.mult)
        store_y(P, io, 2, q0, nq, y[0:64, 0:nq])
        if 'after_q' in io:
            io['after_q'](q0, nq)


TP = 8456
PADC = 2
PADL = 260


def pcol(t0):
    return t0 + PADC if t0 < CTXL else t0 + (PADL - CTXL)


PTILES = [(PADC, 256)] + [(PADL + 512 * i, 512) for i in range(16)]


def conv4(P, dst, src, cw, cb, np_, c_lo, c_hi):
    n = c_hi - c_lo
    P.ts('dve', dst[0:np_, c_lo:c_hi], src[0:np_, c_lo - 2:c_hi - 2], cw[0:np_, 0:1], cb[0:np_, 0:1], ALU.mult, ALU.add)
    for k in range(1, 4):
        P.stt(dst[0:np_, c_lo:c_hi], src[0:np_, c_lo + k - 2:c_hi + k - 2], cw[0:np_, k:k + 1], dst[0:np_, c_lo:c_hi],
              ALU.mult, ALU.add)


def lru_phase(nc, P, A, banks, K, xn_v, io, layer, with_ctx):
    A.reset()
    stage = [A.alloc([192], F32) for _ in range(2)]
    Wsb = load_weights_bf16(nc, P, A, io['w_lru'], 192, stage)
    xbufs = [A.alloc([8, 512], BF16) for _ in range(2)]
    XA = A.alloc([TP], F32)
    XU = A.alloc([TP], F32)
    sg = A.alloc([NTOK], BF16)
    te = [A.alloc([512], F32) for _ in range(2)]
    par = A.alloc([16], F32)
    P.dma('sp', par[:, 0:9], io['lru_par'])
    cw, cb = par[:, 0:4], par[:, 4:5]
    nba, nbx = par[:, 9:10], par[:, 10:11]
    c8, c16 = par[:, 11:12], par[:, 12:13]
    P.ts('dve', nba, par[:, 5:6], 0.5, None, ALU.mult)
    P.ts('dve', nbx, par[:, 6:7], 0.5, None, ALU.mult)
    P.act(c8, par[:, 7:8], AF.Exp, scale=-1.0)
    P.act(c8, c8, AF.Ln, bias=K['one'][:, 0:1])
    P.ts('dve', c16, c8, -16.0, None, ALU.mult)
    P.ts('dve', c8, c8, -8.0, None, ALU.mult)
    gw32 = A.alloc([256], F32)
    P.dma('sp', gw32[0:64, :], io['lru_gw'])
    gw = A.alloc([256], BF16)
    P.copy('pool', gw[0:64, :], gw32[0:64, :])
    II = A.alloc([64], F32)
    P.copy('pool', II[0:64, :], K['ident'][0:64, 0:64])
    P.copy('pool', II[64:128, :], K['ident'][64:128, 64:128])
    P.memset('pool', XA[:, 0:PADC], 0.0)
    P.memset('pool', XA[:, PADC + CTXL:PADL], 0.0)
    P.memset('pool', XA[:, PADL + SEQ:TP], 0.0)

    def ev_x(ps, t0, n, ti):
        c = pcol(t0)
        P.copy('act', XA[:, c:c + n], ps[:, 0:n])

    def ev_gate(ps, t0, n, ti):
        silu_evac(P, sg[0:64, t0:t0 + n], ps[0:64, 0:n], te[ti % 2], 64, n)

    inproj(nc, P, banks, xn_v, Wsb, TOK_TILES, [(0, 128, False, ev_x), (128, 64, False, ev_gate)], None, xbufs,
           bank_ids=(0, 1, 2, 3))
    conv4(P, XU, XA, cw, cb, 128, PADC, PADC + CTXL)
    for i in range(4):
        conv4(P, XU, XA, cw, cb, 128, PADL + 2048 * i, PADL + 2048 * (i + 1))
    xcb = [A.alloc([512], BF16) for _ in range(2)]
    gr = [A.alloc([512], F32) for _ in range(2)]
    gi_ = [A.alloc([512], F32) for _ in range(2)]
    gs = [A.alloc([512], F32) for _ in range(2)]
    for ti, (c0, n) in enumerate(PTILES):
        xb = xcb[ti % 2]
        P.copy('pool', xb[0:64, 0:n], XU[0:64, c0:c0 + n])
        psr, psi = banks.b[(2 * ti) % 8], banks.b[(2 * ti + 1) % 8]
        P.mm(psr[:, 0:n], gw[0:64, 0:128], xb[0:64, 0:n])
        P.mm(psi[:, 0:n], gw[0:64, 128:256], xb[0:64, 0:n])
        r, ii, s = gr[ti % 2], gi_[ti % 2], gs[ti % 2]
        P.act(r[:, 0:n], psr[:, 0:n], AF.Tanh, scale=0.5, bias=nba)
        P.act(ii[:, 0:n], psi[:, 0:n], AF.Tanh, scale=0.5, bias=nbx)
        P.ts('dve', r[:, 0:n], r[:, 0:n], 0.5, 0.5, ALU.mult, ALU.add)
        P.ts('dve', ii[:, 0:n], ii[:, 0:n], 0.5, 0.5, ALU.mult, ALU.add)
        P.act(XA[:, c0:c0 + n], r[:, 0:n], AF.Exp, scale=c8)
        P.tt('dve', XU[:, c0:c0 + n], XU[:, c0:c0 + n], ii[:, 0:n], ALU.mult)
    for ti, (c0, n) in enumerate(PTILES):
        s = gs[ti % 2]
        P.tt('pool', s[:, 0:n], XA[:, c0:c0 + n], XA[:, c0:c0 + n], ALU.mult)
        P.act(s[:, 0:n], s[:, 0:n], AF.Ln, scale=-1.0, bias=K['one'][:, 0:1])
        P.act(s[:, 0:n], s[:, 0:n], AF.Exp, scale=0.5)
        P.tt('dve', XU[:, c0:c0 + n], XU[:, c0:c0 + n], s[:, 0:n], ALU.mult)
    fa, fu = XA[0:64], XU[0:64]
    ba_, bu = XA[64:128], XU[64:128]
    P.scan(fu[:, PADC:PADC + CTXL], fa[:, PADC:PADC + CTXL], fu[:, PADC:PADC + CTXL], 0.0)
    for i in range(4):
        lo = PADL + 2048 * i
        init = fu[:, PADC + CTXL - 1:PADC + CTXL] if i == 0 else fu[:, lo - 1:lo]
        P.scan(fu[:, lo:lo + 2048], fa[:, lo:lo + 2048], fu[:, lo:lo + 2048], init)
    P.scan(bu[:, PADC:PADC + CTXL][:, ::-1], ba_[:, PADC:PADC + CTXL][:, ::-1], bu[:, PADC:PADC + CTXL][:, ::-1], 0.0)
    for i in range(3, -1, -1):
        lo = PADL + 2048 * i
        init = bu[:, PADC:PADC + 1] if i == 3 else bu[:, lo + 2048:lo + 2049]
        P.scan(bu[:, lo:lo + 2048][:, ::-1], ba_[:, lo:lo + 2048][:, ::-1], bu[:, lo:lo + 2048][:, ::-1], init)
    yb = [A.alloc([512], BF16) for _ in range(2)]
    for ti, (t0, n) in enumerate(TOK_TILES):
        if t0 < CTXL and not with_ctx:
            continue
        c0 = pcol(t0)
        ps = banks.b[ti % 4]
        P.mm(ps[0:64, 0:n], II[:, :], XU[:, c0:c0 + n])
        y = yb[ti % 2]
        P.stt(y[0:64, 0:n], ps[0:64, 0:n], 0.5, sg[0:64, t0:t0 + n], ALU.mult, ALU.mult)
        store_y(P, io, 1, t0, n, y[0:64, 0:n])


def gla_phase(nc, P, A, banks, K, xn_v, io, layer, with_ctx):
    A.reset()
    NC_ = 132
    QG = A.alloc([NTOK], BF16)
    KG = A.alloc([NTOK], BF16)
    KDt = A.alloc([66, 64], BF16)
    Vt = A.alloc([66, 64], BF16)
    sg = A.alloc([NTOK], BF16)
    ST = A.alloc([NC_ + 2, 64], F32)
    STb = A.alloc([NC_ + 2, 64], BF16)
    DEC = A.alloc([NC_], F32)
    par = A.alloc([8], F32)
    P.dma('sp', par[0:64, 0:2], io['gla_par'])
    nb2 = par[0:64, 2:3]
    P.ts('dve', nb2, par[0:64, 0:1], -1.0, None, ALU.mult)
    lnsc = par[0:64, 3:4]
    P.memset('dve', lnsc, float(np.log(32.0 ** -0.5)))
    w2_32 = A.alloc([64], F32)
    P.dma('sp', w2_32[64:96, :], io['gla_w2bd'])
    w2b = A.alloc([64], BF16)
    P.copy('dve', w2b[64:96, :], w2_32[64:96, :])
    mk = A.mark()
    stage = [A.alloc([320], F32) for _ in range(2)]
    Wsb = load_weights_bf16(nc, P, A, io['w_gla'], 320, stage)
    xbufs = [A.alloc([8, 512], BF16) for _ in range(2)]
    te = [A.alloc([512], F32) for _ in range(2)]
    lrb = [A.alloc([512], BF16) for _ in range(2)]
    Lb = [A.alloc([512], F32) for _ in range(2)]
    Gb = [A.alloc([512], F32) for _ in range(2)]
    Eq = [A.alloc([512], F32) for _ in range(2)]
    Ek = [A.alloc([512], F32) for _ in range(2)]
    dG = [A.alloc([512], F32) for _ in range(2)]
    kdT = [A.alloc([512], BF16) for _ in range(2)]
    P.memset('dve', ST[0:64, 0:2, :], 0.0)
    P.memset('dve', ST[0:64, NC_:NC_ + 2, :], 0.0)
    st = {}

    qs = [A.alloc([512], F32) for _ in range(2)]
    ks = [A.alloc([512], F32) for _ in range(2)]

    def ev_q(ps, t0, n, ti):
        P.copy('act', qs[ti % 2][0:64, 0:n], ps[0:64, 0:n])

    vT = [A.alloc([512], BF16) for _ in range(2)]

    def ev_gv(ps, t0, n, ti):
        P.copy('dve', sg[0:64, t0:t0 + n], ps[0:64, 0:n])
        v_ = vT[ti % 2]
        P.copy('act', v_[64:128, 0:n], ps[64:128, 0:n])
        for sub in range(n // 128):
            pt = banks.b[5 + sub % 2].bitcast(BF16)
            P.transpose(pt[:, 64:128], v_[64:128, sub * 128:(sub + 1) * 128], K['identb'][64:128, 64:128])
            P.copy('dve', Vt[:, t0 // 128 + sub, :], pt[:, 64:128])

    def ev_lr(ps, t0, n, ti):
        i2 = ti % 2
        nch = n // 64
        c0 = t0 // 64
        P.copy('act', ks[ti % 2][0:64, 0:n], ps[0:64, 0:n])
        P.copy('act', lrb[i2][64:96, 0:n], ps[64:96, 0:n])

        def rest():
            pz = banks.b[7]
            P.mm(pz[0:64, 0:n], w2b[64:96, 0:64], lrb[i2][64:96, 0:n])
            L, G = Lb[i2], Gb[i2]
            P.act(L[0:64, 0:n], pz[0:64, 0:n], AF.Exp, scale=-1.0, bias=nb2)
            P.act(L[0:64, 0:n], L[0:64, 0:n], AF.Ln, bias=K['one'][0:64, 0:1])
            P.scan(G[0:32, 0:n], K['scanmask'][0:32, 0:n], L[0:32, 0:n], 0.0)
            P.scan(G[32:64, 0:n][:, ::-1], K['scanmask'][32:64, 0:n][:, ::-1], L[32:64, 0:n][:, ::-1], 0.0)
            P.act(Eq[i2][0:64, 0:n], G[0:64, 0:n], AF.Exp, scale=-1.0 / 16, bias=lnsc)
            P.act(Ek[i2][0:64, 0:n], G[0:64, 0:n], AF.Exp, scale=1.0 / 16)
            P.tt('dve', QG[0:64, t0:t0 + n], qs[i2][0:64, 0:n], Eq[i2][0:64, 0:n], ALU.mult)
            P.tt('dve', KG[0:64, t0:t0 + n], ks[i2][0:64, 0:n], Ek[i2][0:64, 0:n], ALU.mult)
            G3 = G[:, 0:n].rearrange("p (c s) -> p c s", s=64)
            d3 = dG[i2][:, 0:n].rearrange("p (c s) -> p c s", s=64)
            P.tt('dve', d3[0:32], G3[0:32, :, 63:64].to_broadcast([32, nch, 64]), G3[0:32], ALU.subtract)
            P.tt('dve', d3[32:64], G3[32:64, :, 0:1].to_broadcast([32, nch, 64]), G3[32:64], ALU.subtract)
            P.act(dG[i2][0:64, 0:n], dG[i2][0:64, 0:n], AF.Exp, scale=-1.0 / 16)
            P.tt('dve', kdT[i2][0:64, 0:n], ks[i2][0:64, 0:n], dG[i2][0:64, 0:n], ALU.mult)
            P.act(DEC[0:32, c0:c0 + nch], G3[0:32, :, 63], AF.Exp, scale=-1.0 / 16)
            P.act(DEC[32:64, c0:c0 + nch], G3[32:64, :, 0], AF.Exp, scale=-1.0 / 16)
            def rest2():
                for sub in range(n // 128):
                    pt = banks.b[5 + sub % 2].bitcast(BF16)
                    P.transpose(pt[:, 0:64], kdT[i2][0:64, sub * 128:(sub + 1) * 128], K['identb'][0:64, 0:64])
                    P.copy('dve', KDt[:, t0 // 128 + sub, :], pt[:, 0:64])
            return rest2
        return rest

    fm = [(0, 64, False, ev_q), (64, 96, False, ev_lr), (192, 128, False, ev_gv)]
    inproj(nc, P, banks, xn_v, Wsb, TOK_TILES, fm, None, xbufs, bank_ids=(0, 1, 2))
    for ti, (t0, n) in enumerate(TOK_TILES):
        P.act(te[ti % 2][0:64, 0:n], sg[0:64, t0:t0 + n], AF.Tanh, scale=0.5)
        P.stt(sg[0:64, t0:t0 + n], te[ti % 2][0:64, 0:n], 1.0, sg[0:64, t0:t0 + n], ALU.add, ALU.mult)

    for g0 in range(0, 66, 8):
        npair = min(8, 66 - g0)
        for i in range(npair):
            p = g0 + i
            for j in range(2):
                P.mm(banks.b[j][0:64, i * 64:(i + 1) * 64], KDt[64 * j:64 * j + 64, p, :], Vt[64 * j:64 * j + 64, p, :])
        for j in range(2):
            src = banks.b[j][:, 0:npair * 64].rearrange("p (c v) -> p c v", v=64)
            c0 = 2 * g0 + j
            P.copy('act', ST[0:32, c0 + 2:c0 + 1 + 2 * npair:2, :], src[0:32])
            P.copy('dve', ST[32:64, c0:c0 + 2 * npair - 1:2, :], src[32:64])
    fsteps = [(c + 2, c + 1, c) for c in range(1, NC_)]
    bsteps = [(c, c + 1, c) for c in (2, 1, 0)] + [(131, 0, 131)] + [(c, c + 1, c) for c in range(130, 4, -1)]
    for i in range(max(len(fsteps), len(bsteps))):
        if i < len(fsteps):
            o_, i_, d_ = fsteps[i]
            P.stt(ST[0:32, o_, :], ST[0:32, i_, :], DEC[0:32, d_:d_ + 1], ST[0:32, o_, :], ALU.mult, ALU.add)
        if i < len(bsteps):
            o_, i_, d_ = bsteps[i]
            P.stt(ST[32:64, o_, :], ST[32:64, i_, :], DEC[32:64, d_:d_ + 1], ST[32:64, o_, :], ALU.mult, ALU.add)
    P.copy('dve', ST[32:64, 132, :], ST[32:64, 0, :])
    P.copy('dve', STb[0:64], ST[0:64])

    A.release(mk)
    tm = [A.alloc([512], F32) for _ in range(2)]
    At = [A.alloc([512], BF16) for _ in range(2)]
    ob = [A.alloc([512], F32) for _ in range(2)]
    yb = [A.alloc([512], BF16) for _ in range(2)]
    tiles2 = [(ti, t0, n) for ti, (t0, n) in enumerate(TOK_TILES) if not (t0 < CTXL and not with_ctx)]

    def stageA(ti, t0, n):
        i2 = ti % 2
        npair = n // 128
        X, Y = banks.b[2], banks.b[3]
        for i in range(npair):
            cs = slice(t0 + 128 * i, t0 + 128 * (i + 1))
            P.mm(X[:, 128 * i:128 * (i + 1)], KG[0:32, cs], QG[0:32, cs])
            P.mm(Y[:, 128 * i:128 * (i + 1)], KG[32:64, cs], QG[32:64, cs])
        P.tt('dve', tm[i2][:, 0:n], X[:, 0:n], K['maskF'][:, 0:n], ALU.mult)
        P.tt('dve', At[i2][:, 0:n], Y[:, 0:n], K['maskB'][:, 0:n], ALU.mult)
        P.tt('pool', At[i2][:, 0:n], At[i2][:, 0:n], tm[i2][:, 0:n], ALU.add)

    def stageB(ti, t0, n):
        i2 = ti % 2
        npair = n // 128
        Z = banks.b[4 + i2]
        for i in range(npair):
            p = t0 // 128 + i
            P.mm(Z[0:64, 128 * i:128 * (i + 1)], Vt[:, p, :], At[i2][:, 128 * i:128 * (i + 1)], start=True, stop=False)
            for j in range(2):
                c = 2 * p + j
                cs = slice(t0 + 128 * i + 64 * j, t0 + 128 * i + 64 * (j + 1))
                kk = 32 if c == 3 else 64
                P.mm(Z[0:64, 128 * i + 64 * j:128 * i + 64 * (j + 1)], STb[0:kk, c + 1, :], QG[0:kk, cs],
                     start=False, stop=(j == 1))
        o = ob[i2]
        P.act(o[0:64, 0:n], Z[0:64, 0:n], AF.Square)

    def stageC(ti, t0, n):
        i2 = ti % 2
        Z = banks.b[4 + i2]
        o = ob[i2]
        zb = banks.b[6 + i2]
        P.mm(zb[0:64, 0:n], K['ones'][0:64, 0:64], o[0:64, 0:n])
        P.act(o[0:64, 0:n], zb[0:64, 0:n], AF.Ln, scale=1.0 / 64, bias=K['eps'][0:64, :])
        P.act(o[0:64, 0:n], o[0:64, 0:n], AF.Exp, scale=-0.5)
        P.stt(o[0:64, 0:n], Z[0:64, 0:n], par[0:64, 1:2], o[0:64, 0:n], ALU.mult, ALU.mult)
        y = yb[i2]
        P.stt(y[0:64, 0:n], o[0:64, 0:n], 0.5, sg[0:64, t0:t0 + n], ALU.mult, ALU.mult)
        store_y(P, io, 0, t0, n, y[0:64, 0:n])

    nt2 = len(tiles2)
    for it in range(nt2 + 1):
        if it < nt2:
            stageA(*tiles2[it])
        if it >= 1:
            stageB(*tiles2[it - 1])
            stageC(*tiles2[it - 1])


def ssd_phase(nc, P, A, banks, K, xn_v, io, layer, with_ctx):
    A.reset()
    NC_ = 132
    XB = A.alloc([TP], BF16)
    C2 = A.alloc([TP], BF16)
    sgz = A.alloc([NTOK], BF16)
    par = A.alloc([16], F32)
    P.dma('sp', par[:, 0:5], io['ssd_par'])
    cpar = A.alloc([2, 5], F32)
    P.dma('sp', cpar[:], io['ssd_cpar'])
    na = par[:, 5:7]
    P.act(na, par[:, 2:4], AF.Exp)
    P.ts('dve', na, na, -1.0, None, ALU.mult)
    dts_d = io['ssd_scr'][0:2]
    crow_d = io['ssd_scr'][2:4]
    clr_d = io['ssd_scr'][4:6]
    mk = A.mark()
    stage = [A.alloc([322], F32) for _ in range(2)]
    Wsb = load_weights_bf16(nc, P, A, io['w_ssd'], 322, stage)
    xbufs = [A.alloc([8, 512], BF16) for _ in range(2)]
    XR1 = A.alloc([TP], F32)
    XR2 = A.alloc([TP], F32)
    XC = A.alloc([2052], F32)
    te = [A.alloc([512], F32) for _ in range(2)]
    dtt = [A.alloc([512], F32) for _ in range(2)]
    for X in (XR1, XR2):
        P.memset('pool', X[:, 0:PADC], 0.0)
        P.memset('pool', X[:, PADC + CTXL:PADL], 0.0)
        P.memset('pool', X[:, PADL + SEQ:TP], 0.0)

    def ev_raw(dst):
        def f(ps, t0, n, ti):
            c = pcol(t0)
            P.copy('act', dst[:, c:c + n], ps[:, 0:n])
        return f

    def ev_gate_dt(ps, t0, n, ti):
        silu_evac(P, sgz[0:64, t0:t0 + n], ps[0:64, 0:n], te[ti % 2], 64, n)
        P.copy('dve', dtt[ti % 2][64:66, 0:n], ps[64:66, 0:n])
        P.dma('pool', dts_d[:, t0:t0 + n], dtt[ti % 2][64:66, 0:n])

    fm = [(0, 128, False, ev_raw(XR1)), (128, 128, False, ev_raw(XR2)), (256, 66, False, ev_gate_dt)]
    inproj(nc, P, banks, xn_v, Wsb, TOK_TILES, fm, None, xbufs, bank_ids=(0, 1, 2, 3))
    pieces = [(PADC, PADC + CTXL)] + [(PADL + 2048 * i, PADL + 2048 * (i + 1)) for i in range(4)]
    for gi, (src, dst) in enumerate(((XR1, XB), (XR2, C2))):
        for (lo, hi) in pieces:
            n = hi - lo
            P.act(XC[:, 2:2 + n], src[:, lo - 2:hi - 2], AF.Identity, scale=cpar[:, gi, 0:1], bias=cpar[:, gi, 4:5])
            for k in range(1, 4):
                P.stt(XC[:, 2:2 + n], src[:, lo + k - 2:hi + k - 2], cpar[:, gi, k:k + 1], XC[:, 2:2 + n], ALU.mult, ALU.add)
            P.act(dst[:, lo:hi], XC[:, 2:2 + n], AF.Silu)
    A.release(mk)
    PM = A.alloc([2, 128], F32)
    for d in range(2):
        P.dma('sp', PM[0:66, d, :], dts_d[d].rearrange("(p s) -> p s", s=128))
    DT = A.alloc([2, 128], F32)
    for d in range(2):
        P.act(DT[0:66, d, :], PM[0:66, d, :], AF.Exp, bias=par[0:66, d:d + 1])
    P.act(DT[0:66], DT[0:66], AF.Ln, bias=K['one'][0:66, 0:1])
    DA = A.alloc([2, 128], F32)
    for d in range(2):
        P.ts('dve', DA[0:66, d, :], DT[0:66, d, :], na[0:66, d:d + 1], None, ALU.mult)
    CUM = A.alloc([2, 128], F32)
    P.scan(CUM[0:66, 0, :], K['scanmask2'][0:66, 0, :], DA[0:66, 0, :], 0.0)
    P.scan(CUM[0:66, 1, :][:, ::-1], K['scanmask2'][0:66, 1, :][:, ::-1], DA[0:66, 1, :][:, ::-1], 0.0)
    P.dma('pool', crow_d[0].rearrange("(p s) -> p s", s=128), CUM[0:66, 0, :])
    P.dma('pool', crow_d[1].rearrange("(p s) -> p s", s=128), CUM[0:66, 1, :])
    LND = A.alloc([2, 128], F32)
    P.act(LND[0:66], DT[0:66], AF.Ln)
    CQ0 = A.alloc([2, 128], F32)
    P.tt('dve', CQ0[0:66], CUM[0:66], LND[0:66], ALU.subtract)
    CL = A.alloc([2, 2], F32)
    C4 = CUM[0:66].rearrange("p d (c s) -> p d c s", s=64)
    P.copy('dve', CL[0:66, 0, :], C4[:, 0, :, 63])
    P.copy('dve', CL[0:66, 1, :], C4[:, 1, :, 0])
    P.dma('pool', clr_d[0, 0:132].rearrange("(p c) -> p c", c=2), CL[0:66, 0, :])
    P.dma('pool', clr_d[1, 0:132].rearrange("(p c) -> p c", c=2), CL[0:66, 1, :])
    W0 = A.alloc([2, 128], F32)
    W4 = W0[0:66].rearrange("p d (c s) -> p d c s", s=64)
    P.tt('dve', W4, CL[0:66].unsqueeze(3).to_broadcast([66, 2, 2, 64]), C4, ALU.subtract)
    P.act(W0[0:66], W0[0:66], AF.Exp)
    P.tt('dve', W0[0:66], W0[0:66], DT[0:66], ALU.mult)
    CQ = A.alloc([2, 66], F32)
    WT = A.alloc([2, 66], F32)
    for d in range(2):
        for (src, dst) in ((CQ0, CQ), (W0, WT)):
            pt = banks.b[(2 * d) % 8 + (0 if src is CQ0 else 1)]
            P.transpose(pt[:, 0:66], src[0:66, d, :], K['ident'][0:66, 0:66])
            P.copy('act', dst[:, d, :], pt[:, 0:66])
    clr = A.alloc([132], F32)
    P.dma('sp', clr[0:2, :], clr_d[:, 0:132])
    DEC = A.alloc([132], F32)
    pd = banks.b[4]
    P.mm(pd[:, 0:132], K['sel2'][0:2, :], clr[0:2, :])
    P.act(DEC[:, :], pd[:, 0:132], AF.Exp)
    xBt = A.alloc([66, 128], BF16)
    for p in range(66):
        c0 = pcol(128 * p)
        pt = banks.b[p % 4].bitcast(BF16)
        P.transpose(pt[:, 0:128], XB[:, c0:c0 + 128], K['identb'][:, :])
        P.copy('act' if p % 2 == 0 else 'dve', xBt[:, p, :], pt[:, 0:128])
    BW = A.alloc([66, 128], BF16)
    for d in range(2):
        P.tt('dve', BW[:, :, 64 * d:64 * d + 64], xBt[:, :, 64:128], WT[:, d, :].unsqueeze(2).to_broadcast([128, 66, 64]), ALU.mult)
    ST = A.alloc([NC_ + 2, 64], F32)
    STb = A.alloc([NC_ + 2, 64], BF16)
    P.memset('pool', ST[:, 0:2, :], 0.0)
    P.memset('pool', ST[:, NC_:NC_ + 2, :], 0.0)
    for g0 in range(0, 66, 8):
        npair = min(8, 66 - g0)
        for i in range(npair):
            p = g0 + i
            for j in range(2):
                P.mm(banks.b[j][:, i * 64:(i + 1) * 64], BW[64 * j:64 * j + 64, p, :], xBt[64 * j:64 * j + 64, p, 0:64])
        for j in range(2):
            src = banks.b[j][:, 0:npair * 64].rearrange("p (c v) -> p c v", v=64)
            c0 = 2 * g0 + j
            P.copy('act', ST[0:64, c0 + 2:c0 + 1 + 2 * npair:2, :], src[0:64])
            P.copy('dve', ST[64:128, c0:c0 + 2 * npair - 1:2, :], src[64:128])
    fsteps = [(c + 2, c + 1, c) for c in range(1, NC_)]
    bsteps = [(c, c + 1, c) for c in (2, 1, 0)] + [(131, 0, 131)] + [(c, c + 1, c) for c in range(130, 4, -1)]
    for i in range(max(len(fsteps), len(bsteps))):
        if i < len(fsteps):
            o_, i_, d_ = fsteps[i]
            P.stt(ST[0:64, o_, :], ST[0:64, i_, :], DEC[0:64, d_:d_ + 1], ST[0:64, o_, :], ALU.mult, ALU.add)
        if i < len(bsteps):
            o_, i_, d_ = bsteps[i]
            P.stt(ST[64:128, o_, :], ST[64:128, i_, :], DEC[64:128, d_:d_ + 1], ST[64:128, o_, :], ALU.mult, ALU.add)
    P.copy('dve', ST[64:128, 132, :], ST[64:128, 0, :])
    P.copy('dve', STb[:], ST[:])
    crt = [A.alloc([512], F32) for _ in range(2)]
    SG = [A.alloc([2, 512], F32) for _ in range(2)]
    Ls = [A.alloc([512], F32) for _ in range(2)]
    Mt = [A.alloc([512], BF16) for _ in range(2)]
    ec = [A.alloc([512], F32) for _ in range(2)]
    Cs = [A.alloc([512], BF16) for _ in range(2)]
    yv = [A.alloc([512], F32) for _ in range(2)]
    yb = [A.alloc([512], BF16) for _ in range(2)]
    tiles2 = [(ti, t0, n) for ti, (t0, n) in enumerate(TOK_TILES) if not (t0 < CTXL and not with_ctx)]

    def stageA(ti, t0, n):
        i2 = ti % 2
        npair = n // 128
        p0 = t0 // 128
        pc0 = pcol(t0)
        cr = crt[i2]
        P.dma('sp', cr[0:2, 0:n], crow_d[:, t0:t0 + n])
        pe_ = banks.b[0]
        P.mm(pe_[:, 0:n], K['sel2'][0:2, :], cr[0:2, 0:n])
        P.act(ec[i2][:, 0:n], pe_[:, 0:n], AF.Exp)
        P.tt('dve', Cs[i2][:, 0:n], C2[:, pc0:pc0 + n], ec[i2][:, 0:n], ALU.mult)
        for d in range(2):
            pb = banks.b[1 + d]
            P.mm(pb[:, 0:n], K['selrow'][0:2, d, :], cr[0:2, 0:n], start=True, stop=False)
            P.mm(pb[:, 0:n], K['identb'][:, :], K['negF' if d == 0 else 'negB'][:, 0:n], start=False, stop=True)
            P.tt('dve', SG[i2][:, d, 0:n].rearrange("p (a b) -> p a b", b=128),
                 pb[:, 0:n].rearrange("p (a b) -> p a b", b=128),
                 CQ[:, d, p0:p0 + npair].unsqueeze(2).to_broadcast([128, npair, 128]), ALU.subtract)
        P.act(SG[i2][:, :, 0:n], SG[i2][:, :, 0:n], AF.Exp)
        P.tt('pool', Ls[i2][:, 0:n], SG[i2][:, 0, 0:n], SG[i2][:, 1, 0:n], ALU.add)
        pcb = banks.b[3]
        for i in range(npair):
            cs = slice(pc0 + 128 * i, pc0 + 128 * (i + 1))
            P.mm(pcb[:, 128 * i:128 * (i + 1)], XB[64:128, cs], C2[64:128, cs])
        P.tt('dve', Mt[i2][:, 0:n], pcb[:, 0:n], Ls[i2][:, 0:n], ALU.mult)

    def stageB(ti, t0, n):
        i2 = ti % 2
        npair = n // 128
        p0 = t0 // 128
        pc0 = pcol(t0)
        Y = banks.b[4 + i2]
        for i in range(npair):
            p = p0 + i
            P.mm(Y[0:64, 128 * i:128 * (i + 1)], xBt[:, p, 0:64], Mt[i2][:, 128 * i:128 * (i + 1)], start=True, stop=False)
            for j in range(2):
                c = 2 * p + j
                kk = 64 if c == 3 else 128
                P.mm(Y[0:64, 128 * i + 64 * j:128 * i + 64 * (j + 1)], STb[0:kk, c + 1, :],
                     Cs[i2][0:kk, 128 * i + 64 * j:128 * i + 64 * (j + 1)], start=False, stop=(j == 1))
        P.stt(yv[i2][0:64, 0:n], XB[0:64, pc0:pc0 + n], par[0:64, 4:5], Y[0:64, 0:n], ALU.mult, ALU.add)
        y = yb[i2]
        P.stt(y[0:64, 0:n], yv[i2][0:64, 0:n], 0.5, sgz[0:64, t0:t0 + n], ALU.mult, ALU.mult)
        store_y(P, io, 3, t0, n, y[0:64, 0:n])

    nt2 = len(tiles2)
    for it in range(nt2 + 1):
        if it < nt2:
            stageA(*tiles2[it])
        if it >= 1:
            stageB(*tiles2[it - 1])

OFF_A, OFF_B, OFF_C, OFF_D = 0, 800, 1312, 2336


def rope_tables():
    n_freq = 8
    inv = (np.float32(10000.0) ** (-(np.arange(n_freq, dtype=np.float32)) / np.float32(n_freq))).astype(np.float32)
    t = np.arange(SEQ)
    pos_r = (t // 64).astype(np.float32)
    pos_c = (t % 64).astype(np.float32)
    ang_r = pos_r[:, None] * inv
    ang_c = pos_c[:, None] * inv
    ang = np.concatenate([ang_r, ang_r, ang_c, ang_c], axis=-1).astype(np.float32)
    cos = np.cos(ang).astype(np.float32).T
    sin = np.sin(ang).astype(np.float32).T
    sign = np.ones(32, np.float32)
    for a in range(2):
        sign[a * 16:a * 16 + 8] = -1.0
    sins = sin * sign[:, None]
    cosT = np.zeros((128, SEQ), np.float32)
    sinT = np.zeros((128, SEQ), np.float32)
    for c in range(2):
        cosT[64 * c:64 * c + 32] = cos
        sinT[64 * c:64 * c + 32] = sins
    return cosT, sinT


def rot_perm():
    perm = np.zeros(32, np.int64)
    for a in range(2):
        for f in range(8):
            perm[a * 16 + f] = a * 16 + 8 + f
            perm[a * 16 + 8 + f] = a * 16 + f
    return perm


def prep_w_attn(w_in_l, h):
    W = np.zeros((D, 640), np.float32)
    perm = rot_perm()
    for c in range(2):
        qc = OFF_C + h * 64 + c * 32
        kc = OFF_C + 256 + h * 64 + c * 32
        W[:, 64 * c:64 * c + 32] = w_in_l[:, qc:qc + 32]
        W[:, 128 + 64 * c:128 + 64 * c + 32] = w_in_l[:, qc + perm]
        W[:, 256 + 64 * c:256 + 64 * c + 32] = w_in_l[:, kc:kc + 32]
        W[:, 384 + 64 * c:384 + 64 * c + 32] = w_in_l[:, kc + perm]
    W[:, 512:576] = w_in_l[:, OFF_C + 768 + h * 64:OFF_C + 768 + h * 64 + 64]
    W[:, 576:640] = w_in_l[:, OFF_C + 512 + h * 64:OFF_C + 512 + h * 64 + 64]
    return W


def mix_consts():
    selZ = np.zeros((128, 64), np.float32)
    selZ[64, :] = 1.0
    t = np.arange(512)
    scanmask = np.zeros((64, 512), np.float32)
    scanmask[0:32] = (t % 64 != 0).astype(np.float32)[None]
    scanmask[32:64] = (t % 64 != 63).astype(np.float32)[None]
    j = np.arange(128)[:, None]
    i = np.arange(128)[None, :]
    same = (j // 64) == (i // 64)
    mF = (same & (j <= i)).astype(np.float32)
    mB = (same & (j >= i)).astype(np.float32)
    scanmask2 = np.zeros((128, 2, 128), np.float32)
    s_ = np.arange(128)
    scanmask2[:, 0, :] = (s_ % 64 != 0).astype(np.float32)[None]
    scanmask2[:, 1, :] = (s_ % 64 != 63).astype(np.float32)[None]
    sel2 = np.zeros((2, 128), np.float32)
    sel2[0, 0:64] = 1.0
    sel2[1, 64:128] = 1.0
    selrow = np.zeros((2, 2, 128), np.float32)
    selrow[0, 0, :] = 1.0
    selrow[1, 1, :] = 1.0
    NEG = -30000.0
    negF = np.where(mF > 0, 0.0, NEG).astype(np.float32)
    negB = np.where(mB > 0, 0.0, NEG).astype(np.float32)
    return {'selZ': selZ, 'ident': np.eye(128, dtype=np.float32), 'scanmask': scanmask,
            'maskF': np.tile(mF, (1, 4)), 'maskB': np.tile(mB, (1, 4)), 'scanmask2': scanmask2,
            'sel2': sel2, 'selrow': selrow, 'negF': np.tile(negF, (1, 4)), 'negB': np.tile(negB, (1, 4))}


def build_mix_kernel(nc, P, layer, with_ctx, mixers):
    banks = Banks(P)
    K = load_consts_tok(nc, P)
    epst = P.sbuf("eps_t", [128, 1], F32)
    P.memset('pool', epst[:], EPS)
    K['eps'] = epst
    selZ_d = nc.dram_tensor("selZ", [128, 64], F32, kind="ExternalInput").ap()
    selZ = P.sbuf("selZ_sb", [128, 64], F32)
    P.dma('sp', selZ[:], selZ_d)
    K['selZ'] = selZ
    ident_d = nc.dram_tensor("ident", [128, 128], F32, kind="ExternalInput").ap()
    ident = P.sbuf("ident_sb", [128, 128], F32)
    P.dma('sp', ident[:], ident_d)
    identb = P.sbuf("identb_sb", [128, 128], BF16)
    P.copy('pool', identb[:], ident[:])
    K['ident'] = ident
    K['identb'] = identb
    one = P.sbuf("one_t", [128, 1], F32)
    P.memset('pool', one[:], 1.0)
    K['one'] = one
    for nm, shp in (('scanmask', [64, 512]), ('maskF', [128, 512]), ('maskB', [128, 512]),
                    ('scanmask2', [128, 2, 128]), ('sel2', [2, 128]), ('selrow', [2, 2, 128])):
        d_ = nc.dram_tensor(nm, shp, F32, kind="ExternalInput").ap()
        t_ = P.sbuf(nm + "_sb", shp, F32)
        P.dma('sp', t_[:], d_)
        K[nm] = t_
    for nm in ('negF', 'negB'):
        d_ = nc.dram_tensor(nm, [128, 512], F32, kind="ExternalInput").ap()
        t_ = P.sbuf(nm + "_f", [128, 512], F32)
        P.dma('sp', t_[:], d_)
        tb_ = P.sbuf(nm + "_sb", [128, 512], BF16)
        P.copy('pool', tb_[:], t_[:])
        K[nm] = tb_
    io = {}
    xn_d = nc.dram_tensor("xn", [D, NTOK], BF16, kind="ExternalInput").ap()
    xn_v = xn_d.rearrange("(kc p) t -> p kc t", p=128)
    io['yT'] = nc.dram_tensor("yT", [4, 64, NTOK], BF16, kind="ExternalOutput").ap()
    A = P.arena("arena", 190 * 1024)
    if 'attn' in mixers:
        io['w_attn'] = nc.dram_tensor("w_attn", [D, 640], F32, kind="ExternalInput").ap()
        io['cosT'] = nc.dram_tensor("cosT", [128, SEQ], F32, kind="ExternalInput").ap()
        io['sinT'] = nc.dram_tensor("sinT", [128, SEQ], F32, kind="ExternalInput").ap()
        io['diff_lam'] = nc.dram_tensor("diff_lam", [128], F32, kind="ExternalInput").ap()
        io['diff_subln_w'] = nc.dram_tensor("diff_subln_w", [64, 1], F32, kind="ExternalInput").ap()
        attention_phase(nc, P, A, banks, K, xn_v, io, layer, with_ctx)
    if 'gla' in mixers:
        _add_gla(nc, P, A, banks, K, xn_v, io, layer, with_ctx)
    if 'ssd' in mixers:
        io['w_ssd'] = nc.dram_tensor("w_ssd", [D, 322], F32, kind="ExternalInput").ap()
        io['ssd_par'] = nc.dram_tensor("ssd_par", [128, 5], F32, kind="ExternalInput").ap()
        io['ssd_cpar'] = nc.dram_tensor("ssd_cpar", [128, 2, 5], F32, kind="ExternalInput").ap()
        io['ssd_scr'] = nc.dram_tensor("ssd_scr", [6, NTOK], F32, kind="Internal").ap()
        ssd_phase(nc, P, A, banks, K, xn_v, io, layer, with_ctx)
    if 'lru' in mixers:
        io['w_lru'] = nc.dram_tensor("w_lru", [D, 192], F32, kind="ExternalInput").ap()
        io['lru_par'] = nc.dram_tensor("lru_par", [128, 9], F32, kind="ExternalInput").ap()
        io['lru_gw'] = nc.dram_tensor("lru_gw", [64, 256], F32, kind="ExternalInput").ap()
        lru_phase(nc, P, A, banks, K, xn_v, io, layer, with_ctx)


def _add_gla(nc, P, A, banks, K, xn_v, io, layer, with_ctx):
    io['w_gla'] = nc.dram_tensor("w_gla", [D, 320], F32, kind="ExternalInput").ap()
    io['gla_par'] = nc.dram_tensor("gla_par", [64, 2], F32, kind="ExternalInput").ap()
    io['gla_w2bd'] = nc.dram_tensor("gla_w2bd", [32, 64], F32, kind="ExternalInput").ap()
    gla_phase(nc, P, A, banks, K, xn_v, io, layer, with_ctx)


def prep_gla(inp, l, h):
    w_in_l = inp['w_in'][l]
    W = np.zeros((D, 288), np.float32)
    q = w_in_l[:, OFF_A + h * 32:OFF_A + h * 32 + 32]
    k = w_in_l[:, OFF_A + 128 + h * 32:OFF_A + 128 + h * 32 + 32]
    W[:, 0:32] = q
    W[:, 32:64] = q
    W[:, 64:96] = k
    W[:, 96:128] = k
    W[:, 128:144] = w_in_l[:, OFF_A + 512:OFF_A + 528]
    W[:, 144:160] = w_in_l[:, OFF_A + 528:OFF_A + 544]
    W[:, 192:256] = w_in_l[:, OFF_A + 544 + h * 64:OFF_A + 544 + h * 64 + 64]
    W[:, 256:288] = 0
    Wv = w_in_l[:, OFF_A + 256 + h * 64:OFF_A + 256 + h * 64 + 64]
    W2 = np.zeros((D, 320), np.float32)
    W2[:, 0:256] = W[:, 0:256]
    W2[:, 256:320] = Wv
    par = np.zeros((64, 2), np.float32)
    par[0:32, 0] = inp['gla_b2'][l][0][h * 32:(h + 1) * 32]
    par[32:64, 0] = inp['gla_b2'][l][1][h * 32:(h + 1) * 32]
    par[:, 1] = inp['gla_norm_w'][l]
    w2bd = np.zeros((32, 64), np.float32)
    w2bd[0:16, 0:32] = inp['gla_w2'][l][0][:, h * 32:(h + 1) * 32]
    w2bd[16:32, 32:64] = inp['gla_w2'][l][1][:, h * 32:(h + 1) * 32]
    return {"w_gla": W2, "gla_par": par, "gla_w2bd": w2bd}


def prep_ssd(inp, l, h):
    w_in_l = inp['w_in'][l]
    gr = h // 2
    W = np.zeros((D, 322), np.float32)
    cx = slice(OFF_D + h * 64, OFF_D + h * 64 + 64)
    cB = slice(OFF_D + 256 + gr * 64, OFF_D + 256 + gr * 64 + 64)
    cC = slice(OFF_D + 384 + gr * 64, OFF_D + 384 + gr * 64 + 64)
    W[:, 0:64] = w_in_l[:, cx]
    W[:, 64:128] = w_in_l[:, cB]
    W[:, 128:192] = w_in_l[:, cC]
    W[:, 192:256] = w_in_l[:, cC]
    W[:, 256:320] = w_in_l[:, OFF_D + 520 + h * 64:OFF_D + 520 + h * 64 + 64]
    W[:, 320] = w_in_l[:, OFF_D + 512 + h]
    W[:, 321] = w_in_l[:, OFF_D + 516 + h]
    par = np.zeros((128, 5), np.float32)
    par[:, 0] = inp['ssd_dt_bias'][l][0][h]
    par[:, 1] = inp['ssd_dt_bias'][l][1][h]
    par[:, 2] = inp['ssd_a_log'][l][0][h]
    par[:, 3] = inp['ssd_a_log'][l][1][h]
    par[:, 4] = inp['ssd_d'][l][h]
    cw, cb = inp['ssd_conv_w'][l], inp['ssd_conv_b'][l]
    cpar = np.zeros((128, 2, 5), np.float32)
    ch = [np.r_[h * 64:h * 64 + 64, 256 + gr * 64:256 + gr * 64 + 64],
          np.r_[384 + gr * 64:384 + gr * 64 + 64, 384 + gr * 64:384 + gr * 64 + 64]]
    for g in range(2):
        cpar[:, g, 0:4] = cw[:, ch[g]].T
        cpar[:, g, 4] = cb[ch[g]]
    return {"w_ssd": W, "ssd_par": par, "ssd_cpar": cpar}


def prep_lru(inp, l, h):
    w_in_l = inp['w_in'][l]
    W = np.zeros((D, 192), np.float32)
    xs = w_in_l[:, OFF_B + h * 64:OFF_B + h * 64 + 64]
    W[:, 0:64] = xs
    W[:, 64:128] = xs
    W[:, 128:192] = w_in_l[:, OFF_B + 256 + h * 64:OFF_B + 256 + h * 64 + 64]
    sl = slice(h * 64, h * 64 + 64)
    par = np.zeros((128, 9), np.float32)
    for d in range(2):
        rows = slice(64 * d, 64 * d + 64)
        par[rows, 0:4] = inp['lru_conv_w'][l][:, sl].T
        par[rows, 4] = inp['lru_conv_b'][l][sl]
        par[rows, 5] = inp['lru_ba'][l][d][sl]
        par[rows, 6] = inp['lru_bx'][l][d][sl]
        par[rows, 7] = inp['lru_lam'][l][d][sl]
    gw = np.zeros((64, 256), np.float32)
    for d in range(2):
        gw[:, 64 * d:64 * d + 64] = inp['lru_wa'][l][d][h]
        gw[:, 128 + 64 * d:128 + 64 * d + 64] = inp['lru_wx'][l][d][h]
    return {"w_lru": W, "lru_par": par, "lru_gw": gw}


def mix_inputs(inp, l, h, mixers, cosT, sinT, cst):
    m = dict(cst)
    if 'ssd' in mixers:
        m.update(prep_ssd(inp, l, h))
    if 'gla' in mixers:
        m.update(prep_gla(inp, l, h))
    if 'attn' in mixers:
        m.update({"w_attn": prep_w_attn(inp['w_in'][l], h), "cosT": cosT, "sinT": sinT,
                  "diff_lam": np.ascontiguousarray(inp['diff_lam'][l].reshape(-1)),
                  "diff_subln_w": np.ascontiguousarray(inp['diff_subln_w'][l].reshape(64, 1))})
    if 'lru' in mixers:
        m.update(prep_lru(inp, l, h))
    return m


def _rt_of(hl, hc, b, q):
    return np.ascontiguousarray(np.concatenate([hc[b, 64 * q:64 * q + 64].T, hl[b, 2048 * q:2048 * (q + 1)].T], axis=1))


def _gather_cols(parts):
    return np.ascontiguousarray(np.concatenate([p[:, 0:64] for p in parts] + [p[:, 64:] for p in parts], axis=1))


def kernel_unfused(**inp):
    inp = {k: np.asarray(v) for k, v in inp.items()}
    x, ctx, c, c_ctx = inp['x'], inp['ctx'], inp['c'], inp['c_ctx']
    cosT, sinT = rope_tables()
    cst = mix_consts()
    cvecs = [np.ascontiguousarray(np.stack([pk(c[b]), pk(c_ctx)], axis=2)) for b in range(2)]
    maps = []
    for core in range(8):
        b, q = core // 4, core % 4
        maps.append({"RT": _rt_of(x, ctx, b, q), "cvec": cvecs[b], "wmod_n": inp['w_mod'][0],
                     "bmod_n": pk(inp['b_mod'][0]), "normw": pk(inp['norm_w'][0])})
    res = _launch(lambda nc, P: build_tok_kernel(nc, P, True, False, False), maps)
    RT = [m["RT"] for m in maps]
    xnT = [np.asarray(r["xnT"]) for r in res]
    out = None
    for l in range(2):
        last = (l == 1)
        xn_full = [_gather_cols(xnT[4 * b:4 * b + 4]) for b in range(2)]
        maps = []
        for core in range(8):
            b, h = core // 4, core % 4
            m = {"xn": xn_full[b]}
            m.update(mix_inputs(inp, l, h, ('gla', 'lru', 'attn', 'ssd'), cosT, sinT, cst))
            maps.append(m)
        res = _launch(lambda nc, P: build_mix_kernel(nc, P, l, not last, ('gla', 'lru', 'attn', 'ssd')), maps)
        yT = [np.asarray(r["yT"]) for r in res]
        maps = []
        for core in range(8):
            b, q = core // 4, core % 4
            yb = np.stack([yT[4 * b + h] for h in range(4)], axis=1).reshape(1024, NTOK)
            yq = np.ascontiguousarray(np.concatenate([yb[:, 64 * q:64 * q + 64],
                                                      yb[:, CTXL + 2048 * q:CTXL + 2048 * (q + 1)]], axis=1))
            m = {"RT": RT[core], "cvec": cvecs[b], "yT": yq, "wout": inp['w_out'][l], "ssdnw": pk(inp['ssd_norm_w'][l]),
                 "wmod_g": inp['w_mod'][l], "bmod_g": pk(inp['b_mod'][l])}
            if last:
                m["fnw"] = pk(inp['final_norm_w'])
            else:
                m.update({"wmod_n": inp['w_mod'][l + 1], "bmod_n": pk(inp['b_mod'][l + 1]), "normw": pk(inp['norm_w'][l + 1])})
            maps.append(m)
        res = _launch(lambda nc, P: build_tok_kernel(nc, P, False, last, True), maps)
        if last:
            out = np.zeros((2, SEQ, D), np.float32)
            for core in range(8):
                b, q = core // 4, core % 4
                out[b, 2048 * q:2048 * (q + 1), :] = np.asarray(res[core]["outT"]).T
        else:
            RT = [np.asarray(r["RTout"]) for r in res]
            xnT = [np.asarray(r["xnT"]) for r in res]
    return out

CHUNKS = [(0, 256)] + [(256 + 2048 * k, 2048) for k in range(4)]
GROUPS = [[0, 1, 2, 3], [4, 5, 6, 7]]


def fs_mod(nc, P, K, banks, A, cvec_d, wmod_d, bmod_d, name):
    cT = A.alloc([8, 2], F32)
    P.dma('sp', cT[:], cvec_d)
    e = A.alloc([8, 2], F32)
    P.act(e[:], cT[:], AF.Exp, scale=-1.0)
    P.ts('dve', e[:], e[:], 1.0, None, ALU.add)
    P.recip(e[:], e[:])
    sc = A.alloc([8, 2], F32)
    P.tt('dve', sc[:], cT[:], e[:], ALU.mult)
    bT = A.alloc([6], F32)
    P.dma('sp', bT[:], bmod_d)
    wb = A.alloc([8, 768], F32)
    for kc in range(8):
        P.dma('sp' if kc % 2 == 0 else 'act', wb[:, kc, :], wmod_d[kc * 128:(kc + 1) * 128, :])
    ps = banks.b[7]
    for fc in range(6):
        for kc in range(8):
            P.mm(ps[:, fc * 2:fc * 2 + 2], wb[:, kc, fc * 128:(fc + 1) * 128], sc[:, kc, :], start=(kc == 0), stop=(kc == 7))
    modT = P.sbuf(name, [128, 6, 2], F32)
    P.tt('dve', modT[:], ps[:, 0:12].rearrange("p (a b) -> p a b", b=2), bT[:].unsqueeze(2).to_broadcast([128, 6, 2]), ALU.add)
    return modT


def fs_token_phase(nc, P, K, banks, A, io, L, mode):
    A.reset()
    D_ = io['dram']
    first, last = mode == 'first', mode == 'last'
    if not first:
        gateT = fs_mod(nc, P, K, banks, A, io['cvec'], io['wmod'][L], io['bmod'][L], "modT_g%d" % L)
        wo32 = A.alloc([8, 256], F32)
        for kc in range(8):
            P.dma('sp' if kc % 2 == 0 else 'act', wo32[:, kc, :], io['wout'][L][kc * 128:(kc + 1) * 128, :])
        wo = A.alloc([8, 256], BF16)
        P.copy('dve', wo[:], wo32[:])
        sw = A.alloc([4], F32)
        P.dma('sp', sw[:], io['ssdnw'][L])
    if not last:
        LN = 0 if first else L + 1
        modN = fs_mod(nc, P, K, banks, A, io['cvec'], io['wmod'][LN], io['bmod'][LN], "modT_n%d" % LN)
        nw = A.alloc([2], F32)
        P.dma('sp', nw[:], io['normw'][LN])
        Acoef = A.alloc([2, 2], F32)
        P.ts('dve', Acoef[:], modN[:, 2:4, :], 1.0, None, ALU.add)
        P.tt('dve', Acoef[:], Acoef[:], nw[:].unsqueeze(2).to_broadcast([128, 2, 2]), ALU.mult)
    else:
        fw = A.alloc([2], F32)
        P.dma('sp', fw[:], io['fnw'])
    Rb = [A.alloc([2, 2048], F32) for _ in range(2)]
    yb_ = [A.alloc([8, 512], BF16) for _ in range(2)]
    sq = [A.alloc([512], BF16) for _ in range(4)]
    rst = [A.alloc([512], F32) for _ in range(2)]
    t1 = [A.alloc([512], F32) for _ in range(4)]
    ssr = [A.alloc([2048], F32) for _ in range(2)]
    ssg = [A.alloc([2048], F32) for _ in range(2)]
    xo = [A.alloc([2, 512], BF16) for _ in range(2)]
    Rsrc = io['RT_in'] if first else D_['Rs']
    Rsrc_v = Rsrc.rearrange("(fc p) t -> p fc t", p=128)
    Rs_v = D_['Rs'].rearrange("(fc p) t -> p fc t", p=128)
    stage = 'n%d' % (0 if first else L + 1) if not last else 'fin'
    chunks = [(ci, t0, n) for ci, (t0, n) in enumerate(CHUNKS) if not (last and ci == 0)]
    jobs = []
    for (ci, t0, n) in chunks:
        subs = [(s0, min(512, n - s0)) for s0 in range(0, n, 512)]
        for si, (s0, m_) in enumerate(subs):
            jobs.append(dict(ci=ci, t0=t0, n=n, s0=s0, m=m_, first=(si == 0), lastsub=(si == len(subs) - 1)))
    ybuf3 = yb_ + [A.alloc([8, 512], BF16)]

    def P1(j, ji):
        ci, t0, n, s0, m = j['ci'], j['t0'], j['n'], j['s0'], j['m']
        R = Rb[ci % 2]
        if j['first']:
            P.dma('sp', R[:, :, 0:n], Rsrc_v[:, :, t0:t0 + n])
        if first:
            return
        gy = D_['gy%d' % L][ci].rearrange("(kc p) t -> p kc t", p=128)
        yt = ybuf3[ji % 3]
        P.dma('act', yt[:, :, 0:m], gy[:, :, s0:s0 + m])
        ps = banks.b[0 + ji % 2]
        for i, kc in enumerate((1, 3, 5, 7)):
            s_ = sq[i % 2]
            P.act(s_[64:128, 0:m], yt[64:128, kc, 0:m], AF.Square)
            P.mm(ps[:, 0:m], K['onesb'][64:128, :], s_[64:128, 0:m], start=(i == 0), stop=(i == 3))
        r_ = rst[ji % 2]
        P.act(r_[:, 0:m], ps[:, 0:m], AF.Ln, scale=1.0 / 256, bias=K['eps'][:])
        P.act(r_[:, 0:m], r_[:, 0:m], AF.Exp, scale=-0.5)
        for i, kc in enumerate((1, 3, 5, 7)):
            P.stt(yt[64:128, kc, 0:m], yt[64:128, kc, 0:m], sw[64:128, i:i + 1], r_[64:128, 0:m], ALU.mult, ALU.mult)

    def P2(j, ji):
        if first:
            return
        ci, s0, m = j['ci'], j['s0'], j['m']
        isctx = 1 if ci == 0 else 0
        R = Rb[ci % 2]
        yt = ybuf3[ji % 3]
        for fc in range(2):
            po = banks.b[2 + (2 * ji + fc) % 4]
            for kc in range(8):
                P.mm(po[:, 0:m], wo[:, kc, fc * 128:(fc + 1) * 128], yt[:, kc, 0:m], start=(kc == 0), stop=(kc == 7))
            P.stt(R[:, fc, s0:s0 + m], po[:, 0:m], gateT[:, 4 + fc, isctx:isctx + 1], R[:, fc, s0:s0 + m], ALU.mult, ALU.add)

    def P3(j, ji):
        ci, t0, n, s0, m = j['ci'], j['t0'], j['n'], j['s0'], j['m']
        R = Rb[ci % 2]
        srow = ssr[ci % 2]
        pss = banks.b[6 + ji % 2]
        for fc in range(2):
            s_ = sq[2 + fc]
            P.act(s_[:, 0:m], R[:, fc, s0:s0 + m], AF.Square)
            P.mm(pss[:, 0:m], K['onesb'][:, :], s_[:, 0:m], start=(fc == 0), stop=(fc == 1))
        P.copy('dve', srow[0:1, s0:s0 + m], pss[0:1, 0:m])
        if j['lastsub']:
            P.dma('sp', D_['ssb_' + stage][:, t0:t0 + n], srow[0:1, 0:n])
            P.dma('sp', Rs_v[:, :, t0:t0 + n], R[:, :, 0:n])

    nj = len(jobs)
    for it in range(nj + 2):
        if it < nj:
            P1(jobs[it], it)
        if 0 <= it - 1 < nj:
            P2(jobs[it - 1], it - 1)
        if 0 <= it - 2 < nj:
            P3(jobs[it - 2], it - 2)
    P.collective("AllGather", [D_['ssb_' + stage]], [D_['ssg_' + stage]], GROUPS)
    for (ci, t0, n) in chunks:
        isctx = 1 if ci == 0 else 0
        R = Rb[ci % 2]
        P.dma('sp', R[:, :, 0:n], Rs_v[:, :, t0:t0 + n])
        subs = [(s0, min(512, n - s0)) for s0 in range(0, n, 512)]
        sg_ = ssg[ci % 2]
        P.dma('act', sg_[0:4, 0:n], D_['ssg_' + stage][:, t0:t0 + n])
        for si, (s0, m) in enumerate(subs):
            pt = banks.b[6 + si % 2]
            P.mm(pt[:, 0:m], K['ones'][0:4, :], sg_[0:4, s0:s0 + m])
            r_ = rst[si % 2]
            P.act(r_[:, 0:m], pt[:, 0:m], AF.Ln, scale=1.0 / D, bias=K['eps'][:])
            P.act(r_[:, 0:m], r_[:, 0:m], AF.Exp, scale=-0.5)
            if not last:
                x_ = xo[si % 2]
                for fc in range(2):
                    t_ = t1[fc]
                    P.stt(t_[:, 0:m], R[:, fc, s0:s0 + m], Acoef[:, fc, isctx:isctx + 1], r_[:, 0:m], ALU.mult, ALU.mult)
                    P.act(x_[:, fc, 0:m], t_[:, 0:m], AF.Identity, bias=modN[:, fc, isctx:isctx + 1])
                xb_v = D_['xnb_' + stage][ci].rearrange("(fc p) t -> p fc t", p=128)
                P.dma('sp', xb_v[:, :, s0:s0 + m], x_[:, :, 0:m])
            else:
                o_v = io['outT'].rearrange("(fc p) t -> p fc t", p=128)
                for fc in range(2):
                    t_ = t1[(si * 2 + fc) % 4]
                    P.stt(t_[:, 0:m], R[:, fc, s0:s0 + m], fw[:, fc:fc + 1], r_[:, 0:m], ALU.mult, ALU.mult)
                    P.dma('sp', o_v[:, fc, t0 - CTXL + s0:t0 - CTXL + s0 + m], t_[:, 0:m], final=True)
        if not last:
            P.collective("AllGather", [D_['xnb_' + stage][ci]], [D_['xng_' + stage][ci]], GROUPS)


def build_fused(nc, P):
    banks = Banks(P)
    K = load_consts_tok(nc, P)
    epst = P.sbuf("eps_t", [128, 1], F32)
    P.memset('pool', epst[:], EPS)
    K['eps'] = epst
    one = P.sbuf("one_t", [128, 1], F32)
    P.memset('pool', one[:], 1.0)
    K['one'] = one
    for nm, shp in (('selZ', [128, 64]), ('ident', [128, 128]), ('scanmask', [64, 512]), ('maskF', [128, 512]),
                    ('maskB', [128, 512]), ('scanmask2', [128, 2, 128]), ('sel2', [2, 128]), ('selrow', [2, 2, 128])):
        d_ = nc.dram_tensor(nm, shp, F32, kind="ExternalInput").ap()
        t_ = P.sbuf(nm + "_sb", shp, F32)
        P.dma('sp', t_[:], d_)
        K[nm] = t_
    identb = P.sbuf("identb_sb", [128, 128], BF16)
    P.copy('pool', identb[:], K['ident'][:])
    K['identb'] = identb
    onesb = P.sbuf("onesb_sb", [128, 128], BF16)
    P.memset('pool', onesb[:], 1.0)
    K['onesb'] = onesb
    for nm in ('negF', 'negB'):
        d_ = nc.dram_tensor(nm, [128, 512], F32, kind="ExternalInput").ap()
        t_ = P.sbuf(nm + "_f", [128, 512], F32)
        P.dma('sp', t_[:], d_)
        tb_ = P.sbuf(nm + "_sb", [128, 512], BF16)
        P.copy('pool', tb_[:], t_[:])
        K[nm] = tb_
    A = P.arena("arena", 184 * 1024)

    def din(name, shape, dt=F32):
        return nc.dram_tensor(name, list(shape), dt, kind="ExternalInput").ap()

    def dscr(name, shape, dt=F32):
        return nc.dram_tensor(name, list(shape), dt).ap()

    io = {'RT_in': din("RT", [256, NTOK]), 'cvec': din("cvec", [128, 8, 2]),
          'wmod': [din("wmod%d" % l, [D, 768]) for l in range(2)], 'bmod': [din("bmod%d" % l, [128, 6]) for l in range(2)],
          'normw': [din("normw%d" % l, [128, 2]) for l in range(2)], 'wout': [din("wout%d" % l, [D, 256]) for l in range(2)],
          'ssdnw': [din("ssdnw%d" % l, [128, 4]) for l in range(2)], 'fnw': din("fnw", [128, 2]),
          'outT': nc.dram_tensor("outT", [256, SEQ], F32, kind="ExternalOutput").ap()}
    Dm = {'Rs': dscr("Rs", [256, NTOK])}
    for stage in ('n0', 'n1', 'fin'):
        Dm['ssb_' + stage] = dscr("ssb_%s" % stage, [1, NTOK])
        Dm['ssg_' + stage] = dscr("ssg_%s" % stage, [4, NTOK])
    for stage in ('n0', 'n1'):
        Dm['xnb_' + stage] = [dscr("xnb_%s_%d" % (stage, ci), [256, n], BF16) for ci, (t0, n) in enumerate(CHUNKS)]
        Dm['xng_' + stage] = [dscr("xng_%s_%d" % (stage, ci), [D, n], BF16) for ci, (t0, n) in enumerate(CHUNKS)]
    for l in range(2):
        Dm['yb%d' % l] = [dscr("yb%d_%d" % (l, ci), [256, n], BF16) for ci, (t0, n) in enumerate(CHUNKS)]
        Dm['gy%d' % l] = [dscr("gy%d_%d" % (l, ci), [D, n], BF16) for ci, (t0, n) in enumerate(CHUNKS)]
    io['dram'] = Dm
    cosT = din("cosT", [128, SEQ])
    sinT = din("sinT", [128, SEQ])
    fs_token_phase(nc, P, K, banks, A, io, 0, 'first')
    for l in range(2):
        last = (l == 1)
        with_ctx = not last
        xng = Dm['xng_n%d' % l]

        def xn_load(xb, t0, n, xng=xng):
            if t0 < CTXL:
                P.dma('sp', xb[:, :, 0:n], xng[0].rearrange("(kc p) t -> p kc t", p=128)[:, :, t0:t0 + n])
            else:
                k = (t0 - CTXL) // 2048
                c0 = (t0 - CTXL) % 2048
                P.dma('sp', xb[:, :, 0:n], xng[k + 1].rearrange("(kc p) t -> p kc t", p=128)[:, :, c0:c0 + n])

        ybl = Dm['yb%d' % l]

        def y_store(mi, t0, n, src, ybl=ybl):
            if t0 < CTXL:
                P.dma('pool', ybl[0][mi * 64:(mi + 1) * 64, t0:t0 + n], src)
            else:
                k = (t0 - CTXL) // 2048
                c0 = (t0 - CTXL) % 2048
                P.dma('pool', ybl[k + 1][mi * 64:(mi + 1) * 64, c0:c0 + n], src)

        mio = {'y_store': y_store, 'cosT': cosT, 'sinT': sinT}
        mio['w_gla'] = din("w_gla%d" % l, [D, 320])
        mio['gla_par'] = din("gla_par%d" % l, [64, 2])
        mio['gla_w2bd'] = din("gla_w2bd%d" % l, [32, 64])
        mio['w_lru'] = din("w_lru%d" % l, [D, 192])
        mio['lru_par'] = din("lru_par%d" % l, [128, 9])
        mio['lru_gw'] = din("lru_gw%d" % l, [64, 256])
        mio['w_ssd'] = din("w_ssd%d" % l, [D, 322])
        mio['ssd_par'] = din("ssd_par%d" % l, [128, 5])
        mio['ssd_cpar'] = din("ssd_cpar%d" % l, [128, 2, 5])
        mio['ssd_scr'] = dscr("ssd_scr%d" % l, [6, NTOK])
        mio['w_attn'] = din("w_attn%d" % l, [D, 640])
        mio['diff_lam'] = din("diff_lam%d" % l, [128])
        mio['diff_subln_w'] = din("diff_subln_w%d" % l, [64, 1])
        gla_phase(nc, P, A, banks, K, xn_load, mio, l, with_ctx)
        lru_phase(nc, P, A, banks, K, xn_load, mio, l, with_ctx)
        ssd_phase(nc, P, A, banks, K, xn_load, mio, l, with_ctx)
        def after_q(q0, nq, l=l):
            if q0 < CTXL:
                P.collective("AllGather", [Dm['yb%d' % l][0]], [Dm['gy%d' % l][0]], GROUPS)
            elif (q0 - CTXL + nq) % 2048 == 0:
                k = (q0 - CTXL) // 2048
                P.collective("AllGather", [Dm['yb%d' % l][k + 1]], [Dm['gy%d' % l][k + 1]], GROUPS)
        mio['after_q'] = after_q
        attention_phase(nc, P, A, banks, K, xn_load, mio, l, with_ctx)
        fs_token_phase(nc, P, K, banks, A, io, l, 'last' if last else 'mid')


def fused_inputs(inp, core, cosT, sinT, cst):
    b, q = core // 4, core % 4
    h = q
    x, ctx, c, c_ctx = inp['x'], inp['ctx'], inp['c'], inp['c_ctx']
    fsl = slice(256 * q, 256 * q + 256)
    m = dict(cst)
    m['RT'] = np.ascontiguousarray(np.concatenate([ctx[b][:, fsl].T, x[b][:, fsl].T], axis=1))
    m['cvec'] = np.ascontiguousarray(np.stack([pk(c[b]), pk(c_ctx)], axis=2))
    m['cosT'] = cosT
    m['sinT'] = sinT
    m['fnw'] = pk(inp['final_norm_w'][fsl])
    perm = np.array([mi * 256 + hh * 64 + j for hh in range(4) for mi in range(4) for j in range(64)])
    for l in range(2):
        cols = np.r_[256 * q:256 * q + 256, 1024 + 256 * q:1024 + 256 * q + 256, 2048 + 256 * q:2048 + 256 * q + 256]
        m['wmod%d' % l] = np.ascontiguousarray(inp['w_mod'][l][:, cols])
        m['bmod%d' % l] = pk(inp['b_mod'][l][cols])
        m['normw%d' % l] = pk(inp['norm_w'][l][fsl])
        m['wout%d' % l] = np.ascontiguousarray(inp['w_out'][l][perm][:, fsl])
        sw = np.zeros((128, 4), np.float32)
        for i in range(4):
            sw[64:128, i] = inp['ssd_norm_w'][l][i * 64:(i + 1) * 64]
        m['ssdnw%d' % l] = sw
        mi_ = mix_inputs(inp, l, h, ('gla', 'lru', 'attn', 'ssd'), cosT, sinT, cst)
        for k_, v_ in mi_.items():
            if k_ in cst or k_ in ('cosT', 'sinT'):
                continue
            m[k_ + str(l)] = v_
    return m


def kernel(**inp):
    inp = {k: np.asarray(v) for k, v in inp.items()}
    cosT, sinT = rope_tables()
    cst = mix_consts()
    maps = [fused_inputs(inp, core, cosT, sinT, cst) for core in range(8)]
    res = _launch(build_fused, maps)
    out = np.zeros((2, SEQ, D), np.float32)
    for core in range(8):
        b, q = core // 4, core % 4
        out[b, :, 256 * q:256 * q + 256] = np.asarray(res[core]["outT"]).T
    return out
```

```python
import numpy as np
from contextlib import ExitStack
import concourse.bass as bass
import concourse.mybir as mybir
from concourse.bass_utils import run_bass_kernel_spmd

F32 = mybir.dt.float32
BF16 = mybir.dt.bfloat16
AF = mybir.ActivationFunctionType
ALU = mybir.AluOpType
AX = mybir.AxisListType
ENGS = ('pe', 'act', 'dve', 'pool', 'sp')
ESZ = {F32: 4, BF16: 2}


class Prog:
    def __init__(self, nc, stack, n_dma_sems=40):
        self.nc = nc
        self.stack = stack
        self.sem = {e: stack.enter_context(nc.semaphore('sem_' + e)) for e in ENGS}
        self.cnt = {e: 0 for e in ENGS}
        self.known = {e: {} for e in ENGS}
        self.stream = {e: [] for e in ENGS}
        self.dsem = [stack.enter_context(nc.semaphore('dsem%d' % i)) for i in range(n_dma_sems)]
        self.dcum = [0] * n_dma_sems
        self.drr = 0
        self.rows = {}
        self.trk = {}
        self.final_events = []
        self.nwaits = 0
        self.ndma = 0
        self.arenas = {}

    def sbuf(self, name, shape, dtype=F32):
        t = self.stack.enter_context(self.nc.sbuf_tensor(name, list(shape), dtype))
        self.rows[name] = int(np.prod(shape[1:])) * ESZ[dtype]
        return t

    def psum(self, name, shape, dtype=F32):
        t = self.stack.enter_context(self.nc.psum_tensor(name, list(shape), dtype))
        self.rows[name] = int(np.prod(shape[1:])) * ESZ[dtype]
        return t

    def arena(self, name, nbytes):
        t = self.sbuf(name, [128, nbytes // 4], F32)
        a = Arena(self, name, t, nbytes)
        self.arenas[name] = a
        return a

    def _box(self, ap):
        b = self._box0(ap)
        a = self.arenas.get(b[0])
        if a is not None:
            rid = a.region_of(b[3], b[4])
            return ((b[0], rid),) + b[1:]
        return b

    def _box0(self, ap):
        name = ap.tensor.name
        es = ESZ.get(ap.dtype, 4)
        off = int(ap.offset) * es
        pairs = ap.ap
        if name in self.rows:
            rs = self.rows[name]
            p0 = off // rs
            pst, pc = pairs[0]
            p1 = p0 + (pc if pst != 0 else 1)
            fo = off % rs
            rest = pairs[1:]
        else:
            p0, p1 = 0, 1
            fo = off
            rest = pairs
        lo = fo
        hi = fo
        for st, c in rest:
            d = st * (c - 1) * es
            if d < 0:
                lo += d
            else:
                hi += d
        return (name, p0, p1, lo, hi + es)

    @staticmethod
    def _ov(a, b):
        return a[1] < b[2] and b[1] < a[2] and a[3] < b[4] and b[3] < a[4]

    @staticmethod
    def _inside(a, b):
        return a[1] >= b[1] and a[2] <= b[2] and a[3] >= b[3] and a[4] <= b[4]

    def _deps(self, reads, writes):
        deps = []
        for ap in reads:
            b = self._box(ap)
            t = self.trk.get(b[0])
            if t:
                for (wb, ev, clk) in t['w']:
                    if self._ov(b, wb):
                        deps.append((ev, clk, 'raw'))
        for ap in writes:
            b = self._box(ap)
            t = self.trk.get(b[0])
            if t:
                for (wb, ev, clk) in t['w']:
                    if self._ov(b, wb):
                        deps.append((ev, clk, 'waw'))
                for (rb, ev, clk) in t['r']:
                    if self._ov(b, rb):
                        deps.append((ev, clk, 'war'))
        return deps

    @staticmethod
    def _merge(lst):
        merged = {}
        for (rb, rev, rclk) in lst:
            m = merged.get(rev[0])
            if m is None:
                merged[rev[0]] = (rb, rev, rclk)
            else:
                mb, mev, mclk = m
                nb = (rb[0], min(rb[1], mb[1]), max(rb[2], mb[2]), min(rb[3], mb[3]), max(rb[4], mb[4]))
                merged[rev[0]] = (nb, rev, rclk) if rev[1] > mev[1] else (nb, mev, mclk)
        return list(merged.values())

    def _record(self, reads, writes, ev, clk):
        for ap in reads:
            b = self._box(ap)
            t = self.trk.setdefault(b[0], {'w': [], 'r': []})
            rl = t['r']
            for i, (rb, rev, rclk) in enumerate(rl):
                if rev[0] == ev[0] and self._inside(rb, b):
                    rl[i] = (b, ev, clk)
                    break
            else:
                rl.append((b, ev, clk))
                if len(rl) > 40:
                    t['r'] = self._merge(rl)
        for ap in writes:
            b = self._box(ap)
            t = self.trk.setdefault(b[0], {'w': [], 'r': []})
            t['w'] = [e for e in t['w'] if not self._inside(e[0], b)]
            t['r'] = [e for e in t['r'] if not self._inside(e[0], b)]
            t['w'].append((b, ev, clk))
            if len(t['w']) > 40:
                t['w'] = self._merge(t['w'])

    def _waits_for(self, eng, deps):
        kn = self.known[eng]
        waits = {}
        for (ev, clk, kind) in deps:
            key, val = ev
            if key == eng:
                if eng == 'pe':
                    continue
                if eng in ('act', 'dve') and kind != 'raw':
                    continue
            if kn.get(key, 0) >= val:
                continue
            if waits.get(key, 0) < val:
                waits[key] = val
        for (ev, clk, kind) in deps:
            key, val = ev
            if key in waits and waits[key] >= val:
                for k2, v2 in clk.items():
                    if kn.get(k2, 0) < v2:
                        kn[k2] = v2
        for k, v in waits.items():
            if kn.get(k, 0) < v:
                kn[k] = v
        return list(waits.items())

    def op(self, eng, fn, reads=(), writes=()):
        deps = self._deps(reads, writes)
        waits = self._waits_for(eng, deps)
        self.cnt[eng] += 1
        ev = (eng, self.cnt[eng])
        clk = dict(self.known[eng])
        clk[eng] = self.cnt[eng]
        self.stream[eng].append((waits, fn, ('e', eng)))
        self._record(reads, writes, ev, clk)
        self.nwaits += len(waits)
        return ev

    def dma(self, q, out, in_, final=False, **kw):
        i = self.drr
        self.drr = (self.drr + 1) % len(self.dsem)
        deps = self._deps([in_], [out])
        key = ('d', i)
        deps.append(((key, self.dcum[i]), {}, 'raw'))
        waits = self._waits_for(q, deps)
        self.dcum[i] += 16
        ev = (key, self.dcum[i])
        clk = dict(self.known[q])
        clk[key] = self.dcum[i]
        self.stream[q].append((waits, (lambda e, out=out, in_=in_, kw=kw: e.dma_start(out=out, in_=in_, **kw)), ('d', i)))
        self._record([in_], [out], ev, clk)
        if final:
            self.final_events.append(ev)
        self.nwaits += len(waits)
        self.ndma += 1
        return ev

    def collective(self, kind, ins, outs, groups, q='pool'):
        if not hasattr(self, 'csem'):
            self.csem = []
            self.ccum = []
        self.csem.append(self.stack.enter_context(self.nc.semaphore('csem%d' % len(self.csem))))
        self.ccum.append(0)
        i = len(self.csem) - 1
        deps = self._deps(list(ins), list(outs))
        key = ('c', i)
        waits = self._waits_for(q, deps)
        self.ccum[i] += 1
        ev = (key, self.ccum[i])
        clk = dict(self.known[q])
        clk[key] = self.ccum[i]
        self.stream[q].append((waits, (lambda e: e.collective_compute(kind, ALU.bypass, replica_groups=groups,
                                                                      ins=[a.opt() for a in ins], outs=[a.opt() for a in outs])),
                               ('c', i)))
        self._record(list(ins), list(outs), ev, clk)
        self.nwaits += len(waits)
        return ev

    def finish(self, eng='sp'):
        self.stream[eng].append((list(self.final_events), None, None))

    def _semobj(self, key):
        if isinstance(key, tuple):
            return self.dsem[key[1]] if key[0] == 'd' else self.csem[key[1]]
        return self.sem[key]

    def emit(self):
        for e in ENGS:
            assert self.cnt[e] < 60000, (e, self.cnt[e])

        def replay(name, eng):
            for (waits, fn, inc) in self.stream[name]:
                for (key, val) in waits:
                    eng.wait_ge(self._semobj(key), val)
                if fn is None:
                    continue
                ins = fn(eng)
                if inc[0] == 'e':
                    ins.then_inc(self.sem[inc[1]], 1)
                elif inc[0] == 'c':
                    ins.then_inc(self.csem[inc[1]])
                else:
                    ins.then_inc(self.dsem[inc[1]], 16)

        with self.nc.Block() as block:
            @block.sync
            def _(e):
                replay('sp', e)

            @block.scalar
            def _(e):
                replay('act', e)

            @block.vector
            def _(e):
                replay('dve', e)

            @block.gpsimd
            def _(e):
                replay('pool', e)

            @block.tensor
            def _(e):
                replay('pe', e)

    def mm(self, out, lhsT, rhs, start=True, stop=True):
        return self.op('pe', lambda e: e.matmul(out, lhsT, rhs, start=start, stop=stop),
                       reads=[lhsT, rhs], writes=[out])

    def transpose(self, out, in_, ident):
        return self.op('pe', lambda e: e.transpose(out, in_, ident), reads=[in_, ident], writes=[out])

    def act(self, out, in_, func, bias=None, scale=None, accum_out=None):
        reads = [in_]
        kw = {}
        if bias is not None:
            kw['bias'] = bias
            if not isinstance(bias, (int, float)):
                reads.append(bias)
        if scale is not None:
            kw['scale'] = scale
            if not isinstance(scale, (int, float)):
                reads.append(scale)
        writes = [out]
        if accum_out is not None:
            kw['accum_out'] = accum_out
            writes.append(accum_out)
        return self.op('act', lambda e: e.activation(out, in_, func, **kw), reads=reads, writes=writes)

    def tt(self, eng, out, in0, in1, op):
        return self.op(eng, lambda e: e.tensor_tensor(out, in0, in1, op), reads=[in0, in1], writes=[out])

    def ts(self, eng, out, in0, s1, s2, op0, op1=None):
        reads = [in0]
        for s in (s1, s2):
            if s is not None and not isinstance(s, (int, float)):
                reads.append(s)
        if op1 is None:
            return self.op(eng, lambda e: e.tensor_scalar(out, in0, s1, None, op0), reads=reads, writes=[out])
        return self.op(eng, lambda e: e.tensor_scalar(out, in0, s1, s2, op0, op1), reads=reads, writes=[out])

    def stt(self, out, in0, scalar, in1, op0, op1):
        reads = [in0, in1]
        if not isinstance(scalar, (int, float)):
            reads.append(scalar)
        return self.op('dve', lambda e: e.scalar_tensor_tensor(out, in0, scalar, in1, op0, op1),
                       reads=reads, writes=[out])

    def copy(self, eng, out, in_):
        if eng == 'act':
            return self.op('act', lambda e: e.copy(out, in_), reads=[in_], writes=[out])
        return self.op(eng, lambda e: e.tensor_copy(out, in_), reads=[in_], writes=[out])

    def memset(self, eng, ap, val):
        return self.op(eng, lambda e: e.memset(ap, val), reads=[], writes=[ap])

    def scan(self, out, d0, d1, init, op0=ALU.mult, op1=ALU.add):
        reads = [d0, d1]
        if not isinstance(init, (int, float)):
            reads.append(init)
        return self.op('dve', lambda e: e.tensor_tensor_scan(out, d0, d1, init, op0, op1), reads=reads, writes=[out])

    def recip(self, out, in_):
        return self.op('dve', lambda e: e.reciprocal(out, in_), reads=[in_], writes=[out])


class Arena:
    def __init__(self, P, name, t, nbytes):
        self.P, self.name, self.t, self.nbytes = P, name, t, nbytes
        self.top = 0
        self.regions = []
        self.old = []
        self.nrid = 0

    def reset(self):
        self.old.extend(self.regions)
        self.regions = []
        self.top = 0

    def mark(self):
        return (self.top, len(self.regions))

    def release(self, mk):
        top, nreg = mk
        self.old.extend(self.regions[nreg:])
        self.regions = self.regions[:nreg]
        self.top = top

    def region_of(self, lo, hi):
        for (a, b, rid) in self.regions:
            if lo >= a and hi <= b:
                return rid
        raise AssertionError("arena access outside any region %s %d %d" % (self.name, lo, hi))

    def alloc(self, shape, dtype=F32):
        n = int(np.prod(shape)) * ESZ[dtype]
        n = (n + 63) // 64 * 64
        lo, hi = self.top, self.top + n
        assert hi <= self.nbytes, ("arena overflow", self.name, hi, self.nbytes)
        self.top = hi
        rid = self.nrid
        self.nrid += 1
        self.regions.append((lo, hi, rid))
        inh = []
        keep = []
        for (a, b, orid) in self.old:
            if a < hi and lo < b:
                t = self.P.trk.get((self.name, orid))
                if t:
                    inh.extend(t['w'])
                    inh.extend(t['r'])
                keep.append((a, b, orid))
            else:
                keep.append((a, b, orid))
        self.old = keep
        if inh:
            full = ((self.name, rid), 0, 128, lo, hi)
            m = Prog._merge([(full, ev, clk) for (_, ev, clk) in inh])
            self.P.trk[(self.name, rid)] = {'w': m, 'r': []}
        v = self.t[:, lo // 4:hi // 4]
        if dtype != F32:
            v = v.bitcast(dtype)
        nel = int(np.prod(shape))
        v = v[:, 0:nel]
        if len(shape) == 2:
            v = v.rearrange("p (a b) -> p a b", a=shape[0])
        elif len(shape) == 3:
            v = v.rearrange("p (a b c) -> p a b c", a=shape[0], b=shape[1])
        return v


def _launch(build, in_maps, n_cores=8, trace=False):
    nc = bass.Bass("TRN2", target_bir_lowering=False)
    with ExitStack() as stack:
        P = Prog(nc, stack)
        build(nc, P)
        P.finish()
        P.emit()
    res = run_bass_kernel_spmd(nc, in_maps, core_ids=list(range(n_cores)), trace=trace)
    if trace:
        return res
    return res.results

D = 1024
SEQ = 8192
CTXL = 256
NTOK = SEQ + CTXL
QT = 2112
EPS = 1e-6
TILES_Q = [(0, 64, 1), (64, 512, 0), (576, 512, 0), (1088, 512, 0), (1600, 512, 0)]


class Banks:
    def __init__(self, P):
        self.pp = [P.psum("pp%d" % i, [128, 1024], F32) for i in range(4)]
        self.b = [self.pp[i // 2][:, (i % 2) * 512:(i % 2 + 1) * 512] for i in range(8)]


def load_consts_tok(nc, P):
    ones = P.sbuf("ones128", [128, 128], F32)
    P.memset('pool', ones[:], 1.0)
    return {'ones': ones}


def compute_mod(nc, P, K, banks, cvec_d, wmod_d, bmod_d, fc_list, modT):
    cT = P.sbuf("cT", [128, 8, 2], F32)
    P.dma('sp', cT[:], cvec_d)
    e = P.sbuf("cTe", [128, 8, 2], F32)
    P.act(e[:], cT[:], AF.Exp, scale=-1.0)
    P.ts('dve', e[:], e[:], 1.0, None, ALU.add)
    P.recip(e[:], e[:])
    sc = P.sbuf("cTs", [128, 8, 2], F32)
    P.tt('dve', sc[:], cT[:], e[:], ALU.mult)
    bT = P.sbuf("bmodT", [128, 24], F32)
    P.dma('sp', bT[:], bmod_d)
    wbuf = [P.sbuf("wmodbuf%d" % i, [128, 8, 512], F32) for i in range(2)]
    ps = banks.b[7]
    groups = sorted(set(fc // 4 for fc in fc_list))
    for gi, g in enumerate(groups):
        wb = wbuf[gi % 2]
        for kc in range(8):
            P.dma('sp' if kc % 2 == 0 else 'act', wb[:, kc, :], wmod_d[kc * 128:(kc + 1) * 128, g * 512:(g + 1) * 512])
        for fc in range(g * 4, g * 4 + 4):
            if fc not in fc_list:
                continue
            for kc in range(8):
                P.mm(ps[:, fc * 2:fc * 2 + 2], wb[:, kc, (fc % 4) * 128:(fc % 4 + 1) * 128], sc[:, kc, :],
                     start=(kc == 0), stop=(kc == 7))
    for fc in fc_list:
        P.tt('dve', modT[:, fc, :], ps[:, fc * 2:fc * 2 + 2], bT[:, fc:fc + 1].to_broadcast([128, 2]), ALU.add)


def norm_coeffs(nc, P, modT, normw_d, name):
    nw = P.sbuf(name + "_nw", [128, 8], F32)
    P.dma('sp', nw[:], normw_d)
    A = P.sbuf(name + "_A", [128, 8, 2], F32)
    P.ts('dve', A[:], modT[:, 8:16, :], 1.0, None, ALU.add)
    P.tt('dve', A[:], A[:], nw[:].unsqueeze(2).to_broadcast([128, 8, 2]), ALU.mult)
    return A


def phase_norm(nc, P, K, banks, RT, A, Bsh, xn_d, tiles, tmp):
    xn_v = xn_d.rearrange("(kc p) t -> p kc t", p=128)
    for ti, (c0, n, isctx) in enumerate(tiles):
        j = 1 if isctx else 0
        ps = banks.b[ti % 2]
        for kc in range(8):
            sq = tmp['sq'][kc % 2]
            P.act(sq[:, 0:n], RT[:, kc, c0:c0 + n], AF.Square)
            P.mm(ps[:, 0:n], K['ones'][:], sq[:, 0:n], start=(kc == 0), stop=(kc == 7))
        rstd = tmp['rstd'][ti % 2]
        P.act(rstd[:, 0:n], ps[:, 0:n], AF.Ln, scale=1.0 / D, bias=K['eps'][:])
        P.act(rstd[:, 0:n], rstd[:, 0:n], AF.Exp, scale=-0.5)
        xb = tmp['xb'][ti % 2]
        for kc in range(8):
            t1 = tmp['t1'][kc % 2]
            P.stt(t1[:, 0:n], RT[:, kc, c0:c0 + n], A[:, kc, j:j + 1], rstd[:, 0:n], ALU.mult, ALU.mult)
            P.act(xb[:, kc, 0:n], t1[:, 0:n], AF.Identity, bias=Bsh[:, kc, j:j + 1])
        P.dma('pool', xn_v[:, :, c0:c0 + n], xb[:, :, 0:n], final=True)


def phase_final(nc, P, K, banks, RT, fnw_d, out_d, tiles, tmp):
    fw = P.sbuf("fnw_sb", [128, 8], F32)
    P.dma('sp', fw[:], fnw_d)
    out_v = out_d.rearrange("(kc p) t -> p kc t", p=128)
    for ti, (c0, n, isctx) in enumerate(tiles):
        ps = banks.b[ti % 2]
        for kc in range(8):
            sq = tmp['sq'][kc % 2]
            P.act(sq[:, 0:n], RT[:, kc, c0:c0 + n], AF.Square)
            P.mm(ps[:, 0:n], K['ones'][:], sq[:, 0:n], start=(kc == 0), stop=(kc == 7))
        rstd = tmp['rstd'][ti % 2]
        P.act(rstd[:, 0:n], ps[:, 0:n], AF.Ln, scale=1.0 / D, bias=K['eps'][:])
        P.act(rstd[:, 0:n], rstd[:, 0:n], AF.Exp, scale=-0.5)
        for kc in range(8):
            P.stt(RT[:, kc, c0:c0 + n], RT[:, kc, c0:c0 + n], fw[:, kc:kc + 1], rstd[:, 0:n], ALU.mult, ALU.mult)
        P.dma('pool', out_v[:, :, c0 - 64:c0 - 64 + n], RT[:, :, c0:c0 + n], final=True)


def phase_outproj(nc, P, K, banks, RT, yT_d, wout_d, ssdw_d, gateT, tiles, tmp):
    wo = P.sbuf("wout_bf", [128, 8, D], BF16)
    for kc in range(8):
        st = tmp['wstage'][kc % 2]
        P.dma('sp' if kc % 2 == 0 else 'act', st[:], wout_d[kc * 128:(kc + 1) * 128, :])
        P.copy('pool', wo[:, kc, :], st[:])
    sw = P.sbuf("ssdnw_sb", [128, 2], F32)
    P.dma('sp', sw[:], ssdw_d)
    yT_v = yT_d.rearrange("(kc p) t -> p kc t", p=128)
    for ti, (c0, n, isctx) in enumerate(tiles):
        j = 1 if isctx else 0
        yb = tmp['yb'][ti % 2]
        P.dma('sp', yb[:, :, 0:n], yT_v[:, :, c0:c0 + n])
        ps = banks.b[2 + ti % 2]
        for i, kc in enumerate((6, 7)):
            sq = tmp['sq'][i]
            P.act(sq[:, 0:n], yb[:, kc, 0:n], AF.Square)
            P.mm(ps[:, 0:n], K['ones'][:], sq[:, 0:n], start=(i == 0), stop=(i == 1))
        rstd = tmp['rstd'][ti % 2]
        P.act(rstd[:, 0:n], ps[:, 0:n], AF.Ln, scale=1.0 / 256, bias=K['eps'][:])
        P.act(rstd[:, 0:n], rstd[:, 0:n], AF.Exp, scale=-0.5)
        for i, kc in enumerate((6, 7)):
            P.stt(yb[:, kc, 0:n], yb[:, kc, 0:n], sw[:, i:i + 1], rstd[:, 0:n], ALU.mult, ALU.mult)
        for fc in range(8):
            po = banks.b[4 + fc % 4]
            for kc in range(8):
                P.mm(po[:, 0:n], wo[:, kc, fc * 128:(fc + 1) * 128], yb[:, kc, 0:n], start=(kc == 0), stop=(kc == 7))
            P.stt(RT[:, fc, c0:c0 + n], po[:, 0:n], gateT[:, 16 + fc, j:j + 1], RT[:, fc, c0:c0 + n], ALU.mult, ALU.add)


def alloc_tok_tmp(P):
    return {
        'sq': [P.sbuf("t_sq%d" % i, [128, 512], F32) for i in range(2)],
        'rstd': [P.sbuf("t_rstd%d" % i, [128, 512], F32) for i in range(2)],
        't1': [P.sbuf("t_t1%d" % i, [128, 512], F32) for i in range(2)],
        'xb': [P.sbuf("t_xb%d" % i, [128, 8, 512], BF16) for i in range(2)],
    }


def build_tok_kernel(nc, P, first, last, with_outproj):
    banks = Banks(P)
    K = load_consts_tok(nc, P)
    epst = P.sbuf("eps_t", [128, 1], F32)
    P.memset('pool', epst[:], EPS)
    K['eps'] = epst
    tmp = alloc_tok_tmp(P)
    RT_d = nc.dram_tensor("RT", [D, QT], F32, kind="ExternalInput").ap()
    cvec_d = nc.dram_tensor("cvec", [128, 8, 2], F32, kind="ExternalInput").ap()
    RT = P.sbuf("RT_sb", [128, 8, QT], F32)
    RT_v = RT_d.rearrange("(kc p) t -> p kc t", p=128)
    for kc in range(8):
        P.dma('sp' if kc % 2 == 0 else 'act', RT[:, kc, :], RT_v[:, kc, :])
    modT = P.sbuf("modT", [128, 24, 2], F32)
    if with_outproj:
        wmodg_d = nc.dram_tensor("wmod_g", [D, 3 * D], F32, kind="ExternalInput").ap()
        bmodg_d = nc.dram_tensor("bmod_g", [128, 24], F32, kind="ExternalInput").ap()
        yT_d = nc.dram_tensor("yT", [D, QT], BF16, kind="ExternalInput").ap()
        wout_d = nc.dram_tensor("wout", [D, D], F32, kind="ExternalInput").ap()
        ssdw_d = nc.dram_tensor("ssdnw", [128, 2], F32, kind="ExternalInput").ap()
        tmp['wstage'] = [P.sbuf("t_wst%d" % i, [128, D], F32) for i in range(2)]
        tmp['yb'] = [P.sbuf("t_yb%d" % i, [128, 8, 512], BF16) for i in range(2)]
        gateT = P.sbuf("gateT", [128, 24, 2], F32)
        compute_mod(nc, P, K, banks, cvec_d, wmodg_d, bmodg_d, list(range(16, 24)), gateT)
        tiles = TILES_Q[1:] if last else TILES_Q
        phase_outproj(nc, P, K, banks, RT, yT_d, wout_d, ssdw_d, gateT, tiles, tmp)
    if last:
        fnw_d = nc.dram_tensor("fnw", [128, 8], F32, kind="ExternalInput").ap()
        out_d = nc.dram_tensor("outT", [D, 2048], F32, kind="ExternalOutput").ap()
        phase_final(nc, P, K, banks, RT, fnw_d, out_d, TILES_Q[1:], tmp)
    else:
        wmod_d = nc.dram_tensor("wmod_n", [D, 3 * D], F32, kind="ExternalInput").ap()
        bmod_d = nc.dram_tensor("bmod_n", [128, 24], F32, kind="ExternalInput").ap()
        normw_d = nc.dram_tensor("normw", [128, 8], F32, kind="ExternalInput").ap()
        xn_d = nc.dram_tensor("xnT", [D, QT], BF16, kind="ExternalOutput").ap()
        if with_outproj:
            Rout_d = nc.dram_tensor("RTout", [D, QT], F32, kind="ExternalOutput").ap()
        modT2 = modT
        _cm_second(nc, P, K, banks, cvec_d, wmod_d, bmod_d, list(range(0, 16)), modT2, with_outproj)
        A = norm_coeffs(nc, P, modT2, normw_d, "nc")
        phase_norm(nc, P, K, banks, RT, A, modT2, xn_d, TILES_Q, tmp)
        if with_outproj:
            Ro_v = Rout_d.rearrange("(kc p) t -> p kc t", p=128)
            for kc in range(8):
                P.dma('pool', Ro_v[:, kc, :], RT[:, kc, :], final=True)


_CM_STATE = {}


def _cm_second(nc, P, K, banks, cvec_d, wmod_d, bmod_d, fc_list, modT, second):
    if not second:
        return compute_mod(nc, P, K, banks, cvec_d, wmod_d, bmod_d, fc_list, modT)
    orig = P.sbuf

    def renamed(name, shape, dtype=F32):
        return orig(name + "_2", shape, dtype)
    P.sbuf = renamed
    try:
        compute_mod(nc, P, K, banks, cvec_d, wmod_d, bmod_d, fc_list, modT)
    finally:
        P.sbuf = orig


def pk(v):
    v = np.asarray(v)
    return np.ascontiguousarray(v.reshape(-1, 128).T)

TOK_TILES = [(0, 256)] + [(256 + 512 * i, 512) for i in range(16)]
SCALE_QK = 32.0 ** -0.5
QK_REP = 1
DEFER = True


def store_y(P, io, mi, t0, n, src):
    if 'y_store' in io:
        io['y_store'](mi, t0, n, src)
    else:
        P.dma('pool', io['yT'][mi, :, t0:t0 + n], src, final=True)


def load_weights_bf16(nc, P, A, w_d, ncols, stage):
    Wsb = A.alloc([8, ncols], BF16)
    for kc in range(8):
        st = stage[kc % 2]
        P.dma('sp' if kc % 2 == 0 else 'act', st[:, 0:ncols], w_d[kc * 128:(kc + 1) * 128, :])
        P.copy('dve', Wsb[:, kc, :], st[:, 0:ncols])
    return Wsb


def inproj(nc, P, banks, xn_v, Wsb, tiles, fm_groups, tm_group, xbufs, bank_ids=(0, 1, 2, 3), pre_tile=None):
    pending = []
    for ti, (t0, n) in enumerate(tiles):
        xb = xbufs[ti % 2]
        if callable(xn_v):
            xn_v(xb, t0, n)
        else:
            P.dma('sp', xb[:, :, 0:n], xn_v[:, :, t0:t0 + n])
        if pre_tile is not None:
            pre_tile(ti, t0, n)
        new_pending = []
        for gi, (c0, M, lat_only, fn) in enumerate(fm_groups):
            if lat_only and t0 < CTXL:
                continue
            ps = banks.b[bank_ids[gi % len(bank_ids)]]
            for kc in range(8):
                P.mm(ps[0:M, 0:n], Wsb[:, kc, c0:c0 + M], xb[:, kc, 0:n], start=(kc == 0), stop=(kc == 7))
            if fn is not None:
                r = fn(ps, t0, n, ti)
                if r is not None:
                    if DEFER:
                        new_pending.append(r)
                    else:
                        r()
        if tm_group is not None:
            c0, ncols, fn, tmbanks = tm_group
            for sub in range(n // 128):
                ps = banks.b[tmbanks[sub % len(tmbanks)]]
                for kc in range(8):
                    P.mm(ps[:, 0:ncols], xb[:, kc, sub * 128:(sub + 1) * 128], Wsb[:, kc, c0:c0 + ncols],
                         start=(kc == 0), stop=(kc == 7))
                fn(ps, t0 + sub * 128)
        for r in pending:
            r2 = r()
            if r2 is not None:
                new_pending.append(r2)
        pending = new_pending
    while pending:
        nxt = []
        for r in pending:
            r2 = r()
            if r2 is not None:
                nxt.append(r2)
        pending = nxt


def silu_evac(P, out_bf, ps_ap, tmp_e, M, n):
    P.act(tmp_e[0:M, 0:n], ps_ap, AF.Tanh, scale=0.5)
    P.stt(out_bf, tmp_e[0:M, 0:n], 1.0, ps_ap, ALU.add, ALU.mult)


def attention_phase(nc, P, A, banks, K, xn_v, io, layer, with_ctx):
    A.reset()
    lam_init = 0.8 - 0.6 * float(np.exp(-0.3 * layer))
    stage = [A.alloc([640], F32) for _ in range(2)]
    Wsb = load_weights_bf16(nc, P, A, io['w_attn'], 640, stage)
    xbufs = [A.alloc([8, 512], BF16) for _ in range(2)]
    Qa = A.alloc([NTOK], BF16)
    Ka = A.alloc([NTOK], BF16)
    Vt = A.alloc([66, 128], BF16)
    sg = A.alloc([NTOK], BF16)
    cosb = [A.alloc([512], F32) for _ in range(2)]
    sinb = [A.alloc([512], F32) for _ in range(2)]
    t1 = [A.alloc([512], F32) for _ in range(2)]
    t2 = [A.alloc([512], F32) for _ in range(2)]
    te = [A.alloc([512], F32) for _ in range(2)]
    P.memset('pool', Vt[:, :, 65:128], 0.0)
    P.memset('pool', Vt[:, :, 64:65], 1.0)

    def pre_tile(ti, t0, n):
        if t0 >= CTXL:
            P.dma('act', cosb[ti % 2][:], io['cosT'][:, t0 - CTXL:t0 - CTXL + n])
            P.dma('act', sinb[ti % 2][:], io['sinT'][:, t0 - CTXL:t0 - CTXL + n])

    state = {}

    def ev_plain(dst):
        def f(ps, t0, n, ti):
            if t0 < CTXL:
                P.copy('act', dst[:, t0:t0 + n], ps[:, 0:n])
            else:
                state['ps1'] = ps
        return f

    def ev_rot(dst):
        def f(ps, t0, n, ti):
            a = t1[ti % 2]
            b = t2[ti % 2]
            P.tt('dve', a[:, 0:n], state['ps1'][:, 0:n], cosb[ti % 2][:, 0:n], ALU.mult)
            P.tt('dve', b[:, 0:n], ps[:, 0:n], sinb[ti % 2][:, 0:n], ALU.mult)
            P.tt('pool', dst[:, t0:t0 + n], a[:, 0:n], b[:, 0:n], ALU.add)
        return f

    vT = [A.alloc([512], BF16) for _ in range(2)]

    def ev_gv(ps, t0, n, ti):
        silu_evac(P, sg[0:64, t0:t0 + n], ps[0:64, 0:n], te[ti % 2], 64, n)
        v_ = vT[ti % 2]
        P.copy('act', v_[64:128, 0:n], ps[64:128, 0:n])

        def rest():
            for sub in range(n // 128):
                pt = banks.b[6 + sub % 2].bitcast(BF16)
                P.transpose(pt[:, 0:64], v_[64:128, sub * 128:(sub + 1) * 128], K['identb'][64:128, 64:128])
                P.copy('dve', Vt[:, t0 // 128 + sub, 0:64], pt[:, 0:64])
        return rest

    fm = [(0, 128, False, ev_plain(Qa)), (128, 128, True, ev_rot(Qa)),
          (256, 128, False, ev_plain(Ka)), (384, 128, True, ev_rot(Ka)),
          (512, 128, False, ev_gv)]
    inproj(nc, P, banks, xn_v, Wsb, TOK_TILES, fm, None, xbufs, bank_ids=(0, 1, 2, 3, 4), pre_tile=pre_tile)

    lv = A.alloc([128], F32)
    P.dma('sp', lv[0:64, :], io['diff_lam'].partition_broadcast(64))
    pr = A.alloc([64], F32)
    s2 = A.alloc([2], F32)
    P.tt('dve', pr[0:64, 0:32], lv[0:64, 0:32], lv[0:64, 32:64], ALU.mult)
    P.tt('dve', pr[0:64, 32:64], lv[0:64, 64:96], lv[0:64, 96:128], ALU.mult)
    P.op('dve', lambda e: e.tensor_reduce(s2[0:64, 0:2], pr[0:64, :].rearrange("p (a b) -> p a b", a=2), AX.X, ALU.add),
         reads=[pr[0:64, :]], writes=[s2[0:64, 0:2]])
    P.act(s2[0:64, :], s2[0:64, :], AF.Exp)
    nlam = A.alloc([1], F32)
    P.tt('dve', nlam[0:64, :], s2[0:64, 1:2], s2[0:64, 0:1], ALU.subtract)
    P.ts('dve', nlam[0:64, :], nlam[0:64, :], -lam_init, None, ALU.add)
    subw = A.alloc([1], F32)
    P.dma('sp', subw[0:64, :], io['diff_subln_w'])
    P.ts('dve', subw[0:64, :], subw[0:64, :], 1.0 - lam_init, None, ALU.mult)

    Eb = [A.alloc([1024], BF16) for _ in range(4)]
    osb = [A.alloc([512], F32) for _ in range(2)]
    fz = [A.alloc([512], F32) for _ in range(4)]
    yb = [A.alloc([512], BF16) for _ in range(2)]
    qblocks = []
    if with_ctx:
        qblocks.append((0, 256, [0, 1]))
    for i in range(16):
        qblocks.append((256 + 512 * i, 512, list(range(66))))
    o_acc = [banks.b[6], banks.b[7]]
    for qi, (q0, nq, kbs) in enumerate(qblocks):
        def qk(kb):
            S = banks.pp[kb % 3]
            for rep in range(QK_REP):
                P.mm(S[:, 0:nq], Ka[0:32, kb * 128:(kb + 1) * 128], Qa[0:32, q0:q0 + nq])
                P.mm(S[:, 512:512 + nq], Ka[64:96, kb * 128:(kb + 1) * 128], Qa[64:96, q0:q0 + nq])
        for k_ in kbs[0:3]:
            qk(k_)
        for ki, kb in enumerate(kbs):
            S = banks.pp[kb % 3]
            E = Eb[ki % 4]
            if nq == 512:
                P.act(E[:, :], S[:, :], AF.Exp, scale=SCALE_QK)
            else:
                Ev = E.rearrange("p (a b) -> p a b", a=2)[:, :, 0:nq]
                Sv = S.rearrange("p (a b) -> p a b", a=2)[:, :, 0:nq]
                P.act(Ev, Sv, AF.Exp, scale=SCALE_QK)
            if ki + 3 < len(kbs):
                qk(kbs[ki + 3])
            for c in range(2):
                P.mm(o_acc[c][:, 0:nq], Vt[:, kb, :], E[:, c * 512:c * 512 + nq], start=(ki == 0), stop=(ki == len(kbs) - 1))
        for c in range(2):
            P.copy('act', osb[c][0:65, 0:nq], o_acc[c][0:65, 0:nq])
        zb = [banks.b[0], banks.b[1]]
        for c in range(2):
            P.mm(zb[c][0:64, 0:nq], K['selZ'][0:65, :], osb[c][0:65, 0:nq])
        for c in range(2):
            P.act(fz[c][0:64, 0:nq], zb[c][0:64, 0:nq], AF.Ln)
            P.act(fz[c][0:64, 0:nq], fz[c][0:64, 0:nq], AF.Exp, scale=-1.0)
            P.tt('dve', fz[c][0:64, 0:nq], fz[c][0:64, 0:nq], osb[c][0:64, 0:nq], ALU.mult)
        o = fz[2]
        P.stt(o[0:64, 0:nq], fz[1][0:64, 0:nq], nlam[0:64, 0:1], fz[0][0:64, 0:nq], ALU.mult, ALU.add)
        P.act(fz[3][0:64, 0:nq], o[0:64, 0:nq], AF.Square)
        P.mm(zb[0][0:64, 0:nq], K['ones'][0:64, 0:64], fz[3][0:64, 0:nq])
        P.act(fz[3][0:64, 0:nq], zb[0][0:64, 0:nq], AF.Ln, scale=1.0 / 64, bias=K['eps'][0:64, :])
        P.act(fz[3][0:64, 0:nq], fz[3][0:64, 0:nq], AF.Exp, scale=-0.5)
        P.stt(o[0:64, 0:nq], o[0:64, 0:nq], subw[0:64, 0:1], fz[3][0:64, 0:nq], ALU.mult, ALU.mult)
        y = yb[qi % 2]
        P.stt(y[0:64, 0:nq], o[0:64, 0:nq], 0.5, sg[0:64, q0:q0 + nq], ALU.mult, ALU.mult)
        store_y(P, io, 2, q0, nq, y[0:64, 0:nq])
        if 'after_q' in io:
            io['after_q'](q0, nq)


TP = 8456
PADC = 2
PADL = 260


def pcol(t0):
    return t0 + PADC if t0 < CTXL else t0 + (PADL - CTXL)


PTILES = [(PADC, 256)] + [(PADL + 512 * i, 512) for i in range(16)]


def conv4(P, dst, src, cw, cb, np_, c_lo, c_hi):
    n = c_hi - c_lo
    P.act(dst[0:np_, c_lo:c_hi], src[0:np_, c_lo - 2:c_hi - 2], AF.Identity, scale=cw[0:np_, 0:1], bias=cb[0:np_, 0:1])
    for k in range(1, 4):
        P.stt(dst[0:np_, c_lo:c_hi], src[0:np_, c_lo + k - 2:c_hi + k - 2], cw[0:np_, k:k + 1], dst[0:np_, c_lo:c_hi],
              ALU.mult, ALU.add)


def lru_phase(nc, P, A, banks, K, xn_v, io, layer, with_ctx):
    A.reset()
    stage = [A.alloc([192], F32) for _ in range(2)]
    Wsb = load_weights_bf16(nc, P, A, io['w_lru'], 192, stage)
    xbufs = [A.alloc([8, 512], BF16) for _ in range(2)]
    XA = A.alloc([TP], F32)
    XU = A.alloc([TP], F32)
    sg = A.alloc([NTOK], BF16)
    te = [A.alloc([512], F32) for _ in range(2)]
    par = A.alloc([16], F32)
    P.dma('sp', par[:, 0:9], io['lru_par'])
    cw, cb = par[:, 0:4], par[:, 4:5]
    nba, nbx = par[:, 9:10], par[:, 10:11]
    c8, c16 = par[:, 11:12], par[:, 12:13]
    P.ts('dve', nba, par[:, 5:6], 0.5, None, ALU.mult)
    P.ts('dve', nbx, par[:, 6:7], 0.5, None, ALU.mult)
    P.act(c8, par[:, 7:8], AF.Exp, scale=-1.0)
    P.act(c8, c8, AF.Ln, bias=K['one'][:, 0:1])
    P.ts('dve', c16, c8, -16.0, None, ALU.mult)
    P.ts('dve', c8, c8, -8.0, None, ALU.mult)
    gw32 = A.alloc([256], F32)
    P.dma('sp', gw32[0:64, :], io['lru_gw'])
    gw = A.alloc([256], BF16)
    P.copy('pool', gw[0:64, :], gw32[0:64, :])
    II = A.alloc([64], F32)
    P.copy('pool', II[0:64, :], K['ident'][0:64, 0:64])
    P.copy('pool', II[64:128, :], K['ident'][64:128, 64:128])
    P.memset('pool', XA[:, 0:PADC], 0.0)
    P.memset('pool', XA[:, PADC + CTXL:PADL], 0.0)
    P.memset('pool', XA[:, PADL + SEQ:TP], 0.0)

    def ev_x(ps, t0, n, ti):
        c = pcol(t0)
        P.copy('act', XA[:, c:c + n], ps[:, 0:n])

    def ev_gate(ps, t0, n, ti):
        silu_evac(P, sg[0:64, t0:t0 + n], ps[0:64, 0:n], te[ti % 2], 64, n)

    inproj(nc, P, banks, xn_v, Wsb, TOK_TILES, [(0, 128, False, ev_x), (128, 64, False, ev_gate)], None, xbufs,
           bank_ids=(0, 1, 2, 3))
    conv4(P, XU, XA, cw, cb, 128, PADC, PADC + CTXL)
    for i in range(4):
        conv4(P, XU, XA, cw, cb, 128, PADL + 2048 * i, PADL + 2048 * (i + 1))
    xcb = [A.alloc([512], BF16) for _ in range(2)]
    gr = [A.alloc([512], F32) for _ in range(2)]
    gi_ = [A.alloc([512], F32) for _ in range(2)]
    gs = [A.alloc([512], F32) for _ in range(2)]
    for ti, (c0, n) in enumerate(PTILES):
        xb = xcb[ti % 2]
        P.copy('pool', xb[0:64, 0:n], XU[0:64, c0:c0 + n])
        psr, psi = banks.b[(2 * ti) % 8], banks.b[(2 * ti + 1) % 8]
        P.mm(psr[:, 0:n], gw[0:64, 0:128], xb[0:64, 0:n])
        P.mm(psi[:, 0:n], gw[0:64, 128:256], xb[0:64, 0:n])
        r, ii, s = gr[ti % 2], gi_[ti % 2], gs[ti % 2]
        P.act(r[:, 0:n], psr[:, 0:n], AF.Tanh, scale=0.5, bias=nba)
        P.act(ii[:, 0:n], psi[:, 0:n], AF.Tanh, scale=0.5, bias=nbx)
        P.ts('dve', r[:, 0:n], r[:, 0:n], 0.5, 0.5, ALU.mult, ALU.add)
        P.ts('dve', ii[:, 0:n], ii[:, 0:n], 0.5, 0.5, ALU.mult, ALU.add)
        P.act(XA[:, c0:c0 + n], r[:, 0:n], AF.Exp, scale=c8)
        P.tt('dve', XU[:, c0:c0 + n], XU[:, c0:c0 + n], ii[:, 0:n], ALU.mult)
    for ti, (c0, n) in enumerate(PTILES):
        s = gs[ti % 2]
        P.tt('pool', s[:, 0:n], XA[:, c0:c0 + n], XA[:, c0:c0 + n], ALU.mult)
        P.act(s[:, 0:n], s[:, 0:n], AF.Ln, scale=-1.0, bias=K['one'][:, 0:1])
        P.act(s[:, 0:n], s[:, 0:n], AF.Exp, scale=0.5)
        P.tt('dve', XU[:, c0:c0 + n], XU[:, c0:c0 + n], s[:, 0:n], ALU.mult)
    fa, fu = XA[0:64], XU[0:64]
    ba_, bu = XA[64:128], XU[64:128]
    P.scan(fu[:, PADC:PADC + CTXL], fa[:, PADC:PADC + CTXL], fu[:, PADC:PADC + CTXL], 0.0)
    for i in range(4):
        lo = PADL + 2048 * i
        init = fu[:, PADC + CTXL - 1:PADC + CTXL] if i == 0 else fu[:, lo - 1:lo]
        P.scan(fu[:, lo:lo + 2048], fa[:, lo:lo + 2048], fu[:, lo:lo + 2048], init)
    P.scan(bu[:, PADC:PADC + CTXL][:, ::-1], ba_[:, PADC:PADC + CTXL][:, ::-1], bu[:, PADC:PADC + CTXL][:, ::-1], 0.0)
    for i in range(3, -1, -1):
        lo = PADL + 2048 * i
        init = bu[:, PADC:PADC + 1] if i == 3 else bu[:, lo + 2048:lo + 2049]
        P.scan(bu[:, lo:lo + 2048][:, ::-1], ba_[:, lo:lo + 2048][:, ::-1], bu[:, lo:lo + 2048][:, ::-1], init)
    yb = [A.alloc([512], BF16) for _ in range(2)]
    for ti, (t0, n) in enumerate(TOK_TILES):
        if t0 < CTXL and not with_ctx:
            continue
        c0 = pcol(t0)
        ps = banks.b[ti % 4]
        P.mm(ps[0:64, 0:n], II[:, :], XU[:, c0:c0 + n])
        y = yb[ti % 2]
        P.stt(y[0:64, 0:n], ps[0:64, 0:n], 0.5, sg[0:64, t0:t0 + n], ALU.mult, ALU.mult)
        store_y(P, io, 1, t0, n, y[0:64, 0:n])


def gla_phase(nc, P, A, banks, K, xn_v, io, layer, with_ctx):
    A.reset()
    NC_ = 132
    QG = A.alloc([NTOK], BF16)
    KG = A.alloc([NTOK], BF16)
    KDt = A.alloc([66, 64], BF16)
    Vt = A.alloc([66, 64], BF16)
    sg = A.alloc([NTOK], BF16)
    ST = A.alloc([NC_ + 2, 64], F32)
    STb = A.alloc([NC_ + 2, 64], BF16)
    DEC = A.alloc([NC_], F32)
    par = A.alloc([8], F32)
    P.dma('sp', par[0:64, 0:2], io['gla_par'])
    nb2 = par[0:64, 2:3]
    P.ts('dve', nb2, par[0:64, 0:1], -1.0, None, ALU.mult)
    lnsc = par[0:64, 3:4]
    P.memset('dve', lnsc, float(np.log(32.0 ** -0.5)))
    w2_32 = A.alloc([64], F32)
    P.dma('sp', w2_32[64:96, :], io['gla_w2bd'])
    w2b = A.alloc([64], BF16)
    P.copy('dve', w2b[64:96, :], w2_32[64:96, :])
    mk = A.mark()
    stage = [A.alloc([320], F32) for _ in range(2)]
    Wsb = load_weights_bf16(nc, P, A, io['w_gla'], 320, stage)
    xbufs = [A.alloc([8, 512], BF16) for _ in range(2)]
    te = [A.alloc([512], F32) for _ in range(2)]
    lrb = [A.alloc([512], BF16) for _ in range(2)]
    Lb = [A.alloc([512], F32) for _ in range(2)]
    Gb = [A.alloc([512], F32) for _ in range(2)]
    Eq = [A.alloc([512], F32) for _ in range(2)]
    Ek = [A.alloc([512], F32) for _ in range(2)]
    dG = [A.alloc([512], F32) for _ in range(2)]
    kdT = [A.alloc([512], BF16) for _ in range(2)]
    P.memset('dve', ST[0:64, 0:2, :], 0.0)
    P.memset('dve', ST[0:64, NC_:NC_ + 2, :], 0.0)
    st = {}

    qs = [A.alloc([512], F32) for _ in range(2)]
    ks = [A.alloc([512], F32) for _ in range(2)]

    def ev_q(ps, t0, n, ti):
        P.copy('act', qs[ti % 2][0:64, 0:n], ps[0:64, 0:n])

    vT = [A.alloc([512], BF16) for _ in range(2)]

    def ev_gv(ps, t0, n, ti):
        P.copy('dve', sg[0:64, t0:t0 + n], ps[0:64, 0:n])
        v_ = vT[ti % 2]
        P.copy('act', v_[64:128, 0:n], ps[64:128, 0:n])
        for sub in range(n // 128):
            pt = banks.b[5 + sub % 2].bitcast(BF16)
            P.transpose(pt[:, 64:128], v_[64:128, sub * 128:(sub + 1) * 128], K['identb'][64:128, 64:128])
            P.copy('dve', Vt[:, t0 // 128 + sub, :], pt[:, 64:128])

    def ev_lr(ps, t0, n, ti):
        i2 = ti % 2
        nch = n // 64
        c0 = t0 // 64
        P.copy('act', ks[ti % 2][0:64, 0:n], ps[0:64, 0:n])
        P.copy('act', lrb[i2][64:96, 0:n], ps[64:96, 0:n])

        def rest():
            pz = banks.b[7]
            P.mm(pz[0:64, 0:n], w2b[64:96, 0:64], lrb[i2][64:96, 0:n])
            L, G = Lb[i2], Gb[i2]
            P.act(L[0:64, 0:n], pz[0:64, 0:n], AF.Exp, scale=-1.0, bias=nb2)
            P.act(L[0:64, 0:n], L[0:64, 0:n], AF.Ln, bias=K['one'][0:64, 0:1])
            P.scan(G[0:32, 0:n], K['scanmask'][0:32, 0:n], L[0:32, 0:n], 0.0)
            P.scan(G[32:64, 0:n][:, ::-1], K['scanmask'][32:64, 0:n][:, ::-1], L[32:64, 0:n][:, ::-1], 0.0)
            P.act(Eq[i2][0:64, 0:n], G[0:64, 0:n], AF.Exp, scale=-1.0 / 16, bias=lnsc)
            P.act(Ek[i2][0:64, 0:n], G[0:64, 0:n], AF.Exp, scale=1.0 / 16)
            P.tt('dve', QG[0:64, t0:t0 + n], qs[i2][0:64, 0:n], Eq[i2][0:64, 0:n], ALU.mult)
            P.tt('dve', KG[0:64, t0:t0 + n], ks[i2][0:64, 0:n], Ek[i2][0:64, 0:n], ALU.mult)
            G3 = G[:, 0:n].rearrange("p (c s) -> p c s", s=64)
            d3 = dG[i2][:, 0:n].rearrange("p (c s) -> p c s", s=64)
            P.tt('dve', d3[0:32], G3[0:32, :, 63:64].to_broadcast([32, nch, 64]), G3[0:32], ALU.subtract)
            P.tt('dve', d3[32:64], G3[32:64, :, 0:1].to_broadcast([32, nch, 64]), G3[32:64], ALU.subtract)
            P.act(dG[i2][0:64, 0:n], dG[i2][0:64, 0:n], AF.Exp, scale=-1.0 / 16)
            P.tt('dve', kdT[i2][0:64, 0:n], ks[i2][0:64, 0:n], dG[i2][0:64, 0:n], ALU.mult)
            P.act(DEC[0:32, c0:c0 + nch], G3[0:32, :, 63], AF.Exp, scale=-1.0 / 16)
            P.act(DEC[32:64, c0:c0 + nch], G3[32:64, :, 0], AF.Exp, scale=-1.0 / 16)
            def rest2():
                for sub in range(n // 128):
                    pt = banks.b[5 + sub % 2].bitcast(BF16)
                    P.transpose(pt[:, 0:64], kdT[i2][0:64, sub * 128:(sub + 1) * 128], K['identb'][0:64, 0:64])
                    P.copy('dve', KDt[:, t0 // 128 + sub, :], pt[:, 0:64])
            return rest2
        return rest

    fm = [(0, 64, False, ev_q), (64, 96, False, ev_lr), (192, 128, False, ev_gv)]
    inproj(nc, P, banks, xn_v, Wsb, TOK_TILES, fm, None, xbufs, bank_ids=(0, 1, 2))
    for ti, (t0, n) in enumerate(TOK_TILES):
        P.act(te[ti % 2][0:64, 0:n], sg[0:64, t0:t0 + n], AF.Tanh, scale=0.5)
        P.stt(sg[0:64, t0:t0 + n], te[ti % 2][0:64, 0:n], 1.0, sg[0:64, t0:t0 + n], ALU.add, ALU.mult)

    for g0 in range(0, 66, 8):
        npair = min(8, 66 - g0)
        for i in range(npair):
            p = g0 + i
            for j in range(2):
                P.mm(banks.b[j][0:64, i * 64:(i + 1) * 64], KDt[64 * j:64 * j + 64, p, :], Vt[64 * j:64 * j + 64, p, :])
        for j in range(2):
            src = banks.b[j][:, 0:npair * 64].rearrange("p (c v) -> p c v", v=64)
            c0 = 2 * g0 + j
            P.copy('act', ST[0:32, c0 + 2:c0 + 1 + 2 * npair:2, :], src[0:32])
            P.copy('dve', ST[32:64, c0:c0 + 2 * npair - 1:2, :], src[32:64])
    fsteps = [(c + 2, c + 1, c) for c in range(1, NC_)]
    bsteps = [(c, c + 1, c) for c in (2, 1, 0)] + [(131, 0, 131)] + [(c, c + 1, c) for c in range(130, 4, -1)]
    for i in range(max(len(fsteps), len(bsteps))):
        if i < len(fsteps):
            o_, i_, d_ = fsteps[i]
            P.stt(ST[0:32, o_, :], ST[0:32, i_, :], DEC[0:32, d_:d_ + 1], ST[0:32, o_, :], ALU.mult, ALU.add)
        if i < len(bsteps):
            o_, i_, d_ = bsteps[i]
            P.stt(ST[32:64, o_, :], ST[32:64, i_, :], DEC[32:64, d_:d_ + 1], ST[32:64, o_, :], ALU.mult, ALU.add)
    P.copy('dve', ST[32:64, 132, :], ST[32:64, 0, :])
    P.copy('dve', STb[0:64], ST[0:64])

    A.release(mk)
    tm = [A.alloc([512], F32) for _ in range(2)]
    At = [A.alloc([512], BF16) for _ in range(2)]
    ob = [A.alloc([512], F32) for _ in range(2)]
    yb = [A.alloc([512], BF16) for _ in range(2)]
    tiles2 = [(ti, t0, n) for ti, (t0, n) in enumerate(TOK_TILES) if not (t0 < CTXL and not with_ctx)]

    def stageA(ti, t0, n):
        i2 = ti % 2
        npair = n // 128
        X, Y = banks.b[2], banks.b[3]
        for i in range(npair):
            cs = slice(t0 + 128 * i, t0 + 128 * (i + 1))
            P.mm(X[:, 128 * i:128 * (i + 1)], KG[0:32, cs], QG[0:32, cs])
            P.mm(Y[:, 128 * i:128 * (i + 1)], KG[32:64, cs], QG[32:64, cs])
        P.tt('dve', tm[i2][:, 0:n], X[:, 0:n], K['maskF'][:, 0:n], ALU.mult)
        P.tt('dve', At[i2][:, 0:n], Y[:, 0:n], K['maskB'][:, 0:n], ALU.mult)
        P.tt('pool', At[i2][:, 0:n], At[i2][:, 0:n], tm[i2][:, 0:n], ALU.add)

    def stageB(ti, t0, n):
        i2 = ti % 2
        npair = n // 128
        Z = banks.b[4 + i2]
        for i in range(npair):
            p = t0 // 128 + i
            P.mm(Z[0:64, 128 * i:128 * (i + 1)], Vt[:, p, :], At[i2][:, 128 * i:128 * (i + 1)], start=True, stop=False)
            for j in range(2):
                c = 2 * p + j
                cs = slice(t0 + 128 * i + 64 * j, t0 + 128 * i + 64 * (j + 1))
                kk = 32 if c == 3 else 64
                P.mm(Z[0:64, 128 * i + 64 * j:128 * i + 64 * (j + 1)], STb[0:kk, c + 1, :], QG[0:kk, cs],
                     start=False, stop=(j == 1))
        o = ob[i2]
        P.act(o[0:64, 0:n], Z[0:64, 0:n], AF.Square)

    def stageC(ti, t0, n):
        i2 = ti % 2
        Z = banks.b[4 + i2]
        o = ob[i2]
        zb = banks.b[6 + i2]
        P.mm(zb[0:64, 0:n], K['ones'][0:64, 0:64], o[0:64, 0:n])
        P.act(o[0:64, 0:n], zb[0:64, 0:n], AF.Ln, scale=1.0 / 64, bias=K['eps'][0:64, :])
        P.act(o[0:64, 0:n], o[0:64, 0:n], AF.Exp, scale=-0.5)
        P.stt(o[0:64, 0:n], Z[0:64, 0:n], par[0:64, 1:2], o[0:64, 0:n], ALU.mult, ALU.mult)
        y = yb[i2]
        P.stt(y[0:64, 0:n], o[0:64, 0:n], 0.5, sg[0:64, t0:t0 + n], ALU.mult, ALU.mult)
        store_y(P, io, 0, t0, n, y[0:64, 0:n])

    nt2 = len(tiles2)
    for it in range(nt2 + 1):
        if it < nt2:
            stageA(*tiles2[it])
        if it >= 1:
            stageB(*tiles2[it - 1])
            stageC(*tiles2[it - 1])


def ssd_phase(nc, P, A, banks, K, xn_v, io, layer, with_ctx):
    A.reset()
    NC_ = 132
    XB = A.alloc([TP], BF16)
    C2 = A.alloc([TP], BF16)
    sgz = A.alloc([NTOK], BF16)
    par = A.alloc([16], F32)
    P.dma('sp', par[:, 0:5], io['ssd_par'])
    cpar = A.alloc([2, 5], F32)
    P.dma('sp', cpar[:], io['ssd_cpar'])
    na = par[:, 5:7]
    P.act(na, par[:, 2:4], AF.Exp)
    P.ts('dve', na, na, -1.0, None, ALU.mult)
    dts_d = io['ssd_scr'][0:2]
    crow_d = io['ssd_scr'][2:4]
    clr_d = io['ssd_scr'][4:6]
    mk = A.mark()
    stage = [A.alloc([322], F32) for _ in range(2)]
    Wsb = load_weights_bf16(nc, P, A, io['w_ssd'], 322, stage)
    xbufs = [A.alloc([8, 512], BF16) for _ in range(2)]
    XR1 = A.alloc([TP], F32)
    XR2 = A.alloc([TP], F32)
    XCs = [A.alloc([2052], F32) for _ in range(2)]
    te = [A.alloc([512], F32) for _ in range(2)]
    dtt = [A.alloc([512], F32) for _ in range(2)]
    for X in (XR1, XR2):
        P.memset('pool', X[:, 0:PADC], 0.0)
        P.memset('pool', X[:, PADC + CTXL:PADL], 0.0)
        P.memset('pool', X[:, PADL + SEQ:TP], 0.0)

    def ev_raw(dst):
        def f(ps, t0, n, ti):
            c = pcol(t0)
            P.copy('act', dst[:, c:c + n], ps[:, 0:n])
        return f

    def ev_gate_dt(ps, t0, n, ti):
        silu_evac(P, sgz[0:64, t0:t0 + n], ps[0:64, 0:n], te[ti % 2], 64, n)
        P.copy('dve', dtt[ti % 2][64:66, 0:n], ps[64:66, 0:n])
        P.dma('pool', dts_d[:, t0:t0 + n], dtt[ti % 2][64:66, 0:n])

    fm = [(0, 128, False, ev_raw(XR1)), (128, 128, False, ev_raw(XR2)), (256, 66, False, ev_gate_dt)]
    inproj(nc, P, banks, xn_v, Wsb, TOK_TILES, fm, None, xbufs, bank_ids=(0, 1, 2, 3))
    pieces = [(PADC, PADC + CTXL)] + [(PADL + 2048 * i, PADL + 2048 * (i + 1)) for i in range(4)]
    for gi, (src, dst) in enumerate(((XR1, XB), (XR2, C2))):
        for pi_, (lo, hi) in enumerate(pieces):
            n = hi - lo
            XC = XCs[(gi * len(pieces) + pi_) % 2]
            P.act(XC[:, 2:2 + n], src[:, lo - 2:hi - 2], AF.Identity, scale=cpar[:, gi, 0:1], bias=cpar[:, gi, 4:5])
            for k in range(1, 4):
                P.stt(XC[:, 2:2 + n], src[:, lo + k - 2:hi + k - 2], cpar[:, gi, k:k + 1], XC[:, 2:2 + n], ALU.mult, ALU.add)
            P.act(dst[:, lo:hi], XC[:, 2:2 + n], AF.Silu)
    A.release(mk)
    PM = A.alloc([2, 128], F32)
    for d in range(2):
        P.dma('sp', PM[0:66, d, :], dts_d[d].rearrange("(p s) -> p s", s=128))
    DT = A.alloc([2, 128], F32)
    for d in range(2):
        P.act(DT[0:66, d, :], PM[0:66, d, :], AF.Exp, bias=par[0:66, d:d + 1])
    P.act(DT[0:66], DT[0:66], AF.Ln, bias=K['one'][0:66, 0:1])
    DA = A.alloc([2, 128], F32)
    for d in range(2):
        P.ts('dve', DA[0:66, d, :], DT[0:66, d, :], na[0:66, d:d + 1], None, ALU.mult)
    CUM = A.alloc([2, 128], F32)
    P.scan(CUM[0:66, 0, :], K['scanmask2'][0:66, 0, :], DA[0:66, 0, :], 0.0)
    P.scan(CUM[0:66, 1, :][:, ::-1], K['scanmask2'][0:66, 1, :][:, ::-1], DA[0:66, 1, :][:, ::-1], 0.0)
    P.dma('pool', crow_d[0].rearrange("(p s) -> p s", s=128), CUM[0:66, 0, :])
    P.dma('pool', crow_d[1].rearrange("(p s) -> p s", s=128), CUM[0:66, 1, :])
    LND = A.alloc([2, 128], F32)
    P.act(LND[0:66], DT[0:66], AF.Ln)
    CQ0 = A.alloc([2, 128], F32)
    P.tt('dve', CQ0[0:66], CUM[0:66], LND[0:66], ALU.subtract)
    CL = A.alloc([2, 2], F32)
    C4 = CUM[0:66].rearrange("p d (c s) -> p d c s", s=64)
    P.copy('dve', CL[0:66, 0, :], C4[:, 0, :, 63])
    P.copy('dve', CL[0:66, 1, :], C4[:, 1, :, 0])
    P.dma('pool', clr_d[0, 0:132].rearrange("(p c) -> p c", c=2), CL[0:66, 0, :])
    P.dma('pool', clr_d[1, 0:132].rearrange("(p c) -> p c", c=2), CL[0:66, 1, :])
    W0 = A.alloc([2, 128], F32)
    W4 = W0[0:66].rearrange("p d (c s) -> p d c s", s=64)
    P.tt('dve', W4, CL[0:66].unsqueeze(3).to_broadcast([66, 2, 2, 64]), C4, ALU.subtract)
    P.act(W0[0:66], W0[0:66], AF.Exp)
    P.tt('dve', W0[0:66], W0[0:66], DT[0:66], ALU.mult)
    CQ = A.alloc([2, 66], F32)
    WT = A.alloc([2, 66], F32)
    for d in range(2):
        for (src, dst) in ((CQ0, CQ), (W0, WT)):
            pt = banks.b[(2 * d) % 8 + (0 if src is CQ0 else 1)]
            P.transpose(pt[:, 0:66], src[0:66, d, :], K['ident'][0:66, 0:66])
            P.copy('act', dst[:, d, :], pt[:, 0:66])
    clr = A.alloc([132], F32)
    P.dma('sp', clr[0:2, :], clr_d[:, 0:132])
    DEC = A.alloc([132], F32)
    pd = banks.b[4]
    P.mm(pd[:, 0:132], K['sel2'][0:2, :], clr[0:2, :])
    P.act(DEC[:, :], pd[:, 0:132], AF.Exp)
    xBt = A.alloc([66, 128], BF16)
    for p in range(66):
        c0 = pcol(128 * p)
        pt = banks.b[p % 4].bitcast(BF16)
        P.transpose(pt[:, 0:128], XB[:, c0:c0 + 128], K['identb'][:, :])
        P.copy('act' if p % 2 == 0 else 'dve', xBt[:, p, :], pt[:, 0:128])
    BW = A.alloc([66, 128], BF16)
    for d in range(2):
        P.tt('dve', BW[:, :, 64 * d:64 * d + 64], xBt[:, :, 64:128], WT[:, d, :].unsqueeze(2).to_broadcast([128, 66, 64]), ALU.mult)
    ST = A.alloc([NC_ + 2, 64], F32)
    STb = A.alloc([NC_ + 2, 64], BF16)
    P.memset('pool', ST[:, 0:2, :], 0.0)
    P.memset('pool', ST[:, NC_:NC_ + 2, :], 0.0)
    for g0 in range(0, 66, 8):
        npair = min(8, 66 - g0)
        for i in range(npair):
            p = g0 + i
            for j in range(2):
                P.mm(banks.b[j][:, i * 64:(i + 1) * 64], BW[64 * j:64 * j + 64, p, :], xBt[64 * j:64 * j + 64, p, 0:64])
        for j in range(2):
            src = banks.b[j][:, 0:npair * 64].rearrange("p (c v) -> p c v", v=64)
            c0 = 2 * g0 + j
            P.copy('act', ST[0:64, c0 + 2:c0 + 1 + 2 * npair:2, :], src[0:64])
            P.copy('dve', ST[64:128, c0:c0 + 2 * npair - 1:2, :], src[64:128])
    fsteps = [(c + 2, c + 1, c) for c in range(1, NC_)]
    bsteps = [(c, c + 1, c) for c in (2, 1, 0)] + [(131, 0, 131)] + [(c, c + 1, c) for c in range(130, 4, -1)]
    for i in range(max(len(fsteps), len(bsteps))):
        if i < len(fsteps):
            o_, i_, d_ = fsteps[i]
            P.stt(ST[0:64, o_, :], ST[0:64, i_, :], DEC[0:64, d_:d_ + 1], ST[0:64, o_, :], ALU.mult, ALU.add)
        if i < len(bsteps):
            o_, i_, d_ = bsteps[i]
            P.stt(ST[64:128, o_, :], ST[64:128, i_, :], DEC[64:128, d_:d_ + 1], ST[64:128, o_, :], ALU.mult, ALU.add)
    P.copy('dve', ST[64:128, 132, :], ST[64:128, 0, :])
    P.copy('dve', STb[:], ST[:])
    crt = [A.alloc([512], F32) for _ in range(2)]
    SG = [A.alloc([2, 512], F32) for _ in range(2)]
    Ls = [A.alloc([512], F32) for _ in range(2)]
    Mt = [A.alloc([512], BF16) for _ in range(2)]
    ec = [A.alloc([512], F32) for _ in range(2)]
    Cs = [A.alloc([512], BF16) for _ in range(2)]
    yv = [A.alloc([512], F32) for _ in range(2)]
    yb = [A.alloc([512], BF16) for _ in range(2)]
    tiles2 = [(ti, t0, n) for ti, (t0, n) in enumerate(TOK_TILES) if not (t0 < CTXL and not with_ctx)]

    def stageA(ti, t0, n):
        i2 = ti % 2
        npair = n // 128
        p0 = t0 // 128
        pc0 = pcol(t0)
        cr = crt[i2]
        P.dma('sp', cr[0:2, 0:n], crow_d[:, t0:t0 + n])
        pe_ = banks.b[0]
        P.mm(pe_[:, 0:n], K['sel2'][0:2, :], cr[0:2, 0:n])
        P.act(ec[i2][:, 0:n], pe_[:, 0:n], AF.Exp)
        P.tt('dve', Cs[i2][:, 0:n], C2[:, pc0:pc0 + n], ec[i2][:, 0:n], ALU.mult)
        for d in range(2):
            pb = banks.b[1 + d]
            P.mm(pb[:, 0:n], K['selrow'][0:2, d, :], cr[0:2, 0:n], start=True, stop=False)
            P.mm(pb[:, 0:n], K['identb'][:, :], K['negF' if d == 0 else 'negB'][:, 0:n], start=False, stop=True)
            P.tt('dve', SG[i2][:, d, 0:n].rearrange("p (a b) -> p a b", b=128),
                 pb[:, 0:n].rearrange("p (a b) -> p a b", b=128),
                 CQ[:, d, p0:p0 + npair].unsqueeze(2).to_broadcast([128, npair, 128]), ALU.subtract)
        P.act(SG[i2][:, :, 0:n], SG[i2][:, :, 0:n], AF.Exp)
        P.tt('pool', Ls[i2][:, 0:n], SG[i2][:, 0, 0:n], SG[i2][:, 1, 0:n], ALU.add)
        pcb = banks.b[3]
        for i in range(npair):
            cs = slice(pc0 + 128 * i, pc0 + 128 * (i + 1))
            P.mm(pcb[:, 128 * i:128 * (i + 1)], XB[64:128, cs], C2[64:128, cs])
        P.tt('dve', Mt[i2][:, 0:n], pcb[:, 0:n], Ls[i2][:, 0:n], ALU.mult)

    def stageB(ti, t0, n):
        i2 = ti % 2
        npair = n // 128
        p0 = t0 // 128
        pc0 = pcol(t0)
        Y = banks.b[4 + i2]
        for i in range(npair):
            p = p0 + i
            P.mm(Y[0:64, 128 * i:128 * (i + 1)], xBt[:, p, 0:64], Mt[i2][:, 128 * i:128 * (i + 1)], start=True, stop=False)
            for j in range(2):
                c = 2 * p + j
                kk = 64 if c == 3 else 128
                P.mm(Y[0:64, 128 * i + 64 * j:128 * i + 64 * (j + 1)], STb[0:kk, c + 1, :],
                     Cs[i2][0:kk, 128 * i + 64 * j:128 * i + 64 * (j + 1)], start=False, stop=(j == 1))
        P.stt(yv[i2][0:64, 0:n], XB[0:64, pc0:pc0 + n], par[0:64, 4:5], Y[0:64, 0:n], ALU.mult, ALU.add)
        y = yb[i2]
        P.stt(y[0:64, 0:n], yv[i2][0:64, 0:n], 0.5, sgz[0:64, t0:t0 + n], ALU.mult, ALU.mult)
        store_y(P, io, 3, t0, n, y[0:64, 0:n])

    nt2 = len(tiles2)
    for it in range(nt2 + 1):
        if it < nt2:
            stageA(*tiles2[it])
        if it >= 1:
            stageB(*tiles2[it - 1])

OFF_A, OFF_B, OFF_C, OFF_D = 0, 800, 1312, 2336


def rope_tables():
    n_freq = 8
    inv = (np.float32(10000.0) ** (-(np.arange(n_freq, dtype=np.float32)) / np.float32(n_freq))).astype(np.float32)
    t = np.arange(SEQ)
    pos_r = (t // 64).astype(np.float32)
    pos_c = (t % 64).astype(np.float32)
    ang_r = pos_r[:, None] * inv
    ang_c = pos_c[:, None] * inv
    ang = np.concatenate([ang_r, ang_r, ang_c, ang_c], axis=-1).astype(np.float32)
    cos = np.cos(ang).astype(np.float32).T
    sin = np.sin(ang).astype(np.float32).T
    sign = np.ones(32, np.float32)
    for a in range(2):
        sign[a * 16:a * 16 + 8] = -1.0
    sins = sin * sign[:, None]
    cosT = np.zeros((128, SEQ), np.float32)
    sinT = np.zeros((128, SEQ), np.float32)
    for c in range(2):
        cosT[64 * c:64 * c + 32] = cos
        sinT[64 * c:64 * c + 32] = sins
    return cosT, sinT


def rot_perm():
    perm = np.zeros(32, np.int64)
    for a in range(2):
        for f in range(8):
            perm[a * 16 + f] = a * 16 + 8 + f
            perm[a * 16 + 8 + f] = a * 16 + f
    return perm


def prep_w_attn(w_in_l, h):
    W = np.zeros((D, 640), np.float32)
    perm = rot_perm()
    for c in range(2):
        qc = OFF_C + h * 64 + c * 32
        kc = OFF_C + 256 + h * 64 + c * 32
        W[:, 64 * c:64 * c + 32] = w_in_l[:, qc:qc + 32]
        W[:, 128 + 64 * c:128 + 64 * c + 32] = w_in_l[:, qc + perm]
        W[:, 256 + 64 * c:256 + 64 * c + 32] = w_in_l[:, kc:kc + 32]
        W[:, 384 + 64 * c:384 + 64 * c + 32] = w_in_l[:, kc + perm]
    W[:, 512:576] = w_in_l[:, OFF_C + 768 + h * 64:OFF_C + 768 + h * 64 + 64]
    W[:, 576:640] = w_in_l[:, OFF_C + 512 + h * 64:OFF_C + 512 + h * 64 + 64]
    return W


def mix_consts():
    selZ = np.zeros((128, 64), np.float32)
    selZ[64, :] = 1.0
    t = np.arange(512)
    scanmask = np.zeros((64, 512), np.float32)
    scanmask[0:32] = (t % 64 != 0).astype(np.float32)[None]
    scanmask[32:64] = (t % 64 != 63).astype(np.float32)[None]
    j = np.arange(128)[:, None]
    i = np.arange(128)[None, :]
    same = (j // 64) == (i // 64)
    mF = (same & (j <= i)).astype(np.float32)
    mB = (same & (j >= i)).astype(np.float32)
    scanmask2 = np.zeros((128, 2, 128), np.float32)
    s_ = np.arange(128)
    scanmask2[:, 0, :] = (s_ % 64 != 0).astype(np.float32)[None]
    scanmask2[:, 1, :] = (s_ % 64 != 63).astype(np.float32)[None]
    sel2 = np.zeros((2, 128), np.float32)
    sel2[0, 0:64] = 1.0
    sel2[1, 64:128] = 1.0
    selrow = np.zeros((2, 2, 128), np.float32)
    selrow[0, 0, :] = 1.0
    selrow[1, 1, :] = 1.0
    NEG = -30000.0
    negF = np.where(mF > 0, 0.0, NEG).astype(np.float32)
    negB = np.where(mB > 0, 0.0, NEG).astype(np.float32)
    return {'selZ': selZ, 'ident': np.eye(128, dtype=np.float32), 'scanmask': scanmask,
            'maskF': np.tile(mF, (1, 4)), 'maskB': np.tile(mB, (1, 4)), 'scanmask2': scanmask2,
            'sel2': sel2, 'selrow': selrow, 'negF': np.tile(negF, (1, 4)), 'negB': np.tile(negB, (1, 4))}


def build_mix_kernel(nc, P, layer, with_ctx, mixers):
    banks = Banks(P)
    K = load_consts_tok(nc, P)
    epst = P.sbuf("eps_t", [128, 1], F32)
    P.memset('pool', epst[:], EPS)
    K['eps'] = epst
    selZ_d = nc.dram_tensor("selZ", [128, 64], F32, kind="ExternalInput").ap()
    selZ = P.sbuf("selZ_sb", [128, 64], F32)
    P.dma('sp', selZ[:], selZ_d)
    K['selZ'] = selZ
    ident_d = nc.dram_tensor("ident", [128, 128], F32, kind="ExternalInput").ap()
    ident = P.sbuf("ident_sb", [128, 128], F32)
    P.dma('sp', ident[:], ident_d)
    identb = P.sbuf("identb_sb", [128, 128], BF16)
    P.copy('pool', identb[:], ident[:])
    K['ident'] = ident
    K['identb'] = identb
    one = P.sbuf("one_t", [128, 1], F32)
    P.memset('pool', one[:], 1.0)
    K['one'] = one
    for nm, shp in (('scanmask', [64, 512]), ('maskF', [128, 512]), ('maskB', [128, 512]),
                    ('scanmask2', [128, 2, 128]), ('sel2', [2, 128]), ('selrow', [2, 2, 128])):
        d_ = nc.dram_tensor(nm, shp, F32, kind="ExternalInput").ap()
        t_ = P.sbuf(nm + "_sb", shp, F32)
        P.dma('sp', t_[:], d_)
        K[nm] = t_
    for nm in ('negF', 'negB'):
        d_ = nc.dram_tensor(nm, [128, 512], F32, kind="ExternalInput").ap()
        t_ = P.sbuf(nm + "_f", [128, 512], F32)
        P.dma('sp', t_[:], d_)
        tb_ = P.sbuf(nm + "_sb", [128, 512], BF16)
        P.copy('pool', tb_[:], t_[:])
        K[nm] = tb_
    io = {}
    xn_d = nc.dram_tensor("xn", [D, NTOK], BF16, kind="ExternalInput").ap()
    xn_v = xn_d.rearrange("(kc p) t -> p kc t", p=128)
    io['yT'] = nc.dram_tensor("yT", [4, 64, NTOK], BF16, kind="ExternalOutput").ap()
    A = P.arena("arena", 190 * 1024)
    if 'attn' in mixers:
        io['w_attn'] = nc.dram_tensor("w_attn", [D, 640], F32, kind="ExternalInput").ap()
        io['cosT'] = nc.dram_tensor("cosT", [128, SEQ], F32, kind="ExternalInput").ap()
        io['sinT'] = nc.dram_tensor("sinT", [128, SEQ], F32, kind="ExternalInput").ap()
        io['diff_lam'] = nc.dram_tensor("diff_lam", [128], F32, kind="ExternalInput").ap()
        io['diff_subln_w'] = nc.dram_tensor("diff_subln_w", [64, 1], F32, kind="ExternalInput").ap()
        attention_phase(nc, P, A, banks, K, xn_v, io, layer, with_ctx)
    if 'gla' in mixers:
        _add_gla(nc, P, A, banks, K, xn_v, io, layer, with_ctx)
    if 'ssd' in mixers:
        io['w_ssd'] = nc.dram_tensor("w_ssd", [D, 322], F32, kind="ExternalInput").ap()
        io['ssd_par'] = nc.dram_tensor("ssd_par", [128, 5], F32, kind="ExternalInput").ap()
        io['ssd_cpar'] = nc.dram_tensor("ssd_cpar", [128, 2, 5], F32, kind="ExternalInput").ap()
        io['ssd_scr'] = nc.dram_tensor("ssd_scr", [6, NTOK], F32, kind="Internal").ap()
        ssd_phase(nc, P, A, banks, K, xn_v, io, layer, with_ctx)
    if 'lru' in mixers:
        io['w_lru'] = nc.dram_tensor("w_lru", [D, 192], F32, kind="ExternalInput").ap()
        io['lru_par'] = nc.dram_tensor("lru_par", [128, 9], F32, kind="ExternalInput").ap()
        io['lru_gw'] = nc.dram_tensor("lru_gw", [64, 256], F32, kind="ExternalInput").ap()
        lru_phase(nc, P, A, banks, K, xn_v, io, layer, with_ctx)


def _add_gla(nc, P, A, banks, K, xn_v, io, layer, with_ctx):
    io['w_gla'] = nc.dram_tensor("w_gla", [D, 320], F32, kind="ExternalInput").ap()
    io['gla_par'] = nc.dram_tensor("gla_par", [64, 2], F32, kind="ExternalInput").ap()
    io['gla_w2bd'] = nc.dram_tensor("gla_w2bd", [32, 64], F32, kind="ExternalInput").ap()
    gla_phase(nc, P, A, banks, K, xn_v, io, layer, with_ctx)


def prep_gla(inp, l, h):
    w_in_l = inp['w_in'][l]
    W = np.zeros((D, 288), np.float32)
    q = w_in_l[:, OFF_A + h * 32:OFF_A + h * 32 + 32]
    k = w_in_l[:, OFF_A + 128 + h * 32:OFF_A + 128 + h * 32 + 32]
    W[:, 0:32] = q
    W[:, 32:64] = q
    W[:, 64:96] = k
    W[:, 96:128] = k
    W[:, 128:144] = w_in_l[:, OFF_A + 512:OFF_A + 528]
    W[:, 144:160] = w_in_l[:, OFF_A + 528:OFF_A + 544]
    W[:, 192:256] = w_in_l[:, OFF_A + 544 + h * 64:OFF_A + 544 + h * 64 + 64]
    W[:, 256:288] = 0
    Wv = w_in_l[:, OFF_A + 256 + h * 64:OFF_A + 256 + h * 64 + 64]
    W2 = np.zeros((D, 320), np.float32)
    W2[:, 0:256] = W[:, 0:256]
    W2[:, 256:320] = Wv
    par = np.zeros((64, 2), np.float32)
    par[0:32, 0] = inp['gla_b2'][l][0][h * 32:(h + 1) * 32]
    par[32:64, 0] = inp['gla_b2'][l][1][h * 32:(h + 1) * 32]
    par[:, 1] = inp['gla_norm_w'][l]
    w2bd = np.zeros((32, 64), np.float32)
    w2bd[0:16, 0:32] = inp['gla_w2'][l][0][:, h * 32:(h + 1) * 32]
    w2bd[16:32, 32:64] = inp['gla_w2'][l][1][:, h * 32:(h + 1) * 32]
    return {"w_gla": W2, "gla_par": par, "gla_w2bd": w2bd}


def prep_ssd(inp, l, h):
    w_in_l = inp['w_in'][l]
    gr = h // 2
    W = np.zeros((D, 322), np.float32)
    cx = slice(OFF_D + h * 64, OFF_D + h * 64 + 64)
    cB = slice(OFF_D + 256 + gr * 64, OFF_D + 256 + gr * 64 + 64)
    cC = slice(OFF_D + 384 + gr * 64, OFF_D + 384 + gr * 64 + 64)
    W[:, 0:64] = w_in_l[:, cx]
    W[:, 64:128] = w_in_l[:, cB]
    W[:, 128:192] = w_in_l[:, cC]
    W[:, 192:256] = w_in_l[:, cC]
    W[:, 256:320] = w_in_l[:, OFF_D + 520 + h * 64:OFF_D + 520 + h * 64 + 64]
    W[:, 320] = w_in_l[:, OFF_D + 512 + h]
    W[:, 321] = w_in_l[:, OFF_D + 516 + h]
    par = np.zeros((128, 5), np.float32)
    par[:, 0] = inp['ssd_dt_bias'][l][0][h]
    par[:, 1] = inp['ssd_dt_bias'][l][1][h]
    par[:, 2] = inp['ssd_a_log'][l][0][h]
    par[:, 3] = inp['ssd_a_log'][l][1][h]
    par[:, 4] = inp['ssd_d'][l][h]
    cw, cb = inp['ssd_conv_w'][l], inp['ssd_conv_b'][l]
    cpar = np.zeros((128, 2, 5), np.float32)
    ch = [np.r_[h * 64:h * 64 + 64, 256 + gr * 64:256 + gr * 64 + 64],
          np.r_[384 + gr * 64:384 + gr * 64 + 64, 384 + gr * 64:384 + gr * 64 + 64]]
    for g in range(2):
        cpar[:, g, 0:4] = cw[:, ch[g]].T
        cpar[:, g, 4] = cb[ch[g]]
    return {"w_ssd": W, "ssd_par": par, "ssd_cpar": cpar}


def prep_lru(inp, l, h):
    w_in_l = inp['w_in'][l]
    W = np.zeros((D, 192), np.float32)
    xs = w_in_l[:, OFF_B + h * 64:OFF_B + h * 64 + 64]
    W[:, 0:64] = xs
    W[:, 64:128] = xs
    W[:, 128:192] = w_in_l[:, OFF_B + 256 + h * 64:OFF_B + 256 + h * 64 + 64]
    sl = slice(h * 64, h * 64 + 64)
    par = np.zeros((128, 9), np.float32)
    for d in range(2):
        rows = slice(64 * d, 64 * d + 64)
        par[rows, 0:4] = inp['lru_conv_w'][l][:, sl].T
        par[rows, 4] = inp['lru_conv_b'][l][sl]
        par[rows, 5] = inp['lru_ba'][l][d][sl]
        par[rows, 6] = inp['lru_bx'][l][d][sl]
        par[rows, 7] = inp['lru_lam'][l][d][sl]
    gw = np.zeros((64, 256), np.float32)
    for d in range(2):
        gw[:, 64 * d:64 * d + 64] = inp['lru_wa'][l][d][h]
        gw[:, 128 + 64 * d:128 + 64 * d + 64] = inp['lru_wx'][l][d][h]
    return {"w_lru": W, "lru_par": par, "lru_gw": gw}


def mix_inputs(inp, l, h, mixers, cosT, sinT, cst):
    m = dict(cst)
    if 'ssd' in mixers:
        m.update(prep_ssd(inp, l, h))
    if 'gla' in mixers:
        m.update(prep_gla(inp, l, h))
    if 'attn' in mixers:
        m.update({"w_attn": prep_w_attn(inp['w_in'][l], h), "cosT": cosT, "sinT": sinT,
                  "diff_lam": np.ascontiguousarray(inp['diff_lam'][l].reshape(-1)),
                  "diff_subln_w": np.ascontiguousarray(inp['diff_subln_w'][l].reshape(64, 1))})
    if 'lru' in mixers:
        m.update(prep_lru(inp, l, h))
    return m


def _rt_of(hl, hc, b, q):
    return np.ascontiguousarray(np.concatenate([hc[b, 64 * q:64 * q + 64].T, hl[b, 2048 * q:2048 * (q + 1)].T], axis=1))


def _gather_cols(parts):
    return np.ascontiguousarray(np.concatenate([p[:, 0:64] for p in parts] + [p[:, 64:] for p in parts], axis=1))


def kernel_unfused(**inp):
    inp = {k: np.asarray(v) for k, v in inp.items()}
    x, ctx, c, c_ctx = inp['x'], inp['ctx'], inp['c'], inp['c_ctx']
    cosT, sinT = rope_tables()
    cst = mix_consts()
    cvecs = [np.ascontiguousarray(np.stack([pk(c[b]), pk(c_ctx)], axis=2)) for b in range(2)]
    maps = []
    for core in range(8):
        b, q = core // 4, core % 4
        maps.append({"RT": _rt_of(x, ctx, b, q), "cvec": cvecs[b], "wmod_n": inp['w_mod'][0],
                     "bmod_n": pk(inp['b_mod'][0]), "normw": pk(inp['norm_w'][0])})
    res = _launch(lambda nc, P: build_tok_kernel(nc, P, True, False, False), maps)
    RT = [m["RT"] for m in maps]
    xnT = [np.asarray(r["xnT"]) for r in res]
    out = None
    for l in range(2):
        last = (l == 1)
        xn_full = [_gather_cols(xnT[4 * b:4 * b + 4]) for b in range(2)]
        maps = []
        for core in range(8):
            b, h = core // 4, core % 4
            m = {"xn": xn_full[b]}
            m.update(mix_inputs(inp, l, h, ('gla', 'lru', 'attn', 'ssd'), cosT, sinT, cst))
            maps.append(m)
        res = _launch(lambda nc, P: build_mix_kernel(nc, P, l, not last, ('gla', 'lru', 'attn', 'ssd')), maps)
        yT = [np.asarray(r["yT"]) for r in res]
        maps = []
        for core in range(8):
            b, q = core // 4, core % 4
            yb = np.stack([yT[4 * b + h] for h in range(4)], axis=1).reshape(1024, NTOK)
            yq = np.ascontiguousarray(np.concatenate([yb[:, 64 * q:64 * q + 64],
                                                      yb[:, CTXL + 2048 * q:CTXL + 2048 * (q + 1)]], axis=1))
            m = {"RT": RT[core], "cvec": cvecs[b], "yT": yq, "wout": inp['w_out'][l], "ssdnw": pk(inp['ssd_norm_w'][l]),
                 "wmod_g": inp['w_mod'][l], "bmod_g": pk(inp['b_mod'][l])}
            if last:
                m["fnw"] = pk(inp['final_norm_w'])
            else:
                m.update({"wmod_n": inp['w_mod'][l + 1], "bmod_n": pk(inp['b_mod'][l + 1]), "normw": pk(inp['norm_w'][l + 1])})
            maps.append(m)
        res = _launch(lambda nc, P: build_tok_kernel(nc, P, False, last, True), maps)
        if last:
            out = np.zeros((2, SEQ, D), np.float32)
            for core in range(8):
                b, q = core // 4, core % 4
                out[b, 2048 * q:2048 * (q + 1), :] = np.asarray(res[core]["outT"]).T
        else:
            RT = [np.asarray(r["RTout"]) for r in res]
            xnT = [np.asarray(r["xnT"]) for r in res]
    return out

CHUNKS = [(0, 256)] + [(256 + 2048 * k, 2048) for k in range(4)]
GROUPS = [[0, 1, 2, 3], [4, 5, 6, 7]]


def fs_mod(nc, P, K, banks, A, cvec_d, wmod_d, bmod_d, name):
    cT = A.alloc([8, 2], F32)
    P.dma('sp', cT[:], cvec_d)
    e = A.alloc([8, 2], F32)
    P.act(e[:], cT[:], AF.Exp, scale=-1.0)
    P.ts('dve', e[:], e[:], 1.0, None, ALU.add)
    P.recip(e[:], e[:])
    sc = A.alloc([8, 2], F32)
    P.tt('dve', sc[:], cT[:], e[:], ALU.mult)
    bT = A.alloc([6], F32)
    P.dma('sp', bT[:], bmod_d)
    wb = A.alloc([8, 768], F32)
    for kc in range(8):
        P.dma('sp' if kc % 2 == 0 else 'act', wb[:, kc, :], wmod_d[kc * 128:(kc + 1) * 128, :])
    ps = banks.b[7]
    for fc in range(6):
        for kc in range(8):
            P.mm(ps[:, fc * 2:fc * 2 + 2], wb[:, kc, fc * 128:(fc + 1) * 128], sc[:, kc, :], start=(kc == 0), stop=(kc == 7))
    modT = P.sbuf(name, [128, 6, 2], F32)
    P.tt('dve', modT[:], ps[:, 0:12].rearrange("p (a b) -> p a b", b=2), bT[:].unsqueeze(2).to_broadcast([128, 6, 2]), ALU.add)
    return modT


def fs_token_phase(nc, P, K, banks, A, io, L, mode):
    A.reset()
    D_ = io['dram']
    first, last = mode == 'first', mode == 'last'
    if not first:
        gateT = fs_mod(nc, P, K, banks, A, io['cvec'], io['wmod'][L], io['bmod'][L], "modT_g%d" % L)
        wo32 = A.alloc([8, 256], F32)
        for kc in range(8):
            P.dma('sp' if kc % 2 == 0 else 'act', wo32[:, kc, :], io['wout'][L][kc * 128:(kc + 1) * 128, :])
        wo = A.alloc([8, 256], BF16)
        P.copy('dve', wo[:], wo32[:])
        sw = A.alloc([4], F32)
        P.dma('sp', sw[:], io['ssdnw'][L])
    if not last:
        LN = 0 if first else L + 1
        modN = fs_mod(nc, P, K, banks, A, io['cvec'], io['wmod'][LN], io['bmod'][LN], "modT_n%d" % LN)
        nw = A.alloc([2], F32)
        P.dma('sp', nw[:], io['normw'][LN])
        Acoef = A.alloc([2, 2], F32)
        P.ts('dve', Acoef[:], modN[:, 2:4, :], 1.0, None, ALU.add)
        P.tt('dve', Acoef[:], Acoef[:], nw[:].unsqueeze(2).to_broadcast([128, 2, 2]), ALU.mult)
    else:
        fw = A.alloc([2], F32)
        P.dma('sp', fw[:], io['fnw'])
    Rb = [A.alloc([2, 2048], F32) for _ in range(2)]
    yb_ = [A.alloc([8, 512], BF16) for _ in range(2)]
    sq = [A.alloc([512], BF16) for _ in range(4)]
    rst = [A.alloc([512], F32) for _ in range(2)]
    t1 = [A.alloc([512], F32) for _ in range(4)]
    ssr = [A.alloc([2048], F32) for _ in range(2)]
    ssg = [A.alloc([2048], F32) for _ in range(2)]
    xo = [A.alloc([2, 512], BF16) for _ in range(2)]
    Rsrc = io['RT_in'] if first else D_['Rs']
    Rsrc_v = Rsrc.rearrange("(fc p) t -> p fc t", p=128)
    Rs_v = D_['Rs'].rearrange("(fc p) t -> p fc t", p=128)
    stage = 'n%d' % (0 if first else L + 1) if not last else 'fin'
    chunks = [(ci, t0, n) for ci, (t0, n) in enumerate(CHUNKS) if not (last and ci == 0)]
    jobs = []
    for (ci, t0, n) in chunks:
        subs = [(s0, min(512, n - s0)) for s0 in range(0, n, 512)]
        for si, (s0, m_) in enumerate(subs):
            jobs.append(dict(ci=ci, t0=t0, n=n, s0=s0, m=m_, first=(si == 0), lastsub=(si == len(subs) - 1)))
    ybuf3 = yb_ + [A.alloc([8, 512], BF16)]

    def P1(j, ji):
        ci, t0, n, s0, m = j['ci'], j['t0'], j['n'], j['s0'], j['m']
        R = Rb[ci % 2]
        if j['first']:
            P.dma('sp', R[:, :, 0:n], Rsrc_v[:, :, t0:t0 + n])
        if first:
            return
        gy = D_['gy%d' % L][ci].rearrange("(kc p) t -> p kc t", p=128)
        yt = ybuf3[ji % 3]
        P.dma('act', yt[:, :, 0:m], gy[:, :, s0:s0 + m])
        ps = banks.b[0 + ji % 2]
        for i, kc in enumerate((1, 3, 5, 7)):
            s_ = sq[i % 2]
            P.act(s_[64:128, 0:m], yt[64:128, kc, 0:m], AF.Square)
            P.mm(ps[:, 0:m], K['onesb'][64:128, :], s_[64:128, 0:m], start=(i == 0), stop=(i == 3))
        r_ = rst[ji % 2]
        P.act(r_[:, 0:m], ps[:, 0:m], AF.Ln, scale=1.0 / 256, bias=K['eps'][:])
        P.act(r_[:, 0:m], r_[:, 0:m], AF.Exp, scale=-0.5)
        for i, kc in enumerate((1, 3, 5, 7)):
            P.stt(yt[64:128, kc, 0:m], yt[64:128, kc, 0:m], sw[64:128, i:i + 1], r_[64:128, 0:m], ALU.mult, ALU.mult)

    def P2(j, ji):
        if first:
            return
        ci, s0, m = j['ci'], j['s0'], j['m']
        isctx = 1 if ci == 0 else 0
        R = Rb[ci % 2]
        yt = ybuf3[ji % 3]
        for fc in range(2):
            po = banks.b[2 + (2 * ji + fc) % 4]
            for kc in range(8):
                P.mm(po[:, 0:m], wo[:, kc, fc * 128:(fc + 1) * 128], yt[:, kc, 0:m], start=(kc == 0), stop=(kc == 7))
            P.stt(R[:, fc, s0:s0 + m], po[:, 0:m], gateT[:, 4 + fc, isctx:isctx + 1], R[:, fc, s0:s0 + m], ALU.mult, ALU.add)

    def P3(j, ji):
        ci, t0, n, s0, m = j['ci'], j['t0'], j['n'], j['s0'], j['m']
        R = Rb[ci % 2]
        srow = ssr[ci % 2]
        pss = banks.b[6 + ji % 2]
        for fc in range(2):
            s_ = sq[2 + fc]
            P.act(s_[:, 0:m], R[:, fc, s0:s0 + m], AF.Square)
            P.mm(pss[:, 0:m], K['onesb'][:, :], s_[:, 0:m], start=(fc == 0), stop=(fc == 1))
        P.copy('dve', srow[0:1, s0:s0 + m], pss[0:1, 0:m])
        if j['lastsub']:
            P.dma('sp', D_['ssb_' + stage][:, t0:t0 + n], srow[0:1, 0:n])
            P.dma('sp', Rs_v[:, :, t0:t0 + n], R[:, :, 0:n])

    nj = len(jobs)
    for it in range(nj + 2):
        if it < nj:
            P1(jobs[it], it)
        if 0 <= it - 1 < nj:
            P2(jobs[it - 1], it - 1)
        if 0 <= it - 2 < nj:
            P3(jobs[it - 2], it - 2)
    P.collective("AllGather", [D_['ssb_' + stage]], [D_['ssg_' + stage]], GROUPS)
    for (ci, t0, n) in chunks:
        isctx = 1 if ci == 0 else 0
        R = Rb[ci % 2]
        P.dma('sp', R[:, :, 0:n], Rs_v[:, :, t0:t0 + n])
        subs = [(s0, min(512, n - s0)) for s0 in range(0, n, 512)]
        sg_ = ssg[ci % 2]
        P.dma('act', sg_[0:4, 0:n], D_['ssg_' + stage][:, t0:t0 + n])
        for si, (s0, m) in enumerate(subs):
            pt = banks.b[6 + si % 2]
            P.mm(pt[:, 0:m], K['ones'][0:4, :], sg_[0:4, s0:s0 + m])
            r_ = rst[si % 2]
            P.act(r_[:, 0:m], pt[:, 0:m], AF.Ln, scale=1.0 / D, bias=K['eps'][:])
            P.act(r_[:, 0:m], r_[:, 0:m], AF.Exp, scale=-0.5)
            if not last:
                x_ = xo[si % 2]
                for fc in range(2):
                    t_ = t1[fc]
                    P.stt(t_[:, 0:m], R[:, fc, s0:s0 + m], Acoef[:, fc, isctx:isctx + 1], r_[:, 0:m], ALU.mult, ALU.mult)
                    P.act(x_[:, fc, 0:m], t_[:, 0:m], AF.Identity, bias=modN[:, fc, isctx:isctx + 1])
                xb_v = D_['xnb_' + stage][ci].rearrange("(fc p) t -> p fc t", p=128)
                P.dma('sp', xb_v[:, :, s0:s0 + m], x_[:, :, 0:m])
            else:
                o_v = io['outT'].rearrange("(fc p) t -> p fc t", p=128)
                for fc in range(2):
                    t_ = t1[(si * 2 + fc) % 4]
                    P.stt(t_[:, 0:m], R[:, fc, s0:s0 + m], fw[:, fc:fc + 1], r_[:, 0:m], ALU.mult, ALU.mult)
                    P.dma('sp', o_v[:, fc, t0 - CTXL + s0:t0 - CTXL + s0 + m], t_[:, 0:m], final=True)
        if not last:
            P.collective("AllGather", [D_['xnb_' + stage][ci]], [D_['xng_' + stage][ci]], GROUPS)


def build_fused(nc, P):
    banks = Banks(P)
    K = load_consts_tok(nc, P)
    epst = P.sbuf("eps_t", [128, 1], F32)
    P.memset('pool', epst[:], EPS)
    K['eps'] = epst
    one = P.sbuf("one_t", [128, 1], F32)
    P.memset('pool', one[:], 1.0)
    K['one'] = one
    for nm, shp in (('selZ', [128, 64]), ('ident', [128, 128]), ('scanmask', [64, 512]), ('maskF', [128, 512]),
                    ('maskB', [128, 512]), ('scanmask2', [128, 2, 128]), ('sel2', [2, 128]), ('selrow', [2, 2, 128])):
        d_ = nc.dram_tensor(nm, shp, F32, kind="ExternalInput").ap()
        t_ = P.sbuf(nm + "_sb", shp, F32)
        P.dma('sp', t_[:], d_)
        K[nm] = t_
    identb = P.sbuf("identb_sb", [128, 128], BF16)
    P.copy('pool', identb[:], K['ident'][:])
    K['identb'] = identb
    onesb = P.sbuf("onesb_sb", [128, 128], BF16)
    P.memset('pool', onesb[:], 1.0)
    K['onesb'] = onesb
    for nm in ('negF', 'negB'):
        d_ = nc.dram_tensor(nm, [128, 512], F32, kind="ExternalInput").ap()
        t_ = P.sbuf(nm + "_f", [128, 512], F32)
        P.dma('sp', t_[:], d_)
        tb_ = P.sbuf(nm + "_sb", [128, 512], BF16)
        P.copy('pool', tb_[:], t_[:])
        K[nm] = tb_
    A = P.arena("arena", 184 * 1024)

    def din(name, shape, dt=F32):
        return nc.dram_tensor(name, list(shape), dt, kind="ExternalInput").ap()

    def dscr(name, shape, dt=F32):
        return nc.dram_tensor(name, list(shape), dt).ap()

    io = {'RT_in': din("RT", [256, NTOK]), 'cvec': din("cvec", [128, 8, 2]),
          'wmod': [din("wmod%d" % l, [D, 768]) for l in range(2)], 'bmod': [din("bmod%d" % l, [128, 6]) for l in range(2)],
          'normw': [din("normw%d" % l, [128, 2]) for l in range(2)], 'wout': [din("wout%d" % l, [D, 256]) for l in range(2)],
          'ssdnw': [din("ssdnw%d" % l, [128, 4]) for l in range(2)], 'fnw': din("fnw", [128, 2]),
          'outT': nc.dram_tensor("outT", [256, SEQ], F32, kind="ExternalOutput").ap()}
    Dm = {'Rs': dscr("Rs", [256, NTOK])}
    for stage in ('n0', 'n1', 'fin'):
        Dm['ssb_' + stage] = dscr("ssb_%s" % stage, [1, NTOK])
        Dm['ssg_' + stage] = dscr("ssg_%s" % stage, [4, NTOK])
    for stage in ('n0', 'n1'):
        Dm['xnb_' + stage] = [dscr("xnb_%s_%d" % (stage, ci), [256, n], BF16) for ci, (t0, n) in enumerate(CHUNKS)]
        Dm['xng_' + stage] = [dscr("xng_%s_%d" % (stage, ci), [D, n], BF16) for ci, (t0, n) in enumerate(CHUNKS)]
    for l in range(2):
        Dm['yb%d' % l] = [dscr("yb%d_%d" % (l, ci), [256, n], BF16) for ci, (t0, n) in enumerate(CHUNKS)]
        Dm['gy%d' % l] = [dscr("gy%d_%d" % (l, ci), [D, n], BF16) for ci, (t0, n) in enumerate(CHUNKS)]
    io['dram'] = Dm
    cosT = din("cosT", [128, SEQ])
    sinT = din("sinT", [128, SEQ])
    fs_token_phase(nc, P, K, banks, A, io, 0, 'first')
    for l in range(2):
        last = (l == 1)
        with_ctx = not last
        xng = Dm['xng_n%d' % l]

        def xn_load(xb, t0, n, xng=xng):
            if t0 < CTXL:
                P.dma('sp', xb[:, :, 0:n], xng[0].rearrange("(kc p) t -> p kc t", p=128)[:, :, t0:t0 + n])
            else:
                k = (t0 - CTXL) // 2048
                c0 = (t0 - CTXL) % 2048
                P.dma('sp', xb[:, :, 0:n], xng[k + 1].rearrange("(kc p) t -> p kc t", p=128)[:, :, c0:c0 + n])

        ybl = Dm['yb%d' % l]

        def y_store(mi, t0, n, src, ybl=ybl):
            if t0 < CTXL:
                P.dma('pool', ybl[0][mi * 64:(mi + 1) * 64, t0:t0 + n], src)
            else:
                k = (t0 - CTXL) // 2048
                c0 = (t0 - CTXL) % 2048
                P.dma('pool', ybl[k + 1][mi * 64:(mi + 1) * 64, c0:c0 + n], src)

        mio = {'y_store': y_store, 'cosT': cosT, 'sinT': sinT}
        mio['w_gla'] = din("w_gla%d" % l, [D, 320])
        mio['gla_par'] = din("gla_par%d" % l, [64, 2])
        mio['gla_w2bd'] = din("gla_w2bd%d" % l, [32, 64])
        mio['w_lru'] = din("w_lru%d" % l, [D, 192])
        mio['lru_par'] = din("lru_par%d" % l, [128, 9])
        mio['lru_gw'] = din("lru_gw%d" % l, [64, 256])
        mio['w_ssd'] = din("w_ssd%d" % l, [D, 322])
        mio['ssd_par'] = din("ssd_par%d" % l, [128, 5])
        mio['ssd_cpar'] = din("ssd_cpar%d" % l, [128, 2, 5])
        mio['ssd_scr'] = dscr("ssd_scr%d" % l, [6, NTOK])
        mio['w_attn'] = din("w_attn%d" % l, [D, 640])
        mio['diff_lam'] = din("diff_lam%d" % l, [128])
        mio['diff_subln_w'] = din("diff_subln_w%d" % l, [64, 1])
        gla_phase(nc, P, A, banks, K, xn_load, mio, l, with_ctx)
        lru_phase(nc, P, A, banks, K, xn_load, mio, l, with_ctx)
        ssd_phase(nc, P, A, banks, K, xn_load, mio, l, with_ctx)
        def after_q(q0, nq, l=l):
            if q0 < CTXL:
                P.collective("AllGather", [Dm['yb%d' % l][0]], [Dm['gy%d' % l][0]], GROUPS)
            elif (q0 - CTXL + nq) % 2048 == 0:
                k = (q0 - CTXL) // 2048
                P.collective("AllGather", [Dm['yb%d' % l][k + 1]], [Dm['gy%d' % l][k + 1]], GROUPS)
        mio['after_q'] = after_q
        attention_phase(nc, P, A, banks, K, xn_load, mio, l, with_ctx)
        fs_token_phase(nc, P, K, banks, A, io, l, 'last' if last else 'mid')


def fused_inputs(inp, core, cosT, sinT, cst):
    b, q = core // 4, core % 4
    h = q
    x, ctx, c, c_ctx = inp['x'], inp['ctx'], inp['c'], inp['c_ctx']
    fsl = slice(256 * q, 256 * q + 256)
    m = dict(cst)
    m['RT'] = np.ascontiguousarray(np.concatenate([ctx[b][:, fsl].T, x[b][:, fsl].T], axis=1))
    m['cvec'] = np.ascontiguousarray(np.stack([pk(c[b]), pk(c_ctx)], axis=2))
    m['cosT'] = cosT
    m['sinT'] = sinT
    m['fnw'] = pk(inp['final_norm_w'][fsl])
    perm = np.array([mi * 256 + hh * 64 + j for hh in range(4) for mi in range(4) for j in range(64)])
    for l in range(2):
        cols = np.r_[256 * q:256 * q + 256, 1024 + 256 * q:1024 + 256 * q + 256, 2048 + 256 * q:2048 + 256 * q + 256]
        m['wmod%d' % l] = np.ascontiguousarray(inp['w_mod'][l][:, cols])
        m['bmod%d' % l] = pk(inp['b_mod'][l][cols])
        m['normw%d' % l] = pk(inp['norm_w'][l][fsl])
        m['wout%d' % l] = np.ascontiguousarray(inp['w_out'][l][perm][:, fsl])
        sw = np.zeros((128, 4), np.float32)
        for i in range(4):
            sw[64:128, i] = inp['ssd_norm_w'][l][i * 64:(i + 1) * 64]
        m['ssdnw%d' % l] = sw
        mi_ = mix_inputs(inp, l, h, ('gla', 'lru', 'attn', 'ssd'), cosT, sinT, cst)
        for k_, v_ in mi_.items():
            if k_ in cst or k_ in ('cosT', 'sinT'):
                continue
            m[k_ + str(l)] = v_
    return m


def kernel(**inp):
    inp = {k: np.asarray(v) for k, v in inp.items()}
    cosT, sinT = rope_tables()
    cst = mix_consts()
    maps = [fused_inputs(inp, core, cosT, sinT, cst) for core in range(8)]
    res = _launch(build_fused, maps)
    out = np.zeros((2, SEQ, D), np.float32)
    for core in range(8):
        b, q = core // 4, core % 4
        out[b, :, 256 * q:256 * q + 256] = np.asarray(res[core]["outT"]).T
    return out
```

```python
import numpy as np
from contextlib import ExitStack
import concourse.bass as bass
import concourse.mybir as mybir
from concourse.bass_utils import run_bass_kernel_spmd

F32 = mybir.dt.float32
BF16 = mybir.dt.bfloat16
AF = mybir.ActivationFunctionType
ALU = mybir.AluOpType
AX = mybir.AxisListType
ENGS = ('pe', 'act', 'dve', 'pool', 'sp')
ESZ = {F32: 4, BF16: 2}


class Prog:
    def __init__(self, nc, stack, n_dma_sems=40):
        self.nc = nc
        self.stack = stack
        self.sem = {e: stack.enter_context(nc.semaphore('sem_' + e)) for e in ENGS}
        self.cnt = {e: 0 for e in ENGS}
        self.known = {e: {} for e in ENGS}
        self.stream = {e: [] for e in ENGS}
        self.dsem = [stack.enter_context(nc.semaphore('dsem%d' % i)) for i in range(n_dma_sems)]
        self.dcum = [0] * n_dma_sems
        self.drr = 0
        self.rows = {}
        self.trk = {}
        self.final_events = []
        self.nwaits = 0
        self.ndma = 0
        self.arenas = {}

    def sbuf(self, name, shape, dtype=F32):
        t = self.stack.enter_context(self.nc.sbuf_tensor(name, list(shape), dtype))
        self.rows[name] = int(np.prod(shape[1:])) * ESZ[dtype]
        return t

    def psum(self, name, shape, dtype=F32):
        t = self.stack.enter_context(self.nc.psum_tensor(name, list(shape), dtype))
        self.rows[name] = int(np.prod(shape[1:])) * ESZ[dtype]
        return t

    def arena(self, name, nbytes):
        t = self.sbuf(name, [128, nbytes // 4], F32)
        a = Arena(self, name, t, nbytes)
        self.arenas[name] = a
        return a

    def _box(self, ap):
        b = self._box0(ap)
        a = self.arenas.get(b[0])
        if a is not None:
            rid = a.region_of(b[3], b[4])
            return ((b[0], rid),) + b[1:]
        return b

    def _box0(self, ap):
        name = ap.tensor.name
        es = ESZ.get(ap.dtype, 4)
        off = int(ap.offset) * es
        pairs = ap.ap
        if name in self.rows:
            rs = self.rows[name]
            p0 = off // rs
            pst, pc = pairs[0]
            p1 = p0 + (pc if pst != 0 else 1)
            fo = off % rs
            rest = pairs[1:]
        else:
            p0, p1 = 0, 1
            fo = off
            rest = pairs
        lo = fo
        hi = fo
        for st, c in rest:
            d = st * (c - 1) * es
            if d < 0:
                lo += d
            else:
                hi += d
        return (name, p0, p1, lo, hi + es)

    @staticmethod
    def _ov(a, b):
        return a[1] < b[2] and b[1] < a[2] and a[3] < b[4] and b[3] < a[4]

    @staticmethod
    def _inside(a, b):
        return a[1] >= b[1] and a[2] <= b[2] and a[3] >= b[3] and a[4] <= b[4]

    def _deps(self, reads, writes):
        deps = []
        for ap in reads:
            b = self._box(ap)
            t = self.trk.get(b[0])
            if t:
                for (wb, ev, clk) in t['w']:
                    if self._ov(b, wb):
                        deps.append((ev, clk, 'raw'))
        for ap in writes:
            b = self._box(ap)
            t = self.trk.get(b[0])
            if t:
                for (wb, ev, clk) in t['w']:
                    if self._ov(b, wb):
                        deps.append((ev, clk, 'waw'))
                for (rb, ev, clk) in t['r']:
                    if self._ov(b, rb):
                        deps.append((ev, clk, 'war'))
        return deps

    @staticmethod
    def _merge(lst):
        merged = {}
        for (rb, rev, rclk) in lst:
            m = merged.get(rev[0])
            if m is None:
                merged[rev[0]] = (rb, rev, rclk)
            else:
                mb, mev, mclk = m
                nb = (rb[0], min(rb[1], mb[1]), max(rb[2], mb[2]), min(rb[3], mb[3]), max(rb[4], mb[4]))
                merged[rev[0]] = (nb, rev, rclk) if rev[1] > mev[1] else (nb, mev, mclk)
        return list(merged.values())

    def _record(self, reads, writes, ev, clk):
        for ap in reads:
            b = self._box(ap)
            t = self.trk.setdefault(b[0], {'w': [], 'r': []})
            rl = t['r']
            for i, (rb, rev, rclk) in enumerate(rl):
                if rev[0] == ev[0] and self._inside(rb, b):
                    rl[i] = (b, ev, clk)
                    break
            else:
                rl.append((b, ev, clk))
                if len(rl) > 40:
                    t['r'] = self._merge(rl)
        for ap in writes:
            b = self._box(ap)
            t = self.trk.setdefault(b[0], {'w': [], 'r': []})
            t['w'] = [e for e in t['w'] if not self._inside(e[0], b)]
            t['r'] = [e for e in t['r'] if not self._inside(e[0], b)]
            t['w'].append((b, ev, clk))
            if len(t['w']) > 40:
                t['w'] = self._merge(t['w'])

    def _waits_for(self, eng, deps):
        kn = self.known[eng]
        waits = {}
        for (ev, clk, kind) in deps:
            key, val = ev
            if key == eng:
                if eng == 'pe':
                    continue
                if eng in ('act', 'dve') and kind != 'raw':
                    continue
            if kn.get(key, 0) >= val:
                continue
            if waits.get(key, 0) < val:
                waits[key] = val
        for (ev, clk, kind) in deps:
            key, val = ev
            if key in waits and waits[key] >= val:
                for k2, v2 in clk.items():
                    if kn.get(k2, 0) < v2:
                        kn[k2] = v2
        for k, v in waits.items():
            if kn.get(k, 0) < v:
                kn[k] = v
        return list(waits.items())

    def op(self, eng, fn, reads=(), writes=()):
        deps = self._deps(reads, writes)
        waits = self._waits_for(eng, deps)
        self.cnt[eng] += 1
        ev = (eng, self.cnt[eng])
        clk = dict(self.known[eng])
        clk[eng] = self.cnt[eng]
        self.stream[eng].append((waits, fn, ('e', eng)))
        self._record(reads, writes, ev, clk)
        self.nwaits += len(waits)
        return ev

    def dma(self, q, out, in_, final=False, **kw):
        i = self.drr
        self.drr = (self.drr + 1) % len(self.dsem)
        deps = self._deps([in_], [out])
        key = ('d', i)
        deps.append(((key, self.dcum[i]), {}, 'raw'))
        waits = self._waits_for(q, deps)
        self.dcum[i] += 16
        ev = (key, self.dcum[i])
        clk = dict(self.known[q])
        clk[key] = self.dcum[i]
        self.stream[q].append((waits, (lambda e, out=out, in_=in_, kw=kw: e.dma_start(out=out, in_=in_, **kw)), ('d', i)))
        self._record([in_], [out], ev, clk)
        if final:
            self.final_events.append(ev)
        self.nwaits += len(waits)
        self.ndma += 1
        return ev

    def collective(self, kind, ins, outs, groups, q='pool'):
        if not hasattr(self, 'csem'):
            self.csem = []
            self.ccum = []
        self.csem.append(self.stack.enter_context(self.nc.semaphore('csem%d' % len(self.csem))))
        self.ccum.append(0)
        i = len(self.csem) - 1
        deps = self._deps(list(ins), list(outs))
        key = ('c', i)
        waits = self._waits_for(q, deps)
        self.ccum[i] += 1
        ev = (key, self.ccum[i])
        clk = dict(self.known[q])
        clk[key] = self.ccum[i]
        self.stream[q].append((waits, (lambda e: e.collective_compute(kind, ALU.bypass, replica_groups=groups,
                                                                      ins=[a.opt() for a in ins], outs=[a.opt() for a in outs])),
                               ('c', i)))
        self._record(list(ins), list(outs), ev, clk)
        self.nwaits += len(waits)
        return ev

    def finish(self, eng='sp'):
        self.stream[eng].append((list(self.final_events), None, None))

    def _semobj(self, key):
        if isinstance(key, tuple):
            return self.dsem[key[1]] if key[0] == 'd' else self.csem[key[1]]
        return self.sem[key]

    def emit(self):
        for e in ENGS:
            assert self.cnt[e] < 60000, (e, self.cnt[e])

        def replay(name, eng):
            for (waits, fn, inc) in self.stream[name]:
                for (key, val) in waits:
                    eng.wait_ge(self._semobj(key), val)
                if fn is None:
                    continue
                ins = fn(eng)
                if inc[0] == 'e':
                    ins.then_inc(self.sem[inc[1]], 1)
                elif inc[0] == 'c':
                    ins.then_inc(self.csem[inc[1]])
                else:
                    ins.then_inc(self.dsem[inc[1]], 16)

        with self.nc.Block() as block:
            @block.sync
            def _(e):
                replay('sp', e)

            @block.scalar
            def _(e):
                replay('act', e)

            @block.vector
            def _(e):
                replay('dve', e)

            @block.gpsimd
            def _(e):
                replay('pool', e)

            @block.tensor
            def _(e):
                replay('pe', e)

    def mm(self, out, lhsT, rhs, start=True, stop=True):
        return self.op('pe', lambda e: e.matmul(out, lhsT, rhs, start=start, stop=stop),
                       reads=[lhsT, rhs], writes=[out])

    def transpose(self, out, in_, ident):
        return self.op('pe', lambda e: e.transpose(out, in_, ident), reads=[in_, ident], writes=[out])

    def act(self, out, in_, func, bias=None, scale=None, accum_out=None):
        reads = [in_]
        kw = {}
        if bias is not None:
            kw['bias'] = bias
            if not isinstance(bias, (int, float)):
                reads.append(bias)
        if scale is not None:
            kw['scale'] = scale
            if not isinstance(scale, (int, float)):
                reads.append(scale)
        writes = [out]
        if accum_out is not None:
            kw['accum_out'] = accum_out
            writes.append(accum_out)
        return self.op('act', lambda e: e.activation(out, in_, func, **kw), reads=reads, writes=writes)

    def tt(self, eng, out, in0, in1, op):
        return self.op(eng, lambda e: e.tensor_tensor(out, in0, in1, op), reads=[in0, in1], writes=[out])

    def ts(self, eng, out, in0, s1, s2, op0, op1=None):
        reads = [in0]
        for s in (s1, s2):
            if s is not None and not isinstance(s, (int, float)):
                reads.append(s)
        if op1 is None:
            return self.op(eng, lambda e: e.tensor_scalar(out, in0, s1, None, op0), reads=reads, writes=[out])
        return self.op(eng, lambda e: e.tensor_scalar(out, in0, s1, s2, op0, op1), reads=reads, writes=[out])

    def stt(self, out, in0, scalar, in1, op0, op1):
        reads = [in0, in1]
        if not isinstance(scalar, (int, float)):
            reads.append(scalar)
        return self.op('dve', lambda e: e.scalar_tensor_tensor(out, in0, scalar, in1, op0, op1),
                       reads=reads, writes=[out])

    def copy(self, eng, out, in_):
        if eng == 'act':
            return self.op('act', lambda e: e.copy(out, in_), reads=[in_], writes=[out])
        return self.op(eng, lambda e: e.tensor_copy(out, in_), reads=[in_], writes=[out])

    def memset(self, eng, ap, val):
        return self.op(eng, lambda e: e.memset(ap, val), reads=[], writes=[ap])

    def scan(self, out, d0, d1, init, op0=ALU.mult, op1=ALU.add):
        reads = [d0, d1]
        if not isinstance(init, (int, float)):
            reads.append(init)
        return self.op('dve', lambda e: e.tensor_tensor_scan(out, d0, d1, init, op0, op1), reads=reads, writes=[out])

    def recip(self, out, in_):
        return self.op('dve', lambda e: e.reciprocal(out, in_), reads=[in_], writes=[out])


class Arena:
    def __init__(self, P, name, t, nbytes):
        self.P, self.name, self.t, self.nbytes = P, name, t, nbytes
        self.top = 0
        self.regions = []
        self.old = []
        self.nrid = 0

    def reset(self):
        self.old.extend(self.regions)
        self.regions = []
        self.top = 0

    def mark(self):
        return (self.top, len(self.regions))

    def release(self, mk):
        top, nreg = mk
        self.old.extend(self.regions[nreg:])
        self.regions = self.regions[:nreg]
        self.top = top

    def region_of(self, lo, hi):
        for (a, b, rid) in self.regions:
            if lo >= a and hi <= b:
                return rid
        raise AssertionError("arena access outside any region %s %d %d" % (self.name, lo, hi))

    def alloc(self, shape, dtype=F32):
        n = int(np.prod(shape)) * ESZ[dtype]
        n = (n + 63) // 64 * 64
        lo, hi = self.top, self.top + n
        assert hi <= self.nbytes, ("arena overflow", self.name, hi, self.nbytes)
        self.top = hi
        rid = self.nrid
        self.nrid += 1
        self.regions.append((lo, hi, rid))
        inh = []
        keep = []
        for (a, b, orid) in self.old:
            if a < hi and lo < b:
                t = self.P.trk.get((self.name, orid))
                if t:
                    inh.extend(t['w'])
                    inh.extend(t['r'])
                keep.append((a, b, orid))
            else:
                keep.append((a, b, orid))
        self.old = keep
        if inh:
            full = ((self.name, rid), 0, 128, lo, hi)
            m = Prog._merge([(full, ev, clk) for (_, ev, clk) in inh])
            self.P.trk[(self.name, rid)] = {'w': m, 'r': []}
        v = self.t[:, lo // 4:hi // 4]
        if dtype != F32:
            v = v.bitcast(dtype)
        nel = int(np.prod(shape))
        v = v[:, 0:nel]
        if len(shape) == 2:
            v = v.rearrange("p (a b) -> p a b", a=shape[0])
        elif len(shape) == 3:
            v = v.rearrange("p (a b c) -> p a b c", a=shape[0], b=shape[1])
        return v


def _launch(build, in_maps, n_cores=8, trace=False):
    nc = bass.Bass("TRN2", target_bir_lowering=False)
    with ExitStack() as stack:
        P = Prog(nc, stack)
        build(nc, P)
        P.finish()
        P.emit()
    res = run_bass_kernel_spmd(nc, in_maps, core_ids=list(range(n_cores)), trace=trace)
    if trace:
        return res
    return res.results

D = 1024
SEQ = 8192
CTXL = 256
NTOK = SEQ + CTXL
QT = 2112
EPS = 1e-6
TILES_Q = [(0, 64, 1), (64, 512, 0), (576, 512, 0), (1088, 512, 0), (1600, 512, 0)]


class Banks:
    def __init__(self, P):
        self.pp = [P.psum("pp%d" % i, [128, 1024], F32) for i in range(4)]
        self.b = [self.pp[i // 2][:, (i % 2) * 512:(i % 2 + 1) * 512] for i in range(8)]


def load_consts_tok(nc, P):
    ones = P.sbuf("ones128", [128, 128], F32)
    P.memset('pool', ones[:], 1.0)
    return {'ones': ones}


def compute_mod(nc, P, K, banks, cvec_d, wmod_d, bmod_d, fc_list, modT):
    cT = P.sbuf("cT", [128, 8, 2], F32)
    P.dma('sp', cT[:], cvec_d)
    e = P.sbuf("cTe", [128, 8, 2], F32)
    P.act(e[:], cT[:], AF.Exp, scale=-1.0)
    P.ts('dve', e[:], e[:], 1.0, None, ALU.add)
    P.recip(e[:], e[:])
    sc = P.sbuf("cTs", [128, 8, 2], F32)
    P.tt('dve', sc[:], cT[:], e[:], ALU.mult)
    bT = P.sbuf("bmodT", [128, 24], F32)
    P.dma('sp', bT[:], bmod_d)
    wbuf = [P.sbuf("wmodbuf%d" % i, [128, 8, 512], F32) for i in range(2)]
    ps = banks.b[7]
    groups = sorted(set(fc // 4 for fc in fc_list))
    for gi, g in enumerate(groups):
        wb = wbuf[gi % 2]
        for kc in range(8):
            P.dma('sp' if kc % 2 == 0 else 'act', wb[:, kc, :], wmod_d[kc * 128:(kc + 1) * 128, g * 512:(g + 1) * 512])
        for fc in range(g * 4, g * 4 + 4):
            if fc not in fc_list:
                continue
            for kc in range(8):
                P.mm(ps[:, fc * 2:fc * 2 + 2], wb[:, kc, (fc % 4) * 128:(fc % 4 + 1) * 128], sc[:, kc, :],
                     start=(kc == 0), stop=(kc == 7))
    for fc in fc_list:
        P.tt('dve', modT[:, fc, :], ps[:, fc * 2:fc * 2 + 2], bT[:, fc:fc + 1].to_broadcast([128, 2]), ALU.add)


def norm_coeffs(nc, P, modT, normw_d, name):
    nw = P.sbuf(name + "_nw", [128, 8], F32)
    P.dma('sp', nw[:], normw_d)
    A = P.sbuf(name + "_A", [128, 8, 2], F32)
    P.ts('dve', A[:], modT[:, 8:16, :], 1.0, None, ALU.add)
    P.tt('dve', A[:], A[:], nw[:].unsqueeze(2).to_broadcast([128, 8, 2]), ALU.mult)
    return A


def phase_norm(nc, P, K, banks, RT, A, Bsh, xn_d, tiles, tmp):
    xn_v = xn_d.rearrange("(kc p) t -> p kc t", p=128)
    for ti, (c0, n, isctx) in enumerate(tiles):
        j = 1 if isctx else 0
        ps = banks.b[ti % 2]
        for kc in range(8):
            sq = tmp['sq'][kc % 2]
            P.act(sq[:, 0:n], RT[:, kc, c0:c0 + n], AF.Square)
            P.mm(ps[:, 0:n], K['ones'][:], sq[:, 0:n], start=(kc == 0), stop=(kc == 7))
        rstd = tmp['rstd'][ti % 2]
        P.act(rstd[:, 0:n], ps[:, 0:n], AF.Ln, scale=1.0 / D, bias=K['eps'][:])
        P.act(rstd[:, 0:n], rstd[:, 0:n], AF.Exp, scale=-0.5)
        xb = tmp['xb'][ti % 2]
        for kc in range(8):
            t1 = tmp['t1'][kc % 2]
            P.stt(t1[:, 0:n], RT[:, kc, c0:c0 + n], A[:, kc, j:j + 1], rstd[:, 0:n], ALU.mult, ALU.mult)
            P.act(xb[:, kc, 0:n], t1[:, 0:n], AF.Identity, bias=Bsh[:, kc, j:j + 1])
        P.dma('pool', xn_v[:, :, c0:c0 + n], xb[:, :, 0:n], final=True)


def phase_final(nc, P, K, banks, RT, fnw_d, out_d, tiles, tmp):
    fw = P.sbuf("fnw_sb", [128, 8], F32)
    P.dma('sp', fw[:], fnw_d)
    out_v = out_d.rearrange("(kc p) t -> p kc t", p=128)
    for ti, (c0, n, isctx) in enumerate(tiles):
        ps = banks.b[ti % 2]
        for kc in range(8):
            sq = tmp['sq'][kc % 2]
            P.act(sq[:, 0:n], RT[:, kc, c0:c0 + n], AF.Square)
            P.mm(ps[:, 0:n], K['ones'][:], sq[:, 0:n], start=(kc == 0), stop=(kc == 7))
        rstd = tmp['rstd'][ti % 2]
        P.act(rstd[:, 0:n], ps[:, 0:n], AF.Ln, scale=1.0 / D, bias=K['eps'][:])
        P.act(rstd[:, 0:n], rstd[:, 0:n], AF.Exp, scale=-0.5)
        for kc in range(8):
            P.stt(RT[:, kc, c0:c0 + n], RT[:, kc, c0:c0 + n], fw[:, kc:kc + 1], rstd[:, 0:n], ALU.mult, ALU.mult)
        P.dma('pool', out_v[:, :, c0 - 64:c0 - 64 + n], RT[:, :, c0:c0 + n], final=True)


def phase_outproj(nc, P, K, banks, RT, yT_d, wout_d, ssdw_d, gateT, tiles, tmp):
    wo = P.sbuf("wout_bf", [128, 8, D], BF16)
    for kc in range(8):
        st = tmp['wstage'][kc % 2]
        P.dma('sp' if kc % 2 == 0 else 'act', st[:], wout_d[kc * 128:(kc + 1) * 128, :])
        P.copy('pool', wo[:, kc, :], st[:])
    sw = P.sbuf("ssdnw_sb", [128, 2], F32)
    P.dma('sp', sw[:], ssdw_d)
    yT_v = yT_d.rearrange("(kc p) t -> p kc t", p=128)
    for ti, (c0, n, isctx) in enumerate(tiles):
        j = 1 if isctx else 0
        yb = tmp['yb'][ti % 2]
        P.dma('sp', yb[:, :, 0:n], yT_v[:, :, c0:c0 + n])
        ps = banks.b[2 + ti % 2]
        for i, kc in enumerate((6, 7)):
            sq = tmp['sq'][i]
            P.act(sq[:, 0:n], yb[:, kc, 0:n], AF.Square)
            P.mm(ps[:, 0:n], K['ones'][:], sq[:, 0:n], start=(i == 0), stop=(i == 1))
        rstd = tmp['rstd'][ti % 2]
        P.act(rstd[:, 0:n], ps[:, 0:n], AF.Ln, scale=1.0 / 256, bias=K['eps'][:])
        P.act(rstd[:, 0:n], rstd[:, 0:n], AF.Exp, scale=-0.5)
        for i, kc in enumerate((6, 7)):
            P.stt(yb[:, kc, 0:n], yb[:, kc, 0:n], sw[:, i:i + 1], rstd[:, 0:n], ALU.mult, ALU.mult)
        for fc in range(8):
            po = banks.b[4 + fc % 4]
            for kc in range(8):
                P.mm(po[:, 0:n], wo[:, kc, fc * 128:(fc + 1) * 128], yb[:, kc, 0:n], start=(kc == 0), stop=(kc == 7))
            P.stt(RT[:, fc, c0:c0 + n], po[:, 0:n], gateT[:, 16 + fc, j:j + 1], RT[:, fc, c0:c0 + n], ALU.mult, ALU.add)


def alloc_tok_tmp(P):
    return {
        'sq': [P.sbuf("t_sq%d" % i, [128, 512], F32) for i in range(2)],
        'rstd': [P.sbuf("t_rstd%d" % i, [128, 512], F32) for i in range(2)],
        't1': [P.sbuf("t_t1%d" % i, [128, 512], F32) for i in range(2)],
        'xb': [P.sbuf("t_xb%d" % i, [128, 8, 512], BF16) for i in range(2)],
    }


def build_tok_kernel(nc, P, first, last, with_outproj):
    banks = Banks(P)
    K = load_consts_tok(nc, P)
    epst = P.sbuf("eps_t", [128, 1], F32)
    P.memset('pool', epst[:], EPS)
    K['eps'] = epst
    tmp = alloc_tok_tmp(P)
    RT_d = nc.dram_tensor("RT", [D, QT], F32, kind="ExternalInput").ap()
    cvec_d = nc.dram_tensor("cvec", [128, 8, 2], F32, kind="ExternalInput").ap()
    RT = P.sbuf("RT_sb", [128, 8, QT], F32)
    RT_v = RT_d.rearrange("(kc p) t -> p kc t", p=128)
    for kc in range(8):
        P.dma('sp' if kc % 2 == 0 else 'act', RT[:, kc, :], RT_v[:, kc, :])
    modT = P.sbuf("modT", [128, 24, 2], F32)
    if with_outproj:
        wmodg_d = nc.dram_tensor("wmod_g", [D, 3 * D], F32, kind="ExternalInput").ap()
        bmodg_d = nc.dram_tensor("bmod_g", [128, 24], F32, kind="ExternalInput").ap()
        yT_d = nc.dram_tensor("yT", [D, QT], BF16, kind="ExternalInput").ap()
        wout_d = nc.dram_tensor("wout", [D, D], F32, kind="ExternalInput").ap()
        ssdw_d = nc.dram_tensor("ssdnw", [128, 2], F32, kind="ExternalInput").ap()
        tmp['wstage'] = [P.sbuf("t_wst%d" % i, [128, D], F32) for i in range(2)]
        tmp['yb'] = [P.sbuf("t_yb%d" % i, [128, 8, 512], BF16) for i in range(2)]
        gateT = P.sbuf("gateT", [128, 24, 2], F32)
        compute_mod(nc, P, K, banks, cvec_d, wmodg_d, bmodg_d, list(range(16, 24)), gateT)
        tiles = TILES_Q[1:] if last else TILES_Q
        phase_outproj(nc, P, K, banks, RT, yT_d, wout_d, ssdw_d, gateT, tiles, tmp)
    if last:
        fnw_d = nc.dram_tensor("fnw", [128, 8], F32, kind="ExternalInput").ap()
        out_d = nc.dram_tensor("outT", [D, 2048], F32, kind="ExternalOutput").ap()
        phase_final(nc, P, K, banks, RT, fnw_d, out_d, TILES_Q[1:], tmp)
    else:
        wmod_d = nc.dram_tensor("wmod_n", [D, 3 * D], F32, kind="ExternalInput").ap()
        bmod_d = nc.dram_tensor("bmod_n", [128, 24], F32, kind="ExternalInput").ap()
        normw_d = nc.dram_tensor("normw", [128, 8], F32, kind="ExternalInput").ap()
        xn_d = nc.dram_tensor("xnT", [D, QT], BF16, kind="ExternalOutput").ap()
        if with_outproj:
            Rout_d = nc.dram_tensor("RTout", [D, QT], F32, kind="ExternalOutput").ap()
        modT2 = modT
        _cm_second(nc, P, K, banks, cvec_d, wmod_d, bmod_d, list(range(0, 16)), modT2, with_outproj)
        A = norm_coeffs(nc, P, modT2, normw_d, "nc")
        phase_norm(nc, P, K, banks, RT, A, modT2, xn_d, TILES_Q, tmp)
        if with_outproj:
            Ro_v = Rout_d.rearrange("(kc p) t -> p kc t", p=128)
            for kc in range(8):
                P.dma('pool', Ro_v[:, kc, :], RT[:, kc, :], final=True)


_CM_STATE = {}


def _cm_second(nc, P, K, banks, cvec_d, wmod_d, bmod_d, fc_list, modT, second):
    if not second:
        return compute_mod(nc, P, K, banks, cvec_d, wmod_d, bmod_d, fc_list, modT)
    orig = P.sbuf

    def renamed(name, shape, dtype=F32):
        return orig(name + "_2", shape, dtype)
    P.sbuf = renamed
    try:
        compute_mod(nc, P, K, banks, cvec_d, wmod_d, bmod_d, fc_list, modT)
    finally:
        P.sbuf = orig


def pk(v):
    v = np.asarray(v)
    return np.ascontiguousarray(v.reshape(-1, 128).T)

TOK_TILES = [(0, 256)] + [(256 + 512 * i, 512) for i in range(16)]
SCALE_QK = 32.0 ** -0.5
QK_REP = 1
DEFER = True


def store_y(P, io, mi, t0, n, src):
    if 'y_store' in io:
        io['y_store'](mi, t0, n, src)
    else:
        P.dma('pool', io['yT'][mi, :, t0:t0 + n], src, final=True)


def load_weights_bf16(nc, P, A, w_d, ncols, stage):
    Wsb = A.alloc([8, ncols], BF16)
    for kc in range(8):
        st = stage[kc % 2]
        P.dma('sp' if kc % 2 == 0 else 'act', st[:, 0:ncols], w_d[kc * 128:(kc + 1) * 128, :])
        P.copy('dve', Wsb[:, kc, :], st[:, 0:ncols])
    return Wsb


def inproj(nc, P, banks, xn_v, Wsb, tiles, fm_groups, tm_group, xbufs, bank_ids=(0, 1, 2, 3), pre_tile=None):
    pending = []
    for ti, (t0, n) in enumerate(tiles):
        xb = xbufs[ti % 2]
        if callable(xn_v):
            xn_v(xb, t0, n)
        else:
            P.dma('sp', xb[:, :, 0:n], xn_v[:, :, t0:t0 + n])
        if pre_tile is not None:
            pre_tile(ti, t0, n)
        new_pending = []
        for gi, (c0, M, lat_only, fn) in enumerate(fm_groups):
            if lat_only and t0 < CTXL:
                continue
            ps = banks.b[bank_ids[gi % len(bank_ids)]]
            for kc in range(8):
                P.mm(ps[0:M, 0:n], Wsb[:, kc, c0:c0 + M], xb[:, kc, 0:n], start=(kc == 0), stop=(kc == 7))
            if fn is not None:
                r = fn(ps, t0, n, ti)
                if r is not None:
                    if DEFER:
                        new_pending.append(r)
                    else:
                        r()
        if tm_group is not None:
            c0, ncols, fn, tmbanks = tm_group
            for sub in range(n // 128):
                ps = banks.b[tmbanks[sub % len(tmbanks)]]
                for kc in range(8):
                    P.mm(ps[:, 0:ncols], xb[:, kc, sub * 128:(sub + 1) * 128], Wsb[:, kc, c0:c0 + ncols],
                         start=(kc == 0), stop=(kc == 7))
                fn(ps, t0 + sub * 128)
        for r in pending:
            r2 = r()
            if r2 is not None:
                new_pending.append(r2)
        pending = new_pending
    while pending:
        nxt = []
        for r in pending:
            r2 = r()
            if r2 is not None:
                nxt.append(r2)
        pending = nxt


def silu_evac(P, out_bf, ps_ap, tmp_e, M, n):
    P.act(tmp_e[0:M, 0:n], ps_ap, AF.Tanh, scale=0.5)
    P.stt(out_bf, tmp_e[0:M, 0:n], 1.0, ps_ap, ALU.add, ALU.mult)


def attention_phase(nc, P, A, banks, K, xn_v, io, layer, with_ctx):
    A.reset()
    lam_init = 0.8 - 0.6 * float(np.exp(-0.3 * layer))
    stage = [A.alloc([640], F32) for _ in range(2)]
    Wsb = load_weights_bf16(nc, P, A, io['w_attn'], 640, stage)
    xbufs = [A.alloc([8, 512], BF16) for _ in range(2)]
    Qa = A.alloc([NTOK], BF16)
    Ka = A.alloc([NTOK], BF16)
    Vt = A.alloc([66, 128], BF16)
    sg = A.alloc([NTOK], BF16)
    cosb = [A.alloc([512], F32) for _ in range(2)]
    sinb = [A.alloc([512], F32) for _ in range(2)]
    t1 = [A.alloc([512], F32) for _ in range(2)]
    t2 = [A.alloc([512], F32) for _ in range(2)]
    te = [A.alloc([512], F32) for _ in range(2)]
    P.memset('pool', Vt[:, :, 65:128], 0.0)
    P.memset('pool', Vt[:, :, 64:65], 1.0)

    def pre_tile(ti, t0, n):
        if t0 >= CTXL:
            P.dma('act', cosb[ti % 2][:], io['cosT'][:, t0 - CTXL:t0 - CTXL + n])
            P.dma('act', sinb[ti % 2][:], io['sinT'][:, t0 - CTXL:t0 - CTXL + n])

    state = {}

    def ev_plain(dst):
        def f(ps, t0, n, ti):
            if t0 < CTXL:
                P.copy('act', dst[:, t0:t0 + n], ps[:, 0:n])
            else:
                state['ps1'] = ps
        return f

    def ev_rot(dst):
        def f(ps, t0, n, ti):
            a = t1[ti % 2]
            b = t2[ti % 2]
            P.tt('dve', a[:, 0:n], state['ps1'][:, 0:n], cosb[ti % 2][:, 0:n], ALU.mult)
            P.tt('dve', b[:, 0:n], ps[:, 0:n], sinb[ti % 2][:, 0:n], ALU.mult)
            P.tt('pool', dst[:, t0:t0 + n], a[:, 0:n], b[:, 0:n], ALU.add)
        return f

    vT = [A.alloc([512], BF16) for _ in range(2)]

    def ev_gv(ps, t0, n, ti):
        silu_evac(P, sg[0:64, t0:t0 + n], ps[0:64, 0:n], te[ti % 2], 64, n)
        v_ = vT[ti % 2]
        P.copy('act', v_[64:128, 0:n], ps[64:128, 0:n])

        def rest():
            for sub in range(n // 128):
                pt = banks.b[6 + sub % 2].bitcast(BF16)
                P.transpose(pt[:, 0:64], v_[64:128, sub * 128:(sub + 1) * 128], K['identb'][64:128, 64:128])
                P.copy('dve', Vt[:, t0 // 128 + sub, 0:64], pt[:, 0:64])
        return rest

    fm = [(0, 128, False, ev_plain(Qa)), (128, 128, True, ev_rot(Qa)),
          (256, 128, False, ev_plain(Ka)), (384, 128, True, ev_rot(Ka)),
          (512, 128, False, ev_gv)]
    inproj(nc, P, banks, xn_v, Wsb, TOK_TILES, fm, None, xbufs, bank_ids=(0, 1, 2, 3, 4), pre_tile=pre_tile)

    lv = A.alloc([128], F32)
    P.dma('sp', lv[0:64, :], io['diff_lam'].partition_broadcast(64))
    pr = A.alloc([64], F32)
    s2 = A.alloc([2], F32)
    P.tt('dve', pr[0:64, 0:32], lv[0:64, 0:32], lv[0:64, 32:64], ALU.mult)
    P.tt('dve', pr[0:64, 32:64], lv[0:64, 64:96], lv[0:64, 96:128], ALU.mult)
    P.op('dve', lambda e: e.tensor_reduce(s2[0:64, 0:2], pr[0:64, :].rearrange("p (a b) -> p a b", a=2), AX.X, ALU.add),
         reads=[pr[0:64, :]], writes=[s2[0:64, 0:2]])
    P.act(s2[0:64, :], s2[0:64, :], AF.Exp)
    nlam = A.alloc([1], F32)
    P.tt('dve', nlam[0:64, :], s2[0:64, 1:2], s2[0:64, 0:1], ALU.subtract)
    P.ts('dve', nlam[0:64, :], nlam[0:64, :], -lam_init, None, ALU.add)
    subw = A.alloc([1], F32)
    P.dma('sp', subw[0:64, :], io['diff_subln_w'])
    P.ts('dve', subw[0:64, :], subw[0:64, :], 1.0 - lam_init, None, ALU.mult)

    Eb = [A.alloc([1024], BF16) for _ in range(4)]
    osb = [A.alloc([512], F32) for _ in range(2)]
    fz = [A.alloc([512], F32) for _ in range(4)]
    yb = [A.alloc([512], BF16) for _ in range(2)]
    qblocks = []
    if with_ctx:
        qblocks.append((0, 256, [0, 1]))
    for i in range(16):
        qblocks.append((256 + 512 * i, 512, list(range(66))))
    o_acc = [banks.b[6], banks.b[7]]
    for qi, (q0, nq, kbs) in enumerate(qblocks):
        def qk(kb):
            S = banks.pp[kb % 3]
            for rep in range(QK_REP):
                P.mm(S[:, 0:nq], Ka[0:32, kb * 128:(kb + 1) * 128], Qa[0:32, q0:q0 + nq])
                P.mm(S[:, 512:512 + nq], Ka[64:96, kb * 128:(kb + 1) * 128], Qa[64:96, q0:q0 + nq])
        for k_ in kbs[0:3]:
            qk(k_)
        for ki, kb in enumerate(kbs):
            S = banks.pp[kb % 3]
            E = Eb[ki % 4]
            if nq == 512:
                P.act(E[:, :], S[:, :], AF.Exp, scale=SCALE_QK)
            else:
                Ev = E.rearrange("p (a b) -> p a b", a=2)[:, :, 0:nq]
                Sv = S.rearrange("p (a b) -> p a b", a=2)[:, :, 0:nq]
                P.act(Ev, Sv, AF.Exp, scale=SCALE_QK)
            if ki + 3 < len(kbs):
                qk(kbs[ki + 3])
            for c in range(2):
                P.mm(o_acc[c][:, 0:nq], Vt[:, kb, :], E[:, c * 512:c * 512 + nq], start=(ki == 0), stop=(ki == len(kbs) - 1))
        for c in range(2):
            P.copy('dve', osb[c][0:65, 0:nq], o_acc[c][0:65, 0:nq])
        zb = [banks.b[0], banks.b[1]]
        for c in range(2):
            P.mm(zb[c][0:64, 0:nq], K['selZ'][0:65, :], osb[c][0:65, 0:nq])
        for c in range(2):
            P.act(fz[c][0:64, 0:nq], zb[c][0:64, 0:nq], AF.Ln)
            P.act(fz[c][0:64, 0:nq], fz[c][0:64, 0:nq], AF.Exp, scale=-1.0)
            P.tt('dve', fz[c][0:64, 0:nq], fz[c][0:64, 0:nq], osb[c][0:64, 0:nq], ALU.mult)
        o = fz[2]
        P.stt(o[0:64, 0:nq], fz[1][0:64, 0:nq], nlam[0:64, 0:1], fz[0][0:64, 0:nq], ALU.mult, ALU.add)
        P.tt('dve', fz[3][0:64, 0:nq], o[0:64, 0:nq], o[0:64, 0:nq], ALU.mult)
        P.mm(zb[0][0:64, 0:nq], K['ones'][0:64, 0:64], fz[3][0:64, 0:nq])
        P.act(fz[3][0:64, 0:nq], zb[0][0:64, 0:nq], AF.Ln, scale=1.0 / 64, bias=K['eps'][0:64, :])
        P.act(fz[3][0:64, 0:nq], fz[3][0:64, 0:nq], AF.Exp, scale=-0.5)
        P.stt(o[0:64, 0:nq], o[0:64, 0:nq], subw[0:64, 0:1], fz[3][0:64, 0:nq], ALU.mult, ALU.mult)
        y = yb[qi % 2]
        P.stt(y[0:64, 0:nq], o[0:64, 0:nq], 0.5, sg[0:64, q0:q0 + nq], ALU.mult, ALU.mult)
        store_y(P, io, 2, q0, nq, y[0:64, 0:nq])
        if 'after_q' in io:
            io['after_q'](q0, nq)


TP = 8456
PADC = 2
PADL = 260


def pcol(t0):
    return t0 + PADC if t0 < CTXL else t0 + (PADL - CTXL)


PTILES = [(PADC, 256)] + [(PADL + 512 * i, 512) for i in range(16)]


def conv4(P, dst, src, cw, cb, np_, c_lo, c_hi):
    n = c_hi - c_lo
    P.act(dst[0:np_, c_lo:c_hi], src[0:np_, c_lo - 2:c_hi - 2], AF.Identity, scale=cw[0:np_, 0:1], bias=cb[0:np_, 0:1])
    for k in range(1, 4):
        P.stt(dst[0:np_, c_lo:c_hi], src[0:np_, c_lo + k - 2:c_hi + k - 2], cw[0:np_, k:k + 1], dst[0:np_, c_lo:c_hi],
              ALU.mult, ALU.add)


def lru_phase(nc, P, A, banks, K, xn_v, io, layer, with_ctx):
    A.reset()
    stage = [A.alloc([192], F32) for _ in range(2)]
    Wsb = load_weights_bf16(nc, P, A, io['w_lru'], 192, stage)
    xbufs = [A.alloc([8, 512], BF16) for _ in range(2)]
    XA = A.alloc([TP], F32)
    XU = A.alloc([TP], F32)
    sg = A.alloc([NTOK], BF16)
    te = [A.alloc([512], F32) for _ in range(2)]
    par = A.alloc([16], F32)
    P.dma('sp', par[:, 0:9], io['lru_par'])
    cw, cb = par[:, 0:4], par[:, 4:5]
    nba, nbx = par[:, 9:10], par[:, 10:11]
    c8, c16 = par[:, 11:12], par[:, 12:13]
    P.ts('dve', nba, par[:, 5:6], 0.5, None, ALU.mult)
    P.ts('dve', nbx, par[:, 6:7], 0.5, None, ALU.mult)
    P.act(c8, par[:, 7:8], AF.Exp, scale=-1.0)
    P.act(c8, c8, AF.Ln, bias=K['one'][:, 0:1])
    P.ts('dve', c16, c8, -16.0, None, ALU.mult)
    P.ts('dve', c8, c8, -8.0, None, ALU.mult)
    gw32 = A.alloc([256], F32)
    P.dma('sp', gw32[0:64, :], io['lru_gw'])
    gw = A.alloc([256], BF16)
    P.copy('pool', gw[0:64, :], gw32[0:64, :])
    II = A.alloc([64], F32)
    P.copy('pool', II[0:64, :], K['ident'][0:64, 0:64])
    P.copy('pool', II[64:128, :], K['ident'][64:128, 64:128])
    P.memset('pool', XA[:, 0:PADC], 0.0)
    P.memset('pool', XA[:, PADC + CTXL:PADL], 0.0)
    P.memset('pool', XA[:, PADL + SEQ:TP], 0.0)

    def ev_x(ps, t0, n, ti):
        c = pcol(t0)
        P.copy('act', XA[:, c:c + n], ps[:, 0:n])

    def ev_gate(ps, t0, n, ti):
        silu_evac(P, sg[0:64, t0:t0 + n], ps[0:64, 0:n], te[ti % 2], 64, n)

    inproj(nc, P, banks, xn_v, Wsb, TOK_TILES, [(0, 128, False, ev_x), (128, 64, False, ev_gate)], None, xbufs,
           bank_ids=(0, 1, 2, 3))
    conv4(P, XU, XA, cw, cb, 128, PADC, PADC + CTXL)
    for i in range(4):
        conv4(P, XU, XA, cw, cb, 128, PADL + 2048 * i, PADL + 2048 * (i + 1))
    xcb = [A.alloc([512], BF16) for _ in range(2)]
    gr = [A.alloc([512], F32) for _ in range(2)]
    gi_ = [A.alloc([512], F32) for _ in range(2)]
    gs = [A.alloc([512], F32) for _ in range(2)]
    def gateA(ti, c0, n):
        xb = xcb[ti % 2]
        P.copy('pool', xb[0:64, 0:n], XU[0:64, c0:c0 + n])
        psr, psi = banks.b[(2 * ti) % 8], banks.b[(2 * ti + 1) % 8]
        P.mm(psr[:, 0:n], gw[0:64, 0:128], xb[0:64, 0:n])
        P.mm(psi[:, 0:n], gw[0:64, 128:256], xb[0:64, 0:n])
        r, ii = gr[ti % 2], gi_[ti % 2]
        P.act(r[:, 0:n], psr[:, 0:n], AF.Tanh, scale=0.5, bias=nba)
        P.act(ii[:, 0:n], psi[:, 0:n], AF.Tanh, scale=0.5, bias=nbx)

    def gateB(ti, c0, n):
        r, ii = gr[ti % 2], gi_[ti % 2]
        P.ts('dve', r[:, 0:n], r[:, 0:n], 0.5, 0.5, ALU.mult, ALU.add)
        P.ts('dve', ii[:, 0:n], ii[:, 0:n], 0.5, 0.5, ALU.mult, ALU.add)
        P.act(XA[:, c0:c0 + n], r[:, 0:n], AF.Exp, scale=c8)
        P.tt('dve', XU[:, c0:c0 + n], XU[:, c0:c0 + n], ii[:, 0:n], ALU.mult)

    npt = len(PTILES)
    for it in range(npt + 1):
        if it < npt:
            gateA(it, *PTILES[it])
        if it >= 1:
            gateB(it - 1, *PTILES[it - 1])
    for ti, (c0, n) in enumerate(PTILES):
        s = gs[ti % 2]
        P.tt('pool', s[:, 0:n], XA[:, c0:c0 + n], XA[:, c0:c0 + n], ALU.mult)
        P.act(s[:, 0:n], s[:, 0:n], AF.Ln, scale=-1.0, bias=K['one'][:, 0:1])
        P.act(s[:, 0:n], s[:, 0:n], AF.Exp, scale=0.5)
        P.tt('dve', XU[:, c0:c0 + n], XU[:, c0:c0 + n], s[:, 0:n], ALU.mult)
    fa, fu = XA[0:64], XU[0:64]
    ba_, bu = XA[64:128], XU[64:128]
    P.scan(fu[:, PADC:PADC + CTXL], fa[:, PADC:PADC + CTXL], fu[:, PADC:PADC + CTXL], 0.0)
    for i in range(4):
        lo = PADL + 2048 * i
        init = fu[:, PADC + CTXL - 1:PADC + CTXL] if i == 0 else fu[:, lo - 1:lo]
        P.scan(fu[:, lo:lo + 2048], fa[:, lo:lo + 2048], fu[:, lo:lo + 2048], init)
    P.scan(bu[:, PADC:PADC + CTXL][:, ::-1], ba_[:, PADC:PADC + CTXL][:, ::-1], bu[:, PADC:PADC + CTXL][:, ::-1], 0.0)
    for i in range(3, -1, -1):
        lo = PADL + 2048 * i
        init = bu[:, PADC:PADC + 1] if i == 3 else bu[:, lo + 2048:lo + 2049]
        P.scan(bu[:, lo:lo + 2048][:, ::-1], ba_[:, lo:lo + 2048][:, ::-1], bu[:, lo:lo + 2048][:, ::-1], init)
    yb = [A.alloc([512], BF16) for _ in range(2)]
    for ti, (t0, n) in enumerate(TOK_TILES):
        if t0 < CTXL and not with_ctx:
            continue
        c0 = pcol(t0)
        ps = banks.b[ti % 4]
        P.mm(ps[0:64, 0:n], II[:, :], XU[:, c0:c0 + n])
        y = yb[ti % 2]
        P.stt(y[0:64, 0:n], ps[0:64, 0:n], 0.5, sg[0:64, t0:t0 + n], ALU.mult, ALU.mult)
        store_y(P, io, 1, t0, n, y[0:64, 0:n])


def gla_phase(nc, P, A, banks, K, xn_v, io, layer, with_ctx):
    A.reset()
    NC_ = 132
    QG = A.alloc([NTOK], BF16)
    KG = A.alloc([NTOK], BF16)
    KDt = A.alloc([66, 64], BF16)
    Vt = A.alloc([66, 64], BF16)
    sg = A.alloc([NTOK], BF16)
    ST = A.alloc([NC_ + 2, 64], F32)
    STb = A.alloc([NC_ + 2, 64], BF16)
    DEC = A.alloc([NC_], F32)
    par = A.alloc([8], F32)
    P.dma('sp', par[0:64, 0:2], io['gla_par'])
    nb2 = par[0:64, 2:3]
    P.ts('dve', nb2, par[0:64, 0:1], -1.0, None, ALU.mult)
    lnsc = par[0:64, 3:4]
    P.memset('dve', lnsc, float(np.log(32.0 ** -0.5)))
    w2_32 = A.alloc([64], F32)
    P.dma('sp', w2_32[64:96, :], io['gla_w2bd'])
    w2b = A.alloc([64], BF16)
    P.copy('dve', w2b[64:96, :], w2_32[64:96, :])
    mk = A.mark()
    stage = [A.alloc([320], F32) for _ in range(2)]
    Wsb = load_weights_bf16(nc, P, A, io['w_gla'], 320, stage)
    xbufs = [A.alloc([8, 512], BF16) for _ in range(2)]
    te = [A.alloc([512], F32) for _ in range(2)]
    lrb = [A.alloc([512], BF16) for _ in range(2)]
    Lb = [A.alloc([512], F32) for _ in range(2)]
    Gb = [A.alloc([512], F32) for _ in range(2)]
    Eq = [A.alloc([512], F32) for _ in range(2)]
    Ek = [A.alloc([512], F32) for _ in range(2)]
    dG = [A.alloc([512], F32) for _ in range(2)]
    kdT = [A.alloc([512], BF16) for _ in range(2)]
    P.memset('dve', ST[0:64, 0:2, :], 0.0)
    P.memset('dve', ST[0:64, NC_:NC_ + 2, :], 0.0)
    st = {}

    qs = [A.alloc([512], F32) for _ in range(2)]
    ks = [A.alloc([512], F32) for _ in range(2)]

    def ev_q(ps, t0, n, ti):
        P.copy('act', qs[ti % 2][0:64, 0:n], ps[0:64, 0:n])

    vT = [A.alloc([512], BF16) for _ in range(2)]

    def ev_gv(ps, t0, n, ti):
        P.copy('dve', sg[0:64, t0:t0 + n], ps[0:64, 0:n])
        v_ = vT[ti % 2]
        P.copy('act', v_[64:128, 0:n], ps[64:128, 0:n])
        for sub in range(n // 128):
            pt = banks.b[5 + sub % 2].bitcast(BF16)
            P.transpose(pt[:, 64:128], v_[64:128, sub * 128:(sub + 1) * 128], K['identb'][64:128, 64:128])
            P.copy('dve', Vt[:, t0 // 128 + sub, :], pt[:, 64:128])

    def ev_lr(ps, t0, n, ti):
        i2 = ti % 2
        nch = n // 64
        c0 = t0 // 64
        P.copy('act', ks[ti % 2][0:64, 0:n], ps[0:64, 0:n])
        P.copy('act', lrb[i2][64:96, 0:n], ps[64:96, 0:n])

        def rest():
            pz = banks.b[7]
            P.mm(pz[0:64, 0:n], w2b[64:96, 0:64], lrb[i2][64:96, 0:n])
            L, G = Lb[i2], Gb[i2]
            P.act(L[0:64, 0:n], pz[0:64, 0:n], AF.Exp, scale=-1.0, bias=nb2)
            P.act(L[0:64, 0:n], L[0:64, 0:n], AF.Ln, bias=K['one'][0:64, 0:1])
            P.scan(G[0:32, 0:n], K['scanmask'][0:32, 0:n], L[0:32, 0:n], 0.0)
            P.scan(G[32:64, 0:n][:, ::-1], K['scanmask'][32:64, 0:n][:, ::-1], L[32:64, 0:n][:, ::-1], 0.0)
            P.act(Eq[i2][0:64, 0:n], G[0:64, 0:n], AF.Exp, scale=-1.0 / 16, bias=lnsc)
            P.act(Ek[i2][0:64, 0:n], G[0:64, 0:n], AF.Exp, scale=1.0 / 16)
            P.tt('dve', QG[0:64, t0:t0 + n], qs[i2][0:64, 0:n], Eq[i2][0:64, 0:n], ALU.mult)
            P.tt('dve', KG[0:64, t0:t0 + n], ks[i2][0:64, 0:n], Ek[i2][0:64, 0:n], ALU.mult)
            G3 = G[:, 0:n].rearrange("p (c s) -> p c s", s=64)
            d3 = dG[i2][:, 0:n].rearrange("p (c s) -> p c s", s=64)
            P.tt('dve', d3[0:32], G3[0:32, :, 63:64].to_broadcast([32, nch, 64]), G3[0:32], ALU.subtract)
            P.tt('dve', d3[32:64], G3[32:64, :, 0:1].to_broadcast([32, nch, 64]), G3[32:64], ALU.subtract)
            P.act(dG[i2][0:64, 0:n], dG[i2][0:64, 0:n], AF.Exp, scale=-1.0 / 16)
            P.tt('dve', kdT[i2][0:64, 0:n], ks[i2][0:64, 0:n], dG[i2][0:64, 0:n], ALU.mult)
            P.act(DEC[0:32, c0:c0 + nch], G3[0:32, :, 63], AF.Exp, scale=-1.0 / 16)
            P.act(DEC[32:64, c0:c0 + nch], G3[32:64, :, 0], AF.Exp, scale=-1.0 / 16)
            def rest2():
                for sub in range(n // 128):
                    pt = banks.b[5 + sub % 2].bitcast(BF16)
                    P.transpose(pt[:, 0:64], kdT[i2][0:64, sub * 128:(sub + 1) * 128], K['identb'][0:64, 0:64])
                    P.copy('dve', KDt[:, t0 // 128 + sub, :], pt[:, 0:64])
            return rest2
        return rest

    fm = [(0, 64, False, ev_q), (64, 96, False, ev_lr), (192, 128, False, ev_gv)]
    inproj(nc, P, banks, xn_v, Wsb, TOK_TILES, fm, None, xbufs, bank_ids=(0, 1, 2))
    for ti, (t0, n) in enumerate(TOK_TILES):
        P.act(te[ti % 2][0:64, 0:n], sg[0:64, t0:t0 + n], AF.Tanh, scale=0.5)
        P.stt(sg[0:64, t0:t0 + n], te[ti % 2][0:64, 0:n], 1.0, sg[0:64, t0:t0 + n], ALU.add, ALU.mult)

    for g0 in range(0, 66, 8):
        npair = min(8, 66 - g0)
        for i in range(npair):
            p = g0 + i
            for j in range(2):
                P.mm(banks.b[j][0:64, i * 64:(i + 1) * 64], KDt[64 * j:64 * j + 64, p, :], Vt[64 * j:64 * j + 64, p, :])
        for j in range(2):
            src = banks.b[j][:, 0:npair * 64].rearrange("p (c v) -> p c v", v=64)
            c0 = 2 * g0 + j
            P.copy('act', ST[0:32, c0 + 2:c0 + 1 + 2 * npair:2, :], src[0:32])
            P.copy('dve', ST[32:64, c0:c0 + 2 * npair - 1:2, :], src[32:64])
    fsteps = [(c + 2, c + 1, c) for c in range(1, NC_)]
    bsteps = [(c, c + 1, c) for c in (2, 1, 0)] + [(131, 0, 131)] + [(c, c + 1, c) for c in range(130, 4, -1)]
    for i in range(max(len(fsteps), len(bsteps))):
        if i < len(fsteps):
            o_, i_, d_ = fsteps[i]
            P.stt(ST[0:32, o_, :], ST[0:32, i_, :], DEC[0:32, d_:d_ + 1], ST[0:32, o_, :], ALU.mult, ALU.add)
        if i < len(bsteps):
            o_, i_, d_ = bsteps[i]
            P.stt(ST[32:64, o_, :], ST[32:64, i_, :], DEC[32:64, d_:d_ + 1], ST[32:64, o_, :], ALU.mult, ALU.add)
    P.copy('dve', ST[32:64, 132, :], ST[32:64, 0, :])
    P.copy('dve', STb[0:64], ST[0:64])

    A.release(mk)
    tm = [A.alloc([512], F32) for _ in range(2)]
    At = [A.alloc([512], BF16) for _ in range(2)]
    ob = [A.alloc([512], F32) for _ in range(2)]
    yb = [A.alloc([512], BF16) for _ in range(2)]
    tiles2 = [(ti, t0, n) for ti, (t0, n) in enumerate(TOK_TILES) if not (t0 < CTXL and not with_ctx)]

    def stageA(ti, t0, n):
        i2 = ti % 2
        npair = n // 128
        X, Y = banks.b[2], banks.b[3]
        for i in range(npair):
            cs = slice(t0 + 128 * i, t0 + 128 * (i + 1))
            P.mm(X[:, 128 * i:128 * (i + 1)], KG[0:32, cs], QG[0:32, cs])
            P.mm(Y[:, 128 * i:128 * (i + 1)], KG[32:64, cs], QG[32:64, cs])
        P.tt('dve', tm[i2][:, 0:n], X[:, 0:n], K['maskF'][:, 0:n], ALU.mult)
        P.tt('dve', At[i2][:, 0:n], Y[:, 0:n], K['maskB'][:, 0:n], ALU.mult)
        P.tt('pool', At[i2][:, 0:n], At[i2][:, 0:n], tm[i2][:, 0:n], ALU.add)

    def stageB(ti, t0, n):
        i2 = ti % 2
        npair = n // 128
        Z = banks.b[4 + i2]
        for i in range(npair):
            p = t0 // 128 + i
            P.mm(Z[0:64, 128 * i:128 * (i + 1)], Vt[:, p, :], At[i2][:, 128 * i:128 * (i + 1)], start=True, stop=False)
            for j in range(2):
                c = 2 * p + j
                cs = slice(t0 + 128 * i + 64 * j, t0 + 128 * i + 64 * (j + 1))
                kk = 32 if c == 3 else 64
                P.mm(Z[0:64, 128 * i + 64 * j:128 * i + 64 * (j + 1)], STb[0:kk, c + 1, :], QG[0:kk, cs],
                     start=False, stop=(j == 1))
        o = ob[i2]
        P.act(o[0:64, 0:n], Z[0:64, 0:n], AF.Square)

    def stageC(ti, t0, n):
        i2 = ti % 2
        Z = banks.b[4 + i2]
        o = ob[i2]
        zb = banks.b[6 + i2]
        P.mm(zb[0:64, 0:n], K['ones'][0:64, 0:64], o[0:64, 0:n])
        P.act(o[0:64, 0:n], zb[0:64, 0:n], AF.Ln, scale=1.0 / 64, bias=K['eps'][0:64, :])
        P.act(o[0:64, 0:n], o[0:64, 0:n], AF.Exp, scale=-0.5)
        P.stt(o[0:64, 0:n], Z[0:64, 0:n], par[0:64, 1:2], o[0:64, 0:n], ALU.mult, ALU.mult)
        y = yb[i2]
        P.stt(y[0:64, 0:n], o[0:64, 0:n], 0.5, sg[0:64, t0:t0 + n], ALU.mult, ALU.mult)
        store_y(P, io, 0, t0, n, y[0:64, 0:n])

    nt2 = len(tiles2)
    for it in range(nt2 + 1):
        if it < nt2:
            stageA(*tiles2[it])
        if it >= 1:
            stageB(*tiles2[it - 1])
            stageC(*tiles2[it - 1])


def ssd_phase(nc, P, A, banks, K, xn_v, io, layer, with_ctx):
    A.reset()
    NC_ = 132
    XB = A.alloc([TP], BF16)
    C2 = A.alloc([TP], BF16)
    sgz = A.alloc([NTOK], BF16)
    par = A.alloc([16], F32)
    P.dma('sp', par[:, 0:5], io['ssd_par'])
    cpar = A.alloc([2, 5], F32)
    P.dma('sp', cpar[:], io['ssd_cpar'])
    na = par[:, 5:7]
    P.act(na, par[:, 2:4], AF.Exp)
    P.ts('dve', na, na, -1.0, None, ALU.mult)
    dts_d = io['ssd_scr'][0:2]
    crow_d = io['ssd_scr'][2:4]
    clr_d = io['ssd_scr'][4:6]
    mk = A.mark()
    stage = [A.alloc([322], F32) for _ in range(2)]
    Wsb = load_weights_bf16(nc, P, A, io['w_ssd'], 322, stage)
    xbufs = [A.alloc([8, 512], BF16) for _ in range(2)]
    XR1 = A.alloc([TP], F32)
    XR2 = A.alloc([TP], F32)
    XCs = [A.alloc([2052], F32) for _ in range(2)]
    te = [A.alloc([512], F32) for _ in range(2)]
    dtt = [A.alloc([512], F32) for _ in range(2)]
    for X in (XR1, XR2):
        P.memset('pool', X[:, 0:PADC], 0.0)
        P.memset('pool', X[:, PADC + CTXL:PADL], 0.0)
        P.memset('pool', X[:, PADL + SEQ:TP], 0.0)

    def ev_raw(dst):
        def f(ps, t0, n, ti):
            c = pcol(t0)
            P.copy('act', dst[:, c:c + n], ps[:, 0:n])
        return f

    def ev_gate_dt(ps, t0, n, ti):
        silu_evac(P, sgz[0:64, t0:t0 + n], ps[0:64, 0:n], te[ti % 2], 64, n)
        P.copy('dve', dtt[ti % 2][64:66, 0:n], ps[64:66, 0:n])
        P.dma('pool', dts_d[:, t0:t0 + n], dtt[ti % 2][64:66, 0:n])

    fm = [(0, 128, False, ev_raw(XR1)), (128, 128, False, ev_raw(XR2)), (256, 66, False, ev_gate_dt)]
    inproj(nc, P, banks, xn_v, Wsb, TOK_TILES, fm, None, xbufs, bank_ids=(0, 1, 2, 3))
    pieces = [(PADC, PADC + CTXL)] + [(PADL + 2048 * i, PADL + 2048 * (i + 1)) for i in range(4)]
    for gi, (src, dst) in enumerate(((XR1, XB), (XR2, C2))):
        for pi_, (lo, hi) in enumerate(pieces):
            n = hi - lo
            XC = XCs[(gi * len(pieces) + pi_) % 2]
            P.act(XC[:, 2:2 + n], src[:, lo - 2:hi - 2], AF.Identity, scale=cpar[:, gi, 0:1], bias=cpar[:, gi, 4:5])
            for k in range(1, 4):
                P.stt(XC[:, 2:2 + n], src[:, lo + k - 2:hi + k - 2], cpar[:, gi, k:k + 1], XC[:, 2:2 + n], ALU.mult, ALU.add)
            P.act(dst[:, lo:hi], XC[:, 2:2 + n], AF.Silu)
    A.release(mk)
    PM = A.alloc([2, 128], F32)
    for d in range(2):
        P.dma('sp', PM[0:66, d, :], dts_d[d].rearrange("(p s) -> p s", s=128))
    DT = A.alloc([2, 128], F32)
    for d in range(2):
        P.act(DT[0:66, d, :], PM[0:66, d, :], AF.Exp, bias=par[0:66, d:d + 1])
    P.act(DT[0:66], DT[0:66], AF.Ln, bias=K['one'][0:66, 0:1])
    DA = A.alloc([2, 128], F32)
    for d in range(2):
        P.ts('dve', DA[0:66, d, :], DT[0:66, d, :], na[0:66, d:d + 1], None, ALU.mult)
    CUM = A.alloc([2, 128], F32)
    P.scan(CUM[0:66, 0, :], K['scanmask2'][0:66, 0, :], DA[0:66, 0, :], 0.0)
    P.scan(CUM[0:66, 1, :][:, ::-1], K['scanmask2'][0:66, 1, :][:, ::-1], DA[0:66, 1, :][:, ::-1], 0.0)
    P.dma('pool', crow_d[0].rearrange("(p s) -> p s", s=128), CUM[0:66, 0, :])
    P.dma('pool', crow_d[1].rearrange("(p s) -> p s", s=128), CUM[0:66, 1, :])
    LND = A.alloc([2, 128], F32)
    P.act(LND[0:66], DT[0:66], AF.Ln)
    CQ0 = A.alloc([2, 128], F32)
    P.tt('dve', CQ0[0:66], CUM[0:66], LND[0:66], ALU.subtract)
    CL = A.alloc([2, 2], F32)
    C4 = CUM[0:66].rearrange("p d (c s) -> p d c s", s=64)
    P.copy('dve', CL[0:66, 0, :], C4[:, 0, :, 63])
    P.copy('dve', CL[0:66, 1, :], C4[:, 1, :, 0])
    P.dma('pool', clr_d[0, 0:132].rearrange("(p c) -> p c", c=2), CL[0:66, 0, :])
    P.dma('pool', clr_d[1, 0:132].rearrange("(p c) -> p c", c=2), CL[0:66, 1, :])
    W0 = A.alloc([2, 128], F32)
    W4 = W0[0:66].rearrange("p d (c s) -> p d c s", s=64)
    P.tt('dve', W4, CL[0:66].unsqueeze(3).to_broadcast([66, 2, 2, 64]), C4, ALU.subtract)
    P.act(W0[0:66], W0[0:66], AF.Exp)
    P.tt('dve', W0[0:66], W0[0:66], DT[0:66], ALU.mult)
    CQ = A.alloc([2, 66], F32)
    WT = A.alloc([2, 66], F32)
    for d in range(2):
        for (src, dst) in ((CQ0, CQ), (W0, WT)):
            pt = banks.b[(2 * d) % 8 + (0 if src is CQ0 else 1)]
            P.transpose(pt[:, 0:66], src[0:66, d, :], K['ident'][0:66, 0:66])
            P.copy('act', dst[:, d, :], pt[:, 0:66])
    clr = A.alloc([132], F32)
    P.dma('sp', clr[0:2, :], clr_d[:, 0:132])
    DEC = A.alloc([132], F32)
    pd = banks.b[4]
    P.mm(pd[:, 0:132], K['sel2'][0:2, :], clr[0:2, :])
    P.act(DEC[:, :], pd[:, 0:132], AF.Exp)
    xBt = A.alloc([66, 128], BF16)
    for p in range(66):
        c0 = pcol(128 * p)
        pt = banks.b[p % 4].bitcast(BF16)
        P.transpose(pt[:, 0:128], XB[:, c0:c0 + 128], K['identb'][:, :])
        P.copy('act' if p % 2 == 0 else 'dve', xBt[:, p, :], pt[:, 0:128])
    BW = A.alloc([66, 128], BF16)
    for d in range(2):
        P.tt('dve', BW[:, :, 64 * d:64 * d + 64], xBt[:, :, 64:128], WT[:, d, :].unsqueeze(2).to_broadcast([128, 66, 64]), ALU.mult)
    ST = A.alloc([NC_ + 2, 64], F32)
    STb = A.alloc([NC_ + 2, 64], BF16)
    P.memset('pool', ST[:, 0:2, :], 0.0)
    P.memset('pool', ST[:, NC_:NC_ + 2, :], 0.0)
    for g0 in range(0, 66, 8):
        npair = min(8, 66 - g0)
        for i in range(npair):
            p = g0 + i
            for j in range(2):
                P.mm(banks.b[j][:, i * 64:(i + 1) * 64], BW[64 * j:64 * j + 64, p, :], xBt[64 * j:64 * j + 64, p, 0:64])
        for j in range(2):
            src = banks.b[j][:, 0:npair * 64].rearrange("p (c v) -> p c v", v=64)
            c0 = 2 * g0 + j
            P.copy('act', ST[0:64, c0 + 2:c0 + 1 + 2 * npair:2, :], src[0:64])
            P.copy('dve', ST[64:128, c0:c0 + 2 * npair - 1:2, :], src[64:128])
    fsteps = [(c + 2, c + 1, c) for c in range(1, NC_)]
    bsteps = [(c, c + 1, c) for c in (2, 1, 0)] + [(131, 0, 131)] + [(c, c + 1, c) for c in range(130, 4, -1)]
    for i in range(max(len(fsteps), len(bsteps))):
        if i < len(fsteps):
            o_, i_, d_ = fsteps[i]
            P.stt(ST[0:64, o_, :], ST[0:64, i_, :], DEC[0:64, d_:d_ + 1], ST[0:64, o_, :], ALU.mult, ALU.add)
        if i < len(bsteps):
            o_, i_, d_ = bsteps[i]
            P.stt(ST[64:128, o_, :], ST[64:128, i_, :], DEC[64:128, d_:d_ + 1], ST[64:128, o_, :], ALU.mult, ALU.add)
    P.copy('dve', ST[64:128, 132, :], ST[64:128, 0, :])
    P.copy('dve', STb[:], ST[:])
    crt = [A.alloc([512], F32) for _ in range(2)]
    SG = [A.alloc([2, 512], F32) for _ in range(2)]
    Ls = [A.alloc([512], F32) for _ in range(2)]
    Mt = [A.alloc([512], BF16) for _ in range(2)]
    ec = [A.alloc([512], F32) for _ in range(2)]
    Cs = [A.alloc([512], BF16) for _ in range(2)]
    yv = [A.alloc([512], F32) for _ in range(2)]
    yb = [A.alloc([512], BF16) for _ in range(2)]
    tiles2 = [(ti, t0, n) for ti, (t0, n) in enumerate(TOK_TILES) if not (t0 < CTXL and not with_ctx)]

    def stageA(ti, t0, n):
        i2 = ti % 2
        npair = n // 128
        p0 = t0 // 128
        pc0 = pcol(t0)
        cr = crt[i2]
        P.dma('sp', cr[0:2, 0:n], crow_d[:, t0:t0 + n])
        pe_ = banks.b[0]
        P.mm(pe_[:, 0:n], K['sel2'][0:2, :], cr[0:2, 0:n])
        P.act(ec[i2][:, 0:n], pe_[:, 0:n], AF.Exp)
        P.tt('dve', Cs[i2][:, 0:n], C2[:, pc0:pc0 + n], ec[i2][:, 0:n], ALU.mult)
        for d in range(2):
            pb = banks.b[1 + d]
            P.mm(pb[:, 0:n], K['selrow'][0:2, d, :], cr[0:2, 0:n], start=True, stop=False)
            P.mm(pb[:, 0:n], K['identb'][:, :], K['negF' if d == 0 else 'negB'][:, 0:n], start=False, stop=True)
            P.tt('dve', SG[i2][:, d, 0:n].rearrange("p (a b) -> p a b", b=128),
                 pb[:, 0:n].rearrange("p (a b) -> p a b", b=128),
                 CQ[:, d, p0:p0 + npair].unsqueeze(2).to_broadcast([128, npair, 128]), ALU.subtract)
        P.act(SG[i2][:, :, 0:n], SG[i2][:, :, 0:n], AF.Exp)
        P.tt('pool', Ls[i2][:, 0:n], SG[i2][:, 0, 0:n], SG[i2][:, 1, 0:n], ALU.add)
        pcb = banks.b[3]
        for i in range(npair):
            cs = slice(pc0 + 128 * i, pc0 + 128 * (i + 1))
            P.mm(pcb[:, 128 * i:128 * (i + 1)], XB[64:128, cs], C2[64:128, cs])
        P.tt('dve', Mt[i2][:, 0:n], pcb[:, 0:n], Ls[i2][:, 0:n], ALU.mult)

    def stageB(ti, t0, n):
        i2 = ti % 2
        npair = n // 128
        p0 = t0 // 128
        pc0 = pcol(t0)
        Y = banks.b[4 + i2]
        for i in range(npair):
            p = p0 + i
            P.mm(Y[0:64, 128 * i:128 * (i + 1)], xBt[:, p, 0:64], Mt[i2][:, 128 * i:128 * (i + 1)], start=True, stop=False)
            for j in range(2):
                c = 2 * p + j
                kk = 64 if c == 3 else 128
                P.mm(Y[0:64, 128 * i + 64 * j:128 * i + 64 * (j + 1)], STb[0:kk, c + 1, :],
                     Cs[i2][0:kk, 128 * i + 64 * j:128 * i + 64 * (j + 1)], start=False, stop=(j == 1))
        P.stt(yv[i2][0:64, 0:n], XB[0:64, pc0:pc0 + n], par[0:64, 4:5], Y[0:64, 0:n], ALU.mult, ALU.add)
        y = yb[i2]
        P.stt(y[0:64, 0:n], yv[i2][0:64, 0:n], 0.5, sgz[0:64, t0:t0 + n], ALU.mult, ALU.mult)
        store_y(P, io, 3, t0, n, y[0:64, 0:n])

    nt2 = len(tiles2)
    for it in range(nt2 + 1):
        if it < nt2:
            stageA(*tiles2[it])
        if it >= 1:
            stageB(*tiles2[it - 1])

OFF_A, OFF_B, OFF_C, OFF_D = 0, 800, 1312, 2336


def rope_tables():
    n_freq = 8
    inv = (np.float32(10000.0) ** (-(np.arange(n_freq, dtype=np.float32)) / np.float32(n_freq))).astype(np.float32)
    t = np.arange(SEQ)
    pos_r = (t // 64).astype(np.float32)
    pos_c = (t % 64).astype(np.float32)
    ang_r = pos_r[:, None] * inv
    ang_c = pos_c[:, None] * inv
    ang = np.concatenate([ang_r, ang_r, ang_c, ang_c], axis=-1).astype(np.float32)
    cos = np.cos(ang).astype(np.float32).T
    sin = np.sin(ang).astype(np.float32).T
    sign = np.ones(32, np.float32)
    for a in range(2):
        sign[a * 16:a * 16 + 8] = -1.0
    sins = sin * sign[:, None]
    cosT = np.zeros((128, SEQ), np.float32)
    sinT = np.zeros((128, SEQ), np.float32)
    for c in range(2):
        cosT[64 * c:64 * c + 32] = cos
        sinT[64 * c:64 * c + 32] = sins
    return cosT, sinT


def rot_perm():
    perm = np.zeros(32, np.int64)
    for a in range(2):
        for f in range(8):
            perm[a * 16 + f] = a * 16 + 8 + f
            perm[a * 16 + 8 + f] = a * 16 + f
    return perm


def prep_w_attn(w_in_l, h):
    W = np.zeros((D, 640), np.float32)
    perm = rot_perm()
    for c in range(2):
        qc = OFF_C + h * 64 + c * 32
        kc = OFF_C + 256 + h * 64 + c * 32
        W[:, 64 * c:64 * c + 32] = w_in_l[:, qc:qc + 32]
        W[:, 128 + 64 * c:128 + 64 * c + 32] = w_in_l[:, qc + perm]
        W[:, 256 + 64 * c:256 + 64 * c + 32] = w_in_l[:, kc:kc + 32]
        W[:, 384 + 64 * c:384 + 64 * c + 32] = w_in_l[:, kc + perm]
    W[:, 512:576] = w_in_l[:, OFF_C + 768 + h * 64:OFF_C + 768 + h * 64 + 64]
    W[:, 576:640] = w_in_l[:, OFF_C + 512 + h * 64:OFF_C + 512 + h * 64 + 64]
    return W


def mix_consts():
    selZ = np.zeros((128, 64), np.float32)
    selZ[64, :] = 1.0
    t = np.arange(512)
    scanmask = np.zeros((64, 512), np.float32)
    scanmask[0:32] = (t % 64 != 0).astype(np.float32)[None]
    scanmask[32:64] = (t % 64 != 63).astype(np.float32)[None]
    j = np.arange(128)[:, None]
    i = np.arange(128)[None, :]
    same = (j // 64) == (i // 64)
    mF = (same & (j <= i)).astype(np.float32)
    mB = (same & (j >= i)).astype(np.float32)
    scanmask2 = np.zeros((128, 2, 128), np.float32)
    s_ = np.arange(128)
    scanmask2[:, 0, :] = (s_ % 64 != 0).astype(np.float32)[None]
    scanmask2[:, 1, :] = (s_ % 64 != 63).astype(np.float32)[None]
    sel2 = np.zeros((2, 128), np.float32)
    sel2[0, 0:64] = 1.0
    sel2[1, 64:128] = 1.0
    selrow = np.zeros((2, 2, 128), np.float32)
    selrow[0, 0, :] = 1.0
    selrow[1, 1, :] = 1.0
    NEG = -30000.0
    negF = np.where(mF > 0, 0.0, NEG).astype(np.float32)
    negB = np.where(mB > 0, 0.0, NEG).astype(np.float32)
    return {'selZ': selZ, 'ident': np.eye(128, dtype=np.float32), 'scanmask': scanmask,
            'maskF': np.tile(mF, (1, 4)), 'maskB': np.tile(mB, (1, 4)), 'scanmask2': scanmask2,
            'sel2': sel2, 'selrow': selrow, 'negF': np.tile(negF, (1, 4)), 'negB': np.tile(negB, (1, 4))}


def build_mix_kernel(nc, P, layer, with_ctx, mixers):
    banks = Banks(P)
    K = load_consts_tok(nc, P)
    epst = P.sbuf("eps_t", [128, 1], F32)
    P.memset('pool', epst[:], EPS)
    K['eps'] = epst
    selZ_d = nc.dram_tensor("selZ", [128, 64], F32, kind="ExternalInput").ap()
    selZ = P.sbuf("selZ_sb", [128, 64], F32)
    P.dma('sp', selZ[:], selZ_d)
    K['selZ'] = selZ
    ident_d = nc.dram_tensor("ident", [128, 128], F32, kind="ExternalInput").ap()
    ident = P.sbuf("ident_sb", [128, 128], F32)
    P.dma('sp', ident[:], ident_d)
    identb = P.sbuf("identb_sb", [128, 128], BF16)
    P.copy('pool', identb[:], ident[:])
    K['ident'] = ident
    K['identb'] = identb
    one = P.sbuf("one_t", [128, 1], F32)
    P.memset('pool', one[:], 1.0)
    K['one'] = one
    for nm, shp in (('scanmask', [64, 512]), ('maskF', [128, 512]), ('maskB', [128, 512]),
                    ('scanmask2', [128, 2, 128]), ('sel2', [2, 128]), ('selrow', [2, 2, 128])):
        d_ = nc.dram_tensor(nm, shp, F32, kind="ExternalInput").ap()
        t_ = P.sbuf(nm + "_sb", shp, F32)
        P.dma('sp', t_[:], d_)
        K[nm] = t_
    for nm in ('negF', 'negB'):
        d_ = nc.dram_tensor(nm, [128, 512], F32, kind="ExternalInput").ap()
        t_ = P.sbuf(nm + "_f", [128, 512], F32)
        P.dma('sp', t_[:], d_)
        tb_ = P.sbuf(nm + "_sb", [128, 512], BF16)
        P.copy('pool', tb_[:], t_[:])
        K[nm] = tb_
    io = {}
    xn_d = nc.dram_tensor("xn", [D, NTOK], BF16, kind="ExternalInput").ap()
    xn_v = xn_d.rearrange("(kc p) t -> p kc t", p=128)
    io['yT'] = nc.dram_tensor("yT", [4, 64, NTOK], BF16, kind="ExternalOutput").ap()
    A = P.arena("arena", 190 * 1024)
    if 'attn' in mixers:
        io['w_attn'] = nc.dram_tensor("w_attn", [D, 640], F32, kind="ExternalInput").ap()
        io['cosT'] = nc.dram_tensor("cosT", [128, SEQ], F32, kind="ExternalInput").ap()
        io['sinT'] = nc.dram_tensor("sinT", [128, SEQ], F32, kind="ExternalInput").ap()
        io['diff_lam'] = nc.dram_tensor("diff_lam", [128], F32, kind="ExternalInput").ap()
        io['diff_subln_w'] = nc.dram_tensor("diff_subln_w", [64, 1], F32, kind="ExternalInput").ap()
        attention_phase(nc, P, A, banks, K, xn_v, io, layer, with_ctx)
    if 'gla' in mixers:
        _add_gla(nc, P, A, banks, K, xn_v, io, layer, with_ctx)
    if 'ssd' in mixers:
        io['w_ssd'] = nc.dram_tensor("w_ssd", [D, 322], F32, kind="ExternalInput").ap()
        io['ssd_par'] = nc.dram_tensor("ssd_par", [128, 5], F32, kind="ExternalInput").ap()
        io['ssd_cpar'] = nc.dram_tensor("ssd_cpar", [128, 2, 5], F32, kind="ExternalInput").ap()
        io['ssd_scr'] = nc.dram_tensor("ssd_scr", [6, NTOK], F32, kind="Internal").ap()
        ssd_phase(nc, P, A, banks, K, xn_v, io, layer, with_ctx)
    if 'lru' in mixers:
        io['w_lru'] = nc.dram_tensor("w_lru", [D, 192], F32, kind="ExternalInput").ap()
        io['lru_par'] = nc.dram_tensor("lru_par", [128, 9], F32, kind="ExternalInput").ap()
        io['lru_gw'] = nc.dram_tensor("lru_gw", [64, 256], F32, kind="ExternalInput").ap()
        lru_phase(nc, P, A, banks, K, xn_v, io, layer, with_ctx)


def _add_gla(nc, P, A, banks, K, xn_v, io, layer, with_ctx):
    io['w_gla'] = nc.dram_tensor("w_gla", [D, 320], F32, kind="ExternalInput").ap()
    io['gla_par'] = nc.dram_tensor("gla_par", [64, 2], F32, kind="ExternalInput").ap()
    io['gla_w2bd'] = nc.dram_tensor("gla_w2bd", [32, 64], F32, kind="ExternalInput").ap()
    gla_phase(nc, P, A, banks, K, xn_v, io, layer, with_ctx)


def prep_gla(inp, l, h):
    w_in_l = inp['w_in'][l]
    W = np.zeros((D, 288), np.float32)
    q = w_in_l[:, OFF_A + h * 32:OFF_A + h * 32 + 32]
    k = w_in_l[:, OFF_A + 128 + h * 32:OFF_A + 128 + h * 32 + 32]
    W[:, 0:32] = q
    W[:, 32:64] = q
    W[:, 64:96] = k
    W[:, 96:128] = k
    W[:, 128:144] = w_in_l[:, OFF_A + 512:OFF_A + 528]
    W[:, 144:160] = w_in_l[:, OFF_A + 528:OFF_A + 544]
    W[:, 192:256] = w_in_l[:, OFF_A + 544 + h * 64:OFF_A + 544 + h * 64 + 64]
    W[:, 256:288] = 0
    Wv = w_in_l[:, OFF_A + 256 + h * 64:OFF_A + 256 + h * 64 + 64]
    W2 = np.zeros((D, 320), np.float32)
    W2[:, 0:256] = W[:, 0:256]
    W2[:, 256:320] = Wv
    par = np.zeros((64, 2), np.float32)
    par[0:32, 0] = inp['gla_b2'][l][0][h * 32:(h + 1) * 32]
    par[32:64, 0] = inp['gla_b2'][l][1][h * 32:(h + 1) * 32]
    par[:, 1] = inp['gla_norm_w'][l]
    w2bd = np.zeros((32, 64), np.float32)
    w2bd[0:16, 0:32] = inp['gla_w2'][l][0][:, h * 32:(h + 1) * 32]
    w2bd[16:32, 32:64] = inp['gla_w2'][l][1][:, h * 32:(h + 1) * 32]
    return {"w_gla": W2, "gla_par": par, "gla_w2bd": w2bd}


def prep_ssd(inp, l, h):
    w_in_l = inp['w_in'][l]
    gr = h // 2
    W = np.zeros((D, 322), np.float32)
    cx = slice(OFF_D + h * 64, OFF_D + h * 64 + 64)
    cB = slice(OFF_D + 256 + gr * 64, OFF_D + 256 + gr * 64 + 64)
    cC = slice(OFF_D + 384 + gr * 64, OFF_D + 384 + gr * 64 + 64)
    W[:, 0:64] = w_in_l[:, cx]
    W[:, 64:128] = w_in_l[:, cB]
    W[:, 128:192] = w_in_l[:, cC]
    W[:, 192:256] = w_in_l[:, cC]
    W[:, 256:320] = w_in_l[:, OFF_D + 520 + h * 64:OFF_D + 520 + h * 64 + 64]
    W[:, 320] = w_in_l[:, OFF_D + 512 + h]
    W[:, 321] = w_in_l[:, OFF_D + 516 + h]
    par = np.zeros((128, 5), np.float32)
    par[:, 0] = inp['ssd_dt_bias'][l][0][h]
    par[:, 1] = inp['ssd_dt_bias'][l][1][h]
    par[:, 2] = inp['ssd_a_log'][l][0][h]
    par[:, 3] = inp['ssd_a_log'][l][1][h]
    par[:, 4] = inp['ssd_d'][l][h]
    cw, cb = inp['ssd_conv_w'][l], inp['ssd_conv_b'][l]
    cpar = np.zeros((128, 2, 5), np.float32)
    ch = [np.r_[h * 64:h * 64 + 64, 256 + gr * 64:256 + gr * 64 + 64],
          np.r_[384 + gr * 64:384 + gr * 64 + 64, 384 + gr * 64:384 + gr * 64 + 64]]
    for g in range(2):
        cpar[:, g, 0:4] = cw[:, ch[g]].T
        cpar[:, g, 4] = cb[ch[g]]
    return {"w_ssd": W, "ssd_par": par, "ssd_cpar": cpar}


def prep_lru(inp, l, h):
    w_in_l = inp['w_in'][l]
    W = np.zeros((D, 192), np.float32)
    xs = w_in_l[:, OFF_B + h * 64:OFF_B + h * 64 + 64]
    W[:, 0:64] = xs
    W[:, 64:128] = xs
    W[:, 128:192] = w_in_l[:, OFF_B + 256 + h * 64:OFF_B + 256 + h * 64 + 64]
    sl = slice(h * 64, h * 64 + 64)
    par = np.zeros((128, 9), np.float32)
    for d in range(2):
        rows = slice(64 * d, 64 * d + 64)
        par[rows, 0:4] = inp['lru_conv_w'][l][:, sl].T
        par[rows, 4] = inp['lru_conv_b'][l][sl]
        par[rows, 5] = inp['lru_ba'][l][d][sl]
        par[rows, 6] = inp['lru_bx'][l][d][sl]
        par[rows, 7] = inp['lru_lam'][l][d][sl]
    gw = np.zeros((64, 256), np.float32)
    for d in range(2):
        gw[:, 64 * d:64 * d + 64] = inp['lru_wa'][l][d][h]
        gw[:, 128 + 64 * d:128 + 64 * d + 64] = inp['lru_wx'][l][d][h]
    return {"w_lru": W, "lru_par": par, "lru_gw": gw}


def mix_inputs(inp, l, h, mixers, cosT, sinT, cst):
    m = dict(cst)
    if 'ssd' in mixers:
        m.update(prep_ssd(inp, l, h))
    if 'gla' in mixers:
        m.update(prep_gla(inp, l, h))
    if 'attn' in mixers:
        m.update({"w_attn": prep_w_attn(inp['w_in'][l], h), "cosT": cosT, "sinT": sinT,
                  "diff_lam": np.ascontiguousarray(inp['diff_lam'][l].reshape(-1)),
                  "diff_subln_w": np.ascontiguousarray(inp['diff_subln_w'][l].reshape(64, 1))})
    if 'lru' in mixers:
        m.update(prep_lru(inp, l, h))
    return m


def _rt_of(hl, hc, b, q):
    return np.ascontiguousarray(np.concatenate([hc[b, 64 * q:64 * q + 64].T, hl[b, 2048 * q:2048 * (q + 1)].T], axis=1))


def _gather_cols(parts):
    return np.ascontiguousarray(np.concatenate([p[:, 0:64] for p in parts] + [p[:, 64:] for p in parts], axis=1))


def kernel_unfused(**inp):
    inp = {k: np.asarray(v) for k, v in inp.items()}
    x, ctx, c, c_ctx = inp['x'], inp['ctx'], inp['c'], inp['c_ctx']
    cosT, sinT = rope_tables()
    cst = mix_consts()
    cvecs = [np.ascontiguousarray(np.stack([pk(c[b]), pk(c_ctx)], axis=2)) for b in range(2)]
    maps = []
    for core in range(8):
        b, q = core // 4, core % 4
        maps.append({"RT": _rt_of(x, ctx, b, q), "cvec": cvecs[b], "wmod_n": inp['w_mod'][0],
                     "bmod_n": pk(inp['b_mod'][0]), "normw": pk(inp['norm_w'][0])})
    res = _launch(lambda nc, P: build_tok_kernel(nc, P, True, False, False), maps)
    RT = [m["RT"] for m in maps]
    xnT = [np.asarray(r["xnT"]) for r in res]
    out = None
    for l in range(2):
        last = (l == 1)
        xn_full = [_gather_cols(xnT[4 * b:4 * b + 4]) for b in range(2)]
        maps = []
        for core in range(8):
            b, h = core // 4, core % 4
            m = {"xn": xn_full[b]}
            m.update(mix_inputs(inp, l, h, ('gla', 'lru', 'attn', 'ssd'), cosT, sinT, cst))
            maps.append(m)
        res = _launch(lambda nc, P: build_mix_kernel(nc, P, l, not last, ('gla', 'lru', 'attn', 'ssd')), maps)
        yT = [np.asarray(r["yT"]) for r in res]
        maps = []
        for core in range(8):
            b, q = core // 4, core % 4
            yb = np.stack([yT[4 * b + h] for h in range(4)], axis=1).reshape(1024, NTOK)
            yq = np.ascontiguousarray(np.concatenate([yb[:, 64 * q:64 * q + 64],
                                                      yb[:, CTXL + 2048 * q:CTXL + 2048 * (q + 1)]], axis=1))
            m = {"RT": RT[core], "cvec": cvecs[b], "yT": yq, "wout": inp['w_out'][l], "ssdnw": pk(inp['ssd_norm_w'][l]),
                 "wmod_g": inp['w_mod'][l], "bmod_g": pk(inp['b_mod'][l])}
            if last:
                m["fnw"] = pk(inp['final_norm_w'])
            else:
                m.update({"wmod_n": inp['w_mod'][l + 1], "bmod_n": pk(inp['b_mod'][l + 1]), "normw": pk(inp['norm_w'][l + 1])})
            maps.append(m)
        res = _launch(lambda nc, P: build_tok_kernel(nc, P, False, last, True), maps)
        if last:
            out = np.zeros((2, SEQ, D), np.float32)
            for core in range(8):
                b, q = core // 4, core % 4
                out[b, 2048 * q:2048 * (q + 1), :] = np.asarray(res[core]["outT"]).T
        else:
            RT = [np.asarray(r["RTout"]) for r in res]
            xnT = [np.asarray(r["xnT"]) for r in res]
    return out

CHUNKS = [(0, 256)] + [(256 + 2048 * k, 2048) for k in range(4)]
GROUPS = [[0, 1, 2, 3], [4, 5, 6, 7]]


def fs_mod(nc, P, K, banks, A, cvec_d, wmod_d, bmod_d, name):
    cT = A.alloc([8, 2], F32)
    P.dma('sp', cT[:], cvec_d)
    e = A.alloc([8, 2], F32)
    P.act(e[:], cT[:], AF.Exp, scale=-1.0)
    P.ts('dve', e[:], e[:], 1.0, None, ALU.add)
    P.recip(e[:], e[:])
    sc = A.alloc([8, 2], F32)
    P.tt('dve', sc[:], cT[:], e[:], ALU.mult)
    bT = A.alloc([6], F32)
    P.dma('sp', bT[:], bmod_d)
    wb = A.alloc([8, 768], F32)
    for kc in range(8):
        P.dma('sp' if kc % 2 == 0 else 'act', wb[:, kc, :], wmod_d[kc * 128:(kc + 1) * 128, :])
    ps = banks.b[7]
    for fc in range(6):
        for kc in range(8):
            P.mm(ps[:, fc * 2:fc * 2 + 2], wb[:, kc, fc * 128:(fc + 1) * 128], sc[:, kc, :], start=(kc == 0), stop=(kc == 7))
    modT = P.sbuf(name, [128, 6, 2], F32)
    P.tt('dve', modT[:], ps[:, 0:12].rearrange("p (a b) -> p a b", b=2), bT[:].unsqueeze(2).to_broadcast([128, 6, 2]), ALU.add)
    return modT


def fs_token_phase(nc, P, K, banks, A, io, L, mode):
    A.reset()
    D_ = io['dram']
    first, last = mode == 'first', mode == 'last'
    if not first:
        gateT = fs_mod(nc, P, K, banks, A, io['cvec'], io['wmod'][L], io['bmod'][L], "modT_g%d" % L)
        wo32 = A.alloc([8, 256], F32)
        for kc in range(8):
            P.dma('sp' if kc % 2 == 0 else 'act', wo32[:, kc, :], io['wout'][L][kc * 128:(kc + 1) * 128, :])
        wo = A.alloc([8, 256], BF16)
        P.copy('dve', wo[:], wo32[:])
        sw = A.alloc([4], F32)
        P.dma('sp', sw[:], io['ssdnw'][L])
    if not last:
        LN = 0 if first else L + 1
        modN = fs_mod(nc, P, K, banks, A, io['cvec'], io['wmod'][LN], io['bmod'][LN], "modT_n%d" % LN)
        nw = A.alloc([2], F32)
        P.dma('sp', nw[:], io['normw'][LN])
        Acoef = A.alloc([2, 2], F32)
        P.ts('dve', Acoef[:], modN[:, 2:4, :], 1.0, None, ALU.add)
        P.tt('dve', Acoef[:], Acoef[:], nw[:].unsqueeze(2).to_broadcast([128, 2, 2]), ALU.mult)
    else:
        fw = A.alloc([2], F32)
        P.dma('sp', fw[:], io['fnw'])
    Rb = [A.alloc([2, 2048], F32) for _ in range(2)]
    yb_ = [A.alloc([8, 512], BF16) for _ in range(2)]
    sq = [A.alloc([512], BF16) for _ in range(4)]
    rst = [A.alloc([512], F32) for _ in range(2)]
    t1 = [A.alloc([512], F32) for _ in range(4)]
    ssr = [A.alloc([2048], F32) for _ in range(2)]
    ssg = [A.alloc([2048], F32) for _ in range(2)]
    xo = [A.alloc([2, 512], BF16) for _ in range(2)]
    Rsrc = io['RT_in'] if first else D_['Rs']
    Rsrc_v = Rsrc.rearrange("(fc p) t -> p fc t", p=128)
    Rs_v = D_['Rs'].rearrange("(fc p) t -> p fc t", p=128)
    stage = 'n%d' % (0 if first else L + 1) if not last else 'fin'
    chunks = [(ci, t0, n) for ci, (t0, n) in enumerate(CHUNKS) if not (last and ci == 0)]
    jobs = []
    for (ci, t0, n) in chunks:
        subs = [(s0, min(512, n - s0)) for s0 in range(0, n, 512)]
        for si, (s0, m_) in enumerate(subs):
            jobs.append(dict(ci=ci, t0=t0, n=n, s0=s0, m=m_, first=(si == 0), lastsub=(si == len(subs) - 1)))
    ybuf3 = yb_ + [A.alloc([8, 512], BF16)]

    def P1(j, ji):
        ci, t0, n, s0, m = j['ci'], j['t0'], j['n'], j['s0'], j['m']
        R = Rb[ci % 2]
        if j['first']:
            P.dma('sp', R[:, :, 0:n], Rsrc_v[:, :, t0:t0 + n])
        if first:
            return
        gy = D_['gy%d' % L][ci].rearrange("(kc p) t -> p kc t", p=128)
        yt = ybuf3[ji % 3]
        P.dma('act', yt[:, :, 0:m], gy[:, :, s0:s0 + m])
        ps = banks.b[0 + ji % 2]
        for i, kc in enumerate((1, 3, 5, 7)):
            s_ = sq[i % 2]
            P.act(s_[64:128, 0:m], yt[64:128, kc, 0:m], AF.Square)
            P.mm(ps[:, 0:m], K['onesb'][64:128, :], s_[64:128, 0:m], start=(i == 0), stop=(i == 3))
        r_ = rst[ji % 2]
        P.act(r_[:, 0:m], ps[:, 0:m], AF.Ln, scale=1.0 / 256, bias=K['eps'][:])
        P.act(r_[:, 0:m], r_[:, 0:m], AF.Exp, scale=-0.5)
        for i, kc in enumerate((1, 3, 5, 7)):
            P.stt(yt[64:128, kc, 0:m], yt[64:128, kc, 0:m], sw[64:128, i:i + 1], r_[64:128, 0:m], ALU.mult, ALU.mult)

    def P2(j, ji):
        if first:
            return
        ci, s0, m = j['ci'], j['s0'], j['m']
        isctx = 1 if ci == 0 else 0
        R = Rb[ci % 2]
        yt = ybuf3[ji % 3]
        for fc in range(2):
            po = banks.b[2 + (2 * ji + fc) % 4]
            for kc in range(8):
                P.mm(po[:, 0:m], wo[:, kc, fc * 128:(fc + 1) * 128], yt[:, kc, 0:m], start=(kc == 0), stop=(kc == 7))
            P.stt(R[:, fc, s0:s0 + m], po[:, 0:m], gateT[:, 4 + fc, isctx:isctx + 1], R[:, fc, s0:s0 + m], ALU.mult, ALU.add)

    def P3(j, ji):
        ci, t0, n, s0, m = j['ci'], j['t0'], j['n'], j['s0'], j['m']
        R = Rb[ci % 2]
        srow = ssr[ci % 2]
        pss = banks.b[6 + ji % 2]
        for fc in range(2):
            s_ = sq[2 + fc]
            P.act(s_[:, 0:m], R[:, fc, s0:s0 + m], AF.Square)
            P.mm(pss[:, 0:m], K['onesb'][:, :], s_[:, 0:m], start=(fc == 0), stop=(fc == 1))
        P.copy('dve', srow[0:1, s0:s0 + m], pss[0:1, 0:m])
        if j['lastsub']:
            P.dma('sp', D_['ssb_' + stage][:, t0:t0 + n], srow[0:1, 0:n])
            P.dma('sp', Rs_v[:, :, t0:t0 + n], R[:, :, 0:n])

    nj = len(jobs)
    for it in range(nj + 2):
        if it < nj:
            P1(jobs[it], it)
        if 0 <= it - 1 < nj:
            P2(jobs[it - 1], it - 1)
        if 0 <= it - 2 < nj:
            P3(jobs[it - 2], it - 2)
    P.collective("AllGather", [D_['ssb_' + stage]], [D_['ssg_' + stage]], GROUPS)
    for (ci, t0, n) in chunks:
        isctx = 1 if ci == 0 else 0
        R = Rb[ci % 2]
        P.dma('sp', R[:, :, 0:n], Rs_v[:, :, t0:t0 + n])
        subs = [(s0, min(512, n - s0)) for s0 in range(0, n, 512)]
        sg_ = ssg[ci % 2]
        P.dma('act', sg_[0:4, 0:n], D_['ssg_' + stage][:, t0:t0 + n])
        for si, (s0, m) in enumerate(subs):
            pt = banks.b[6 + si % 2]
            P.mm(pt[:, 0:m], K['ones'][0:4, :], sg_[0:4, s0:s0 + m])
            r_ = rst[si % 2]
            P.act(r_[:, 0:m], pt[:, 0:m], AF.Ln, scale=1.0 / D, bias=K['eps'][:])
            P.act(r_[:, 0:m], r_[:, 0:m], AF.Exp, scale=-0.5)
            if not last:
                x_ = xo[si % 2]
                for fc in range(2):
                    t_ = t1[fc]
                    P.stt(t_[:, 0:m], R[:, fc, s0:s0 + m], Acoef[:, fc, isctx:isctx + 1], r_[:, 0:m], ALU.mult, ALU.mult)
                    P.act(x_[:, fc, 0:m], t_[:, 0:m], AF.Identity, bias=modN[:, fc, isctx:isctx + 1])
                xb_v = D_['xnb_' + stage][ci].rearrange("(fc p) t -> p fc t", p=128)
                P.dma('sp', xb_v[:, :, s0:s0 + m], x_[:, :, 0:m])
            else:
                o_v = io['outT'].rearrange("(fc p) t -> p fc t", p=128)
                for fc in range(2):
                    t_ = t1[(si * 2 + fc) % 4]
                    P.stt(t_[:, 0:m], R[:, fc, s0:s0 + m], fw[:, fc:fc + 1], r_[:, 0:m], ALU.mult, ALU.mult)
                    P.dma('sp', o_v[:, fc, t0 - CTXL + s0:t0 - CTXL + s0 + m], t_[:, 0:m], final=True)
        if not last:
            P.collective("AllGather", [D_['xnb_' + stage][ci]], [D_['xng_' + stage][ci]], GROUPS)


def build_fused(nc, P):
    banks = Banks(P)
    K = load_consts_tok(nc, P)
    epst = P.sbuf("eps_t", [128, 1], F32)
    P.memset('pool', epst[:], EPS)
    K['eps'] = epst
    one = P.sbuf("one_t", [128, 1], F32)
    P.memset('pool', one[:], 1.0)
    K['one'] = one
    for nm, shp in (('selZ', [128, 64]), ('ident', [128, 128]), ('scanmask', [64, 512]), ('maskF', [128, 512]),
                    ('maskB', [128, 512]), ('scanmask2', [128, 2, 128]), ('sel2', [2, 128]), ('selrow', [2, 2, 128])):
        d_ = nc.dram_tensor(nm, shp, F32, kind="ExternalInput").ap()
        t_ = P.sbuf(nm + "_sb", shp, F32)
        P.dma('sp', t_[:], d_)
        K[nm] = t_
    identb = P.sbuf("identb_sb", [128, 128], BF16)
    P.copy('pool', identb[:], K['ident'][:])
    K['identb'] = identb
    onesb = P.sbuf("onesb_sb", [128, 128], BF16)
    P.memset('pool', onesb[:], 1.0)
    K['onesb'] = onesb
    for nm in ('negF', 'negB'):
        d_ = nc.dram_tensor(nm, [128, 512], F32, kind="ExternalInput").ap()
        t_ = P.sbuf(nm + "_f", [128, 512], F32)
        P.dma('sp', t_[:], d_)
        tb_ = P.sbuf(nm + "_sb", [128, 512], BF16)
        P.copy('pool', tb_[:], t_[:])
        K[nm] = tb_
    A = P.arena("arena", 184 * 1024)

    def din(name, shape, dt=F32):
        return nc.dram_tensor(name, list(shape), dt, kind="ExternalInput").ap()

    def dscr(name, shape, dt=F32):
        return nc.dram_tensor(name, list(shape), dt).ap()

    io = {'RT_in': din("RT", [256, NTOK]), 'cvec': din("cvec", [128, 8, 2]),
          'wmod': [din("wmod%d" % l, [D, 768]) for l in range(2)], 'bmod': [din("bmod%d" % l, [128, 6]) for l in range(2)],
          'normw': [din("normw%d" % l, [128, 2]) for l in range(2)], 'wout': [din("wout%d" % l, [D, 256]) for l in range(2)],
          'ssdnw': [din("ssdnw%d" % l, [128, 4]) for l in range(2)], 'fnw': din("fnw", [128, 2]),
          'outT': nc.dram_tensor("outT", [256, SEQ], F32, kind="ExternalOutput").ap()}
    Dm = {'Rs': dscr("Rs", [256, NTOK])}
    for stage in ('n0', 'n1', 'fin'):
        Dm['ssb_' + stage] = dscr("ssb_%s" % stage, [1, NTOK])
        Dm['ssg_' + stage] = dscr("ssg_%s" % stage, [4, NTOK])
    for stage in ('n0', 'n1'):
        Dm['xnb_' + stage] = [dscr("xnb_%s_%d" % (stage, ci), [256, n], BF16) for ci, (t0, n) in enumerate(CHUNKS)]
        Dm['xng_' + stage] = [dscr("xng_%s_%d" % (stage, ci), [D, n], BF16) for ci, (t0, n) in enumerate(CHUNKS)]
    for l in range(2):
        Dm['yb%d' % l] = [dscr("yb%d_%d" % (l, ci), [256, n], BF16) for ci, (t0, n) in enumerate(CHUNKS)]
        Dm['gy%d' % l] = [dscr("gy%d_%d" % (l, ci), [D, n], BF16) for ci, (t0, n) in enumerate(CHUNKS)]
    io['dram'] = Dm
    cosT = din("cosT", [128, SEQ])
    sinT = din("sinT", [128, SEQ])
    fs_token_phase(nc, P, K, banks, A, io, 0, 'first')
    for l in range(2):
        last = (l == 1)
        with_ctx = not last
        xng = Dm['xng_n%d' % l]

        def xn_load(xb, t0, n, xng=xng):
            if t0 < CTXL:
                P.dma('sp', xb[:, :, 0:n], xng[0].rearrange("(kc p) t -> p kc t", p=128)[:, :, t0:t0 + n])
            else:
                k = (t0 - CTXL) // 2048
                c0 = (t0 - CTXL) % 2048
                P.dma('sp', xb[:, :, 0:n], xng[k + 1].rearrange("(kc p) t -> p kc t", p=128)[:, :, c0:c0 + n])

        ybl = Dm['yb%d' % l]

        def y_store(mi, t0, n, src, ybl=ybl):
            if t0 < CTXL:
                P.dma('pool', ybl[0][mi * 64:(mi + 1) * 64, t0:t0 + n], src)
            else:
                k = (t0 - CTXL) // 2048
                c0 = (t0 - CTXL) % 2048
                P.dma('pool', ybl[k + 1][mi * 64:(mi + 1) * 64, c0:c0 + n], src)

        mio = {'y_store': y_store, 'cosT': cosT, 'sinT': sinT}
        mio['w_gla'] = din("w_gla%d" % l, [D, 320])
        mio['gla_par'] = din("gla_par%d" % l, [64, 2])
        mio['gla_w2bd'] = din("gla_w2bd%d" % l, [32, 64])
        mio['w_lru'] = din("w_lru%d" % l, [D, 192])
        mio['lru_par'] = din("lru_par%d" % l, [128, 9])
        mio['lru_gw'] = din("lru_gw%d" % l, [64, 256])
        mio['w_ssd'] = din("w_ssd%d" % l, [D, 322])
        mio['ssd_par'] = din("ssd_par%d" % l, [128, 5])
        mio['ssd_cpar'] = din("ssd_cpar%d" % l, [128, 2, 5])
        mio['ssd_scr'] = dscr("ssd_scr%d" % l, [6, NTOK])
        mio['w_attn'] = din("w_attn%d" % l, [D, 640])
        mio['diff_lam'] = din("diff_lam%d" % l, [128])
        mio['diff_subln_w'] = din("diff_subln_w%d" % l, [64, 1])
        gla_phase(nc, P, A, banks, K, xn_load, mio, l, with_ctx)
        lru_phase(nc, P, A, banks, K, xn_load, mio, l, with_ctx)
        ssd_phase(nc, P, A, banks, K, xn_load, mio, l, with_ctx)
        def after_q(q0, nq, l=l):
            if q0 < CTXL:
                P.collective("AllGather", [Dm['yb%d' % l][0]], [Dm['gy%d' % l][0]], GROUPS)
            elif (q0 - CTXL + nq) % 2048 == 0:
                k = (q0 - CTXL) // 2048
                P.collective("AllGather", [Dm['yb%d' % l][k + 1]], [Dm['gy%d' % l][k + 1]], GROUPS)
        mio['after_q'] = after_q
        attention_phase(nc, P, A, banks, K, xn_load, mio, l, with_ctx)
        fs_token_phase(nc, P, K, banks, A, io, l, 'last' if last else 'mid')


def fused_inputs(inp, core, cosT, sinT, cst):
    b, q = core // 4, core % 4
    h = q
    x, ctx, c, c_ctx = inp['x'], inp['ctx'], inp['c'], inp['c_ctx']
    fsl = slice(256 * q, 256 * q + 256)
    m = dict(cst)
    m['RT'] = np.ascontiguousarray(np.concatenate([ctx[b][:, fsl].T, x[b][:, fsl].T], axis=1))
    m['cvec'] = np.ascontiguousarray(np.stack([pk(c[b]), pk(c_ctx)], axis=2))
    m['cosT'] = cosT
    m['sinT'] = sinT
    m['fnw'] = pk(inp['final_norm_w'][fsl])
    perm = np.array([mi * 256 + hh * 64 + j for hh in range(4) for mi in range(4) for j in range(64)])
    for l in range(2):
        cols = np.r_[256 * q:256 * q + 256, 1024 + 256 * q:1024 + 256 * q + 256, 2048 + 256 * q:2048 + 256 * q + 256]
        m['wmod%d' % l] = np.ascontiguousarray(inp['w_mod'][l][:, cols])
        m['bmod%d' % l] = pk(inp['b_mod'][l][cols])
        m['normw%d' % l] = pk(inp['norm_w'][l][fsl])
        m['wout%d' % l] = np.ascontiguousarray(inp['w_out'][l][perm][:, fsl])
        sw = np.zeros((128, 4), np.float32)
        for i in range(4):
            sw[64:128, i] = inp['ssd_norm_w'][l][i * 64:(i + 1) * 64]
        m['ssdnw%d' % l] = sw
        mi_ = mix_inputs(inp, l, h, ('gla', 'lru', 'attn', 'ssd'), cosT, sinT, cst)
        for k_, v_ in mi_.items():
            if k_ in cst or k_ in ('cosT', 'sinT'):
                continue
            m[k_ + str(l)] = v_
    return m


def kernel(**inp):
    inp = {k: np.asarray(v) for k, v in inp.items()}
    cosT, sinT = rope_tables()
    cst = mix_consts()
    maps = [fused_inputs(inp, core, cosT, sinT, cst) for core in range(8)]
    res = _launch(build_fused, maps)
    out = np.zeros((2, SEQ, D), np.float32)
    for core in range(8):
        b, q = core // 4, core % 4
        out[b, :, 256 * q:256 * q + 256] = np.asarray(res[core]["outT"]).T
    return out
```

```python
import numpy as np
from contextlib import ExitStack
import concourse.bass as bass
import concourse.mybir as mybir
from concourse.bass_utils import run_bass_kernel_spmd

F32 = mybir.dt.float32
BF16 = mybir.dt.bfloat16
AF = mybir.ActivationFunctionType
ALU = mybir.AluOpType
AX = mybir.AxisListType
ENGS = ('pe', 'act', 'dve', 'pool', 'sp')
ESZ = {F32: 4, BF16: 2}


class Prog:
    def __init__(self, nc, stack, n_dma_sems=40):
        self.nc = nc
        self.stack = stack
        self.sem = {e: stack.enter_context(nc.semaphore('sem_' + e)) for e in ENGS}
        self.cnt = {e: 0 for e in ENGS}
        self.known = {e: {} for e in ENGS}
        self.stream = {e: [] for e in ENGS}
        self.dsem = [stack.enter_context(nc.semaphore('dsem%d' % i)) for i in range(n_dma_sems)]
        self.dcum = [0] * n_dma_sems
        self.drr = 0
        self.rows = {}
        self.trk = {}
        self.final_events = []
        self.nwaits = 0
        self.ndma = 0
        self.arenas = {}

    def sbuf(self, name, shape, dtype=F32):
        t = self.stack.enter_context(self.nc.sbuf_tensor(name, list(shape), dtype))
        self.rows[name] = int(np.prod(shape[1:])) * ESZ[dtype]
        return t

    def psum(self, name, shape, dtype=F32):
        t = self.stack.enter_context(self.nc.psum_tensor(name, list(shape), dtype))
        self.rows[name] = int(np.prod(shape[1:])) * ESZ[dtype]
        return t

    def arena(self, name, nbytes):
        t = self.sbuf(name, [128, nbytes // 4], F32)
        a = Arena(self, name, t, nbytes)
        self.arenas[name] = a
        return a

    def _box(self, ap):
        b = self._box0(ap)
        a = self.arenas.get(b[0])
        if a is not None:
            rid = a.region_of(b[3], b[4])
            return ((b[0], rid),) + b[1:]
        return b

    def _box0(self, ap):
        name = ap.tensor.name
        es = ESZ.get(ap.dtype, 4)
        off = int(ap.offset) * es
        pairs = ap.ap
        if name in self.rows:
            rs = self.rows[name]
            p0 = off // rs
            pst, pc = pairs[0]
            p1 = p0 + (pc if pst != 0 else 1)
            fo = off % rs
            rest = pairs[1:]
        else:
            p0, p1 = 0, 1
            fo = off
            rest = pairs
        lo = fo
        hi = fo
        for st, c in rest:
            d = st * (c - 1) * es
            if d < 0:
                lo += d
            else:
                hi += d
        return (name, p0, p1, lo, hi + es)

    @staticmethod
    def _ov(a, b):
        return a[1] < b[2] and b[1] < a[2] and a[3] < b[4] and b[3] < a[4]

    @staticmethod
    def _inside(a, b):
        return a[1] >= b[1] and a[2] <= b[2] and a[3] >= b[3] and a[4] <= b[4]

    def _deps(self, reads, writes):
        deps = []
        for ap in reads:
            b = self._box(ap)
            t = self.trk.get(b[0])
            if t:
                for (wb, ev, clk) in t['w']:
                    if self._ov(b, wb):
                        deps.append((ev, clk, 'raw'))
        for ap in writes:
            b = self._box(ap)
            t = self.trk.get(b[0])
            if t:
                for (wb, ev, clk) in t['w']:
                    if self._ov(b, wb):
                        deps.append((ev, clk, 'waw'))
                for (rb, ev, clk) in t['r']:
                    if self._ov(b, rb):
                        deps.append((ev, clk, 'war'))
        return deps

    @staticmethod
    def _merge(lst):
        merged = {}
        for (rb, rev, rclk) in lst:
            m = merged.get(rev[0])
            if m is None:
                merged[rev[0]] = (rb, rev, rclk)
            else:
                mb, mev, mclk = m
                nb = (rb[0], min(rb[1], mb[1]), max(rb[2], mb[2]), min(rb[3], mb[3]), max(rb[4], mb[4]))
                merged[rev[0]] = (nb, rev, rclk) if rev[1] > mev[1] else (nb, mev, mclk)
        return list(merged.values())

    def _record(self, reads, writes, ev, clk):
        for ap in reads:
            b = self._box(ap)
            t = self.trk.setdefault(b[0], {'w': [], 'r': []})
            rl = t['r']
            for i, (rb, rev, rclk) in enumerate(rl):
                if rev[0] == ev[0] and self._inside(rb, b):
                    rl[i] = (b, ev, clk)
                    break
            else:
                rl.append((b, ev, clk))
                if len(rl) > 40:
                    t['r'] = self._merge(rl)
        for ap in writes:
            b = self._box(ap)
            t = self.trk.setdefault(b[0], {'w': [], 'r': []})
            t['w'] = [e for e in t['w'] if not self._inside(e[0], b)]
            t['r'] = [e for e in t['r'] if not self._inside(e[0], b)]
            t['w'].append((b, ev, clk))
            if len(t['w']) > 40:
                t['w'] = self._merge(t['w'])

    def _waits_for(self, eng, deps):
        kn = self.known[eng]
        waits = {}
        for (ev, clk, kind) in deps:
            key, val = ev
            if key == eng:
                if eng == 'pe':
                    continue
                if eng in ('act', 'dve') and kind != 'raw':
                    continue
            if kn.get(key, 0) >= val:
                continue
            if waits.get(key, 0) < val:
                waits[key] = val
        for (ev, clk, kind) in deps:
            key, val = ev
            if key in waits and waits[key] >= val:
                for k2, v2 in clk.items():
                    if kn.get(k2, 0) < v2:
                        kn[k2] = v2
        for k, v in waits.items():
            if kn.get(k, 0) < v:
                kn[k] = v
        return list(waits.items())

    def op(self, eng, fn, reads=(), writes=()):
        deps = self._deps(reads, writes)
        waits = self._waits_for(eng, deps)
        self.cnt[eng] += 1
        ev = (eng, self.cnt[eng])
        clk = dict(self.known[eng])
        clk[eng] = self.cnt[eng]
        self.stream[eng].append((waits, fn, ('e', eng)))
        self._record(reads, writes, ev, clk)
        self.nwaits += len(waits)
        return ev

    def dma(self, q, out, in_, final=False, **kw):
        i = self.drr
        self.drr = (self.drr + 1) % len(self.dsem)
        deps = self._deps([in_], [out])
        key = ('d', i)
        deps.append(((key, self.dcum[i]), {}, 'raw'))
        waits = self._waits_for(q, deps)
        self.dcum[i] += 16
        ev = (key, self.dcum[i])
        clk = dict(self.known[q])
        clk[key] = self.dcum[i]
        self.stream[q].append((waits, (lambda e, out=out, in_=in_, kw=kw: e.dma_start(out=out, in_=in_, **kw)), ('d', i)))
        self._record([in_], [out], ev, clk)
        if final:
            self.final_events.append(ev)
        self.nwaits += len(waits)
        self.ndma += 1
        return ev

    def collective(self, kind, ins, outs, groups, q='pool'):
        if not hasattr(self, 'csem'):
            self.csem = []
            self.ccum = []
        self.csem.append(self.stack.enter_context(self.nc.semaphore('csem%d' % len(self.csem))))
        self.ccum.append(0)
        i = len(self.csem) - 1
        deps = self._deps(list(ins), list(outs))
        key = ('c', i)
        waits = self._waits_for(q, deps)
        self.ccum[i] += 1
        ev = (key, self.ccum[i])
        clk = dict(self.known[q])
        clk[key] = self.ccum[i]
        self.stream[q].append((waits, (lambda e: e.collective_compute(kind, ALU.bypass, replica_groups=groups,
                                                                      ins=[a.opt() for a in ins], outs=[a.opt() for a in outs])),
                               ('c', i)))
        self._record(list(ins), list(outs), ev, clk)
        self.nwaits += len(waits)
        return ev

    def finish(self, eng='sp'):
        self.stream[eng].append((list(self.final_events), None, None))

    def _semobj(self, key):
        if isinstance(key, tuple):
            return self.dsem[key[1]] if key[0] == 'd' else self.csem[key[1]]
        return self.sem[key]

    def emit(self):
        for e in ENGS:
            assert self.cnt[e] < 60000, (e, self.cnt[e])

        def replay(name, eng):
            for (waits, fn, inc) in self.stream[name]:
                for (key, val) in waits:
                    eng.wait_ge(self._semobj(key), val)
                if fn is None:
                    continue
                ins = fn(eng)
                if inc[0] == 'e':
                    ins.then_inc(self.sem[inc[1]], 1)
                elif inc[0] == 'c':
                    ins.then_inc(self.csem[inc[1]])
                else:
                    ins.then_inc(self.dsem[inc[1]], 16)

        with self.nc.Block() as block:
            @block.sync
            def _(e):
                replay('sp', e)

            @block.scalar
            def _(e):
                replay('act', e)

            @block.vector
            def _(e):
                replay('dve', e)

            @block.gpsimd
            def _(e):
                replay('pool', e)

            @block.tensor
            def _(e):
                replay('pe', e)

    def mm(self, out, lhsT, rhs, start=True, stop=True):
        return self.op('pe', lambda e: e.matmul(out, lhsT, rhs, start=start, stop=stop),
                       reads=[lhsT, rhs], writes=[out])

    def transpose(self, out, in_, ident):
        return self.op('pe', lambda e: e.transpose(out, in_, ident), reads=[in_, ident], writes=[out])

    def act(self, out, in_, func, bias=None, scale=None, accum_out=None):
        reads = [in_]
        kw = {}
        if bias is not None:
            kw['bias'] = bias
            if not isinstance(bias, (int, float)):
                reads.append(bias)
        if scale is not None:
            kw['scale'] = scale
            if not isinstance(scale, (int, float)):
                reads.append(scale)
        writes = [out]
        if accum_out is not None:
            kw['accum_out'] = accum_out
            writes.append(accum_out)
        return self.op('act', lambda e: e.activation(out, in_, func, **kw), reads=reads, writes=writes)

    def tt(self, eng, out, in0, in1, op):
        return self.op(eng, lambda e: e.tensor_tensor(out, in0, in1, op), reads=[in0, in1], writes=[out])

    def ts(self, eng, out, in0, s1, s2, op0, op1=None):
        reads = [in0]
        for s in (s1, s2):
            if s is not None and not isinstance(s, (int, float)):
                reads.append(s)
        if op1 is None:
            return self.op(eng, lambda e: e.tensor_scalar(out, in0, s1, None, op0), reads=reads, writes=[out])
        return self.op(eng, lambda e: e.tensor_scalar(out, in0, s1, s2, op0, op1), reads=reads, writes=[out])

    def stt(self, out, in0, scalar, in1, op0, op1):
        reads = [in0, in1]
        if not isinstance(scalar, (int, float)):
            reads.append(scalar)
        return self.op('dve', lambda e: e.scalar_tensor_tensor(out, in0, scalar, in1, op0, op1),
                       reads=reads, writes=[out])

    def copy(self, eng, out, in_):
        if eng == 'act':
            return self.op('act', lambda e: e.copy(out, in_), reads=[in_], writes=[out])
        return self.op(eng, lambda e: e.tensor_copy(out, in_), reads=[in_], writes=[out])

    def memset(self, eng, ap, val):
        return self.op(eng, lambda e: e.memset(ap, val), reads=[], writes=[ap])

    def scan(self, out, d0, d1, init, op0=ALU.mult, op1=ALU.add):
        reads = [d0, d1]
        if not isinstance(init, (int, float)):
            reads.append(init)
        return self.op('dve', lambda e: e.tensor_tensor_scan(out, d0, d1, init, op0, op1), reads=reads, writes=[out])

    def recip(self, out, in_):
        return self.op('dve', lambda e: e.reciprocal(out, in_), reads=[in_], writes=[out])


class Arena:
    def __init__(self, P, name, t, nbytes):
        self.P, self.name, self.t, self.nbytes = P, name, t, nbytes
        self.top = 0
        self.regions = []
        self.old = []
        self.nrid = 0

    def reset(self):
        self.old.extend(self.regions)
        self.regions = []
        self.top = 0

    def mark(self):
        return (self.top, len(self.regions))

    def release(self, mk):
        top, nreg = mk
        self.old.extend(self.regions[nreg:])
        self.regions = self.regions[:nreg]
        self.top = top

    def region_of(self, lo, hi):
        for (a, b, rid) in self.regions:
            if lo >= a and hi <= b:
                return rid
        raise AssertionError("arena access outside any region %s %d %d" % (self.name, lo, hi))

    def alloc(self, shape, dtype=F32):
        n = int(np.prod(shape)) * ESZ[dtype]
        n = (n + 63) // 64 * 64
        lo, hi = self.top, self.top + n
        assert hi <= self.nbytes, ("arena overflow", self.name, hi, self.nbytes)
        self.top = hi
        rid = self.nrid
        self.nrid += 1
        self.regions.append((lo, hi, rid))
        inh = []
        keep = []
        for (a, b, orid) in self.old:
            if a < hi and lo < b:
                t = self.P.trk.get((self.name, orid))
                if t:
                    inh.extend(t['w'])
                    inh.extend(t['r'])
                keep.append((a, b, orid))
            else:
                keep.append((a, b, orid))
        self.old = keep
        if inh:
            full = ((self.name, rid), 0, 128, lo, hi)
            m = Prog._merge([(full, ev, clk) for (_, ev, clk) in inh])
            self.P.trk[(self.name, rid)] = {'w': m, 'r': []}
        v = self.t[:, lo // 4:hi // 4]
        if dtype != F32:
            v = v.bitcast(dtype)
        nel = int(np.prod(shape))
        v = v[:, 0:nel]
        if len(shape) == 2:
            v = v.rearrange("p (a b) -> p a b", a=shape[0])
        elif len(shape) == 3:
            v = v.rearrange("p (a b c) -> p a b c", a=shape[0], b=shape[1])
        return v


def _launch(build, in_maps, n_cores=8, trace=False):
    nc = bass.Bass("TRN2", target_bir_lowering=False)
    with ExitStack() as stack:
        P = Prog(nc, stack)
        build(nc, P)
        P.finish()
        P.emit()
    res = run_bass_kernel_spmd(nc, in_maps, core_ids=list(range(n_cores)), trace=trace)
    if trace:
        return res
    return res.results

D = 1024
SEQ = 8192
CTXL = 256
NTOK = SEQ + CTXL
QT = 2112
EPS = 1e-6
TILES_Q = [(0, 64, 1), (64, 512, 0), (576, 512, 0), (1088, 512, 0), (1600, 512, 0)]


class Banks:
    def __init__(self, P):
        self.pp = [P.psum("pp%d" % i, [128, 1024], F32) for i in range(4)]
        self.b = [self.pp[i // 2][:, (i % 2) * 512:(i % 2 + 1) * 512] for i in range(8)]


def load_consts_tok(nc, P):
    ones = P.sbuf("ones128", [128, 128], F32)
    P.memset('pool', ones[:], 1.0)
    return {'ones': ones}


def compute_mod(nc, P, K, banks, cvec_d, wmod_d, bmod_d, fc_list, modT):
    cT = P.sbuf("cT", [128, 8, 2], F32)
    P.dma('sp', cT[:], cvec_d)
    e = P.sbuf("cTe", [128, 8, 2], F32)
    P.act(e[:], cT[:], AF.Exp, scale=-1.0)
    P.ts('dve', e[:], e[:], 1.0, None, ALU.add)
    P.recip(e[:], e[:])
    sc = P.sbuf("cTs", [128, 8, 2], F32)
    P.tt('dve', sc[:], cT[:], e[:], ALU.mult)
    bT = P.sbuf("bmodT", [128, 24], F32)
    P.dma('sp', bT[:], bmod_d)
    wbuf = [P.sbuf("wmodbuf%d" % i, [128, 8, 512], F32) for i in range(2)]
    ps = banks.b[7]
    groups = sorted(set(fc // 4 for fc in fc_list))
    for gi, g in enumerate(groups):
        wb = wbuf[gi % 2]
        for kc in range(8):
            P.dma('sp' if kc % 2 == 0 else 'act', wb[:, kc, :], wmod_d[kc * 128:(kc + 1) * 128, g * 512:(g + 1) * 512])
        for fc in range(g * 4, g * 4 + 4):
            if fc not in fc_list:
                continue
            for kc in range(8):
                P.mm(ps[:, fc * 2:fc * 2 + 2], wb[:, kc, (fc % 4) * 128:(fc % 4 + 1) * 128], sc[:, kc, :],
                     start=(kc == 0), stop=(kc == 7))
    for fc in fc_list:
        P.tt('dve', modT[:, fc, :], ps[:, fc * 2:fc * 2 + 2], bT[:, fc:fc + 1].to_broadcast([128, 2]), ALU.add)


def norm_coeffs(nc, P, modT, normw_d, name):
    nw = P.sbuf(name + "_nw", [128, 8], F32)
    P.dma('sp', nw[:], normw_d)
    A = P.sbuf(name + "_A", [128, 8, 2], F32)
    P.ts('dve', A[:], modT[:, 8:16, :], 1.0, None, ALU.add)
    P.tt('dve', A[:], A[:], nw[:].unsqueeze(2).to_broadcast([128, 8, 2]), ALU.mult)
    return A


def phase_norm(nc, P, K, banks, RT, A, Bsh, xn_d, tiles, tmp):
    xn_v = xn_d.rearrange("(kc p) t -> p kc t", p=128)
    for ti, (c0, n, isctx) in enumerate(tiles):
        j = 1 if isctx else 0
        ps = banks.b[ti % 2]
        for kc in range(8):
            sq = tmp['sq'][kc % 2]
            P.act(sq[:, 0:n], RT[:, kc, c0:c0 + n], AF.Square)
            P.mm(ps[:, 0:n], K['ones'][:], sq[:, 0:n], start=(kc == 0), stop=(kc == 7))
        rstd = tmp['rstd'][ti % 2]
        P.act(rstd[:, 0:n], ps[:, 0:n], AF.Ln, scale=1.0 / D, bias=K['eps'][:])
        P.act(rstd[:, 0:n], rstd[:, 0:n], AF.Exp, scale=-0.5)
        xb = tmp['xb'][ti % 2]
        for kc in range(8):
            t1 = tmp['t1'][kc % 2]
            P.stt(t1[:, 0:n], RT[:, kc, c0:c0 + n], A[:, kc, j:j + 1], rstd[:, 0:n], ALU.mult, ALU.mult)
            P.act(xb[:, kc, 0:n], t1[:, 0:n], AF.Identity, bias=Bsh[:, kc, j:j + 1])
        P.dma('pool', xn_v[:, :, c0:c0 + n], xb[:, :, 0:n], final=True)


def phase_final(nc, P, K, banks, RT, fnw_d, out_d, tiles, tmp):
    fw = P.sbuf("fnw_sb", [128, 8], F32)
    P.dma('sp', fw[:], fnw_d)
    out_v = out_d.rearrange("(kc p) t -> p kc t", p=128)
    for ti, (c0, n, isctx) in enumerate(tiles):
        ps = banks.b[ti % 2]
        for kc in range(8):
            sq = tmp['sq'][kc % 2]
            P.act(sq[:, 0:n], RT[:, kc, c0:c0 + n], AF.Square)
            P.mm(ps[:, 0:n], K['ones'][:], sq[:, 0:n], start=(kc == 0), stop=(kc == 7))
        rstd = tmp['rstd'][ti % 2]
        P.act(rstd[:, 0:n], ps[:, 0:n], AF.Ln, scale=1.0 / D, bias=K['eps'][:])
        P.act(rstd[:, 0:n], rstd[:, 0:n], AF.Exp, scale=-0.5)
        for kc in range(8):
            P.stt(RT[:, kc, c0:c0 + n], RT[:, kc, c0:c0 + n], fw[:, kc:kc + 1], rstd[:, 0:n], ALU.mult, ALU.mult)
        P.dma('pool', out_v[:, :, c0 - 64:c0 - 64 + n], RT[:, :, c0:c0 + n], final=True)


def phase_outproj(nc, P, K, banks, RT, yT_d, wout_d, ssdw_d, gateT, tiles, tmp):
    wo = P.sbuf("wout_bf", [128, 8, D], BF16)
    for kc in range(8):
        st = tmp['wstage'][kc % 2]
        P.dma('sp' if kc % 2 == 0 else 'act', st[:], wout_d[kc * 128:(kc + 1) * 128, :])
        P.copy('pool', wo[:, kc, :], st[:])
    sw = P.sbuf("ssdnw_sb", [128, 2], F32)
    P.dma('sp', sw[:], ssdw_d)
    yT_v = yT_d.rearrange("(kc p) t -> p kc t", p=128)
    for ti, (c0, n, isctx) in enumerate(tiles):
        j = 1 if isctx else 0
        yb = tmp['yb'][ti % 2]
        P.dma('sp', yb[:, :, 0:n], yT_v[:, :, c0:c0 + n])
        ps = banks.b[2 + ti % 2]
        for i, kc in enumerate((6, 7)):
            sq = tmp['sq'][i]
            P.act(sq[:, 0:n], yb[:, kc, 0:n], AF.Square)
            P.mm(ps[:, 0:n], K['ones'][:], sq[:, 0:n], start=(i == 0), stop=(i == 1))
        rstd = tmp['rstd'][ti % 2]
        P.act(rstd[:, 0:n], ps[:, 0:n], AF.Ln, scale=1.0 / 256, bias=K['eps'][:])
        P.act(rstd[:, 0:n], rstd[:, 0:n], AF.Exp, scale=-0.5)
        for i, kc in enumerate((6, 7)):
            P.stt(yb[:, kc, 0:n], yb[:, kc, 0:n], sw[:, i:i + 1], rstd[:, 0:n], ALU.mult, ALU.mult)
        for fc in range(8):
            po = banks.b[4 + fc % 4]
            for kc in range(8):
                P.mm(po[:, 0:n], wo[:, kc, fc * 128:(fc + 1) * 128], yb[:, kc, 0:n], start=(kc == 0), stop=(kc == 7))
            P.stt(RT[:, fc, c0:c0 + n], po[:, 0:n], gateT[:, 16 + fc, j:j + 1], RT[:, fc, c0:c0 + n], ALU.mult, ALU.add)


def alloc_tok_tmp(P):
    return {
        'sq': [P.sbuf("t_sq%d" % i, [128, 512], F32) for i in range(2)],
        'rstd': [P.sbuf("t_rstd%d" % i, [128, 512], F32) for i in range(2)],
        't1': [P.sbuf("t_t1%d" % i, [128, 512], F32) for i in range(2)],
        'xb': [P.sbuf("t_xb%d" % i, [128, 8, 512], BF16) for i in range(2)],
    }


def build_tok_kernel(nc, P, first, last, with_outproj):
    banks = Banks(P)
    K = load_consts_tok(nc, P)
    epst = P.sbuf("eps_t", [128, 1], F32)
    P.memset('pool', epst[:], EPS)
    K['eps'] = epst
    tmp = alloc_tok_tmp(P)
    RT_d = nc.dram_tensor("RT", [D, QT], F32, kind="ExternalInput").ap()
    cvec_d = nc.dram_tensor("cvec", [128, 8, 2], F32, kind="ExternalInput").ap()
    RT = P.sbuf("RT_sb", [128, 8, QT], F32)
    RT_v = RT_d.rearrange("(kc p) t -> p kc t", p=128)
    for kc in range(8):
        P.dma('sp' if kc % 2 == 0 else 'act', RT[:, kc, :], RT_v[:, kc, :])
    modT = P.sbuf("modT", [128, 24, 2], F32)
    if with_outproj:
        wmodg_d = nc.dram_tensor("wmod_g", [D, 3 * D], F32, kind="ExternalInput").ap()
        bmodg_d = nc.dram_tensor("bmod_g", [128, 24], F32, kind="ExternalInput").ap()
        yT_d = nc.dram_tensor("yT", [D, QT], BF16, kind="ExternalInput").ap()
        wout_d = nc.dram_tensor("wout", [D, D], F32, kind="ExternalInput").ap()
        ssdw_d = nc.dram_tensor("ssdnw", [128, 2], F32, kind="ExternalInput").ap()
        tmp['wstage'] = [P.sbuf("t_wst%d" % i, [128, D], F32) for i in range(2)]
        tmp['yb'] = [P.sbuf("t_yb%d" % i, [128, 8, 512], BF16) for i in range(2)]
        gateT = P.sbuf("gateT", [128, 24, 2], F32)
        compute_mod(nc, P, K, banks, cvec_d, wmodg_d, bmodg_d, list(range(16, 24)), gateT)
        tiles = TILES_Q[1:] if last else TILES_Q
        phase_outproj(nc, P, K, banks, RT, yT_d, wout_d, ssdw_d, gateT, tiles, tmp)
    if last:
        fnw_d = nc.dram_tensor("fnw", [128, 8], F32, kind="ExternalInput").ap()
        out_d = nc.dram_tensor("outT", [D, 2048], F32, kind="ExternalOutput").ap()
        phase_final(nc, P, K, banks, RT, fnw_d, out_d, TILES_Q[1:], tmp)
    else:
        wmod_d = nc.dram_tensor("wmod_n", [D, 3 * D], F32, kind="ExternalInput").ap()
        bmod_d = nc.dram_tensor("bmod_n", [128, 24], F32, kind="ExternalInput").ap()
        normw_d = nc.dram_tensor("normw", [128, 8], F32, kind="ExternalInput").ap()
        xn_d = nc.dram_tensor("xnT", [D, QT], BF16, kind="ExternalOutput").ap()
        if with_outproj:
            Rout_d = nc.dram_tensor("RTout", [D, QT], F32, kind="ExternalOutput").ap()
        modT2 = modT
        _cm_second(nc, P, K, banks, cvec_d, wmod_d, bmod_d, list(range(0, 16)), modT2, with_outproj)
        A = norm_coeffs(nc, P, modT2, normw_d, "nc")
        phase_norm(nc, P, K, banks, RT, A, modT2, xn_d, TILES_Q, tmp)
        if with_outproj:
            Ro_v = Rout_d.rearrange("(kc p) t -> p kc t", p=128)
            for kc in range(8):
                P.dma('pool', Ro_v[:, kc, :], RT[:, kc, :], final=True)


_CM_STATE = {}


def _cm_second(nc, P, K, banks, cvec_d, wmod_d, bmod_d, fc_list, modT, second):
    if not second:
        return compute_mod(nc, P, K, banks, cvec_d, wmod_d, bmod_d, fc_list, modT)
    orig = P.sbuf

    def renamed(name, shape, dtype=F32):
        return orig(name + "_2", shape, dtype)
    P.sbuf = renamed
    try:
        compute_mod(nc, P, K, banks, cvec_d, wmod_d, bmod_d, fc_list, modT)
    finally:
        P.sbuf = orig


def pk(v):
    v = np.asarray(v)
    return np.ascontiguousarray(v.reshape(-1, 128).T)

TOK_TILES = [(0, 256)] + [(256 + 512 * i, 512) for i in range(16)]
SCALE_QK = 32.0 ** -0.5
QK_REP = 1
DEFER = True


def store_y(P, io, mi, t0, n, src):
    if 'y_store' in io:
        io['y_store'](mi, t0, n, src)
    else:
        P.dma('pool', io['yT'][mi, :, t0:t0 + n], src, final=True)


def load_weights_bf16(nc, P, A, w_d, ncols, stage):
    Wsb = A.alloc([8, ncols], BF16)
    for kc in range(8):
        st = stage[kc % 2]
        P.dma('sp' if kc % 2 == 0 else 'act', st[:, 0:ncols], w_d[kc * 128:(kc + 1) * 128, :])
        P.copy('dve', Wsb[:, kc, :], st[:, 0:ncols])
    return Wsb


def inproj(nc, P, banks, xn_v, Wsb, tiles, fm_groups, tm_group, xbufs, bank_ids=(0, 1, 2, 3), pre_tile=None):
    pending = []
    for ti, (t0, n) in enumerate(tiles):
        xb = xbufs[ti % 2]
        if callable(xn_v):
            xn_v(xb, t0, n)
        else:
            P.dma('sp', xb[:, :, 0:n], xn_v[:, :, t0:t0 + n])
        if pre_tile is not None:
            pre_tile(ti, t0, n)
        new_pending = []
        for gi, (c0, M, lat_only, fn) in enumerate(fm_groups):
            if lat_only and t0 < CTXL:
                continue
            ps = banks.b[bank_ids[gi % len(bank_ids)]]
            for kc in range(8):
                P.mm(ps[0:M, 0:n], Wsb[:, kc, c0:c0 + M], xb[:, kc, 0:n], start=(kc == 0), stop=(kc == 7))
            if fn is not None:
                r = fn(ps, t0, n, ti)
                if r is not None:
                    if DEFER:
                        new_pending.append(r)
                    else:
                        r()
        if tm_group is not None:
            c0, ncols, fn, tmbanks = tm_group
            for sub in range(n // 128):
                ps = banks.b[tmbanks[sub % len(tmbanks)]]
                for kc in range(8):
                    P.mm(ps[:, 0:ncols], xb[:, kc, sub * 128:(sub + 1) * 128], Wsb[:, kc, c0:c0 + ncols],
                         start=(kc == 0), stop=(kc == 7))
                fn(ps, t0 + sub * 128)
        for r in pending:
            r2 = r()
            if r2 is not None:
                new_pending.append(r2)
        pending = new_pending
    while pending:
        nxt = []
        for r in pending:
            r2 = r()
            if r2 is not None:
                nxt.append(r2)
        pending = nxt


def silu_evac(P, out_bf, ps_ap, tmp_e, M, n):
    P.act(tmp_e[0:M, 0:n], ps_ap, AF.Tanh, scale=0.5)
    P.stt(out_bf, tmp_e[0:M, 0:n], 1.0, ps_ap, ALU.add, ALU.mult)


def attention_phase(nc, P, A, banks, K, xn_v, io, layer, with_ctx):
    A.reset()
    lam_init = 0.8 - 0.6 * float(np.exp(-0.3 * layer))
    stage = [A.alloc([640], F32) for _ in range(2)]
    Wsb = load_weights_bf16(nc, P, A, io['w_attn'], 640, stage)
    xbufs = [A.alloc([8, 512], BF16) for _ in range(2)]
    Qa = A.alloc([NTOK], BF16)
    Ka = A.alloc([NTOK], BF16)
    Vt = A.alloc([66, 128], BF16)
    sg = A.alloc([NTOK], BF16)
    cosb = [A.alloc([512], F32) for _ in range(2)]
    sinb = [A.alloc([512], F32) for _ in range(2)]
    t1 = [A.alloc([512], F32) for _ in range(2)]
    t2 = [A.alloc([512], F32) for _ in range(2)]
    te = [A.alloc([512], F32) for _ in range(2)]
    P.memset('pool', Vt[:, :, 65:128], 0.0)
    P.memset('pool', Vt[:, :, 64:65], 1.0)

    def pre_tile(ti, t0, n):
        if t0 >= CTXL:
            P.dma('act', cosb[ti % 2][:], io['cosT'][:, t0 - CTXL:t0 - CTXL + n])
            P.dma('act', sinb[ti % 2][:], io['sinT'][:, t0 - CTXL:t0 - CTXL + n])

    state = {}

    def ev_plain(dst):
        def f(ps, t0, n, ti):
            if t0 < CTXL:
                P.copy('act', dst[:, t0:t0 + n], ps[:, 0:n])
            else:
                state['ps1'] = ps
        return f

    def ev_rot(dst):
        def f(ps, t0, n, ti):
            a = t1[ti % 2]
            b = t2[ti % 2]
            P.tt('dve', a[:, 0:n], state['ps1'][:, 0:n], cosb[ti % 2][:, 0:n], ALU.mult)
            P.tt('dve', b[:, 0:n], ps[:, 0:n], sinb[ti % 2][:, 0:n], ALU.mult)
            P.tt('pool', dst[:, t0:t0 + n], a[:, 0:n], b[:, 0:n], ALU.add)
        return f

    vT = [A.alloc([512], BF16) for _ in range(2)]

    def ev_gv(ps, t0, n, ti):
        silu_evac(P, sg[0:64, t0:t0 + n], ps[0:64, 0:n], te[ti % 2], 64, n)
        v_ = vT[ti % 2]
        P.copy('act', v_[64:128, 0:n], ps[64:128, 0:n])

        def rest():
            for sub in range(n // 128):
                pt = banks.b[6 + sub % 2].bitcast(BF16)
                P.transpose(pt[:, 0:64], v_[64:128, sub * 128:(sub + 1) * 128], K['identb'][64:128, 64:128])
                P.copy('dve', Vt[:, t0 // 128 + sub, 0:64], pt[:, 0:64])
        return rest

    fm = [(0, 128, False, ev_plain(Qa)), (128, 128, True, ev_rot(Qa)),
          (256, 128, False, ev_plain(Ka)), (384, 128, True, ev_rot(Ka)),
          (512, 128, False, ev_gv)]
    inproj(nc, P, banks, xn_v, Wsb, TOK_TILES, fm, None, xbufs, bank_ids=(0, 1, 2, 3, 4), pre_tile=pre_tile)

    lv = A.alloc([128], F32)
    P.dma('sp', lv[0:64, :], io['diff_lam'].partition_broadcast(64))
    pr = A.alloc([64], F32)
    s2 = A.alloc([2], F32)
    P.tt('dve', pr[0:64, 0:32], lv[0:64, 0:32], lv[0:64, 32:64], ALU.mult)
    P.tt('dve', pr[0:64, 32:64], lv[0:64, 64:96], lv[0:64, 96:128], ALU.mult)
    P.op('dve', lambda e: e.tensor_reduce(s2[0:64, 0:2], pr[0:64, :].rearrange("p (a b) -> p a b", a=2), AX.X, ALU.add),
         reads=[pr[0:64, :]], writes=[s2[0:64, 0:2]])
    P.act(s2[0:64, :], s2[0:64, :], AF.Exp)
    nlam = A.alloc([1], F32)
    P.tt('dve', nlam[0:64, :], s2[0:64, 1:2], s2[0:64, 0:1], ALU.subtract)
    P.ts('dve', nlam[0:64, :], nlam[0:64, :], -lam_init, None, ALU.add)
    subw = A.alloc([1], F32)
    P.dma('sp', subw[0:64, :], io['diff_subln_w'])
    P.ts('dve', subw[0:64, :], subw[0:64, :], 1.0 - lam_init, None, ALU.mult)

    Eb = [A.alloc([1024], BF16) for _ in range(4)]
    osb = [A.alloc([512], F32) for _ in range(2)]
    fz = [A.alloc([512], F32) for _ in range(4)]
    yb = [A.alloc([512], BF16) for _ in range(2)]
    qblocks = []
    if with_ctx:
        qblocks.append((0, 256, [0, 1]))
    for i in range(16):
        qblocks.append((256 + 512 * i, 512, list(range(66))))
    o_acc = [banks.b[6], banks.b[7]]
    for qi, (q0, nq, kbs) in enumerate(qblocks):
        def qk(kb):
            S = banks.pp[kb % 3]
            for rep in range(QK_REP):
                P.mm(S[:, 0:nq], Ka[0:32, kb * 128:(kb + 1) * 128], Qa[0:32, q0:q0 + nq])
                P.mm(S[:, 512:512 + nq], Ka[64:96, kb * 128:(kb + 1) * 128], Qa[64:96, q0:q0 + nq])
        for k_ in kbs[0:3]:
            qk(k_)
        for ki, kb in enumerate(kbs):
            S = banks.pp[kb % 3]
            E = Eb[ki % 4]
            if nq == 512:
                P.act(E[:, :], S[:, :], AF.Exp, scale=SCALE_QK)
            else:
                Ev = E.rearrange("p (a b) -> p a b", a=2)[:, :, 0:nq]
                Sv = S.rearrange("p (a b) -> p a b", a=2)[:, :, 0:nq]
                P.act(Ev, Sv, AF.Exp, scale=SCALE_QK)
            if ki + 3 < len(kbs):
                qk(kbs[ki + 3])
            for c in range(2):
                P.mm(o_acc[c][:, 0:nq], Vt[:, kb, :], E[:, c * 512:c * 512 + nq], start=(ki == 0), stop=(ki == len(kbs) - 1))
        for c in range(2):
            P.copy('dve', osb[c][0:65, 0:nq], o_acc[c][0:65, 0:nq])
        zb = [banks.b[0], banks.b[1]]
        for c in range(2):
            P.mm(zb[c][0:64, 0:nq], K['selZ'][0:65, :], osb[c][0:65, 0:nq])
        for c in range(2):
            P.act(fz[c][0:64, 0:nq], zb[c][0:64, 0:nq], AF.Ln)
            P.act(fz[c][0:64, 0:nq], fz[c][0:64, 0:nq], AF.Exp, scale=-1.0)
            P.tt('dve', fz[c][0:64, 0:nq], fz[c][0:64, 0:nq], osb[c][0:64, 0:nq], ALU.mult)
        o = fz[2]
        P.stt(o[0:64, 0:nq], fz[1][0:64, 0:nq], nlam[0:64, 0:1], fz[0][0:64, 0:nq], ALU.mult, ALU.add)
        P.tt('dve', fz[3][0:64, 0:nq], o[0:64, 0:nq], o[0:64, 0:nq], ALU.mult)
        P.mm(zb[0][0:64, 0:nq], K['ones'][0:64, 0:64], fz[3][0:64, 0:nq])
        P.act(fz[3][0:64, 0:nq], zb[0][0:64, 0:nq], AF.Ln, scale=1.0 / 64, bias=K['eps'][0:64, :])
        P.act(fz[3][0:64, 0:nq], fz[3][0:64, 0:nq], AF.Exp, scale=-0.5)
        P.stt(o[0:64, 0:nq], o[0:64, 0:nq], subw[0:64, 0:1], fz[3][0:64, 0:nq], ALU.mult, ALU.mult)
        y = yb[qi % 2]
        P.stt(y[0:64, 0:nq], o[0:64, 0:nq], 0.5, sg[0:64, q0:q0 + nq], ALU.mult, ALU.mult)
        store_y(P, io, 2, q0, nq, y[0:64, 0:nq])
        if 'after_q' in io:
            io['after_q'](q0, nq)


TP = 8456
PADC = 2
PADL = 260


def pcol(t0):
    return t0 + PADC if t0 < CTXL else t0 + (PADL - CTXL)


PTILES = [(PADC, 256)] + [(PADL + 512 * i, 512) for i in range(16)]


def conv4(P, dst, src, cw, cb, np_, c_lo, c_hi):
    n = c_hi - c_lo
    P.act(dst[0:np_, c_lo:c_hi], src[0:np_, c_lo - 2:c_hi - 2], AF.Identity, scale=cw[0:np_, 0:1], bias=cb[0:np_, 0:1])
    for k in range(1, 4):
        P.stt(dst[0:np_, c_lo:c_hi], src[0:np_, c_lo + k - 2:c_hi + k - 2], cw[0:np_, k:k + 1], dst[0:np_, c_lo:c_hi],
              ALU.mult, ALU.add)


def lru_phase(nc, P, A, banks, K, xn_v, io, layer, with_ctx):
    A.reset()
    stage = [A.alloc([192], F32) for _ in range(2)]
    Wsb = load_weights_bf16(nc, P, A, io['w_lru'], 192, stage)
    xbufs = [A.alloc([8, 512], BF16) for _ in range(2)]
    XA = A.alloc([TP], F32)
    XU = A.alloc([TP], F32)
    sg = A.alloc([NTOK], BF16)
    te = [A.alloc([512], F32) for _ in range(2)]
    par = A.alloc([16], F32)
    P.dma('sp', par[:, 0:9], io['lru_par'])
    cw, cb = par[:, 0:4], par[:, 4:5]
    nba, nbx = par[:, 9:10], par[:, 10:11]
    c8, c16 = par[:, 11:12], par[:, 12:13]
    P.ts('dve', nba, par[:, 5:6], 0.5, None, ALU.mult)
    P.ts('dve', nbx, par[:, 6:7], 0.5, None, ALU.mult)
    P.act(c8, par[:, 7:8], AF.Exp, scale=-1.0)
    P.act(c8, c8, AF.Ln, bias=K['one'][:, 0:1])
    P.ts('dve', c16, c8, -16.0, None, ALU.mult)
    P.ts('dve', c8, c8, -8.0, None, ALU.mult)
    gw32 = A.alloc([256], F32)
    P.dma('sp', gw32[0:64, :], io['lru_gw'])
    gw = A.alloc([256], BF16)
    P.copy('pool', gw[0:64, :], gw32[0:64, :])
    II = A.alloc([64], F32)
    P.copy('pool', II[0:64, :], K['ident'][0:64, 0:64])
    P.copy('pool', II[64:128, :], K['ident'][64:128, 64:128])
    P.memset('pool', XA[:, 0:PADC], 0.0)
    P.memset('pool', XA[:, PADC + CTXL:PADL], 0.0)
    P.memset('pool', XA[:, PADL + SEQ:TP], 0.0)

    def ev_x(ps, t0, n, ti):
        c = pcol(t0)
        P.copy('act', XA[:, c:c + n], ps[:, 0:n])

    def ev_gate(ps, t0, n, ti):
        silu_evac(P, sg[0:64, t0:t0 + n], ps[0:64, 0:n], te[ti % 2], 64, n)

    inproj(nc, P, banks, xn_v, Wsb, TOK_TILES, [(0, 128, False, ev_x), (128, 64, False, ev_gate)], None, xbufs,
           bank_ids=(0, 1, 2, 3))
    conv4(P, XU, XA, cw, cb, 128, PADC, PADC + CTXL)
    for i in range(4):
        conv4(P, XU, XA, cw, cb, 128, PADL + 2048 * i, PADL + 2048 * (i + 1))
    xcb = [A.alloc([512], BF16) for _ in range(2)]
    gr = [A.alloc([512], F32) for _ in range(2)]
    gi_ = [A.alloc([512], F32) for _ in range(2)]
    gs = [A.alloc([512], F32) for _ in range(2)]
    def gateA(ti, c0, n):
        xb = xcb[ti % 2]
        P.copy('pool', xb[0:64, 0:n], XU[0:64, c0:c0 + n])
        psr, psi = banks.b[(2 * ti) % 8], banks.b[(2 * ti + 1) % 8]
        P.mm(psr[:, 0:n], gw[0:64, 0:128], xb[0:64, 0:n])
        P.mm(psi[:, 0:n], gw[0:64, 128:256], xb[0:64, 0:n])
        r, ii = gr[ti % 2], gi_[ti % 2]
        P.act(r[:, 0:n], psr[:, 0:n], AF.Tanh, scale=0.5, bias=nba)
        P.act(ii[:, 0:n], psi[:, 0:n], AF.Tanh, scale=0.5, bias=nbx)

    def gateB(ti, c0, n):
        r, ii = gr[ti % 2], gi_[ti % 2]
        P.ts('dve', r[:, 0:n], r[:, 0:n], 0.5, 0.5, ALU.mult, ALU.add)
        P.ts('dve', ii[:, 0:n], ii[:, 0:n], 0.5, 0.5, ALU.mult, ALU.add)
        P.act(XA[:, c0:c0 + n], r[:, 0:n], AF.Exp, scale=c8)
        P.tt('dve', XU[:, c0:c0 + n], XU[:, c0:c0 + n], ii[:, 0:n], ALU.mult)

    npt = len(PTILES)
    for it in range(npt + 1):
        if it < npt:
            gateA(it, *PTILES[it])
        if it >= 1:
            gateB(it - 1, *PTILES[it - 1])
    for ti, (c0, n) in enumerate(PTILES):
        s = gs[ti % 2]
        P.tt('pool', s[:, 0:n], XA[:, c0:c0 + n], XA[:, c0:c0 + n], ALU.mult)
        P.act(s[:, 0:n], s[:, 0:n], AF.Ln, scale=-1.0, bias=K['one'][:, 0:1])
        P.act(s[:, 0:n], s[:, 0:n], AF.Exp, scale=0.5)
        P.tt('dve', XU[:, c0:c0 + n], XU[:, c0:c0 + n], s[:, 0:n], ALU.mult)
    fa, fu = XA[0:64], XU[0:64]
    ba_, bu = XA[64:128], XU[64:128]
    P.scan(fu[:, PADC:PADC + CTXL], fa[:, PADC:PADC + CTXL], fu[:, PADC:PADC + CTXL], 0.0)
    for i in range(4):
        lo = PADL + 2048 * i
        init = fu[:, PADC + CTXL - 1:PADC + CTXL] if i == 0 else fu[:, lo - 1:lo]
        P.scan(fu[:, lo:lo + 2048], fa[:, lo:lo + 2048], fu[:, lo:lo + 2048], init)
    P.scan(bu[:, PADC:PADC + CTXL][:, ::-1], ba_[:, PADC:PADC + CTXL][:, ::-1], bu[:, PADC:PADC + CTXL][:, ::-1], 0.0)
    for i in range(3, -1, -1):
        lo = PADL + 2048 * i
        init = bu[:, PADC:PADC + 1] if i == 3 else bu[:, lo + 2048:lo + 2049]
        P.scan(bu[:, lo:lo + 2048][:, ::-1], ba_[:, lo:lo + 2048][:, ::-1], bu[:, lo:lo + 2048][:, ::-1], init)
    yb = [A.alloc([512], BF16) for _ in range(2)]
    for ti, (t0, n) in enumerate(TOK_TILES):
        if t0 < CTXL and not with_ctx:
            continue
        c0 = pcol(t0)
        ps = banks.b[ti % 4]
        P.mm(ps[0:64, 0:n], II[:, :], XU[:, c0:c0 + n])
        y = yb[ti % 2]
        P.stt(y[0:64, 0:n], ps[0:64, 0:n], 0.5, sg[0:64, t0:t0 + n], ALU.mult, ALU.mult)
        store_y(P, io, 1, t0, n, y[0:64, 0:n])


def gla_phase(nc, P, A, banks, K, xn_v, io, layer, with_ctx):
    A.reset()
    NC_ = 132
    QG = A.alloc([NTOK], BF16)
    KG = A.alloc([NTOK], BF16)
    KDt = A.alloc([66, 64], BF16)
    Vt = A.alloc([66, 64], BF16)
    sg = A.alloc([NTOK], BF16)
    ST = A.alloc([NC_ + 2, 64], F32)
    STb = A.alloc([NC_ + 2, 64], BF16)
    DEC = A.alloc([NC_], F32)
    par = A.alloc([8], F32)
    P.dma('sp', par[0:64, 0:2], io['gla_par'])
    nb2 = par[0:64, 2:3]
    P.ts('dve', nb2, par[0:64, 0:1], -1.0, None, ALU.mult)
    lnsc = par[0:64, 3:4]
    P.memset('dve', lnsc, float(np.log(32.0 ** -0.5)))
    w2_32 = A.alloc([64], F32)
    P.dma('sp', w2_32[64:96, :], io['gla_w2bd'])
    w2b = A.alloc([64], BF16)
    P.copy('dve', w2b[64:96, :], w2_32[64:96, :])
    mk = A.mark()
    stage = [A.alloc([320], F32) for _ in range(2)]
    Wsb = load_weights_bf16(nc, P, A, io['w_gla'], 320, stage)
    xbufs = [A.alloc([8, 512], BF16) for _ in range(2)]
    te = [A.alloc([512], F32) for _ in range(2)]
    lrb = [A.alloc([512], BF16) for _ in range(2)]
    Lb = [A.alloc([512], F32) for _ in range(2)]
    Gb = [A.alloc([512], F32) for _ in range(2)]
    Eq = [A.alloc([512], F32) for _ in range(2)]
    Ek = [A.alloc([512], F32) for _ in range(2)]
    dG = [A.alloc([512], F32) for _ in range(2)]
    kdT = [A.alloc([512], BF16) for _ in range(2)]
    P.memset('dve', ST[0:64, 0:2, :], 0.0)
    P.memset('dve', ST[0:64, NC_:NC_ + 2, :], 0.0)
    st = {}

    qs = [A.alloc([512], F32) for _ in range(2)]
    ks = [A.alloc([512], F32) for _ in range(2)]

    def ev_q(ps, t0, n, ti):
        P.copy('act', qs[ti % 2][0:64, 0:n], ps[0:64, 0:n])

    vT = [A.alloc([512], BF16) for _ in range(2)]

    def ev_gv(ps, t0, n, ti):
        P.copy('dve', sg[0:64, t0:t0 + n], ps[0:64, 0:n])
        v_ = vT[ti % 2]
        P.copy('act', v_[64:128, 0:n], ps[64:128, 0:n])
        for sub in range(n // 128):
            pt = banks.b[5 + sub % 2].bitcast(BF16)
            P.transpose(pt[:, 64:128], v_[64:128, sub * 128:(sub + 1) * 128], K['identb'][64:128, 64:128])
            P.copy('dve', Vt[:, t0 // 128 + sub, :], pt[:, 64:128])

    def ev_lr(ps, t0, n, ti):
        i2 = ti % 2
        nch = n // 64
        c0 = t0 // 64
        P.copy('act', ks[ti % 2][0:64, 0:n], ps[0:64, 0:n])
        P.copy('act', lrb[i2][64:96, 0:n], ps[64:96, 0:n])

        def rest():
            pz = banks.b[7]
            P.mm(pz[0:64, 0:n], w2b[64:96, 0:64], lrb[i2][64:96, 0:n])
            L, G = Lb[i2], Gb[i2]
            P.act(L[0:64, 0:n], pz[0:64, 0:n], AF.Exp, scale=-1.0, bias=nb2)
            P.act(L[0:64, 0:n], L[0:64, 0:n], AF.Ln, bias=K['one'][0:64, 0:1])
            P.scan(G[0:32, 0:n], K['scanmask'][0:32, 0:n], L[0:32, 0:n], 0.0)
            P.scan(G[32:64, 0:n][:, ::-1], K['scanmask'][32:64, 0:n][:, ::-1], L[32:64, 0:n][:, ::-1], 0.0)
            P.act(Eq[i2][0:64, 0:n], G[0:64, 0:n], AF.Exp, scale=-1.0 / 16, bias=lnsc)
            P.act(Ek[i2][0:64, 0:n], G[0:64, 0:n], AF.Exp, scale=1.0 / 16)
            P.tt('dve', QG[0:64, t0:t0 + n], qs[i2][0:64, 0:n], Eq[i2][0:64, 0:n], ALU.mult)
            P.tt('dve', KG[0:64, t0:t0 + n], ks[i2][0:64, 0:n], Ek[i2][0:64, 0:n], ALU.mult)
            G3 = G[:, 0:n].rearrange("p (c s) -> p c s", s=64)
            d3 = dG[i2][:, 0:n].rearrange("p (c s) -> p c s", s=64)
            P.tt('dve', d3[0:32], G3[0:32, :, 63:64].to_broadcast([32, nch, 64]), G3[0:32], ALU.subtract)
            P.tt('dve', d3[32:64], G3[32:64, :, 0:1].to_broadcast([32, nch, 64]), G3[32:64], ALU.subtract)
            P.act(dG[i2][0:64, 0:n], dG[i2][0:64, 0:n], AF.Exp, scale=-1.0 / 16)
            P.tt('dve', kdT[i2][0:64, 0:n], ks[i2][0:64, 0:n], dG[i2][0:64, 0:n], ALU.mult)
            P.act(DEC[0:32, c0:c0 + nch], G3[0:32, :, 63], AF.Exp, scale=-1.0 / 16)
            P.act(DEC[32:64, c0:c0 + nch], G3[32:64, :, 0], AF.Exp, scale=-1.0 / 16)
            def rest2():
                for sub in range(n // 128):
                    pt = banks.b[5 + sub % 2].bitcast(BF16)
                    P.transpose(pt[:, 0:64], kdT[i2][0:64, sub * 128:(sub + 1) * 128], K['identb'][0:64, 0:64])
                    P.copy('dve', KDt[:, t0 // 128 + sub, :], pt[:, 0:64])
            return rest2
        return rest

    fm = [(0, 64, False, ev_q), (64, 96, False, ev_lr), (192, 128, False, ev_gv)]
    inproj(nc, P, banks, xn_v, Wsb, TOK_TILES, fm, None, xbufs, bank_ids=(0, 1, 2))
    for ti, (t0, n) in enumerate(TOK_TILES):
        P.act(te[ti % 2][0:64, 0:n], sg[0:64, t0:t0 + n], AF.Tanh, scale=0.5)
        P.stt(sg[0:64, t0:t0 + n], te[ti % 2][0:64, 0:n], 1.0, sg[0:64, t0:t0 + n], ALU.add, ALU.mult)

    for g0 in range(0, 66, 8):
        npair = min(8, 66 - g0)
        for i in range(npair):
            p = g0 + i
            for j in range(2):
                P.mm(banks.b[j + 2 * ((g0 // 8) % 2)][0:64, i * 64:(i + 1) * 64], KDt[64 * j:64 * j + 64, p, :], Vt[64 * j:64 * j + 64, p, :])
        for j in range(2):
            src = banks.b[j + 2 * ((g0 // 8) % 2)][:, 0:npair * 64].rearrange("p (c v) -> p c v", v=64)
            c0 = 2 * g0 + j
            P.copy('act', ST[0:32, c0 + 2:c0 + 1 + 2 * npair:2, :], src[0:32])
            P.copy('dve', ST[32:64, c0:c0 + 2 * npair - 1:2, :], src[32:64])
    fsteps = [(c + 2, c + 1, c) for c in range(1, NC_)]
    bsteps = [(c, c + 1, c) for c in (2, 1, 0)] + [(131, 0, 131)] + [(c, c + 1, c) for c in range(130, 4, -1)]
    for i in range(max(len(fsteps), len(bsteps))):
        if i < len(fsteps):
            o_, i_, d_ = fsteps[i]
            P.stt(ST[0:32, o_, :], ST[0:32, i_, :], DEC[0:32, d_:d_ + 1], ST[0:32, o_, :], ALU.mult, ALU.add)
        if i < len(bsteps):
            o_, i_, d_ = bsteps[i]
            P.stt(ST[32:64, o_, :], ST[32:64, i_, :], DEC[32:64, d_:d_ + 1], ST[32:64, o_, :], ALU.mult, ALU.add)
    P.copy('dve', ST[32:64, 132, :], ST[32:64, 0, :])
    P.copy('dve', STb[0:64], ST[0:64])

    A.release(mk)
    tm = [A.alloc([512], F32) for _ in range(2)]
    At = [A.alloc([512], BF16) for _ in range(2)]
    ob = [A.alloc([512], F32) for _ in range(2)]
    yb = [A.alloc([512], BF16) for _ in range(2)]
    tiles2 = [(ti, t0, n) for ti, (t0, n) in enumerate(TOK_TILES) if not (t0 < CTXL and not with_ctx)]

    def stageA(ti, t0, n):
        i2 = ti % 2
        npair = n // 128
        X, Y = banks.b[2], banks.b[3]
        for i in range(npair):
            cs = slice(t0 + 128 * i, t0 + 128 * (i + 1))
            P.mm(X[:, 128 * i:128 * (i + 1)], KG[0:32, cs], QG[0:32, cs])
            P.mm(Y[:, 128 * i:128 * (i + 1)], KG[32:64, cs], QG[32:64, cs])
        P.tt('dve', tm[i2][:, 0:n], X[:, 0:n], K['maskF'][:, 0:n], ALU.mult)
        P.tt('dve', At[i2][:, 0:n], Y[:, 0:n], K['maskB'][:, 0:n], ALU.mult)
        P.tt('pool', At[i2][:, 0:n], At[i2][:, 0:n], tm[i2][:, 0:n], ALU.add)

    def stageB(ti, t0, n):
        i2 = ti % 2
        npair = n // 128
        Z = banks.b[4 + i2]
        for i in range(npair):
            p = t0 // 128 + i
            P.mm(Z[0:64, 128 * i:128 * (i + 1)], Vt[:, p, :], At[i2][:, 128 * i:128 * (i + 1)], start=True, stop=False)
            for j in range(2):
                c = 2 * p + j
                cs = slice(t0 + 128 * i + 64 * j, t0 + 128 * i + 64 * (j + 1))
                kk = 32 if c == 3 else 64
                P.mm(Z[0:64, 128 * i + 64 * j:128 * i + 64 * (j + 1)], STb[0:kk, c + 1, :], QG[0:kk, cs],
                     start=False, stop=(j == 1))
        o = ob[i2]
        P.act(o[0:64, 0:n], Z[0:64, 0:n], AF.Square)

    def stageC(ti, t0, n):
        i2 = ti % 2
        Z = banks.b[4 + i2]
        o = ob[i2]
        zb = banks.b[6 + i2]
        P.mm(zb[0:64, 0:n], K['ones'][0:64, 0:64], o[0:64, 0:n])
        P.act(o[0:64, 0:n], zb[0:64, 0:n], AF.Ln, scale=1.0 / 64, bias=K['eps'][0:64, :])
        P.act(o[0:64, 0:n], o[0:64, 0:n], AF.Exp, scale=-0.5)
        P.stt(o[0:64, 0:n], Z[0:64, 0:n], par[0:64, 1:2], o[0:64, 0:n], ALU.mult, ALU.mult)
        y = yb[i2]
        P.stt(y[0:64, 0:n], o[0:64, 0:n], 0.5, sg[0:64, t0:t0 + n], ALU.mult, ALU.mult)
        store_y(P, io, 0, t0, n, y[0:64, 0:n])

    nt2 = len(tiles2)
    for it in range(nt2 + 1):
        if it < nt2:
            stageA(*tiles2[it])
        if it >= 1:
            stageB(*tiles2[it - 1])
            stageC(*tiles2[it - 1])


def ssd_phase(nc, P, A, banks, K, xn_v, io, layer, with_ctx):
    A.reset()
    NC_ = 132
    XB = A.alloc([TP], BF16)
    C2 = A.alloc([TP], BF16)
    sgz = A.alloc([NTOK], BF16)
    par = A.alloc([16], F32)
    P.dma('sp', par[:, 0:5], io['ssd_par'])
    cpar = A.alloc([2, 5], F32)
    P.dma('sp', cpar[:], io['ssd_cpar'])
    na = par[:, 5:7]
    P.act(na, par[:, 2:4], AF.Exp)
    P.ts('dve', na, na, -1.0, None, ALU.mult)
    dts_d = io['ssd_scr'][0:2]
    crow_d = io['ssd_scr'][2:4]
    clr_d = io['ssd_scr'][4:6]
    mk = A.mark()
    stage = [A.alloc([322], F32) for _ in range(2)]
    Wsb = load_weights_bf16(nc, P, A, io['w_ssd'], 322, stage)
    xbufs = [A.alloc([8, 512], BF16) for _ in range(2)]
    XR1 = A.alloc([TP], F32)
    XR2 = A.alloc([TP], F32)
    XCs = [A.alloc([2052], F32) for _ in range(2)]
    te = [A.alloc([512], F32) for _ in range(2)]
    dtt = [A.alloc([512], F32) for _ in range(2)]
    for X in (XR1, XR2):
        P.memset('pool', X[:, 0:PADC], 0.0)
        P.memset('pool', X[:, PADC + CTXL:PADL], 0.0)
        P.memset('pool', X[:, PADL + SEQ:TP], 0.0)

    def ev_raw(dst):
        def f(ps, t0, n, ti):
            c = pcol(t0)
            P.copy('act', dst[:, c:c + n], ps[:, 0:n])
        return f

    def ev_gate_dt(ps, t0, n, ti):
        silu_evac(P, sgz[0:64, t0:t0 + n], ps[0:64, 0:n], te[ti % 2], 64, n)
        P.copy('dve', dtt[ti % 2][64:66, 0:n], ps[64:66, 0:n])
        P.dma('pool', dts_d[:, t0:t0 + n], dtt[ti % 2][64:66, 0:n])

    fm = [(0, 128, False, ev_raw(XR1)), (128, 128, False, ev_raw(XR2)), (256, 66, False, ev_gate_dt)]
    inproj(nc, P, banks, xn_v, Wsb, TOK_TILES, fm, None, xbufs, bank_ids=(0, 1, 2, 3))
    pieces = [(PADC, PADC + CTXL)] + [(PADL + 2048 * i, PADL + 2048 * (i + 1)) for i in range(4)]
    for gi, (src, dst) in enumerate(((XR1, XB), (XR2, C2))):
        for pi_, (lo, hi) in enumerate(pieces):
            n = hi - lo
            XC = XCs[(gi * len(pieces) + pi_) % 2]
            P.act(XC[:, 2:2 + n], src[:, lo - 2:hi - 2], AF.Identity, scale=cpar[:, gi, 0:1], bias=cpar[:, gi, 4:5])
            for k in range(1, 4):
                P.stt(XC[:, 2:2 + n], src[:, lo + k - 2:hi + k - 2], cpar[:, gi, k:k + 1], XC[:, 2:2 + n], ALU.mult, ALU.add)
            P.act(dst[:, lo:hi], XC[:, 2:2 + n], AF.Silu)
    A.release(mk)
    PM = A.alloc([2, 128], F32)
    for d in range(2):
        P.dma('sp', PM[0:66, d, :], dts_d[d].rearrange("(p s) -> p s", s=128))
    DT = A.alloc([2, 128], F32)
    for d in range(2):
        P.act(DT[0:66, d, :], PM[0:66, d, :], AF.Exp, bias=par[0:66, d:d + 1])
    P.act(DT[0:66], DT[0:66], AF.Ln, bias=K['one'][0:66, 0:1])
    DA = A.alloc([2, 128], F32)
    for d in range(2):
        P.ts('dve', DA[0:66, d, :], DT[0:66, d, :], na[0:66, d:d + 1], None, ALU.mult)
    CUM = A.alloc([2, 128], F32)
    P.scan(CUM[0:66, 0, :], K['scanmask2'][0:66, 0, :], DA[0:66, 0, :], 0.0)
    P.scan(CUM[0:66, 1, :][:, ::-1], K['scanmask2'][0:66, 1, :][:, ::-1], DA[0:66, 1, :][:, ::-1], 0.0)
    P.dma('pool', crow_d[0].rearrange("(p s) -> p s", s=128), CUM[0:66, 0, :])
    P.dma('pool', crow_d[1].rearrange("(p s) -> p s", s=128), CUM[0:66, 1, :])
    LND = A.alloc([2, 128], F32)
    P.act(LND[0:66], DT[0:66], AF.Ln)
    CQ0 = A.alloc([2, 128], F32)
    P.tt('dve', CQ0[0:66], CUM[0:66], LND[0:66], ALU.subtract)
    CL = A.alloc([2, 2], F32)
    C4 = CUM[0:66].rearrange("p d (c s) -> p d c s", s=64)
    P.copy('dve', CL[0:66, 0, :], C4[:, 0, :, 63])
    P.copy('dve', CL[0:66, 1, :], C4[:, 1, :, 0])
    P.dma('pool', clr_d[0, 0:132].rearrange("(p c) -> p c", c=2), CL[0:66, 0, :])
    P.dma('pool', clr_d[1, 0:132].rearrange("(p c) -> p c", c=2), CL[0:66, 1, :])
    W0 = A.alloc([2, 128], F32)
    W4 = W0[0:66].rearrange("p d (c s) -> p d c s", s=64)
    P.tt('dve', W4, CL[0:66].unsqueeze(3).to_broadcast([66, 2, 2, 64]), C4, ALU.subtract)
    P.act(W0[0:66], W0[0:66], AF.Exp)
    P.tt('dve', W0[0:66], W0[0:66], DT[0:66], ALU.mult)
    CQ = A.alloc([2, 66], F32)
    WT = A.alloc([2, 66], F32)
    for d in range(2):
        for (src, dst) in ((CQ0, CQ), (W0, WT)):
            pt = banks.b[(2 * d) % 8 + (0 if src is CQ0 else 1)]
            P.transpose(pt[:, 0:66], src[0:66, d, :], K['ident'][0:66, 0:66])
            P.copy('act', dst[:, d, :], pt[:, 0:66])
    clr = A.alloc([132], F32)
    P.dma('sp', clr[0:2, :], clr_d[:, 0:132])
    DEC = A.alloc([132], F32)
    pd = banks.b[4]
    P.mm(pd[:, 0:132], K['sel2'][0:2, :], clr[0:2, :])
    P.act(DEC[:, :], pd[:, 0:132], AF.Exp)
    xBt = A.alloc([66, 128], BF16)
    for p in range(66):
        c0 = pcol(128 * p)
        pt = banks.b[p % 4].bitcast(BF16)
        P.transpose(pt[:, 0:128], XB[:, c0:c0 + 128], K['identb'][:, :])
        P.copy('act' if p % 2 == 0 else 'dve', xBt[:, p, :], pt[:, 0:128])
    BW = A.alloc([66, 128], BF16)
    for d in range(2):
        P.tt('dve', BW[:, :, 64 * d:64 * d + 64], xBt[:, :, 64:128], WT[:, d, :].unsqueeze(2).to_broadcast([128, 66, 64]), ALU.mult)
    ST = A.alloc([NC_ + 2, 64], F32)
    STb = A.alloc([NC_ + 2, 64], BF16)
    P.memset('pool', ST[:, 0:2, :], 0.0)
    P.memset('pool', ST[:, NC_:NC_ + 2, :], 0.0)
    for g0 in range(0, 66, 8):
        npair = min(8, 66 - g0)
        for i in range(npair):
            p = g0 + i
            for j in range(2):
                P.mm(banks.b[j + 2 * ((g0 // 8) % 2)][:, i * 64:(i + 1) * 64], BW[64 * j:64 * j + 64, p, :], xBt[64 * j:64 * j + 64, p, 0:64])
        for j in range(2):
            src = banks.b[j + 2 * ((g0 // 8) % 2)][:, 0:npair * 64].rearrange("p (c v) -> p c v", v=64)
            c0 = 2 * g0 + j
            P.copy('act', ST[0:64, c0 + 2:c0 + 1 + 2 * npair:2, :], src[0:64])
            P.copy('dve', ST[64:128, c0:c0 + 2 * npair - 1:2, :], src[64:128])
    fsteps = [(c + 2, c + 1, c) for c in range(1, NC_)]
    bsteps = [(c, c + 1, c) for c in (2, 1, 0)] + [(131, 0, 131)] + [(c, c + 1, c) for c in range(130, 4, -1)]
    for i in range(max(len(fsteps), len(bsteps))):
        if i < len(fsteps):
            o_, i_, d_ = fsteps[i]
            P.stt(ST[0:64, o_, :], ST[0:64, i_, :], DEC[0:64, d_:d_ + 1], ST[0:64, o_, :], ALU.mult, ALU.add)
        if i < len(bsteps):
            o_, i_, d_ = bsteps[i]
            P.stt(ST[64:128, o_, :], ST[64:128, i_, :], DEC[64:128, d_:d_ + 1], ST[64:128, o_, :], ALU.mult, ALU.add)
    P.copy('dve', ST[64:128, 132, :], ST[64:128, 0, :])
    P.copy('dve', STb[:], ST[:])
    crt = [A.alloc([512], F32) for _ in range(2)]
    SG = [A.alloc([2, 512], F32) for _ in range(2)]
    Ls = [A.alloc([512], F32) for _ in range(2)]
    Mt = [A.alloc([512], BF16) for _ in range(2)]
    ec = [A.alloc([512], F32) for _ in range(2)]
    Cs = [A.alloc([512], BF16) for _ in range(2)]
    yv = [A.alloc([512], F32) for _ in range(2)]
    yb = [A.alloc([512], BF16) for _ in range(2)]
    tiles2 = [(ti, t0, n) for ti, (t0, n) in enumerate(TOK_TILES) if not (t0 < CTXL and not with_ctx)]

    def stageA(ti, t0, n):
        i2 = ti % 2
        npair = n // 128
        p0 = t0 // 128
        pc0 = pcol(t0)
        cr = crt[i2]
        P.dma('sp', cr[0:2, 0:n], crow_d[:, t0:t0 + n])
        pe_ = banks.b[0]
        P.mm(pe_[:, 0:n], K['sel2'][0:2, :], cr[0:2, 0:n])
        P.act(ec[i2][:, 0:n], pe_[:, 0:n], AF.Exp)
        P.tt('dve', Cs[i2][:, 0:n], C2[:, pc0:pc0 + n], ec[i2][:, 0:n], ALU.mult)
        for d in range(2):
            pb = banks.b[1 + d]
            P.mm(pb[:, 0:n], K['selrow'][0:2, d, :], cr[0:2, 0:n], start=True, stop=False)
            P.mm(pb[:, 0:n], K['identb'][:, :], K['negF' if d == 0 else 'negB'][:, 0:n], start=False, stop=True)
            P.tt('dve', SG[i2][:, d, 0:n].rearrange("p (a b) -> p a b", b=128),
                 pb[:, 0:n].rearrange("p (a b) -> p a b", b=128),
                 CQ[:, d, p0:p0 + npair].unsqueeze(2).to_broadcast([128, npair, 128]), ALU.subtract)
        P.act(SG[i2][:, :, 0:n], SG[i2][:, :, 0:n], AF.Exp)
        P.tt('pool', Ls[i2][:, 0:n], SG[i2][:, 0, 0:n], SG[i2][:, 1, 0:n], ALU.add)
        pcb = banks.b[3]
        for i in range(npair):
            cs = slice(pc0 + 128 * i, pc0 + 128 * (i + 1))
            P.mm(pcb[:, 128 * i:128 * (i + 1)], XB[64:128, cs], C2[64:128, cs])
        P.tt('dve', Mt[i2][:, 0:n], pcb[:, 0:n], Ls[i2][:, 0:n], ALU.mult)

    def stageB(ti, t0, n):
        i2 = ti % 2
        npair = n // 128
        p0 = t0 // 128
        pc0 = pcol(t0)
        Y = banks.b[4 + i2]
        for i in range(npair):
            p = p0 + i
            P.mm(Y[0:64, 128 * i:128 * (i + 1)], xBt[:, p, 0:64], Mt[i2][:, 128 * i:128 * (i + 1)], start=True, stop=False)
            for j in range(2):
                c = 2 * p + j
                kk = 64 if c == 3 else 128
                P.mm(Y[0:64, 128 * i + 64 * j:128 * i + 64 * (j + 1)], STb[0:kk, c + 1, :],
                     Cs[i2][0:kk, 128 * i + 64 * j:128 * i + 64 * (j + 1)], start=False, stop=(j == 1))
        P.stt(yv[i2][0:64, 0:n], XB[0:64, pc0:pc0 + n], par[0:64, 4:5], Y[0:64, 0:n], ALU.mult, ALU.add)
        y = yb[i2]
        P.stt(y[0:64, 0:n], yv[i2][0:64, 0:n], 0.5, sgz[0:64, t0:t0 + n], ALU.mult, ALU.mult)
        store_y(P, io, 3, t0, n, y[0:64, 0:n])

    nt2 = len(tiles2)
    for it in range(nt2 + 1):
        if it < nt2:
            stageA(*tiles2[it])
        if it >= 1:
            stageB(*tiles2[it - 1])

OFF_A, OFF_B, OFF_C, OFF_D = 0, 800, 1312, 2336


def rope_tables():
    n_freq = 8
    inv = (np.float32(10000.0) ** (-(np.arange(n_freq, dtype=np.float32)) / np.float32(n_freq))).astype(np.float32)
    t = np.arange(SEQ)
    pos_r = (t // 64).astype(np.float32)
    pos_c = (t % 64).astype(np.float32)
    ang_r = pos_r[:, None] * inv
    ang_c = pos_c[:, None] * inv
    ang = np.concatenate([ang_r, ang_r, ang_c, ang_c], axis=-1).astype(np.float32)
    cos = np.cos(ang).astype(np.float32).T
    sin = np.sin(ang).astype(np.float32).T
    sign = np.ones(32, np.float32)
    for a in range(2):
        sign[a * 16:a * 16 + 8] = -1.0
    sins = sin * sign[:, None]
    cosT = np.zeros((128, SEQ), np.float32)
    sinT = np.zeros((128, SEQ), np.float32)
    for c in range(2):
        cosT[64 * c:64 * c + 32] = cos
        sinT[64 * c:64 * c + 32] = sins
    return cosT, sinT


def rot_perm():
    perm = np.zeros(32, np.int64)
    for a in range(2):
        for f in range(8):
            perm[a * 16 + f] = a * 16 + 8 + f
            perm[a * 16 + 8 + f] = a * 16 + f
    return perm


def prep_w_attn(w_in_l, h):
    W = np.zeros((D, 640), np.float32)
    perm = rot_perm()
    for c in range(2):
        qc = OFF_C + h * 64 + c * 32
        kc = OFF_C + 256 + h * 64 + c * 32
        W[:, 64 * c:64 * c + 32] = w_in_l[:, qc:qc + 32]
        W[:, 128 + 64 * c:128 + 64 * c + 32] = w_in_l[:, qc + perm]
        W[:, 256 + 64 * c:256 + 64 * c + 32] = w_in_l[:, kc:kc + 32]
        W[:, 384 + 64 * c:384 + 64 * c + 32] = w_in_l[:, kc + perm]
    W[:, 512:576] = w_in_l[:, OFF_C + 768 + h * 64:OFF_C + 768 + h * 64 + 64]
    W[:, 576:640] = w_in_l[:, OFF_C + 512 + h * 64:OFF_C + 512 + h * 64 + 64]
    return W


def mix_consts():
    selZ = np.zeros((128, 64), np.float32)
    selZ[64, :] = 1.0
    t = np.arange(512)
    scanmask = np.zeros((64, 512), np.float32)
    scanmask[0:32] = (t % 64 != 0).astype(np.float32)[None]
    scanmask[32:64] = (t % 64 != 63).astype(np.float32)[None]
    j = np.arange(128)[:, None]
    i = np.arange(128)[None, :]
    same = (j // 64) == (i // 64)
    mF = (same & (j <= i)).astype(np.float32)
    mB = (same & (j >= i)).astype(np.float32)
    scanmask2 = np.zeros((128, 2, 128), np.float32)
    s_ = np.arange(128)
    scanmask2[:, 0, :] = (s_ % 64 != 0).astype(np.float32)[None]
    scanmask2[:, 1, :] = (s_ % 64 != 63).astype(np.float32)[None]
    sel2 = np.zeros((2, 128), np.float32)
    sel2[0, 0:64] = 1.0
    sel2[1, 64:128] = 1.0
    selrow = np.zeros((2, 2, 128), np.float32)
    selrow[0, 0, :] = 1.0
    selrow[1, 1, :] = 1.0
    NEG = -30000.0
    negF = np.where(mF > 0, 0.0, NEG).astype(np.float32)
    negB = np.where(mB > 0, 0.0, NEG).astype(np.float32)
    return {'selZ': selZ, 'ident': np.eye(128, dtype=np.float32), 'scanmask': scanmask,
            'maskF': np.tile(mF, (1, 4)), 'maskB': np.tile(mB, (1, 4)), 'scanmask2': scanmask2,
            'sel2': sel2, 'selrow': selrow, 'negF': np.tile(negF, (1, 4)), 'negB': np.tile(negB, (1, 4))}


def build_mix_kernel(nc, P, layer, with_ctx, mixers):
    banks = Banks(P)
    K = load_consts_tok(nc, P)
    epst = P.sbuf("eps_t", [128, 1], F32)
    P.memset('pool', epst[:], EPS)
    K['eps'] = epst
    selZ_d = nc.dram_tensor("selZ", [128, 64], F32, kind="ExternalInput").ap()
    selZ = P.sbuf("selZ_sb", [128, 64], F32)
    P.dma('sp', selZ[:], selZ_d)
    K['selZ'] = selZ
    ident_d = nc.dram_tensor("ident", [128, 128], F32, kind="ExternalInput").ap()
    ident = P.sbuf("ident_sb", [128, 128], F32)
    P.dma('sp', ident[:], ident_d)
    identb = P.sbuf("identb_sb", [128, 128], BF16)
    P.copy('pool', identb[:], ident[:])
    K['ident'] = ident
    K['identb'] = identb
    one = P.sbuf("one_t", [128, 1], F32)
    P.memset('pool', one[:], 1.0)
    K['one'] = one
    for nm, shp in (('scanmask', [64, 512]), ('maskF', [128, 512]), ('maskB', [128, 512]),
                    ('scanmask2', [128, 2, 128]), ('sel2', [2, 128]), ('selrow', [2, 2, 128])):
        d_ = nc.dram_tensor(nm, shp, F32, kind="ExternalInput").ap()
        t_ = P.sbuf(nm + "_sb", shp, F32)
        P.dma('sp', t_[:], d_)
        K[nm] = t_
    for nm in ('negF', 'negB'):
        d_ = nc.dram_tensor(nm, [128, 512], F32, kind="ExternalInput").ap()
        t_ = P.sbuf(nm + "_f", [128, 512], F32)
        P.dma('sp', t_[:], d_)
        tb_ = P.sbuf(nm + "_sb", [128, 512], BF16)
        P.copy('pool', tb_[:], t_[:])
        K[nm] = tb_
    io = {}
    xn_d = nc.dram_tensor("xn", [D, NTOK], BF16, kind="ExternalInput").ap()
    xn_v = xn_d.rearrange("(kc p) t -> p kc t", p=128)
    io['yT'] = nc.dram_tensor("yT", [4, 64, NTOK], BF16, kind="ExternalOutput").ap()
    A = P.arena("arena", 190 * 1024)
    if 'attn' in mixers:
        io['w_attn'] = nc.dram_tensor("w_attn", [D, 640], F32, kind="ExternalInput").ap()
        io['cosT'] = nc.dram_tensor("cosT", [128, SEQ], F32, kind="ExternalInput").ap()
        io['sinT'] = nc.dram_tensor("sinT", [128, SEQ], F32, kind="ExternalInput").ap()
        io['diff_lam'] = nc.dram_tensor("diff_lam", [128], F32, kind="ExternalInput").ap()
        io['diff_subln_w'] = nc.dram_tensor("diff_subln_w", [64, 1], F32, kind="ExternalInput").ap()
        attention_phase(nc, P, A, banks, K, xn_v, io, layer, with_ctx)
    if 'gla' in mixers:
        _add_gla(nc, P, A, banks, K, xn_v, io, layer, with_ctx)
    if 'ssd' in mixers:
        io['w_ssd'] = nc.dram_tensor("w_ssd", [D, 322], F32, kind="ExternalInput").ap()
        io['ssd_par'] = nc.dram_tensor("ssd_par", [128, 5], F32, kind="ExternalInput").ap()
        io['ssd_cpar'] = nc.dram_tensor("ssd_cpar", [128, 2, 5], F32, kind="ExternalInput").ap()
        io['ssd_scr'] = nc.dram_tensor("ssd_scr", [6, NTOK], F32, kind="Internal").ap()
        ssd_phase(nc, P, A, banks, K, xn_v, io, layer, with_ctx)
    if 'lru' in mixers:
        io['w_lru'] = nc.dram_tensor("w_lru", [D, 192], F32, kind="ExternalInput").ap()
        io['lru_par'] = nc.dram_tensor("lru_par", [128, 9], F32, kind="ExternalInput").ap()
        io['lru_gw'] = nc.dram_tensor("lru_gw", [64, 256], F32, kind="ExternalInput").ap()
        lru_phase(nc, P, A, banks, K, xn_v, io, layer, with_ctx)


def _add_gla(nc, P, A, banks, K, xn_v, io, layer, with_ctx):
    io['w_gla'] = nc.dram_tensor("w_gla", [D, 320], F32, kind="ExternalInput").ap()
    io['gla_par'] = nc.dram_tensor("gla_par", [64, 2], F32, kind="ExternalInput").ap()
    io['gla_w2bd'] = nc.dram_tensor("gla_w2bd", [32, 64], F32, kind="ExternalInput").ap()
    gla_phase(nc, P, A, banks, K, xn_v, io, layer, with_ctx)


def prep_gla(inp, l, h):
    w_in_l = inp['w_in'][l]
    W = np.zeros((D, 288), np.float32)
    q = w_in_l[:, OFF_A + h * 32:OFF_A + h * 32 + 32]
    k = w_in_l[:, OFF_A + 128 + h * 32:OFF_A + 128 + h * 32 + 32]
    W[:, 0:32] = q
    W[:, 32:64] = q
    W[:, 64:96] = k
    W[:, 96:128] = k
    W[:, 128:144] = w_in_l[:, OFF_A + 512:OFF_A + 528]
    W[:, 144:160] = w_in_l[:, OFF_A + 528:OFF_A + 544]
    W[:, 192:256] = w_in_l[:, OFF_A + 544 + h * 64:OFF_A + 544 + h * 64 + 64]
    W[:, 256:288] = 0
    Wv = w_in_l[:, OFF_A + 256 + h * 64:OFF_A + 256 + h * 64 + 64]
    W2 = np.zeros((D, 320), np.float32)
    W2[:, 0:256] = W[:, 0:256]
    W2[:, 256:320] = Wv
    par = np.zeros((64, 2), np.float32)
    par[0:32, 0] = inp['gla_b2'][l][0][h * 32:(h + 1) * 32]
    par[32:64, 0] = inp['gla_b2'][l][1][h * 32:(h + 1) * 32]
    par[:, 1] = inp['gla_norm_w'][l]
    w2bd = np.zeros((32, 64), np.float32)
    w2bd[0:16, 0:32] = inp['gla_w2'][l][0][:, h * 32:(h + 1) * 32]
    w2bd[16:32, 32:64] = inp['gla_w2'][l][1][:, h * 32:(h + 1) * 32]
    return {"w_gla": W2, "gla_par": par, "gla_w2bd": w2bd}


def prep_ssd(inp, l, h):
    w_in_l = inp['w_in'][l]
    gr = h // 2
    W = np.zeros((D, 322), np.float32)
    cx = slice(OFF_D + h * 64, OFF_D + h * 64 + 64)
    cB = slice(OFF_D + 256 + gr * 64, OFF_D + 256 + gr * 64 + 64)
    cC = slice(OFF_D + 384 + gr * 64, OFF_D + 384 + gr * 64 + 64)
    W[:, 0:64] = w_in_l[:, cx]
    W[:, 64:128] = w_in_l[:, cB]
    W[:, 128:192] = w_in_l[:, cC]
    W[:, 192:256] = w_in_l[:, cC]
    W[:, 256:320] = w_in_l[:, OFF_D + 520 + h * 64:OFF_D + 520 + h * 64 + 64]
    W[:, 320] = w_in_l[:, OFF_D + 512 + h]
    W[:, 321] = w_in_l[:, OFF_D + 516 + h]
    par = np.zeros((128, 5), np.float32)
    par[:, 0] = inp['ssd_dt_bias'][l][0][h]
    par[:, 1] = inp['ssd_dt_bias'][l][1][h]
    par[:, 2] = inp['ssd_a_log'][l][0][h]
    par[:, 3] = inp['ssd_a_log'][l][1][h]
    par[:, 4] = inp['ssd_d'][l][h]
    cw, cb = inp['ssd_conv_w'][l], inp['ssd_conv_b'][l]
    cpar = np.zeros((128, 2, 5), np.float32)
    ch = [np.r_[h * 64:h * 64 + 64, 256 + gr * 64:256 + gr * 64 + 64],
          np.r_[384 + gr * 64:384 + gr * 64 + 64, 384 + gr * 64:384 + gr * 64 + 64]]
    for g in range(2):
        cpar[:, g, 0:4] = cw[:, ch[g]].T
        cpar[:, g, 4] = cb[ch[g]]
    return {"w_ssd": W, "ssd_par": par, "ssd_cpar": cpar}


def prep_lru(inp, l, h):
    w_in_l = inp['w_in'][l]
    W = np.zeros((D, 192), np.float32)
    xs = w_in_l[:, OFF_B + h * 64:OFF_B + h * 64 + 64]
    W[:, 0:64] = xs
    W[:, 64:128] = xs
    W[:, 128:192] = w_in_l[:, OFF_B + 256 + h * 64:OFF_B + 256 + h * 64 + 64]
    sl = slice(h * 64, h * 64 + 64)
    par = np.zeros((128, 9), np.float32)
    for d in range(2):
        rows = slice(64 * d, 64 * d + 64)
        par[rows, 0:4] = inp['lru_conv_w'][l][:, sl].T
        par[rows, 4] = inp['lru_conv_b'][l][sl]
        par[rows, 5] = inp['lru_ba'][l][d][sl]
        par[rows, 6] = inp['lru_bx'][l][d][sl]
        par[rows, 7] = inp['lru_lam'][l][d][sl]
    gw = np.zeros((64, 256), np.float32)
    for d in range(2):
        gw[:, 64 * d:64 * d + 64] = inp['lru_wa'][l][d][h]
        gw[:, 128 + 64 * d:128 + 64 * d + 64] = inp['lru_wx'][l][d][h]
    return {"w_lru": W, "lru_par": par, "lru_gw": gw}


def mix_inputs(inp, l, h, mixers, cosT, sinT, cst):
    m = dict(cst)
    if 'ssd' in mixers:
        m.update(prep_ssd(inp, l, h))
    if 'gla' in mixers:
        m.update(prep_gla(inp, l, h))
    if 'attn' in mixers:
        m.update({"w_attn": prep_w_attn(inp['w_in'][l], h), "cosT": cosT, "sinT": sinT,
                  "diff_lam": np.ascontiguousarray(inp['diff_lam'][l].reshape(-1)),
                  "diff_subln_w": np.ascontiguousarray(inp['diff_subln_w'][l].reshape(64, 1))})
    if 'lru' in mixers:
        m.update(prep_lru(inp, l, h))
    return m


def _rt_of(hl, hc, b, q):
    return np.ascontiguousarray(np.concatenate([hc[b, 64 * q:64 * q + 64].T, hl[b, 2048 * q:2048 * (q + 1)].T], axis=1))


def _gather_cols(parts):
    return np.ascontiguousarray(np.concatenate([p[:, 0:64] for p in parts] + [p[:, 64:] for p in parts], axis=1))


def kernel_unfused(**inp):
    inp = {k: np.asarray(v) for k, v in inp.items()}
    x, ctx, c, c_ctx = inp['x'], inp['ctx'], inp['c'], inp['c_ctx']
    cosT, sinT = rope_tables()
    cst = mix_consts()
    cvecs = [np.ascontiguousarray(np.stack([pk(c[b]), pk(c_ctx)], axis=2)) for b in range(2)]
    maps = []
    for core in range(8):
        b, q = core // 4, core % 4
        maps.append({"RT": _rt_of(x, ctx, b, q), "cvec": cvecs[b], "wmod_n": inp['w_mod'][0],
                     "bmod_n": pk(inp['b_mod'][0]), "normw": pk(inp['norm_w'][0])})
    res = _launch(lambda nc, P: build_tok_kernel(nc, P, True, False, False), maps)
    RT = [m["RT"] for m in maps]
    xnT = [np.asarray(r["xnT"]) for r in res]
    out = None
    for l in range(2):
        last = (l == 1)
        xn_full = [_gather_cols(xnT[4 * b:4 * b + 4]) for b in range(2)]
        maps = []
        for core in range(8):
            b, h = core // 4, core % 4
            m = {"xn": xn_full[b]}
            m.update(mix_inputs(inp, l, h, ('gla', 'lru', 'attn', 'ssd'), cosT, sinT, cst))
            maps.append(m)
        res = _launch(lambda nc, P: build_mix_kernel(nc, P, l, not last, ('gla', 'lru', 'attn', 'ssd')), maps)
        yT = [np.asarray(r["yT"]) for r in res]
        maps = []
        for core in range(8):
            b, q = core // 4, core % 4
            yb = np.stack([yT[4 * b + h] for h in range(4)], axis=1).reshape(1024, NTOK)
            yq = np.ascontiguousarray(np.concatenate([yb[:, 64 * q:64 * q + 64],
                                                      yb[:, CTXL + 2048 * q:CTXL + 2048 * (q + 1)]], axis=1))
            m = {"RT": RT[core], "cvec": cvecs[b], "yT": yq, "wout": inp['w_out'][l], "ssdnw": pk(inp['ssd_norm_w'][l]),
                 "wmod_g": inp['w_mod'][l], "bmod_g": pk(inp['b_mod'][l])}
            if last:
                m["fnw"] = pk(inp['final_norm_w'])
            else:
                m.update({"wmod_n": inp['w_mod'][l + 1], "bmod_n": pk(inp['b_mod'][l + 1]), "normw": pk(inp['norm_w'][l + 1])})
            maps.append(m)
        res = _launch(lambda nc, P: build_tok_kernel(nc, P, False, last, True), maps)
        if last:
            out = np.zeros((2, SEQ, D), np.float32)
            for core in range(8):
                b, q = core // 4, core % 4
                out[b, 2048 * q:2048 * (q + 1), :] = np.asarray(res[core]["outT"]).T
        else:
            RT = [np.asarray(r["RTout"]) for r in res]
            xnT = [np.asarray(r["xnT"]) for r in res]
    return out

CHUNKS = [(0, 256)] + [(256 + 2048 * k, 2048) for k in range(4)]
GROUPS = [[0, 1, 2, 3], [4, 5, 6, 7]]


def fs_mod(nc, P, K, banks, A, cvec_d, wmod_d, bmod_d, name):
    cT = A.alloc([8, 2], F32)
    P.dma('sp', cT[:], cvec_d)
    e = A.alloc([8, 2], F32)
    P.act(e[:], cT[:], AF.Exp, scale=-1.0)
    P.ts('dve', e[:], e[:], 1.0, None, ALU.add)
    P.recip(e[:], e[:])
    sc = A.alloc([8, 2], F32)
    P.tt('dve', sc[:], cT[:], e[:], ALU.mult)
    bT = A.alloc([6], F32)
    P.dma('sp', bT[:], bmod_d)
    wb = A.alloc([8, 768], F32)
    for kc in range(8):
        P.dma('sp' if kc % 2 == 0 else 'act', wb[:, kc, :], wmod_d[kc * 128:(kc + 1) * 128, :])
    ps = banks.b[7]
    for fc in range(6):
        for kc in range(8):
            P.mm(ps[:, fc * 2:fc * 2 + 2], wb[:, kc, fc * 128:(fc + 1) * 128], sc[:, kc, :], start=(kc == 0), stop=(kc == 7))
    modT = P.sbuf(name, [128, 6, 2], F32)
    P.tt('dve', modT[:], ps[:, 0:12].rearrange("p (a b) -> p a b", b=2), bT[:].unsqueeze(2).to_broadcast([128, 6, 2]), ALU.add)
    return modT


def fs_token_phase(nc, P, K, banks, A, io, L, mode):
    A.reset()
    D_ = io['dram']
    first, last = mode == 'first', mode == 'last'
    if not first:
        gateT = fs_mod(nc, P, K, banks, A, io['cvec'], io['wmod'][L], io['bmod'][L], "modT_g%d" % L)
        wo32 = A.alloc([8, 256], F32)
        for kc in range(8):
            P.dma('sp' if kc % 2 == 0 else 'act', wo32[:, kc, :], io['wout'][L][kc * 128:(kc + 1) * 128, :])
        wo = A.alloc([8, 256], BF16)
        P.copy('dve', wo[:], wo32[:])
        sw = A.alloc([4], F32)
        P.dma('sp', sw[:], io['ssdnw'][L])
    if not last:
        LN = 0 if first else L + 1
        modN = fs_mod(nc, P, K, banks, A, io['cvec'], io['wmod'][LN], io['bmod'][LN], "modT_n%d" % LN)
        nw = A.alloc([2], F32)
        P.dma('sp', nw[:], io['normw'][LN])
        Acoef = A.alloc([2, 2], F32)
        P.ts('dve', Acoef[:], modN[:, 2:4, :], 1.0, None, ALU.add)
        P.tt('dve', Acoef[:], Acoef[:], nw[:].unsqueeze(2).to_broadcast([128, 2, 2]), ALU.mult)
    else:
        fw = A.alloc([2], F32)
        P.dma('sp', fw[:], io['fnw'])
    Rb = [A.alloc([2, 2048], F32) for _ in range(2)]
    yb_ = [A.alloc([8, 512], BF16) for _ in range(2)]
    sq = [A.alloc([512], BF16) for _ in range(4)]
    rst = [A.alloc([512], F32) for _ in range(2)]
    t1 = [A.alloc([512], F32) for _ in range(4)]
    ssr = [A.alloc([2048], F32) for _ in range(2)]
    ssg = [A.alloc([2048], F32) for _ in range(2)]
    xo = [A.alloc([2, 512], BF16) for _ in range(2)]
    Rsrc = io['RT_in'] if first else D_['Rs']
    Rsrc_v = Rsrc.rearrange("(fc p) t -> p fc t", p=128)
    Rs_v = D_['Rs'].rearrange("(fc p) t -> p fc t", p=128)
    stage = 'n%d' % (0 if first else L + 1) if not last else 'fin'
    chunks = [(ci, t0, n) for ci, (t0, n) in enumerate(CHUNKS) if not (last and ci == 0)]
    jobs = []
    for (ci, t0, n) in chunks:
        subs = [(s0, min(512, n - s0)) for s0 in range(0, n, 512)]
        for si, (s0, m_) in enumerate(subs):
            jobs.append(dict(ci=ci, t0=t0, n=n, s0=s0, m=m_, first=(si == 0), lastsub=(si == len(subs) - 1)))
    ybuf3 = yb_ + [A.alloc([8, 512], BF16)]

    def P1(j, ji):
        ci, t0, n, s0, m = j['ci'], j['t0'], j['n'], j['s0'], j['m']
        R = Rb[ci % 2]
        if j['first']:
            P.dma('sp', R[:, :, 0:n], Rsrc_v[:, :, t0:t0 + n])
        if first:
            return
        gy = D_['gy%d' % L][ci].rearrange("(kc p) t -> p kc t", p=128)
        yt = ybuf3[ji % 3]
        P.dma('act', yt[:, :, 0:m], gy[:, :, s0:s0 + m])
        ps = banks.b[0 + ji % 2]
        for i, kc in enumerate((1, 3, 5, 7)):
            s_ = sq[i % 2]
            P.act(s_[64:128, 0:m], yt[64:128, kc, 0:m], AF.Square)
            P.mm(ps[:, 0:m], K['onesb'][64:128, :], s_[64:128, 0:m], start=(i == 0), stop=(i == 3))
        r_ = rst[ji % 2]
        P.act(r_[:, 0:m], ps[:, 0:m], AF.Ln, scale=1.0 / 256, bias=K['eps'][:])
        P.act(r_[:, 0:m], r_[:, 0:m], AF.Exp, scale=-0.5)
        for i, kc in enumerate((1, 3, 5, 7)):
            P.stt(yt[64:128, kc, 0:m], yt[64:128, kc, 0:m], sw[64:128, i:i + 1], r_[64:128, 0:m], ALU.mult, ALU.mult)

    def P2(j, ji):
        if first:
            return
        ci, s0, m = j['ci'], j['s0'], j['m']
        isctx = 1 if ci == 0 else 0
        R = Rb[ci % 2]
        yt = ybuf3[ji % 3]
        for fc in range(2):
            po = banks.b[2 + (2 * ji + fc) % 4]
            for kc in range(8):
                P.mm(po[:, 0:m], wo[:, kc, fc * 128:(fc + 1) * 128], yt[:, kc, 0:m], start=(kc == 0), stop=(kc == 7))
            P.stt(R[:, fc, s0:s0 + m], po[:, 0:m], gateT[:, 4 + fc, isctx:isctx + 1], R[:, fc, s0:s0 + m], ALU.mult, ALU.add)

    def P3(j, ji):
        ci, t0, n, s0, m = j['ci'], j['t0'], j['n'], j['s0'], j['m']
        R = Rb[ci % 2]
        srow = ssr[ci % 2]
        pss = banks.b[6 + ji % 2]
        for fc in range(2):
            s_ = sq[2 + fc]
            P.act(s_[:, 0:m], R[:, fc, s0:s0 + m], AF.Square)
            P.mm(pss[:, 0:m], K['onesb'][:, :], s_[:, 0:m], start=(fc == 0), stop=(fc == 1))
        P.copy('dve', srow[0:1, s0:s0 + m], pss[0:1, 0:m])
        if j['lastsub']:
            P.dma('sp', D_['ssb_' + stage][:, t0:t0 + n], srow[0:1, 0:n])
            P.dma('sp', Rs_v[:, :, t0:t0 + n], R[:, :, 0:n])

    nj = len(jobs)
    for it in range(nj + 2):
        if it < nj:
            P1(jobs[it], it)
        if 0 <= it - 1 < nj:
            P2(jobs[it - 1], it - 1)
        if 0 <= it - 2 < nj:
            P3(jobs[it - 2], it - 2)
    P.collective("AllGather", [D_['ssb_' + stage]], [D_['ssg_' + stage]], GROUPS)
    for (ci, t0, n) in chunks:
        isctx = 1 if ci == 0 else 0
        R = Rb[ci % 2]
        P.dma('sp', R[:, :, 0:n], Rs_v[:, :, t0:t0 + n])
        subs = [(s0, min(512, n - s0)) for s0 in range(0, n, 512)]
        sg_ = ssg[ci % 2]
        P.dma('act', sg_[0:4, 0:n], D_['ssg_' + stage][:, t0:t0 + n])
        for si, (s0, m) in enumerate(subs):
            pt = banks.b[6 + si % 2]
            P.mm(pt[:, 0:m], K['ones'][0:4, :], sg_[0:4, s0:s0 + m])
            r_ = rst[si % 2]
            P.act(r_[:, 0:m], pt[:, 0:m], AF.Ln, scale=1.0 / D, bias=K['eps'][:])
            P.act(r_[:, 0:m], r_[:, 0:m], AF.Exp, scale=-0.5)
            if not last:
                x_ = xo[si % 2]
                for fc in range(2):
                    t_ = t1[fc]
                    P.stt(t_[:, 0:m], R[:, fc, s0:s0 + m], Acoef[:, fc, isctx:isctx + 1], r_[:, 0:m], ALU.mult, ALU.mult)
                    P.act(x_[:, fc, 0:m], t_[:, 0:m], AF.Identity, bias=modN[:, fc, isctx:isctx + 1])
                xb_v = D_['xnb_' + stage][ci].rearrange("(fc p) t -> p fc t", p=128)
                P.dma('sp', xb_v[:, :, s0:s0 + m], x_[:, :, 0:m])
            else:
                o_v = io['outT'].rearrange("(fc p) t -> p fc t", p=128)
                for fc in range(2):
                    t_ = t1[(si * 2 + fc) % 4]
                    P.stt(t_[:, 0:m], R[:, fc, s0:s0 + m], fw[:, fc:fc + 1], r_[:, 0:m], ALU.mult, ALU.mult)
                    P.dma('sp', o_v[:, fc, t0 - CTXL + s0:t0 - CTXL + s0 + m], t_[:, 0:m], final=True)
        if not last:
            P.collective("AllGather", [D_['xnb_' + stage][ci]], [D_['xng_' + stage][ci]], GROUPS)


def build_fused(nc, P):
    banks = Banks(P)
    K = load_consts_tok(nc, P)
    epst = P.sbuf("eps_t", [128, 1], F32)
    P.memset('pool', epst[:], EPS)
    K['eps'] = epst
    one = P.sbuf("one_t", [128, 1], F32)
    P.memset('pool', one[:], 1.0)
    K['one'] = one
    for nm, shp in (('selZ', [128, 64]), ('ident', [128, 128]), ('scanmask', [64, 512]), ('maskF', [128, 512]),
                    ('maskB', [128, 512]), ('scanmask2', [128, 2, 128]), ('sel2', [2, 128]), ('selrow', [2, 2, 128])):
        d_ = nc.dram_tensor(nm, shp, F32, kind="ExternalInput").ap()
        t_ = P.sbuf(nm + "_sb", shp, F32)
        P.dma('sp', t_[:], d_)
        K[nm] = t_
    identb = P.sbuf("identb_sb", [128, 128], BF16)
    P.copy('pool', identb[:], K['ident'][:])
    K['identb'] = identb
    onesb = P.sbuf("onesb_sb", [128, 128], BF16)
    P.memset('pool', onesb[:], 1.0)
    K['onesb'] = onesb
    for nm in ('negF', 'negB'):
        d_ = nc.dram_tensor(nm, [128, 512], F32, kind="ExternalInput").ap()
        t_ = P.sbuf(nm + "_f", [128, 512], F32)
        P.dma('sp', t_[:], d_)
        tb_ = P.sbuf(nm + "_sb", [128, 512], BF16)
        P.copy('pool', tb_[:], t_[:])
        K[nm] = tb_
    A = P.arena("arena", 184 * 1024)

    def din(name, shape, dt=F32):
        return nc.dram_tensor(name, list(shape), dt, kind="ExternalInput").ap()

    def dscr(name, shape, dt=F32):
        return nc.dram_tensor(name, list(shape), dt).ap()

    io = {'RT_in': din("RT", [256, NTOK]), 'cvec': din("cvec", [128, 8, 2]),
          'wmod': [din("wmod%d" % l, [D, 768]) for l in range(2)], 'bmod': [din("bmod%d" % l, [128, 6]) for l in range(2)],
          'normw': [din("normw%d" % l, [128, 2]) for l in range(2)], 'wout': [din("wout%d" % l, [D, 256]) for l in range(2)],
          'ssdnw': [din("ssdnw%d" % l, [128, 4]) for l in range(2)], 'fnw': din("fnw", [128, 2]),
          'outT': nc.dram_tensor("outT", [256, SEQ], F32, kind="ExternalOutput").ap()}
    Dm = {'Rs': dscr("Rs", [256, NTOK])}
    for stage in ('n0', 'n1', 'fin'):
        Dm['ssb_' + stage] = dscr("ssb_%s" % stage, [1, NTOK])
        Dm['ssg_' + stage] = dscr("ssg_%s" % stage, [4, NTOK])
    for stage in ('n0', 'n1'):
        Dm['xnb_' + stage] = [dscr("xnb_%s_%d" % (stage, ci), [256, n], BF16) for ci, (t0, n) in enumerate(CHUNKS)]
        Dm['xng_' + stage] = [dscr("xng_%s_%d" % (stage, ci), [D, n], BF16) for ci, (t0, n) in enumerate(CHUNKS)]
    for l in range(2):
        Dm['yb%d' % l] = [dscr("yb%d_%d" % (l, ci), [256, n], BF16) for ci, (t0, n) in enumerate(CHUNKS)]
        Dm['gy%d' % l] = [dscr("gy%d_%d" % (l, ci), [D, n], BF16) for ci, (t0, n) in enumerate(CHUNKS)]
    io['dram'] = Dm
    cosT = din("cosT", [128, SEQ])
    sinT = din("sinT", [128, SEQ])
    fs_token_phase(nc, P, K, banks, A, io, 0, 'first')
    for l in range(2):
        last = (l == 1)
        with_ctx = not last
        xng = Dm['xng_n%d' % l]

        def xn_load(xb, t0, n, xng=xng):
            if t0 < CTXL:
                P.dma('sp', xb[:, :, 0:n], xng[0].rearrange("(kc p) t -> p kc t", p=128)[:, :, t0:t0 + n])
            else:
                k = (t0 - CTXL) // 2048
                c0 = (t0 - CTXL) % 2048
                P.dma('sp', xb[:, :, 0:n], xng[k + 1].rearrange("(kc p) t -> p kc t", p=128)[:, :, c0:c0 + n])

        ybl = Dm['yb%d' % l]

        def y_store(mi, t0, n, src, ybl=ybl):
            if t0 < CTXL:
                P.dma('pool', ybl[0][mi * 64:(mi + 1) * 64, t0:t0 + n], src)
            else:
                k = (t0 - CTXL) // 2048
                c0 = (t0 - CTXL) % 2048
                P.dma('pool', ybl[k + 1][mi * 64:(mi + 1) * 64, c0:c0 + n], src)

        mio = {'y_store': y_store, 'cosT': cosT, 'sinT': sinT}
        mio['w_gla'] = din("w_gla%d" % l, [D, 320])
        mio['gla_par'] = din("gla_par%d" % l, [64, 2])
        mio['gla_w2bd'] = din("gla_w2bd%d" % l, [32, 64])
        mio['w_lru'] = din("w_lru%d" % l, [D, 192])
        mio['lru_par'] = din("lru_par%d" % l, [128, 9])
        mio['lru_gw'] = din("lru_gw%d" % l, [64, 256])
        mio['w_ssd'] = din("w_ssd%d" % l, [D, 322])
        mio['ssd_par'] = din("ssd_par%d" % l, [128, 5])
        mio['ssd_cpar'] = din("ssd_cpar%d" % l, [128, 2, 5])
        mio['ssd_scr'] = dscr("ssd_scr%d" % l, [6, NTOK])
        mio['w_attn'] = din("w_attn%d" % l, [D, 640])
        mio['diff_lam'] = din("diff_lam%d" % l, [128])
        mio['diff_subln_w'] = din("diff_subln_w%d" % l, [64, 1])
        gla_phase(nc, P, A, banks, K, xn_load, mio, l, with_ctx)
        lru_phase(nc, P, A, banks, K, xn_load, mio, l, with_ctx)
        ssd_phase(nc, P, A, banks, K, xn_load, mio, l, with_ctx)
        def after_q(q0, nq, l=l):
            if q0 < CTXL:
                P.collective("AllGather", [Dm['yb%d' % l][0]], [Dm['gy%d' % l][0]], GROUPS)
            elif (q0 - CTXL + nq) % 2048 == 0:
                k = (q0 - CTXL) // 2048
                P.collective("AllGather", [Dm['yb%d' % l][k + 1]], [Dm['gy%d' % l][k + 1]], GROUPS)
        mio['after_q'] = after_q
        attention_phase(nc, P, A, banks, K, xn_load, mio, l, with_ctx)
        fs_token_phase(nc, P, K, banks, A, io, l, 'last' if last else 'mid')


def fused_inputs(inp, core, cosT, sinT, cst):
    b, q = core // 4, core % 4
    h = q
    x, ctx, c, c_ctx = inp['x'], inp['ctx'], inp['c'], inp['c_ctx']
    fsl = slice(256 * q, 256 * q + 256)
    m = dict(cst)
    m['RT'] = np.ascontiguousarray(np.concatenate([ctx[b][:, fsl].T, x[b][:, fsl].T], axis=1))
    m['cvec'] = np.ascontiguousarray(np.stack([pk(c[b]), pk(c_ctx)], axis=2))
    m['cosT'] = cosT
    m['sinT'] = sinT
    m['fnw'] = pk(inp['final_norm_w'][fsl])
    perm = np.array([mi * 256 + hh * 64 + j for hh in range(4) for mi in range(4) for j in range(64)])
    for l in range(2):
        cols = np.r_[256 * q:256 * q + 256, 1024 + 256 * q:1024 + 256 * q + 256, 2048 + 256 * q:2048 + 256 * q + 256]
        m['wmod%d' % l] = np.ascontiguousarray(inp['w_mod'][l][:, cols])
        m['bmod%d' % l] = pk(inp['b_mod'][l][cols])
        m['normw%d' % l] = pk(inp['norm_w'][l][fsl])
        m['wout%d' % l] = np.ascontiguousarray(inp['w_out'][l][perm][:, fsl])
        sw = np.zeros((128, 4), np.float32)
        for i in range(4):
            sw[64:128, i] = inp['ssd_norm_w'][l][i * 64:(i + 1) * 64]
        m['ssdnw%d' % l] = sw
        mi_ = mix_inputs(inp, l, h, ('gla', 'lru', 'attn', 'ssd'), cosT, sinT, cst)
        for k_, v_ in mi_.items():
            if k_ in cst or k_ in ('cosT', 'sinT'):
                continue
            m[k_ + str(l)] = v_
    return m


def kernel(**inp):
    inp = {k: np.asarray(v) for k, v in inp.items()}
    cosT, sinT = rope_tables()
    cst = mix_consts()
    maps = [fused_inputs(inp, core, cosT, sinT, cst) for core in range(8)]
    res = _launch(build_fused, maps)
    out = np.zeros((2, SEQ, D), np.float32)
    for core in range(8):
        b, q = core // 4, core % 4
        out[b, :, 256 * q:256 * q + 256] = np.asarray(res[core]["outT"]).T
    return out
```
